# Optimizing a Trainium2 kernel written in Bass

```python
import math
import jax, jax.numpy as jnp
from jax import lax
import numpy as np

D_MODEL = 1024
BATCH = 8
SEQ = 2048
DEPTH = 1
DEC_BATCH = 128
DEC_SEQ = 1
PAST_LEN = 16384
PAGE_SIZE = 128

D_MIX = D_MODEL
D_GDN = D_MIX // 2
D_SGU = D_MIX - D_GDN
GDN_HEADS = 4
GDN_DK = D_GDN // GDN_HEADS
GDN_DV = D_GDN // GDN_HEADS
GDN_CHUNK = 64
CONV_W = 4
D_QKV = 3 * D_GDN
SGU_HEADS = 4
SGU_DH = D_SGU // SGU_HEADS
SGU_CHUNK = 128
D_FF = 4 * D_MODEL
D_PLE = 256
D_IN = D_QKV + D_GDN + 2 * GDN_HEADS + 2 * D_SGU
EPS = 1e-6

kernel_name = 'hybrid_gdn_chunkmlp_decode_step'


def rms_norm(x, gain):
    xf = x.astype(jnp.float32)
    y = xf * lax.rsqrt(jnp.mean(xf * xf, axis=-1, keepdims=True) + EPS)
    return (y * gain.astype(jnp.float32)).astype(x.dtype)


def layer_norm(x, gain, bias):
    xf = x.astype(jnp.float32)
    mu = jnp.mean(xf, axis=-1, keepdims=True)
    var = jnp.mean(jnp.square(xf - mu), axis=-1, keepdims=True)
    y = (xf - mu) * lax.rsqrt(var + EPS)
    return (y * gain.astype(jnp.float32) + bias.astype(jnp.float32)).astype(x.dtype)


def l2_normalize(x):
    return x * lax.rsqrt(jnp.sum(x * x, axis=-1, keepdims=True) + 1e-6)


def causal_short_conv(x, hist, w):
    l = x.shape[1]
    xx = jnp.concatenate([hist.astype(x.dtype), x], axis=1)
    y = sum(xx[:, j:j + l] * w[j] for j in range(CONV_W))
    return jax.nn.silu(y), xx[:, l:]


def gated_delta_chunked(q, k, v, g, beta, s0):
    b, l, h, dk = q.shape
    dv = v.shape[-1]
    c = GDN_CHUNK
    pad = (-l) % c
    n = (l + pad) // c

    def blocks4(t):
        t = jnp.pad(t, ((0, 0), (0, pad), (0, 0), (0, 0))).reshape(b, n, c, h, t.shape[-1])
        return jnp.transpose(t, (1, 0, 3, 2, 4))

    def blocks3(t):
        t = jnp.pad(t, ((0, 0), (0, pad), (0, 0))).reshape(b, n, c, h)
        return jnp.transpose(t, (1, 0, 3, 2))

    q = blocks4(q) * (dk ** -0.5)
    k = blocks4(k)
    v = blocks4(v)
    beta = blocks3(beta)
    gc = jnp.cumsum(blocks3(g), axis=-1)
    incl = jnp.tril(jnp.ones((c, c), bool))
    strict = jnp.tril(jnp.ones((c, c), bool), -1)
    decay = jnp.exp(jnp.where(incl, gc[..., :, None] - gc[..., None, :], -jnp.inf))
    kb = k * beta[..., None]
    a_mat = jnp.where(strict, jnp.einsum('nbhcd,nbhsd->nbhcs', kb, k) * decay, 0.0)
    lhs = a_mat + jnp.eye(c, dtype=a_mat.dtype)
    rhs = jnp.concatenate([v * beta[..., None], kb * jnp.exp(gc)[..., None]], axis=-1)
    sol = lax.linalg.triangular_solve(lhs, rhs, left_side=True, lower=True, unit_diagonal=True)
    u_wy = sol[..., :dv]
    w_wy = sol[..., dv:]
    attn = jnp.where(incl, jnp.einsum('nbhcd,nbhsd->nbhcs', q, k) * decay, 0.0)

    def step(s, inp):
        qi, ki, ui, wi, gi, ai = inp
        v_new = ui - jnp.einsum('bhcd,bhde->bhce', wi, s)
        o = (jnp.einsum('bhcd,bhde->bhce', qi * jnp.exp(gi)[..., None], s)
             + jnp.einsum('bhcs,bhse->bhce', ai, v_new))
        g_last = gi[..., -1]
        s = (s * jnp.exp(g_last)[..., None, None]
             + jnp.einsum('bhcd,bhce->bhde', ki * jnp.exp(g_last[..., None] - gi)[..., None], v_new))
        return s, o

    s_fin, o = lax.scan(step, s0, (q, k, u_wy, w_wy, gc, attn))
    o = jnp.transpose(o, (1, 0, 3, 2, 4)).reshape(b, n * c, h, dv)[:, :l]
    return o, s_fin


def chunk_spatial_gate(u, v, w_s, b_s):
    b, l, h, dh = u.shape
    c = SGU_CHUNK
    pad = (-l) % c
    n = (l + pad) // c
    vp = jnp.pad(v, ((0, 0), (0, pad), (0, 0), (0, 0))).reshape(b, n, c, h, dh)
    w = jnp.where(jnp.tril(jnp.ones((c, c), bool))[None], w_s, 0.0)
    mix = jnp.einsum('hts,bnshd->bnthd', w, vp) + jnp.transpose(b_s)[None, None, :, :, None]
    return u * mix.reshape(b, n * c, h, dh)[:, :l]


def hybrid_layer(x, p, conv_hist, s0, g_mix, w_in, w_conv, a_log, dt_bias, gdn_norm,
                 sgu_ln_g, sgu_ln_b, w_s, b_s, w_out, g_ff, w_up, w_down, g_ple, w_ple, w_ple_gate):
    bsz, l, _ = x.shape
    f32 = jnp.float32
    h = rms_norm(x, g_mix)
    proj = h @ w_in
    o0 = D_QKV
    o1 = o0 + D_GDN
    o2 = o1 + GDN_HEADS
    o3 = o2 + GDN_HEADS
    qkv, new_hist = causal_short_conv(proj[..., :o0], conv_hist, w_conv)
    qkv = qkv.astype(f32).reshape(bsz, l, 3, GDN_HEADS, GDN_DK)
    q = l2_normalize(qkv[:, :, 0])
    k = l2_normalize(qkv[:, :, 1])
    v = qkv[:, :, 2]
    z = proj[..., o0:o1].astype(f32).reshape(bsz, l, GDN_HEADS, GDN_DV)
    beta = jax.nn.sigmoid(proj[..., o1:o2].astype(f32))
    g = -jnp.exp(a_log.astype(f32)) * jax.nn.softplus(proj[..., o2:o3].astype(f32) + dt_bias.astype(f32))
    o, s_new = gated_delta_chunked(q, k, v, g, beta, s0.astype(f32))
    o = (rms_norm(o, gdn_norm) * jax.nn.silu(z)).astype(x.dtype).reshape(bsz, l, D_GDN)
    uv = jax.nn.gelu(proj[..., o3:])
    u = uv[..., :D_SGU]
    vg = layer_norm(uv[..., D_SGU:], sgu_ln_g, sgu_ln_b)
    sg = chunk_spatial_gate(u.reshape(bsz, l, SGU_HEADS, SGU_DH), vg.reshape(bsz, l, SGU_HEADS, SGU_DH),
                            w_s, b_s).reshape(bsz, l, D_SGU)
    x = x + jnp.concatenate([o, sg.astype(x.dtype)], axis=-1) @ w_out
    x = x + jnp.square(jax.nn.relu(rms_norm(x, g_ff) @ w_up)) @ w_down
    x = x + (p @ w_ple) * jax.nn.sigmoid(rms_norm(x, g_ple) @ w_ple_gate)
    return x, new_hist, s_new.astype(x.dtype), vg


def setup_inputs(seed: int = 0) -> dict:
    key = jax.random.key(seed)
    ks = jax.random.split(key, 32)
    nrm = lambda k, shape, scale: jax.random.normal(k, shape, jnp.float32) * scale
    gain = lambda k, shape: 1.0 + 0.02 * jax.random.normal(k, shape, jnp.float32)
    dt = jnp.exp(jax.random.uniform(ks[9], (DEPTH, GDN_HEADS), jnp.float32,
                                    minval=math.log(1e-3), maxval=math.log(1e-1)))
    return {
        'x_prompt': nrm(ks[0], (BATCH, SEQ, D_MODEL), 1.0),
        'x_sample': nrm(ks[1], (DEC_BATCH, DEC_SEQ, D_MODEL), 1.0),
        'state_conv': nrm(ks[2], (DEPTH, DEC_BATCH, CONV_W - 1, D_QKV), 1.0),
        'state_gdn': nrm(ks[3], (DEPTH, DEC_BATCH, GDN_HEADS, GDN_DK, GDN_DV), GDN_DK ** -0.5),
        'p_prompt': nrm(ks[4], (DEPTH, BATCH, SEQ, D_PLE), 1.0),
        'p_sample': nrm(ks[5], (DEPTH, DEC_BATCH, DEC_SEQ, D_PLE), 1.0),
        'g_mix': gain(ks[6], (DEPTH, D_MODEL)),
        'w_in': nrm(ks[7], (DEPTH, D_MODEL, D_IN), D_MODEL ** -0.5),
        'w_conv': nrm(ks[8], (DEPTH, CONV_W, D_QKV), CONV_W ** -0.5),
        'a_log': jnp.log(jax.random.uniform(ks[10], (DEPTH, GDN_HEADS), jnp.float32, minval=1.0, maxval=16.0)),
        'dt_bias': dt + jnp.log(-jnp.expm1(-dt)),
        'gdn_norm': gain(ks[11], (DEPTH, GDN_DV)),
        'sgu_ln_g': gain(ks[12], (DEPTH, D_SGU)),
        'sgu_ln_b': nrm(ks[13], (DEPTH, D_SGU), 0.02),
        'w_s': nrm(ks[14], (DEPTH, SGU_HEADS, SGU_CHUNK, SGU_CHUNK), SGU_CHUNK ** -0.5),
        'b_s': gain(ks[15], (DEPTH, SGU_HEADS, SGU_CHUNK)),
        'w_out': nrm(ks[16], (DEPTH, D_MIX, D_MODEL), D_MIX ** -0.5),
        'g_ff': gain(ks[17], (DEPTH, D_MODEL)),
        'w_up': nrm(ks[18], (DEPTH, D_MODEL, D_FF), D_MODEL ** -0.5),
        'w_down': nrm(ks[19], (DEPTH, D_FF, D_MODEL), D_FF ** -0.5),
        'g_ple': gain(ks[20], (DEPTH, D_MODEL)),
        'w_ple': nrm(ks[21], (DEPTH, D_PLE, D_MODEL), D_PLE ** -0.5),
        'w_ple_gate': nrm(ks[22], (DEPTH, D_MODEL, D_MODEL), D_MODEL ** -0.5),
        'g_final': gain(ks[23], (D_MODEL,)),
    }


def reference(x_prompt, x_sample, state_conv, state_gdn, p_prompt, p_sample, g_mix, w_in, w_conv,
              a_log, dt_bias, gdn_norm, sgu_ln_g, sgu_ln_b, w_s, b_s, w_out, g_ff, w_up, w_down,
              g_ple, w_ple, w_ple_gate, g_final):
    hp = x_prompt
    hs = x_sample
    bp = x_prompt.shape[0]
    zero_hist = jnp.zeros((bp, CONV_W - 1, D_QKV), x_prompt.dtype)
    zero_state = jnp.zeros((bp, GDN_HEADS, GDN_DK, GDN_DV), jnp.float32)
    conv_p, gdn_p, conv_s, gdn_s, v_s = [], [], [], [], []
    for i in range(DEPTH):
        lw = (g_mix[i], w_in[i], w_conv[i], a_log[i], dt_bias[i], gdn_norm[i], sgu_ln_g[i], sgu_ln_b[i],
              w_s[i], b_s[i], w_out[i], g_ff[i], w_up[i], w_down[i], g_ple[i], w_ple[i], w_ple_gate[i])
        hp, c_p, s_p, _ = hybrid_layer(hp, p_prompt[i], zero_hist, zero_state, *lw)
        hs, c_s, s_s, vg_s = hybrid_layer(hs, p_sample[i], state_conv[i], state_gdn[i], *lw)
        conv_p.append(c_p)
        gdn_p.append(s_p)
        conv_s.append(c_s)
        gdn_s.append(s_s)
        v_s.append(vg_s)
    y_prompt = rms_norm(hp, g_final)
    y_sample = rms_norm(hs, g_final)
    return (y_prompt, y_sample, jnp.stack(conv_p), jnp.stack(gdn_p), jnp.stack(conv_s), jnp.stack(gdn_s), jnp.stack(v_s))
```

```python
import os
import numpy as np
import concourse.bass as bass
import concourse.mybir as mybir
from concourse.bass_utils import run_bass_kernel_spmd

F32 = mybir.dt.float32
BF16 = mybir.dt.bfloat16
AF = mybir.ActivationFunctionType
ALU = mybir.AluOpType
AX = mybir.AxisListType


class Sched:
    ENGS = ("pe", "act", "dve", "pool", "sp")

    def __init__(self, nc, stack):
        self.nc = nc
        self.stack = stack
        self.streams = {e: [] for e in self.ENGS}
        self.esem = {e: stack.enter_context(nc.semaphore("c_" + e)) for e in self.ENGS[:4]}
        self.ecnt = {e: 0 for e in self.ENGS}
        self.waited = {e: {} for e in self.ENGS}
        self.res = {}
        self.dsem = {}
        self.sem_by_name = {}
        for e in self.ENGS[:4]:
            self.sem_by_name[self.esem[e].name] = self.esem[e]

    def _need(self, eng, ev, waits):
        if ev is None:
            return
        name, val, src = ev
        if src == eng and eng == "pe":
            return
        cur = waits.get(name, 0)
        if val > cur:
            waits[name] = val

    def _deps(self, eng, reads, writes):
        waits = {}
        for k in reads:
            r = self.res.get(k)
            if r is not None:
                self._need(eng, r[0], waits)
        for k in writes:
            r = self.res.get(k)
            if r is not None:
                if r[0] is not None and not (r[0][2] == eng):
                    self._need(eng, r[0], waits)
                for ev in r[1]:
                    self._need(eng, ev, waits)
        out = []
        w = self.waited[eng]
        for name, val in waits.items():
            if w.get(name, 0) < val:
                w[name] = val
                out.append((name, val))
        return out

    def _commit(self, ev, reads, writes):
        for k in reads:
            r = self.res.setdefault(k, [None, []])
            r[1].append(ev)
        for k in writes:
            self.res[k] = [ev, []]

    def op(self, eng, fn, reads=(), writes=()):
        waits = self._deps(eng, reads, writes)
        self.ecnt[eng] += 1
        ev = (self.esem[eng].name, self.ecnt[eng], eng)
        self.streams[eng].append((waits, [fn], ("inc", self.esem[eng], 1)))
        self._commit(ev, reads, writes)
        return ev

    def group(self, eng, fns, reads=(), writes=()):
        waits = self._deps(eng, reads, writes)
        self.ecnt[eng] += 1
        ev = (self.esem[eng].name, self.ecnt[eng], eng)
        self.streams[eng].append((waits, list(fns), ("inc", self.esem[eng], 1)))
        self._commit(ev, reads, writes)
        return ev

    def dma(self, eng, slot, fn, reads=(), writes=(), n=1):
        if slot not in self.dsem:
            s = self.stack.enter_context(self.nc.semaphore("d_" + slot))
            self.dsem[slot] = [s, 0]
            self.sem_by_name[s.name] = s
        waits = self._deps(eng, reads, writes)
        d = self.dsem[slot]
        fns = fn if isinstance(fn, (list, tuple)) else [fn]
        d[1] += 16 * len(fns)
        ev = (d[0].name, d[1], "dma")
        self.streams[eng].append((waits, list(fns), ("dmainc", d[0], 16)))
        self._commit(ev, reads, writes)
        return ev

    def barrier(self):
        evs = []
        for e in self.ENGS[:4]:
            if self.ecnt[e] > 0:
                evs.append((self.esem[e].name, self.ecnt[e]))
        for slot, (s, c) in self.dsem.items():
            if c > 0:
                evs.append((s.name, c))
        for eng in self.ENGS:
            w = self.waited[eng]
            waits = []
            for name, val in evs:
                if w.get(name, 0) < val:
                    w[name] = val
                    waits.append((name, val))
            if waits:
                self.streams[eng].append((waits, [], None))
        self.res.clear()

    def finish(self):
        eng = "sp"
        waits = []
        for slot, (s, c) in self.dsem.items():
            if c > 0:
                waits.append((s.name, c))
        for e in self.ENGS[:4]:
            if self.ecnt[e] > 0:
                waits.append((self.esem[e].name, self.ecnt[e]))
        self.streams[eng].append((waits, [], None))

    def replay(self, block):
        sbn = self.sem_by_name

        def run(e, items):
            for waits, fns, inc in items:
                for name, val in waits:
                    e.wait_ge(sbn[name], val)
                last = None
                for i, f in enumerate(fns):
                    ins = f(e)
                    if inc is not None and inc[0] == "dmainc":
                        ins.then_inc(inc[1], 16)
                    last = ins
                if inc is not None and inc[0] == "inc" and last is not None:
                    last.then_inc(inc[1], 1)

        st = self.streams

        @block.tensor
        def _(e):
            run(e, st["pe"])

        @block.scalar
        def _(e):
            run(e, st["act"])

        @block.vector
        def _(e):
            run(e, st["dve"])

        @block.gpsimd
        def _(e):
            run(e, st["pool"])

        @block.sync
        def _(e):
            run(e, st["sp"])


U8 = mybir.dt.uint8
T_P = 2048
T_S = 16
T_ALL = T_P + T_S
SEGS = [(0, 512), (512, 512), (1024, 512), (1536, 512), (2048, 16)]
NT = 17
EPS = 1e-6
D_IN = 3080
C_Q, C_K, C_V, C_Z, C_BA, C_U, C_VS = 0, 512, 1024, 1536, 2048, 2056, 2568


def mm(out, lhsT, rhs, start=True, stop=True):
    return lambda e: e.matmul(out, lhsT=lhsT, rhs=rhs, start=start, stop=stop)


def actf(out, in_, func, **kw):
    return lambda e: e.activation(out=out, in_=in_, func=func, **kw)


def tt(out, a, b, op):
    return lambda e: e.tensor_tensor(out=out, in0=a, in1=b, op=op)


def ts(out, a, s1, op0, s2=None, op1=None):
    if op1 is None:
        return lambda e: e.tensor_scalar(out=out, in0=a, scalar1=s1, scalar2=None, op0=op0)
    return lambda e: e.tensor_scalar(out=out, in0=a, scalar1=s1, scalar2=s2, op0=op0, op1=op1)


def stt(out, a, s, b, op0, op1):
    return lambda e: e.scalar_tensor_tensor(out=out, in0=a, scalar=s, in1=b, op0=op0, op1=op1)


def stt_pool(out, a, colap):
    return lambda e: e.tensor_tensor(out=out, in0=a, in1=colap.to_broadcast([128, 128]), op=ALU.mult)


def cp(out, in_):
    return lambda e: e.tensor_copy(out=out, in_=in_)


def dmaf(out, in_):
    return lambda e: e.dma_start(out=out, in_=in_)


class _Item:
    __slots__ = ("thunk", "eng", "reads", "writes", "dur")

    def __init__(self, thunk, eng, reads, writes, dur):
        self.thunk, self.eng, self.reads, self.writes, self.dur = thunk, eng, tuple(reads), tuple(writes), dur


_DUR = {"act": 0.5, "dve": 0.45, "pool": 0.5}


def mk_recorders(S, ops):
    def Lop(eng, fn, reads=(), writes=()):
        ops.append(_Item(lambda: S.op(eng, fn, reads=reads, writes=writes), eng, reads, writes, _DUR.get(eng, 0.4)))

    def Lgr(eng, fns, reads=(), writes=()):
        ops.append(_Item(lambda: S.group(eng, fns, reads=reads, writes=writes), eng, reads, writes, 0.1 + 0.13 * len(fns)))

    def Ldma(eng, slot, fn, reads=(), writes=()):
        ops.append(_Item(lambda: S.dma(eng, slot, fn, reads=reads, writes=writes), "q_" + eng, reads, writes, 2.5))
    return Lop, Lgr, Ldma


def zipper(lists):
    lists = [l for l in lists if l]
    idx = [0] * len(lists)
    if os.environ.get("ZIP", "rr") == "rr":
        live = True
        while live:
            live = False
            for i, l in enumerate(lists):
                if idx[i] < len(l):
                    it = l[idx[i]]
                    idx[i] += 1
                    live = True
                    if isinstance(it, _Item):
                        it.thunk()
                    else:
                        it()
        return
    t_eng, t_w, t_r = {}, {}, {}
    remaining = sum(len(l) for l in lists)
    while remaining:
        best = None
        for i, l in enumerate(lists):
            if idx[i] >= len(l):
                continue
            it = l[idx[i]]
            if not isinstance(it, _Item):
                best = (-1.0, i, it)
                break
            rdy = t_eng.get(it.eng, 0.0)
            for k in it.reads:
                rdy = max(rdy, t_w.get(k, 0.0))
            for k in it.writes:
                rdy = max(rdy, t_w.get(k, 0.0), t_r.get(k, 0.0))
            if best is None or rdy < best[0]:
                best = (rdy, i, it)
        rdy, i, it = best
        idx[i] += 1
        remaining -= 1
        if not isinstance(it, _Item):
            it()
            continue
        it.thunk()
        fin = rdy + it.dur
        if it.eng.startswith("q_"):
            t_eng[it.eng] = rdy + 0.1
        else:
            t_eng[it.eng] = fin
        for k in it.reads:
            t_r[k] = max(t_r.get(k, 0.0), fin)
        for k in it.writes:
            t_w[k] = fin
            t_r[k] = 0.0


class Bump:
    def __init__(self, arena, start, limit):
        self.t, self.off, self.limit = arena, start, limit

    def alloc(self, dtype, shape):
        esz = 4 if dtype == F32 else 2
        n = 1
        for s in shape[1:]:
            n *= s
        nb = (n * esz + 63) // 64 * 64
        o = self.off
        self.off += nb
        assert self.off <= self.limit, ("SBUF arena overflow", self.off, self.limit)
        ap = self.t[:, o:o + n * esz].bitcast(dtype)
        if len(shape) == 3:
            ap = ap.rearrange("p (a b) -> p a b", a=shape[1])
        elif len(shape) == 4:
            ap = ap.rearrange("p (a b c) -> p a b c", a=shape[1], b=shape[2])
        return ap


def build_program():
    from contextlib import ExitStack
    nc = bass.Bass("TRN2", target_bir_lowering=False)

    def din(name, shape):
        return nc.dram_tensor(name, shape, F32, kind="ExternalInput").ap()

    def dout(name, shape):
        return nc.dram_tensor(name, shape, F32, kind="ExternalOutput").ap()

    x_p = din("x_p", [T_P, 1024]); x_s = din("x_s", [T_S, 1024])
    st_conv = din("st_conv", [T_S, 3, 1536]); st_gdn = din("st_gdn", [T_S, 4, 128, 128])
    p_p = din("p_p", [T_P, 256]); p_s = din("p_s", [T_S, 256])
    g_mix = din("g_mix", [1, 1024]); w_in = din("w_in", [1024, D_IN]); w_conv = din("w_conv", [4, 1536])
    a_log = din("a_log", [1, 4]); dt_bias = din("dt_bias", [1, 4]); gdn_norm = din("gdn_norm", [1, 128])
    ln_g = din("ln_g", [1, 512]); ln_b = din("ln_b", [1, 512]); w_s = din("w_s", [4, 128, 128]); b_s = din("b_s", [1, 512])
    w_out = din("w_out", [1024, 1024]); g_ff = din("g_ff", [8, 128]); w_up = din("w_up", [1024, 4096]); w_down = din("w_down", [4096, 1024])
    g_ple = din("g_ple", [8, 128]); w_ple = din("w_ple", [256, 1024]); w_gate = din("w_gate", [1024, 1024]); g_fin = din("g_fin", [8, 128])
    y_p = dout("y_p", [T_P, 1024]); y_s = dout("y_s", [T_S, 1024])
    ncp = dout("ncp", [3, 1536]); ngp = dout("ngp", [4, 128, 128])
    ncs = dout("ncs", [T_S, 3, 1536]); ngs = dout("ngs", [T_S, 4, 128, 128]); nsv = dout("nsv", [T_S, 512])

    w_in_v = w_in.rearrange("(k p) c -> p k c", p=128)
    w_out_v = w_out.rearrange("(k p) c -> p k c", p=128)
    w_up_v = w_up.rearrange("(k p) c -> p k c", p=128)
    w_down_v = w_down.rearrange("(k p) c -> p k c", p=128)
    w_gate_v = w_gate.rearrange("(k p) c -> p k c", p=128)
    w_ple_v = w_ple.rearrange("(k p) c -> p k c", p=128)

    with ExitStack() as st:
        S = Sched(nc, st)
        ARENA = 206 * 1024
        arena = st.enter_context(nc.sbuf_tensor("arena", [128, ARENA], U8))
        ps = [st.enter_context(nc.psum_tensor("ps%d" % i, [128, 512], F32)) for i in range(8)]
        psb = [p[:, :].bitcast(BF16) for p in ps]
        PK = [("ps", i) for i in range(8)]

        P = Bump(arena, 0, ARENA)
        ident_f = P.alloc(F32, [128, 128]); ident_b = P.alloc(BF16, [128, 128])
        ones_f = P.alloc(F32, [128, 128]); ones_b = P.alloc(BF16, [128, 128])
        mask_incl = P.alloc(F32, [128, 128])
        mask_su = P.alloc(F32, [128, 128])
        nmask_sl = P.alloc(F32, [128, 128])
        sel127 = P.alloc(F32, [128, 128])
        nmask_su = P.alloc(F32, [128, 128])
        rowstage = P.alloc(F32, [128, 128])
        cols = P.alloc(F32, [128, 128])
        wsT = P.alloc(BF16, [128, 4, 128])
        selws = P.alloc(BF16, [128, 4, 16])
        ws00 = P.alloc(F32, [128, 4])
        bs_row = P.alloc(F32, [128, 4, 128])
        lng_row = P.alloc(F32, [128, 512]); lnb_row = P.alloc(F32, [128, 512])
        alog_row = P.alloc(F32, [128, 4]); dtb_row = P.alloc(F32, [128, 4]); nexpA_row = P.alloc(F32, [128, 4])
        zcol = P.alloc(F32, [128, 4])
        zeros_f = P.alloc(F32, [128, 128])
        cat = P.alloc(BF16, [128, 8, T_ALL])
        P_C0 = P.off
        ba = P.alloc(F32, [128, NT, 8])
        beta_c = P.alloc(F32, [128, NT, 4]); g_c = P.alloc(F32, [128, NT, 4]); gc_c = P.alloc(F32, [128, NT, 4])
        bexp_c = P.alloc(F32, [128, NT, 4]); kd_c = P.alloc(F32, [128, NT, 4]); egl_c = P.alloc(F32, [128, NT, 4])
        tmp68 = P.alloc(F32, [128, NT, 4])
        qkv = P.alloc(BF16, [128, 12, T_ALL])
        zs = P.alloc(BF16, [128, 4, T_ALL])
        qks_f = P.alloc(F32, [128, 12, 16])
        histT = P.alloc(F32, [128, 12, 3, 16])
        ncp_st = P.alloc(F32, [128, 12, 3]); ncs_st = P.alloc(F32, [128, 12, 16])
        S_f = P.alloc(F32, [128, 4, 128]); S_b = P.alloc(BF16, [128, 4, 128])
        X0 = P.off
        XB = Bump(arena, X0, ARENA)
        hT = XB.alloc(BF16, [128, 8, T_ALL])
        Y0 = XB.off

        def wcol(j, c):
            return cols[:, 24 + j * 12 + c: 24 + j * 12 + c + 1]

        def gcol(which, m):
            return cols[:, which * 8 + m: which * 8 + m + 1]
        gdnn_col = cols[:, 72:73]
        negone = zcol[:, 1:2]

        S.op("pool", lambda e: e.memset(ones_f, 1.0), writes=["ones_f"])
        S.op("pool", lambda e: e.memset(ones_b, 1.0), writes=["ones_b"])
        S.op("pool", lambda e: e.memset(zcol, 0.0), writes=["zcol"])
        S.op("pool", lambda e: e.memset(zcol[:, 1:2], -1.0), reads=["zcol"], writes=["zcol"])
        S.op("pool", lambda e: e.memset(zeros_f, 0.0), writes=["zeros_f"])
        S.op("pool", lambda e: e.affine_select(out=ident_f, in_=ones_f, pattern=[[-1, 128]], compare_op=ALU.is_equal, fill=0.0, base=0, channel_multiplier=1), reads=["ones_f"], writes=["ident_f"])
        S.op("pool", lambda e: e.affine_select(out=mask_incl, in_=ones_f, pattern=[[1, 128]], compare_op=ALU.is_ge, fill=0.0, base=0, channel_multiplier=-1), reads=["ones_f"], writes=["mask_incl"])
        S.op("pool", lambda e: e.affine_select(out=mask_su, in_=ones_f, pattern=[[1, 128]], compare_op=ALU.is_gt, fill=0.0, base=0, channel_multiplier=-1), reads=["ones_f"], writes=["mask_su"])
        S.op("pool", lambda e: e.affine_select(out=nmask_sl, in_=ones_f, pattern=[[-1, 128]], compare_op=ALU.is_gt, fill=0.0, base=0, channel_multiplier=1), reads=["ones_f"], writes=["nmask_sl"])
        S.op("pool", ts(nmask_sl, nmask_sl, -1.0, ALU.mult), reads=["nmask_sl"], writes=["nmask_sl"])
        S.op("pool", ts(nmask_su, mask_su, -1.0, ALU.mult), reads=["mask_su"], writes=["nmask_su"])
        S.op("pool", lambda e: e.affine_select(out=sel127, in_=ones_f, pattern=[[0, 128]], compare_op=ALU.is_equal, fill=0.0, base=-127, channel_multiplier=1), reads=["ones_f"], writes=["sel127"])
        S.op("dve", cp(ident_b, ident_f), reads=["ident_f"], writes=["ident_b"])
        S.op("pool", lambda e: e.memset(rowstage, 0.0), writes=["rowstage"])
        S.dma("sp", "c0", [dmaf(rowstage[0:8, :], g_ff), dmaf(rowstage[8:16, :], g_ple), dmaf(rowstage[16:24, :], g_fin),
                           dmaf(rowstage[24:72, :], w_conv.rearrange("j (c p) -> (j c) p", p=128)), dmaf(rowstage[72:73, :], gdn_norm)],
              writes=["rowstage"])
        S.group("pe", [mm(ps[0][:, 0:128], rowstage, ident_f)], reads=["rowstage", "ident_f"], writes=[PK[0]])
        S.op("dve", cp(cols, ps[0][:, 0:128]), reads=[PK[0]], writes=["cols"])
        S.dma("sp", "c1", [dmaf(bs_row.rearrange("p h t -> p (h t)"), b_s.partition_broadcast(128)),
                           dmaf(lng_row, ln_g.partition_broadcast(128)), dmaf(lnb_row, ln_b.partition_broadcast(128)),
                           dmaf(alog_row, a_log.partition_broadcast(128)), dmaf(dtb_row, dt_bias.partition_broadcast(128)),
                           ] + [dmaf(ws00[:, h:h + 1], w_s[h, 0, 0:1].partition_broadcast(128)) for h in range(4)],
              writes=["rows"])
        S.op("act", actf(nexpA_row, alog_row, AF.Exp), reads=["rows"], writes=["nexpA"])
        S.op("dve", ts(nexpA_row, nexpA_row, -1.0, ALU.mult), reads=["nexpA"], writes=["nexpA"])
        for h in range(4):
            S.op("dve", ts(selws[0:16, h, :], ident_f[0:16, 0:16], ws00[0:16, h:h + 1], ALU.mult), reads=["rows", "ident_f"], writes=[("selws", h)])

        YA = Bump(arena, Y0, ARENA)
        wstmp = YA.alloc(F32, [128, 4, 128])
        S.dma("sp", "c2", dmaf(wstmp, w_s.rearrange("h t s -> t h s")), writes=["wstmp"])
        for h in range(4):
            S.op("pool", lambda e, h=h: e.affine_select(out=wstmp[:, h, :], in_=wstmp[:, h, :], pattern=[[-1, 128]], compare_op=ALU.is_ge, fill=0.0, base=0, channel_multiplier=1),
                 reads=["wstmp"], writes=["wstmp"])
        S.group("pe", [mm(ps[1][:, h * 128:(h + 1) * 128], wstmp[:, h, :], ident_f) for h in range(4)], reads=["wstmp", "ident_f"], writes=[PK[1]])
        S.op("dve", cp(wsT.rearrange("p h t -> p (h t)"), ps[1][:, 0:512]), reads=[PK[1]], writes=["wsT"])

        gmix_row = YA.alloc(F32, [128, 1024])
        S.dma("sp", "c3", dmaf(gmix_row, g_mix.partition_broadcast(128)), writes=["gmix"])
        xt = [YA.alloc(F32, [128, 1024]) for _ in range(3)]
        xsq = [YA.alloc(F32, [128, 1024]) for _ in range(3)]
        xn = [YA.alloc(BF16, [128, 1024]) for _ in range(3)]
        stat = YA.alloc(F32, [128, NT, 2])

        def phaseA_tile(i):
            ops = []
            Lop, Lgr, Ldma = mk_recorders(S, ops)
            r = 128 if i < 16 else 16
            sl = i % 3
            src = x_p[i * 128:(i + 1) * 128, :] if i < 16 else x_s
            Ldma("sp", "xt%d" % sl, dmaf(xt[sl][0:r, :], src), writes=[("xt", sl)])
            Lop("act", actf(xsq[sl][0:r, :], xt[sl][0:r, :], AF.Square), reads=[("xt", sl)], writes=[("xsq", sl)])
            Lop("dve", lambda e: e.reduce_sum(out=stat[0:r, i, 0:1], in_=xsq[sl][0:r, :], axis=AX.X), reads=[("xsq", sl)], writes=[("stat", i)])
            Lop("dve", ts(stat[0:r, i, 1:2], stat[0:r, i, 0:1], 1.0 / 1024, ALU.mult, EPS, ALU.add), reads=[("stat", i)], writes=[("stat", i)])
            Lop("act", actf(stat[0:r, i, 1:2], stat[0:r, i, 1:2], AF.Sqrt), reads=[("stat", i)], writes=[("stat", i)])
            Lop("dve", lambda e: e.reciprocal(out=stat[0:r, i, 1:2], in_=stat[0:r, i, 1:2]), reads=[("stat", i)], writes=[("stat", i)])
            Lop("dve", stt(xn[sl][0:r, :], xt[sl][0:r, :], stat[0:r, i, 1:2], gmix_row[0:r, :], ALU.mult, ALU.mult),
                reads=[("xt", sl), ("stat", i), "gmix"], writes=[("xn", sl)])
            b = i % 3
            Lgr("pe", [lambda e, k=k: e.transpose(out=psb[b][:, k * 128:k * 128 + r], in_=xn[sl][0:r, k * 128:(k + 1) * 128], identity=ident_b[0:r, 0:r]) for k in range(8)],
                reads=[("xn", sl), "ident_b"], writes=[PK[b]])
            if i % 2 == 0:
                Lop("act", actf(hT[:, :, i * 128:i * 128 + r], psb[b].rearrange("p (k t) -> p k t", k=8)[:, :, 0:r], AF.Copy), reads=[PK[b]], writes=[("hT", i)])
            else:
                Lop("dve", cp(hT[:, :, i * 128:i * 128 + r], psb[b].rearrange("p (k t) -> p k t", k=8)[:, :, 0:r]), reads=[PK[b]], writes=[("hT", i)])
            return ops

        tilesA = [phaseA_tile(i) for i in range(NT)]
        for g0 in range(0, NT, 3):
            zipper(tilesA[g0:g0 + 3])
        S.barrier()

        YB = Bump(arena, Y0, ARENA)
        wb = [YB.alloc(BF16, [128, 8, 512]) for _ in range(2)]
        wb8 = YB.alloc(BF16, [128, 8, 8])
        lnst = YB.alloc(F32, [128, NT, 8])
        sct = [YB.alloc(F32, [128, 3, 128]) for _ in range(2)]
        YB_MID = YB.off
        vg = YB.alloc(BF16, [128, NT, 512])
        F6 = YB.alloc(F32, [128, 6, 512])
        f512 = [F6[:, i, :] for i in range(6)]
        vgs_f = f512[5]
        YB5 = Bump(arena, YB_MID, ARENA)
        NBS = 4
        pre = [YB5.alloc(F32, [128, 515]) for _ in range(NBS)]
        accb = [YB5.alloc(F32, [128, 512]) for _ in range(NBS)]
        rnb = [YB5.alloc(F32, [128, 512]) for _ in range(NBS)]
        sqb = [YB5.alloc(BF16, [128, 512]) for _ in range(NBS)]
        wdiag = [YB5.alloc(F32, [128, 4, 128]) for _ in range(2)]
        YB6 = Bump(arena, YB_MID, ARENA)
        stage_tok = YB6.alloc(F32, [128, 1536])
        stage2 = YB6.alloc(F32, [128, 1536])
        wb_n = [0]
        bank_n = [0]

        def next_bank(lo=0, hi=4):
            b = lo + bank_n[0] % (hi - lo)
            bank_n[0] += 1
            return b

        def load_w(view, c0, ncol=512):
            sl = wb_n[0] % 2
            wb_n[0] += 1
            S.dma("pool", "wb%d" % sl, dmaf(wb[sl][:, :, 0:ncol], view[:, :, c0:c0 + ncol]), writes=[("wb", sl)])
            return sl

        hT_keys = [("hT", i) for i in range(NT)]

        def seg_hT_keys(t0, n):
            return [("hT", i) for i in range(t0 // 128, (t0 + n + 127) // 128)]

        sl = load_w(w_in_v, C_VS)

        def vsgu_tile(i):
            ops = []
            Lop, Lgr, Ldma = mk_recorders(S, ops)
            r = 128 if i < 16 else 16
            b = (i % 3)
            Lgr("pe", [mm(ps[b][0:r, :], hT[:, k, i * 128:i * 128 + r], wb[sl][:, k, :], start=(k == 0), stop=(k == 7)) for k in range(8)],
                    reads=[("hT", i), ("wb", sl)], writes=[PK[b]])
            g1 = f512[i % 3]; g2 = f512[3 + i % 3]
            Lop("act", actf(g1[0:r, :], ps[b][0:r, :], AF.Gelu_apprx_tanh), reads=[PK[b]], writes=[("g1", i % 3)])
            Lop("pool", tt(g2[0:r, :], g1[0:r, :], g1[0:r, :], ALU.mult), reads=[("g1", i % 3)], writes=[("g2", i % 3)])
            Lop("dve", lambda e, i=i, r=r, g1=g1: e.reduce_sum(out=lnst[0:r, i, 0:1], in_=g1[0:r, :], axis=AX.X), reads=[("g1", i % 3)], writes=[("lnst", i)])
            Lop("dve", lambda e, i=i, r=r, g2=g2: e.reduce_sum(out=lnst[0:r, i, 1:2], in_=g2[0:r, :], axis=AX.X), reads=[("g2", i % 3)], writes=[("lnst", i)])
            L = lambda a, bb: lnst[0:r, i, a:bb]
            Lop("dve", ts(L(2, 3), L(0, 1), 1.0 / 512, ALU.mult), reads=[("lnst", i)], writes=[("lnst", i)])
            Lop("dve", tt(L(3, 4), L(2, 3), L(2, 3), ALU.mult), reads=[("lnst", i)], writes=[("lnst", i)])
            Lop("dve", stt(L(4, 5), L(1, 2), 1.0 / 512, L(3, 4), ALU.mult, ALU.subtract), reads=[("lnst", i)], writes=[("lnst", i)])
            Lop("dve", ts(L(4, 5), L(4, 5), EPS, ALU.add), reads=[("lnst", i)], writes=[("lnst", i)])
            Lop("act", actf(L(4, 5), L(4, 5), AF.Sqrt), reads=[("lnst", i)], writes=[("lnst", i)])
            Lop("dve", lambda e, i=i, r=r: e.reciprocal(out=lnst[0:r, i, 5:6], in_=lnst[0:r, i, 4:5]), reads=[("lnst", i)], writes=[("lnst", i)])
            Lop("dve", ts(g2[0:r, :], g1[0:r, :], L(2, 3), ALU.subtract, L(5, 6), ALU.mult), reads=[("g1", i % 3), ("lnst", i)], writes=[("g2", i % 3)])
            Lop("pool", tt(g2[0:r, :], g2[0:r, :], lng_row[0:r, :], ALU.mult), reads=[("g2", i % 3), "rows"], writes=[("g2", i % 3)])
            if i < 16:
                Lop("pool", tt(vg[0:r, i, :], g2[0:r, :], lnb_row[0:r, :], ALU.add), reads=[("g2", i % 3), "rows"], writes=[("vg", i)])
            else:
                Lop("pool", tt(vgs_f[0:r, :], g2[0:r, :], lnb_row[0:r, :], ALU.add), reads=[("g2", i % 3), "rows"], writes=[("g2", 2)])
                Lop("pool", cp(vg[0:r, i, :], vgs_f[0:r, :]), reads=[("g2", 2)], writes=[("vg", i)])
                Ldma("sp", "o_nsv", dmaf(nsv, vgs_f[0:r, :]), reads=[("g2", 2)])
            return ops

        tilesV = [vsgu_tile(i) for i in range(NT)]
        for g0 in range(0, NT, 3):
            zipper(tilesV[g0:g0 + 3])

        S.dma("pool", "wb8", dmaf(wb8, w_in_v[:, :, C_BA:C_BA + 8]), writes=["wb8"])
        S.op("pool", lambda e: e.memset(ba, 0.0), writes=["ba"])
        bq = 4
        for i in range(NT):
            r = 128 if i < 16 else 16
            S.group("pe", [mm(ps[bq][0:r, i * 8:(i + 1) * 8], hT[:, k, i * 128:i * 128 + r], wb8[:, k, :], start=(k == 0), stop=(k == 7)) for k in range(8)],
                    reads=[("hT", i), "wb8"], writes=[PK[bq]])
        S.op("dve", cp(ba[:, 0:16, :], ps[bq][:, 0:128].rearrange("p (i c) -> p i c", c=8)), reads=[PK[bq], "ba"], writes=["ba"])
        S.op("dve", cp(ba[0:16, 16, :], ps[bq][0:16, 128:136]), reads=[PK[bq], "ba"], writes=["ba"])
        S.op("act", actf(beta_c, ba[:, :, 0:4], AF.Sigmoid), reads=["ba"], writes=["beta_c"])
        S.op("dve", tt(tmp68, ba[:, :, 4:8], dtb_row.unsqueeze(1).to_broadcast([128, NT, 4]), ALU.add), reads=["ba", "rows"], writes=["tmp68"])
        S.op("act", actf(tmp68, tmp68, AF.Exp), reads=["tmp68"], writes=["tmp68"])
        S.op("act", actf(tmp68, tmp68, AF.Ln, bias=1.0), reads=["tmp68"], writes=["tmp68"])
        S.op("dve", tt(g_c, tmp68, nexpA_row.unsqueeze(1).to_broadcast([128, NT, 4]), ALU.mult), reads=["tmp68", "nexpA"], writes=["g_c"])
        g68 = g_c.rearrange("p i h -> p (i h)"); gc68 = gc_c.rearrange("p i h -> p (i h)")
        S.group("pe", [mm(ps[5][:, 0:68], mask_incl, g68)], reads=["mask_incl", "g_c"], writes=[PK[5]])
        S.op("dve", cp(gc68, ps[5][:, 0:68]), reads=[PK[5]], writes=["gc_c"])
        S.group("pe", [mm(ps[5][:, 128:196], sel127, gc68)], reads=["sel127", "gc_c"], writes=[PK[5]])
        S.op("dve", cp(egl_c.rearrange("p i h -> p (i h)"), ps[5][:, 128:196]), reads=[PK[5]], writes=["egl_c"])
        S.op("dve", tt(tmp68.rearrange("p i h -> p (i h)"), egl_c.rearrange("p i h -> p (i h)"), gc68, ALU.subtract), reads=["egl_c", "gc_c"], writes=["tmp68"])
        S.op("act", actf(egl_c, egl_c, AF.Exp), reads=["egl_c", "tmp68"], writes=["egl_c"])
        S.op("act", actf(kd_c, tmp68, AF.Exp), reads=["tmp68"], writes=["kd_c"])
        S.op("act", actf(bexp_c, gc_c, AF.Exp), reads=["gc_c"], writes=["bexp_c"])
        S.op("dve", tt(bexp_c, bexp_c, beta_c, ALU.mult), reads=["bexp_c", "beta_c"], writes=["bexp_c"])

        sl = load_w(w_in_v, C_U)
        for h in range(4):
            for (t0, n) in SEGS:
                b = next_bank()
                S.group("pe", [mm(ps[b][:, 0:n], wb[sl][:, k, h * 128:(h + 1) * 128], hT[:, k, t0:t0 + n], start=(k == 0), stop=(k == 7)) for k in range(8)],
                        reads=seg_hT_keys(t0, n) + [("wb", sl)], writes=[PK[b]])
                u = f512[bank_n[0] % 2]
                S.op("act", actf(u[:, 0:n], ps[b][:, 0:n], AF.Gelu_apprx_tanh), reads=[PK[b]], writes=[("u", bank_n[0] % 2)])
                b2 = 4 + bank_n[0] % 2
                if n == 512:
                    tiles = [t0 // 128 + j for j in range(4)]
                    S.group("pe", [mm(ps[b2][:, j * 128:(j + 1) * 128], vg[:, tiles[j], h * 128:(h + 1) * 128], wsT[:, h, :]) for j in range(4)],
                            reads=[("vg", ti) for ti in tiles] + ["wsT"], writes=[PK[b2]])
                    S.op("dve", tt(f512[4][:, :].rearrange("p (j t) -> p j t", j=4), ps[b2][:, :].rearrange("p (j t) -> p j t", j=4),
                                   bs_row[:, h:h + 1, :].to_broadcast([128, 4, 128]), ALU.add), reads=[PK[b2], "rows"], writes=["mixt"])
                else:
                    S.group("pe", [mm(ps[b2][:, 0:16], vg[0:16, 16, h * 128:(h + 1) * 128], selws[0:16, h, :])],
                            reads=[("vg", 16), ("selws", h)], writes=[PK[b2]])
                    S.op("dve", tt(f512[4][:, 0:16], ps[b2][:, 0:16], bs_row[:, h, 0:1].to_broadcast([128, 16]), ALU.add), reads=[PK[b2], "rows"], writes=["mixt"])
                S.op("pool", tt(cat[:, 4 + h, t0:t0 + n], f512[4][:, 0:n], u[:, 0:n], ALU.mult), reads=["mixt", ("u", bank_n[0] % 2)], writes=[("cat", 4 + h, t0)])

        sl = load_w(w_in_v, C_Z)
        for h in range(4):
            for (t0, n) in SEGS:
                b = next_bank()
                S.group("pe", [mm(ps[b][:, 0:n], wb[sl][:, k, h * 128:(h + 1) * 128], hT[:, k, t0:t0 + n], start=(k == 0), stop=(k == 7)) for k in range(8)],
                        reads=seg_hT_keys(t0, n) + [("wb", sl)], writes=[PK[b]])
                S.op("act", actf(zs[:, h, t0:t0 + n], ps[b][:, 0:n], AF.Silu), reads=[PK[b]], writes=[("zs", h, t0)])

        S.barrier()

        def qkv_unit(blk, h, si, ui, wsl):
            ops = []
            Lop, Lgr, Ldma = mk_recorders(S, ops)
            c = blk * 4 + h
            t0, n = SEGS[si]
            bs = ui % NBS
            pr, acc, sq, rn = pre[bs], accb[bs], sqb[bs], rnb[bs]
            kp, ka, ks, kr = ("pre", bs), ("acc", bs), ("sq", bs), ("rn", bs)
            b = ui % 4; bn = 4 + ui % 4
            Lgr("pe", [mm(ps[b][:, 0:n], wb[wsl][:, k, h * 128:(h + 1) * 128], hT[:, k, t0:t0 + n], start=(k == 0), stop=(k == 7)) for k in range(8)],
                reads=[("wb", wsl)], writes=[PK[b]])
            if si == 0:
                Lop("dve", lambda e: e.memset(pr[:, 0:3], 0.0), writes=[kp])
            elif si < 4:
                Lgr("pe", [mm(ps[bn][:, 0:3], wb[wsl][:, k, h * 128:(h + 1) * 128], hT[:, k, t0 - 3:t0], start=(k == 0), stop=(k == 7)) for k in range(8)],
                    reads=[("wb", wsl)], writes=[PK[bn]])
                Lop("dve", cp(pr[:, 0:3], ps[bn][:, 0:3]), reads=[PK[bn]], writes=[kp])
            Lop("act", actf(pr[:, 3:3 + n], ps[b][:, 0:n], AF.Copy), reads=[PK[b], kp], writes=[kp])
            if si == 3:
                Lop("dve", cp(ncp_st[:, c, :], pr[:, 512:515]), reads=[kp], writes=[("ncp_st", c)])
            if si < 4:
                wd = wdiag[c % 2]
                bc_ = 4 + ui % 4
                fns = []
                for t4 in range(4):
                    for j in range(4):
                        fns.append(mm(ps[bc_][:, t4 * 128:(t4 + 1) * 128], wd[:, j, :], pr[:, j + t4 * 128:j + (t4 + 1) * 128], start=(j == 0), stop=(j == 3)))
                Lgr("pe", fns, reads=[kp, ("wdiag", c % 2)], writes=[PK[bc_]])
                Lop("act", actf(acc[:, 0:n], ps[bc_][:, 0:n], AF.Silu), reads=[PK[bc_]], writes=[ka])
            else:
                Lop("dve", cp(ncs_st[:, c, :], pr[:, 3:19]), reads=[kp], writes=[("ncs_st", c)])
                Lop("act", actf(acc[:, 0:n], pr[:, 3:3 + n], AF.Copy, scale=wcol(3, c)), reads=[kp, "cols"], writes=[ka])
                for j in (2, 1, 0):
                    Lop("dve", stt(acc[:, 0:n], histT[:, c, j, :], wcol(j, c), acc[:, 0:n], ALU.mult, ALU.add), reads=[("histT", c), ka, "cols"], writes=[ka])
                Lop("act", actf(acc[:, 0:n], acc[:, 0:n], AF.Silu), reads=[ka], writes=[ka])
            if blk == 2:
                Lop("pool", cp(qkv[:, c, t0:t0 + n], acc[:, 0:n]), reads=[ka], writes=[("qkv", c, t0)])
                if si == 4:
                    Lop("pool", cp(qks_f[:, c, :], acc[:, 0:16]), reads=[ka], writes=[("qks_f", c)])
            else:
                Lop("pool", tt(sq[:, 0:n], acc[:, 0:n], acc[:, 0:n], ALU.mult), reads=[ka], writes=[ks])
                Lgr("pe", [mm(ps[bn][:, 0:n], ones_b, sq[:, 0:n])], reads=[ks, "ones_b"], writes=[PK[bn]])
                Lop("act", actf(rn[:, 0:n], ps[bn][:, 0:n], AF.Sqrt, bias=1e-6), reads=[PK[bn]], writes=[kr])
                Lop("dve", lambda e: e.reciprocal(out=rn[:, 0:n], in_=rn[:, 0:n]), reads=[kr], writes=[kr])
                scl = (128.0 ** -0.5) if blk == 0 else 1.0
                Lop("dve", stt(qkv[:, c, t0:t0 + n], acc[:, 0:n], scl, rn[:, 0:n], ALU.mult, ALU.mult), reads=[ka, kr], writes=[("qkv", c, t0)])
                if si == 4:
                    Lop("dve", stt(qks_f[:, c, :], acc[:, 0:16], scl, rn[:, 0:16], ALU.mult, ALU.mult), reads=[ka, kr], writes=[("qks_f", c)])
            return ops

        ui = 0
        for blk in range(3):
            wsl = load_w(w_in_v, blk * 512)
            units = []
            for h in range(4):
                c = blk * 4 + h
                scs = sct[c % 2]
                S.dma("sp", "sct%d" % (c % 2), dmaf(scs[0:16, :, :], st_conv[:, :, c * 128:(c + 1) * 128]), writes=[("sct", c % 2)])
                S.group("pe", [mm(ps[4 + c % 4][:, j * 16:(j + 1) * 16], scs[0:16, j, :], ident_f[0:16, 0:16]) for j in range(3)],
                        reads=[("sct", c % 2), "ident_f"], writes=[PK[4 + c % 4]])
                S.op("dve", cp(histT[:, c, :, :], ps[4 + c % 4][:, 0:48].rearrange("p (j b) -> p j b", j=3)), reads=[PK[4 + c % 4]], writes=[("histT", c)])
                for si in range(5):
                    uo = qkv_unit(blk, h, si, ui, wsl)
                    if si == 0:
                        pre_ops = [(lambda j=j, c=c: S.op("pool", stt_pool(wdiag[c % 2][:, j, :], ident_f, wcol(j, c)), reads=["ident_f", "cols"], writes=[("wdiag", c % 2)])) for j in range(4)]
                        uo = pre_ops + uo
                    units.append(uo)
                    ui += 1
            for g0 in range(0, len(units), 4):
                zipper(units[g0:g0 + 4])

        S.barrier()
        S.group("pe", [mm(ps[c // 4][0:3, (c % 4) * 128:(c % 4 + 1) * 128], ncp_st[:, c, :], ident_f) for c in range(12)],
                reads=[("ncp_st", c) for c in range(12)] + ["ident_f"], writes=[PK[0], PK[1], PK[2]])
        for q3 in range(3):
            S.op("dve", cp(stage_tok[0:3, q3 * 512:(q3 + 1) * 512], ps[q3][0:3, :]), reads=[PK[q3]], writes=["stage_tok"])
        S.dma("sp", "o_ncp", dmaf(ncp, stage_tok[0:3, :]), reads=["stage_tok"])
        S.group("pe", [mm(ps[c // 4][0:16, (c % 4) * 128:(c % 4 + 1) * 128], ncs_st[:, c, :], ident_f) for c in range(12)],
                reads=[("ncs_st", c) for c in range(12)] + ["ident_f"], writes=[PK[0], PK[1], PK[2]])
        for q3 in range(3):
            S.op("act", actf(stage2[0:16, q3 * 512:(q3 + 1) * 512], ps[q3][0:16, :], AF.Copy), reads=[PK[q3]], writes=["stage2"])
        S.dma("sp", "o_ncs", [dmaf(ncs[:, 2, :], stage2[0:16, :]), dmaf(ncs[:, 0:2, :], st_conv[:, 1:3, :])], reads=["stage2"])
        S.barrier()

        YG = Bump(arena, Y0, ARENA)
        osq = YG.alloc(BF16, [128, 512]); rn_o = YG.alloc(F32, [128, 512]); on_o = YG.alloc(F32, [128, 512])
        YG_EPI = YG.off
        NPW, DP = 4, 8
        CHDT = F32
        gN = lambda n_, dt_, shp: [YG.alloc(dt_, shp) for _ in range(n_)]
        Rs = gN(NPW, F32, [128, 256]); rhsR = gN(NPW, F32, [128, 256]); D0 = gN(NPW, F32, [128, 128]); E0 = gN(NPW, F32, [128, 128])
        EGr = gN(NPW, F32, [128, 128]); MB = gN(NPW, F32, [128, 128]); Qf = gN(NPW, F32, [128, 128]); Qs = gN(NPW, BF16, [128, 128])
        NNa = gN(NPW, CHDT, [128, 256]); NNb = gN(NPW, CHDT, [128, 256]); Xs = gN(NPW, BF16, [128, 128])
        NHa = gN(NPW, BF16, [128, 256]); NHb = gN(NPW, BF16, [128, 256])
        J0 = int(os.environ.get('GDN_J0', '6'))
        Qm = gN(DP, BF16, [128, 128]); attnT = gN(DP, BF16, [128, 128]); Kd = gN(DP, BF16, [128, 128]); Vb = gN(DP, BF16, [128, 128])
        qg = gN(DP, BF16, [128, 128]); nWT = gN(DP, BF16, [128, 128]); vn = gN(4, BF16, [128, 128])
        S.op("pool", lambda e: e.memset(S_f.rearrange("p h e -> p (h e)"), 0.0), writes=[("S_f", h) for h in range(4)])
        S.op("pool", lambda e: e.memset(S_b.rearrange("p h e -> p (h e)"), 0.0), writes=[("S_b", h) for h in range(4)])

        def gdn_P(n, h):
            ops = []
            Lop, Lgr, Ldma = mk_recorders(S, ops)
            u = n * 4 + h
            q = u % NPW; s = u % DP
            tok = slice(n * 128, (n + 1) * 128)
            kT = qkv[:, 4 + h, tok]; qT = qkv[:, h, tok]; vT = qkv[:, 8 + h, tok]
            col = lambda t: t[:, n, h:h + 1]
            K = lambda name: (name, q)
            H = lambda name: (name, s)
            bk = PK[q]; pb = ps[q]; pbb = psb[q]
            kk = pb[:, 256:384]; qk = pb[:, 384:512]
            Lop("pool", stt_pool(rhsR[q][:, 0:128], mask_incl, col(g_c)), reads=["mask_incl", "g_c"], writes=[K("rhsR")])
            Lop("pool", stt_pool(rhsR[q][:, 128:256], ident_f, col(beta_c)), reads=["ident_f", "beta_c"], writes=[K("rhsR")])
            Lgr("pe", [lambda e: e.transpose(out=pbb[:, 0:128], in_=kT, identity=ident_b),
                       lambda e: e.transpose(out=pbb[:, 128:256], in_=vT, identity=ident_b)], reads=[("qkv", n), "ident_b"], writes=[bk])
            ktok = pbb[:, 0:128]; vtok = pbb[:, 128:256]
            Lop("act", actf(Xs[q], ktok, AF.Copy, scale=col(bexp_c)), reads=[bk, "bexp_c"], writes=[K("Xs")])
            Lop("act", actf(Kd[s], ktok, AF.Copy, scale=col(kd_c)), reads=[bk, "kd_c"], writes=[H("Kd")])
            Lop("act", actf(Vb[s], vtok, AF.Copy, scale=col(beta_c)), reads=[bk, "beta_c"], writes=[H("Vb")])
            Lgr("pe", [mm(pb[:, 0:128], ones_f, rhsR[q][:, 0:128]), mm(pb[:, 128:256], ones_f, rhsR[q][:, 128:256]),
                       mm(kk, kT, kT), mm(qk, kT, qT)], reads=[K("rhsR"), "ones_f", ("qkv", n)], writes=[bk])
            Lop("dve", cp(Rs[q], pb[:, 0:256]), reads=[bk], writes=[K("Rs")])
            R_gc = Rs[q][:, 0:128]; R_be = Rs[q][:, 128:256]
            Lop("pool", lambda e: e.tensor_tensor(out=D0[q], in0=R_gc, in1=col(gc_c).to_broadcast([128, 128]), op=ALU.subtract), reads=[K("Rs"), "gc_c"], writes=[K("D0")])
            Lop("pool", ts(D0[q], D0[q], 0.0, ALU.min), reads=[K("D0")], writes=[K("D0")])
            Lop("act", actf(D0[q], D0[q], AF.Exp), reads=[K("D0")], writes=[K("D0")])
            Lop("act", actf(EGr[q], R_gc, AF.Exp), reads=[K("Rs")], writes=[K("EGr")])
            Lop("pool", tt(MB[q], R_be, D0[q], ALU.mult), reads=[K("Rs"), K("D0")], writes=[K("MB")])
            Lop("pool", tt(MB[q], MB[q], nmask_su, ALU.mult), reads=[K("MB"), "nmask_su"], writes=[K("MB")])
            Lop("pool", tt(D0[q], D0[q], mask_incl, ALU.mult), reads=[K("D0"), K("MB"), "mask_incl"], writes=[K("D0")])
            Lop("pool", tt(qg[s], qT, EGr[q], ALU.mult), reads=[("qkv", n), K("EGr")], writes=[H("qg")])
            Lop("dve", tt(NNa[q][:, 0:128], kk, MB[q], ALU.mult), reads=[bk, K("MB")], writes=[K("NNa")])
            Lop("dve", tt(attnT[s], qk, D0[q], ALU.mult), reads=[bk, K("D0")], writes=[H("attnT")])
            Lgr("pe", [mm(pb[:, 0:128], NNa[q][:, 0:128], ident_f)], reads=[K("NNa"), "ident_f"], writes=[bk])
            Lop("act", actf(NNa[q][:, 128:256], pb[:, 0:128], AF.Copy), reads=[bk], writes=[K("NNa")])
            Lop("pool", tt(Qf[q], ident_f, NNa[q][:, 0:128], ALU.add), reads=["ident_f", K("NNa")], writes=[K("Qf")])
            cur, nxt, kc, kn = NNa[q], NNb[q], K("NNa"), K("NNb")
            cur16, nxt16, kc16, kn16 = NHa[q], NHb[q], K("NHa"), K("NHb")
            if J0 == 0:
                Lop("pool", cp(cur16, cur), reads=[kc], writes=[kc16])
            for j in range(1, 7):
                f32lvl = j <= J0
                src, ksrc = (cur, kc) if f32lvl else (cur16, kc16)
                fns = []
                if j < 6:
                    fns.append(mm(pb[:, 0:128], src[:, 128:256], src[:, 0:128]))
                fns.append(mm(pb[:, 128:256], src[:, 0:128], src[:, 128:256]))
                Lgr("pe", fns, reads=[ksrc], writes=[bk])
                lo = 0 if j < 6 else 128
                if f32lvl:
                    Lop("act", actf(nxt[:, lo:256], pb[:, lo:256], AF.Copy), reads=[bk], writes=[kn])
                    if j == J0 and j < 6:
                        Lop("pool", cp(nxt16[:, lo:256], nxt[:, lo:256]), reads=[kn], writes=[kn16])
                    Lgr("pe", [mm(pb[:, 256:384], nxt[:, 128:256], Qf[q])], reads=[kn, K("Qf")], writes=[bk])
                else:
                    Lop("act", actf(nxt16[:, lo:256], pb[:, lo:256], AF.Copy), reads=[bk], writes=[kn16])
                    Lop("pool", cp(Qs[q], Qf[q]), reads=[K("Qf")], writes=[K("Qs")])
                    Lgr("pe", [mm(pb[:, 256:384], nxt16[:, 128:256], Qs[q])], reads=[kn16, K("Qs")], writes=[bk])
                Lop("dve", tt(Qf[q], Qf[q], pb[:, 256:384], ALU.add), reads=[bk, K("Qf")], writes=[K("Qf")])
                cur, nxt, kc, kn = nxt, cur, kn, kc
                cur16, nxt16, kc16, kn16 = nxt16, cur16, kn16, kc16
            Lop("pool", cp(Qm[s], Qf[q]), reads=[K("Qf")], writes=[H("Qm")])
            Lgr("pe", [mm(pb[:, 384:512], Xs[q], Qm[s])], reads=[K("Xs"), H("Qm")], writes=[bk])
            Lop("act", actf(nWT[s], pb[:, 384:512], AF.Copy, scale=negone), reads=[bk], writes=[H("nWT")])
            return ops

        def gdn_R(n, h):
            ops = []
            Lop, Lgr, Ldma = mk_recorders(S, ops)
            u = n * 4 + h
            s = u % DP
            H = lambda name: (name, s)
            col = lambda t: t[:, n, h:h + 1]
            bR = PK[4]; ob = 5 + n % 2
            V = ps[4][:, h * 128:(h + 1) * 128]
            Lgr("pe", [mm(V, Qm[s], Vb[s], start=True, stop=False),
                       mm(V, nWT[s], S_b[:, h, :], start=False, stop=True)],
                reads=[H("Qm"), H("Vb"), H("nWT"), ("S_b", h)], writes=[bR])
            Lop("dve", cp(vn[h], V), reads=[bR], writes=[("vn", h)])
            Lgr("pe", [mm(V, Kd[s], vn[h])], reads=[H("Kd"), ("vn", h)], writes=[bR])
            Lgr("pe", [mm(ps[ob][:, h * 128:(h + 1) * 128], S_b[:, h, :], qg[s], start=True, stop=False),
                       mm(ps[ob][:, h * 128:(h + 1) * 128], vn[h], attnT[s], start=False, stop=True)],
                reads=[("S_b", h), H("qg"), ("vn", h), H("attnT")], writes=[PK[ob]])
            Lop("dve", stt(S_f[:, h, :], S_f[:, h, :], col(egl_c), V, ALU.mult, ALU.add), reads=[bR, ("S_f", h), "egl_c"], writes=[("S_f", h)])
            Lop("act", actf(S_b[:, h, :], S_f[:, h, :], AF.Copy), reads=[("S_f", h)], writes=[("S_b", h)])
            return ops

        def gdn_epilogue(o_ps, ss_ps, ncol, t0, okeys, sskey):
            w = 4 * ncol
            S.op("act", actf(osq[:, 0:w], o_ps, AF.Square), reads=okeys, writes=["osq"])
            S.group("pe", [mm(ss_ps, ones_b, osq[:, 0:w])], reads=["osq", "ones_b"], writes=[sskey])
            S.op("dve", ts(rn_o[:, 0:w], ss_ps, 1.0 / 128, ALU.mult, EPS, ALU.add), reads=[sskey], writes=["rn_o"])
            S.op("act", actf(rn_o[:, 0:w], rn_o[:, 0:w], AF.Sqrt), reads=["rn_o"], writes=["rn_o"])
            S.op("dve", lambda e: e.reciprocal(out=rn_o[:, 0:w], in_=rn_o[:, 0:w]), reads=["rn_o"], writes=["rn_o"])
            S.op("dve", stt(on_o[:, 0:w], o_ps, gdnn_col, rn_o[:, 0:w], ALU.mult, ALU.mult), reads=okeys + ["rn_o", "cols"], writes=["on_o"])
            S.op("pool", tt(cat[:, 0:4, t0:t0 + ncol], on_o[:, 0:w].rearrange("p (h t) -> p h t", h=4), zs[:, :, t0:t0 + ncol], ALU.mult),
                 reads=["on_o", "zs"], writes=[("cat_o", t0)])

        _SK = os.environ.get('KSKIP', '')
        NCH = 0 if 'prompt' in _SK else int(os.environ.get('GDN_N', '16'))

        def epi_ops(n):
            ob = 5 + n % 2
            return [lambda: gdn_epilogue(ps[ob][:, :], ps[7][:, :], 128, n * 128, [PK[ob]], PK[7])]

        STAG = int(os.environ.get('GDN_STAG', '16'))
        lanes = [[(lambda: None)] * (h * STAG) for h in range(4)]
        for n in range(NCH):
            for h in range(4):
                lanes[h] += gdn_P(n, h) + gdn_R(n, h)
                if h == 3:
                    lanes[h] += epi_ops(n)
        zipper(lanes)
        S.dma("sp", "o_ngp", dmaf(ngp.rearrange("h d e -> d h e"), S_f), reads=[("S_f", h) for h in range(4)])

        S.barrier()
        _DO_SAMPLE = 'sample' not in _SK
        def _sample_section():
            XS = Bump(arena, X0, ARENA)
            S_all = XS.alloc(F32, [128, 64, 128])
            for q4 in range(4):
                S.dma("sp", "sall%d" % q4, dmaf(S_all[:, q4 * 16:(q4 + 1) * 16, :], st_gdn[q4 * 4:(q4 + 1) * 4].rearrange("b h d e -> d (b h) e")), writes=[("S_all", q4)])
            YG = Bump(arena, YG_EPI, ARENA)
            sv = YG.alloc(F32, [128, 8])
            rexp = YG.alloc(F32, [128, 8, 16])
            bcs = YG.alloc(F32, [128, 128])
            dcol = YG.alloc(F32, [128, 64])
            dtok = YG.alloc(F32, [128, 512]); ktoks = YG.alloc(F32, [128, 512])
            kmask = [YG.alloc(F32, [128, 512]) for _ in range(2)]
            S.op("dve", cp(sv[0:16, 0:4], beta_c[0:16, 16, :]), reads=["beta_c"], writes=["sv"])
            S.op("act", actf(sv[0:16, 4:8], g_c[0:16, 16, :], AF.Exp), reads=["g_c"], writes=["sv"])
            for j in range(8):
                S.op("dve", ts(rexp[0:16, j, :], ident_f[0:16, 0:16], sv[0:16, j:j + 1], ALU.mult), reads=["sv", "ident_f"], writes=["rexp"])
            S.group("pe", [mm(ps[0][:, 0:128], ones_f[0:16, :], rexp[0:16, :, :].rearrange("p j b -> p (j b)"))], reads=["rexp", "ones_f"], writes=[PK[0]])
            S.op("dve", cp(bcs, ps[0][:, 0:128]), reads=[PK[0]], writes=["bcs"])
            beta_bc = bcs[:, 0:64]; eg_bc = bcs[:, 64:128]
            S.group("pe", [mm(ps[1][:, h * 16 + b:h * 16 + b + 1], S_all[:, b * 4 + h, :], qks_f[:, 4 + h, b:b + 1]) for b in range(16) for h in range(4)],
                    reads=[("S_all", q4) for q4 in range(4)] + ["qks_f"], writes=[PK[1]])
            S.op("dve", tt(dcol, ps[1][:, 0:64], eg_bc, ALU.mult), reads=[PK[1], "bcs"], writes=["dcol"])
            S.op("dve", tt(dcol, qks_f[:, 8:12, :].rearrange("p h b -> p (h b)"), dcol, ALU.subtract), reads=["dcol", "qks_f"], writes=["dcol"])
            S.op("dve", tt(dcol, dcol, beta_bc, ALU.mult), reads=["dcol", "bcs"], writes=["dcol"])
            S.group("pe", [mm(ps[2][0:16, h * 128:(h + 1) * 128], dcol[:, h * 16:(h + 1) * 16], ident_f) for h in range(4)], reads=["dcol", "ident_f"], writes=[PK[2]])
            S.group("pe", [mm(ps[3][0:16, h * 128:(h + 1) * 128], qks_f[:, 4 + h, :], ident_f) for h in range(4)], reads=["qks_f", "ident_f"], writes=[PK[3]])
            S.op("dve", cp(dtok[0:16, :], ps[2][0:16, :]), reads=[PK[2]], writes=["dtok"])
            S.op("act", actf(ktoks[0:16, :], ps[3][0:16, :], AF.Copy), reads=[PK[3]], writes=["ktoks"])
            for b in range(16):
                km = kmask[b % 2]; pb = 4 + b % 2
                S.op("dve", ts(km[0:16, :], ktoks[0:16, :], ident_f[0:16, b:b + 1], ALU.mult), reads=["ktoks", "ident_f"], writes=[("kmask", b % 2)])
                S.group("pe", [mm(ps[pb][:, h * 128:(h + 1) * 128], km[0:16, h * 128:(h + 1) * 128], dtok[0:16, h * 128:(h + 1) * 128]) for h in range(4)],
                        reads=[("kmask", b % 2), "dtok"], writes=[PK[pb]])
                for h in range(4):
                    S.op("dve", stt(S_all[:, b * 4 + h, :], S_all[:, b * 4 + h, :], eg_bc[:, h * 16 + b:h * 16 + b + 1], ps[pb][:, h * 128:(h + 1) * 128], ALU.mult, ALU.add),
                         reads=[PK[pb], "bcs", ("S_all", b // 4)], writes=[("S_all", b // 4)])
            S.group("pe", [mm(ps[1][:, 64 + h * 16 + b:64 + h * 16 + b + 1], S_all[:, b * 4 + h, :], qks_f[:, h, b:b + 1]) for b in range(16) for h in range(4)],
                    reads=[("S_all", q4) for q4 in range(4)] + ["qks_f"], writes=[PK[1]])
            gdn_epilogue(ps[1][:, 64:128], ps[0][:, 128:192], 16, T_P, [PK[1]], PK[0])
            for q4 in range(4):
                S.dma("sp", "o_ngs%d" % q4, dmaf(ngs[q4 * 4:(q4 + 1) * 4].rearrange("b h d e -> d (b h) e"), S_all[:, q4 * 16:(q4 + 1) * 16, :]), reads=[("S_all", q4)])
        if _DO_SAMPLE:
            _sample_section()
        S.barrier()

        if 'phasec' in _SK:
            S.finish()
            with nc.Block() as block:
                S.replay(block)
            return nc
        YC = Bump(arena, P_C0, ARENA)
        R = YC.alloc(F32, [128, 8, 528]); xnC = YC.alloc(BF16, [128, 8, 528]); hid = YC.alloc(BF16, [128, 32, 528])
        r8 = [YC.alloc(BF16, [128, 8, 512]) for _ in range(3)]
        r16 = [YC.alloc(BF16, [128, 32, 256]) for _ in range(2)]
        xres = YC.alloc(F32, [128, 4, 1024]); xres_s = YC.alloc(F32, [128, 1024])
        pw = YC.alloc(BF16, [128, 2, 1024]); ptok = [YC.alloc(BF16, [128, 256]) for _ in range(2)]
        pT = YC.alloc(BF16, [128, 2, 528]); sqr = [YC.alloc(BF16, [128, 528]) for _ in range(2)]; rnC = YC.alloc(F32, [128, 528])
        sig = [YC.alloc(F32, [128, 528]) for _ in range(2)]; relu_t = [YC.alloc(F32, [128, 528]) for _ in range(2)]
        ytile = [YC.alloc(F32, [128, 1024]) for _ in range(1)]
        rncol = YC.alloc(F32, [128, 8])
        r8_n = [0]; r16_n = [0]; misc_n = [0]

        r8_seq = []
        for _p in range(4):
            r8_seq += [(w_out_v, 0), (w_out_v, 512)] + [(w_up_v, bb * 512) for bb in range(8)] + [(w_gate_v, 0), (w_gate_v, 512)]
        r8_issued = [0]

        def load_r8(view, c0):
            idx = r8_n[0]
            r8_n[0] += 1
            assert r8_seq[idx][1] == c0
            while r8_issued[0] < min(len(r8_seq), idx + 3):
                j = r8_issued[0]
                vw, cc = r8_seq[j]
                S.dma("pool", "r8_%d" % (j % 3), dmaf(r8[j % 3], vw[:, :, cc:cc + 512]), writes=[("r8", j % 3)])
                r8_issued[0] += 1
            return idx % 3

        def load_r16(c0):
            sl = r16_n[0] % 2
            r16_n[0] += 1
            S.dma("pool", "r16_%d" % sl, dmaf(r16[sl], w_down_v[:, :, c0:c0 + 256]), writes=[("r16", sl)])
            return sl

        S.dma("pool", "pw", dmaf(pw, w_ple_v), writes=["pw"])
        PASSES = [[(0, 512, 0)], [(512, 512, 0)], [(1024, 512, 0)], [(1536, 512, 0), (2048, 16, 512)]]

        def rms_norm_C(which, out_fn, segs, W, tag):
            bns = []
            for (t0, n, l0) in segs:
                bns.append(6 + misc_n[0] % 2)
                misc_n[0] += 1
            for m in range(8):
                sq = sqr[m % 2]
                S.op("act", actf(sq[:, 0:W], R[:, m, 0:W], AF.Square), reads=[("R", m)], writes=[("sqr", m % 2)])
                for si_, (t0, n, l0) in enumerate(segs):
                    bn = bns[si_]
                    S.group("pe", [mm(ps[bn][:, 0:n], ones_b, sq[:, l0:l0 + n], start=(m == 0), stop=(m == 7))],
                            reads=[("sqr", m % 2), "ones_b"], writes=[PK[bn]])
            for si_, (t0, n, l0) in enumerate(segs):
                bn = bns[si_]
                S.op("dve", ts(rnC[:, l0:l0 + n], ps[bn][:, 0:n], 1.0 / 1024, ALU.mult, EPS, ALU.add), reads=[PK[bn]], writes=["rnC"])
            S.op("act", actf(rnC[:, 0:W], rnC[:, 0:W], AF.Sqrt), reads=["rnC"], writes=["rnC"])
            S.op("dve", lambda e: e.reciprocal(out=rnC[:, 0:W], in_=rnC[:, 0:W]), reads=["rnC"], writes=["rnC"])
            for m in range(8):
                out_ap, wkey = out_fn(m)
                S.op("dve", stt(out_ap, R[:, m, 0:W], gcol(which, m), rnC[:, 0:W], ALU.mult, ALU.mult), reads=[("R", m), "rnC", "cols"], writes=[wkey])

        for pi, segs in enumerate(PASSES):
            W = sum(n for (_, n, _) in segs)
            t00 = segs[0][0]
            has_s = len(segs) > 1
            if pi == 0:
                S.dma("sp", "xres", dmaf(xres, x_p[0:512, :].rearrange("(j p) f -> p j f", p=128)), writes=["xres"])
            def stats_act(m):
                S.op("act", actf(sqr[m % 2][:, 0:W], R[:, m, 0:W], AF.Square), reads=[("R", m)], writes=[("sqr", m % 2)])

            def stats_pe(m, bns):
                for si_, (t0, n, l0) in enumerate(segs):
                    S.group("pe", [mm(ps[bns[si_]][:, 0:n], ones_b, sqr[m % 2][:, l0:l0 + n], start=(m == 0), stop=(m == 7))],
                            reads=[("sqr", m % 2), "ones_b"], writes=[PK[bns[si_]]])

            def norm_finish_row(bns, out_t, key, square):
                for si_, (t0, n, l0) in enumerate(segs):
                    S.op("dve", ts(out_t[:, l0:l0 + n], ps[bns[si_]][:, 0:n], 1.0 / 1024, ALU.mult, EPS, ALU.add), reads=[PK[bns[si_]]], writes=[key])
                if not square:
                    S.op("act", actf(out_t[:, 0:W], out_t[:, 0:W], AF.Sqrt), reads=[key], writes=[key])
                S.op("dve", lambda e: e.reciprocal(out=out_t[:, 0:W], in_=out_t[:, 0:W]), reads=[key], writes=[key])

            def pick_bns():
                o = []
                for _ in segs:
                    o.append(6 + misc_n[0] % 2)
                    misc_n[0] += 1
                return o

            bns1 = pick_bns()
            for blk in range(2):
                sl = load_r8(w_out_v, blk * 512)
                for m4 in range(4):
                    m = blk * 4 + m4
                    for (t0, n, l0) in segs:
                        b = next_bank()
                        fns = [mm(ps[b][:, 0:n], r8[sl][:, k, m4 * 128:(m4 + 1) * 128], cat[:, k, t0:t0 + n], start=(k == 0), stop=False) for k in range(8)]
                        if n == 512:
                            fns += [mm(ps[b][:, j * 128:(j + 1) * 128], xres[:, j, m * 128:(m + 1) * 128], ident_f, start=False, stop=(j == 3)) for j in range(4)]
                            rk = ["xres"]
                        else:
                            fns += [mm(ps[b][:, 0:16], xres_s[0:16, m * 128:(m + 1) * 128], ident_f[0:16, 0:16], start=False, stop=True)]
                            rk = ["xres_s"]
                        S.group("pe", fns, reads=[("r8", sl), "cat", "ident_f"] + rk, writes=[PK[b]])
                        S.op("act", actf(R[:, m, l0:l0 + n], ps[b][:, 0:n], AF.Copy), reads=[PK[b]], writes=[("R", m)])
                        S.op("act", actf(xnC[:, m, l0:l0 + n], ps[b][:, 0:n], AF.Copy, scale=gcol(0, m)), reads=[PK[b], "cols"], writes=[("xnC", m)])
                    stats_act(m)
                    if m >= 1:
                        stats_pe(m - 1, bns1)
            stats_pe(7, bns1)
            norm_finish_row(bns1, rnC, "rnC", True)
            if pi + 1 < len(PASSES):
                tn = PASSES[pi + 1][0][0]
                S.dma("sp", "xres", dmaf(xres, x_p[tn:tn + 512, :].rearrange("(j p) f -> p j f", p=128)), writes=["xres"])
                if len(PASSES[pi + 1]) > 1:
                    S.dma("sp", "xres_s", dmaf(xres_s[0:16, :], x_s), writes=["xres_s"])
            for blk in range(8):
                sl = load_r8(w_up_v, blk * 512)
                for m4 in range(4):
                    hc = blk * 4 + m4
                    for (t0, n, l0) in segs:
                        b = next_bank()
                        S.group("pe", [mm(ps[b][:, 0:n], r8[sl][:, k, m4 * 128:(m4 + 1) * 128], xnC[:, k, l0:l0 + n], start=(k == 0), stop=(k == 7)) for k in range(8)],
                                reads=[("r8", sl)] + [("xnC", k) for k in range(8)], writes=[PK[b]])
                        rt = relu_t[misc_n[0] % 2]; rkey = ("relu_t", misc_n[0] % 2)
                        misc_n[0] += 1
                        S.op("act", actf(rt[:, 0:n], ps[b][:, 0:n], AF.Relu), reads=[PK[b]], writes=[rkey])
                        S.op("dve", tt(hid[:, hc, l0:l0 + n], rt[:, 0:n], rt[:, 0:n], ALU.mult), reads=[rkey], writes=[("hid", hc)])
            bns2 = pick_bns()
            for blk in range(4):
                sl = load_r16(blk * 256)
                for m2 in range(2):
                    m = blk * 2 + m2
                    for (t0, n, l0) in segs:
                        b = next_bank()
                        S.group("pe", [mm(ps[b][:, 0:n], r16[sl][:, k, m2 * 128:(m2 + 1) * 128], hid[:, k, l0:l0 + n], start=(k == 0), stop=(k == 31)) for k in range(32)],
                                reads=[("r16", sl)] + [("hid", k) for k in range(32)], writes=[PK[b]])
                        sg = sig[misc_n[0] % 2]; skey = ("sig", misc_n[0] % 2)
                        misc_n[0] += 1
                        S.op("dve", tt(sg[:, 0:n], ps[b][:, 0:n], rnC[:, l0:l0 + n], ALU.mult), reads=[PK[b], "rnC"], writes=[skey])
                        S.op("dve", tt(R[:, m, l0:l0 + n], R[:, m, l0:l0 + n], sg[:, 0:n], ALU.add), reads=[skey, ("R", m)], writes=[("R", m)])
                    S.op("act", actf(xnC[:, m, 0:W], R[:, m, 0:W], AF.Copy, scale=gcol(1, m)), reads=[("R", m), "cols"], writes=[("xnC", m)])
                    stats_act(m)
                    if m >= 1:
                        stats_pe(m - 1, bns2)
            stats_pe(7, bns2)
            norm_finish_row(bns2, rnC, "rnC", False)
            for (t0, n, l0) in segs:
                ntile = (n + 127) // 128
                for j in range(ntile):
                    r = min(128, n - j * 128)
                    sl = misc_n[0] % 2
                    misc_n[0] += 1
                    src = p_p[t0 + j * 128:t0 + j * 128 + r, :] if n == 512 else p_s
                    S.dma("pool", "ptok%d" % sl, dmaf(ptok[sl][0:r, :], src), writes=[("ptok", sl)])
                    S.group("pe", [lambda e, kk=kk, sl=sl, r=r: e.transpose(out=psb[5][:, kk * 128:kk * 128 + r], in_=ptok[sl][0:r, kk * 128:(kk + 1) * 128], identity=ident_b[0:r, 0:r]) for kk in range(2)],
                            reads=[("ptok", sl), "ident_b"], writes=[PK[5]])
                    S.op("act", actf(pT[:, :, l0 + j * 128:l0 + j * 128 + r], psb[5][:, 0:256].rearrange("p (k t) -> p k t", k=2)[:, :, 0:r], AF.Copy), reads=[PK[5]], writes=["pT"])
            ntt = sum((n + 127) // 128 for (_, n, _) in segs)
            sigbufs = [(sig[0], ("sig", 0)), (sig[1], ("sig", 1)), (relu_t[0], ("relu_t", 0)), (relu_t[1], ("relu_t", 1))]

            def gate_chunk(m, sl, m4):
                ops = []
                Lop, Lgr, Ldma = mk_recorders(S, ops)
                for si_, (t0, n, l0) in enumerate(segs):
                    b = (2 * m + si_) % 4
                    pb_ = 6 + m % 2
                    sg, skey = sigbufs[(2 * m + si_) % 4]
                    Lgr("pe", [mm(ps[b][:, 0:n], r8[sl][:, k, m4 * 128:(m4 + 1) * 128], xnC[:, k, l0:l0 + n], start=(k == 0), stop=(k == 7)) for k in range(8)],
                        reads=[("r8", sl)] + [("xnC", k) for k in range(8)], writes=[PK[b]])
                    Lgr("pe", [mm(ps[pb_][:, 0:n], pw[:, kk, m * 128:(m + 1) * 128], pT[:, kk, l0:l0 + n], start=(kk == 0), stop=(kk == 1)) for kk in range(2)],
                        reads=["pw", "pT"], writes=[PK[pb_]])
                    Lop("dve", tt(sg[:, 0:n], ps[b][:, 0:n], rnC[:, l0:l0 + n], ALU.mult), reads=[PK[b], "rnC"], writes=[skey])
                    Lop("act", actf(sg[:, 0:n], sg[:, 0:n], AF.Sigmoid), reads=[skey], writes=[skey])
                    Lop("dve", tt(sg[:, 0:n], sg[:, 0:n], ps[pb_][:, 0:n], ALU.mult), reads=[PK[pb_], skey], writes=[skey])
                    Lop("dve", tt(R[:, m, l0:l0 + n], R[:, m, l0:l0 + n], sg[:, 0:n], ALU.add), reads=[skey, ("R", m)], writes=[("R", m)])
                sq = sqr[m % 2]
                Lop("act", actf(sq[:, 0:W], R[:, m, 0:W], AF.Square), reads=[("R", m)], writes=[("sqr", m % 2)])
                fns = []
                if m == 0:
                    fns.append(mm(ps[5][:, 256:256 + ntt], zeros_f, zeros_f[:, 0:ntt], start=True, stop=False))
                jt = 0
                for (t0, n, l0) in segs:
                    for j in range((n + 127) // 128):
                        r = min(128, n - j * 128)
                        fns.append(mm(ps[5][0:r, 256 + jt:257 + jt], sq[:, l0 + j * 128:l0 + j * 128 + r], ones_b[:, 0:1], start=False, stop=False))
                        jt += 1
                if m == 7:
                    fns.append(mm(ps[5][:, 256:256 + ntt], zeros_f, zeros_f[:, 0:ntt], start=False, stop=True))
                Lgr("pe", fns, reads=[("sqr", m % 2), "ones_b"], writes=[PK[5]])
                Lop("act", actf(R[:, m, 0:W], R[:, m, 0:W], AF.Copy, scale=gcol(2, m)), reads=[("R", m), ("sqr", m % 2), "cols"], writes=[("R", m)])
                return ops

            for blk in range(2):
                sl = load_r8(w_gate_v, blk * 512)
                chunks = [gate_chunk(blk * 4 + m4, sl, m4) for m4 in range(4)]
                zipper(chunks[0:2])
                zipper(chunks[2:4])
            S.op("dve", ts(rncol[:, 0:ntt], ps[5][:, 256:256 + ntt], 1.0 / 1024, ALU.mult, EPS, ALU.add), reads=[PK[5]], writes=["rncol"])
            S.op("act", actf(rncol[:, 0:ntt], rncol[:, 0:ntt], AF.Sqrt), reads=["rncol"], writes=["rncol"])
            S.op("dve", lambda e: e.reciprocal(out=rncol[:, 0:ntt], in_=rncol[:, 0:ntt]), reads=["rncol"], writes=["rncol"])
            jt = 0
            for (t0, n, l0) in segs:
                ntile = (n + 127) // 128
                for j in range(ntile):
                    r = min(128, n - j * 128)
                    ysl = 0
                    for half in range(2):
                        b = next_bank()
                        S.group("pe", [mm(ps[b][0:r, m4 * 128:(m4 + 1) * 128], R[:, half * 4 + m4, l0 + j * 128:l0 + j * 128 + r], ident_f) for m4 in range(4)],
                                reads=[("R", half * 4 + m4) for m4 in range(4)] + ["ident_f"], writes=[PK[b]])
                        S.op("act", actf(ytile[ysl][0:r, half * 512:(half + 1) * 512], ps[b][0:r, :], AF.Copy, scale=rncol[0:r, jt:jt + 1]), reads=[PK[b], "rncol"], writes=[("ytile", ysl)])
                    jt += 1
                    dst = y_p[t0 + j * 128:t0 + j * 128 + r, :] if n == 512 else y_s
                    S.dma("sp", "o_y%d" % ysl, dmaf(dst, ytile[ysl][0:r, :]), reads=[("ytile", ysl)])
        S.finish()
        with nc.Block() as block:
            S.replay(block)
    return nc


_PROG = {}


def _make_in_maps(inputs):
    f = lambda a: np.ascontiguousarray(np.asarray(a, dtype=np.float32))
    g = {k: f(v) for k, v in inputs.items()}
    shared = {
        "g_mix": g["g_mix"].reshape(1, 1024), "w_in": g["w_in"][0], "w_conv": g["w_conv"][0],
        "a_log": g["a_log"].reshape(1, 4), "dt_bias": g["dt_bias"].reshape(1, 4), "gdn_norm": g["gdn_norm"].reshape(1, 128),
        "ln_g": g["sgu_ln_g"].reshape(1, 512), "ln_b": g["sgu_ln_b"].reshape(1, 512), "w_s": g["w_s"][0],
        "b_s": g["b_s"].reshape(1, 512), "w_out": g["w_out"][0], "g_ff": g["g_ff"].reshape(8, 128), "w_up": g["w_up"][0],
        "w_down": g["w_down"][0], "g_ple": g["g_ple"].reshape(8, 128), "w_ple": g["w_ple"][0], "w_gate": g["w_ple_gate"][0],
        "g_fin": g["g_final"].reshape(8, 128),
    }
    maps = []
    for i in range(8):
        m = dict(shared)
        sl = slice(16 * i, 16 * i + 16)
        m["x_p"] = g["x_prompt"][i]
        m["x_s"] = g["x_sample"][sl, 0]
        m["st_conv"] = g["state_conv"][0, sl]
        m["st_gdn"] = g["state_gdn"][0, sl]
        m["p_p"] = g["p_prompt"][0, i]
        m["p_s"] = g["p_sample"][0, sl, 0]
        maps.append(m)
    return maps


def kernel(**inputs):
    if "nc" not in _PROG:
        _PROG["nc"] = build_program()
    nc = _PROG["nc"]
    maps = _make_in_maps(inputs)
    res = run_bass_kernel_spmd(nc, maps, core_ids=list(range(8)))
    R = res.results
    st = lambda name: np.stack([np.asarray(r[name], dtype=np.float32) for r in R])
    cc = lambda name: np.concatenate([np.asarray(r[name], dtype=np.float32) for r in R], axis=0)
    y_prompt = st("y_p")
    y_sample = cc("y_s")[:, None, :]
    new_conv_prompt = st("ncp")[None]
    new_gdn_prompt = st("ngp")[None]
    new_conv_sample = cc("ncs")[None]
    new_gdn_sample = cc("ngs")[None]
    new_sgu_v_sample = cc("nsv")[None, :, None, :]
    return (y_prompt, y_sample, new_conv_prompt, new_gdn_prompt, new_conv_sample, new_gdn_sample, new_sgu_v_sample)
```

```python
import os
import numpy as np
import concourse.bass as bass
import concourse.mybir as mybir
from concourse.bass_utils import run_bass_kernel_spmd

F32 = mybir.dt.float32
BF16 = mybir.dt.bfloat16
AF = mybir.ActivationFunctionType
ALU = mybir.AluOpType
AX = mybir.AxisListType


class Sched:
    ENGS = ("pe", "act", "dve", "pool", "sp")

    def __init__(self, nc, stack):
        self.nc = nc
        self.stack = stack
        self.streams = {e: [] for e in self.ENGS}
        self.esem = {e: stack.enter_context(nc.semaphore("c_" + e)) for e in self.ENGS[:4]}
        self.ecnt = {e: 0 for e in self.ENGS}
        self.waited = {e: {} for e in self.ENGS}
        self.res = {}
        self.dsem = {}
        self.sem_by_name = {}
        for e in self.ENGS[:4]:
            self.sem_by_name[self.esem[e].name] = self.esem[e]

    def _need(self, eng, ev, waits):
        if ev is None:
            return
        name, val, src = ev
        if src == eng and eng == "pe":
            return
        cur = waits.get(name, 0)
        if val > cur:
            waits[name] = val

    def _deps(self, eng, reads, writes):
        waits = {}
        for k in reads:
            r = self.res.get(k)
            if r is not None:
                self._need(eng, r[0], waits)
        for k in writes:
            r = self.res.get(k)
            if r is not None:
                if r[0] is not None and not (r[0][2] == eng):
                    self._need(eng, r[0], waits)
                for ev in r[1]:
                    self._need(eng, ev, waits)
        out = []
        w = self.waited[eng]
        for name, val in waits.items():
            if w.get(name, 0) < val:
                w[name] = val
                out.append((name, val))
        return out

    def _commit(self, ev, reads, writes):
        for k in reads:
            r = self.res.setdefault(k, [None, []])
            r[1].append(ev)
        for k in writes:
            self.res[k] = [ev, []]

    def op(self, eng, fn, reads=(), writes=()):
        waits = self._deps(eng, reads, writes)
        self.ecnt[eng] += 1
        ev = (self.esem[eng].name, self.ecnt[eng], eng)
        self.streams[eng].append((waits, [fn], ("inc", self.esem[eng], 1)))
        self._commit(ev, reads, writes)
        return ev

    def group(self, eng, fns, reads=(), writes=()):
        waits = self._deps(eng, reads, writes)
        self.ecnt[eng] += 1
        ev = (self.esem[eng].name, self.ecnt[eng], eng)
        self.streams[eng].append((waits, list(fns), ("inc", self.esem[eng], 1)))
        self._commit(ev, reads, writes)
        return ev

    def dma(self, eng, slot, fn, reads=(), writes=(), n=1):
        if slot not in self.dsem:
            s = self.stack.enter_context(self.nc.semaphore("d_" + slot))
            self.dsem[slot] = [s, 0]
            self.sem_by_name[s.name] = s
        waits = self._deps(eng, reads, writes)
        d = self.dsem[slot]
        fns = fn if isinstance(fn, (list, tuple)) else [fn]
        d[1] += 16 * len(fns)
        ev = (d[0].name, d[1], "dma")
        self.streams[eng].append((waits, list(fns), ("dmainc", d[0], 16)))
        self._commit(ev, reads, writes)
        return ev

    def barrier(self, skip=()):
        evs = []
        for e in self.ENGS[:4]:
            if self.ecnt[e] > 0:
                evs.append((self.esem[e].name, self.ecnt[e]))
        for slot, (s, c) in self.dsem.items():
            if c > 0 and not any(slot.startswith(p) for p in skip):
                evs.append((s.name, c))
        for eng in self.ENGS:
            w = self.waited[eng]
            waits = []
            for name, val in evs:
                if w.get(name, 0) < val:
                    w[name] = val
                    waits.append((name, val))
            if waits:
                self.streams[eng].append((waits, [], None))
        self.res.clear()

    def finish(self):
        eng = "sp"
        waits = []
        for slot, (s, c) in self.dsem.items():
            if c > 0:
                waits.append((s.name, c))
        for e in self.ENGS[:4]:
            if self.ecnt[e] > 0:
                waits.append((self.esem[e].name, self.ecnt[e]))
        self.streams[eng].append((waits, [], None))

    def replay(self, block):
        sbn = self.sem_by_name

        def run(e, items):
            for waits, fns, inc in items:
                for name, val in waits:
                    e.wait_ge(sbn[name], val)
                last = None
                for i, f in enumerate(fns):
                    ins = f(e)
                    if inc is not None and inc[0] == "dmainc":
                        ins.then_inc(inc[1], 16)
                    last = ins
                if inc is not None and inc[0] == "inc" and last is not None:
                    last.then_inc(inc[1], 1)

        st = self.streams

        @block.tensor
        def _(e):
            run(e, st["pe"])

        @block.scalar
        def _(e):
            run(e, st["act"])

        @block.vector
        def _(e):
            run(e, st["dve"])

        @block.gpsimd
        def _(e):
            run(e, st["pool"])

        @block.sync
        def _(e):
            run(e, st["sp"])


U8 = mybir.dt.uint8
T_P = 2048
T_S = 16
T_ALL = T_P + T_S
SEGS = [(0, 512), (512, 512), (1024, 512), (1536, 512), (2048, 16)]
NT = 17
EPS = 1e-6
D_IN = 3080
C_Q, C_K, C_V, C_Z, C_BA, C_U, C_VS = 0, 512, 1024, 1536, 2048, 2056, 2568


def mm(out, lhsT, rhs, start=True, stop=True):
    return lambda e: e.matmul(out, lhsT=lhsT, rhs=rhs, start=start, stop=stop)


def actf(out, in_, func, **kw):
    return lambda e: e.activation(out=out, in_=in_, func=func, **kw)


def tt(out, a, b, op):
    return lambda e: e.tensor_tensor(out=out, in0=a, in1=b, op=op)


def ts(out, a, s1, op0, s2=None, op1=None):
    if op1 is None:
        return lambda e: e.tensor_scalar(out=out, in0=a, scalar1=s1, scalar2=None, op0=op0)
    return lambda e: e.tensor_scalar(out=out, in0=a, scalar1=s1, scalar2=s2, op0=op0, op1=op1)


def stt(out, a, s, b, op0, op1):
    return lambda e: e.scalar_tensor_tensor(out=out, in0=a, scalar=s, in1=b, op0=op0, op1=op1)


def stt_pool(out, a, colap):
    return lambda e: e.tensor_tensor(out=out, in0=a, in1=colap.to_broadcast([128, 128]), op=ALU.mult)


def cp(out, in_):
    return lambda e: e.tensor_copy(out=out, in_=in_)


def dmaf(out, in_):
    return lambda e: e.dma_start(out=out, in_=in_)


class _Item:
    __slots__ = ("thunk", "eng", "reads", "writes", "dur")

    def __init__(self, thunk, eng, reads, writes, dur):
        self.thunk, self.eng, self.reads, self.writes, self.dur = thunk, eng, tuple(reads), tuple(writes), dur


_DUR = {"act": 0.5, "dve": 0.45, "pool": 0.5}


def mk_recorders(S, ops):
    def Lop(eng, fn, reads=(), writes=()):
        ops.append(_Item(lambda: S.op(eng, fn, reads=reads, writes=writes), eng, reads, writes, _DUR.get(eng, 0.4)))

    def Lgr(eng, fns, reads=(), writes=()):
        ops.append(_Item(lambda: S.group(eng, fns, reads=reads, writes=writes), eng, reads, writes, 0.1 + 0.13 * len(fns)))

    def Ldma(eng, slot, fn, reads=(), writes=()):
        ops.append(_Item(lambda: S.dma(eng, slot, fn, reads=reads, writes=writes), "q_" + eng, reads, writes, 2.5))
    return Lop, Lgr, Ldma


def zipper(lists):
    lists = [l for l in lists if l]
    idx = [0] * len(lists)
    if os.environ.get("ZIP", "rr") == "rr":
        live = True
        while live:
            live = False
            for i, l in enumerate(lists):
                if idx[i] < len(l):
                    it = l[idx[i]]
                    idx[i] += 1
                    live = True
                    if isinstance(it, _Item):
                        it.thunk()
                    else:
                        it()
        return
    t_eng, t_w, t_r = {}, {}, {}
    remaining = sum(len(l) for l in lists)
    while remaining:
        best = None
        for i, l in enumerate(lists):
            if idx[i] >= len(l):
                continue
            it = l[idx[i]]
            if not isinstance(it, _Item):
                best = (-1.0, i, it)
                break
            rdy = t_eng.get(it.eng, 0.0)
            for k in it.reads:
                rdy = max(rdy, t_w.get(k, 0.0))
            for k in it.writes:
                rdy = max(rdy, t_w.get(k, 0.0), t_r.get(k, 0.0))
            if best is None or rdy < best[0]:
                best = (rdy, i, it)
        rdy, i, it = best
        idx[i] += 1
        remaining -= 1
        if not isinstance(it, _Item):
            it()
            continue
        it.thunk()
        fin = rdy + it.dur
        if it.eng.startswith("q_"):
            t_eng[it.eng] = rdy + 0.1
        else:
            t_eng[it.eng] = fin
        for k in it.reads:
            t_r[k] = max(t_r.get(k, 0.0), fin)
        for k in it.writes:
            t_w[k] = fin
            t_r[k] = 0.0


class Bump:
    def __init__(self, arena, start, limit):
        self.t, self.off, self.limit = arena, start, limit

    def alloc(self, dtype, shape):
        esz = 4 if dtype == F32 else 2
        n = 1
        for s in shape[1:]:
            n *= s
        nb = (n * esz + 63) // 64 * 64
        o = self.off
        self.off += nb
        assert self.off <= self.limit, ("SBUF arena overflow", self.off, self.limit)
        ap = self.t[:, o:o + n * esz].bitcast(dtype)
        if len(shape) == 3:
            ap = ap.rearrange("p (a b) -> p a b", a=shape[1])
        elif len(shape) == 4:
            ap = ap.rearrange("p (a b c) -> p a b c", a=shape[1], b=shape[2])
        return ap


def build_program():
    from contextlib import ExitStack
    nc = bass.Bass("TRN2", target_bir_lowering=False)

    def din(name, shape):
        return nc.dram_tensor(name, shape, F32, kind="ExternalInput").ap()

    def dout(name, shape):
        return nc.dram_tensor(name, shape, F32, kind="ExternalOutput").ap()

    x_p = din("x_p", [T_P, 1024]); x_s = din("x_s", [T_S, 1024])
    st_conv = din("st_conv", [T_S, 3, 1536]); st_gdn = din("st_gdn", [T_S, 4, 128, 128])
    p_p = din("p_p", [T_P, 256]); p_s = din("p_s", [T_S, 256])
    g_mix = din("g_mix", [1, 1024]); w_in = din("w_in", [1024, D_IN]); w_conv = din("w_conv", [4, 1536])
    a_log = din("a_log", [1, 4]); dt_bias = din("dt_bias", [1, 4]); gdn_norm = din("gdn_norm", [1, 128])
    ln_g = din("ln_g", [1, 512]); ln_b = din("ln_b", [1, 512]); w_s = din("w_s", [4, 128, 128]); b_s = din("b_s", [1, 512])
    w_out = din("w_out", [1024, 1024]); g_ff = din("g_ff", [8, 128]); w_up = din("w_up", [1024, 4096]); w_down = din("w_down", [4096, 1024])
    g_ple = din("g_ple", [8, 128]); w_ple = din("w_ple", [256, 1024]); w_gate = din("w_gate", [1024, 1024]); g_fin = din("g_fin", [8, 128])
    y_p = dout("y_p", [T_P, 1024]); y_s = dout("y_s", [T_S, 1024])
    ncp = dout("ncp", [3, 1536]); ngp = dout("ngp", [4, 128, 128])
    ncs = dout("ncs", [T_S, 3, 1536]); ngs = dout("ngs", [T_S, 4, 128, 128]); nsv = dout("nsv", [T_S, 512])

    w_in_v = w_in.rearrange("(k p) c -> p k c", p=128)
    w_out_v = w_out.rearrange("(k p) c -> p k c", p=128)
    w_up_v = w_up.rearrange("(k p) c -> p k c", p=128)
    w_down_v = w_down.rearrange("(k p) c -> p k c", p=128)
    w_gate_v = w_gate.rearrange("(k p) c -> p k c", p=128)
    w_ple_v = w_ple.rearrange("(k p) c -> p k c", p=128)

    with ExitStack() as st:
        S = Sched(nc, st)
        ARENA = 206 * 1024
        arena = st.enter_context(nc.sbuf_tensor("arena", [128, ARENA], U8))
        ps = [st.enter_context(nc.psum_tensor("ps%d" % i, [128, 512], F32)) for i in range(8)]
        psb = [p[:, :].bitcast(BF16) for p in ps]
        PK = [("ps", i) for i in range(8)]

        P = Bump(arena, 0, ARENA)
        ident_f = P.alloc(F32, [128, 128]); ident_b = P.alloc(BF16, [128, 128])
        ones_f = P.alloc(F32, [128, 128]); ones_b = P.alloc(BF16, [128, 128])
        mask_incl = P.alloc(F32, [128, 128])
        mask_su = P.alloc(F32, [128, 128])
        nmask_sl = P.alloc(F32, [128, 128])
        sel127 = P.alloc(F32, [128, 128])
        nmask_su = P.alloc(F32, [128, 128])
        rowstage = P.alloc(F32, [128, 128])
        cols = P.alloc(F32, [128, 128])
        wsT = P.alloc(BF16, [128, 4, 128])
        selws = P.alloc(BF16, [128, 4, 16])
        ws00 = P.alloc(F32, [128, 4])
        bs_row = P.alloc(F32, [128, 4, 128])
        lng_row = P.alloc(F32, [128, 512]); lnb_row = P.alloc(F32, [128, 512])
        alog_row = P.alloc(F32, [128, 4]); dtb_row = P.alloc(F32, [128, 4]); nexpA_row = P.alloc(F32, [128, 4])
        zcol = P.alloc(F32, [128, 4])
        zeros_f = P.alloc(F32, [128, 128])
        cat = P.alloc(BF16, [128, 8, T_ALL])
        P_C0 = P.off
        ba = P.alloc(F32, [128, NT, 8])
        beta_c = P.alloc(F32, [128, NT, 4]); g_c = P.alloc(F32, [128, NT, 4]); gc_c = P.alloc(F32, [128, NT, 4])
        bexp_c = P.alloc(F32, [128, NT, 4]); kd_c = P.alloc(F32, [128, NT, 4]); egl_c = P.alloc(F32, [128, NT, 4])
        tmp68 = P.alloc(F32, [128, NT, 4])
        qkv = P.alloc(BF16, [128, 12, T_ALL])
        zs = P.alloc(BF16, [128, 4, T_ALL])
        qks_f = P.alloc(F32, [128, 12, 16])
        histT = P.alloc(F32, [128, 12, 3, 16])
        ncp_st = P.alloc(F32, [128, 12, 3]); ncs_st = P.alloc(F32, [128, 12, 16])
        S_f = P.alloc(F32, [128, 4, 128]); S_b = P.alloc(BF16, [128, 4, 128])
        X0 = P.off
        XB = Bump(arena, X0, ARENA)
        hT = XB.alloc(BF16, [128, 8, T_ALL])
        Y0 = XB.off

        def wcol(j, c):
            return cols[:, 24 + j * 12 + c: 24 + j * 12 + c + 1]

        def gcol(which, m):
            return cols[:, which * 8 + m: which * 8 + m + 1]
        gdnn_col = cols[:, 72:73]
        negone = zcol[:, 1:2]

        S.op("pool", lambda e: e.memset(ones_f, 1.0), writes=["ones_f"])
        S.op("pool", lambda e: e.memset(ones_b, 1.0), writes=["ones_b"])
        S.op("pool", lambda e: e.memset(zcol, 0.0), writes=["zcol"])
        S.op("pool", lambda e: e.memset(zcol[:, 1:2], -1.0), reads=["zcol"], writes=["zcol"])
        S.op("pool", lambda e: e.memset(zeros_f, 0.0), writes=["zeros_f"])
        S.op("pool", lambda e: e.affine_select(out=ident_f, in_=ones_f, pattern=[[-1, 128]], compare_op=ALU.is_equal, fill=0.0, base=0, channel_multiplier=1), reads=["ones_f"], writes=["ident_f"])
        S.op("pool", lambda e: e.affine_select(out=mask_incl, in_=ones_f, pattern=[[1, 128]], compare_op=ALU.is_ge, fill=0.0, base=0, channel_multiplier=-1), reads=["ones_f"], writes=["mask_incl"])
        S.op("pool", lambda e: e.affine_select(out=mask_su, in_=ones_f, pattern=[[1, 128]], compare_op=ALU.is_gt, fill=0.0, base=0, channel_multiplier=-1), reads=["ones_f"], writes=["mask_su"])
        S.op("pool", lambda e: e.affine_select(out=nmask_sl, in_=ones_f, pattern=[[-1, 128]], compare_op=ALU.is_gt, fill=0.0, base=0, channel_multiplier=1), reads=["ones_f"], writes=["nmask_sl"])
        S.op("pool", ts(nmask_sl, nmask_sl, -1.0, ALU.mult), reads=["nmask_sl"], writes=["nmask_sl"])
        S.op("pool", ts(nmask_su, mask_su, -1.0, ALU.mult), reads=["mask_su"], writes=["nmask_su"])
        S.op("pool", lambda e: e.affine_select(out=sel127, in_=ones_f, pattern=[[0, 128]], compare_op=ALU.is_equal, fill=0.0, base=-127, channel_multiplier=1), reads=["ones_f"], writes=["sel127"])
        S.op("dve", cp(ident_b, ident_f), reads=["ident_f"], writes=["ident_b"])
        S.op("pool", lambda e: e.memset(rowstage, 0.0), writes=["rowstage"])
        S.dma("sp", "c0", [dmaf(rowstage[0:8, :], g_ff), dmaf(rowstage[8:16, :], g_ple), dmaf(rowstage[16:24, :], g_fin),
                           dmaf(rowstage[24:72, :], w_conv.rearrange("j (c p) -> (j c) p", p=128)), dmaf(rowstage[72:73, :], gdn_norm)],
              writes=["rowstage"])
        S.group("pe", [mm(ps[0][:, 0:128], rowstage, ident_f)], reads=["rowstage", "ident_f"], writes=[PK[0]])
        S.op("dve", cp(cols, ps[0][:, 0:128]), reads=[PK[0]], writes=["cols"])
        S.dma("sp", "c1", [dmaf(bs_row.rearrange("p h t -> p (h t)"), b_s.partition_broadcast(128)),
                           dmaf(lng_row, ln_g.partition_broadcast(128)), dmaf(lnb_row, ln_b.partition_broadcast(128)),
                           dmaf(alog_row, a_log.partition_broadcast(128)), dmaf(dtb_row, dt_bias.partition_broadcast(128)),
                           ] + [dmaf(ws00[:, h:h + 1], w_s[h, 0, 0:1].partition_broadcast(128)) for h in range(4)],
              writes=["rows"])
        S.op("act", actf(nexpA_row, alog_row, AF.Exp), reads=["rows"], writes=["nexpA"])
        S.op("dve", ts(nexpA_row, nexpA_row, -1.0, ALU.mult), reads=["nexpA"], writes=["nexpA"])
        for h in range(4):
            S.op("dve", ts(selws[0:16, h, :], ident_f[0:16, 0:16], ws00[0:16, h:h + 1], ALU.mult), reads=["rows", "ident_f"], writes=[("selws", h)])

        YA = Bump(arena, Y0, ARENA)
        wstmp = YA.alloc(F32, [128, 4, 128])
        S.dma("sp", "c2", dmaf(wstmp, w_s.rearrange("h t s -> t h s")), writes=["wstmp"])
        for h in range(4):
            S.op("pool", lambda e, h=h: e.affine_select(out=wstmp[:, h, :], in_=wstmp[:, h, :], pattern=[[-1, 128]], compare_op=ALU.is_ge, fill=0.0, base=0, channel_multiplier=1),
                 reads=["wstmp"], writes=["wstmp"])
        S.group("pe", [mm(ps[1][:, h * 128:(h + 1) * 128], wstmp[:, h, :], ident_f) for h in range(4)], reads=["wstmp", "ident_f"], writes=[PK[1]])
        S.op("dve", cp(wsT.rearrange("p h t -> p (h t)"), ps[1][:, 0:512]), reads=[PK[1]], writes=["wsT"])

        gmix_row = YA.alloc(F32, [128, 1024])
        S.dma("sp", "c3", dmaf(gmix_row, g_mix.partition_broadcast(128)), writes=["gmix"])
        xt = [YA.alloc(F32, [128, 1024]) for _ in range(3)]
        xsq = [YA.alloc(F32, [128, 1024]) for _ in range(3)]
        xn = [YA.alloc(BF16, [128, 1024]) for _ in range(3)]
        stat = YA.alloc(F32, [128, NT, 2])

        def phaseA_tile(i):
            ops = []
            Lop, Lgr, Ldma = mk_recorders(S, ops)
            r = 128 if i < 16 else 16
            sl = i % 3
            src = x_p[i * 128:(i + 1) * 128, :] if i < 16 else x_s
            Ldma("sp", "xt%d" % sl, dmaf(xt[sl][0:r, :], src), writes=[("xt", sl)])
            Lop("act", actf(xsq[sl][0:r, :], xt[sl][0:r, :], AF.Square), reads=[("xt", sl)], writes=[("xsq", sl)])
            Lop("dve", lambda e: e.reduce_sum(out=stat[0:r, i, 0:1], in_=xsq[sl][0:r, :], axis=AX.X), reads=[("xsq", sl)], writes=[("stat", i)])
            Lop("dve", ts(stat[0:r, i, 1:2], stat[0:r, i, 0:1], 1.0 / 1024, ALU.mult, EPS, ALU.add), reads=[("stat", i)], writes=[("stat", i)])
            Lop("act", actf(stat[0:r, i, 1:2], stat[0:r, i, 1:2], AF.Sqrt), reads=[("stat", i)], writes=[("stat", i)])
            Lop("dve", lambda e: e.reciprocal(out=stat[0:r, i, 1:2], in_=stat[0:r, i, 1:2]), reads=[("stat", i)], writes=[("stat", i)])
            Lop("dve", stt(xn[sl][0:r, :], xt[sl][0:r, :], stat[0:r, i, 1:2], gmix_row[0:r, :], ALU.mult, ALU.mult),
                reads=[("xt", sl), ("stat", i), "gmix"], writes=[("xn", sl)])
            b = i % 3
            Lgr("pe", [lambda e, k=k: e.transpose(out=psb[b][:, k * 128:k * 128 + r], in_=xn[sl][0:r, k * 128:(k + 1) * 128], identity=ident_b[0:r, 0:r]) for k in range(8)],
                reads=[("xn", sl), "ident_b"], writes=[PK[b]])
            if i % 2 == 0:
                Lop("act", actf(hT[:, :, i * 128:i * 128 + r], psb[b].rearrange("p (k t) -> p k t", k=8)[:, :, 0:r], AF.Copy), reads=[PK[b]], writes=[("hT", i)])
            else:
                Lop("dve", cp(hT[:, :, i * 128:i * 128 + r], psb[b].rearrange("p (k t) -> p k t", k=8)[:, :, 0:r]), reads=[PK[b]], writes=[("hT", i)])
            return ops

        tilesA = [phaseA_tile(i) for i in range(NT)]
        for g0 in range(0, NT, 3):
            zipper(tilesA[g0:g0 + 3])
        S.barrier()

        YB = Bump(arena, Y0, ARENA)
        wb = [YB.alloc(BF16, [128, 8, 512]) for _ in range(2)]
        wb8 = YB.alloc(BF16, [128, 8, 8])
        lnst = YB.alloc(F32, [128, NT, 8])
        sct = [YB.alloc(F32, [128, 3, 128]) for _ in range(2)]
        YB_MID = YB.off
        vg = YB.alloc(BF16, [128, NT, 512])
        F6 = YB.alloc(F32, [128, 6, 512])
        f512 = [F6[:, i, :] for i in range(6)]
        vgs_f = f512[5]
        YB5 = Bump(arena, YB_MID, ARENA)
        NBS = 4
        pre = [YB5.alloc(F32, [128, 515]) for _ in range(NBS)]
        accb = [YB5.alloc(F32, [128, 512]) for _ in range(NBS)]
        rnb = [YB5.alloc(F32, [128, 512]) for _ in range(NBS)]
        sqb = [YB5.alloc(BF16, [128, 512]) for _ in range(NBS)]
        wdiag = [YB5.alloc(F32, [128, 4, 128]) for _ in range(2)]
        YB6 = Bump(arena, YB_MID, ARENA)
        stage_tok = YB6.alloc(F32, [128, 1536])
        stage2 = YB6.alloc(F32, [128, 1536])
        wb_n = [0]
        bank_n = [0]

        def next_bank(lo=0, hi=4):
            b = lo + bank_n[0] % (hi - lo)
            bank_n[0] += 1
            return b

        wb_seq = [C_VS, C_U, C_Z, 0, 512, 1024]
        wb_issued = [0]

        def load_w(view, c0, ncol=512):
            idx = wb_n[0]
            wb_n[0] += 1
            assert wb_seq[idx] == c0
            while wb_issued[0] < min(len(wb_seq), idx + 2):
                j = wb_issued[0]
                S.dma("pool", "wb%d" % (j % 2), dmaf(wb[j % 2][:, :, 0:512], view[:, :, wb_seq[j]:wb_seq[j] + 512]), writes=[("wb", j % 2)])
                wb_issued[0] += 1
            return idx % 2

        hT_keys = [("hT", i) for i in range(NT)]

        def seg_hT_keys(t0, n):
            return [("hT", i) for i in range(t0 // 128, (t0 + n + 127) // 128)]

        sl = load_w(w_in_v, C_VS)

        def vsgu_tile(i):
            ops = []
            Lop, Lgr, Ldma = mk_recorders(S, ops)
            r = 128 if i < 16 else 16
            b = (i % 3)
            Lgr("pe", [mm(ps[b][0:r, :], hT[:, k, i * 128:i * 128 + r], wb[sl][:, k, :], start=(k == 0), stop=(k == 7)) for k in range(8)],
                    reads=[("hT", i), ("wb", sl)], writes=[PK[b]])
            g1 = f512[i % 3]; g2 = f512[3 + i % 3]
            Lop("act", actf(g1[0:r, :], ps[b][0:r, :], AF.Gelu_apprx_tanh), reads=[PK[b]], writes=[("g1", i % 3)])
            Lop("pool", tt(g2[0:r, :], g1[0:r, :], g1[0:r, :], ALU.mult), reads=[("g1", i % 3)], writes=[("g2", i % 3)])
            Lop("dve", lambda e, i=i, r=r, g1=g1: e.reduce_sum(out=lnst[0:r, i, 0:1], in_=g1[0:r, :], axis=AX.X), reads=[("g1", i % 3)], writes=[("lnst", i)])
            Lop("dve", lambda e, i=i, r=r, g2=g2: e.reduce_sum(out=lnst[0:r, i, 1:2], in_=g2[0:r, :], axis=AX.X), reads=[("g2", i % 3)], writes=[("lnst", i)])
            L = lambda a, bb: lnst[0:r, i, a:bb]
            Lop("dve", ts(L(2, 3), L(0, 1), 1.0 / 512, ALU.mult), reads=[("lnst", i)], writes=[("lnst", i)])
            Lop("dve", tt(L(3, 4), L(2, 3), L(2, 3), ALU.mult), reads=[("lnst", i)], writes=[("lnst", i)])
            Lop("dve", stt(L(4, 5), L(1, 2), 1.0 / 512, L(3, 4), ALU.mult, ALU.subtract), reads=[("lnst", i)], writes=[("lnst", i)])
            Lop("dve", ts(L(4, 5), L(4, 5), EPS, ALU.add), reads=[("lnst", i)], writes=[("lnst", i)])
            Lop("act", actf(L(4, 5), L(4, 5), AF.Sqrt), reads=[("lnst", i)], writes=[("lnst", i)])
            Lop("dve", lambda e, i=i, r=r: e.reciprocal(out=lnst[0:r, i, 5:6], in_=lnst[0:r, i, 4:5]), reads=[("lnst", i)], writes=[("lnst", i)])
            Lop("dve", ts(g2[0:r, :], g1[0:r, :], L(2, 3), ALU.subtract, L(5, 6), ALU.mult), reads=[("g1", i % 3), ("lnst", i)], writes=[("g2", i % 3)])
            Lop("pool", tt(g2[0:r, :], g2[0:r, :], lng_row[0:r, :], ALU.mult), reads=[("g2", i % 3), "rows"], writes=[("g2", i % 3)])
            if i < 16:
                Lop("pool", tt(vg[0:r, i, :], g2[0:r, :], lnb_row[0:r, :], ALU.add), reads=[("g2", i % 3), "rows"], writes=[("vg", i)])
            else:
                Lop("pool", tt(vgs_f[0:r, :], g2[0:r, :], lnb_row[0:r, :], ALU.add), reads=[("g2", i % 3), "rows"], writes=[("g2", 2)])
                Lop("pool", cp(vg[0:r, i, :], vgs_f[0:r, :]), reads=[("g2", 2)], writes=[("vg", i)])
                Ldma("sp", "o_nsv", dmaf(nsv, vgs_f[0:r, :]), reads=[("g2", 2)])
            return ops

        tilesV = [vsgu_tile(i) for i in range(NT)]
        for g0 in range(0, NT, 3):
            zipper(tilesV[g0:g0 + 3])

        S.dma("pool", "wb8", dmaf(wb8, w_in_v[:, :, C_BA:C_BA + 8]), writes=["wb8"])
        S.op("pool", lambda e: e.memset(ba, 0.0), writes=["ba"])
        bq = 4
        for i in range(NT):
            r = 128 if i < 16 else 16
            S.group("pe", [mm(ps[bq][0:r, i * 8:(i + 1) * 8], hT[:, k, i * 128:i * 128 + r], wb8[:, k, :], start=(k == 0), stop=(k == 7)) for k in range(8)],
                    reads=[("hT", i), "wb8"], writes=[PK[bq]])
        S.op("dve", cp(ba[:, 0:16, :], ps[bq][:, 0:128].rearrange("p (i c) -> p i c", c=8)), reads=[PK[bq], "ba"], writes=["ba"])
        S.op("dve", cp(ba[0:16, 16, :], ps[bq][0:16, 128:136]), reads=[PK[bq], "ba"], writes=["ba"])
        S.op("act", actf(beta_c, ba[:, :, 0:4], AF.Sigmoid), reads=["ba"], writes=["beta_c"])
        S.op("dve", tt(tmp68, ba[:, :, 4:8], dtb_row.unsqueeze(1).to_broadcast([128, NT, 4]), ALU.add), reads=["ba", "rows"], writes=["tmp68"])
        S.op("act", actf(tmp68, tmp68, AF.Exp), reads=["tmp68"], writes=["tmp68"])
        S.op("act", actf(tmp68, tmp68, AF.Ln, bias=1.0), reads=["tmp68"], writes=["tmp68"])
        S.op("dve", tt(g_c, tmp68, nexpA_row.unsqueeze(1).to_broadcast([128, NT, 4]), ALU.mult), reads=["tmp68", "nexpA"], writes=["g_c"])
        g68 = g_c.rearrange("p i h -> p (i h)"); gc68 = gc_c.rearrange("p i h -> p (i h)")
        S.group("pe", [mm(ps[5][:, 0:68], mask_incl, g68)], reads=["mask_incl", "g_c"], writes=[PK[5]])
        S.op("dve", cp(gc68, ps[5][:, 0:68]), reads=[PK[5]], writes=["gc_c"])
        S.group("pe", [mm(ps[5][:, 128:196], sel127, gc68)], reads=["sel127", "gc_c"], writes=[PK[5]])
        S.op("dve", cp(egl_c.rearrange("p i h -> p (i h)"), ps[5][:, 128:196]), reads=[PK[5]], writes=["egl_c"])
        S.op("dve", tt(tmp68.rearrange("p i h -> p (i h)"), egl_c.rearrange("p i h -> p (i h)"), gc68, ALU.subtract), reads=["egl_c", "gc_c"], writes=["tmp68"])
        S.op("act", actf(egl_c, egl_c, AF.Exp), reads=["egl_c", "tmp68"], writes=["egl_c"])
        S.op("act", actf(kd_c, tmp68, AF.Exp), reads=["tmp68"], writes=["kd_c"])
        S.op("act", actf(bexp_c, gc_c, AF.Exp), reads=["gc_c"], writes=["bexp_c"])
        S.op("dve", tt(bexp_c, bexp_c, beta_c, ALU.mult), reads=["bexp_c", "beta_c"], writes=["bexp_c"])

        sl = load_w(w_in_v, C_U)
        for h in range(4):
            for (t0, n) in SEGS:
                b = next_bank()
                S.group("pe", [mm(ps[b][:, 0:n], wb[sl][:, k, h * 128:(h + 1) * 128], hT[:, k, t0:t0 + n], start=(k == 0), stop=(k == 7)) for k in range(8)],
                        reads=seg_hT_keys(t0, n) + [("wb", sl)], writes=[PK[b]])
                u = f512[bank_n[0] % 2]
                S.op("act", actf(u[:, 0:n], ps[b][:, 0:n], AF.Gelu_apprx_tanh), reads=[PK[b]], writes=[("u", bank_n[0] % 2)])
                b2 = 4 + bank_n[0] % 2
                if n == 512:
                    tiles = [t0 // 128 + j for j in range(4)]
                    S.group("pe", [mm(ps[b2][:, j * 128:(j + 1) * 128], vg[:, tiles[j], h * 128:(h + 1) * 128], wsT[:, h, :]) for j in range(4)],
                            reads=[("vg", ti) for ti in tiles] + ["wsT"], writes=[PK[b2]])
                    S.op("dve", tt(f512[4][:, :].rearrange("p (j t) -> p j t", j=4), ps[b2][:, :].rearrange("p (j t) -> p j t", j=4),
                                   bs_row[:, h:h + 1, :].to_broadcast([128, 4, 128]), ALU.add), reads=[PK[b2], "rows"], writes=["mixt"])
                else:
                    S.group("pe", [mm(ps[b2][:, 0:16], vg[0:16, 16, h * 128:(h + 1) * 128], selws[0:16, h, :])],
                            reads=[("vg", 16), ("selws", h)], writes=[PK[b2]])
                    S.op("dve", tt(f512[4][:, 0:16], ps[b2][:, 0:16], bs_row[:, h, 0:1].to_broadcast([128, 16]), ALU.add), reads=[PK[b2], "rows"], writes=["mixt"])
                S.op("pool", tt(cat[:, 4 + h, t0:t0 + n], f512[4][:, 0:n], u[:, 0:n], ALU.mult), reads=["mixt", ("u", bank_n[0] % 2)], writes=[("cat", 4 + h, t0)])

        sl = load_w(w_in_v, C_Z)
        for h in range(4):
            for (t0, n) in SEGS:
                b = next_bank()
                S.group("pe", [mm(ps[b][:, 0:n], wb[sl][:, k, h * 128:(h + 1) * 128], hT[:, k, t0:t0 + n], start=(k == 0), stop=(k == 7)) for k in range(8)],
                        reads=seg_hT_keys(t0, n) + [("wb", sl)], writes=[PK[b]])
                S.op("act", actf(zs[:, h, t0:t0 + n], ps[b][:, 0:n], AF.Silu), reads=[PK[b]], writes=[("zs", h, t0)])

        S.barrier()

        def qkv_unit(blk, h, si, ui, wsl):
            ops = []
            Lop, Lgr, Ldma = mk_recorders(S, ops)
            c = blk * 4 + h
            t0, n = SEGS[si]
            bs = ui % NBS
            pr, acc, sq, rn = pre[bs], accb[bs], sqb[bs], rnb[bs]
            kp, ka, ks, kr = ("pre", bs), ("acc", bs), ("sq", bs), ("rn", bs)
            b = ui % 4; bn = 4 + ui % 4
            Lgr("pe", [mm(ps[b][:, 0:n], wb[wsl][:, k, h * 128:(h + 1) * 128], hT[:, k, t0:t0 + n], start=(k == 0), stop=(k == 7)) for k in range(8)],
                reads=[("wb", wsl)], writes=[PK[b]])
            if si == 0:
                Lop("dve", lambda e: e.memset(pr[:, 0:3], 0.0), writes=[kp])
            elif si < 4:
                Lgr("pe", [mm(ps[bn][:, 0:3], wb[wsl][:, k, h * 128:(h + 1) * 128], hT[:, k, t0 - 3:t0], start=(k == 0), stop=(k == 7)) for k in range(8)],
                    reads=[("wb", wsl)], writes=[PK[bn]])
                Lop("dve", cp(pr[:, 0:3], ps[bn][:, 0:3]), reads=[PK[bn]], writes=[kp])
            Lop("act", actf(pr[:, 3:3 + n], ps[b][:, 0:n], AF.Copy), reads=[PK[b], kp], writes=[kp])
            if si == 3:
                Lop("dve", cp(ncp_st[:, c, :], pr[:, 512:515]), reads=[kp], writes=[("ncp_st", c)])
            if si < 4:
                wd = wdiag[c % 2]
                bc_ = 4 + ui % 4
                fns = []
                for t4 in range(4):
                    for j in range(4):
                        fns.append(mm(ps[bc_][:, t4 * 128:(t4 + 1) * 128], wd[:, j, :], pr[:, j + t4 * 128:j + (t4 + 1) * 128], start=(j == 0), stop=(j == 3)))
                Lgr("pe", fns, reads=[kp, ("wdiag", c % 2)], writes=[PK[bc_]])
                Lop("act", actf(acc[:, 0:n], ps[bc_][:, 0:n], AF.Silu), reads=[PK[bc_]], writes=[ka])
            else:
                Lop("dve", cp(ncs_st[:, c, :], pr[:, 3:19]), reads=[kp], writes=[("ncs_st", c)])
                Lop("act", actf(acc[:, 0:n], pr[:, 3:3 + n], AF.Copy, scale=wcol(3, c)), reads=[kp, "cols"], writes=[ka])
                for j in (2, 1, 0):
                    Lop("dve", stt(acc[:, 0:n], histT[:, c, j, :], wcol(j, c), acc[:, 0:n], ALU.mult, ALU.add), reads=[("histT", c), ka, "cols"], writes=[ka])
                Lop("act", actf(acc[:, 0:n], acc[:, 0:n], AF.Silu), reads=[ka], writes=[ka])
            if blk == 2:
                Lop("pool", cp(qkv[:, c, t0:t0 + n], acc[:, 0:n]), reads=[ka], writes=[("qkv", c, t0)])
                if si == 4:
                    Lop("pool", cp(qks_f[:, c, :], acc[:, 0:16]), reads=[ka], writes=[("qks_f", c)])
            else:
                Lop("pool", tt(sq[:, 0:n], acc[:, 0:n], acc[:, 0:n], ALU.mult), reads=[ka], writes=[ks])
                Lgr("pe", [mm(ps[bn][:, 0:n], ones_b, sq[:, 0:n])], reads=[ks, "ones_b"], writes=[PK[bn]])
                Lop("act", actf(rn[:, 0:n], ps[bn][:, 0:n], AF.Sqrt, bias=1e-6), reads=[PK[bn]], writes=[kr])
                Lop("dve", lambda e: e.reciprocal(out=rn[:, 0:n], in_=rn[:, 0:n]), reads=[kr], writes=[kr])
                scl = (128.0 ** -0.5) if blk == 0 else 1.0
                Lop("dve", stt(qkv[:, c, t0:t0 + n], acc[:, 0:n], scl, rn[:, 0:n], ALU.mult, ALU.mult), reads=[ka, kr], writes=[("qkv", c, t0)])
                if si == 4:
                    Lop("dve", stt(qks_f[:, c, :], acc[:, 0:16], scl, rn[:, 0:16], ALU.mult, ALU.mult), reads=[ka, kr], writes=[("qks_f", c)])
            return ops

        ui = 0
        for blk in range(3):
            wsl = load_w(w_in_v, blk * 512)
            units = []
            for h in range(4):
                c = blk * 4 + h
                scs = sct[c % 2]
                S.dma("sp", "sct%d" % (c % 2), dmaf(scs[0:16, :, :], st_conv[:, :, c * 128:(c + 1) * 128]), writes=[("sct", c % 2)])
                S.group("pe", [mm(ps[4 + c % 4][:, j * 16:(j + 1) * 16], scs[0:16, j, :], ident_f[0:16, 0:16]) for j in range(3)],
                        reads=[("sct", c % 2), "ident_f"], writes=[PK[4 + c % 4]])
                S.op("dve", cp(histT[:, c, :, :], ps[4 + c % 4][:, 0:48].rearrange("p (j b) -> p j b", j=3)), reads=[PK[4 + c % 4]], writes=[("histT", c)])
                for si in range(5):
                    uo = qkv_unit(blk, h, si, ui, wsl)
                    if si == 0:
                        pre_ops = [(lambda j=j, c=c: S.op("pool", stt_pool(wdiag[c % 2][:, j, :], ident_f, wcol(j, c)), reads=["ident_f", "cols"], writes=[("wdiag", c % 2)])) for j in range(4)]
                        uo = pre_ops + uo
                    units.append(uo)
                    ui += 1
            for g0 in range(0, len(units), 4):
                zipper(units[g0:g0 + 4])

        S.barrier()
        XS = Bump(arena, X0, ARENA)
        S_all = XS.alloc(F32, [128, 64, 128])
        if 'sample' not in os.environ.get('KSKIP', ''):
            for q4 in range(4):
                S.dma("sp", "sall%d" % q4, dmaf(S_all[:, q4 * 16:(q4 + 1) * 16, :], st_gdn[q4 * 4:(q4 + 1) * 4].rearrange("b h d e -> d (b h) e")), writes=[("S_all", q4)])
        S.group("pe", [mm(ps[c // 4][0:3, (c % 4) * 128:(c % 4 + 1) * 128], ncp_st[:, c, :], ident_f) for c in range(12)],
                reads=[("ncp_st", c) for c in range(12)] + ["ident_f"], writes=[PK[0], PK[1], PK[2]])
        for q3 in range(3):
            S.op("dve", cp(stage_tok[0:3, q3 * 512:(q3 + 1) * 512], ps[q3][0:3, :]), reads=[PK[q3]], writes=["stage_tok"])
        S.dma("sp", "o_ncp", dmaf(ncp, stage_tok[0:3, :]), reads=["stage_tok"])
        S.group("pe", [mm(ps[c // 4][0:16, (c % 4) * 128:(c % 4 + 1) * 128], ncs_st[:, c, :], ident_f) for c in range(12)],
                reads=[("ncs_st", c) for c in range(12)] + ["ident_f"], writes=[PK[0], PK[1], PK[2]])
        for q3 in range(3):
            S.op("act", actf(stage2[0:16, q3 * 512:(q3 + 1) * 512], ps[q3][0:16, :], AF.Copy), reads=[PK[q3]], writes=["stage2"])
        S.dma("sp", "o_ncs", [dmaf(ncs[:, 2, :], stage2[0:16, :]), dmaf(ncs[:, 0:2, :], st_conv[:, 1:3, :])], reads=["stage2"])
        S.barrier()

        YG = Bump(arena, Y0, ARENA)
        osq = YG.alloc(BF16, [128, 512]); rn_o = YG.alloc(F32, [128, 512]); on_o = YG.alloc(F32, [128, 512])
        YG_EPI = YG.off
        NPW, DP = 4, 8
        CHDT = F32
        gN = lambda n_, dt_, shp: [YG.alloc(dt_, shp) for _ in range(n_)]
        Rs = gN(NPW, F32, [128, 256]); rhsR = gN(NPW, F32, [128, 256]); D0 = gN(NPW, F32, [128, 128]); E0 = gN(NPW, F32, [128, 128])
        EGr = gN(NPW, F32, [128, 128]); MB = gN(NPW, F32, [128, 128]); Qf = gN(NPW, F32, [128, 128]); Qs = gN(NPW, BF16, [128, 128])
        NNa = gN(NPW, CHDT, [128, 256]); NNb = gN(NPW, CHDT, [128, 256]); Xs = gN(NPW, BF16, [128, 128])
        NHa = gN(NPW, BF16, [128, 256]); NHb = gN(NPW, BF16, [128, 256])
        J0 = int(os.environ.get('GDN_J0', '6'))
        Qm = gN(DP, BF16, [128, 128]); attnT = gN(DP, BF16, [128, 128]); Kd = gN(DP, BF16, [128, 128]); Vb = gN(DP, BF16, [128, 128])
        qg = gN(DP, BF16, [128, 128]); nWT = gN(DP, BF16, [128, 128]); vn = gN(4, BF16, [128, 128])
        S.op("pool", lambda e: e.memset(S_f.rearrange("p h e -> p (h e)"), 0.0), writes=[("S_f", h) for h in range(4)])
        S.op("pool", lambda e: e.memset(S_b.rearrange("p h e -> p (h e)"), 0.0), writes=[("S_b", h) for h in range(4)])

        def gdn_P(n, h):
            ops = []
            Lop, Lgr, Ldma = mk_recorders(S, ops)
            u = n * 4 + h
            q = u % NPW; s = u % DP
            tok = slice(n * 128, (n + 1) * 128)
            kT = qkv[:, 4 + h, tok]; qT = qkv[:, h, tok]; vT = qkv[:, 8 + h, tok]
            col = lambda t: t[:, n, h:h + 1]
            K = lambda name: (name, q)
            H = lambda name: (name, s)
            bk = PK[q]; pb = ps[q]; pbb = psb[q]
            kk = pb[:, 256:384]; qk = pb[:, 384:512]
            Lop("pool", stt_pool(rhsR[q][:, 0:128], mask_incl, col(g_c)), reads=["mask_incl", "g_c"], writes=[K("rhsR")])
            Lop("pool", stt_pool(rhsR[q][:, 128:256], ident_f, col(beta_c)), reads=["ident_f", "beta_c"], writes=[K("rhsR")])
            Lgr("pe", [lambda e: e.transpose(out=pbb[:, 0:128], in_=kT, identity=ident_b),
                       lambda e: e.transpose(out=pbb[:, 128:256], in_=vT, identity=ident_b)], reads=[("qkv", n), "ident_b"], writes=[bk])
            ktok = pbb[:, 0:128]; vtok = pbb[:, 128:256]
            Lop("act", actf(Xs[q], ktok, AF.Copy, scale=col(bexp_c)), reads=[bk, "bexp_c"], writes=[K("Xs")])
            Lop("act", actf(Kd[s], ktok, AF.Copy, scale=col(kd_c)), reads=[bk, "kd_c"], writes=[H("Kd")])
            Lop("act", actf(Vb[s], vtok, AF.Copy, scale=col(beta_c)), reads=[bk, "beta_c"], writes=[H("Vb")])
            Lgr("pe", [mm(pb[:, 0:128], ones_f, rhsR[q][:, 0:128]), mm(pb[:, 128:256], ones_f, rhsR[q][:, 128:256]),
                       mm(kk, kT, kT), mm(qk, kT, qT)], reads=[K("rhsR"), "ones_f", ("qkv", n)], writes=[bk])
            Lop("dve", cp(Rs[q], pb[:, 0:256]), reads=[bk], writes=[K("Rs")])
            R_gc = Rs[q][:, 0:128]; R_be = Rs[q][:, 128:256]
            Lop("pool", lambda e: e.tensor_tensor(out=D0[q], in0=R_gc, in1=col(gc_c).to_broadcast([128, 128]), op=ALU.subtract), reads=[K("Rs"), "gc_c"], writes=[K("D0")])
            Lop("pool", ts(D0[q], D0[q], 0.0, ALU.min), reads=[K("D0")], writes=[K("D0")])
            Lop("act", actf(D0[q], D0[q], AF.Exp), reads=[K("D0")], writes=[K("D0")])
            Lop("act", actf(EGr[q], R_gc, AF.Exp), reads=[K("Rs")], writes=[K("EGr")])
            Lop("pool", tt(MB[q], R_be, D0[q], ALU.mult), reads=[K("Rs"), K("D0")], writes=[K("MB")])
            Lop("pool", tt(MB[q], MB[q], nmask_su, ALU.mult), reads=[K("MB"), "nmask_su"], writes=[K("MB")])
            Lop("pool", tt(D0[q], D0[q], mask_incl, ALU.mult), reads=[K("D0"), K("MB"), "mask_incl"], writes=[K("D0")])
            Lop("pool", tt(qg[s], qT, EGr[q], ALU.mult), reads=[("qkv", n), K("EGr")], writes=[H("qg")])
            Lop("dve", tt(NNa[q][:, 0:128], kk, MB[q], ALU.mult), reads=[bk, K("MB")], writes=[K("NNa")])
            Lop("dve", tt(attnT[s], qk, D0[q], ALU.mult), reads=[bk, K("D0")], writes=[H("attnT")])
            Lgr("pe", [mm(pb[:, 0:128], NNa[q][:, 0:128], ident_f)], reads=[K("NNa"), "ident_f"], writes=[bk])
            Lop("act", actf(NNa[q][:, 128:256], pb[:, 0:128], AF.Copy), reads=[bk], writes=[K("NNa")])
            Lop("pool", tt(Qf[q], ident_f, NNa[q][:, 0:128], ALU.add), reads=["ident_f", K("NNa")], writes=[K("Qf")])
            cur, nxt, kc, kn = NNa[q], NNb[q], K("NNa"), K("NNb")
            cur16, nxt16, kc16, kn16 = NHa[q], NHb[q], K("NHa"), K("NHb")
            if J0 == 0:
                Lop("pool", cp(cur16, cur), reads=[kc], writes=[kc16])
            for j in range(1, 7):
                f32lvl = j <= J0
                src, ksrc = (cur, kc) if f32lvl else (cur16, kc16)
                fns = []
                if j < 6:
                    fns.append(mm(pb[:, 0:128], src[:, 128:256], src[:, 0:128]))
                fns.append(mm(pb[:, 128:256], src[:, 0:128], src[:, 128:256]))
                Lgr("pe", fns, reads=[ksrc], writes=[bk])
                lo = 0 if j < 6 else 128
                if f32lvl:
                    Lop("act", actf(nxt[:, lo:256], pb[:, lo:256], AF.Copy), reads=[bk], writes=[kn])
                    if j == J0 and j < 6:
                        Lop("pool", cp(nxt16[:, lo:256], nxt[:, lo:256]), reads=[kn], writes=[kn16])
                    Lgr("pe", [mm(pb[:, 256:384], nxt[:, 128:256], Qf[q])], reads=[kn, K("Qf")], writes=[bk])
                else:
                    Lop("act", actf(nxt16[:, lo:256], pb[:, lo:256], AF.Copy), reads=[bk], writes=[kn16])
                    Lop("pool", cp(Qs[q], Qf[q]), reads=[K("Qf")], writes=[K("Qs")])
                    Lgr("pe", [mm(pb[:, 256:384], nxt16[:, 128:256], Qs[q])], reads=[kn16, K("Qs")], writes=[bk])
                Lop("dve", tt(Qf[q], Qf[q], pb[:, 256:384], ALU.add), reads=[bk, K("Qf")], writes=[K("Qf")])
                cur, nxt, kc, kn = nxt, cur, kn, kc
                cur16, nxt16, kc16, kn16 = nxt16, cur16, kn16, kc16
            Lop("pool", cp(Qm[s], Qf[q]), reads=[K("Qf")], writes=[H("Qm")])
            Lgr("pe", [mm(pb[:, 384:512], Xs[q], Qm[s])], reads=[K("Xs"), H("Qm")], writes=[bk])
            Lop("act", actf(nWT[s], pb[:, 384:512], AF.Copy, scale=negone), reads=[bk], writes=[H("nWT")])
            return ops

        def gdn_R(n, h):
            ops = []
            Lop, Lgr, Ldma = mk_recorders(S, ops)
            u = n * 4 + h
            s = u % DP
            H = lambda name: (name, s)
            col = lambda t: t[:, n, h:h + 1]
            bR = PK[4]; ob = 5 + n % 2
            V = ps[4][:, h * 128:(h + 1) * 128]
            Lgr("pe", [mm(V, Qm[s], Vb[s], start=True, stop=False),
                       mm(V, nWT[s], S_b[:, h, :], start=False, stop=True)],
                reads=[H("Qm"), H("Vb"), H("nWT"), ("S_b", h)], writes=[bR])
            Lop("dve", cp(vn[h], V), reads=[bR], writes=[("vn", h)])
            Lgr("pe", [mm(V, Kd[s], vn[h])], reads=[H("Kd"), ("vn", h)], writes=[bR])
            Lgr("pe", [mm(ps[ob][:, h * 128:(h + 1) * 128], S_b[:, h, :], qg[s], start=True, stop=False),
                       mm(ps[ob][:, h * 128:(h + 1) * 128], vn[h], attnT[s], start=False, stop=True)],
                reads=[("S_b", h), H("qg"), ("vn", h), H("attnT")], writes=[PK[ob]])
            Lop("dve", stt(S_f[:, h, :], S_f[:, h, :], col(egl_c), V, ALU.mult, ALU.add), reads=[bR, ("S_f", h), "egl_c"], writes=[("S_f", h)])
            Lop("act", actf(S_b[:, h, :], S_f[:, h, :], AF.Copy), reads=[("S_f", h)], writes=[("S_b", h)])
            return ops

        def gdn_epilogue(o_ps, ss_ps, ncol, t0, okeys, sskey):
            w = 4 * ncol
            S.op("act", actf(osq[:, 0:w], o_ps, AF.Square), reads=okeys, writes=["osq"])
            S.group("pe", [mm(ss_ps, ones_b, osq[:, 0:w])], reads=["osq", "ones_b"], writes=[sskey])
            S.op("dve", ts(rn_o[:, 0:w], ss_ps, 1.0 / 128, ALU.mult, EPS, ALU.add), reads=[sskey], writes=["rn_o"])
            S.op("act", actf(rn_o[:, 0:w], rn_o[:, 0:w], AF.Sqrt), reads=["rn_o"], writes=["rn_o"])
            S.op("dve", lambda e: e.reciprocal(out=rn_o[:, 0:w], in_=rn_o[:, 0:w]), reads=["rn_o"], writes=["rn_o"])
            S.op("dve", stt(on_o[:, 0:w], o_ps, gdnn_col, rn_o[:, 0:w], ALU.mult, ALU.mult), reads=okeys + ["rn_o", "cols"], writes=["on_o"])
            S.op("pool", tt(cat[:, 0:4, t0:t0 + ncol], on_o[:, 0:w].rearrange("p (h t) -> p h t", h=4), zs[:, :, t0:t0 + ncol], ALU.mult),
                 reads=["on_o", "zs"], writes=[("cat_o", t0)])

        _SK = os.environ.get('KSKIP', '')
        NCH = 0 if 'prompt' in _SK else int(os.environ.get('GDN_N', '16'))

        def epi_ops(n):
            ob = 5 + n % 2
            return [lambda: gdn_epilogue(ps[ob][:, :], ps[7][:, :], 128, n * 128, [PK[ob]], PK[7])]

        _DO_SAMPLE = 'sample' not in _SK
        def _sample_section():
            YG = Bump(arena, YG_EPI, ARENA)
            sv = YG.alloc(F32, [128, 8])
            rexp = YG.alloc(F32, [128, 8, 16])
            bcs = YG.alloc(F32, [128, 128])
            dcol = YG.alloc(F32, [128, 64])
            dtok = YG.alloc(F32, [128, 512]); ktoks = YG.alloc(F32, [128, 512])
            kmask = [YG.alloc(F32, [128, 512]) for _ in range(2)]
            S.op("dve", cp(sv[0:16, 0:4], beta_c[0:16, 16, :]), reads=["beta_c"], writes=["sv"])
            S.op("act", actf(sv[0:16, 4:8], g_c[0:16, 16, :], AF.Exp), reads=["g_c"], writes=["sv"])
            for j in range(8):
                S.op("dve", ts(rexp[0:16, j, :], ident_f[0:16, 0:16], sv[0:16, j:j + 1], ALU.mult), reads=["sv", "ident_f"], writes=["rexp"])
            S.group("pe", [mm(ps[0][:, 0:128], ones_f[0:16, :], rexp[0:16, :, :].rearrange("p j b -> p (j b)"))], reads=["rexp", "ones_f"], writes=[PK[0]])
            S.op("dve", cp(bcs, ps[0][:, 0:128]), reads=[PK[0]], writes=["bcs"])
            beta_bc = bcs[:, 0:64]; eg_bc = bcs[:, 64:128]
            S.group("pe", [mm(ps[1][:, h * 16 + b:h * 16 + b + 1], S_all[:, b * 4 + h, :], qks_f[:, 4 + h, b:b + 1]) for b in range(16) for h in range(4)],
                    reads=[("S_all", q4) for q4 in range(4)] + ["qks_f"], writes=[PK[1]])
            S.op("dve", tt(dcol, ps[1][:, 0:64], eg_bc, ALU.mult), reads=[PK[1], "bcs"], writes=["dcol"])
            S.op("dve", tt(dcol, qks_f[:, 8:12, :].rearrange("p h b -> p (h b)"), dcol, ALU.subtract), reads=["dcol", "qks_f"], writes=["dcol"])
            S.op("dve", tt(dcol, dcol, beta_bc, ALU.mult), reads=["dcol", "bcs"], writes=["dcol"])
            S.group("pe", [mm(ps[2][0:16, h * 128:(h + 1) * 128], dcol[:, h * 16:(h + 1) * 16], ident_f) for h in range(4)], reads=["dcol", "ident_f"], writes=[PK[2]])
            S.group("pe", [mm(ps[3][0:16, h * 128:(h + 1) * 128], qks_f[:, 4 + h, :], ident_f) for h in range(4)], reads=["qks_f", "ident_f"], writes=[PK[3]])
            S.op("dve", cp(dtok[0:16, :], ps[2][0:16, :]), reads=[PK[2]], writes=["dtok"])
            S.op("act", actf(ktoks[0:16, :], ps[3][0:16, :], AF.Copy), reads=[PK[3]], writes=["ktoks"])
            for b in range(16):
                km = kmask[b % 2]; pb = 4 + b % 2
                S.op("dve", ts(km[0:16, :], ktoks[0:16, :], ident_f[0:16, b:b + 1], ALU.mult), reads=["ktoks", "ident_f"], writes=[("kmask", b % 2)])
                S.group("pe", [mm(ps[pb][:, h * 128:(h + 1) * 128], km[0:16, h * 128:(h + 1) * 128], dtok[0:16, h * 128:(h + 1) * 128]) for h in range(4)],
                        reads=[("kmask", b % 2), "dtok"], writes=[PK[pb]])
                for h in range(4):
                    S.op("dve", stt(S_all[:, b * 4 + h, :], S_all[:, b * 4 + h, :], eg_bc[:, h * 16 + b:h * 16 + b + 1], ps[pb][:, h * 128:(h + 1) * 128], ALU.mult, ALU.add),
                         reads=[PK[pb], "bcs", ("S_all", b // 4)], writes=[("S_all", b // 4)])
            S.group("pe", [mm(ps[1][:, 64 + h * 16 + b:64 + h * 16 + b + 1], S_all[:, b * 4 + h, :], qks_f[:, h, b:b + 1]) for b in range(16) for h in range(4)],
                    reads=[("S_all", q4) for q4 in range(4)] + ["qks_f"], writes=[PK[1]])
            gdn_epilogue(ps[1][:, 64:128], ps[0][:, 128:192], 16, T_P, [PK[1]], PK[0])
            for q4 in range(4):
                S.dma("sp", "o_ngs%d" % q4, dmaf(ngs[q4 * 4:(q4 + 1) * 4].rearrange("b h d e -> d (b h) e"), S_all[:, q4 * 16:(q4 + 1) * 16, :]), reads=[("S_all", q4)])
        if _DO_SAMPLE:
            _sample_section()
        S.barrier(skip=("o_ngs",))

        STAG = int(os.environ.get('GDN_STAG', '16'))
        lanes = [[(lambda: None)] * (h * STAG) for h in range(4)]
        for n in range(NCH):
            for h in range(4):
                lanes[h] += gdn_P(n, h) + gdn_R(n, h)
                if h == 3:
                    lanes[h] += epi_ops(n)
        zipper(lanes)
        S.dma("sp", "o_ngp", dmaf(ngp.rearrange("h d e -> d h e"), S_f), reads=[("S_f", h) for h in range(4)])

        S.barrier()

        if 'phasec' in _SK:
            S.finish()
            with nc.Block() as block:
                S.replay(block)
            return nc
        YC = Bump(arena, P_C0, ARENA)
        R = YC.alloc(F32, [128, 8, 528]); xnC = YC.alloc(BF16, [128, 8, 528]); hid = YC.alloc(BF16, [128, 32, 528])
        r8 = [YC.alloc(BF16, [128, 8, 512]) for _ in range(3)]
        r16 = [YC.alloc(BF16, [128, 32, 256]) for _ in range(2)]
        xres = YC.alloc(F32, [128, 4, 1024]); xres_s = YC.alloc(F32, [128, 1024])
        pw = YC.alloc(BF16, [128, 2, 1024]); ptok = [YC.alloc(BF16, [128, 256]) for _ in range(2)]
        pT = YC.alloc(BF16, [128, 2, 528]); sqr = [YC.alloc(BF16, [128, 528]) for _ in range(2)]; rnC = YC.alloc(F32, [128, 528])
        sig = [YC.alloc(F32, [128, 528]) for _ in range(2)]; relu_t = [YC.alloc(F32, [128, 528]) for _ in range(2)]
        ytile = [YC.alloc(F32, [128, 1024]) for _ in range(1)]
        rncol = YC.alloc(F32, [128, 8])
        r8_n = [0]; r16_n = [0]; misc_n = [0]

        r8_seq = []
        for _p in range(4):
            r8_seq += [(w_out_v, 0), (w_out_v, 512)] + [(w_up_v, bb * 512) for bb in range(8)] + [(w_gate_v, 0), (w_gate_v, 512)]
        r8_issued = [0]

        def load_r8(view, c0):
            idx = r8_n[0]
            r8_n[0] += 1
            assert r8_seq[idx][1] == c0
            while r8_issued[0] < min(len(r8_seq), idx + 3):
                j = r8_issued[0]
                vw, cc = r8_seq[j]
                S.dma("pool", "r8_%d" % (j % 3), dmaf(r8[j % 3], vw[:, :, cc:cc + 512]), writes=[("r8", j % 3)])
                r8_issued[0] += 1
            return idx % 3

        def load_r16(c0):
            sl = r16_n[0] % 2
            r16_n[0] += 1
            S.dma("pool", "r16_%d" % sl, dmaf(r16[sl], w_down_v[:, :, c0:c0 + 256]), writes=[("r16", sl)])
            return sl

        S.dma("pool", "pw", dmaf(pw, w_ple_v), writes=["pw"])
        PASSES = [[(0, 512, 0)], [(512, 512, 0)], [(1024, 512, 0)], [(1536, 512, 0), (2048, 16, 512)]]

        def rms_norm_C(which, out_fn, segs, W, tag):
            bns = []
            for (t0, n, l0) in segs:
                bns.append(6 + misc_n[0] % 2)
                misc_n[0] += 1
            for m in range(8):
                sq = sqr[m % 2]
                S.op("act", actf(sq[:, 0:W], R[:, m, 0:W], AF.Square), reads=[("R", m)], writes=[("sqr", m % 2)])
                for si_, (t0, n, l0) in enumerate(segs):
                    bn = bns[si_]
                    S.group("pe", [mm(ps[bn][:, 0:n], ones_b, sq[:, l0:l0 + n], start=(m == 0), stop=(m == 7))],
                            reads=[("sqr", m % 2), "ones_b"], writes=[PK[bn]])
            for si_, (t0, n, l0) in enumerate(segs):
                bn = bns[si_]
                S.op("dve", ts(rnC[:, l0:l0 + n], ps[bn][:, 0:n], 1.0 / 1024, ALU.mult, EPS, ALU.add), reads=[PK[bn]], writes=["rnC"])
            S.op("act", actf(rnC[:, 0:W], rnC[:, 0:W], AF.Sqrt), reads=["rnC"], writes=["rnC"])
            S.op("dve", lambda e: e.reciprocal(out=rnC[:, 0:W], in_=rnC[:, 0:W]), reads=["rnC"], writes=["rnC"])
            for m in range(8):
                out_ap, wkey = out_fn(m)
                S.op("dve", stt(out_ap, R[:, m, 0:W], gcol(which, m), rnC[:, 0:W], ALU.mult, ALU.mult), reads=[("R", m), "rnC", "cols"], writes=[wkey])

        for pi, segs in enumerate(PASSES):
            W = sum(n for (_, n, _) in segs)
            t00 = segs[0][0]
            has_s = len(segs) > 1
            if pi == 0:
                S.dma("sp", "xres", dmaf(xres, x_p[0:512, :].rearrange("(j p) f -> p j f", p=128)), writes=["xres"])
            def stats_act(m):
                S.op("act", actf(sqr[m % 2][:, 0:W], R[:, m, 0:W], AF.Square), reads=[("R", m)], writes=[("sqr", m % 2)])

            def stats_pe(m, bns):
                for si_, (t0, n, l0) in enumerate(segs):
                    S.group("pe", [mm(ps[bns[si_]][:, 0:n], ones_b, sqr[m % 2][:, l0:l0 + n], start=(m == 0), stop=(m == 7))],
                            reads=[("sqr", m % 2), "ones_b"], writes=[PK[bns[si_]]])

            def norm_finish_row(bns, out_t, key, square):
                for si_, (t0, n, l0) in enumerate(segs):
                    S.op("dve", ts(out_t[:, l0:l0 + n], ps[bns[si_]][:, 0:n], 1.0 / 1024, ALU.mult, EPS, ALU.add), reads=[PK[bns[si_]]], writes=[key])
                if not square:
                    S.op("act", actf(out_t[:, 0:W], out_t[:, 0:W], AF.Sqrt), reads=[key], writes=[key])
                S.op("dve", lambda e: e.reciprocal(out=out_t[:, 0:W], in_=out_t[:, 0:W]), reads=[key], writes=[key])

            def pick_bns():
                o = []
                for _ in segs:
                    o.append(6 + misc_n[0] % 2)
                    misc_n[0] += 1
                return o

            bns1 = pick_bns()
            for blk in range(2):
                sl = load_r8(w_out_v, blk * 512)
                for m4 in range(4):
                    m = blk * 4 + m4
                    for (t0, n, l0) in segs:
                        b = next_bank()
                        fns = [mm(ps[b][:, 0:n], r8[sl][:, k, m4 * 128:(m4 + 1) * 128], cat[:, k, t0:t0 + n], start=(k == 0), stop=False) for k in range(8)]
                        if n == 512:
                            fns += [mm(ps[b][:, j * 128:(j + 1) * 128], xres[:, j, m * 128:(m + 1) * 128], ident_f, start=False, stop=(j == 3)) for j in range(4)]
                            rk = ["xres"]
                        else:
                            fns += [mm(ps[b][:, 0:16], xres_s[0:16, m * 128:(m + 1) * 128], ident_f[0:16, 0:16], start=False, stop=True)]
                            rk = ["xres_s"]
                        S.group("pe", fns, reads=[("r8", sl), "cat", "ident_f"] + rk, writes=[PK[b]])
                        S.op("act", actf(R[:, m, l0:l0 + n], ps[b][:, 0:n], AF.Copy), reads=[PK[b]], writes=[("R", m)])
                        S.op("act", actf(xnC[:, m, l0:l0 + n], ps[b][:, 0:n], AF.Copy, scale=gcol(0, m)), reads=[PK[b], "cols"], writes=[("xnC", m)])
                    stats_act(m)
                    if m >= 1:
                        stats_pe(m - 1, bns1)
            stats_pe(7, bns1)
            norm_finish_row(bns1, rnC, "rnC", True)
            if pi + 1 < len(PASSES):
                tn = PASSES[pi + 1][0][0]
                S.dma("sp", "xres", dmaf(xres, x_p[tn:tn + 512, :].rearrange("(j p) f -> p j f", p=128)), writes=["xres"])
                if len(PASSES[pi + 1]) > 1:
                    S.dma("sp", "xres_s", dmaf(xres_s[0:16, :], x_s), writes=["xres_s"])
            for blk in range(8):
                sl = load_r8(w_up_v, blk * 512)
                for m4 in range(4):
                    hc = blk * 4 + m4
                    for (t0, n, l0) in segs:
                        b = next_bank()
                        S.group("pe", [mm(ps[b][:, 0:n], r8[sl][:, k, m4 * 128:(m4 + 1) * 128], xnC[:, k, l0:l0 + n], start=(k == 0), stop=(k == 7)) for k in range(8)],
                                reads=[("r8", sl)] + [("xnC", k) for k in range(8)], writes=[PK[b]])
                        rt = relu_t[misc_n[0] % 2]; rkey = ("relu_t", misc_n[0] % 2)
                        misc_n[0] += 1
                        S.op("act", actf(rt[:, 0:n], ps[b][:, 0:n], AF.Relu), reads=[PK[b]], writes=[rkey])
                        S.op("dve", tt(hid[:, hc, l0:l0 + n], rt[:, 0:n], rt[:, 0:n], ALU.mult), reads=[rkey], writes=[("hid", hc)])
            bns2 = pick_bns()
            for blk in range(4):
                sl = load_r16(blk * 256)
                for m2 in range(2):
                    m = blk * 2 + m2
                    for (t0, n, l0) in segs:
                        b = next_bank()
                        S.group("pe", [mm(ps[b][:, 0:n], r16[sl][:, k, m2 * 128:(m2 + 1) * 128], hid[:, k, l0:l0 + n], start=(k == 0), stop=(k == 31)) for k in range(32)],
                                reads=[("r16", sl)] + [("hid", k) for k in range(32)], writes=[PK[b]])
                        sg = sig[misc_n[0] % 2]; skey = ("sig", misc_n[0] % 2)
                        misc_n[0] += 1
                        S.op("dve", tt(sg[:, 0:n], ps[b][:, 0:n], rnC[:, l0:l0 + n], ALU.mult), reads=[PK[b], "rnC"], writes=[skey])
                        S.op("dve", tt(R[:, m, l0:l0 + n], R[:, m, l0:l0 + n], sg[:, 0:n], ALU.add), reads=[skey, ("R", m)], writes=[("R", m)])
                    S.op("act", actf(xnC[:, m, 0:W], R[:, m, 0:W], AF.Copy, scale=gcol(1, m)), reads=[("R", m), "cols"], writes=[("xnC", m)])
                    stats_act(m)
                    if m >= 1:
                        stats_pe(m - 1, bns2)
            stats_pe(7, bns2)
            norm_finish_row(bns2, rnC, "rnC", False)
            for (t0, n, l0) in segs:
                ntile = (n + 127) // 128
                for j in range(ntile):
                    r = min(128, n - j * 128)
                    sl = misc_n[0] % 2
                    misc_n[0] += 1
                    src = p_p[t0 + j * 128:t0 + j * 128 + r, :] if n == 512 else p_s
                    S.dma("pool", "ptok%d" % sl, dmaf(ptok[sl][0:r, :], src), writes=[("ptok", sl)])
                    S.group("pe", [lambda e, kk=kk, sl=sl, r=r: e.transpose(out=psb[5][:, kk * 128:kk * 128 + r], in_=ptok[sl][0:r, kk * 128:(kk + 1) * 128], identity=ident_b[0:r, 0:r]) for kk in range(2)],
                            reads=[("ptok", sl), "ident_b"], writes=[PK[5]])
                    S.op("act", actf(pT[:, :, l0 + j * 128:l0 + j * 128 + r], psb[5][:, 0:256].rearrange("p (k t) -> p k t", k=2)[:, :, 0:r], AF.Copy), reads=[PK[5]], writes=["pT"])
            ntt = sum((n + 127) // 128 for (_, n, _) in segs)
            sigbufs = [(sig[0], ("sig", 0)), (sig[1], ("sig", 1)), (relu_t[0], ("relu_t", 0)), (relu_t[1], ("relu_t", 1))]

            def gate_chunk(m, sl, m4):
                ops = []
                Lop, Lgr, Ldma = mk_recorders(S, ops)
                for si_, (t0, n, l0) in enumerate(segs):
                    b = (2 * m + si_) % 4
                    pb_ = 6 + m % 2
                    sg, skey = sigbufs[(2 * m + si_) % 4]
                    Lgr("pe", [mm(ps[b][:, 0:n], r8[sl][:, k, m4 * 128:(m4 + 1) * 128], xnC[:, k, l0:l0 + n], start=(k == 0), stop=(k == 7)) for k in range(8)],
                        reads=[("r8", sl)] + [("xnC", k) for k in range(8)], writes=[PK[b]])
                    Lgr("pe", [mm(ps[pb_][:, 0:n], pw[:, kk, m * 128:(m + 1) * 128], pT[:, kk, l0:l0 + n], start=(kk == 0), stop=(kk == 1)) for kk in range(2)],
                        reads=["pw", "pT"], writes=[PK[pb_]])
                    Lop("dve", tt(sg[:, 0:n], ps[b][:, 0:n], rnC[:, l0:l0 + n], ALU.mult), reads=[PK[b], "rnC"], writes=[skey])
                    Lop("act", actf(sg[:, 0:n], sg[:, 0:n], AF.Sigmoid), reads=[skey], writes=[skey])
                    Lop("dve", tt(sg[:, 0:n], sg[:, 0:n], ps[pb_][:, 0:n], ALU.mult), reads=[PK[pb_], skey], writes=[skey])
                    Lop("dve", tt(R[:, m, l0:l0 + n], R[:, m, l0:l0 + n], sg[:, 0:n], ALU.add), reads=[skey, ("R", m)], writes=[("R", m)])
                sq = sqr[m % 2]
                Lop("act", actf(sq[:, 0:W], R[:, m, 0:W], AF.Square), reads=[("R", m)], writes=[("sqr", m % 2)])
                fns = []
                if m == 0:
                    fns.append(mm(ps[5][:, 256:256 + ntt], zeros_f, zeros_f[:, 0:ntt], start=True, stop=False))
                jt = 0
                for (t0, n, l0) in segs:
                    for j in range((n + 127) // 128):
                        r = min(128, n - j * 128)
                        fns.append(mm(ps[5][0:r, 256 + jt:257 + jt], sq[:, l0 + j * 128:l0 + j * 128 + r], ones_b[:, 0:1], start=False, stop=False))
                        jt += 1
                if m == 7:
                    fns.append(mm(ps[5][:, 256:256 + ntt], zeros_f, zeros_f[:, 0:ntt], start=False, stop=True))
                Lgr("pe", fns, reads=[("sqr", m % 2), "ones_b"], writes=[PK[5]])
                Lop("act", actf(R[:, m, 0:W], R[:, m, 0:W], AF.Copy, scale=gcol(2, m)), reads=[("R", m), ("sqr", m % 2), "cols"], writes=[("R", m)])
                return ops

            for blk in range(2):
                sl = load_r8(w_gate_v, blk * 512)
                chunks = [gate_chunk(blk * 4 + m4, sl, m4) for m4 in range(4)]
                zipper(chunks[0:2])
                zipper(chunks[2:4])
            S.op("dve", ts(rncol[:, 0:ntt], ps[5][:, 256:256 + ntt], 1.0 / 1024, ALU.mult, EPS, ALU.add), reads=[PK[5]], writes=["rncol"])
            S.op("act", actf(rncol[:, 0:ntt], rncol[:, 0:ntt], AF.Sqrt), reads=["rncol"], writes=["rncol"])
            S.op("dve", lambda e: e.reciprocal(out=rncol[:, 0:ntt], in_=rncol[:, 0:ntt]), reads=["rncol"], writes=["rncol"])
            jt = 0
            for (t0, n, l0) in segs:
                ntile = (n + 127) // 128
                for j in range(ntile):
                    r = min(128, n - j * 128)
                    ysl = 0
                    for half in range(2):
                        b = next_bank()
                        S.group("pe", [mm(ps[b][0:r, m4 * 128:(m4 + 1) * 128], R[:, half * 4 + m4, l0 + j * 128:l0 + j * 128 + r], ident_f) for m4 in range(4)],
                                reads=[("R", half * 4 + m4) for m4 in range(4)] + ["ident_f"], writes=[PK[b]])
                        S.op("act", actf(ytile[ysl][0:r, half * 512:(half + 1) * 512], ps[b][0:r, :], AF.Copy, scale=rncol[0:r, jt:jt + 1]), reads=[PK[b], "rncol"], writes=[("ytile", ysl)])
                    jt += 1
                    dst = y_p[t0 + j * 128:t0 + j * 128 + r, :] if n == 512 else y_s
                    S.dma("sp", "o_y%d" % ysl, dmaf(dst, ytile[ysl][0:r, :]), reads=[("ytile", ysl)])
        S.finish()
        with nc.Block() as block:
            S.replay(block)
    return nc


_PROG = {}


def _make_in_maps(inputs):
    f = lambda a: np.ascontiguousarray(np.asarray(a, dtype=np.float32))
    g = {k: f(v) for k, v in inputs.items()}
    shared = {
        "g_mix": g["g_mix"].reshape(1, 1024), "w_in": g["w_in"][0], "w_conv": g["w_conv"][0],
        "a_log": g["a_log"].reshape(1, 4), "dt_bias": g["dt_bias"].reshape(1, 4), "gdn_norm": g["gdn_norm"].reshape(1, 128),
        "ln_g": g["sgu_ln_g"].reshape(1, 512), "ln_b": g["sgu_ln_b"].reshape(1, 512), "w_s": g["w_s"][0],
        "b_s": g["b_s"].reshape(1, 512), "w_out": g["w_out"][0], "g_ff": g["g_ff"].reshape(8, 128), "w_up": g["w_up"][0],
        "w_down": g["w_down"][0], "g_ple": g["g_ple"].reshape(8, 128), "w_ple": g["w_ple"][0], "w_gate": g["w_ple_gate"][0],
        "g_fin": g["g_final"].reshape(8, 128),
    }
    maps = []
    for i in range(8):
        m = dict(shared)
        sl = slice(16 * i, 16 * i + 16)
        m["x_p"] = g["x_prompt"][i]
        m["x_s"] = g["x_sample"][sl, 0]
        m["st_conv"] = g["state_conv"][0, sl]
        m["st_gdn"] = g["state_gdn"][0, sl]
        m["p_p"] = g["p_prompt"][0, i]
        m["p_s"] = g["p_sample"][0, sl, 0]
        maps.append(m)
    return maps


def kernel(**inputs):
    if "nc" not in _PROG:
        _PROG["nc"] = build_program()
    nc = _PROG["nc"]
    maps = _make_in_maps(inputs)
    res = run_bass_kernel_spmd(nc, maps, core_ids=list(range(8)))
    R = res.results
    st = lambda name: np.stack([np.asarray(r[name], dtype=np.float32) for r in R])
    cc = lambda name: np.concatenate([np.asarray(r[name], dtype=np.float32) for r in R], axis=0)
    y_prompt = st("y_p")
    y_sample = cc("y_s")[:, None, :]
    new_conv_prompt = st("ncp")[None]
    new_gdn_prompt = st("ngp")[None]
    new_conv_sample = cc("ncs")[None]
    new_gdn_sample = cc("ngs")[None]
    new_sgu_v_sample = cc("nsv")[None, :, None, :]
    return (y_prompt, y_sample, new_conv_prompt, new_gdn_prompt, new_conv_sample, new_gdn_sample, new_sgu_v_sample)
```

```python
import os
import numpy as np
import concourse.bass as bass
import concourse.mybir as mybir
from concourse.bass_utils import run_bass_kernel_spmd

F32 = mybir.dt.float32
BF16 = mybir.dt.bfloat16
AF = mybir.ActivationFunctionType
ALU = mybir.AluOpType
AX = mybir.AxisListType


class Sched:
    ENGS = ("pe", "act", "dve", "pool", "sp")

    def __init__(self, nc, stack):
        self.nc = nc
        self.stack = stack
        self.streams = {e: [] for e in self.ENGS}
        self.esem = {e: stack.enter_context(nc.semaphore("c_" + e)) for e in self.ENGS[:4]}
        self.ecnt = {e: 0 for e in self.ENGS}
        self.waited = {e: {} for e in self.ENGS}
        self.res = {}
        self.dsem = {}
        self.sem_by_name = {}
        for e in self.ENGS[:4]:
            self.sem_by_name[self.esem[e].name] = self.esem[e]

    def _need(self, eng, ev, waits):
        if ev is None:
            return
        name, val, src = ev
        if src == eng and eng == "pe":
            return
        cur = waits.get(name, 0)
        if val > cur:
            waits[name] = val

    def _deps(self, eng, reads, writes):
        waits = {}
        for k in reads:
            r = self.res.get(k)
            if r is not None:
                self._need(eng, r[0], waits)
        for k in writes:
            r = self.res.get(k)
            if r is not None:
                if r[0] is not None and not (r[0][2] == eng):
                    self._need(eng, r[0], waits)
                for ev in r[1]:
                    self._need(eng, ev, waits)
        out = []
        w = self.waited[eng]
        for name, val in waits.items():
            if w.get(name, 0) < val:
                w[name] = val
                out.append((name, val))
        return out

    def _commit(self, ev, reads, writes):
        for k in reads:
            r = self.res.setdefault(k, [None, []])
            r[1].append(ev)
        for k in writes:
            self.res[k] = [ev, []]

    def op(self, eng, fn, reads=(), writes=()):
        waits = self._deps(eng, reads, writes)
        self.ecnt[eng] += 1
        ev = (self.esem[eng].name, self.ecnt[eng], eng)
        self.streams[eng].append((waits, [fn], ("inc", self.esem[eng], 1)))
        self._commit(ev, reads, writes)
        return ev

    def group(self, eng, fns, reads=(), writes=()):
        waits = self._deps(eng, reads, writes)
        self.ecnt[eng] += 1
        ev = (self.esem[eng].name, self.ecnt[eng], eng)
        self.streams[eng].append((waits, list(fns), ("inc", self.esem[eng], 1)))
        self._commit(ev, reads, writes)
        return ev

    def dma(self, eng, slot, fn, reads=(), writes=(), n=1):
        if slot not in self.dsem:
            s = self.stack.enter_context(self.nc.semaphore("d_" + slot))
            self.dsem[slot] = [s, 0]
            self.sem_by_name[s.name] = s
        waits = self._deps(eng, reads, writes)
        d = self.dsem[slot]
        fns = fn if isinstance(fn, (list, tuple)) else [fn]
        d[1] += 16 * len(fns)
        ev = (d[0].name, d[1], "dma")
        self.streams[eng].append((waits, list(fns), ("dmainc", d[0], 16)))
        self._commit(ev, reads, writes)
        return ev

    def barrier(self, skip=()):
        evs = []
        for e in self.ENGS[:4]:
            if self.ecnt[e] > 0:
                evs.append((self.esem[e].name, self.ecnt[e]))
        for slot, (s, c) in self.dsem.items():
            if c > 0 and not any(slot.startswith(p) for p in skip):
                evs.append((s.name, c))
        for eng in self.ENGS:
            w = self.waited[eng]
            waits = []
            for name, val in evs:
                if w.get(name, 0) < val:
                    w[name] = val
                    waits.append((name, val))
            if waits:
                self.streams[eng].append((waits, [], None))
        self.res.clear()

    def finish(self):
        eng = "sp"
        waits = []
        for slot, (s, c) in self.dsem.items():
            if c > 0:
                waits.append((s.name, c))
        for e in self.ENGS[:4]:
            if self.ecnt[e] > 0:
                waits.append((self.esem[e].name, self.ecnt[e]))
        self.streams[eng].append((waits, [], None))

    def replay(self, block):
        sbn = self.sem_by_name

        def run(e, items):
            for waits, fns, inc in items:
                for name, val in waits:
                    e.wait_ge(sbn[name], val)
                last = None
                for i, f in enumerate(fns):
                    ins = f(e)
                    if inc is not None and inc[0] == "dmainc":
                        ins.then_inc(inc[1], 16)
                    last = ins
                if inc is not None and inc[0] == "inc" and last is not None:
                    last.then_inc(inc[1], 1)

        st = self.streams

        @block.tensor
        def _(e):
            run(e, st["pe"])

        @block.scalar
        def _(e):
            run(e, st["act"])

        @block.vector
        def _(e):
            run(e, st["dve"])

        @block.gpsimd
        def _(e):
            run(e, st["pool"])

        @block.sync
        def _(e):
            run(e, st["sp"])


U8 = mybir.dt.uint8
T_P = 2048
T_S = 16
T_ALL = T_P + T_S
SEGS = [(0, 512), (512, 512), (1024, 512), (1536, 512), (2048, 16)]
NT = 17
EPS = 1e-6
D_IN = 3080
C_Q, C_K, C_V, C_Z, C_BA, C_U, C_VS = 0, 512, 1024, 1536, 2048, 2056, 2568


def mm(out, lhsT, rhs, start=True, stop=True):
    return lambda e: e.matmul(out, lhsT=lhsT, rhs=rhs, start=start, stop=stop)


def actf(out, in_, func, **kw):
    return lambda e: e.activation(out=out, in_=in_, func=func, **kw)


def tt(out, a, b, op):
    return lambda e: e.tensor_tensor(out=out, in0=a, in1=b, op=op)


def ts(out, a, s1, op0, s2=None, op1=None):
    if op1 is None:
        return lambda e: e.tensor_scalar(out=out, in0=a, scalar1=s1, scalar2=None, op0=op0)
    return lambda e: e.tensor_scalar(out=out, in0=a, scalar1=s1, scalar2=s2, op0=op0, op1=op1)


def stt(out, a, s, b, op0, op1):
    return lambda e: e.scalar_tensor_tensor(out=out, in0=a, scalar=s, in1=b, op0=op0, op1=op1)


def stt_pool(out, a, colap):
    return lambda e: e.tensor_tensor(out=out, in0=a, in1=colap.to_broadcast([128, 128]), op=ALU.mult)


def cp(out, in_):
    return lambda e: e.tensor_copy(out=out, in_=in_)


def dmaf(out, in_):
    return lambda e: e.dma_start(out=out, in_=in_)


class _Item:
    __slots__ = ("thunk", "eng", "reads", "writes", "dur")

    def __init__(self, thunk, eng, reads, writes, dur):
        self.thunk, self.eng, self.reads, self.writes, self.dur = thunk, eng, tuple(reads), tuple(writes), dur


_DUR = {"act": 0.5, "dve": 0.45, "pool": 0.5}


def mk_recorders(S, ops):
    def Lop(eng, fn, reads=(), writes=()):
        ops.append(_Item(lambda: S.op(eng, fn, reads=reads, writes=writes), eng, reads, writes, _DUR.get(eng, 0.4)))

    def Lgr(eng, fns, reads=(), writes=()):
        ops.append(_Item(lambda: S.group(eng, fns, reads=reads, writes=writes), eng, reads, writes, 0.1 + 0.13 * len(fns)))

    def Ldma(eng, slot, fn, reads=(), writes=()):
        ops.append(_Item(lambda: S.dma(eng, slot, fn, reads=reads, writes=writes), "q_" + eng, reads, writes, 2.5))
    return Lop, Lgr, Ldma


def zipper(lists):
    lists = [l for l in lists if l]
    idx = [0] * len(lists)
    if os.environ.get("ZIP", "rr") == "rr":
        live = True
        while live:
            live = False
            for i, l in enumerate(lists):
                if idx[i] < len(l):
                    it = l[idx[i]]
                    idx[i] += 1
                    live = True
                    if isinstance(it, _Item):
                        it.thunk()
                    else:
                        it()
        return
    t_eng, t_w, t_r = {}, {}, {}
    remaining = sum(len(l) for l in lists)
    while remaining:
        best = None
        for i, l in enumerate(lists):
            if idx[i] >= len(l):
                continue
            it = l[idx[i]]
            if not isinstance(it, _Item):
                best = (-1.0, i, it)
                break
            rdy = t_eng.get(it.eng, 0.0)
            for k in it.reads:
                rdy = max(rdy, t_w.get(k, 0.0))
            for k in it.writes:
                rdy = max(rdy, t_w.get(k, 0.0), t_r.get(k, 0.0))
            if best is None or rdy < best[0]:
                best = (rdy, i, it)
        rdy, i, it = best
        idx[i] += 1
        remaining -= 1
        if not isinstance(it, _Item):
            it()
            continue
        it.thunk()
        fin = rdy + it.dur
        if it.eng.startswith("q_"):
            t_eng[it.eng] = rdy + 0.1
        else:
            t_eng[it.eng] = fin
        for k in it.reads:
            t_r[k] = max(t_r.get(k, 0.0), fin)
        for k in it.writes:
            t_w[k] = fin
            t_r[k] = 0.0


class Bump:
    def __init__(self, arena, start, limit):
        self.t, self.off, self.limit = arena, start, limit

    def alloc(self, dtype, shape):
        esz = 4 if dtype == F32 else 2
        n = 1
        for s in shape[1:]:
            n *= s
        nb = (n * esz + 63) // 64 * 64
        o = self.off
        self.off += nb
        assert self.off <= self.limit, ("SBUF arena overflow", self.off, self.limit)
        ap = self.t[:, o:o + n * esz].bitcast(dtype)
        if len(shape) == 3:
            ap = ap.rearrange("p (a b) -> p a b", a=shape[1])
        elif len(shape) == 4:
            ap = ap.rearrange("p (a b c) -> p a b c", a=shape[1], b=shape[2])
        return ap


def build_program():
    from contextlib import ExitStack
    nc = bass.Bass("TRN2", target_bir_lowering=False)

    def din(name, shape):
        return nc.dram_tensor(name, shape, F32, kind="ExternalInput").ap()

    def dout(name, shape):
        return nc.dram_tensor(name, shape, F32, kind="ExternalOutput").ap()

    x_p = din("x_p", [T_P, 1024]); x_s = din("x_s", [T_S, 1024])
    st_conv = din("st_conv", [T_S, 3, 1536]); st_gdn = din("st_gdn", [T_S, 4, 128, 128])
    p_p = din("p_p", [T_P, 256]); p_s = din("p_s", [T_S, 256])
    g_mix = din("g_mix", [1, 1024]); w_in = din("w_in", [1024, D_IN]); w_conv = din("w_conv", [4, 1536])
    a_log = din("a_log", [1, 4]); dt_bias = din("dt_bias", [1, 4]); gdn_norm = din("gdn_norm", [1, 128])
    ln_g = din("ln_g", [1, 512]); ln_b = din("ln_b", [1, 512]); w_s = din("w_s", [4, 128, 128]); b_s = din("b_s", [1, 512])
    w_out = din("w_out", [1024, 1024]); g_ff = din("g_ff", [8, 128]); w_up = din("w_up", [1024, 4096]); w_down = din("w_down", [4096, 1024])
    g_ple = din("g_ple", [8, 128]); w_ple = din("w_ple", [256, 1024]); w_gate = din("w_gate", [1024, 1024]); g_fin = din("g_fin", [8, 128])
    y_p = dout("y_p", [T_P, 1024]); y_s = dout("y_s", [T_S, 1024])
    ncp = dout("ncp", [3, 1536]); ngp = dout("ngp", [4, 128, 128])
    ncs = dout("ncs", [T_S, 3, 1536]); ngs = dout("ngs", [T_S, 4, 128, 128]); nsv = dout("nsv", [T_S, 512])

    w_in_v = w_in.rearrange("(k p) c -> p k c", p=128)
    w_out_v = w_out.rearrange("(k p) c -> p k c", p=128)
    w_up_v = w_up.rearrange("(k p) c -> p k c", p=128)
    w_down_v = w_down.rearrange("(k p) c -> p k c", p=128)
    w_gate_v = w_gate.rearrange("(k p) c -> p k c", p=128)
    w_ple_v = w_ple.rearrange("(k p) c -> p k c", p=128)

    with ExitStack() as st:
        S = Sched(nc, st)
        ARENA = 206 * 1024
        arena = st.enter_context(nc.sbuf_tensor("arena", [128, ARENA], U8))
        ps = [st.enter_context(nc.psum_tensor("ps%d" % i, [128, 512], F32)) for i in range(8)]
        psb = [p[:, :].bitcast(BF16) for p in ps]
        PK = [("ps", i) for i in range(8)]

        P = Bump(arena, 0, ARENA)
        ident_f = P.alloc(F32, [128, 128]); ident_b = P.alloc(BF16, [128, 128])
        ones_f = P.alloc(F32, [128, 128]); ones_b = P.alloc(BF16, [128, 128])
        mask_incl = P.alloc(F32, [128, 128])
        mask_su = P.alloc(F32, [128, 128])
        nmask_sl = P.alloc(F32, [128, 128])
        sel127 = P.alloc(F32, [128, 128])
        nmask_su = P.alloc(F32, [128, 128])
        rowstage = P.alloc(F32, [128, 128])
        cols = P.alloc(F32, [128, 128])
        wsT = P.alloc(BF16, [128, 4, 128])
        selws = P.alloc(BF16, [128, 4, 16])
        ws00 = P.alloc(F32, [128, 4])
        bs_row = P.alloc(F32, [128, 4, 128])
        lng_row = P.alloc(F32, [128, 512]); lnb_row = P.alloc(F32, [128, 512])
        alog_row = P.alloc(F32, [128, 4]); dtb_row = P.alloc(F32, [128, 4]); nexpA_row = P.alloc(F32, [128, 4])
        zcol = P.alloc(F32, [128, 4])
        zeros_f = P.alloc(F32, [128, 128])
        cat = P.alloc(BF16, [128, 8, T_ALL])
        P_C0 = P.off
        ba = P.alloc(F32, [128, NT, 8])
        beta_c = P.alloc(F32, [128, NT, 4]); g_c = P.alloc(F32, [128, NT, 4]); gc_c = P.alloc(F32, [128, NT, 4])
        bexp_c = P.alloc(F32, [128, NT, 4]); kd_c = P.alloc(F32, [128, NT, 4]); egl_c = P.alloc(F32, [128, NT, 4])
        tmp68 = P.alloc(F32, [128, NT, 4])
        qkv = P.alloc(BF16, [128, 12, T_ALL])
        zs = P.alloc(BF16, [128, 4, T_ALL])
        qks_f = P.alloc(F32, [128, 12, 16])
        histT = P.alloc(F32, [128, 12, 3, 16])
        ncp_st = P.alloc(F32, [128, 12, 3]); ncs_st = P.alloc(F32, [128, 12, 16])
        S_f = P.alloc(F32, [128, 4, 128]); S_b = P.alloc(BF16, [128, 4, 128])
        X0 = P.off
        XB = Bump(arena, X0, ARENA)
        hT = XB.alloc(BF16, [128, 8, T_ALL])
        Y0 = XB.off

        def wcol(j, c):
            return cols[:, 24 + j * 12 + c: 24 + j * 12 + c + 1]

        def gcol(which, m):
            return cols[:, which * 8 + m: which * 8 + m + 1]
        gdnn_col = cols[:, 72:73]
        negone = zcol[:, 1:2]

        S.op("pool", lambda e: e.memset(ones_f, 1.0), writes=["ones_f"])
        S.op("pool", lambda e: e.memset(ones_b, 1.0), writes=["ones_b"])
        S.op("pool", lambda e: e.memset(zcol, 0.0), writes=["zcol"])
        S.op("pool", lambda e: e.memset(zcol[:, 1:2], -1.0), reads=["zcol"], writes=["zcol"])
        S.op("pool", lambda e: e.memset(zeros_f, 0.0), writes=["zeros_f"])
        S.op("pool", lambda e: e.affine_select(out=ident_f, in_=ones_f, pattern=[[-1, 128]], compare_op=ALU.is_equal, fill=0.0, base=0, channel_multiplier=1), reads=["ones_f"], writes=["ident_f"])
        S.op("pool", lambda e: e.affine_select(out=mask_incl, in_=ones_f, pattern=[[1, 128]], compare_op=ALU.is_ge, fill=0.0, base=0, channel_multiplier=-1), reads=["ones_f"], writes=["mask_incl"])
        S.op("pool", lambda e: e.affine_select(out=mask_su, in_=ones_f, pattern=[[1, 128]], compare_op=ALU.is_gt, fill=0.0, base=0, channel_multiplier=-1), reads=["ones_f"], writes=["mask_su"])
        S.op("pool", lambda e: e.affine_select(out=nmask_sl, in_=ones_f, pattern=[[-1, 128]], compare_op=ALU.is_gt, fill=0.0, base=0, channel_multiplier=1), reads=["ones_f"], writes=["nmask_sl"])
        S.op("pool", ts(nmask_sl, nmask_sl, -1.0, ALU.mult), reads=["nmask_sl"], writes=["nmask_sl"])
        S.op("pool", ts(nmask_su, mask_su, -1.0, ALU.mult), reads=["mask_su"], writes=["nmask_su"])
        S.op("pool", lambda e: e.affine_select(out=sel127, in_=ones_f, pattern=[[0, 128]], compare_op=ALU.is_equal, fill=0.0, base=-127, channel_multiplier=1), reads=["ones_f"], writes=["sel127"])
        S.op("dve", cp(ident_b, ident_f), reads=["ident_f"], writes=["ident_b"])
        S.op("pool", lambda e: e.memset(rowstage, 0.0), writes=["rowstage"])
        S.dma("sp", "c0", [dmaf(rowstage[0:8, :], g_ff), dmaf(rowstage[8:16, :], g_ple), dmaf(rowstage[16:24, :], g_fin),
                           dmaf(rowstage[24:72, :], w_conv.rearrange("j (c p) -> (j c) p", p=128)), dmaf(rowstage[72:73, :], gdn_norm)],
              writes=["rowstage"])
        S.group("pe", [mm(ps[0][:, 0:128], rowstage, ident_f)], reads=["rowstage", "ident_f"], writes=[PK[0]])
        S.op("dve", cp(cols, ps[0][:, 0:128]), reads=[PK[0]], writes=["cols"])
        S.dma("sp", "c1", [dmaf(bs_row.rearrange("p h t -> p (h t)"), b_s.partition_broadcast(128)),
                           dmaf(lng_row, ln_g.partition_broadcast(128)), dmaf(lnb_row, ln_b.partition_broadcast(128)),
                           dmaf(alog_row, a_log.partition_broadcast(128)), dmaf(dtb_row, dt_bias.partition_broadcast(128)),
                           ] + [dmaf(ws00[:, h:h + 1], w_s[h, 0, 0:1].partition_broadcast(128)) for h in range(4)],
              writes=["rows"])
        S.op("act", actf(nexpA_row, alog_row, AF.Exp), reads=["rows"], writes=["nexpA"])
        S.op("dve", ts(nexpA_row, nexpA_row, -1.0, ALU.mult), reads=["nexpA"], writes=["nexpA"])
        for h in range(4):
            S.op("dve", ts(selws[0:16, h, :], ident_f[0:16, 0:16], ws00[0:16, h:h + 1], ALU.mult), reads=["rows", "ident_f"], writes=[("selws", h)])

        YA = Bump(arena, Y0, ARENA)
        wstmp = YA.alloc(F32, [128, 4, 128])
        S.dma("sp", "c2", dmaf(wstmp, w_s.rearrange("h t s -> t h s")), writes=["wstmp"])
        for h in range(4):
            S.op("pool", lambda e, h=h: e.affine_select(out=wstmp[:, h, :], in_=wstmp[:, h, :], pattern=[[-1, 128]], compare_op=ALU.is_ge, fill=0.0, base=0, channel_multiplier=1),
                 reads=["wstmp"], writes=["wstmp"])
        S.group("pe", [mm(ps[1][:, h * 128:(h + 1) * 128], wstmp[:, h, :], ident_f) for h in range(4)], reads=["wstmp", "ident_f"], writes=[PK[1]])
        S.op("dve", cp(wsT.rearrange("p h t -> p (h t)"), ps[1][:, 0:512]), reads=[PK[1]], writes=["wsT"])

        gmix_row = YA.alloc(F32, [128, 1024])
        S.dma("sp", "c3", dmaf(gmix_row, g_mix.partition_broadcast(128)), writes=["gmix"])
        xt = [YA.alloc(F32, [128, 1024]) for _ in range(3)]
        xsq = [YA.alloc(F32, [128, 1024]) for _ in range(3)]
        xn = [YA.alloc(BF16, [128, 1024]) for _ in range(3)]
        stat = YA.alloc(F32, [128, NT, 2])

        def phaseA_tile(i):
            ops = []
            Lop, Lgr, Ldma = mk_recorders(S, ops)
            r = 128 if i < 16 else 16
            sl = i % 3
            src = x_p[i * 128:(i + 1) * 128, :] if i < 16 else x_s
            Ldma("sp", "xt%d" % sl, dmaf(xt[sl][0:r, :], src), writes=[("xt", sl)])
            Lop("act", actf(xsq[sl][0:r, :], xt[sl][0:r, :], AF.Square), reads=[("xt", sl)], writes=[("xsq", sl)])
            Lop("dve", lambda e: e.reduce_sum(out=stat[0:r, i, 0:1], in_=xsq[sl][0:r, :], axis=AX.X), reads=[("xsq", sl)], writes=[("stat", i)])
            Lop("dve", ts(stat[0:r, i, 1:2], stat[0:r, i, 0:1], 1.0 / 1024, ALU.mult, EPS, ALU.add), reads=[("stat", i)], writes=[("stat", i)])
            Lop("act", actf(stat[0:r, i, 1:2], stat[0:r, i, 1:2], AF.Sqrt), reads=[("stat", i)], writes=[("stat", i)])
            Lop("dve", lambda e: e.reciprocal(out=stat[0:r, i, 1:2], in_=stat[0:r, i, 1:2]), reads=[("stat", i)], writes=[("stat", i)])
            Lop("dve", stt(xn[sl][0:r, :], xt[sl][0:r, :], stat[0:r, i, 1:2], gmix_row[0:r, :], ALU.mult, ALU.mult),
                reads=[("xt", sl), ("stat", i), "gmix"], writes=[("xn", sl)])
            b = i % 3
            Lgr("pe", [lambda e, k=k: e.transpose(out=psb[b][:, k * 128:k * 128 + r], in_=xn[sl][0:r, k * 128:(k + 1) * 128], identity=ident_b[0:r, 0:r]) for k in range(8)],
                reads=[("xn", sl), "ident_b"], writes=[PK[b]])
            if i % 2 == 0:
                Lop("act", actf(hT[:, :, i * 128:i * 128 + r], psb[b].rearrange("p (k t) -> p k t", k=8)[:, :, 0:r], AF.Copy), reads=[PK[b]], writes=[("hT", i)])
            else:
                Lop("dve", cp(hT[:, :, i * 128:i * 128 + r], psb[b].rearrange("p (k t) -> p k t", k=8)[:, :, 0:r]), reads=[PK[b]], writes=[("hT", i)])
            return ops

        tilesA = [phaseA_tile(i) for i in range(NT)]
        for g0 in range(0, NT, 3):
            zipper(tilesA[g0:g0 + 3])
        S.barrier()

        YB = Bump(arena, Y0, ARENA)
        wb = [YB.alloc(BF16, [128, 8, 512]) for _ in range(2)]
        wb8 = YB.alloc(BF16, [128, 8, 8])
        lnst = YB.alloc(F32, [128, NT, 8])
        sct = [YB.alloc(F32, [128, 3, 128]) for _ in range(2)]
        YB_MID = YB.off
        vg = YB.alloc(BF16, [128, NT, 512])
        F6 = YB.alloc(F32, [128, 6, 512])
        f512 = [F6[:, i, :] for i in range(6)]
        vgs_f = f512[5]
        YB5 = Bump(arena, YB_MID, ARENA)
        NBS = 4
        pre = [YB5.alloc(F32, [128, 515]) for _ in range(NBS)]
        accb = [YB5.alloc(F32, [128, 512]) for _ in range(NBS)]
        rnb = [YB5.alloc(F32, [128, 512]) for _ in range(NBS)]
        sqb = [YB5.alloc(BF16, [128, 512]) for _ in range(NBS)]
        wdiag = [YB5.alloc(F32, [128, 4, 128]) for _ in range(2)]
        YB6 = Bump(arena, YB_MID, ARENA)
        stage_tok = YB6.alloc(F32, [128, 1536])
        stage2 = YB6.alloc(F32, [128, 1536])
        wb_n = [0]
        bank_n = [0]

        def next_bank(lo=0, hi=4):
            b = lo + bank_n[0] % (hi - lo)
            bank_n[0] += 1
            return b

        wb_seq = [C_VS, C_U, C_Z, 0, 512, 1024]
        wb_issued = [0]

        def load_w(view, c0, ncol=512):
            idx = wb_n[0]
            wb_n[0] += 1
            assert wb_seq[idx] == c0
            while wb_issued[0] < min(len(wb_seq), idx + 2):
                j = wb_issued[0]
                S.dma("pool", "wb%d" % (j % 2), dmaf(wb[j % 2][:, :, 0:512], view[:, :, wb_seq[j]:wb_seq[j] + 512]), writes=[("wb", j % 2)])
                wb_issued[0] += 1
            return idx % 2

        hT_keys = [("hT", i) for i in range(NT)]

        def seg_hT_keys(t0, n):
            return [("hT", i) for i in range(t0 // 128, (t0 + n + 127) // 128)]

        sl = load_w(w_in_v, C_VS)

        def vsgu_tile(i):
            ops = []
            Lop, Lgr, Ldma = mk_recorders(S, ops)
            r = 128 if i < 16 else 16
            b = (i % 3)
            Lgr("pe", [mm(ps[b][0:r, :], hT[:, k, i * 128:i * 128 + r], wb[sl][:, k, :], start=(k == 0), stop=(k == 7)) for k in range(8)],
                    reads=[("hT", i), ("wb", sl)], writes=[PK[b]])
            g1 = f512[i % 3]; g2 = f512[3 + i % 3]
            Lop("act", actf(g1[0:r, :], ps[b][0:r, :], AF.Gelu_apprx_tanh), reads=[PK[b]], writes=[("g1", i % 3)])
            Lop("pool", tt(g2[0:r, :], g1[0:r, :], g1[0:r, :], ALU.mult), reads=[("g1", i % 3)], writes=[("g2", i % 3)])
            Lop("dve", lambda e, i=i, r=r, g1=g1: e.reduce_sum(out=lnst[0:r, i, 0:1], in_=g1[0:r, :], axis=AX.X), reads=[("g1", i % 3)], writes=[("lnst", i)])
            Lop("dve", lambda e, i=i, r=r, g2=g2: e.reduce_sum(out=lnst[0:r, i, 1:2], in_=g2[0:r, :], axis=AX.X), reads=[("g2", i % 3)], writes=[("lnst", i)])
            L = lambda a, bb: lnst[0:r, i, a:bb]
            Lop("dve", ts(L(2, 3), L(0, 1), 1.0 / 512, ALU.mult), reads=[("lnst", i)], writes=[("lnst", i)])
            Lop("dve", tt(L(3, 4), L(2, 3), L(2, 3), ALU.mult), reads=[("lnst", i)], writes=[("lnst", i)])
            Lop("dve", stt(L(4, 5), L(1, 2), 1.0 / 512, L(3, 4), ALU.mult, ALU.subtract), reads=[("lnst", i)], writes=[("lnst", i)])
            Lop("dve", ts(L(4, 5), L(4, 5), EPS, ALU.add), reads=[("lnst", i)], writes=[("lnst", i)])
            Lop("act", actf(L(4, 5), L(4, 5), AF.Sqrt), reads=[("lnst", i)], writes=[("lnst", i)])
            Lop("dve", lambda e, i=i, r=r: e.reciprocal(out=lnst[0:r, i, 5:6], in_=lnst[0:r, i, 4:5]), reads=[("lnst", i)], writes=[("lnst", i)])
            Lop("dve", ts(g2[0:r, :], g1[0:r, :], L(2, 3), ALU.subtract, L(5, 6), ALU.mult), reads=[("g1", i % 3), ("lnst", i)], writes=[("g2", i % 3)])
            Lop("pool", tt(g2[0:r, :], g2[0:r, :], lng_row[0:r, :], ALU.mult), reads=[("g2", i % 3), "rows"], writes=[("g2", i % 3)])
            if i < 16:
                Lop("pool", tt(vg[0:r, i, :], g2[0:r, :], lnb_row[0:r, :], ALU.add), reads=[("g2", i % 3), "rows"], writes=[("vg", i)])
            else:
                Lop("pool", tt(vgs_f[0:r, :], g2[0:r, :], lnb_row[0:r, :], ALU.add), reads=[("g2", i % 3), "rows"], writes=[("g2", 2)])
                Lop("pool", cp(vg[0:r, i, :], vgs_f[0:r, :]), reads=[("g2", 2)], writes=[("vg", i)])
                Ldma("sp", "o_nsv", dmaf(nsv, vgs_f[0:r, :]), reads=[("g2", 2)])
            return ops

        tilesV = [vsgu_tile(i) for i in range(NT)]
        for g0 in range(0, NT, 3):
            zipper(tilesV[g0:g0 + 3])

        S.dma("pool", "wb8", dmaf(wb8, w_in_v[:, :, C_BA:C_BA + 8]), writes=["wb8"])
        S.op("pool", lambda e: e.memset(ba, 0.0), writes=["ba"])
        bq = 4
        for i in range(NT):
            r = 128 if i < 16 else 16
            S.group("pe", [mm(ps[bq][0:r, i * 8:(i + 1) * 8], hT[:, k, i * 128:i * 128 + r], wb8[:, k, :], start=(k == 0), stop=(k == 7)) for k in range(8)],
                    reads=[("hT", i), "wb8"], writes=[PK[bq]])
        S.op("dve", cp(ba[:, 0:16, :], ps[bq][:, 0:128].rearrange("p (i c) -> p i c", c=8)), reads=[PK[bq], "ba"], writes=["ba"])
        S.op("dve", cp(ba[0:16, 16, :], ps[bq][0:16, 128:136]), reads=[PK[bq], "ba"], writes=["ba"])
        S.op("act", actf(beta_c, ba[:, :, 0:4], AF.Sigmoid), reads=["ba"], writes=["beta_c"])
        S.op("dve", tt(tmp68, ba[:, :, 4:8], dtb_row.unsqueeze(1).to_broadcast([128, NT, 4]), ALU.add), reads=["ba", "rows"], writes=["tmp68"])
        S.op("act", actf(tmp68, tmp68, AF.Exp), reads=["tmp68"], writes=["tmp68"])
        S.op("act", actf(tmp68, tmp68, AF.Ln, bias=1.0), reads=["tmp68"], writes=["tmp68"])
        S.op("dve", tt(g_c, tmp68, nexpA_row.unsqueeze(1).to_broadcast([128, NT, 4]), ALU.mult), reads=["tmp68", "nexpA"], writes=["g_c"])
        g68 = g_c.rearrange("p i h -> p (i h)"); gc68 = gc_c.rearrange("p i h -> p (i h)")
        S.group("pe", [mm(ps[5][:, 0:68], mask_incl, g68)], reads=["mask_incl", "g_c"], writes=[PK[5]])
        S.op("dve", cp(gc68, ps[5][:, 0:68]), reads=[PK[5]], writes=["gc_c"])
        S.group("pe", [mm(ps[5][:, 128:196], sel127, gc68)], reads=["sel127", "gc_c"], writes=[PK[5]])
        S.op("dve", cp(egl_c.rearrange("p i h -> p (i h)"), ps[5][:, 128:196]), reads=[PK[5]], writes=["egl_c"])
        S.op("dve", tt(tmp68.rearrange("p i h -> p (i h)"), egl_c.rearrange("p i h -> p (i h)"), gc68, ALU.subtract), reads=["egl_c", "gc_c"], writes=["tmp68"])
        S.op("act", actf(egl_c, egl_c, AF.Exp), reads=["egl_c", "tmp68"], writes=["egl_c"])
        S.op("act", actf(kd_c, tmp68, AF.Exp), reads=["tmp68"], writes=["kd_c"])
        S.op("act", actf(bexp_c, gc_c, AF.Exp), reads=["gc_c"], writes=["bexp_c"])
        S.op("dve", tt(bexp_c, bexp_c, beta_c, ALU.mult), reads=["bexp_c", "beta_c"], writes=["bexp_c"])

        sl = load_w(w_in_v, C_U)
        for h in range(4):
            for (t0, n) in SEGS:
                b = next_bank()
                S.group("pe", [mm(ps[b][:, 0:n], wb[sl][:, k, h * 128:(h + 1) * 128], hT[:, k, t0:t0 + n], start=(k == 0), stop=(k == 7)) for k in range(8)],
                        reads=seg_hT_keys(t0, n) + [("wb", sl)], writes=[PK[b]])
                u = f512[bank_n[0] % 2]
                S.op("act", actf(u[:, 0:n], ps[b][:, 0:n], AF.Gelu_apprx_tanh), reads=[PK[b]], writes=[("u", bank_n[0] % 2)])
                b2 = 4 + bank_n[0] % 2
                if n == 512:
                    tiles = [t0 // 128 + j for j in range(4)]
                    S.group("pe", [mm(ps[b2][:, j * 128:(j + 1) * 128], vg[:, tiles[j], h * 128:(h + 1) * 128], wsT[:, h, :]) for j in range(4)],
                            reads=[("vg", ti) for ti in tiles] + ["wsT"], writes=[PK[b2]])
                    S.op("dve", tt(f512[4][:, :].rearrange("p (j t) -> p j t", j=4), ps[b2][:, :].rearrange("p (j t) -> p j t", j=4),
                                   bs_row[:, h:h + 1, :].to_broadcast([128, 4, 128]), ALU.add), reads=[PK[b2], "rows"], writes=["mixt"])
                else:
                    S.group("pe", [mm(ps[b2][:, 0:16], vg[0:16, 16, h * 128:(h + 1) * 128], selws[0:16, h, :])],
                            reads=[("vg", 16), ("selws", h)], writes=[PK[b2]])
                    S.op("dve", tt(f512[4][:, 0:16], ps[b2][:, 0:16], bs_row[:, h, 0:1].to_broadcast([128, 16]), ALU.add), reads=[PK[b2], "rows"], writes=["mixt"])
                S.op("pool", tt(cat[:, 4 + h, t0:t0 + n], f512[4][:, 0:n], u[:, 0:n], ALU.mult), reads=["mixt", ("u", bank_n[0] % 2)], writes=[("cat", 4 + h, t0)])

        sl = load_w(w_in_v, C_Z)
        for h in range(4):
            for (t0, n) in SEGS:
                b = next_bank()
                S.group("pe", [mm(ps[b][:, 0:n], wb[sl][:, k, h * 128:(h + 1) * 128], hT[:, k, t0:t0 + n], start=(k == 0), stop=(k == 7)) for k in range(8)],
                        reads=seg_hT_keys(t0, n) + [("wb", sl)], writes=[PK[b]])
                S.op("act", actf(zs[:, h, t0:t0 + n], ps[b][:, 0:n], AF.Silu), reads=[PK[b]], writes=[("zs", h, t0)])

        S.barrier()

        def qkv_unit(blk, h, si, ui, wsl):
            ops = []
            Lop, Lgr, Ldma = mk_recorders(S, ops)
            c = blk * 4 + h
            t0, n = SEGS[si]
            bs = ui % NBS
            pr, acc, sq, rn = pre[bs], accb[bs], sqb[bs], rnb[bs]
            kp, ka, ks, kr = ("pre", bs), ("acc", bs), ("sq", bs), ("rn", bs)
            b = ui % 4; bn = 4 + ui % 4
            Lgr("pe", [mm(ps[b][:, 0:n], wb[wsl][:, k, h * 128:(h + 1) * 128], hT[:, k, t0:t0 + n], start=(k == 0), stop=(k == 7)) for k in range(8)],
                reads=[("wb", wsl)], writes=[PK[b]])
            if si == 0:
                Lop("dve", lambda e: e.memset(pr[:, 0:3], 0.0), writes=[kp])
            elif si < 4:
                Lgr("pe", [mm(ps[bn][:, 0:3], wb[wsl][:, k, h * 128:(h + 1) * 128], hT[:, k, t0 - 3:t0], start=(k == 0), stop=(k == 7)) for k in range(8)],
                    reads=[("wb", wsl)], writes=[PK[bn]])
                Lop("dve", cp(pr[:, 0:3], ps[bn][:, 0:3]), reads=[PK[bn]], writes=[kp])
            Lop("act", actf(pr[:, 3:3 + n], ps[b][:, 0:n], AF.Copy), reads=[PK[b], kp], writes=[kp])
            if si == 3:
                Lop("dve", cp(ncp_st[:, c, :], pr[:, 512:515]), reads=[kp], writes=[("ncp_st", c)])
            if si < 4:
                wd = wdiag[c % 2]
                bc_ = 4 + ui % 4
                fns = []
                for t4 in range(4):
                    for j in range(4):
                        fns.append(mm(ps[bc_][:, t4 * 128:(t4 + 1) * 128], wd[:, j, :], pr[:, j + t4 * 128:j + (t4 + 1) * 128], start=(j == 0), stop=(j == 3)))
                Lgr("pe", fns, reads=[kp, ("wdiag", c % 2)], writes=[PK[bc_]])
                Lop("act", actf(acc[:, 0:n], ps[bc_][:, 0:n], AF.Silu), reads=[PK[bc_]], writes=[ka])
            else:
                Lop("dve", cp(ncs_st[:, c, :], pr[:, 3:19]), reads=[kp], writes=[("ncs_st", c)])
                Lop("act", actf(acc[:, 0:n], pr[:, 3:3 + n], AF.Copy, scale=wcol(3, c)), reads=[kp, "cols"], writes=[ka])
                for j in (2, 1, 0):
                    Lop("dve", stt(acc[:, 0:n], histT[:, c, j, :], wcol(j, c), acc[:, 0:n], ALU.mult, ALU.add), reads=[("histT", c), ka, "cols"], writes=[ka])
                Lop("act", actf(acc[:, 0:n], acc[:, 0:n], AF.Silu), reads=[ka], writes=[ka])
            if blk == 2:
                Lop("pool", cp(qkv[:, c, t0:t0 + n], acc[:, 0:n]), reads=[ka], writes=[("qkv", c, t0)])
                if si == 4:
                    Lop("pool", cp(qks_f[:, c, :], acc[:, 0:16]), reads=[ka], writes=[("qks_f", c)])
            else:
                Lop("pool", tt(sq[:, 0:n], acc[:, 0:n], acc[:, 0:n], ALU.mult), reads=[ka], writes=[ks])
                Lgr("pe", [mm(ps[bn][:, 0:n], ones_b, sq[:, 0:n])], reads=[ks, "ones_b"], writes=[PK[bn]])
                Lop("act", actf(rn[:, 0:n], ps[bn][:, 0:n], AF.Sqrt, bias=1e-6), reads=[PK[bn]], writes=[kr])
                Lop("dve", lambda e: e.reciprocal(out=rn[:, 0:n], in_=rn[:, 0:n]), reads=[kr], writes=[kr])
                scl = (128.0 ** -0.5) if blk == 0 else 1.0
                Lop("dve", stt(qkv[:, c, t0:t0 + n], acc[:, 0:n], scl, rn[:, 0:n], ALU.mult, ALU.mult), reads=[ka, kr], writes=[("qkv", c, t0)])
                if si == 4:
                    Lop("dve", stt(qks_f[:, c, :], acc[:, 0:16], scl, rn[:, 0:16], ALU.mult, ALU.mult), reads=[ka, kr], writes=[("qks_f", c)])
            return ops

        ui = 0
        for blk in range(3):
            wsl = load_w(w_in_v, blk * 512)
            units = []
            for h in range(4):
                c = blk * 4 + h
                scs = sct[c % 2]
                S.dma("sp", "sct%d" % (c % 2), dmaf(scs[0:16, :, :], st_conv[:, :, c * 128:(c + 1) * 128]), writes=[("sct", c % 2)])
                S.group("pe", [mm(ps[4 + c % 4][:, j * 16:(j + 1) * 16], scs[0:16, j, :], ident_f[0:16, 0:16]) for j in range(3)],
                        reads=[("sct", c % 2), "ident_f"], writes=[PK[4 + c % 4]])
                S.op("dve", cp(histT[:, c, :, :], ps[4 + c % 4][:, 0:48].rearrange("p (j b) -> p j b", j=3)), reads=[PK[4 + c % 4]], writes=[("histT", c)])
                for si in range(5):
                    uo = qkv_unit(blk, h, si, ui, wsl)
                    if si == 0:
                        pre_ops = [(lambda j=j, c=c: S.op("pool", stt_pool(wdiag[c % 2][:, j, :], ident_f, wcol(j, c)), reads=["ident_f", "cols"], writes=[("wdiag", c % 2)])) for j in range(4)]
                        uo = pre_ops + uo
                    units.append(uo)
                    ui += 1
            for g0 in range(0, len(units), 4):
                zipper(units[g0:g0 + 4])

        S.barrier()
        XS = Bump(arena, X0, ARENA)
        S_all = XS.alloc(F32, [128, 64, 128])
        if 'sample' not in os.environ.get('KSKIP', ''):
            for q4 in range(4):
                S.dma("sp", "sall%d" % q4, dmaf(S_all[:, q4 * 16:(q4 + 1) * 16, :], st_gdn[q4 * 4:(q4 + 1) * 4].rearrange("b h d e -> d (b h) e")), writes=[("S_all", q4)])
        S.group("pe", [mm(ps[c // 4][0:3, (c % 4) * 128:(c % 4 + 1) * 128], ncp_st[:, c, :], ident_f) for c in range(12)],
                reads=[("ncp_st", c) for c in range(12)] + ["ident_f"], writes=[PK[0], PK[1], PK[2]])
        for q3 in range(3):
            S.op("dve", cp(stage_tok[0:3, q3 * 512:(q3 + 1) * 512], ps[q3][0:3, :]), reads=[PK[q3]], writes=["stage_tok"])
        S.dma("sp", "o_ncp", dmaf(ncp, stage_tok[0:3, :]), reads=["stage_tok"])
        S.group("pe", [mm(ps[c // 4][0:16, (c % 4) * 128:(c % 4 + 1) * 128], ncs_st[:, c, :], ident_f) for c in range(12)],
                reads=[("ncs_st", c) for c in range(12)] + ["ident_f"], writes=[PK[0], PK[1], PK[2]])
        for q3 in range(3):
            S.op("act", actf(stage2[0:16, q3 * 512:(q3 + 1) * 512], ps[q3][0:16, :], AF.Copy), reads=[PK[q3]], writes=["stage2"])
        S.dma("sp", "o_ncs", [dmaf(ncs[:, 2, :], stage2[0:16, :]), dmaf(ncs[:, 0:2, :], st_conv[:, 1:3, :])], reads=["stage2"])
        S.barrier()

        YG = Bump(arena, Y0, ARENA)
        osq = YG.alloc(BF16, [128, 512]); rn_o = YG.alloc(F32, [128, 512]); on_o = YG.alloc(F32, [128, 512])
        YG_EPI = YG.off
        NPW, DP = 4, 8
        CHDT = F32
        gN = lambda n_, dt_, shp: [YG.alloc(dt_, shp) for _ in range(n_)]
        Rs = gN(NPW, F32, [128, 256]); rhsR = gN(NPW, F32, [128, 256]); D0 = gN(NPW, F32, [128, 128]); E0 = gN(NPW, F32, [128, 128])
        EGr = gN(NPW, F32, [128, 128]); MB = gN(NPW, F32, [128, 128]); Qf = gN(NPW, F32, [128, 128]); Qs = gN(NPW, BF16, [128, 128])
        NNa = gN(NPW, CHDT, [128, 256]); NNb = gN(NPW, CHDT, [128, 256]); Xs = gN(NPW, BF16, [128, 128])
        NHa = gN(NPW, BF16, [128, 256]); NHb = gN(NPW, BF16, [128, 256])
        J0 = int(os.environ.get('GDN_J0', '6'))
        Qm = gN(DP, BF16, [128, 128]); attnT = gN(DP, BF16, [128, 128]); Kd = gN(DP, BF16, [128, 128]); Vb = gN(DP, BF16, [128, 128])
        qg = gN(DP, BF16, [128, 128]); nWT = gN(DP, BF16, [128, 128]); vn = gN(4, BF16, [128, 128])
        S.op("pool", lambda e: e.memset(S_f.rearrange("p h e -> p (h e)"), 0.0), writes=[("S_f", h) for h in range(4)])
        S.op("pool", lambda e: e.memset(S_b.rearrange("p h e -> p (h e)"), 0.0), writes=[("S_b", h) for h in range(4)])

        def gdn_P(n, h):
            ops = []
            Lop, Lgr, Ldma = mk_recorders(S, ops)
            u = n * 4 + h
            q = u % NPW; s = u % DP
            tok = slice(n * 128, (n + 1) * 128)
            kT = qkv[:, 4 + h, tok]; qT = qkv[:, h, tok]; vT = qkv[:, 8 + h, tok]
            col = lambda t: t[:, n, h:h + 1]
            K = lambda name: (name, q)
            H = lambda name: (name, s)
            bk = PK[q]; pb = ps[q]; pbb = psb[q]
            kk = pb[:, 256:384]; qk = pb[:, 384:512]
            Lop("pool", stt_pool(rhsR[q][:, 0:128], mask_incl, col(g_c)), reads=["mask_incl", "g_c"], writes=[K("rhsR")])
            Lop("pool", stt_pool(rhsR[q][:, 128:256], ident_f, col(beta_c)), reads=["ident_f", "beta_c"], writes=[K("rhsR")])
            Lgr("pe", [lambda e: e.transpose(out=pbb[:, 0:128], in_=kT, identity=ident_b),
                       lambda e: e.transpose(out=pbb[:, 128:256], in_=vT, identity=ident_b)], reads=[("qkv", n), "ident_b"], writes=[bk])
            ktok = pbb[:, 0:128]; vtok = pbb[:, 128:256]
            Lop("act", actf(Xs[q], ktok, AF.Copy, scale=col(bexp_c)), reads=[bk, "bexp_c"], writes=[K("Xs")])
            Lop("act", actf(Kd[s], ktok, AF.Copy, scale=col(kd_c)), reads=[bk, "kd_c"], writes=[H("Kd")])
            Lop("act", actf(Vb[s], vtok, AF.Copy, scale=col(beta_c)), reads=[bk, "beta_c"], writes=[H("Vb")])
            Lgr("pe", [mm(pb[:, 0:128], ones_f, rhsR[q][:, 0:128]), mm(pb[:, 128:256], ones_f, rhsR[q][:, 128:256]),
                       mm(kk, kT, kT), mm(qk, kT, qT)], reads=[K("rhsR"), "ones_f", ("qkv", n)], writes=[bk])
            Lop("dve", cp(Rs[q], pb[:, 0:256]), reads=[bk], writes=[K("Rs")])
            R_gc = Rs[q][:, 0:128]; R_be = Rs[q][:, 128:256]
            Lop("pool", lambda e: e.tensor_tensor(out=D0[q], in0=R_gc, in1=col(gc_c).to_broadcast([128, 128]), op=ALU.subtract), reads=[K("Rs"), "gc_c"], writes=[K("D0")])
            Lop("pool", ts(D0[q], D0[q], 0.0, ALU.min), reads=[K("D0")], writes=[K("D0")])
            Lop("act", actf(D0[q], D0[q], AF.Exp), reads=[K("D0")], writes=[K("D0")])
            Lop("act", actf(EGr[q], R_gc, AF.Exp), reads=[K("Rs")], writes=[K("EGr")])
            Lop("pool", tt(MB[q], R_be, D0[q], ALU.mult), reads=[K("Rs"), K("D0")], writes=[K("MB")])
            Lop("pool", tt(MB[q], MB[q], nmask_su, ALU.mult), reads=[K("MB"), "nmask_su"], writes=[K("MB")])
            Lop("pool", tt(D0[q], D0[q], mask_incl, ALU.mult), reads=[K("D0"), K("MB"), "mask_incl"], writes=[K("D0")])
            Lop("pool", tt(qg[s], qT, EGr[q], ALU.mult), reads=[("qkv", n), K("EGr")], writes=[H("qg")])
            Lop("dve", tt(NNa[q][:, 0:128], kk, MB[q], ALU.mult), reads=[bk, K("MB")], writes=[K("NNa")])
            Lop("dve", tt(attnT[s], qk, D0[q], ALU.mult), reads=[bk, K("D0")], writes=[H("attnT")])
            Lgr("pe", [mm(pb[:, 0:128], NNa[q][:, 0:128], ident_f)], reads=[K("NNa"), "ident_f"], writes=[bk])
            Lop("act", actf(NNa[q][:, 128:256], pb[:, 0:128], AF.Copy), reads=[bk], writes=[K("NNa")])
            Lop("pool", tt(Qf[q], ident_f, NNa[q][:, 0:128], ALU.add), reads=["ident_f", K("NNa")], writes=[K("Qf")])
            cur, nxt, kc, kn = NNa[q], NNb[q], K("NNa"), K("NNb")
            cur16, nxt16, kc16, kn16 = NHa[q], NHb[q], K("NHa"), K("NHb")
            if J0 == 0:
                Lop("pool", cp(cur16, cur), reads=[kc], writes=[kc16])
            for j in range(1, 7):
                f32lvl = j <= J0
                src, ksrc = (cur, kc) if f32lvl else (cur16, kc16)
                fns = []
                if j < 6:
                    fns.append(mm(pb[:, 0:128], src[:, 128:256], src[:, 0:128]))
                fns.append(mm(pb[:, 128:256], src[:, 0:128], src[:, 128:256]))
                Lgr("pe", fns, reads=[ksrc], writes=[bk])
                lo = 0 if j < 6 else 128
                if f32lvl:
                    Lop("act", actf(nxt[:, lo:256], pb[:, lo:256], AF.Copy), reads=[bk], writes=[kn])
                    if j == J0 and j < 6:
                        Lop("pool", cp(nxt16[:, lo:256], nxt[:, lo:256]), reads=[kn], writes=[kn16])
                    Lgr("pe", [mm(pb[:, 256:384], nxt[:, 128:256], Qf[q])], reads=[kn, K("Qf")], writes=[bk])
                else:
                    Lop("act", actf(nxt16[:, lo:256], pb[:, lo:256], AF.Copy), reads=[bk], writes=[kn16])
                    Lop("pool", cp(Qs[q], Qf[q]), reads=[K("Qf")], writes=[K("Qs")])
                    Lgr("pe", [mm(pb[:, 256:384], nxt16[:, 128:256], Qs[q])], reads=[kn16, K("Qs")], writes=[bk])
                Lop("dve", tt(Qf[q], Qf[q], pb[:, 256:384], ALU.add), reads=[bk, K("Qf")], writes=[K("Qf")])
                cur, nxt, kc, kn = nxt, cur, kn, kc
                cur16, nxt16, kc16, kn16 = nxt16, cur16, kn16, kc16
            Lop("pool", cp(Qm[s], Qf[q]), reads=[K("Qf")], writes=[H("Qm")])
            Lgr("pe", [mm(pb[:, 384:512], Xs[q], Qm[s])], reads=[K("Xs"), H("Qm")], writes=[bk])
            Lop("act", actf(nWT[s], pb[:, 384:512], AF.Copy, scale=negone), reads=[bk], writes=[H("nWT")])
            return ops

        def gdn_R(n, h):
            ops = []
            Lop, Lgr, Ldma = mk_recorders(S, ops)
            u = n * 4 + h
            s = u % DP
            H = lambda name: (name, s)
            col = lambda t: t[:, n, h:h + 1]
            bR = PK[4]; ob = 5 + n % 2
            V = ps[4][:, h * 128:(h + 1) * 128]
            Lgr("pe", [mm(V, Qm[s], Vb[s], start=True, stop=False),
                       mm(V, nWT[s], S_b[:, h, :], start=False, stop=True)],
                reads=[H("Qm"), H("Vb"), H("nWT"), ("S_b", h)], writes=[bR])
            Lop("dve", cp(vn[h], V), reads=[bR], writes=[("vn", h)])
            Lgr("pe", [mm(V, Kd[s], vn[h])], reads=[H("Kd"), ("vn", h)], writes=[bR])
            Lgr("pe", [mm(ps[ob][:, h * 128:(h + 1) * 128], S_b[:, h, :], qg[s], start=True, stop=False),
                       mm(ps[ob][:, h * 128:(h + 1) * 128], vn[h], attnT[s], start=False, stop=True)],
                reads=[("S_b", h), H("qg"), ("vn", h), H("attnT")], writes=[PK[ob]])
            Lop("dve", stt(S_f[:, h, :], S_f[:, h, :], col(egl_c), V, ALU.mult, ALU.add), reads=[bR, ("S_f", h), "egl_c"], writes=[("S_f", h)])
            Lop("act", actf(S_b[:, h, :], S_f[:, h, :], AF.Copy), reads=[("S_f", h)], writes=[("S_b", h)])
            return ops

        def gdn_epilogue(o_ps, ss_ps, ncol, t0, okeys, sskey):
            w = 4 * ncol
            S.op("act", actf(osq[:, 0:w], o_ps, AF.Square), reads=okeys, writes=["osq"])
            S.group("pe", [mm(ss_ps, ones_b, osq[:, 0:w])], reads=["osq", "ones_b"], writes=[sskey])
            S.op("dve", ts(rn_o[:, 0:w], ss_ps, 1.0 / 128, ALU.mult, EPS, ALU.add), reads=[sskey], writes=["rn_o"])
            S.op("act", actf(rn_o[:, 0:w], rn_o[:, 0:w], AF.Sqrt), reads=["rn_o"], writes=["rn_o"])
            S.op("dve", lambda e: e.reciprocal(out=rn_o[:, 0:w], in_=rn_o[:, 0:w]), reads=["rn_o"], writes=["rn_o"])
            S.op("dve", stt(on_o[:, 0:w], o_ps, gdnn_col, rn_o[:, 0:w], ALU.mult, ALU.mult), reads=okeys + ["rn_o", "cols"], writes=["on_o"])
            S.op("pool", tt(cat[:, 0:4, t0:t0 + ncol], on_o[:, 0:w].rearrange("p (h t) -> p h t", h=4), zs[:, :, t0:t0 + ncol], ALU.mult),
                 reads=["on_o", "zs"], writes=[("cat_o", t0)])

        _SK = os.environ.get('KSKIP', '')
        NCH = 0 if 'prompt' in _SK else int(os.environ.get('GDN_N', '16'))

        def epi_ops(n):
            ob = 5 + n % 2
            return [lambda: gdn_epilogue(ps[ob][:, :], ps[7][:, :], 128, n * 128, [PK[ob]], PK[7])]

        _DO_SAMPLE = 'sample' not in _SK
        def _sample_section():
            YG = Bump(arena, YG_EPI, ARENA)
            sv = YG.alloc(F32, [128, 8])
            rexp = YG.alloc(F32, [128, 8, 16])
            bcs = YG.alloc(F32, [128, 128])
            dcol = YG.alloc(F32, [128, 64])
            dtok = YG.alloc(F32, [128, 512]); ktoks = YG.alloc(F32, [128, 512])
            kmask = [YG.alloc(F32, [128, 512]) for _ in range(2)]
            S.op("dve", cp(sv[0:16, 0:4], beta_c[0:16, 16, :]), reads=["beta_c"], writes=["sv"])
            S.op("act", actf(sv[0:16, 4:8], g_c[0:16, 16, :], AF.Exp), reads=["g_c"], writes=["sv"])
            for j in range(8):
                S.op("dve", ts(rexp[0:16, j, :], ident_f[0:16, 0:16], sv[0:16, j:j + 1], ALU.mult), reads=["sv", "ident_f"], writes=["rexp"])
            S.group("pe", [mm(ps[0][:, 0:128], ones_f[0:16, :], rexp[0:16, :, :].rearrange("p j b -> p (j b)"))], reads=["rexp", "ones_f"], writes=[PK[0]])
            S.op("dve", cp(bcs, ps[0][:, 0:128]), reads=[PK[0]], writes=["bcs"])
            beta_bc = bcs[:, 0:64]; eg_bc = bcs[:, 64:128]
            S.group("pe", [mm(ps[1][:, h * 16 + b:h * 16 + b + 1], S_all[:, b * 4 + h, :], qks_f[:, 4 + h, b:b + 1]) for b in range(16) for h in range(4)],
                    reads=[("S_all", q4) for q4 in range(4)] + ["qks_f"], writes=[PK[1]])
            S.op("dve", tt(dcol, ps[1][:, 0:64], eg_bc, ALU.mult), reads=[PK[1], "bcs"], writes=["dcol"])
            S.op("dve", tt(dcol, qks_f[:, 8:12, :].rearrange("p h b -> p (h b)"), dcol, ALU.subtract), reads=["dcol", "qks_f"], writes=["dcol"])
            S.op("dve", tt(dcol, dcol, beta_bc, ALU.mult), reads=["dcol", "bcs"], writes=["dcol"])
            S.group("pe", [mm(ps[2][0:16, h * 128:(h + 1) * 128], dcol[:, h * 16:(h + 1) * 16], ident_f) for h in range(4)], reads=["dcol", "ident_f"], writes=[PK[2]])
            S.group("pe", [mm(ps[3][0:16, h * 128:(h + 1) * 128], qks_f[:, 4 + h, :], ident_f) for h in range(4)], reads=["qks_f", "ident_f"], writes=[PK[3]])
            S.op("dve", cp(dtok[0:16, :], ps[2][0:16, :]), reads=[PK[2]], writes=["dtok"])
            S.op("act", actf(ktoks[0:16, :], ps[3][0:16, :], AF.Copy), reads=[PK[3]], writes=["ktoks"])
            for b in range(16):
                km = kmask[b % 2]; pb = 4 + b % 2
                S.op("dve", ts(km[0:16, :], ktoks[0:16, :], ident_f[0:16, b:b + 1], ALU.mult), reads=["ktoks", "ident_f"], writes=[("kmask", b % 2)])
                S.group("pe", [mm(ps[pb][:, h * 128:(h + 1) * 128], km[0:16, h * 128:(h + 1) * 128], dtok[0:16, h * 128:(h + 1) * 128]) for h in range(4)],
                        reads=[("kmask", b % 2), "dtok"], writes=[PK[pb]])
                for h in range(4):
                    S.op("dve", stt(S_all[:, b * 4 + h, :], S_all[:, b * 4 + h, :], eg_bc[:, h * 16 + b:h * 16 + b + 1], ps[pb][:, h * 128:(h + 1) * 128], ALU.mult, ALU.add),
                         reads=[PK[pb], "bcs", ("S_all", b // 4)], writes=[("S_all", b // 4)])
            S.group("pe", [mm(ps[1][:, 64 + h * 16 + b:64 + h * 16 + b + 1], S_all[:, b * 4 + h, :], qks_f[:, h, b:b + 1]) for b in range(16) for h in range(4)],
                    reads=[("S_all", q4) for q4 in range(4)] + ["qks_f"], writes=[PK[1]])
            gdn_epilogue(ps[1][:, 64:128], ps[0][:, 128:192], 16, T_P, [PK[1]], PK[0])
            for q4 in range(4):
                S.dma("sp", "o_ngs%d" % q4, dmaf(ngs[q4 * 4:(q4 + 1) * 4].rearrange("b h d e -> d (b h) e"), S_all[:, q4 * 16:(q4 + 1) * 16, :]), reads=[("S_all", q4)])
        if _DO_SAMPLE:
            _sample_section()
        S.barrier()

        LB = Bump(arena, X0, ARENA)
        NL = 8
        lf32 = lambda shp: [LB.alloc(F32, shp) for _ in range(NL)]
        lbf = lambda shp: [LB.alloc(BF16, shp) for _ in range(NL)]
        Rs8 = lf32([128, 256]); rhsR8 = lf32([128, 256]); D08 = lf32([128, 128]); EGr8 = lf32([128, 128]); MB8 = lf32([128, 128]); Qf8 = lf32([128, 128])
        NNa8 = lf32([128, 256]); NNb8 = lf32([128, 256]); rn8 = lf32([128, 128]); on8 = lf32([128, 128])
        Xs8 = lbf([128, 128]); Kd8 = lbf([128, 128]); Vb8 = lbf([128, 128]); qg8 = lbf([128, 128]); at8 = lbf([128, 128])
        Qm8 = lbf([128, 128]); nWT8 = lbf([128, 128]); vn8 = lbf([128, 128]); osq8 = lbf([128, 128])
        r_done = {}

        def gdn_unit(n, h):
            ops = []
            Lop, Lgr, Ldma = mk_recorders(S, ops)
            L = h * 2 + n % 2
            tok = slice(n * 128, (n + 1) * 128)
            kT = qkv[:, 4 + h, tok]; qT = qkv[:, h, tok]; vT = qkv[:, 8 + h, tok]
            col = lambda t: t[:, n, h:h + 1]
            K = lambda name: (name, L)
            bk = PK[L]; pb = ps[L]; pbb = psb[L]
            kk = pb[:, 256:384]; qk = pb[:, 384:512]
            Rs, rhsR, D0, EGr, MB, Qf = Rs8[L], rhsR8[L], D08[L], EGr8[L], MB8[L], Qf8[L]
            Xs, Kd, Vb, qg, attnT, Qm, nWT, vn, osq = Xs8[L], Kd8[L], Vb8[L], qg8[L], at8[L], Qm8[L], nWT8[L], vn8[L], osq8[L]
            Lop("pool", stt_pool(rhsR[:, 0:128], mask_incl, col(g_c)), reads=["mask_incl", "g_c"], writes=[K("rhsR")])
            Lop("pool", stt_pool(rhsR[:, 128:256], ident_f, col(beta_c)), reads=["ident_f", "beta_c"], writes=[K("rhsR")])
            Lgr("pe", [lambda e: e.transpose(out=pbb[:, 0:128], in_=kT, identity=ident_b),
                       lambda e: e.transpose(out=pbb[:, 128:256], in_=vT, identity=ident_b)], reads=["ident_b"], writes=[bk])
            ktok = pbb[:, 0:128]; vtok = pbb[:, 128:256]
            Lop("act", actf(Xs, ktok, AF.Copy, scale=col(bexp_c)), reads=[bk, "bexp_c"], writes=[K("Xs")])
            Lop("act", actf(Kd, ktok, AF.Copy, scale=col(kd_c)), reads=[bk, "kd_c"], writes=[K("Kd")])
            Lop("act", actf(Vb, vtok, AF.Copy, scale=col(beta_c)), reads=[bk, "beta_c"], writes=[K("Vb")])
            Lgr("pe", [mm(pb[:, 0:128], ones_f, rhsR[:, 0:128]), mm(pb[:, 128:256], ones_f, rhsR[:, 128:256]),
                       mm(kk, kT, kT), mm(qk, kT, qT)], reads=[K("rhsR"), "ones_f"], writes=[bk])
            Lop("dve", cp(Rs, pb[:, 0:256]), reads=[bk], writes=[K("Rs")])
            R_gc = Rs[:, 0:128]; R_be = Rs[:, 128:256]
            Lop("pool", lambda e: e.tensor_tensor(out=D0, in0=R_gc, in1=col(gc_c).to_broadcast([128, 128]), op=ALU.subtract), reads=[K("Rs"), "gc_c"], writes=[K("D0")])
            Lop("pool", ts(D0, D0, 0.0, ALU.min), reads=[K("D0")], writes=[K("D0")])
            Lop("act", actf(D0, D0, AF.Exp), reads=[K("D0")], writes=[K("D0")])
            Lop("act", actf(EGr, R_gc, AF.Exp), reads=[K("Rs")], writes=[K("EGr")])
            Lop("pool", tt(MB, R_be, D0, ALU.mult), reads=[K("Rs"), K("D0")], writes=[K("MB")])
            Lop("pool", tt(MB, MB, nmask_su, ALU.mult), reads=[K("MB"), "nmask_su"], writes=[K("MB")])
            Lop("pool", tt(D0, D0, mask_incl, ALU.mult), reads=[K("D0"), K("MB"), "mask_incl"], writes=[K("D0")])
            Lop("pool", tt(qg, qT, EGr, ALU.mult), reads=[K("EGr")], writes=[K("qg")])
            NNa, NNb = NNa8[L], NNb8[L]
            Lop("dve", tt(NNa[:, 0:128], kk, MB, ALU.mult), reads=[bk, K("MB")], writes=[K("NNa")])
            Lop("dve", tt(attnT, qk, D0, ALU.mult), reads=[bk, K("D0")], writes=[K("attnT")])
            Lgr("pe", [mm(pb[:, 0:128], NNa[:, 0:128], ident_f)], reads=[K("NNa"), "ident_f"], writes=[bk])
            Lop("act", actf(NNa[:, 128:256], pb[:, 0:128], AF.Copy), reads=[bk], writes=[K("NNa")])
            Lop("pool", tt(Qf, ident_f, NNa[:, 0:128], ALU.add), reads=["ident_f", K("NNa")], writes=[K("Qf")])
            cur, nxt, kc, kn = NNa, NNb, K("NNa"), K("NNb")
            for j in range(1, 7):
                fns = []
                if j < 6:
                    fns.append(mm(pb[:, 0:128], cur[:, 128:256], cur[:, 0:128]))
                fns.append(mm(pb[:, 128:256], cur[:, 0:128], cur[:, 128:256]))
                Lgr("pe", fns, reads=[kc], writes=[bk])
                lo = 0 if j < 6 else 128
                Lop("act", actf(nxt[:, lo:256], pb[:, lo:256], AF.Copy), reads=[bk], writes=[kn])
                Lgr("pe", [mm(pb[:, 256:384], nxt[:, 128:256], Qf)], reads=[kn, K("Qf")], writes=[bk])
                Lop("dve", tt(Qf, Qf, pb[:, 256:384], ALU.add), reads=[bk, K("Qf")], writes=[K("Qf")])
                cur, nxt, kc, kn = nxt, cur, kn, kc
            Lop("pool", cp(Qm, Qf), reads=[K("Qf")], writes=[K("Qm")])
            Lgr("pe", [mm(pb[:, 384:512], Xs, Qm)], reads=[K("Xs"), K("Qm")], writes=[bk])
            Lop("act", actf(nWT, pb[:, 384:512], AF.Copy, scale=negone), reads=[bk], writes=[K("nWT")])
            V = pb[:, 384:512]; Oh = pb[:, 0:128]; SSh = pb[:, 128:256]

            def chk():
                assert n == 0 or r_done.get((n - 1, h)), ("emission order violated", n, h)
            ops.append(chk)
            Lgr("pe", [mm(V, Qm, Vb, start=True, stop=False), mm(V, nWT, S_b[:, h, :], start=False, stop=True)],
                reads=[K("Qm"), K("Vb"), K("nWT"), ("S_b", h)], writes=[bk])
            Lop("dve", cp(vn, V), reads=[bk], writes=[K("vn")])
            Lgr("pe", [mm(V, Kd, vn),
                       mm(Oh, S_b[:, h, :], qg, start=True, stop=False), mm(Oh, vn, attnT, start=False, stop=True)],
                reads=[K("Kd"), K("vn"), ("S_b", h), K("qg"), K("attnT")], writes=[bk])
            Lop("dve", stt(S_f[:, h, :], S_f[:, h, :], col(egl_c), V, ALU.mult, ALU.add), reads=[bk, ("S_f", h), "egl_c"], writes=[("S_f", h)])
            Lop("act", actf(S_b[:, h, :], S_f[:, h, :], AF.Copy), reads=[("S_f", h)], writes=[("S_b", h)])

            def mark():
                r_done[(n, h)] = True
            ops.append(mark)
            rn, on = rn8[L], on8[L]
            Lop("dve", cp(on, Oh), reads=[bk], writes=[K("on")])
            Lop("act", actf(osq, on, AF.Square), reads=[K("on")], writes=[K("osq")])
            Lgr("pe", [mm(SSh, ones_b, osq)], reads=[K("osq"), "ones_b"], writes=[bk])
            Lop("dve", ts(rn, SSh, 1.0 / 128, ALU.mult, EPS, ALU.add), reads=[bk], writes=[K("rn")])
            Lop("act", actf(rn, rn, AF.Sqrt), reads=[K("rn")], writes=[K("rn")])
            Lop("dve", lambda e: e.reciprocal(out=rn, in_=rn), reads=[K("rn")], writes=[K("rn")])
            Lop("dve", stt(on, on, gdnn_col, rn, ALU.mult, ALU.mult), reads=[K("on"), K("rn"), "cols"], writes=[K("on")])
            Lop("pool", tt(cat[:, h, tok], on, zs[:, h, tok], ALU.mult), reads=[K("on")], writes=[("cat_o", n, h)])
            return ops

        if NCH:
            u0 = gdn_unit(0, 0)
            LU = len(u0)
            STAG8 = int(os.environ.get('GDN_STAG', '7'))
            lanes = []
            for h in range(4):
                for par in range(2):
                    pad = h * STAG8 + par * (LU // 2)
                    lane = [(lambda: None)] * pad
                    for n in range(par, NCH, 2):
                        lane = lane + gdn_unit(n, h)
                    lanes.append(lane)
            zipper(lanes)
        S.dma("sp", "o_ngp", dmaf(ngp.rearrange("h d e -> d h e"), S_f), reads=[("S_f", h) for h in range(4)])

        S.barrier()

        if 'phasec' in _SK:
            S.finish()
            with nc.Block() as block:
                S.replay(block)
            return nc
        YC = Bump(arena, P_C0, ARENA)
        R = YC.alloc(F32, [128, 8, 528]); xnC = YC.alloc(BF16, [128, 8, 528]); hid = YC.alloc(BF16, [128, 32, 528])
        r8 = [YC.alloc(BF16, [128, 8, 512]) for _ in range(3)]
        r16 = [YC.alloc(BF16, [128, 32, 256]) for _ in range(2)]
        xres = YC.alloc(F32, [128, 4, 1024]); xres_s = YC.alloc(F32, [128, 1024])
        pw = YC.alloc(BF16, [128, 2, 1024]); ptok = [YC.alloc(BF16, [128, 256]) for _ in range(2)]
        pT = YC.alloc(BF16, [128, 2, 528]); sqr = [YC.alloc(BF16, [128, 528]) for _ in range(2)]; rnC = YC.alloc(F32, [128, 528])
        sig = [YC.alloc(F32, [128, 528]) for _ in range(2)]; relu_t = [YC.alloc(F32, [128, 528]) for _ in range(2)]
        ytile = [YC.alloc(F32, [128, 1024]) for _ in range(1)]
        rncol = YC.alloc(F32, [128, 8])
        r8_n = [0]; r16_n = [0]; misc_n = [0]

        r8_seq = []
        for _p in range(4):
            r8_seq += [(w_out_v, 0), (w_out_v, 512)] + [(w_up_v, bb * 512) for bb in range(8)] + [(w_gate_v, 0), (w_gate_v, 512)]
        r8_issued = [0]

        def load_r8(view, c0):
            idx = r8_n[0]
            r8_n[0] += 1
            assert r8_seq[idx][1] == c0
            while r8_issued[0] < min(len(r8_seq), idx + 3):
                j = r8_issued[0]
                vw, cc = r8_seq[j]
                S.dma("pool", "r8_%d" % (j % 3), dmaf(r8[j % 3], vw[:, :, cc:cc + 512]), writes=[("r8", j % 3)])
                r8_issued[0] += 1
            return idx % 3

        def load_r16(c0):
            sl = r16_n[0] % 2
            r16_n[0] += 1
            S.dma("pool", "r16_%d" % sl, dmaf(r16[sl], w_down_v[:, :, c0:c0 + 256]), writes=[("r16", sl)])
            return sl

        S.dma("pool", "pw", dmaf(pw, w_ple_v), writes=["pw"])
        PASSES = [[(0, 512, 0)], [(512, 512, 0)], [(1024, 512, 0)], [(1536, 512, 0), (2048, 16, 512)]]

        def rms_norm_C(which, out_fn, segs, W, tag):
            bns = []
            for (t0, n, l0) in segs:
                bns.append(6 + misc_n[0] % 2)
                misc_n[0] += 1
            for m in range(8):
                sq = sqr[m % 2]
                S.op("act", actf(sq[:, 0:W], R[:, m, 0:W], AF.Square), reads=[("R", m)], writes=[("sqr", m % 2)])
                for si_, (t0, n, l0) in enumerate(segs):
                    bn = bns[si_]
                    S.group("pe", [mm(ps[bn][:, 0:n], ones_b, sq[:, l0:l0 + n], start=(m == 0), stop=(m == 7))],
                            reads=[("sqr", m % 2), "ones_b"], writes=[PK[bn]])
            for si_, (t0, n, l0) in enumerate(segs):
                bn = bns[si_]
                S.op("dve", ts(rnC[:, l0:l0 + n], ps[bn][:, 0:n], 1.0 / 1024, ALU.mult, EPS, ALU.add), reads=[PK[bn]], writes=["rnC"])
            S.op("act", actf(rnC[:, 0:W], rnC[:, 0:W], AF.Sqrt), reads=["rnC"], writes=["rnC"])
            S.op("dve", lambda e: e.reciprocal(out=rnC[:, 0:W], in_=rnC[:, 0:W]), reads=["rnC"], writes=["rnC"])
            for m in range(8):
                out_ap, wkey = out_fn(m)
                S.op("dve", stt(out_ap, R[:, m, 0:W], gcol(which, m), rnC[:, 0:W], ALU.mult, ALU.mult), reads=[("R", m), "rnC", "cols"], writes=[wkey])

        for pi, segs in enumerate(PASSES):
            W = sum(n for (_, n, _) in segs)
            t00 = segs[0][0]
            has_s = len(segs) > 1
            if pi == 0:
                S.dma("sp", "xres", dmaf(xres, x_p[0:512, :].rearrange("(j p) f -> p j f", p=128)), writes=["xres"])
            def stats_act(m):
                S.op("act", actf(sqr[m % 2][:, 0:W], R[:, m, 0:W], AF.Square), reads=[("R", m)], writes=[("sqr", m % 2)])

            def stats_pe(m, bns):
                for si_, (t0, n, l0) in enumerate(segs):
                    S.group("pe", [mm(ps[bns[si_]][:, 0:n], ones_b, sqr[m % 2][:, l0:l0 + n], start=(m == 0), stop=(m == 7))],
                            reads=[("sqr", m % 2), "ones_b"], writes=[PK[bns[si_]]])

            def norm_finish_row(bns, out_t, key, square):
                for si_, (t0, n, l0) in enumerate(segs):
                    S.op("dve", ts(out_t[:, l0:l0 + n], ps[bns[si_]][:, 0:n], 1.0 / 1024, ALU.mult, EPS, ALU.add), reads=[PK[bns[si_]]], writes=[key])
                if not square:
                    S.op("act", actf(out_t[:, 0:W], out_t[:, 0:W], AF.Sqrt), reads=[key], writes=[key])
                S.op("dve", lambda e: e.reciprocal(out=out_t[:, 0:W], in_=out_t[:, 0:W]), reads=[key], writes=[key])

            def pick_bns():
                o = []
                for _ in segs:
                    o.append(6 + misc_n[0] % 2)
                    misc_n[0] += 1
                return o

            bns1 = pick_bns()
            for blk in range(2):
                sl = load_r8(w_out_v, blk * 512)
                for m4 in range(4):
                    m = blk * 4 + m4
                    for (t0, n, l0) in segs:
                        b = next_bank()
                        fns = [mm(ps[b][:, 0:n], r8[sl][:, k, m4 * 128:(m4 + 1) * 128], cat[:, k, t0:t0 + n], start=(k == 0), stop=False) for k in range(8)]
                        if n == 512:
                            fns += [mm(ps[b][:, j * 128:(j + 1) * 128], xres[:, j, m * 128:(m + 1) * 128], ident_f, start=False, stop=(j == 3)) for j in range(4)]
                            rk = ["xres"]
                        else:
                            fns += [mm(ps[b][:, 0:16], xres_s[0:16, m * 128:(m + 1) * 128], ident_f[0:16, 0:16], start=False, stop=True)]
                            rk = ["xres_s"]
                        S.group("pe", fns, reads=[("r8", sl), "cat", "ident_f"] + rk, writes=[PK[b]])
                        S.op("act", actf(R[:, m, l0:l0 + n], ps[b][:, 0:n], AF.Copy), reads=[PK[b]], writes=[("R", m)])
                        S.op("act", actf(xnC[:, m, l0:l0 + n], ps[b][:, 0:n], AF.Copy, scale=gcol(0, m)), reads=[PK[b], "cols"], writes=[("xnC", m)])
                    stats_act(m)
                    if m >= 1:
                        stats_pe(m - 1, bns1)
            stats_pe(7, bns1)
            norm_finish_row(bns1, rnC, "rnC", True)
            if pi + 1 < len(PASSES):
                tn = PASSES[pi + 1][0][0]
                S.dma("sp", "xres", dmaf(xres, x_p[tn:tn + 512, :].rearrange("(j p) f -> p j f", p=128)), writes=["xres"])
                if len(PASSES[pi + 1]) > 1:
                    S.dma("sp", "xres_s", dmaf(xres_s[0:16, :], x_s), writes=["xres_s"])
            for blk in range(8):
                sl = load_r8(w_up_v, blk * 512)
                for m4 in range(4):
                    hc = blk * 4 + m4
                    for (t0, n, l0) in segs:
                        b = next_bank()
                        S.group("pe", [mm(ps[b][:, 0:n], r8[sl][:, k, m4 * 128:(m4 + 1) * 128], xnC[:, k, l0:l0 + n], start=(k == 0), stop=(k == 7)) for k in range(8)],
                                reads=[("r8", sl)] + [("xnC", k) for k in range(8)], writes=[PK[b]])
                        rt = relu_t[misc_n[0] % 2]; rkey = ("relu_t", misc_n[0] % 2)
                        misc_n[0] += 1
                        S.op("act", actf(rt[:, 0:n], ps[b][:, 0:n], AF.Relu), reads=[PK[b]], writes=[rkey])
                        S.op("dve", tt(hid[:, hc, l0:l0 + n], rt[:, 0:n], rt[:, 0:n], ALU.mult), reads=[rkey], writes=[("hid", hc)])
            bns2 = pick_bns()
            for blk in range(4):
                sl = load_r16(blk * 256)
                for m2 in range(2):
                    m = blk * 2 + m2
                    for (t0, n, l0) in segs:
                        b = next_bank()
                        S.group("pe", [mm(ps[b][:, 0:n], r16[sl][:, k, m2 * 128:(m2 + 1) * 128], hid[:, k, l0:l0 + n], start=(k == 0), stop=(k == 31)) for k in range(32)],
                                reads=[("r16", sl)] + [("hid", k) for k in range(32)], writes=[PK[b]])
                        sg = sig[misc_n[0] % 2]; skey = ("sig", misc_n[0] % 2)
                        misc_n[0] += 1
                        S.op("dve", tt(sg[:, 0:n], ps[b][:, 0:n], rnC[:, l0:l0 + n], ALU.mult), reads=[PK[b], "rnC"], writes=[skey])
                        S.op("dve", tt(R[:, m, l0:l0 + n], R[:, m, l0:l0 + n], sg[:, 0:n], ALU.add), reads=[skey, ("R", m)], writes=[("R", m)])
                    S.op("act", actf(xnC[:, m, 0:W], R[:, m, 0:W], AF.Copy, scale=gcol(1, m)), reads=[("R", m), "cols"], writes=[("xnC", m)])
                    stats_act(m)
                    if m >= 1:
                        stats_pe(m - 1, bns2)
            stats_pe(7, bns2)
            norm_finish_row(bns2, rnC, "rnC", False)
            for (t0, n, l0) in segs:
                ntile = (n + 127) // 128
                for j in range(ntile):
                    r = min(128, n - j * 128)
                    sl = misc_n[0] % 2
                    misc_n[0] += 1
                    src = p_p[t0 + j * 128:t0 + j * 128 + r, :] if n == 512 else p_s
                    S.dma("pool", "ptok%d" % sl, dmaf(ptok[sl][0:r, :], src), writes=[("ptok", sl)])
                    S.group("pe", [lambda e, kk=kk, sl=sl, r=r: e.transpose(out=psb[5][:, kk * 128:kk * 128 + r], in_=ptok[sl][0:r, kk * 128:(kk + 1) * 128], identity=ident_b[0:r, 0:r]) for kk in range(2)],
                            reads=[("ptok", sl), "ident_b"], writes=[PK[5]])
                    S.op("act", actf(pT[:, :, l0 + j * 128:l0 + j * 128 + r], psb[5][:, 0:256].rearrange("p (k t) -> p k t", k=2)[:, :, 0:r], AF.Copy), reads=[PK[5]], writes=["pT"])
            ntt = sum((n + 127) // 128 for (_, n, _) in segs)
            sigbufs = [(sig[0], ("sig", 0)), (sig[1], ("sig", 1)), (relu_t[0], ("relu_t", 0)), (relu_t[1], ("relu_t", 1))]

            def gate_chunk(m, sl, m4):
                ops = []
                Lop, Lgr, Ldma = mk_recorders(S, ops)
                for si_, (t0, n, l0) in enumerate(segs):
                    b = (2 * m + si_) % 4
                    pb_ = 6 + m % 2
                    sg, skey = sigbufs[(2 * m + si_) % 4]
                    Lgr("pe", [mm(ps[b][:, 0:n], r8[sl][:, k, m4 * 128:(m4 + 1) * 128], xnC[:, k, l0:l0 + n], start=(k == 0), stop=(k == 7)) for k in range(8)],
                        reads=[("r8", sl)] + [("xnC", k) for k in range(8)], writes=[PK[b]])
                    Lgr("pe", [mm(ps[pb_][:, 0:n], pw[:, kk, m * 128:(m + 1) * 128], pT[:, kk, l0:l0 + n], start=(kk == 0), stop=(kk == 1)) for kk in range(2)],
                        reads=["pw", "pT"], writes=[PK[pb_]])
                    Lop("dve", tt(sg[:, 0:n], ps[b][:, 0:n], rnC[:, l0:l0 + n], ALU.mult), reads=[PK[b], "rnC"], writes=[skey])
                    Lop("act", actf(sg[:, 0:n], sg[:, 0:n], AF.Sigmoid), reads=[skey], writes=[skey])
                    Lop("dve", tt(sg[:, 0:n], sg[:, 0:n], ps[pb_][:, 0:n], ALU.mult), reads=[PK[pb_], skey], writes=[skey])
                    Lop("dve", tt(R[:, m, l0:l0 + n], R[:, m, l0:l0 + n], sg[:, 0:n], ALU.add), reads=[skey, ("R", m)], writes=[("R", m)])
                sq = sqr[m % 2]
                Lop("act", actf(sq[:, 0:W], R[:, m, 0:W], AF.Square), reads=[("R", m)], writes=[("sqr", m % 2)])
                fns = []
                if m == 0:
                    fns.append(mm(ps[5][:, 256:256 + ntt], zeros_f, zeros_f[:, 0:ntt], start=True, stop=False))
                jt = 0
                for (t0, n, l0) in segs:
                    for j in range((n + 127) // 128):
                        r = min(128, n - j * 128)
                        fns.append(mm(ps[5][0:r, 256 + jt:257 + jt], sq[:, l0 + j * 128:l0 + j * 128 + r], ones_b[:, 0:1], start=False, stop=False))
                        jt += 1
                if m == 7:
                    fns.append(mm(ps[5][:, 256:256 + ntt], zeros_f, zeros_f[:, 0:ntt], start=False, stop=True))
                Lgr("pe", fns, reads=[("sqr", m % 2), "ones_b"], writes=[PK[5]])
                Lop("act", actf(R[:, m, 0:W], R[:, m, 0:W], AF.Copy, scale=gcol(2, m)), reads=[("R", m), ("sqr", m % 2), "cols"], writes=[("R", m)])
                return ops

            for blk in range(2):
                sl = load_r8(w_gate_v, blk * 512)
                chunks = [gate_chunk(blk * 4 + m4, sl, m4) for m4 in range(4)]
                zipper(chunks[0:2])
                zipper(chunks[2:4])
            S.op("dve", ts(rncol[:, 0:ntt], ps[5][:, 256:256 + ntt], 1.0 / 1024, ALU.mult, EPS, ALU.add), reads=[PK[5]], writes=["rncol"])
            S.op("act", actf(rncol[:, 0:ntt], rncol[:, 0:ntt], AF.Sqrt), reads=["rncol"], writes=["rncol"])
            S.op("dve", lambda e: e.reciprocal(out=rncol[:, 0:ntt], in_=rncol[:, 0:ntt]), reads=["rncol"], writes=["rncol"])
            jt = 0
            for (t0, n, l0) in segs:
                ntile = (n + 127) // 128
                for j in range(ntile):
                    r = min(128, n - j * 128)
                    ysl = 0
                    for half in range(2):
                        b = next_bank()
                        S.group("pe", [mm(ps[b][0:r, m4 * 128:(m4 + 1) * 128], R[:, half * 4 + m4, l0 + j * 128:l0 + j * 128 + r], ident_f) for m4 in range(4)],
                                reads=[("R", half * 4 + m4) for m4 in range(4)] + ["ident_f"], writes=[PK[b]])
                        S.op("act", actf(ytile[ysl][0:r, half * 512:(half + 1) * 512], ps[b][0:r, :], AF.Copy, scale=rncol[0:r, jt:jt + 1]), reads=[PK[b], "rncol"], writes=[("ytile", ysl)])
                    jt += 1
                    dst = y_p[t0 + j * 128:t0 + j * 128 + r, :] if n == 512 else y_s
                    S.dma("sp", "o_y%d" % ysl, dmaf(dst, ytile[ysl][0:r, :]), reads=[("ytile", ysl)])
        S.finish()
        with nc.Block() as block:
            S.replay(block)
    return nc


_PROG = {}


def _make_in_maps(inputs):
    f = lambda a: np.ascontiguousarray(np.asarray(a, dtype=np.float32))
    g = {k: f(v) for k, v in inputs.items()}
    shared = {
        "g_mix": g["g_mix"].reshape(1, 1024), "w_in": g["w_in"][0], "w_conv": g["w_conv"][0],
        "a_log": g["a_log"].reshape(1, 4), "dt_bias": g["dt_bias"].reshape(1, 4), "gdn_norm": g["gdn_norm"].reshape(1, 128),
        "ln_g": g["sgu_ln_g"].reshape(1, 512), "ln_b": g["sgu_ln_b"].reshape(1, 512), "w_s": g["w_s"][0],
        "b_s": g["b_s"].reshape(1, 512), "w_out": g["w_out"][0], "g_ff": g["g_ff"].reshape(8, 128), "w_up": g["w_up"][0],
        "w_down": g["w_down"][0], "g_ple": g["g_ple"].reshape(8, 128), "w_ple": g["w_ple"][0], "w_gate": g["w_ple_gate"][0],
        "g_fin": g["g_final"].reshape(8, 128),
    }
    maps = []
    for i in range(8):
        m = dict(shared)
        sl = slice(16 * i, 16 * i + 16)
        m["x_p"] = g["x_prompt"][i]
        m["x_s"] = g["x_sample"][sl, 0]
        m["st_conv"] = g["state_conv"][0, sl]
        m["st_gdn"] = g["state_gdn"][0, sl]
        m["p_p"] = g["p_prompt"][0, i]
        m["p_s"] = g["p_sample"][0, sl, 0]
        maps.append(m)
    return maps


def kernel(**inputs):
    if "nc" not in _PROG:
        _PROG["nc"] = build_program()
    nc = _PROG["nc"]
    maps = _make_in_maps(inputs)
    res = run_bass_kernel_spmd(nc, maps, core_ids=list(range(8)))
    R = res.results
    st = lambda name: np.stack([np.asarray(r[name], dtype=np.float32) for r in R])
    cc = lambda name: np.concatenate([np.asarray(r[name], dtype=np.float32) for r in R], axis=0)
    y_prompt = st("y_p")
    y_sample = cc("y_s")[:, None, :]
    new_conv_prompt = st("ncp")[None]
    new_gdn_prompt = st("ngp")[None]
    new_conv_sample = cc("ncs")[None]
    new_gdn_sample = cc("ngs")[None]
    new_sgu_v_sample = cc("nsv")[None, :, None, :]
    return (y_prompt, y_sample, new_conv_prompt, new_gdn_prompt, new_conv_sample, new_gdn_sample, new_sgu_v_sample)
```

```python
import os
import numpy as np
import concourse.bass as bass
import concourse.mybir as mybir
from concourse.bass_utils import run_bass_kernel_spmd

F32 = mybir.dt.float32
BF16 = mybir.dt.bfloat16
AF = mybir.ActivationFunctionType
ALU = mybir.AluOpType
AX = mybir.AxisListType


class Sched:
    ENGS = ("pe", "act", "dve", "pool", "sp")

    def __init__(self, nc, stack):
        self.nc = nc
        self.stack = stack
        self.streams = {e: [] for e in self.ENGS}
        self.esem = {e: stack.enter_context(nc.semaphore("c_" + e)) for e in self.ENGS[:4]}
        self.ecnt = {e: 0 for e in self.ENGS}
        self.waited = {e: {} for e in self.ENGS}
        self.res = {}
        self.dsem = {}
        self.sem_by_name = {}
        for e in self.ENGS[:4]:
            self.sem_by_name[self.esem[e].name] = self.esem[e]

    def _need(self, eng, ev, waits):
        if ev is None:
            return
        name, val, src = ev
        if src == eng and eng == "pe":
            return
        cur = waits.get(name, 0)
        if val > cur:
            waits[name] = val

    def _deps(self, eng, reads, writes):
        waits = {}
        for k in reads:
            r = self.res.get(k)
            if r is not None:
                self._need(eng, r[0], waits)
        for k in writes:
            r = self.res.get(k)
            if r is not None:
                if r[0] is not None and not (r[0][2] == eng):
                    self._need(eng, r[0], waits)
                for ev in r[1]:
                    self._need(eng, ev, waits)
        out = []
        w = self.waited[eng]
        for name, val in waits.items():
            if w.get(name, 0) < val:
                w[name] = val
                out.append((name, val))
        return out

    def _commit(self, ev, reads, writes):
        for k in reads:
            r = self.res.setdefault(k, [None, []])
            r[1].append(ev)
        for k in writes:
            self.res[k] = [ev, []]

    def op(self, eng, fn, reads=(), writes=()):
        waits = self._deps(eng, reads, writes)
        self.ecnt[eng] += 1
        ev = (self.esem[eng].name, self.ecnt[eng], eng)
        self.streams[eng].append((waits, [fn], ("inc", self.esem[eng], 1)))
        self._commit(ev, reads, writes)
        return ev

    def group(self, eng, fns, reads=(), writes=()):
        waits = self._deps(eng, reads, writes)
        self.ecnt[eng] += 1
        ev = (self.esem[eng].name, self.ecnt[eng], eng)
        self.streams[eng].append((waits, list(fns), ("inc", self.esem[eng], 1)))
        self._commit(ev, reads, writes)
        return ev

    def dma(self, eng, slot, fn, reads=(), writes=(), n=1):
        if slot not in self.dsem:
            s = self.stack.enter_context(self.nc.semaphore("d_" + slot))
            self.dsem[slot] = [s, 0]
            self.sem_by_name[s.name] = s
        waits = self._deps(eng, reads, writes)
        d = self.dsem[slot]
        fns = fn if isinstance(fn, (list, tuple)) else [fn]
        d[1] += 16 * len(fns)
        ev = (d[0].name, d[1], "dma")
        self.streams[eng].append((waits, list(fns), ("dmainc", d[0], 16)))
        self._commit(ev, reads, writes)
        return ev

    def barrier(self, skip=()):
        evs = []
        for e in self.ENGS[:4]:
            if self.ecnt[e] > 0:
                evs.append((self.esem[e].name, self.ecnt[e]))
        for slot, (s, c) in self.dsem.items():
            if c > 0 and not any(slot.startswith(p) for p in skip):
                evs.append((s.name, c))
        for eng in self.ENGS:
            w = self.waited[eng]
            waits = []
            for name, val in evs:
                if w.get(name, 0) < val:
                    w[name] = val
                    waits.append((name, val))
            if waits:
                self.streams[eng].append((waits, [], None))
        self.res.clear()

    def finish(self):
        eng = "sp"
        waits = []
        for slot, (s, c) in self.dsem.items():
            if c > 0:
                waits.append((s.name, c))
        for e in self.ENGS[:4]:
            if self.ecnt[e] > 0:
                waits.append((self.esem[e].name, self.ecnt[e]))
        self.streams[eng].append((waits, [], None))

    def replay(self, block):
        sbn = self.sem_by_name

        def run(e, items):
            for waits, fns, inc in items:
                for name, val in waits:
                    e.wait_ge(sbn[name], val)
                last = None
                for i, f in enumerate(fns):
                    ins = f(e)
                    if inc is not None and inc[0] == "dmainc":
                        ins.then_inc(inc[1], 16)
                    last = ins
                if inc is not None and inc[0] == "inc" and last is not None:
                    last.then_inc(inc[1], 1)

        st = self.streams

        @block.tensor
        def _(e):
            run(e, st["pe"])

        @block.scalar
        def _(e):
            run(e, st["act"])

        @block.vector
        def _(e):
            run(e, st["dve"])

        @block.gpsimd
        def _(e):
            run(e, st["pool"])

        @block.sync
        def _(e):
            run(e, st["sp"])


U8 = mybir.dt.uint8
T_P = 2048
T_S = 16
T_ALL = T_P + T_S
SEGS = [(0, 512), (512, 512), (1024, 512), (1536, 512), (2048, 16)]
NT = 17
EPS = 1e-6
D_IN = 3080
C_Q, C_K, C_V, C_Z, C_BA, C_U, C_VS = 0, 512, 1024, 1536, 2048, 2056, 2568


def mm(out, lhsT, rhs, start=True, stop=True):
    return lambda e: e.matmul(out, lhsT=lhsT, rhs=rhs, start=start, stop=stop)


def actf(out, in_, func, **kw):
    return lambda e: e.activation(out=out, in_=in_, func=func, **kw)


def tt(out, a, b, op):
    return lambda e: e.tensor_tensor(out=out, in0=a, in1=b, op=op)


def ts(out, a, s1, op0, s2=None, op1=None):
    if op1 is None:
        return lambda e: e.tensor_scalar(out=out, in0=a, scalar1=s1, scalar2=None, op0=op0)
    return lambda e: e.tensor_scalar(out=out, in0=a, scalar1=s1, scalar2=s2, op0=op0, op1=op1)


def stt(out, a, s, b, op0, op1):
    return lambda e: e.scalar_tensor_tensor(out=out, in0=a, scalar=s, in1=b, op0=op0, op1=op1)


def stt_pool(out, a, colap):
    return lambda e: e.tensor_tensor(out=out, in0=a, in1=colap.to_broadcast([128, 128]), op=ALU.mult)


def cp(out, in_):
    return lambda e: e.tensor_copy(out=out, in_=in_)


def dmaf(out, in_):
    return lambda e: e.dma_start(out=out, in_=in_)


class _Item:
    __slots__ = ("thunk", "eng", "reads", "writes", "dur")

    def __init__(self, thunk, eng, reads, writes, dur):
        self.thunk, self.eng, self.reads, self.writes, self.dur = thunk, eng, tuple(reads), tuple(writes), dur


_DUR = {"act": 0.5, "dve": 0.45, "pool": 0.5}


def mk_recorders(S, ops):
    def Lop(eng, fn, reads=(), writes=()):
        ops.append(_Item(lambda: S.op(eng, fn, reads=reads, writes=writes), eng, reads, writes, _DUR.get(eng, 0.4)))

    def Lgr(eng, fns, reads=(), writes=()):
        ops.append(_Item(lambda: S.group(eng, fns, reads=reads, writes=writes), eng, reads, writes, 0.1 + 0.13 * len(fns)))

    def Ldma(eng, slot, fn, reads=(), writes=()):
        ops.append(_Item(lambda: S.dma(eng, slot, fn, reads=reads, writes=writes), "q_" + eng, reads, writes, 2.5))
    return Lop, Lgr, Ldma


def zipper(lists):
    lists = [l for l in lists if l]
    idx = [0] * len(lists)
    if os.environ.get("ZIP", "rr") == "rr":
        live = True
        while live:
            live = False
            for i, l in enumerate(lists):
                if idx[i] < len(l):
                    it = l[idx[i]]
                    idx[i] += 1
                    live = True
                    if isinstance(it, _Item):
                        it.thunk()
                    else:
                        it()
        return
    t_eng, t_w, t_r = {}, {}, {}
    remaining = sum(len(l) for l in lists)
    while remaining:
        best = None
        for i, l in enumerate(lists):
            if idx[i] >= len(l):
                continue
            it = l[idx[i]]
            if not isinstance(it, _Item):
                best = (-1.0, i, it)
                break
            rdy = t_eng.get(it.eng, 0.0)
            for k in it.reads:
                rdy = max(rdy, t_w.get(k, 0.0))
            for k in it.writes:
                rdy = max(rdy, t_w.get(k, 0.0), t_r.get(k, 0.0))
            if best is None or rdy < best[0]:
                best = (rdy, i, it)
        rdy, i, it = best
        idx[i] += 1
        remaining -= 1
        if not isinstance(it, _Item):
            it()
            continue
        it.thunk()
        fin = rdy + it.dur
        if it.eng.startswith("q_"):
            t_eng[it.eng] = rdy + 0.1
        else:
            t_eng[it.eng] = fin
        for k in it.reads:
            t_r[k] = max(t_r.get(k, 0.0), fin)
        for k in it.writes:
            t_w[k] = fin
            t_r[k] = 0.0


class Bump:
    def __init__(self, arena, start, limit):
        self.t, self.off, self.limit = arena, start, limit

    def alloc(self, dtype, shape):
        esz = 4 if dtype == F32 else 2
        n = 1
        for s in shape[1:]:
            n *= s
        nb = (n * esz + 63) // 64 * 64
        o = self.off
        self.off += nb
        assert self.off <= self.limit, ("SBUF arena overflow", self.off, self.limit)
        ap = self.t[:, o:o + n * esz].bitcast(dtype)
        if len(shape) == 3:
            ap = ap.rearrange("p (a b) -> p a b", a=shape[1])
        elif len(shape) == 4:
            ap = ap.rearrange("p (a b c) -> p a b c", a=shape[1], b=shape[2])
        return ap


def build_program():
    from contextlib import ExitStack
    nc = bass.Bass("TRN2", target_bir_lowering=False)

    def din(name, shape):
        return nc.dram_tensor(name, shape, F32, kind="ExternalInput").ap()

    def dout(name, shape):
        return nc.dram_tensor(name, shape, F32, kind="ExternalOutput").ap()

    x_p = din("x_p", [T_P, 1024]); x_s = din("x_s", [T_S, 1024])
    st_conv = din("st_conv", [T_S, 3, 1536]); st_gdn = din("st_gdn", [T_S, 4, 128, 128])
    p_p = din("p_p", [T_P, 256]); p_s = din("p_s", [T_S, 256])
    g_mix = din("g_mix", [1, 1024]); w_in = din("w_in", [1024, D_IN]); w_conv = din("w_conv", [4, 1536])
    a_log = din("a_log", [1, 4]); dt_bias = din("dt_bias", [1, 4]); gdn_norm = din("gdn_norm", [1, 128])
    ln_g = din("ln_g", [1, 512]); ln_b = din("ln_b", [1, 512]); w_s = din("w_s", [4, 128, 128]); b_s = din("b_s", [1, 512])
    w_out = din("w_out", [1024, 1024]); g_ff = din("g_ff", [8, 128]); w_up = din("w_up", [1024, 4096]); w_down = din("w_down", [4096, 1024])
    g_ple = din("g_ple", [8, 128]); w_ple = din("w_ple", [256, 1024]); w_gate = din("w_gate", [1024, 1024]); g_fin = din("g_fin", [8, 128])
    y_p = dout("y_p", [T_P, 1024]); y_s = dout("y_s", [T_S, 1024])
    ncp = dout("ncp", [3, 1536]); ngp = dout("ngp", [4, 128, 128])
    ncs = dout("ncs", [T_S, 3, 1536]); ngs = dout("ngs", [T_S, 4, 128, 128]); nsv = dout("nsv", [T_S, 512])

    w_in_v = w_in.rearrange("(k p) c -> p k c", p=128)
    w_out_v = w_out.rearrange("(k p) c -> p k c", p=128)
    w_up_v = w_up.rearrange("(k p) c -> p k c", p=128)
    w_down_v = w_down.rearrange("(k p) c -> p k c", p=128)
    w_gate_v = w_gate.rearrange("(k p) c -> p k c", p=128)
    w_ple_v = w_ple.rearrange("(k p) c -> p k c", p=128)

    with ExitStack() as st:
        S = Sched(nc, st)
        ARENA = 206 * 1024
        arena = st.enter_context(nc.sbuf_tensor("arena", [128, ARENA], U8))
        ps = [st.enter_context(nc.psum_tensor("ps%d" % i, [128, 512], F32)) for i in range(8)]
        psb = [p[:, :].bitcast(BF16) for p in ps]
        PK = [("ps", i) for i in range(8)]

        P = Bump(arena, 0, ARENA)
        ident_f = P.alloc(F32, [128, 128]); ident_b = P.alloc(BF16, [128, 128])
        ones_f = P.alloc(F32, [128, 128]); ones_b = P.alloc(BF16, [128, 128])
        mask_incl = P.alloc(F32, [128, 128])
        mask_su = P.alloc(F32, [128, 128])
        nmask_sl = P.alloc(F32, [128, 128])
        sel127 = P.alloc(F32, [128, 128])
        nmask_su = P.alloc(F32, [128, 128])
        mask_sl = P.alloc(F32, [128, 128])
        rowstage = P.alloc(F32, [128, 128])
        cols = P.alloc(F32, [128, 128])
        wsT = P.alloc(BF16, [128, 4, 128])
        selws = P.alloc(BF16, [128, 4, 16])
        ws00 = P.alloc(F32, [128, 4])
        bs_row = P.alloc(F32, [128, 4, 128])
        lng_row = P.alloc(F32, [128, 512]); lnb_row = P.alloc(F32, [128, 512])
        alog_row = P.alloc(F32, [128, 4]); dtb_row = P.alloc(F32, [128, 4]); nexpA_row = P.alloc(F32, [128, 4])
        zcol = P.alloc(F32, [128, 4])
        zeros_f = P.alloc(F32, [128, 128])
        cat = P.alloc(BF16, [128, 8, T_ALL])
        P_C0 = P.off
        ba = P.alloc(F32, [128, NT, 8])
        beta_c = P.alloc(F32, [128, NT, 4]); g_c = P.alloc(F32, [128, NT, 4]); gc_c = P.alloc(F32, [128, NT, 4])
        bexp_c = P.alloc(F32, [128, NT, 4]); kd_c = P.alloc(F32, [128, NT, 4]); egl_c = P.alloc(F32, [128, NT, 4])
        tmp68 = P.alloc(F32, [128, NT, 4])
        qkv = P.alloc(BF16, [128, 12, T_ALL])
        zs = P.alloc(BF16, [128, 4, T_ALL])
        qks_f = P.alloc(F32, [128, 12, 16])
        histT = P.alloc(F32, [128, 12, 3, 16])
        ncp_st = P.alloc(F32, [128, 12, 3]); ncs_st = P.alloc(F32, [128, 12, 16])
        S_f = P.alloc(F32, [128, 4, 128]); S_b = P.alloc(BF16, [128, 4, 128])
        X0 = P.off
        XB = Bump(arena, X0, ARENA)
        hT = XB.alloc(BF16, [128, 8, T_ALL])
        Y0 = XB.off

        def wcol(j, c):
            return cols[:, 24 + j * 12 + c: 24 + j * 12 + c + 1]

        def gcol(which, m):
            return cols[:, which * 8 + m: which * 8 + m + 1]
        gdnn_col = cols[:, 72:73]
        negone = zcol[:, 1:2]

        S.op("pool", lambda e: e.memset(ones_f, 1.0), writes=["ones_f"])
        S.op("pool", lambda e: e.memset(ones_b, 1.0), writes=["ones_b"])
        S.op("pool", lambda e: e.memset(zcol, 0.0), writes=["zcol"])
        S.op("pool", lambda e: e.memset(zcol[:, 1:2], -1.0), reads=["zcol"], writes=["zcol"])
        S.op("pool", lambda e: e.memset(zeros_f, 0.0), writes=["zeros_f"])
        S.op("pool", lambda e: e.affine_select(out=ident_f, in_=ones_f, pattern=[[-1, 128]], compare_op=ALU.is_equal, fill=0.0, base=0, channel_multiplier=1), reads=["ones_f"], writes=["ident_f"])
        S.op("pool", lambda e: e.affine_select(out=mask_incl, in_=ones_f, pattern=[[1, 128]], compare_op=ALU.is_ge, fill=0.0, base=0, channel_multiplier=-1), reads=["ones_f"], writes=["mask_incl"])
        S.op("pool", lambda e: e.affine_select(out=mask_su, in_=ones_f, pattern=[[1, 128]], compare_op=ALU.is_gt, fill=0.0, base=0, channel_multiplier=-1), reads=["ones_f"], writes=["mask_su"])
        S.op("pool", lambda e: e.affine_select(out=nmask_sl, in_=ones_f, pattern=[[-1, 128]], compare_op=ALU.is_gt, fill=0.0, base=0, channel_multiplier=1), reads=["ones_f"], writes=["nmask_sl"])
        S.op("pool", ts(nmask_sl, nmask_sl, -1.0, ALU.mult), reads=["nmask_sl"], writes=["nmask_sl"])
        S.op("pool", ts(nmask_su, mask_su, -1.0, ALU.mult), reads=["mask_su"], writes=["nmask_su"])
        S.op("pool", ts(mask_sl, nmask_sl, -1.0, ALU.mult), reads=["nmask_sl"], writes=["mask_sl"])
        S.op("pool", lambda e: e.affine_select(out=sel127, in_=ones_f, pattern=[[0, 128]], compare_op=ALU.is_equal, fill=0.0, base=-127, channel_multiplier=1), reads=["ones_f"], writes=["sel127"])
        S.op("dve", cp(ident_b, ident_f), reads=["ident_f"], writes=["ident_b"])
        S.op("pool", lambda e: e.memset(rowstage, 0.0), writes=["rowstage"])
        S.dma("sp", "c0", [dmaf(rowstage[0:8, :], g_ff), dmaf(rowstage[8:16, :], g_ple), dmaf(rowstage[16:24, :], g_fin),
                           dmaf(rowstage[24:72, :], w_conv.rearrange("j (c p) -> (j c) p", p=128)), dmaf(rowstage[72:73, :], gdn_norm)],
              writes=["rowstage"])
        S.group("pe", [mm(ps[0][:, 0:128], rowstage, ident_f)], reads=["rowstage", "ident_f"], writes=[PK[0]])
        S.op("dve", cp(cols, ps[0][:, 0:128]), reads=[PK[0]], writes=["cols"])
        S.dma("sp", "c1", [dmaf(bs_row.rearrange("p h t -> p (h t)"), b_s.partition_broadcast(128)),
                           dmaf(lng_row, ln_g.partition_broadcast(128)), dmaf(lnb_row, ln_b.partition_broadcast(128)),
                           dmaf(alog_row, a_log.partition_broadcast(128)), dmaf(dtb_row, dt_bias.partition_broadcast(128)),
                           ] + [dmaf(ws00[:, h:h + 1], w_s[h, 0, 0:1].partition_broadcast(128)) for h in range(4)],
              writes=["rows"])
        S.op("act", actf(nexpA_row, alog_row, AF.Exp), reads=["rows"], writes=["nexpA"])
        S.op("dve", ts(nexpA_row, nexpA_row, -1.0, ALU.mult), reads=["nexpA"], writes=["nexpA"])
        for h in range(4):
            S.op("dve", ts(selws[0:16, h, :], ident_f[0:16, 0:16], ws00[0:16, h:h + 1], ALU.mult), reads=["rows", "ident_f"], writes=[("selws", h)])

        YA = Bump(arena, Y0, ARENA)
        wstmp = YA.alloc(F32, [128, 4, 128])
        S.dma("sp", "c2", dmaf(wstmp, w_s.rearrange("h t s -> t h s")), writes=["wstmp"])
        for h in range(4):
            S.op("pool", lambda e, h=h: e.affine_select(out=wstmp[:, h, :], in_=wstmp[:, h, :], pattern=[[-1, 128]], compare_op=ALU.is_ge, fill=0.0, base=0, channel_multiplier=1),
                 reads=["wstmp"], writes=["wstmp"])
        S.group("pe", [mm(ps[1][:, h * 128:(h + 1) * 128], wstmp[:, h, :], ident_f) for h in range(4)], reads=["wstmp", "ident_f"], writes=[PK[1]])
        S.op("dve", cp(wsT.rearrange("p h t -> p (h t)"), ps[1][:, 0:512]), reads=[PK[1]], writes=["wsT"])

        gmix_row = YA.alloc(F32, [128, 1024])
        S.dma("sp", "c3", dmaf(gmix_row, g_mix.partition_broadcast(128)), writes=["gmix"])
        xt = [YA.alloc(F32, [128, 1024]) for _ in range(3)]
        xsq = [YA.alloc(F32, [128, 1024]) for _ in range(3)]
        xn = [YA.alloc(BF16, [128, 1024]) for _ in range(3)]
        stat = YA.alloc(F32, [128, NT, 2])

        def phaseA_tile(i):
            ops = []
            Lop, Lgr, Ldma = mk_recorders(S, ops)
            r = 128 if i < 16 else 16
            sl = i % 3
            src = x_p[i * 128:(i + 1) * 128, :] if i < 16 else x_s
            Ldma("sp", "xt%d" % sl, dmaf(xt[sl][0:r, :], src), writes=[("xt", sl)])
            Lop("act", actf(xsq[sl][0:r, :], xt[sl][0:r, :], AF.Square), reads=[("xt", sl)], writes=[("xsq", sl)])
            Lop("dve", lambda e: e.reduce_sum(out=stat[0:r, i, 0:1], in_=xsq[sl][0:r, :], axis=AX.X), reads=[("xsq", sl)], writes=[("stat", i)])
            Lop("dve", ts(stat[0:r, i, 1:2], stat[0:r, i, 0:1], 1.0 / 1024, ALU.mult, EPS, ALU.add), reads=[("stat", i)], writes=[("stat", i)])
            Lop("act", actf(stat[0:r, i, 1:2], stat[0:r, i, 1:2], AF.Sqrt), reads=[("stat", i)], writes=[("stat", i)])
            Lop("dve", lambda e: e.reciprocal(out=stat[0:r, i, 1:2], in_=stat[0:r, i, 1:2]), reads=[("stat", i)], writes=[("stat", i)])
            Lop("dve", stt(xn[sl][0:r, :], xt[sl][0:r, :], stat[0:r, i, 1:2], gmix_row[0:r, :], ALU.mult, ALU.mult),
                reads=[("xt", sl), ("stat", i), "gmix"], writes=[("xn", sl)])
            b = i % 3
            Lgr("pe", [lambda e, k=k: e.transpose(out=psb[b][:, k * 128:k * 128 + r], in_=xn[sl][0:r, k * 128:(k + 1) * 128], identity=ident_b[0:r, 0:r]) for k in range(8)],
                reads=[("xn", sl), "ident_b"], writes=[PK[b]])
            if i % 2 == 0:
                Lop("act", actf(hT[:, :, i * 128:i * 128 + r], psb[b].rearrange("p (k t) -> p k t", k=8)[:, :, 0:r], AF.Copy), reads=[PK[b]], writes=[("hT", i)])
            else:
                Lop("dve", cp(hT[:, :, i * 128:i * 128 + r], psb[b].rearrange("p (k t) -> p k t", k=8)[:, :, 0:r]), reads=[PK[b]], writes=[("hT", i)])
            return ops

        tilesA = [phaseA_tile(i) for i in range(NT)]
        for g0 in range(0, NT, 3):
            zipper(tilesA[g0:g0 + 3])
        S.barrier()

        YB = Bump(arena, Y0, ARENA)
        wb = [YB.alloc(BF16, [128, 8, 512]) for _ in range(2)]
        wb8 = YB.alloc(BF16, [128, 8, 8])
        lnst = YB.alloc(F32, [128, NT, 8])
        sct = [YB.alloc(F32, [128, 3, 128]) for _ in range(2)]
        YB_MID = YB.off
        vg = YB.alloc(BF16, [128, NT, 512])
        F6 = YB.alloc(F32, [128, 6, 512])
        f512 = [F6[:, i, :] for i in range(6)]
        vgs_f = f512[5]
        YB5 = Bump(arena, YB_MID, ARENA)
        NBS = 4
        pre = [YB5.alloc(F32, [128, 515]) for _ in range(NBS)]
        accb = [YB5.alloc(F32, [128, 512]) for _ in range(NBS)]
        rnb = [YB5.alloc(F32, [128, 512]) for _ in range(NBS)]
        sqb = [YB5.alloc(BF16, [128, 512]) for _ in range(NBS)]
        wdiag = [YB5.alloc(F32, [128, 4, 128]) for _ in range(2)]
        YB6 = Bump(arena, YB_MID, ARENA)
        stage_tok = YB6.alloc(F32, [128, 1536])
        stage2 = YB6.alloc(F32, [128, 1536])
        wb_n = [0]
        bank_n = [0]

        def next_bank(lo=0, hi=4):
            b = lo + bank_n[0] % (hi - lo)
            bank_n[0] += 1
            return b

        wb_seq = [C_VS, C_U, C_Z, 0, 512, 1024]
        wb_issued = [0]

        def load_w(view, c0, ncol=512):
            idx = wb_n[0]
            wb_n[0] += 1
            assert wb_seq[idx] == c0
            while wb_issued[0] < min(len(wb_seq), idx + 2):
                j = wb_issued[0]
                S.dma("pool", "wb%d" % (j % 2), dmaf(wb[j % 2][:, :, 0:512], view[:, :, wb_seq[j]:wb_seq[j] + 512]), writes=[("wb", j % 2)])
                wb_issued[0] += 1
            return idx % 2

        hT_keys = [("hT", i) for i in range(NT)]

        def seg_hT_keys(t0, n):
            return [("hT", i) for i in range(t0 // 128, (t0 + n + 127) // 128)]

        sl = load_w(w_in_v, C_VS)

        def vsgu_tile(i):
            ops = []
            Lop, Lgr, Ldma = mk_recorders(S, ops)
            r = 128 if i < 16 else 16
            b = (i % 3)
            Lgr("pe", [mm(ps[b][0:r, :], hT[:, k, i * 128:i * 128 + r], wb[sl][:, k, :], start=(k == 0), stop=(k == 7)) for k in range(8)],
                    reads=[("hT", i), ("wb", sl)], writes=[PK[b]])
            g1 = f512[i % 3]; g2 = f512[3 + i % 3]
            Lop("act", actf(g1[0:r, :], ps[b][0:r, :], AF.Gelu_apprx_tanh), reads=[PK[b]], writes=[("g1", i % 3)])
            Lop("pool", tt(g2[0:r, :], g1[0:r, :], g1[0:r, :], ALU.mult), reads=[("g1", i % 3)], writes=[("g2", i % 3)])
            Lop("dve", lambda e, i=i, r=r, g1=g1: e.reduce_sum(out=lnst[0:r, i, 0:1], in_=g1[0:r, :], axis=AX.X), reads=[("g1", i % 3)], writes=[("lnst", i)])
            Lop("dve", lambda e, i=i, r=r, g2=g2: e.reduce_sum(out=lnst[0:r, i, 1:2], in_=g2[0:r, :], axis=AX.X), reads=[("g2", i % 3)], writes=[("lnst", i)])
            L = lambda a, bb: lnst[0:r, i, a:bb]
            Lop("dve", ts(L(2, 3), L(0, 1), 1.0 / 512, ALU.mult), reads=[("lnst", i)], writes=[("lnst", i)])
            Lop("dve", tt(L(3, 4), L(2, 3), L(2, 3), ALU.mult), reads=[("lnst", i)], writes=[("lnst", i)])
            Lop("dve", stt(L(4, 5), L(1, 2), 1.0 / 512, L(3, 4), ALU.mult, ALU.subtract), reads=[("lnst", i)], writes=[("lnst", i)])
            Lop("dve", ts(L(4, 5), L(4, 5), EPS, ALU.add), reads=[("lnst", i)], writes=[("lnst", i)])
            Lop("act", actf(L(4, 5), L(4, 5), AF.Sqrt), reads=[("lnst", i)], writes=[("lnst", i)])
            Lop("dve", lambda e, i=i, r=r: e.reciprocal(out=lnst[0:r, i, 5:6], in_=lnst[0:r, i, 4:5]), reads=[("lnst", i)], writes=[("lnst", i)])
            Lop("dve", ts(g2[0:r, :], g1[0:r, :], L(2, 3), ALU.subtract, L(5, 6), ALU.mult), reads=[("g1", i % 3), ("lnst", i)], writes=[("g2", i % 3)])
            Lop("pool", tt(g2[0:r, :], g2[0:r, :], lng_row[0:r, :], ALU.mult), reads=[("g2", i % 3), "rows"], writes=[("g2", i % 3)])
            if i < 16:
                Lop("pool", tt(vg[0:r, i, :], g2[0:r, :], lnb_row[0:r, :], ALU.add), reads=[("g2", i % 3), "rows"], writes=[("vg", i)])
            else:
                Lop("pool", tt(vgs_f[0:r, :], g2[0:r, :], lnb_row[0:r, :], ALU.add), reads=[("g2", i % 3), "rows"], writes=[("g2", 2)])
                Lop("pool", cp(vg[0:r, i, :], vgs_f[0:r, :]), reads=[("g2", 2)], writes=[("vg", i)])
                Ldma("sp", "o_nsv", dmaf(nsv, vgs_f[0:r, :]), reads=[("g2", 2)])
            return ops

        tilesV = [vsgu_tile(i) for i in range(NT)]
        for g0 in range(0, NT, 3):
            zipper(tilesV[g0:g0 + 3])

        S.dma("pool", "wb8", dmaf(wb8, w_in_v[:, :, C_BA:C_BA + 8]), writes=["wb8"])
        S.op("pool", lambda e: e.memset(ba, 0.0), writes=["ba"])
        bq = 4
        for i in range(NT):
            r = 128 if i < 16 else 16
            S.group("pe", [mm(ps[bq][0:r, i * 8:(i + 1) * 8], hT[:, k, i * 128:i * 128 + r], wb8[:, k, :], start=(k == 0), stop=(k == 7)) for k in range(8)],
                    reads=[("hT", i), "wb8"], writes=[PK[bq]])
        S.op("dve", cp(ba[:, 0:16, :], ps[bq][:, 0:128].rearrange("p (i c) -> p i c", c=8)), reads=[PK[bq], "ba"], writes=["ba"])
        S.op("dve", cp(ba[0:16, 16, :], ps[bq][0:16, 128:136]), reads=[PK[bq], "ba"], writes=["ba"])
        S.op("act", actf(beta_c, ba[:, :, 0:4], AF.Sigmoid), reads=["ba"], writes=["beta_c"])
        S.op("dve", tt(tmp68, ba[:, :, 4:8], dtb_row.unsqueeze(1).to_broadcast([128, NT, 4]), ALU.add), reads=["ba", "rows"], writes=["tmp68"])
        S.op("act", actf(tmp68, tmp68, AF.Exp), reads=["tmp68"], writes=["tmp68"])
        S.op("act", actf(tmp68, tmp68, AF.Ln, bias=1.0), reads=["tmp68"], writes=["tmp68"])
        S.op("dve", tt(g_c, tmp68, nexpA_row.unsqueeze(1).to_broadcast([128, NT, 4]), ALU.mult), reads=["tmp68", "nexpA"], writes=["g_c"])
        g68 = g_c.rearrange("p i h -> p (i h)"); gc68 = gc_c.rearrange("p i h -> p (i h)")
        S.group("pe", [mm(ps[5][:, 0:68], mask_incl, g68)], reads=["mask_incl", "g_c"], writes=[PK[5]])
        S.op("dve", cp(gc68, ps[5][:, 0:68]), reads=[PK[5]], writes=["gc_c"])
        S.group("pe", [mm(ps[5][:, 128:196], sel127, gc68)], reads=["sel127", "gc_c"], writes=[PK[5]])
        S.op("dve", cp(egl_c.rearrange("p i h -> p (i h)"), ps[5][:, 128:196]), reads=[PK[5]], writes=["egl_c"])
        S.op("dve", tt(tmp68.rearrange("p i h -> p (i h)"), egl_c.rearrange("p i h -> p (i h)"), gc68, ALU.subtract), reads=["egl_c", "gc_c"], writes=["tmp68"])
        S.op("act", actf(egl_c, egl_c, AF.Exp), reads=["egl_c", "tmp68"], writes=["egl_c"])
        S.op("act", actf(kd_c, tmp68, AF.Exp), reads=["tmp68"], writes=["kd_c"])
        S.op("act", actf(bexp_c, gc_c, AF.Exp), reads=["gc_c"], writes=["bexp_c"])
        S.op("dve", tt(bexp_c, bexp_c, beta_c, ALU.mult), reads=["bexp_c", "beta_c"], writes=["bexp_c"])

        sl = load_w(w_in_v, C_U)
        for h in range(4):
            for (t0, n) in SEGS:
                b = next_bank()
                S.group("pe", [mm(ps[b][:, 0:n], wb[sl][:, k, h * 128:(h + 1) * 128], hT[:, k, t0:t0 + n], start=(k == 0), stop=(k == 7)) for k in range(8)],
                        reads=seg_hT_keys(t0, n) + [("wb", sl)], writes=[PK[b]])
                u = f512[bank_n[0] % 2]
                S.op("act", actf(u[:, 0:n], ps[b][:, 0:n], AF.Gelu_apprx_tanh), reads=[PK[b]], writes=[("u", bank_n[0] % 2)])
                b2 = 4 + bank_n[0] % 2
                if n == 512:
                    tiles = [t0 // 128 + j for j in range(4)]
                    S.group("pe", [mm(ps[b2][:, j * 128:(j + 1) * 128], vg[:, tiles[j], h * 128:(h + 1) * 128], wsT[:, h, :]) for j in range(4)],
                            reads=[("vg", ti) for ti in tiles] + ["wsT"], writes=[PK[b2]])
                    S.op("dve", tt(f512[4][:, :].rearrange("p (j t) -> p j t", j=4), ps[b2][:, :].rearrange("p (j t) -> p j t", j=4),
                                   bs_row[:, h:h + 1, :].to_broadcast([128, 4, 128]), ALU.add), reads=[PK[b2], "rows"], writes=["mixt"])
                else:
                    S.group("pe", [mm(ps[b2][:, 0:16], vg[0:16, 16, h * 128:(h + 1) * 128], selws[0:16, h, :])],
                            reads=[("vg", 16), ("selws", h)], writes=[PK[b2]])
                    S.op("dve", tt(f512[4][:, 0:16], ps[b2][:, 0:16], bs_row[:, h, 0:1].to_broadcast([128, 16]), ALU.add), reads=[PK[b2], "rows"], writes=["mixt"])
                S.op("pool", tt(cat[:, 4 + h, t0:t0 + n], f512[4][:, 0:n], u[:, 0:n], ALU.mult), reads=["mixt", ("u", bank_n[0] % 2)], writes=[("cat", 4 + h, t0)])

        sl = load_w(w_in_v, C_Z)
        for h in range(4):
            for (t0, n) in SEGS:
                b = next_bank()
                S.group("pe", [mm(ps[b][:, 0:n], wb[sl][:, k, h * 128:(h + 1) * 128], hT[:, k, t0:t0 + n], start=(k == 0), stop=(k == 7)) for k in range(8)],
                        reads=seg_hT_keys(t0, n) + [("wb", sl)], writes=[PK[b]])
                S.op("act", actf(zs[:, h, t0:t0 + n], ps[b][:, 0:n], AF.Silu), reads=[PK[b]], writes=[("zs", h, t0)])

        S.barrier()

        def qkv_unit(blk, h, si, ui, wsl):
            ops = []
            Lop, Lgr, Ldma = mk_recorders(S, ops)
            c = blk * 4 + h
            t0, n = SEGS[si]
            bs = ui % NBS
            pr, acc, sq, rn = pre[bs], accb[bs], sqb[bs], rnb[bs]
            kp, ka, ks, kr = ("pre", bs), ("acc", bs), ("sq", bs), ("rn", bs)
            b = ui % 4; bn = 4 + ui % 4
            Lgr("pe", [mm(ps[b][:, 0:n], wb[wsl][:, k, h * 128:(h + 1) * 128], hT[:, k, t0:t0 + n], start=(k == 0), stop=(k == 7)) for k in range(8)],
                reads=[("wb", wsl)], writes=[PK[b]])
            if si == 0:
                Lop("dve", lambda e: e.memset(pr[:, 0:3], 0.0), writes=[kp])
            elif si < 4:
                Lgr("pe", [mm(ps[bn][:, 0:3], wb[wsl][:, k, h * 128:(h + 1) * 128], hT[:, k, t0 - 3:t0], start=(k == 0), stop=(k == 7)) for k in range(8)],
                    reads=[("wb", wsl)], writes=[PK[bn]])
                Lop("dve", cp(pr[:, 0:3], ps[bn][:, 0:3]), reads=[PK[bn]], writes=[kp])
            Lop("act", actf(pr[:, 3:3 + n], ps[b][:, 0:n], AF.Copy), reads=[PK[b], kp], writes=[kp])
            if si == 3:
                Lop("dve", cp(ncp_st[:, c, :], pr[:, 512:515]), reads=[kp], writes=[("ncp_st", c)])
            if si < 4:
                wd = wdiag[c % 2]
                bc_ = 4 + ui % 4
                fns = []
                for t4 in range(4):
                    for j in range(4):
                        fns.append(mm(ps[bc_][:, t4 * 128:(t4 + 1) * 128], wd[:, j, :], pr[:, j + t4 * 128:j + (t4 + 1) * 128], start=(j == 0), stop=(j == 3)))
                Lgr("pe", fns, reads=[kp, ("wdiag", c % 2)], writes=[PK[bc_]])
                Lop("act", actf(acc[:, 0:n], ps[bc_][:, 0:n], AF.Silu), reads=[PK[bc_]], writes=[ka])
            else:
                Lop("dve", cp(ncs_st[:, c, :], pr[:, 3:19]), reads=[kp], writes=[("ncs_st", c)])
                Lop("act", actf(acc[:, 0:n], pr[:, 3:3 + n], AF.Copy, scale=wcol(3, c)), reads=[kp, "cols"], writes=[ka])
                for j in (2, 1, 0):
                    Lop("dve", stt(acc[:, 0:n], histT[:, c, j, :], wcol(j, c), acc[:, 0:n], ALU.mult, ALU.add), reads=[("histT", c), ka, "cols"], writes=[ka])
                Lop("act", actf(acc[:, 0:n], acc[:, 0:n], AF.Silu), reads=[ka], writes=[ka])
            if blk == 2:
                Lop("pool", cp(qkv[:, c, t0:t0 + n], acc[:, 0:n]), reads=[ka], writes=[("qkv", c, t0)])
                if si == 4:
                    Lop("pool", cp(qks_f[:, c, :], acc[:, 0:16]), reads=[ka], writes=[("qks_f", c)])
            else:
                Lop("pool", tt(sq[:, 0:n], acc[:, 0:n], acc[:, 0:n], ALU.mult), reads=[ka], writes=[ks])
                Lgr("pe", [mm(ps[bn][:, 0:n], ones_b, sq[:, 0:n])], reads=[ks, "ones_b"], writes=[PK[bn]])
                Lop("act", actf(rn[:, 0:n], ps[bn][:, 0:n], AF.Sqrt, bias=1e-6), reads=[PK[bn]], writes=[kr])
                Lop("dve", lambda e: e.reciprocal(out=rn[:, 0:n], in_=rn[:, 0:n]), reads=[kr], writes=[kr])
                scl = (128.0 ** -0.5) if blk == 0 else 1.0
                Lop("dve", stt(qkv[:, c, t0:t0 + n], acc[:, 0:n], scl, rn[:, 0:n], ALU.mult, ALU.mult), reads=[ka, kr], writes=[("qkv", c, t0)])
                if si == 4:
                    Lop("dve", stt(qks_f[:, c, :], acc[:, 0:16], scl, rn[:, 0:16], ALU.mult, ALU.mult), reads=[ka, kr], writes=[("qks_f", c)])
            return ops

        ui = 0
        for blk in range(3):
            wsl = load_w(w_in_v, blk * 512)
            units = []
            for h in range(4):
                c = blk * 4 + h
                scs = sct[c % 2]
                S.dma("sp", "sct%d" % (c % 2), dmaf(scs[0:16, :, :], st_conv[:, :, c * 128:(c + 1) * 128]), writes=[("sct", c % 2)])
                S.group("pe", [mm(ps[4 + c % 4][:, j * 16:(j + 1) * 16], scs[0:16, j, :], ident_f[0:16, 0:16]) for j in range(3)],
                        reads=[("sct", c % 2), "ident_f"], writes=[PK[4 + c % 4]])
                S.op("dve", cp(histT[:, c, :, :], ps[4 + c % 4][:, 0:48].rearrange("p (j b) -> p j b", j=3)), reads=[PK[4 + c % 4]], writes=[("histT", c)])
                for si in range(5):
                    uo = qkv_unit(blk, h, si, ui, wsl)
                    if si == 0:
                        pre_ops = [(lambda j=j, c=c: S.op("pool", stt_pool(wdiag[c % 2][:, j, :], ident_f, wcol(j, c)), reads=["ident_f", "cols"], writes=[("wdiag", c % 2)])) for j in range(4)]
                        uo = pre_ops + uo
                    units.append(uo)
                    ui += 1
            for g0 in range(0, len(units), 4):
                zipper(units[g0:g0 + 4])

        S.barrier()
        XS = Bump(arena, X0, ARENA)
        S_all = XS.alloc(F32, [128, 64, 128])
        if 'sample' not in os.environ.get('KSKIP', ''):
            for q4 in range(4):
                S.dma("sp", "sall%d" % q4, dmaf(S_all[:, q4 * 16:(q4 + 1) * 16, :], st_gdn[q4 * 4:(q4 + 1) * 4].rearrange("b h d e -> d (b h) e")), writes=[("S_all", q4)])
        S.group("pe", [mm(ps[c // 4][0:3, (c % 4) * 128:(c % 4 + 1) * 128], ncp_st[:, c, :], ident_f) for c in range(12)],
                reads=[("ncp_st", c) for c in range(12)] + ["ident_f"], writes=[PK[0], PK[1], PK[2]])
        for q3 in range(3):
            S.op("dve", cp(stage_tok[0:3, q3 * 512:(q3 + 1) * 512], ps[q3][0:3, :]), reads=[PK[q3]], writes=["stage_tok"])
        S.dma("sp", "o_ncp", dmaf(ncp, stage_tok[0:3, :]), reads=["stage_tok"])
        S.group("pe", [mm(ps[c // 4][0:16, (c % 4) * 128:(c % 4 + 1) * 128], ncs_st[:, c, :], ident_f) for c in range(12)],
                reads=[("ncs_st", c) for c in range(12)] + ["ident_f"], writes=[PK[0], PK[1], PK[2]])
        for q3 in range(3):
            S.op("act", actf(stage2[0:16, q3 * 512:(q3 + 1) * 512], ps[q3][0:16, :], AF.Copy), reads=[PK[q3]], writes=["stage2"])
        S.dma("sp", "o_ncs", [dmaf(ncs[:, 2, :], stage2[0:16, :]), dmaf(ncs[:, 0:2, :], st_conv[:, 1:3, :])], reads=["stage2"])
        S.barrier()

        YG = Bump(arena, Y0, ARENA)
        osq = YG.alloc(BF16, [128, 512]); rn_o = YG.alloc(F32, [128, 512]); on_o = YG.alloc(F32, [128, 512])
        YG_EPI = YG.off
        NPW, DP = 4, 8
        CHDT = F32
        gN = lambda n_, dt_, shp: [YG.alloc(dt_, shp) for _ in range(n_)]
        Rs = gN(NPW, F32, [128, 256]); rhsR = gN(NPW, F32, [128, 256]); D0 = gN(NPW, F32, [128, 128]); E0 = gN(NPW, F32, [128, 128])
        EGr = gN(NPW, F32, [128, 128]); MB = gN(NPW, F32, [128, 128]); Qf = gN(NPW, F32, [128, 128]); Qs = gN(NPW, BF16, [128, 128])
        NNa = gN(NPW, CHDT, [128, 256]); NNb = gN(NPW, CHDT, [128, 256]); Xs = gN(NPW, BF16, [128, 128])
        NHa = gN(NPW, BF16, [128, 256]); NHb = gN(NPW, BF16, [128, 256])
        J0 = int(os.environ.get('GDN_J0', '6'))
        Qm = gN(DP, BF16, [128, 128]); attnT = gN(DP, BF16, [128, 128]); Kd = gN(DP, BF16, [128, 128]); Vb = gN(DP, BF16, [128, 128])
        qg = gN(DP, BF16, [128, 128]); nWT = gN(DP, BF16, [128, 128]); vn = gN(4, BF16, [128, 128])
        S.op("pool", lambda e: e.memset(S_f.rearrange("p h e -> p (h e)"), 0.0), writes=[("S_f", h) for h in range(4)])
        S.op("pool", lambda e: e.memset(S_b.rearrange("p h e -> p (h e)"), 0.0), writes=[("S_b", h) for h in range(4)])

        def gdn_P(n, h):
            ops = []
            Lop, Lgr, Ldma = mk_recorders(S, ops)
            u = n * 4 + h
            q = u % NPW; s = u % DP
            tok = slice(n * 128, (n + 1) * 128)
            kT = qkv[:, 4 + h, tok]; qT = qkv[:, h, tok]; vT = qkv[:, 8 + h, tok]
            col = lambda t: t[:, n, h:h + 1]
            K = lambda name: (name, q)
            H = lambda name: (name, s)
            bk = PK[q]; pb = ps[q]; pbb = psb[q]
            kk = pb[:, 256:384]; qk = pb[:, 384:512]
            Lop("pool", stt_pool(rhsR[q][:, 0:128], mask_incl, col(g_c)), reads=["mask_incl", "g_c"], writes=[K("rhsR")])
            Lop("pool", stt_pool(rhsR[q][:, 128:256], ident_f, col(beta_c)), reads=["ident_f", "beta_c"], writes=[K("rhsR")])
            Lgr("pe", [lambda e: e.transpose(out=pbb[:, 0:128], in_=kT, identity=ident_b),
                       lambda e: e.transpose(out=pbb[:, 128:256], in_=vT, identity=ident_b)], reads=[("qkv", n), "ident_b"], writes=[bk])
            ktok = pbb[:, 0:128]; vtok = pbb[:, 128:256]
            Lop("act", actf(Xs[q], ktok, AF.Copy, scale=col(bexp_c)), reads=[bk, "bexp_c"], writes=[K("Xs")])
            Lop("act", actf(Kd[s], ktok, AF.Copy, scale=col(kd_c)), reads=[bk, "kd_c"], writes=[H("Kd")])
            Lop("act", actf(Vb[s], vtok, AF.Copy, scale=col(beta_c)), reads=[bk, "beta_c"], writes=[H("Vb")])
            Lgr("pe", [mm(pb[:, 0:128], ones_f, rhsR[q][:, 0:128]), mm(pb[:, 128:256], ones_f, rhsR[q][:, 128:256]),
                       mm(kk, kT, kT), mm(qk, kT, qT)], reads=[K("rhsR"), "ones_f", ("qkv", n)], writes=[bk])
            Lop("dve", cp(Rs[q], pb[:, 0:256]), reads=[bk], writes=[K("Rs")])
            R_gc = Rs[q][:, 0:128]; R_be = Rs[q][:, 128:256]
            Lop("pool", lambda e: e.tensor_tensor(out=D0[q], in0=R_gc, in1=col(gc_c).to_broadcast([128, 128]), op=ALU.subtract), reads=[K("Rs"), "gc_c"], writes=[K("D0")])
            Lop("pool", ts(D0[q], D0[q], 0.0, ALU.min), reads=[K("D0")], writes=[K("D0")])
            Lop("act", actf(D0[q], D0[q], AF.Exp), reads=[K("D0")], writes=[K("D0")])
            Lop("act", actf(EGr[q], R_gc, AF.Exp), reads=[K("Rs")], writes=[K("EGr")])
            Lop("pool", tt(MB[q], R_be, D0[q], ALU.mult), reads=[K("Rs"), K("D0")], writes=[K("MB")])
            Lop("pool", tt(MB[q], MB[q], nmask_su, ALU.mult), reads=[K("MB"), "nmask_su"], writes=[K("MB")])
            Lop("pool", tt(D0[q], D0[q], mask_incl, ALU.mult), reads=[K("D0"), K("MB"), "mask_incl"], writes=[K("D0")])
            Lop("pool", tt(qg[s], qT, EGr[q], ALU.mult), reads=[("qkv", n), K("EGr")], writes=[H("qg")])
            Lop("dve", tt(NNa[q][:, 0:128], kk, MB[q], ALU.mult), reads=[bk, K("MB")], writes=[K("NNa")])
            Lop("dve", tt(attnT[s], qk, D0[q], ALU.mult), reads=[bk, K("D0")], writes=[H("attnT")])
            Lgr("pe", [mm(pb[:, 0:128], NNa[q][:, 0:128], ident_f)], reads=[K("NNa"), "ident_f"], writes=[bk])
            Lop("act", actf(NNa[q][:, 128:256], pb[:, 0:128], AF.Copy), reads=[bk], writes=[K("NNa")])
            Lop("pool", tt(Qf[q], ident_f, NNa[q][:, 0:128], ALU.add), reads=["ident_f", K("NNa")], writes=[K("Qf")])
            cur, nxt, kc, kn = NNa[q], NNb[q], K("NNa"), K("NNb")
            cur16, nxt16, kc16, kn16 = NHa[q], NHb[q], K("NHa"), K("NHb")
            if J0 == 0:
                Lop("pool", cp(cur16, cur), reads=[kc], writes=[kc16])
            for j in range(1, 7):
                f32lvl = j <= J0
                src, ksrc = (cur, kc) if f32lvl else (cur16, kc16)
                fns = []
                if j < 6:
                    fns.append(mm(pb[:, 0:128], src[:, 128:256], src[:, 0:128]))
                fns.append(mm(pb[:, 128:256], src[:, 0:128], src[:, 128:256]))
                Lgr("pe", fns, reads=[ksrc], writes=[bk])
                lo = 0 if j < 6 else 128
                if f32lvl:
                    Lop("act", actf(nxt[:, lo:256], pb[:, lo:256], AF.Copy), reads=[bk], writes=[kn])
                    if j == J0 and j < 6:
                        Lop("pool", cp(nxt16[:, lo:256], nxt[:, lo:256]), reads=[kn], writes=[kn16])
                    Lgr("pe", [mm(pb[:, 256:384], nxt[:, 128:256], Qf[q])], reads=[kn, K("Qf")], writes=[bk])
                else:
                    Lop("act", actf(nxt16[:, lo:256], pb[:, lo:256], AF.Copy), reads=[bk], writes=[kn16])
                    Lop("pool", cp(Qs[q], Qf[q]), reads=[K("Qf")], writes=[K("Qs")])
                    Lgr("pe", [mm(pb[:, 256:384], nxt16[:, 128:256], Qs[q])], reads=[kn16, K("Qs")], writes=[bk])
                Lop("dve", tt(Qf[q], Qf[q], pb[:, 256:384], ALU.add), reads=[bk, K("Qf")], writes=[K("Qf")])
                cur, nxt, kc, kn = nxt, cur, kn, kc
                cur16, nxt16, kc16, kn16 = nxt16, cur16, kn16, kc16
            Lop("pool", cp(Qm[s], Qf[q]), reads=[K("Qf")], writes=[H("Qm")])
            Lgr("pe", [mm(pb[:, 384:512], Xs[q], Qm[s])], reads=[K("Xs"), H("Qm")], writes=[bk])
            Lop("act", actf(nWT[s], pb[:, 384:512], AF.Copy, scale=negone), reads=[bk], writes=[H("nWT")])
            return ops

        def gdn_R(n, h):
            ops = []
            Lop, Lgr, Ldma = mk_recorders(S, ops)
            u = n * 4 + h
            s = u % DP
            H = lambda name: (name, s)
            col = lambda t: t[:, n, h:h + 1]
            bR = PK[4]; ob = 5 + n % 2
            V = ps[4][:, h * 128:(h + 1) * 128]
            Lgr("pe", [mm(V, Qm[s], Vb[s], start=True, stop=False),
                       mm(V, nWT[s], S_b[:, h, :], start=False, stop=True)],
                reads=[H("Qm"), H("Vb"), H("nWT"), ("S_b", h)], writes=[bR])
            Lop("dve", cp(vn[h], V), reads=[bR], writes=[("vn", h)])
            Lgr("pe", [mm(V, Kd[s], vn[h])], reads=[H("Kd"), ("vn", h)], writes=[bR])
            Lgr("pe", [mm(ps[ob][:, h * 128:(h + 1) * 128], S_b[:, h, :], qg[s], start=True, stop=False),
                       mm(ps[ob][:, h * 128:(h + 1) * 128], vn[h], attnT[s], start=False, stop=True)],
                reads=[("S_b", h), H("qg"), ("vn", h), H("attnT")], writes=[PK[ob]])
            Lop("dve", stt(S_f[:, h, :], S_f[:, h, :], col(egl_c), V, ALU.mult, ALU.add), reads=[bR, ("S_f", h), "egl_c"], writes=[("S_f", h)])
            Lop("act", actf(S_b[:, h, :], S_f[:, h, :], AF.Copy), reads=[("S_f", h)], writes=[("S_b", h)])
            return ops

        def gdn_epilogue(o_ps, ss_ps, ncol, t0, okeys, sskey):
            w = 4 * ncol
            S.op("act", actf(osq[:, 0:w], o_ps, AF.Square), reads=okeys, writes=["osq"])
            S.group("pe", [mm(ss_ps, ones_b, osq[:, 0:w])], reads=["osq", "ones_b"], writes=[sskey])
            S.op("dve", ts(rn_o[:, 0:w], ss_ps, 1.0 / 128, ALU.mult, EPS, ALU.add), reads=[sskey], writes=["rn_o"])
            S.op("act", actf(rn_o[:, 0:w], rn_o[:, 0:w], AF.Sqrt), reads=["rn_o"], writes=["rn_o"])
            S.op("dve", lambda e: e.reciprocal(out=rn_o[:, 0:w], in_=rn_o[:, 0:w]), reads=["rn_o"], writes=["rn_o"])
            S.op("dve", stt(on_o[:, 0:w], o_ps, gdnn_col, rn_o[:, 0:w], ALU.mult, ALU.mult), reads=okeys + ["rn_o", "cols"], writes=["on_o"])
            S.op("pool", tt(cat[:, 0:4, t0:t0 + ncol], on_o[:, 0:w].rearrange("p (h t) -> p h t", h=4), zs[:, :, t0:t0 + ncol], ALU.mult),
                 reads=["on_o", "zs"], writes=[("cat_o", t0)])

        _SK = os.environ.get('KSKIP', '')
        NCH = 0 if 'prompt' in _SK else int(os.environ.get('GDN_N', '16'))

        def epi_ops(n):
            ob = 5 + n % 2
            return [lambda: gdn_epilogue(ps[ob][:, :], ps[7][:, :], 128, n * 128, [PK[ob]], PK[7])]

        _DO_SAMPLE = 'sample' not in _SK
        def _sample_section():
            YG = Bump(arena, YG_EPI, ARENA)
            sv = YG.alloc(F32, [128, 8])
            rexp = YG.alloc(F32, [128, 8, 16])
            bcs = YG.alloc(F32, [128, 128])
            dcol = YG.alloc(F32, [128, 64])
            dtok = YG.alloc(F32, [128, 512]); ktoks = YG.alloc(F32, [128, 512])
            kmask = [YG.alloc(F32, [128, 512]) for _ in range(2)]
            S.op("dve", cp(sv[0:16, 0:4], beta_c[0:16, 16, :]), reads=["beta_c"], writes=["sv"])
            S.op("act", actf(sv[0:16, 4:8], g_c[0:16, 16, :], AF.Exp), reads=["g_c"], writes=["sv"])
            for j in range(8):
                S.op("dve", ts(rexp[0:16, j, :], ident_f[0:16, 0:16], sv[0:16, j:j + 1], ALU.mult), reads=["sv", "ident_f"], writes=["rexp"])
            S.group("pe", [mm(ps[0][:, 0:128], ones_f[0:16, :], rexp[0:16, :, :].rearrange("p j b -> p (j b)"))], reads=["rexp", "ones_f"], writes=[PK[0]])
            S.op("dve", cp(bcs, ps[0][:, 0:128]), reads=[PK[0]], writes=["bcs"])
            beta_bc = bcs[:, 0:64]; eg_bc = bcs[:, 64:128]
            S.group("pe", [mm(ps[1][:, h * 16 + b:h * 16 + b + 1], S_all[:, b * 4 + h, :], qks_f[:, 4 + h, b:b + 1]) for b in range(16) for h in range(4)],
                    reads=[("S_all", q4) for q4 in range(4)] + ["qks_f"], writes=[PK[1]])
            S.op("dve", tt(dcol, ps[1][:, 0:64], eg_bc, ALU.mult), reads=[PK[1], "bcs"], writes=["dcol"])
            S.op("dve", tt(dcol, qks_f[:, 8:12, :].rearrange("p h b -> p (h b)"), dcol, ALU.subtract), reads=["dcol", "qks_f"], writes=["dcol"])
            S.op("dve", tt(dcol, dcol, beta_bc, ALU.mult), reads=["dcol", "bcs"], writes=["dcol"])
            S.group("pe", [mm(ps[2][0:16, h * 128:(h + 1) * 128], dcol[:, h * 16:(h + 1) * 16], ident_f) for h in range(4)], reads=["dcol", "ident_f"], writes=[PK[2]])
            S.group("pe", [mm(ps[3][0:16, h * 128:(h + 1) * 128], qks_f[:, 4 + h, :], ident_f) for h in range(4)], reads=["qks_f", "ident_f"], writes=[PK[3]])
            S.op("dve", cp(dtok[0:16, :], ps[2][0:16, :]), reads=[PK[2]], writes=["dtok"])
            S.op("act", actf(ktoks[0:16, :], ps[3][0:16, :], AF.Copy), reads=[PK[3]], writes=["ktoks"])
            for b in range(16):
                km = kmask[b % 2]; pb = 4 + b % 2
                S.op("dve", ts(km[0:16, :], ktoks[0:16, :], ident_f[0:16, b:b + 1], ALU.mult), reads=["ktoks", "ident_f"], writes=[("kmask", b % 2)])
                S.group("pe", [mm(ps[pb][:, h * 128:(h + 1) * 128], km[0:16, h * 128:(h + 1) * 128], dtok[0:16, h * 128:(h + 1) * 128]) for h in range(4)],
                        reads=[("kmask", b % 2), "dtok"], writes=[PK[pb]])
                for h in range(4):
                    S.op("dve", stt(S_all[:, b * 4 + h, :], S_all[:, b * 4 + h, :], eg_bc[:, h * 16 + b:h * 16 + b + 1], ps[pb][:, h * 128:(h + 1) * 128], ALU.mult, ALU.add),
                         reads=[PK[pb], "bcs", ("S_all", b // 4)], writes=[("S_all", b // 4)])
            S.group("pe", [mm(ps[1][:, 64 + h * 16 + b:64 + h * 16 + b + 1], S_all[:, b * 4 + h, :], qks_f[:, h, b:b + 1]) for b in range(16) for h in range(4)],
                    reads=[("S_all", q4) for q4 in range(4)] + ["qks_f"], writes=[PK[1]])
            gdn_epilogue(ps[1][:, 64:128], ps[0][:, 128:192], 16, T_P, [PK[1]], PK[0])
            for q4 in range(4):
                S.dma("sp", "o_ngs%d" % q4, dmaf(ngs[q4 * 4:(q4 + 1) * 4].rearrange("b h d e -> d (b h) e"), S_all[:, q4 * 16:(q4 + 1) * 16, :]), reads=[("S_all", q4)])
        if _DO_SAMPLE:
            _sample_section()
        S.barrier()

        LB = Bump(arena, X0, ARENA)
        NL = 8
        lf32 = lambda shp: [LB.alloc(F32, shp) for _ in range(NL)]
        lbf = lambda shp: [LB.alloc(BF16, shp) for _ in range(NL)]
        Rs8 = lf32([128, 256]); rhsR8 = lf32([128, 256]); D08 = lf32([128, 128]); EGr8 = lf32([128, 128]); MB8 = lf32([128, 128]); Qf8 = lf32([128, 128])
        NNa8 = lf32([128, 256]); NNb8 = lf32([128, 256]); rn8 = lf32([128, 128]); on8 = lf32([128, 128])
        Xs8 = lbf([128, 128]); Kd8 = lbf([128, 128]); Vb8 = lbf([128, 128]); qg8 = lbf([128, 128]); at8 = lbf([128, 128])
        Qm8 = lbf([128, 128]); nWT8 = lbf([128, 128]); vn8 = lbf([128, 128]); osq8 = lbf([128, 128])
        r_done = {}

        def gdn_unit(n, h):
            ops = []
            Lop, Lgr, Ldma = mk_recorders(S, ops)
            L = h * 2 + n % 2
            tok = slice(n * 128, (n + 1) * 128)
            kT = qkv[:, 4 + h, tok]; qT = qkv[:, h, tok]; vT = qkv[:, 8 + h, tok]
            col = lambda t: t[:, n, h:h + 1]
            K = lambda name: (name, L)
            bk = PK[L]; pb = ps[L]; pbb = psb[L]
            kk = pb[:, 256:384]; qk = pb[:, 384:512]
            Rs, rhsR, D0, EGr, MB, Qf = Rs8[L], rhsR8[L], D08[L], EGr8[L], MB8[L], Qf8[L]
            Xs, Kd, Vb, qg, attnT, Qm, nWT, vn, osq = Xs8[L], Kd8[L], Vb8[L], qg8[L], at8[L], Qm8[L], nWT8[L], vn8[L], osq8[L]
            Lop("pool", stt_pool(rhsR[:, 0:128], mask_incl, col(g_c)), reads=["mask_incl", "g_c"], writes=[K("rhsR")])
            Lop("pool", stt_pool(rhsR[:, 128:256], ident_f, col(beta_c)), reads=["ident_f", "beta_c"], writes=[K("rhsR")])
            Lgr("pe", [mm(pb[:, 0:128], mask_sl, rhsR[:, 0:128]), mm(pb[:, 128:256], ones_f, rhsR[:, 0:128]),
                       mm(pb[:, 256:384], nmask_sl, rhsR[:, 128:256]),
                       lambda e: e.transpose(out=pbb[:, 768:896], in_=kT, identity=ident_b),
                       lambda e: e.transpose(out=pbb[:, 896:1024], in_=vT, identity=ident_b)],
                reads=[K("rhsR"), "ones_f", "mask_sl", "nmask_sl", "ident_b"], writes=[bk])
            ktok = pbb[:, 768:896]; vtok = pbb[:, 896:1024]
            Lop("act", actf(Rs, pb[:, 0:256], AF.Exp), reads=[bk], writes=[K("Rs")])
            D0 = Rs[:, 0:128]; EGr = Rs[:, 128:256]
            Lop("act", actf(Xs, ktok, AF.Copy, scale=col(bexp_c)), reads=[bk, "bexp_c"], writes=[K("Xs")])
            Lop("act", actf(Kd, ktok, AF.Copy, scale=col(kd_c)), reads=[bk, "kd_c"], writes=[K("Kd")])
            Lop("act", actf(Vb, vtok, AF.Copy, scale=col(beta_c)), reads=[bk, "beta_c"], writes=[K("Vb")])
            Lop("act", actf(MB, pb[:, 256:384], AF.Copy), reads=[bk], writes=[K("MB")])
            Lop("pool", tt(MB, MB, D0, ALU.mult), reads=[K("MB"), K("Rs")], writes=[K("MB")])
            Lop("pool", tt(qg, qT, EGr, ALU.mult), reads=[K("Rs")], writes=[K("qg")])
            Lgr("pe", [mm(pb[:, 0:128], kT, kT), mm(pb[:, 128:256], kT, qT)], reads=[], writes=[bk])
            kk = pb[:, 0:128]; qk = pb[:, 128:256]
            Lop("pool", tt(D0, D0, mask_incl, ALU.mult), reads=[K("Rs"), K("MB"), K("qg"), "mask_incl"], writes=[K("Rs")])
            NNa, NNb = NNa8[L], NNb8[L]
            Lop("dve", tt(NNa[:, 0:128], kk, MB, ALU.mult), reads=[bk, K("MB")], writes=[K("NNa")])
            Lop("dve", tt(attnT, qk, D0, ALU.mult), reads=[bk, K("Rs")], writes=[K("attnT")])
            Lgr("pe", [mm(pb[:, 256:384], NNa[:, 0:128], ident_f)], reads=[K("NNa"), "ident_f"], writes=[bk])
            Lop("act", actf(NNa[:, 128:256], pb[:, 256:384], AF.Copy), reads=[bk], writes=[K("NNa")])
            Lop("pool", tt(Qf, ident_f, NNa[:, 0:128], ALU.add), reads=["ident_f", K("NNa")], writes=[K("Qf")])
            cur, nxt, kc, kn = NNa, NNb, K("NNa"), K("NNb")
            for j in range(1, 7):
                fns = []
                if j < 6:
                    fns.append(mm(pb[:, 0:128], cur[:, 128:256], cur[:, 0:128]))
                fns.append(mm(pb[:, 128:256], cur[:, 0:128], cur[:, 128:256]))
                Lgr("pe", fns, reads=[kc], writes=[bk])
                lo = 0 if j < 6 else 128
                if j in (3, 5):
                    Lop("dve", cp(nxt[:, lo:256], pb[:, lo:256]), reads=[bk], writes=[kn])
                else:
                    Lop("act", actf(nxt[:, lo:256], pb[:, lo:256], AF.Copy), reads=[bk], writes=[kn])
                Lgr("pe", [mm(pb[:, 256:384], nxt[:, 128:256], Qf)], reads=[kn, K("Qf")], writes=[bk])
                Lop("dve", tt(Qf, Qf, pb[:, 256:384], ALU.add), reads=[bk, K("Qf")], writes=[K("Qf")])
                cur, nxt, kc, kn = nxt, cur, kn, kc
            Lop("pool", cp(Qm, Qf), reads=[K("Qf")], writes=[K("Qm")])
            Lgr("pe", [mm(pb[:, 384:512], Xs, Qm)], reads=[K("Xs"), K("Qm")], writes=[bk])
            Lop("act", actf(nWT, pb[:, 384:512], AF.Copy, scale=negone), reads=[bk], writes=[K("nWT")])
            V = pb[:, 384:512]; Oh = pb[:, 0:128]; SSh = pb[:, 128:256]

            def chk():
                assert n == 0 or r_done.get((n - 1, h)), ("emission order violated", n, h)
            ops.append(chk)
            Lgr("pe", [mm(V, Qm, Vb, start=True, stop=False), mm(V, nWT, S_b[:, h, :], start=False, stop=True)],
                reads=[K("Qm"), K("Vb"), K("nWT"), ("S_b", h)], writes=[bk])
            Lop("dve", cp(vn, V), reads=[bk], writes=[K("vn")])
            Lgr("pe", [mm(V, Kd, vn),
                       mm(Oh, S_b[:, h, :], qg, start=True, stop=False), mm(Oh, vn, attnT, start=False, stop=True)],
                reads=[K("Kd"), K("vn"), ("S_b", h), K("qg"), K("attnT")], writes=[bk])
            Lop("dve", stt(S_f[:, h, :], S_f[:, h, :], col(egl_c), V, ALU.mult, ALU.add), reads=[bk, ("S_f", h), "egl_c"], writes=[("S_f", h)])
            Lop("pool", cp(S_b[:, h, :], S_f[:, h, :]), reads=[("S_f", h)], writes=[("S_b", h)])

            def mark():
                r_done[(n, h)] = True
            ops.append(mark)
            rn, on = rn8[L], on8[L]
            Lop("act", actf(osq, Oh, AF.Square), reads=[bk], writes=[K("osq")])
            Lgr("pe", [mm(SSh, ones_b, osq)], reads=[K("osq"), "ones_b"], writes=[bk])
            Lop("dve", ts(rn, SSh, 1.0 / 128, ALU.mult, EPS, ALU.add), reads=[bk], writes=[K("rn")])
            Lop("act", actf(rn, rn, AF.Sqrt), reads=[K("rn")], writes=[K("rn")])
            Lop("dve", lambda e: e.reciprocal(out=rn, in_=rn), reads=[K("rn")], writes=[K("rn")])
            Lop("dve", stt(on, Oh, gdnn_col, rn, ALU.mult, ALU.mult), reads=[bk, K("rn"), "cols"], writes=[K("on")])
            Lop("pool", tt(cat[:, h, tok], on, zs[:, h, tok], ALU.mult), reads=[K("on")], writes=[("cat_o", n, h)])
            return ops

        if NCH:
            u0 = gdn_unit(0, 0)
            LU = len(u0)
            STAG8 = int(os.environ.get('GDN_STAG', '7'))
            lanes = []
            for h in range(4):
                for par in range(2):
                    pad = h * STAG8 + par * (LU // 2)
                    lane = [(lambda: None)] * pad
                    for n in range(par, NCH, 2):
                        lane = lane + gdn_unit(n, h)
                    lanes.append(lane)
            zipper(lanes)
        S.dma("sp", "o_ngp", dmaf(ngp.rearrange("h d e -> d h e"), S_f), reads=[("S_f", h) for h in range(4)])

        S.barrier()

        if 'phasec' in _SK:
            S.finish()
            with nc.Block() as block:
                S.replay(block)
            return nc
        YC = Bump(arena, P_C0, ARENA)
        R = YC.alloc(F32, [128, 8, 528]); xnC = YC.alloc(BF16, [128, 8, 528]); hid = YC.alloc(BF16, [128, 32, 528])
        r8 = [YC.alloc(BF16, [128, 8, 512]) for _ in range(3)]
        r16 = [YC.alloc(BF16, [128, 32, 256]) for _ in range(2)]
        xres = YC.alloc(F32, [128, 4, 1024]); xres_s = YC.alloc(F32, [128, 1024])
        pw = YC.alloc(BF16, [128, 2, 1024]); ptok = [YC.alloc(BF16, [128, 256]) for _ in range(2)]
        pT = YC.alloc(BF16, [128, 2, 528]); sqr = [YC.alloc(BF16, [128, 528]) for _ in range(2)]; rnC = YC.alloc(F32, [128, 528])
        sig = [YC.alloc(F32, [128, 528]) for _ in range(2)]; relu_t = [YC.alloc(F32, [128, 528]) for _ in range(2)]
        ytile = [YC.alloc(F32, [128, 1024]) for _ in range(1)]
        rncol = YC.alloc(F32, [128, 8])
        r8_n = [0]; r16_n = [0]; misc_n = [0]

        r8_seq = []
        for _p in range(4):
            r8_seq += [(w_out_v, 0), (w_out_v, 512)] + [(w_up_v, bb * 512) for bb in range(8)] + [(w_gate_v, 0), (w_gate_v, 512)]
        r8_issued = [0]

        def load_r8(view, c0):
            idx = r8_n[0]
            r8_n[0] += 1
            assert r8_seq[idx][1] == c0
            while r8_issued[0] < min(len(r8_seq), idx + 3):
                j = r8_issued[0]
                vw, cc = r8_seq[j]
                S.dma("pool", "r8_%d" % (j % 3), dmaf(r8[j % 3], vw[:, :, cc:cc + 512]), writes=[("r8", j % 3)])
                r8_issued[0] += 1
            return idx % 3

        def load_r16(c0):
            sl = r16_n[0] % 2
            r16_n[0] += 1
            S.dma("pool", "r16_%d" % sl, dmaf(r16[sl], w_down_v[:, :, c0:c0 + 256]), writes=[("r16", sl)])
            return sl

        S.dma("pool", "pw", dmaf(pw, w_ple_v), writes=["pw"])
        PASSES = [[(0, 512, 0)], [(512, 512, 0)], [(1024, 512, 0)], [(1536, 512, 0), (2048, 16, 512)]]

        def rms_norm_C(which, out_fn, segs, W, tag):
            bns = []
            for (t0, n, l0) in segs:
                bns.append(6 + misc_n[0] % 2)
                misc_n[0] += 1
            for m in range(8):
                sq = sqr[m % 2]
                S.op("act", actf(sq[:, 0:W], R[:, m, 0:W], AF.Square), reads=[("R", m)], writes=[("sqr", m % 2)])
                for si_, (t0, n, l0) in enumerate(segs):
                    bn = bns[si_]
                    S.group("pe", [mm(ps[bn][:, 0:n], ones_b, sq[:, l0:l0 + n], start=(m == 0), stop=(m == 7))],
                            reads=[("sqr", m % 2), "ones_b"], writes=[PK[bn]])
            for si_, (t0, n, l0) in enumerate(segs):
                bn = bns[si_]
                S.op("dve", ts(rnC[:, l0:l0 + n], ps[bn][:, 0:n], 1.0 / 1024, ALU.mult, EPS, ALU.add), reads=[PK[bn]], writes=["rnC"])
            S.op("act", actf(rnC[:, 0:W], rnC[:, 0:W], AF.Sqrt), reads=["rnC"], writes=["rnC"])
            S.op("dve", lambda e: e.reciprocal(out=rnC[:, 0:W], in_=rnC[:, 0:W]), reads=["rnC"], writes=["rnC"])
            for m in range(8):
                out_ap, wkey = out_fn(m)
                S.op("dve", stt(out_ap, R[:, m, 0:W], gcol(which, m), rnC[:, 0:W], ALU.mult, ALU.mult), reads=[("R", m), "rnC", "cols"], writes=[wkey])

        for pi, segs in enumerate(PASSES):
            W = sum(n for (_, n, _) in segs)
            t00 = segs[0][0]
            has_s = len(segs) > 1
            if pi == 0:
                S.dma("sp", "xres", dmaf(xres, x_p[0:512, :].rearrange("(j p) f -> p j f", p=128)), writes=["xres"])
            def stats_act(m):
                S.op("act", actf(sqr[m % 2][:, 0:W], R[:, m, 0:W], AF.Square), reads=[("R", m)], writes=[("sqr", m % 2)])

            def stats_pe(m, bns):
                for si_, (t0, n, l0) in enumerate(segs):
                    S.group("pe", [mm(ps[bns[si_]][:, 0:n], ones_b, sqr[m % 2][:, l0:l0 + n], start=(m == 0), stop=(m == 7))],
                            reads=[("sqr", m % 2), "ones_b"], writes=[PK[bns[si_]]])

            def norm_finish_row(bns, out_t, key, square):
                for si_, (t0, n, l0) in enumerate(segs):
                    S.op("dve", ts(out_t[:, l0:l0 + n], ps[bns[si_]][:, 0:n], 1.0 / 1024, ALU.mult, EPS, ALU.add), reads=[PK[bns[si_]]], writes=[key])
                if not square:
                    S.op("act", actf(out_t[:, 0:W], out_t[:, 0:W], AF.Sqrt), reads=[key], writes=[key])
                S.op("dve", lambda e: e.reciprocal(out=out_t[:, 0:W], in_=out_t[:, 0:W]), reads=[key], writes=[key])

            def pick_bns():
                o = []
                for _ in segs:
                    o.append(6 + misc_n[0] % 2)
                    misc_n[0] += 1
                return o

            bns1 = pick_bns()
            for blk in range(2):
                sl = load_r8(w_out_v, blk * 512)
                for m4 in range(4):
                    m = blk * 4 + m4
                    for (t0, n, l0) in segs:
                        b = next_bank()
                        fns = [mm(ps[b][:, 0:n], r8[sl][:, k, m4 * 128:(m4 + 1) * 128], cat[:, k, t0:t0 + n], start=(k == 0), stop=False) for k in range(8)]
                        if n == 512:
                            fns += [mm(ps[b][:, j * 128:(j + 1) * 128], xres[:, j, m * 128:(m + 1) * 128], ident_f, start=False, stop=(j == 3)) for j in range(4)]
                            rk = ["xres"]
                        else:
                            fns += [mm(ps[b][:, 0:16], xres_s[0:16, m * 128:(m + 1) * 128], ident_f[0:16, 0:16], start=False, stop=True)]
                            rk = ["xres_s"]
                        S.group("pe", fns, reads=[("r8", sl), "cat", "ident_f"] + rk, writes=[PK[b]])
                        S.op("act", actf(R[:, m, l0:l0 + n], ps[b][:, 0:n], AF.Copy), reads=[PK[b]], writes=[("R", m)])
                        S.op("act", actf(xnC[:, m, l0:l0 + n], ps[b][:, 0:n], AF.Copy, scale=gcol(0, m)), reads=[PK[b], "cols"], writes=[("xnC", m)])
                    stats_act(m)
                    if m >= 1:
                        stats_pe(m - 1, bns1)
            stats_pe(7, bns1)
            norm_finish_row(bns1, rnC, "rnC", True)
            if pi + 1 < len(PASSES):
                tn = PASSES[pi + 1][0][0]
                S.dma("sp", "xres", dmaf(xres, x_p[tn:tn + 512, :].rearrange("(j p) f -> p j f", p=128)), writes=["xres"])
                if len(PASSES[pi + 1]) > 1:
                    S.dma("sp", "xres_s", dmaf(xres_s[0:16, :], x_s), writes=["xres_s"])
            for blk in range(8):
                sl = load_r8(w_up_v, blk * 512)
                for m4 in range(4):
                    hc = blk * 4 + m4
                    for (t0, n, l0) in segs:
                        b = next_bank()
                        S.group("pe", [mm(ps[b][:, 0:n], r8[sl][:, k, m4 * 128:(m4 + 1) * 128], xnC[:, k, l0:l0 + n], start=(k == 0), stop=(k == 7)) for k in range(8)],
                                reads=[("r8", sl)] + [("xnC", k) for k in range(8)], writes=[PK[b]])
                        rt = relu_t[misc_n[0] % 2]; rkey = ("relu_t", misc_n[0] % 2)
                        misc_n[0] += 1
                        S.op("act", actf(rt[:, 0:n], ps[b][:, 0:n], AF.Relu), reads=[PK[b]], writes=[rkey])
                        S.op("dve", tt(hid[:, hc, l0:l0 + n], rt[:, 0:n], rt[:, 0:n], ALU.mult), reads=[rkey], writes=[("hid", hc)])
            bns2 = pick_bns()
            for blk in range(4):
                sl = load_r16(blk * 256)
                for m2 in range(2):
                    m = blk * 2 + m2
                    for (t0, n, l0) in segs:
                        b = next_bank()
                        S.group("pe", [mm(ps[b][:, 0:n], r16[sl][:, k, m2 * 128:(m2 + 1) * 128], hid[:, k, l0:l0 + n], start=(k == 0), stop=(k == 31)) for k in range(32)],
                                reads=[("r16", sl)] + [("hid", k) for k in range(32)], writes=[PK[b]])
                        sg = sig[misc_n[0] % 2]; skey = ("sig", misc_n[0] % 2)
                        misc_n[0] += 1
                        S.op("dve", tt(sg[:, 0:n], ps[b][:, 0:n], rnC[:, l0:l0 + n], ALU.mult), reads=[PK[b], "rnC"], writes=[skey])
                        S.op("dve", tt(R[:, m, l0:l0 + n], R[:, m, l0:l0 + n], sg[:, 0:n], ALU.add), reads=[skey, ("R", m)], writes=[("R", m)])
                    S.op("act", actf(xnC[:, m, 0:W], R[:, m, 0:W], AF.Copy, scale=gcol(1, m)), reads=[("R", m), "cols"], writes=[("xnC", m)])
                    stats_act(m)
                    if m >= 1:
                        stats_pe(m - 1, bns2)
            stats_pe(7, bns2)
            norm_finish_row(bns2, rnC, "rnC", False)
            for (t0, n, l0) in segs:
                ntile = (n + 127) // 128
                for j in range(ntile):
                    r = min(128, n - j * 128)
                    sl = misc_n[0] % 2
                    misc_n[0] += 1
                    src = p_p[t0 + j * 128:t0 + j * 128 + r, :] if n == 512 else p_s
                    S.dma("pool", "ptok%d" % sl, dmaf(ptok[sl][0:r, :], src), writes=[("ptok", sl)])
                    S.group("pe", [lambda e, kk=kk, sl=sl, r=r: e.transpose(out=psb[5][:, kk * 128:kk * 128 + r], in_=ptok[sl][0:r, kk * 128:(kk + 1) * 128], identity=ident_b[0:r, 0:r]) for kk in range(2)],
                            reads=[("ptok", sl), "ident_b"], writes=[PK[5]])
                    S.op("act", actf(pT[:, :, l0 + j * 128:l0 + j * 128 + r], psb[5][:, 0:256].rearrange("p (k t) -> p k t", k=2)[:, :, 0:r], AF.Copy), reads=[PK[5]], writes=["pT"])
            ntt = sum((n + 127) // 128 for (_, n, _) in segs)
            sigbufs = [(sig[0], ("sig", 0)), (sig[1], ("sig", 1)), (relu_t[0], ("relu_t", 0)), (relu_t[1], ("relu_t", 1))]

            def gate_chunk(m, sl, m4):
                ops = []
                Lop, Lgr, Ldma = mk_recorders(S, ops)
                for si_, (t0, n, l0) in enumerate(segs):
                    b = (2 * m + si_) % 4
                    pb_ = 6 + m % 2
                    sg, skey = sigbufs[(2 * m + si_) % 4]
                    Lgr("pe", [mm(ps[b][:, 0:n], r8[sl][:, k, m4 * 128:(m4 + 1) * 128], xnC[:, k, l0:l0 + n], start=(k == 0), stop=(k == 7)) for k in range(8)],
                        reads=[("r8", sl)] + [("xnC", k) for k in range(8)], writes=[PK[b]])
                    Lgr("pe", [mm(ps[pb_][:, 0:n], pw[:, kk, m * 128:(m + 1) * 128], pT[:, kk, l0:l0 + n], start=(kk == 0), stop=(kk == 1)) for kk in range(2)],
                        reads=["pw", "pT"], writes=[PK[pb_]])
                    Lop("dve", tt(sg[:, 0:n], ps[b][:, 0:n], rnC[:, l0:l0 + n], ALU.mult), reads=[PK[b], "rnC"], writes=[skey])
                    Lop("act", actf(sg[:, 0:n], sg[:, 0:n], AF.Sigmoid), reads=[skey], writes=[skey])
                    Lop("dve", tt(sg[:, 0:n], sg[:, 0:n], ps[pb_][:, 0:n], ALU.mult), reads=[PK[pb_], skey], writes=[skey])
                    Lop("dve", tt(R[:, m, l0:l0 + n], R[:, m, l0:l0 + n], sg[:, 0:n], ALU.add), reads=[skey, ("R", m)], writes=[("R", m)])
                sq = sqr[m % 2]
                Lop("act", actf(sq[:, 0:W], R[:, m, 0:W], AF.Square), reads=[("R", m)], writes=[("sqr", m % 2)])
                fns = []
                if m == 0:
                    fns.append(mm(ps[5][:, 256:256 + ntt], zeros_f, zeros_f[:, 0:ntt], start=True, stop=False))
                jt = 0
                for (t0, n, l0) in segs:
                    for j in range((n + 127) // 128):
                        r = min(128, n - j * 128)
                        fns.append(mm(ps[5][0:r, 256 + jt:257 + jt], sq[:, l0 + j * 128:l0 + j * 128 + r], ones_b[:, 0:1], start=False, stop=False))
                        jt += 1
                if m == 7:
                    fns.append(mm(ps[5][:, 256:256 + ntt], zeros_f, zeros_f[:, 0:ntt], start=False, stop=True))
                Lgr("pe", fns, reads=[("sqr", m % 2), "ones_b"], writes=[PK[5]])
                Lop("act", actf(R[:, m, 0:W], R[:, m, 0:W], AF.Copy, scale=gcol(2, m)), reads=[("R", m), ("sqr", m % 2), "cols"], writes=[("R", m)])
                return ops

            for blk in range(2):
                sl = load_r8(w_gate_v, blk * 512)
                chunks = [gate_chunk(blk * 4 + m4, sl, m4) for m4 in range(4)]
                zipper(chunks[0:2])
                zipper(chunks[2:4])
            S.op("dve", ts(rncol[:, 0:ntt], ps[5][:, 256:256 + ntt], 1.0 / 1024, ALU.mult, EPS, ALU.add), reads=[PK[5]], writes=["rncol"])
            S.op("act", actf(rncol[:, 0:ntt], rncol[:, 0:ntt], AF.Sqrt), reads=["rncol"], writes=["rncol"])
            S.op("dve", lambda e: e.reciprocal(out=rncol[:, 0:ntt], in_=rncol[:, 0:ntt]), reads=["rncol"], writes=["rncol"])
            jt = 0
            for (t0, n, l0) in segs:
                ntile = (n + 127) // 128
                for j in range(ntile):
                    r = min(128, n - j * 128)
                    ysl = 0
                    for half in range(2):
                        b = next_bank()
                        S.group("pe", [mm(ps[b][0:r, m4 * 128:(m4 + 1) * 128], R[:, half * 4 + m4, l0 + j * 128:l0 + j * 128 + r], ident_f) for m4 in range(4)],
                                reads=[("R", half * 4 + m4) for m4 in range(4)] + ["ident_f"], writes=[PK[b]])
                        S.op("act", actf(ytile[ysl][0:r, half * 512:(half + 1) * 512], ps[b][0:r, :], AF.Copy, scale=rncol[0:r, jt:jt + 1]), reads=[PK[b], "rncol"], writes=[("ytile", ysl)])
                    jt += 1
                    dst = y_p[t0 + j * 128:t0 + j * 128 + r, :] if n == 512 else y_s
                    S.dma("sp", "o_y%d" % ysl, dmaf(dst, ytile[ysl][0:r, :]), reads=[("ytile", ysl)])
        S.finish()
        with nc.Block() as block:
            S.replay(block)
    return nc


_PROG = {}


def _make_in_maps(inputs):
    f = lambda a: np.ascontiguousarray(np.asarray(a, dtype=np.float32))
    g = {k: f(v) for k, v in inputs.items()}
    shared = {
        "g_mix": g["g_mix"].reshape(1, 1024), "w_in": g["w_in"][0], "w_conv": g["w_conv"][0],
        "a_log": g["a_log"].reshape(1, 4), "dt_bias": g["dt_bias"].reshape(1, 4), "gdn_norm": g["gdn_norm"].reshape(1, 128),
        "ln_g": g["sgu_ln_g"].reshape(1, 512), "ln_b": g["sgu_ln_b"].reshape(1, 512), "w_s": g["w_s"][0],
        "b_s": g["b_s"].reshape(1, 512), "w_out": g["w_out"][0], "g_ff": g["g_ff"].reshape(8, 128), "w_up": g["w_up"][0],
        "w_down": g["w_down"][0], "g_ple": g["g_ple"].reshape(8, 128), "w_ple": g["w_ple"][0], "w_gate": g["w_ple_gate"][0],
        "g_fin": g["g_final"].reshape(8, 128),
    }
    maps = []
    for i in range(8):
        m = dict(shared)
        sl = slice(16 * i, 16 * i + 16)
        m["x_p"] = g["x_prompt"][i]
        m["x_s"] = g["x_sample"][sl, 0]
        m["st_conv"] = g["state_conv"][0, sl]
        m["st_gdn"] = g["state_gdn"][0, sl]
        m["p_p"] = g["p_prompt"][0, i]
        m["p_s"] = g["p_sample"][0, sl, 0]
        maps.append(m)
    return maps


def kernel(**inputs):
    if "nc" not in _PROG:
        _PROG["nc"] = build_program()
    nc = _PROG["nc"]
    maps = _make_in_maps(inputs)
    res = run_bass_kernel_spmd(nc, maps, core_ids=list(range(8)))
    R = res.results
    st = lambda name: np.stack([np.asarray(r[name], dtype=np.float32) for r in R])
    cc = lambda name: np.concatenate([np.asarray(r[name], dtype=np.float32) for r in R], axis=0)
    y_prompt = st("y_p")
    y_sample = cc("y_s")[:, None, :]
    new_conv_prompt = st("ncp")[None]
    new_gdn_prompt = st("ngp")[None]
    new_conv_sample = cc("ncs")[None]
    new_gdn_sample = cc("ngs")[None]
    new_sgu_v_sample = cc("nsv")[None, :, None, :]
    return (y_prompt, y_sample, new_conv_prompt, new_gdn_prompt, new_conv_sample, new_gdn_sample, new_sgu_v_sample)
```

```python
import os
import numpy as np
import concourse.bass as bass
import concourse.mybir as mybir
from concourse.bass_utils import run_bass_kernel_spmd

F32 = mybir.dt.float32
BF16 = mybir.dt.bfloat16
AF = mybir.ActivationFunctionType
ALU = mybir.AluOpType
AX = mybir.AxisListType


class Sched:
    ENGS = ("pe", "act", "dve", "pool", "sp")

    def __init__(self, nc, stack):
        self.nc = nc
        self.stack = stack
        self.streams = {e: [] for e in self.ENGS}
        self.esem = {e: stack.enter_context(nc.semaphore("c_" + e)) for e in self.ENGS[:4]}
        self.ecnt = {e: 0 for e in self.ENGS}
        self.waited = {e: {} for e in self.ENGS}
        self.res = {}
        self.dsem = {}
        self.sem_by_name = {}
        for e in self.ENGS[:4]:
            self.sem_by_name[self.esem[e].name] = self.esem[e]

    def _need(self, eng, ev, waits):
        if ev is None:
            return
        name, val, src = ev
        if src == eng and eng == "pe":
            return
        cur = waits.get(name, 0)
        if val > cur:
            waits[name] = val

    def _deps(self, eng, reads, writes):
        waits = {}
        for k in reads:
            r = self.res.get(k)
            if r is not None:
                self._need(eng, r[0], waits)
        for k in writes:
            r = self.res.get(k)
            if r is not None:
                if r[0] is not None and not (r[0][2] == eng):
                    self._need(eng, r[0], waits)
                for ev in r[1]:
                    self._need(eng, ev, waits)
        out = []
        w = self.waited[eng]
        for name, val in waits.items():
            if w.get(name, 0) < val:
                w[name] = val
                out.append((name, val))
        return out

    def _commit(self, ev, reads, writes):
        for k in reads:
            r = self.res.setdefault(k, [None, []])
            r[1].append(ev)
        for k in writes:
            self.res[k] = [ev, []]

    def op(self, eng, fn, reads=(), writes=()):
        waits = self._deps(eng, reads, writes)
        self.ecnt[eng] += 1
        ev = (self.esem[eng].name, self.ecnt[eng], eng)
        self.streams[eng].append((waits, [fn], ("inc", self.esem[eng], 1)))
        self._commit(ev, reads, writes)
        return ev

    def group(self, eng, fns, reads=(), writes=()):
        waits = self._deps(eng, reads, writes)
        self.ecnt[eng] += 1
        ev = (self.esem[eng].name, self.ecnt[eng], eng)
        self.streams[eng].append((waits, list(fns), ("inc", self.esem[eng], 1)))
        self._commit(ev, reads, writes)
        return ev

    def dma(self, eng, slot, fn, reads=(), writes=(), n=1):
        if slot not in self.dsem:
            s = self.stack.enter_context(self.nc.semaphore("d_" + slot))
            self.dsem[slot] = [s, 0]
            self.sem_by_name[s.name] = s
        waits = self._deps(eng, reads, writes)
        d = self.dsem[slot]
        fns = fn if isinstance(fn, (list, tuple)) else [fn]
        d[1] += 16 * len(fns)
        ev = (d[0].name, d[1], "dma")
        self.streams[eng].append((waits, list(fns), ("dmainc", d[0], 16)))
        self._commit(ev, reads, writes)
        return ev

    def barrier(self, skip=()):
        evs = []
        for e in self.ENGS[:4]:
            if self.ecnt[e] > 0:
                evs.append((self.esem[e].name, self.ecnt[e]))
        for slot, (s, c) in self.dsem.items():
            if c > 0 and not any(slot.startswith(p) for p in skip):
                evs.append((s.name, c))
        for eng in self.ENGS:
            w = self.waited[eng]
            waits = []
            for name, val in evs:
                if w.get(name, 0) < val:
                    w[name] = val
                    waits.append((name, val))
            if waits:
                self.streams[eng].append((waits, [], None))
        self.res.clear()

    def finish(self):
        eng = "sp"
        waits = []
        for slot, (s, c) in self.dsem.items():
            if c > 0:
                waits.append((s.name, c))
        for e in self.ENGS[:4]:
            if self.ecnt[e] > 0:
                waits.append((self.esem[e].name, self.ecnt[e]))
        self.streams[eng].append((waits, [], None))

    def replay(self, block):
        sbn = self.sem_by_name

        def run(e, items):
            for waits, fns, inc in items:
                for name, val in waits:
                    e.wait_ge(sbn[name], val)
                last = None
                for i, f in enumerate(fns):
                    ins = f(e)
                    if inc is not None and inc[0] == "dmainc":
                        ins.then_inc(inc[1], 16)
                    last = ins
                if inc is not None and inc[0] == "inc" and last is not None:
                    last.then_inc(inc[1], 1)

        st = self.streams

        @block.tensor
        def _(e):
            run(e, st["pe"])

        @block.scalar
        def _(e):
            run(e, st["act"])

        @block.vector
        def _(e):
            run(e, st["dve"])

        @block.gpsimd
        def _(e):
            run(e, st["pool"])

        @block.sync
        def _(e):
            run(e, st["sp"])


U8 = mybir.dt.uint8
T_P = 2048
T_S = 16
T_ALL = T_P + T_S
SEGS = [(0, 512), (512, 512), (1024, 512), (1536, 512), (2048, 16)]
NT = 17
EPS = 1e-6
D_IN = 3080
C_Q, C_K, C_V, C_Z, C_BA, C_U, C_VS = 0, 512, 1024, 1536, 2048, 2056, 2568


def mm(out, lhsT, rhs, start=True, stop=True):
    return lambda e: e.matmul(out, lhsT=lhsT, rhs=rhs, start=start, stop=stop)


def actf(out, in_, func, **kw):
    return lambda e: e.activation(out=out, in_=in_, func=func, **kw)


def tt(out, a, b, op):
    return lambda e: e.tensor_tensor(out=out, in0=a, in1=b, op=op)


def ts(out, a, s1, op0, s2=None, op1=None):
    if op1 is None:
        return lambda e: e.tensor_scalar(out=out, in0=a, scalar1=s1, scalar2=None, op0=op0)
    return lambda e: e.tensor_scalar(out=out, in0=a, scalar1=s1, scalar2=s2, op0=op0, op1=op1)


def stt(out, a, s, b, op0, op1):
    return lambda e: e.scalar_tensor_tensor(out=out, in0=a, scalar=s, in1=b, op0=op0, op1=op1)


def stt_pool(out, a, colap):
    return lambda e: e.tensor_tensor(out=out, in0=a, in1=colap.to_broadcast([128, 128]), op=ALU.mult)


def cp(out, in_):
    return lambda e: e.tensor_copy(out=out, in_=in_)


def dmaf(out, in_):
    return lambda e: e.dma_start(out=out, in_=in_)


class _Item:
    __slots__ = ("thunk", "eng", "reads", "writes", "dur")

    def __init__(self, thunk, eng, reads, writes, dur):
        self.thunk, self.eng, self.reads, self.writes, self.dur = thunk, eng, tuple(reads), tuple(writes), dur


_DUR = {"act": 0.5, "dve": 0.45, "pool": 0.5}


def mk_recorders(S, ops):
    def Lop(eng, fn, reads=(), writes=()):
        ops.append(_Item(lambda: S.op(eng, fn, reads=reads, writes=writes), eng, reads, writes, _DUR.get(eng, 0.4)))

    def Lgr(eng, fns, reads=(), writes=()):
        ops.append(_Item(lambda: S.group(eng, fns, reads=reads, writes=writes), eng, reads, writes, 0.1 + 0.13 * len(fns)))

    def Ldma(eng, slot, fn, reads=(), writes=()):
        ops.append(_Item(lambda: S.dma(eng, slot, fn, reads=reads, writes=writes), "q_" + eng, reads, writes, 2.5))
    return Lop, Lgr, Ldma


def zipper(lists):
    lists = [l for l in lists if l]
    idx = [0] * len(lists)
    if os.environ.get("ZIP", "rr") == "rr":
        live = True
        while live:
            live = False
            for i, l in enumerate(lists):
                if idx[i] < len(l):
                    it = l[idx[i]]
                    idx[i] += 1
                    live = True
                    if isinstance(it, _Item):
                        it.thunk()
                    else:
                        it()
        return
    t_eng, t_w, t_r = {}, {}, {}
    remaining = sum(len(l) for l in lists)
    while remaining:
        best = None
        for i, l in enumerate(lists):
            if idx[i] >= len(l):
                continue
            it = l[idx[i]]
            if not isinstance(it, _Item):
                best = (-1.0, i, it)
                break
            rdy = t_eng.get(it.eng, 0.0)
            for k in it.reads:
                rdy = max(rdy, t_w.get(k, 0.0))
            for k in it.writes:
                rdy = max(rdy, t_w.get(k, 0.0), t_r.get(k, 0.0))
            if best is None or rdy < best[0]:
                best = (rdy, i, it)
        rdy, i, it = best
        idx[i] += 1
        remaining -= 1
        if not isinstance(it, _Item):
            it()
            continue
        it.thunk()
        fin = rdy + it.dur
        if it.eng.startswith("q_"):
            t_eng[it.eng] = rdy + 0.1
        else:
            t_eng[it.eng] = fin
        for k in it.reads:
            t_r[k] = max(t_r.get(k, 0.0), fin)
        for k in it.writes:
            t_w[k] = fin
            t_r[k] = 0.0


class Bump:
    def __init__(self, arena, start, limit):
        self.t, self.off, self.limit = arena, start, limit

    def alloc(self, dtype, shape):
        esz = 4 if dtype == F32 else 2
        n = 1
        for s in shape[1:]:
            n *= s
        nb = (n * esz + 63) // 64 * 64
        o = self.off
        self.off += nb
        assert self.off <= self.limit, ("SBUF arena overflow", self.off, self.limit)
        ap = self.t[:, o:o + n * esz].bitcast(dtype)
        if len(shape) == 3:
            ap = ap.rearrange("p (a b) -> p a b", a=shape[1])
        elif len(shape) == 4:
            ap = ap.rearrange("p (a b c) -> p a b c", a=shape[1], b=shape[2])
        return ap


def build_program():
    from contextlib import ExitStack
    nc = bass.Bass("TRN2", target_bir_lowering=False)

    def din(name, shape):
        return nc.dram_tensor(name, shape, F32, kind="ExternalInput").ap()

    def dout(name, shape):
        return nc.dram_tensor(name, shape, F32, kind="ExternalOutput").ap()

    x_p = din("x_p", [T_P, 1024]); x_s = din("x_s", [T_S, 1024])
    st_conv = din("st_conv", [T_S, 3, 1536]); st_gdn = din("st_gdn", [T_S, 4, 128, 128])
    p_p = din("p_p", [T_P, 256]); p_s = din("p_s", [T_S, 256])
    g_mix = din("g_mix", [1, 1024]); w_in = din("w_in", [1024, D_IN]); w_conv = din("w_conv", [4, 1536])
    a_log = din("a_log", [1, 4]); dt_bias = din("dt_bias", [1, 4]); gdn_norm = din("gdn_norm", [1, 128])
    ln_g = din("ln_g", [1, 512]); ln_b = din("ln_b", [1, 512]); w_s = din("w_s", [4, 128, 128]); b_s = din("b_s", [1, 512])
    w_out = din("w_out", [1024, 1024]); g_ff = din("g_ff", [8, 128]); w_up = din("w_up", [1024, 4096]); w_down = din("w_down", [4096, 1024])
    g_ple = din("g_ple", [8, 128]); w_ple = din("w_ple", [256, 1024]); w_gate = din("w_gate", [1024, 1024]); g_fin = din("g_fin", [8, 128])
    y_p = dout("y_p", [T_P, 1024]); y_s = dout("y_s", [T_S, 1024])
    ncp = dout("ncp", [3, 1536]); ngp = dout("ngp", [4, 128, 128])
    ncs = dout("ncs", [T_S, 3, 1536]); ngs = dout("ngs", [T_S, 4, 128, 128]); nsv = dout("nsv", [T_S, 512])

    w_in_v = w_in.rearrange("(k p) c -> p k c", p=128)
    w_out_v = w_out.rearrange("(k p) c -> p k c", p=128)
    w_up_v = w_up.rearrange("(k p) c -> p k c", p=128)
    w_down_v = w_down.rearrange("(k p) c -> p k c", p=128)
    w_gate_v = w_gate.rearrange("(k p) c -> p k c", p=128)
    w_ple_v = w_ple.rearrange("(k p) c -> p k c", p=128)

    with ExitStack() as st:
        S = Sched(nc, st)
        ARENA = 206 * 1024
        arena = st.enter_context(nc.sbuf_tensor("arena", [128, ARENA], U8))
        ps = [st.enter_context(nc.psum_tensor("ps%d" % i, [128, 512], F32)) for i in range(8)]
        psb = [p[:, :].bitcast(BF16) for p in ps]
        PK = [("ps", i) for i in range(8)]

        P = Bump(arena, 0, ARENA)
        ident_f = P.alloc(F32, [128, 128]); ident_b = P.alloc(BF16, [128, 128])
        ones_f = P.alloc(F32, [128, 128]); ones_b = P.alloc(BF16, [128, 128])
        mask_incl = P.alloc(F32, [128, 128])
        mask_su = P.alloc(F32, [128, 128])
        nmask_sl = P.alloc(F32, [128, 128])
        sel127 = P.alloc(F32, [128, 128])
        nmask_su = P.alloc(F32, [128, 128])
        mask_sl = P.alloc(F32, [128, 128])
        rowstage = P.alloc(F32, [128, 128])
        cols = P.alloc(F32, [128, 128])
        wsT = P.alloc(BF16, [128, 4, 128])
        selws = P.alloc(BF16, [128, 4, 16])
        ws00 = P.alloc(F32, [128, 4])
        bs_row = P.alloc(F32, [128, 4, 128])
        lng_row = P.alloc(F32, [128, 512]); lnb_row = P.alloc(F32, [128, 512])
        alog_row = P.alloc(F32, [128, 4]); dtb_row = P.alloc(F32, [128, 4]); nexpA_row = P.alloc(F32, [128, 4])
        zcol = P.alloc(F32, [128, 4])
        zeros_f = P.alloc(F32, [128, 128])
        cat = P.alloc(BF16, [128, 8, T_ALL])
        P_C0 = P.off
        ba = P.alloc(F32, [128, NT, 8])
        beta_c = P.alloc(F32, [128, NT, 4]); g_c = P.alloc(F32, [128, NT, 4]); gc_c = P.alloc(F32, [128, NT, 4])
        bexp_c = P.alloc(F32, [128, NT, 4]); kd_c = P.alloc(F32, [128, NT, 4]); egl_c = P.alloc(F32, [128, NT, 4])
        tmp68 = P.alloc(F32, [128, NT, 4])
        qkv = P.alloc(BF16, [128, 12, T_ALL])
        zs = P.alloc(BF16, [128, 4, T_ALL])
        qks_f = P.alloc(F32, [128, 12, 16])
        histT = P.alloc(F32, [128, 12, 3, 16])
        ncp_st = P.alloc(F32, [128, 12, 3]); ncs_st = P.alloc(F32, [128, 12, 16])
        S_f = P.alloc(F32, [128, 4, 128]); S_b = P.alloc(BF16, [128, 4, 128])
        X0 = P.off
        XB = Bump(arena, X0, ARENA)
        hT = XB.alloc(BF16, [128, 8, T_ALL])
        Y0 = XB.off

        def wcol(j, c):
            return cols[:, 24 + j * 12 + c: 24 + j * 12 + c + 1]

        def gcol(which, m):
            return cols[:, which * 8 + m: which * 8 + m + 1]
        gdnn_col = cols[:, 72:73]
        negone = zcol[:, 1:2]

        S.op("pool", lambda e: e.memset(ones_f, 1.0), writes=["ones_f"])
        S.op("pool", lambda e: e.memset(ones_b, 1.0), writes=["ones_b"])
        S.op("pool", lambda e: e.memset(zcol, 0.0), writes=["zcol"])
        S.op("pool", lambda e: e.memset(zcol[:, 1:2], -1.0), reads=["zcol"], writes=["zcol"])
        S.op("pool", lambda e: e.memset(zeros_f, 0.0), writes=["zeros_f"])
        S.op("pool", lambda e: e.affine_select(out=ident_f, in_=ones_f, pattern=[[-1, 128]], compare_op=ALU.is_equal, fill=0.0, base=0, channel_multiplier=1), reads=["ones_f"], writes=["ident_f"])
        S.op("pool", lambda e: e.affine_select(out=mask_incl, in_=ones_f, pattern=[[1, 128]], compare_op=ALU.is_ge, fill=0.0, base=0, channel_multiplier=-1), reads=["ones_f"], writes=["mask_incl"])
        S.op("pool", lambda e: e.affine_select(out=mask_su, in_=ones_f, pattern=[[1, 128]], compare_op=ALU.is_gt, fill=0.0, base=0, channel_multiplier=-1), reads=["ones_f"], writes=["mask_su"])
        S.op("pool", lambda e: e.affine_select(out=nmask_sl, in_=ones_f, pattern=[[-1, 128]], compare_op=ALU.is_gt, fill=0.0, base=0, channel_multiplier=1), reads=["ones_f"], writes=["nmask_sl"])
        S.op("pool", ts(nmask_sl, nmask_sl, -1.0, ALU.mult), reads=["nmask_sl"], writes=["nmask_sl"])
        S.op("pool", ts(nmask_su, mask_su, -1.0, ALU.mult), reads=["mask_su"], writes=["nmask_su"])
        S.op("pool", ts(mask_sl, nmask_sl, -1.0, ALU.mult), reads=["nmask_sl"], writes=["mask_sl"])
        S.op("pool", lambda e: e.affine_select(out=sel127, in_=ones_f, pattern=[[0, 128]], compare_op=ALU.is_equal, fill=0.0, base=-127, channel_multiplier=1), reads=["ones_f"], writes=["sel127"])
        S.op("dve", cp(ident_b, ident_f), reads=["ident_f"], writes=["ident_b"])
        S.op("pool", lambda e: e.memset(rowstage, 0.0), writes=["rowstage"])
        S.dma("sp", "c0", [dmaf(rowstage[0:8, :], g_ff), dmaf(rowstage[8:16, :], g_ple), dmaf(rowstage[16:24, :], g_fin),
                           dmaf(rowstage[24:72, :], w_conv.rearrange("j (c p) -> (j c) p", p=128)), dmaf(rowstage[72:73, :], gdn_norm)],
              writes=["rowstage"])
        S.group("pe", [mm(ps[0][:, 0:128], rowstage, ident_f)], reads=["rowstage", "ident_f"], writes=[PK[0]])
        S.op("dve", cp(cols, ps[0][:, 0:128]), reads=[PK[0]], writes=["cols"])
        S.dma("sp", "c1", [dmaf(bs_row.rearrange("p h t -> p (h t)"), b_s.partition_broadcast(128)),
                           dmaf(lng_row, ln_g.partition_broadcast(128)), dmaf(lnb_row, ln_b.partition_broadcast(128)),
                           dmaf(alog_row, a_log.partition_broadcast(128)), dmaf(dtb_row, dt_bias.partition_broadcast(128)),
                           ] + [dmaf(ws00[:, h:h + 1], w_s[h, 0, 0:1].partition_broadcast(128)) for h in range(4)],
              writes=["rows"])
        S.op("act", actf(nexpA_row, alog_row, AF.Exp), reads=["rows"], writes=["nexpA"])
        S.op("dve", ts(nexpA_row, nexpA_row, -1.0, ALU.mult), reads=["nexpA"], writes=["nexpA"])
        for h in range(4):
            S.op("dve", ts(selws[0:16, h, :], ident_f[0:16, 0:16], ws00[0:16, h:h + 1], ALU.mult), reads=["rows", "ident_f"], writes=[("selws", h)])

        YA = Bump(arena, Y0, ARENA)
        wstmp = YA.alloc(F32, [128, 4, 128])
        S.dma("sp", "c2", dmaf(wstmp, w_s.rearrange("h t s -> t h s")), writes=["wstmp"])
        for h in range(4):
            S.op("pool", lambda e, h=h: e.affine_select(out=wstmp[:, h, :], in_=wstmp[:, h, :], pattern=[[-1, 128]], compare_op=ALU.is_ge, fill=0.0, base=0, channel_multiplier=1),
                 reads=["wstmp"], writes=["wstmp"])
        S.group("pe", [mm(ps[1][:, h * 128:(h + 1) * 128], wstmp[:, h, :], ident_f) for h in range(4)], reads=["wstmp", "ident_f"], writes=[PK[1]])
        S.op("dve", cp(wsT.rearrange("p h t -> p (h t)"), ps[1][:, 0:512]), reads=[PK[1]], writes=["wsT"])

        gmix_row = YA.alloc(F32, [128, 1024])
        S.dma("sp", "c3", dmaf(gmix_row, g_mix.partition_broadcast(128)), writes=["gmix"])
        xt = [YA.alloc(F32, [128, 1024]) for _ in range(3)]
        xsq = [YA.alloc(F32, [128, 1024]) for _ in range(3)]
        xn = [YA.alloc(BF16, [128, 1024]) for _ in range(3)]
        stat = YA.alloc(F32, [128, NT, 2])

        def phaseA_tile(i):
            ops = []
            Lop, Lgr, Ldma = mk_recorders(S, ops)
            r = 128 if i < 16 else 16
            sl = i % 3
            src = x_p[i * 128:(i + 1) * 128, :] if i < 16 else x_s
            Ldma("sp", "xt%d" % sl, dmaf(xt[sl][0:r, :], src), writes=[("xt", sl)])
            Lop("act", actf(xsq[sl][0:r, :], xt[sl][0:r, :], AF.Square), reads=[("xt", sl)], writes=[("xsq", sl)])
            Lop("dve", lambda e: e.reduce_sum(out=stat[0:r, i, 0:1], in_=xsq[sl][0:r, :], axis=AX.X), reads=[("xsq", sl)], writes=[("stat", i)])
            Lop("dve", ts(stat[0:r, i, 1:2], stat[0:r, i, 0:1], 1.0 / 1024, ALU.mult, EPS, ALU.add), reads=[("stat", i)], writes=[("stat", i)])
            Lop("act", actf(stat[0:r, i, 1:2], stat[0:r, i, 1:2], AF.Sqrt), reads=[("stat", i)], writes=[("stat", i)])
            Lop("dve", lambda e: e.reciprocal(out=stat[0:r, i, 1:2], in_=stat[0:r, i, 1:2]), reads=[("stat", i)], writes=[("stat", i)])
            Lop("dve", stt(xn[sl][0:r, :], xt[sl][0:r, :], stat[0:r, i, 1:2], gmix_row[0:r, :], ALU.mult, ALU.mult),
                reads=[("xt", sl), ("stat", i), "gmix"], writes=[("xn", sl)])
            b = i % 3
            Lgr("pe", [lambda e, k=k: e.transpose(out=psb[b][:, k * 128:k * 128 + r], in_=xn[sl][0:r, k * 128:(k + 1) * 128], identity=ident_b[0:r, 0:r]) for k in range(8)],
                reads=[("xn", sl), "ident_b"], writes=[PK[b]])
            if i % 2 == 0:
                Lop("act", actf(hT[:, :, i * 128:i * 128 + r], psb[b].rearrange("p (k t) -> p k t", k=8)[:, :, 0:r], AF.Copy), reads=[PK[b]], writes=[("hT", i)])
            else:
                Lop("dve", cp(hT[:, :, i * 128:i * 128 + r], psb[b].rearrange("p (k t) -> p k t", k=8)[:, :, 0:r]), reads=[PK[b]], writes=[("hT", i)])
            return ops

        tilesA = [phaseA_tile(i) for i in range(NT)]
        for g0 in range(0, NT, 3):
            zipper(tilesA[g0:g0 + 3])
        S.barrier()

        YB = Bump(arena, Y0, ARENA)
        wb = [YB.alloc(BF16, [128, 8, 512]) for _ in range(2)]
        wb8 = YB.alloc(BF16, [128, 8, 8])
        lnst = YB.alloc(F32, [128, NT, 8])
        sct = [YB.alloc(F32, [128, 3, 128]) for _ in range(2)]
        YB_MID = YB.off
        vg = YB.alloc(BF16, [128, NT, 512])
        F6 = YB.alloc(F32, [128, 6, 512])
        f512 = [F6[:, i, :] for i in range(6)]
        vgs_f = f512[5]
        YB5 = Bump(arena, YB_MID, ARENA)
        NBS = 4
        pre = [YB5.alloc(F32, [128, 515]) for _ in range(NBS)]
        accb = [YB5.alloc(F32, [128, 512]) for _ in range(NBS)]
        rnb = [YB5.alloc(F32, [128, 512]) for _ in range(NBS)]
        sqb = [YB5.alloc(BF16, [128, 512]) for _ in range(NBS)]
        wdiag = [YB5.alloc(F32, [128, 4, 128]) for _ in range(2)]
        YB6 = Bump(arena, YB_MID, ARENA)
        stage_tok = YB6.alloc(F32, [128, 1536])
        stage2 = YB6.alloc(F32, [128, 1536])
        wb_n = [0]
        bank_n = [0]

        def next_bank(lo=0, hi=4):
            b = lo + bank_n[0] % (hi - lo)
            bank_n[0] += 1
            return b

        wb_seq = [C_VS, C_U, C_Z, 0, 512, 1024]
        wb_issued = [0]

        def load_w(view, c0, ncol=512):
            idx = wb_n[0]
            wb_n[0] += 1
            assert wb_seq[idx] == c0
            while wb_issued[0] < min(len(wb_seq), idx + 2):
                j = wb_issued[0]
                S.dma("pool", "wb%d" % (j % 2), dmaf(wb[j % 2][:, :, 0:512], view[:, :, wb_seq[j]:wb_seq[j] + 512]), writes=[("wb", j % 2)])
                wb_issued[0] += 1
            return idx % 2

        hT_keys = [("hT", i) for i in range(NT)]

        def seg_hT_keys(t0, n):
            return [("hT", i) for i in range(t0 // 128, (t0 + n + 127) // 128)]

        sl = load_w(w_in_v, C_VS)

        def vsgu_tile(i):
            ops = []
            Lop, Lgr, Ldma = mk_recorders(S, ops)
            r = 128 if i < 16 else 16
            b = (i % 3)
            Lgr("pe", [mm(ps[b][0:r, :], hT[:, k, i * 128:i * 128 + r], wb[sl][:, k, :], start=(k == 0), stop=(k == 7)) for k in range(8)],
                    reads=[("hT", i), ("wb", sl)], writes=[PK[b]])
            g1 = f512[i % 3]; g2 = f512[3 + i % 3]
            Lop("act", actf(g1[0:r, :], ps[b][0:r, :], AF.Gelu_apprx_tanh), reads=[PK[b]], writes=[("g1", i % 3)])
            Lop("pool", tt(g2[0:r, :], g1[0:r, :], g1[0:r, :], ALU.mult), reads=[("g1", i % 3)], writes=[("g2", i % 3)])
            Lop("dve", lambda e, i=i, r=r, g1=g1: e.reduce_sum(out=lnst[0:r, i, 0:1], in_=g1[0:r, :], axis=AX.X), reads=[("g1", i % 3)], writes=[("lnst", i)])
            Lop("dve", lambda e, i=i, r=r, g2=g2: e.reduce_sum(out=lnst[0:r, i, 1:2], in_=g2[0:r, :], axis=AX.X), reads=[("g2", i % 3)], writes=[("lnst", i)])
            L = lambda a, bb: lnst[0:r, i, a:bb]
            Lop("dve", ts(L(2, 3), L(0, 1), 1.0 / 512, ALU.mult), reads=[("lnst", i)], writes=[("lnst", i)])
            Lop("dve", tt(L(3, 4), L(2, 3), L(2, 3), ALU.mult), reads=[("lnst", i)], writes=[("lnst", i)])
            Lop("dve", stt(L(4, 5), L(1, 2), 1.0 / 512, L(3, 4), ALU.mult, ALU.subtract), reads=[("lnst", i)], writes=[("lnst", i)])
            Lop("dve", ts(L(4, 5), L(4, 5), EPS, ALU.add), reads=[("lnst", i)], writes=[("lnst", i)])
            Lop("act", actf(L(4, 5), L(4, 5), AF.Sqrt), reads=[("lnst", i)], writes=[("lnst", i)])
            Lop("dve", lambda e, i=i, r=r: e.reciprocal(out=lnst[0:r, i, 5:6], in_=lnst[0:r, i, 4:5]), reads=[("lnst", i)], writes=[("lnst", i)])
            Lop("dve", ts(g2[0:r, :], g1[0:r, :], L(2, 3), ALU.subtract, L(5, 6), ALU.mult), reads=[("g1", i % 3), ("lnst", i)], writes=[("g2", i % 3)])
            Lop("pool", tt(g2[0:r, :], g2[0:r, :], lng_row[0:r, :], ALU.mult), reads=[("g2", i % 3), "rows"], writes=[("g2", i % 3)])
            if i < 16:
                Lop("pool", tt(vg[0:r, i, :], g2[0:r, :], lnb_row[0:r, :], ALU.add), reads=[("g2", i % 3), "rows"], writes=[("vg", i)])
            else:
                Lop("pool", tt(vgs_f[0:r, :], g2[0:r, :], lnb_row[0:r, :], ALU.add), reads=[("g2", i % 3), "rows"], writes=[("g2", 2)])
                Lop("pool", cp(vg[0:r, i, :], vgs_f[0:r, :]), reads=[("g2", 2)], writes=[("vg", i)])
                Ldma("sp", "o_nsv", dmaf(nsv, vgs_f[0:r, :]), reads=[("g2", 2)])
            return ops

        tilesV = [vsgu_tile(i) for i in range(NT)]
        for g0 in range(0, NT, 3):
            zipper(tilesV[g0:g0 + 3])

        S.dma("pool", "wb8", dmaf(wb8, w_in_v[:, :, C_BA:C_BA + 8]), writes=["wb8"])
        S.op("pool", lambda e: e.memset(ba, 0.0), writes=["ba"])
        bq = 4
        for i in range(NT):
            r = 128 if i < 16 else 16
            S.group("pe", [mm(ps[bq][0:r, i * 8:(i + 1) * 8], hT[:, k, i * 128:i * 128 + r], wb8[:, k, :], start=(k == 0), stop=(k == 7)) for k in range(8)],
                    reads=[("hT", i), "wb8"], writes=[PK[bq]])
        S.op("dve", cp(ba[:, 0:16, :], ps[bq][:, 0:128].rearrange("p (i c) -> p i c", c=8)), reads=[PK[bq], "ba"], writes=["ba"])
        S.op("dve", cp(ba[0:16, 16, :], ps[bq][0:16, 128:136]), reads=[PK[bq], "ba"], writes=["ba"])
        S.op("act", actf(beta_c, ba[:, :, 0:4], AF.Sigmoid), reads=["ba"], writes=["beta_c"])
        S.op("dve", tt(tmp68, ba[:, :, 4:8], dtb_row.unsqueeze(1).to_broadcast([128, NT, 4]), ALU.add), reads=["ba", "rows"], writes=["tmp68"])
        S.op("act", actf(tmp68, tmp68, AF.Exp), reads=["tmp68"], writes=["tmp68"])
        S.op("act", actf(tmp68, tmp68, AF.Ln, bias=1.0), reads=["tmp68"], writes=["tmp68"])
        S.op("dve", tt(g_c, tmp68, nexpA_row.unsqueeze(1).to_broadcast([128, NT, 4]), ALU.mult), reads=["tmp68", "nexpA"], writes=["g_c"])
        g68 = g_c.rearrange("p i h -> p (i h)"); gc68 = gc_c.rearrange("p i h -> p (i h)")
        S.group("pe", [mm(ps[5][:, 0:68], mask_incl, g68)], reads=["mask_incl", "g_c"], writes=[PK[5]])
        S.op("dve", cp(gc68, ps[5][:, 0:68]), reads=[PK[5]], writes=["gc_c"])
        S.group("pe", [mm(ps[5][:, 128:196], sel127, gc68)], reads=["sel127", "gc_c"], writes=[PK[5]])
        S.op("dve", cp(egl_c.rearrange("p i h -> p (i h)"), ps[5][:, 128:196]), reads=[PK[5]], writes=["egl_c"])
        S.op("dve", tt(tmp68.rearrange("p i h -> p (i h)"), egl_c.rearrange("p i h -> p (i h)"), gc68, ALU.subtract), reads=["egl_c", "gc_c"], writes=["tmp68"])
        S.op("act", actf(egl_c, egl_c, AF.Exp), reads=["egl_c", "tmp68"], writes=["egl_c"])
        S.op("act", actf(kd_c, tmp68, AF.Exp), reads=["tmp68"], writes=["kd_c"])
        S.op("act", actf(bexp_c, gc_c, AF.Exp), reads=["gc_c"], writes=["bexp_c"])
        S.op("dve", tt(bexp_c, bexp_c, beta_c, ALU.mult), reads=["bexp_c", "beta_c"], writes=["bexp_c"])

        sl = load_w(w_in_v, C_U)
        for h in range(4):
            for (t0, n) in SEGS:
                b = next_bank()
                S.group("pe", [mm(ps[b][:, 0:n], wb[sl][:, k, h * 128:(h + 1) * 128], hT[:, k, t0:t0 + n], start=(k == 0), stop=(k == 7)) for k in range(8)],
                        reads=seg_hT_keys(t0, n) + [("wb", sl)], writes=[PK[b]])
                u = f512[bank_n[0] % 2]
                S.op("act", actf(u[:, 0:n], ps[b][:, 0:n], AF.Gelu_apprx_tanh), reads=[PK[b]], writes=[("u", bank_n[0] % 2)])
                b2 = 4 + bank_n[0] % 2
                if n == 512:
                    tiles = [t0 // 128 + j for j in range(4)]
                    S.group("pe", [mm(ps[b2][:, j * 128:(j + 1) * 128], vg[:, tiles[j], h * 128:(h + 1) * 128], wsT[:, h, :]) for j in range(4)],
                            reads=[("vg", ti) for ti in tiles] + ["wsT"], writes=[PK[b2]])
                    S.op("dve", tt(f512[4][:, :].rearrange("p (j t) -> p j t", j=4), ps[b2][:, :].rearrange("p (j t) -> p j t", j=4),
                                   bs_row[:, h:h + 1, :].to_broadcast([128, 4, 128]), ALU.add), reads=[PK[b2], "rows"], writes=["mixt"])
                else:
                    S.group("pe", [mm(ps[b2][:, 0:16], vg[0:16, 16, h * 128:(h + 1) * 128], selws[0:16, h, :])],
                            reads=[("vg", 16), ("selws", h)], writes=[PK[b2]])
                    S.op("dve", tt(f512[4][:, 0:16], ps[b2][:, 0:16], bs_row[:, h, 0:1].to_broadcast([128, 16]), ALU.add), reads=[PK[b2], "rows"], writes=["mixt"])
                S.op("pool", tt(cat[:, 4 + h, t0:t0 + n], f512[4][:, 0:n], u[:, 0:n], ALU.mult), reads=["mixt", ("u", bank_n[0] % 2)], writes=[("cat", 4 + h, t0)])

        sl = load_w(w_in_v, C_Z)
        for h in range(4):
            for (t0, n) in SEGS:
                b = next_bank()
                S.group("pe", [mm(ps[b][:, 0:n], wb[sl][:, k, h * 128:(h + 1) * 128], hT[:, k, t0:t0 + n], start=(k == 0), stop=(k == 7)) for k in range(8)],
                        reads=seg_hT_keys(t0, n) + [("wb", sl)], writes=[PK[b]])
                S.op("act", actf(zs[:, h, t0:t0 + n], ps[b][:, 0:n], AF.Silu), reads=[PK[b]], writes=[("zs", h, t0)])

        S.barrier()

        def qkv_unit(blk, h, si, ui, wsl):
            ops = []
            Lop, Lgr, Ldma = mk_recorders(S, ops)
            c = blk * 4 + h
            t0, n = SEGS[si]
            bs = ui % NBS
            pr, acc, sq, rn = pre[bs], accb[bs], sqb[bs], rnb[bs]
            kp, ka, ks, kr = ("pre", bs), ("acc", bs), ("sq", bs), ("rn", bs)
            b = ui % 4; bn = 4 + ui % 4
            Lgr("pe", [mm(ps[b][:, 0:n], wb[wsl][:, k, h * 128:(h + 1) * 128], hT[:, k, t0:t0 + n], start=(k == 0), stop=(k == 7)) for k in range(8)],
                reads=[("wb", wsl)], writes=[PK[b]])
            if si == 0:
                Lop("dve", lambda e: e.memset(pr[:, 0:3], 0.0), writes=[kp])
            elif si < 4:
                Lgr("pe", [mm(ps[bn][:, 0:3], wb[wsl][:, k, h * 128:(h + 1) * 128], hT[:, k, t0 - 3:t0], start=(k == 0), stop=(k == 7)) for k in range(8)],
                    reads=[("wb", wsl)], writes=[PK[bn]])
                Lop("dve", cp(pr[:, 0:3], ps[bn][:, 0:3]), reads=[PK[bn]], writes=[kp])
            Lop("act", actf(pr[:, 3:3 + n], ps[b][:, 0:n], AF.Copy), reads=[PK[b], kp], writes=[kp])
            if si == 3:
                Lop("dve", cp(ncp_st[:, c, :], pr[:, 512:515]), reads=[kp], writes=[("ncp_st", c)])
            if si < 4:
                wd = wdiag[c % 2]
                bc_ = 4 + ui % 4
                fns = []
                for t4 in range(4):
                    for j in range(4):
                        fns.append(mm(ps[bc_][:, t4 * 128:(t4 + 1) * 128], wd[:, j, :], pr[:, j + t4 * 128:j + (t4 + 1) * 128], start=(j == 0), stop=(j == 3)))
                Lgr("pe", fns, reads=[kp, ("wdiag", c % 2)], writes=[PK[bc_]])
                Lop("act", actf(acc[:, 0:n], ps[bc_][:, 0:n], AF.Silu), reads=[PK[bc_]], writes=[ka])
            else:
                Lop("dve", cp(ncs_st[:, c, :], pr[:, 3:19]), reads=[kp], writes=[("ncs_st", c)])
                Lop("act", actf(acc[:, 0:n], pr[:, 3:3 + n], AF.Copy, scale=wcol(3, c)), reads=[kp, "cols"], writes=[ka])
                for j in (2, 1, 0):
                    Lop("dve", stt(acc[:, 0:n], histT[:, c, j, :], wcol(j, c), acc[:, 0:n], ALU.mult, ALU.add), reads=[("histT", c), ka, "cols"], writes=[ka])
                Lop("act", actf(acc[:, 0:n], acc[:, 0:n], AF.Silu), reads=[ka], writes=[ka])
            if blk == 2:
                Lop("pool", cp(qkv[:, c, t0:t0 + n], acc[:, 0:n]), reads=[ka], writes=[("qkv", c, t0)])
                if si == 4:
                    Lop("pool", cp(qks_f[:, c, :], acc[:, 0:16]), reads=[ka], writes=[("qks_f", c)])
            else:
                Lop("pool", tt(sq[:, 0:n], acc[:, 0:n], acc[:, 0:n], ALU.mult), reads=[ka], writes=[ks])
                Lgr("pe", [mm(ps[bn][:, 0:n], ones_b, sq[:, 0:n])], reads=[ks, "ones_b"], writes=[PK[bn]])
                Lop("act", actf(rn[:, 0:n], ps[bn][:, 0:n], AF.Sqrt, bias=1e-6), reads=[PK[bn]], writes=[kr])
                Lop("dve", lambda e: e.reciprocal(out=rn[:, 0:n], in_=rn[:, 0:n]), reads=[kr], writes=[kr])
                scl = (128.0 ** -0.5) if blk == 0 else 1.0
                Lop("dve", stt(qkv[:, c, t0:t0 + n], acc[:, 0:n], scl, rn[:, 0:n], ALU.mult, ALU.mult), reads=[ka, kr], writes=[("qkv", c, t0)])
                if si == 4:
                    Lop("dve", stt(qks_f[:, c, :], acc[:, 0:16], scl, rn[:, 0:16], ALU.mult, ALU.mult), reads=[ka, kr], writes=[("qks_f", c)])
            return ops

        ui = 0
        for blk in range(3):
            wsl = load_w(w_in_v, blk * 512)
            units = []
            for h in range(4):
                c = blk * 4 + h
                scs = sct[c % 2]
                S.dma("sp", "sct%d" % (c % 2), dmaf(scs[0:16, :, :], st_conv[:, :, c * 128:(c + 1) * 128]), writes=[("sct", c % 2)])
                S.group("pe", [mm(ps[4 + c % 4][:, j * 16:(j + 1) * 16], scs[0:16, j, :], ident_f[0:16, 0:16]) for j in range(3)],
                        reads=[("sct", c % 2), "ident_f"], writes=[PK[4 + c % 4]])
                S.op("dve", cp(histT[:, c, :, :], ps[4 + c % 4][:, 0:48].rearrange("p (j b) -> p j b", j=3)), reads=[PK[4 + c % 4]], writes=[("histT", c)])
                for si in range(5):
                    uo = qkv_unit(blk, h, si, ui, wsl)
                    if si == 0:
                        pre_ops = [(lambda j=j, c=c: S.op("pool", stt_pool(wdiag[c % 2][:, j, :], ident_f, wcol(j, c)), reads=["ident_f", "cols"], writes=[("wdiag", c % 2)])) for j in range(4)]
                        uo = pre_ops + uo
                    units.append(uo)
                    ui += 1
            for g0 in range(0, len(units), 4):
                zipper(units[g0:g0 + 4])

        S.barrier()
        XS = Bump(arena, X0, ARENA)
        S_all = XS.alloc(F32, [128, 64, 128])
        if 'sample' not in os.environ.get('KSKIP', ''):
            for q4 in range(4):
                S.dma("sp", "sall%d" % q4, dmaf(S_all[:, q4 * 16:(q4 + 1) * 16, :], st_gdn[q4 * 4:(q4 + 1) * 4].rearrange("b h d e -> d (b h) e")), writes=[("S_all", q4)])
        S.group("pe", [mm(ps[c // 4][0:3, (c % 4) * 128:(c % 4 + 1) * 128], ncp_st[:, c, :], ident_f) for c in range(12)],
                reads=[("ncp_st", c) for c in range(12)] + ["ident_f"], writes=[PK[0], PK[1], PK[2]])
        for q3 in range(3):
            S.op("dve", cp(stage_tok[0:3, q3 * 512:(q3 + 1) * 512], ps[q3][0:3, :]), reads=[PK[q3]], writes=["stage_tok"])
        S.dma("sp", "o_ncp", dmaf(ncp, stage_tok[0:3, :]), reads=["stage_tok"])
        S.group("pe", [mm(ps[c // 4][0:16, (c % 4) * 128:(c % 4 + 1) * 128], ncs_st[:, c, :], ident_f) for c in range(12)],
                reads=[("ncs_st", c) for c in range(12)] + ["ident_f"], writes=[PK[0], PK[1], PK[2]])
        for q3 in range(3):
            S.op("act", actf(stage2[0:16, q3 * 512:(q3 + 1) * 512], ps[q3][0:16, :], AF.Copy), reads=[PK[q3]], writes=["stage2"])
        S.dma("sp", "o_ncs", [dmaf(ncs[:, 2, :], stage2[0:16, :]), dmaf(ncs[:, 0:2, :], st_conv[:, 1:3, :])], reads=["stage2"])
        S.barrier()

        YG = Bump(arena, Y0, ARENA)
        osq = YG.alloc(BF16, [128, 512]); rn_o = YG.alloc(F32, [128, 512]); on_o = YG.alloc(F32, [128, 512])
        YG_EPI = YG.off
        NPW, DP = 4, 8
        CHDT = F32
        gN = lambda n_, dt_, shp: [YG.alloc(dt_, shp) for _ in range(n_)]
        Rs = gN(NPW, F32, [128, 256]); rhsR = gN(NPW, F32, [128, 256]); D0 = gN(NPW, F32, [128, 128]); E0 = gN(NPW, F32, [128, 128])
        EGr = gN(NPW, F32, [128, 128]); MB = gN(NPW, F32, [128, 128]); Qf = gN(NPW, F32, [128, 128]); Qs = gN(NPW, BF16, [128, 128])
        NNa = gN(NPW, CHDT, [128, 256]); NNb = gN(NPW, CHDT, [128, 256]); Xs = gN(NPW, BF16, [128, 128])
        NHa = gN(NPW, BF16, [128, 256]); NHb = gN(NPW, BF16, [128, 256])
        J0 = int(os.environ.get('GDN_J0', '6'))
        Qm = gN(DP, BF16, [128, 128]); attnT = gN(DP, BF16, [128, 128]); Kd = gN(DP, BF16, [128, 128]); Vb = gN(DP, BF16, [128, 128])
        qg = gN(DP, BF16, [128, 128]); nWT = gN(DP, BF16, [128, 128]); vn = gN(4, BF16, [128, 128])
        S.op("pool", lambda e: e.memset(S_f.rearrange("p h e -> p (h e)"), 0.0), writes=[("S_f", h) for h in range(4)])
        S.op("pool", lambda e: e.memset(S_b.rearrange("p h e -> p (h e)"), 0.0), writes=[("S_b", h) for h in range(4)])

        def gdn_P(n, h):
            ops = []
            Lop, Lgr, Ldma = mk_recorders(S, ops)
            u = n * 4 + h
            q = u % NPW; s = u % DP
            tok = slice(n * 128, (n + 1) * 128)
            kT = qkv[:, 4 + h, tok]; qT = qkv[:, h, tok]; vT = qkv[:, 8 + h, tok]
            col = lambda t: t[:, n, h:h + 1]
            K = lambda name: (name, q)
            H = lambda name: (name, s)
            bk = PK[q]; pb = ps[q]; pbb = psb[q]
            kk = pb[:, 256:384]; qk = pb[:, 384:512]
            Lop("pool", stt_pool(rhsR[q][:, 0:128], mask_incl, col(g_c)), reads=["mask_incl", "g_c"], writes=[K("rhsR")])
            Lop("pool", stt_pool(rhsR[q][:, 128:256], ident_f, col(beta_c)), reads=["ident_f", "beta_c"], writes=[K("rhsR")])
            Lgr("pe", [lambda e: e.transpose(out=pbb[:, 0:128], in_=kT, identity=ident_b),
                       lambda e: e.transpose(out=pbb[:, 128:256], in_=vT, identity=ident_b)], reads=[("qkv", n), "ident_b"], writes=[bk])
            ktok = pbb[:, 0:128]; vtok = pbb[:, 128:256]
            Lop("act", actf(Xs[q], ktok, AF.Copy, scale=col(bexp_c)), reads=[bk, "bexp_c"], writes=[K("Xs")])
            Lop("act", actf(Kd[s], ktok, AF.Copy, scale=col(kd_c)), reads=[bk, "kd_c"], writes=[H("Kd")])
            Lop("act", actf(Vb[s], vtok, AF.Copy, scale=col(beta_c)), reads=[bk, "beta_c"], writes=[H("Vb")])
            Lgr("pe", [mm(pb[:, 0:128], ones_f, rhsR[q][:, 0:128]), mm(pb[:, 128:256], ones_f, rhsR[q][:, 128:256]),
                       mm(kk, kT, kT), mm(qk, kT, qT)], reads=[K("rhsR"), "ones_f", ("qkv", n)], writes=[bk])
            Lop("dve", cp(Rs[q], pb[:, 0:256]), reads=[bk], writes=[K("Rs")])
            R_gc = Rs[q][:, 0:128]; R_be = Rs[q][:, 128:256]
            Lop("pool", lambda e: e.tensor_tensor(out=D0[q], in0=R_gc, in1=col(gc_c).to_broadcast([128, 128]), op=ALU.subtract), reads=[K("Rs"), "gc_c"], writes=[K("D0")])
            Lop("pool", ts(D0[q], D0[q], 0.0, ALU.min), reads=[K("D0")], writes=[K("D0")])
            Lop("act", actf(D0[q], D0[q], AF.Exp), reads=[K("D0")], writes=[K("D0")])
            Lop("act", actf(EGr[q], R_gc, AF.Exp), reads=[K("Rs")], writes=[K("EGr")])
            Lop("pool", tt(MB[q], R_be, D0[q], ALU.mult), reads=[K("Rs"), K("D0")], writes=[K("MB")])
            Lop("pool", tt(MB[q], MB[q], nmask_su, ALU.mult), reads=[K("MB"), "nmask_su"], writes=[K("MB")])
            Lop("pool", tt(D0[q], D0[q], mask_incl, ALU.mult), reads=[K("D0"), K("MB"), "mask_incl"], writes=[K("D0")])
            Lop("pool", tt(qg[s], qT, EGr[q], ALU.mult), reads=[("qkv", n), K("EGr")], writes=[H("qg")])
            Lop("dve", tt(NNa[q][:, 0:128], kk, MB[q], ALU.mult), reads=[bk, K("MB")], writes=[K("NNa")])
            Lop("dve", tt(attnT[s], qk, D0[q], ALU.mult), reads=[bk, K("D0")], writes=[H("attnT")])
            Lgr("pe", [mm(pb[:, 0:128], NNa[q][:, 0:128], ident_f)], reads=[K("NNa"), "ident_f"], writes=[bk])
            Lop("act", actf(NNa[q][:, 128:256], pb[:, 0:128], AF.Copy), reads=[bk], writes=[K("NNa")])
            Lop("pool", tt(Qf[q], ident_f, NNa[q][:, 0:128], ALU.add), reads=["ident_f", K("NNa")], writes=[K("Qf")])
            cur, nxt, kc, kn = NNa[q], NNb[q], K("NNa"), K("NNb")
            cur16, nxt16, kc16, kn16 = NHa[q], NHb[q], K("NHa"), K("NHb")
            if J0 == 0:
                Lop("pool", cp(cur16, cur), reads=[kc], writes=[kc16])
            for j in range(1, 7):
                f32lvl = j <= J0
                src, ksrc = (cur, kc) if f32lvl else (cur16, kc16)
                fns = []
                if j < 6:
                    fns.append(mm(pb[:, 0:128], src[:, 128:256], src[:, 0:128]))
                fns.append(mm(pb[:, 128:256], src[:, 0:128], src[:, 128:256]))
                Lgr("pe", fns, reads=[ksrc], writes=[bk])
                lo = 0 if j < 6 else 128
                if f32lvl:
                    Lop("act", actf(nxt[:, lo:256], pb[:, lo:256], AF.Copy), reads=[bk], writes=[kn])
                    if j == J0 and j < 6:
                        Lop("pool", cp(nxt16[:, lo:256], nxt[:, lo:256]), reads=[kn], writes=[kn16])
                    Lgr("pe", [mm(pb[:, 256:384], nxt[:, 128:256], Qf[q])], reads=[kn, K("Qf")], writes=[bk])
                else:
                    Lop("act", actf(nxt16[:, lo:256], pb[:, lo:256], AF.Copy), reads=[bk], writes=[kn16])
                    Lop("pool", cp(Qs[q], Qf[q]), reads=[K("Qf")], writes=[K("Qs")])
                    Lgr("pe", [mm(pb[:, 256:384], nxt16[:, 128:256], Qs[q])], reads=[kn16, K("Qs")], writes=[bk])
                Lop("dve", tt(Qf[q], Qf[q], pb[:, 256:384], ALU.add), reads=[bk, K("Qf")], writes=[K("Qf")])
                cur, nxt, kc, kn = nxt, cur, kn, kc
                cur16, nxt16, kc16, kn16 = nxt16, cur16, kn16, kc16
            Lop("pool", cp(Qm[s], Qf[q]), reads=[K("Qf")], writes=[H("Qm")])
            Lgr("pe", [mm(pb[:, 384:512], Xs[q], Qm[s])], reads=[K("Xs"), H("Qm")], writes=[bk])
            Lop("act", actf(nWT[s], pb[:, 384:512], AF.Copy, scale=negone), reads=[bk], writes=[H("nWT")])
            return ops

        def gdn_R(n, h):
            ops = []
            Lop, Lgr, Ldma = mk_recorders(S, ops)
            u = n * 4 + h
            s = u % DP
            H = lambda name: (name, s)
            col = lambda t: t[:, n, h:h + 1]
            bR = PK[4]; ob = 5 + n % 2
            V = ps[4][:, h * 128:(h + 1) * 128]
            Lgr("pe", [mm(V, Qm[s], Vb[s], start=True, stop=False),
                       mm(V, nWT[s], S_b[:, h, :], start=False, stop=True)],
                reads=[H("Qm"), H("Vb"), H("nWT"), ("S_b", h)], writes=[bR])
            Lop("dve", cp(vn[h], V), reads=[bR], writes=[("vn", h)])
            Lgr("pe", [mm(V, Kd[s], vn[h])], reads=[H("Kd"), ("vn", h)], writes=[bR])
            Lgr("pe", [mm(ps[ob][:, h * 128:(h + 1) * 128], S_b[:, h, :], qg[s], start=True, stop=False),
                       mm(ps[ob][:, h * 128:(h + 1) * 128], vn[h], attnT[s], start=False, stop=True)],
                reads=[("S_b", h), H("qg"), ("vn", h), H("attnT")], writes=[PK[ob]])
            Lop("dve", stt(S_f[:, h, :], S_f[:, h, :], col(egl_c), V, ALU.mult, ALU.add), reads=[bR, ("S_f", h), "egl_c"], writes=[("S_f", h)])
            Lop("act", actf(S_b[:, h, :], S_f[:, h, :], AF.Copy), reads=[("S_f", h)], writes=[("S_b", h)])
            return ops

        def gdn_epilogue(o_ps, ss_ps, ncol, t0, okeys, sskey):
            w = 4 * ncol
            S.op("act", actf(osq[:, 0:w], o_ps, AF.Square), reads=okeys, writes=["osq"])
            S.group("pe", [mm(ss_ps, ones_b, osq[:, 0:w])], reads=["osq", "ones_b"], writes=[sskey])
            S.op("dve", ts(rn_o[:, 0:w], ss_ps, 1.0 / 128, ALU.mult, EPS, ALU.add), reads=[sskey], writes=["rn_o"])
            S.op("act", actf(rn_o[:, 0:w], rn_o[:, 0:w], AF.Sqrt), reads=["rn_o"], writes=["rn_o"])
            S.op("dve", lambda e: e.reciprocal(out=rn_o[:, 0:w], in_=rn_o[:, 0:w]), reads=["rn_o"], writes=["rn_o"])
            S.op("dve", stt(on_o[:, 0:w], o_ps, gdnn_col, rn_o[:, 0:w], ALU.mult, ALU.mult), reads=okeys + ["rn_o", "cols"], writes=["on_o"])
            S.op("pool", tt(cat[:, 0:4, t0:t0 + ncol], on_o[:, 0:w].rearrange("p (h t) -> p h t", h=4), zs[:, :, t0:t0 + ncol], ALU.mult),
                 reads=["on_o", "zs"], writes=[("cat_o", t0)])

        _SK = os.environ.get('KSKIP', '')
        NCH = 0 if 'prompt' in _SK else int(os.environ.get('GDN_N', '16'))

        def epi_ops(n):
            ob = 5 + n % 2
            return [lambda: gdn_epilogue(ps[ob][:, :], ps[7][:, :], 128, n * 128, [PK[ob]], PK[7])]

        _DO_SAMPLE = 'sample' not in _SK
        def _sample_section():
            YG = Bump(arena, YG_EPI, ARENA)
            sv = YG.alloc(F32, [128, 8])
            rexp = YG.alloc(F32, [128, 8, 16])
            bcs = YG.alloc(F32, [128, 128])
            dcol = YG.alloc(F32, [128, 64])
            dtok = YG.alloc(F32, [128, 512]); ktoks = YG.alloc(F32, [128, 512])
            kmask = [YG.alloc(F32, [128, 512]) for _ in range(2)]
            S.op("dve", cp(sv[0:16, 0:4], beta_c[0:16, 16, :]), reads=["beta_c"], writes=["sv"])
            S.op("act", actf(sv[0:16, 4:8], g_c[0:16, 16, :], AF.Exp), reads=["g_c"], writes=["sv"])
            for j in range(8):
                S.op("dve", ts(rexp[0:16, j, :], ident_f[0:16, 0:16], sv[0:16, j:j + 1], ALU.mult), reads=["sv", "ident_f"], writes=["rexp"])
            S.group("pe", [mm(ps[0][:, 0:128], ones_f[0:16, :], rexp[0:16, :, :].rearrange("p j b -> p (j b)"))], reads=["rexp", "ones_f"], writes=[PK[0]])
            S.op("dve", cp(bcs, ps[0][:, 0:128]), reads=[PK[0]], writes=["bcs"])
            beta_bc = bcs[:, 0:64]; eg_bc = bcs[:, 64:128]
            S.group("pe", [mm(ps[1][:, h * 16 + b:h * 16 + b + 1], S_all[:, b * 4 + h, :], qks_f[:, 4 + h, b:b + 1]) for b in range(16) for h in range(4)],
                    reads=[("S_all", q4) for q4 in range(4)] + ["qks_f"], writes=[PK[1]])
            S.op("dve", tt(dcol, ps[1][:, 0:64], eg_bc, ALU.mult), reads=[PK[1], "bcs"], writes=["dcol"])
            S.op("dve", tt(dcol, qks_f[:, 8:12, :].rearrange("p h b -> p (h b)"), dcol, ALU.subtract), reads=["dcol", "qks_f"], writes=["dcol"])
            S.op("dve", tt(dcol, dcol, beta_bc, ALU.mult), reads=["dcol", "bcs"], writes=["dcol"])
            S.group("pe", [mm(ps[2][0:16, h * 128:(h + 1) * 128], dcol[:, h * 16:(h + 1) * 16], ident_f) for h in range(4)], reads=["dcol", "ident_f"], writes=[PK[2]])
            S.group("pe", [mm(ps[3][0:16, h * 128:(h + 1) * 128], qks_f[:, 4 + h, :], ident_f) for h in range(4)], reads=["qks_f", "ident_f"], writes=[PK[3]])
            S.op("dve", cp(dtok[0:16, :], ps[2][0:16, :]), reads=[PK[2]], writes=["dtok"])
            S.op("act", actf(ktoks[0:16, :], ps[3][0:16, :], AF.Copy), reads=[PK[3]], writes=["ktoks"])
            for b in range(16):
                km = kmask[b % 2]; pb = 4 + b % 2
                S.op("dve", ts(km[0:16, :], ktoks[0:16, :], ident_f[0:16, b:b + 1], ALU.mult), reads=["ktoks", "ident_f"], writes=[("kmask", b % 2)])
                S.group("pe", [mm(ps[pb][:, h * 128:(h + 1) * 128], km[0:16, h * 128:(h + 1) * 128], dtok[0:16, h * 128:(h + 1) * 128]) for h in range(4)],
                        reads=[("kmask", b % 2), "dtok"], writes=[PK[pb]])
                for h in range(4):
                    S.op("dve", stt(S_all[:, b * 4 + h, :], S_all[:, b * 4 + h, :], eg_bc[:, h * 16 + b:h * 16 + b + 1], ps[pb][:, h * 128:(h + 1) * 128], ALU.mult, ALU.add),
                         reads=[PK[pb], "bcs", ("S_all", b // 4)], writes=[("S_all", b // 4)])
            S.group("pe", [mm(ps[1][:, 64 + h * 16 + b:64 + h * 16 + b + 1], S_all[:, b * 4 + h, :], qks_f[:, h, b:b + 1]) for b in range(16) for h in range(4)],
                    reads=[("S_all", q4) for q4 in range(4)] + ["qks_f"], writes=[PK[1]])
            gdn_epilogue(ps[1][:, 64:128], ps[0][:, 128:192], 16, T_P, [PK[1]], PK[0])
            for q4 in range(4):
                S.dma("sp", "o_ngs%d" % q4, dmaf(ngs[q4 * 4:(q4 + 1) * 4].rearrange("b h d e -> d (b h) e"), S_all[:, q4 * 16:(q4 + 1) * 16, :]), reads=[("S_all", q4)])
        if _DO_SAMPLE:
            _sample_section()
        S.barrier()

        LB = Bump(arena, X0, ARENA)
        NL = 8
        lf32 = lambda shp: [LB.alloc(F32, shp) for _ in range(NL)]
        lbf = lambda shp: [LB.alloc(BF16, shp) for _ in range(NL)]
        Rs8 = lf32([128, 256]); rhsR8 = lf32([128, 256]); D08 = lf32([128, 128]); EGr8 = lf32([128, 128]); MB8 = lf32([128, 128]); Qf8 = lf32([128, 128])
        NNa8 = lf32([128, 256]); NNb8 = lf32([128, 256]); rn8 = lf32([128, 128]); on8 = lf32([128, 128])
        Xs8 = lbf([128, 128]); Kd8 = lbf([128, 128]); Vb8 = lbf([128, 128]); qg8 = lbf([128, 128]); at8 = lbf([128, 128])
        Qm8 = lbf([128, 128]); nWT8 = lbf([128, 128]); vn8 = lbf([128, 128]); osq8 = lbf([128, 128])
        r_done = {}

        def gdn_unit(n, h):
            ops = []
            Lop, Lgr, Ldma = mk_recorders(S, ops)
            L = h * 2 + n % 2
            tok = slice(n * 128, (n + 1) * 128)
            kT = qkv[:, 4 + h, tok]; qT = qkv[:, h, tok]; vT = qkv[:, 8 + h, tok]
            col = lambda t: t[:, n, h:h + 1]
            K = lambda name: (name, L)
            bk = PK[L]; pb = ps[L]; pbb = psb[L]
            kk = pb[:, 256:384]; qk = pb[:, 384:512]
            Rs, rhsR, D0, EGr, MB, Qf = Rs8[L], rhsR8[L], D08[L], EGr8[L], MB8[L], Qf8[L]
            Xs, Kd, Vb, qg, attnT, Qm, nWT, vn, osq = Xs8[L], Kd8[L], Vb8[L], qg8[L], at8[L], Qm8[L], nWT8[L], vn8[L], osq8[L]
            Lop("pool", stt_pool(rhsR[:, 0:128], mask_incl, col(g_c)), reads=["mask_incl", "g_c"], writes=[K("rhsR")])
            Lop("pool", stt_pool(rhsR[:, 128:256], ident_f, col(beta_c)), reads=["ident_f", "beta_c"], writes=[K("rhsR")])
            Lgr("pe", [mm(pb[:, 0:128], mask_sl, rhsR[:, 0:128]), mm(pb[:, 128:256], ones_f, rhsR[:, 0:128]),
                       mm(pb[:, 256:384], nmask_sl, rhsR[:, 128:256]),
                       lambda e: e.transpose(out=pbb[:, 768:896], in_=kT, identity=ident_b),
                       lambda e: e.transpose(out=pbb[:, 896:1024], in_=vT, identity=ident_b)],
                reads=[K("rhsR"), "ones_f", "mask_sl", "nmask_sl", "ident_b"], writes=[bk])
            ktok = pbb[:, 768:896]; vtok = pbb[:, 896:1024]
            Lop("act", actf(Rs, pb[:, 0:256], AF.Exp), reads=[bk], writes=[K("Rs")])
            D0 = Rs[:, 0:128]; EGr = Rs[:, 128:256]
            Lop("act", actf(Xs, ktok, AF.Copy, scale=col(bexp_c)), reads=[bk, "bexp_c"], writes=[K("Xs")])
            Lop("act", actf(Kd, ktok, AF.Copy, scale=col(kd_c)), reads=[bk, "kd_c"], writes=[K("Kd")])
            Lop("act", actf(Vb, vtok, AF.Copy, scale=col(beta_c)), reads=[bk, "beta_c"], writes=[K("Vb")])
            Lop("act", actf(MB, pb[:, 256:384], AF.Copy), reads=[bk], writes=[K("MB")])
            Lop("pool", tt(MB, MB, D0, ALU.mult), reads=[K("MB"), K("Rs")], writes=[K("MB")])
            Lop("pool", tt(qg, qT, EGr, ALU.mult), reads=[K("Rs")], writes=[K("qg")])
            Lgr("pe", [mm(pb[:, 0:128], kT, kT), mm(pb[:, 128:256], kT, qT)], reads=[], writes=[bk])
            kk = pb[:, 0:128]; qk = pb[:, 128:256]
            Lop("pool", tt(D0, D0, mask_incl, ALU.mult), reads=[K("Rs"), K("MB"), K("qg"), "mask_incl"], writes=[K("Rs")])
            NNa, NNb = NNa8[L], NNb8[L]
            Lop("dve", tt(NNa[:, 0:128], kk, MB, ALU.mult), reads=[bk, K("MB")], writes=[K("NNa")])
            Lop("dve", tt(attnT, qk, D0, ALU.mult), reads=[bk, K("Rs")], writes=[K("attnT")])
            Lgr("pe", [mm(pb[:, 256:384], NNa[:, 0:128], ident_f)], reads=[K("NNa"), "ident_f"], writes=[bk])
            Lop("act", actf(NNa[:, 128:256], pb[:, 256:384], AF.Copy), reads=[bk], writes=[K("NNa")])
            Lop("pool", tt(Qf, ident_f, NNa[:, 0:128], ALU.add), reads=["ident_f", K("NNa")], writes=[K("Qf")])
            cur, nxt, kc, kn = NNa, NNb, K("NNa"), K("NNb")
            for j in range(1, 7):
                fns = []
                if j < 6:
                    fns.append(mm(pb[:, 0:128], cur[:, 128:256], cur[:, 0:128]))
                fns.append(mm(pb[:, 128:256], cur[:, 0:128], cur[:, 128:256]))
                Lgr("pe", fns, reads=[kc], writes=[bk])
                lo = 0 if j < 6 else 128
                if j in (3, 5):
                    Lop("dve", cp(nxt[:, lo:256], pb[:, lo:256]), reads=[bk], writes=[kn])
                else:
                    Lop("act", actf(nxt[:, lo:256], pb[:, lo:256], AF.Copy), reads=[bk], writes=[kn])
                Lgr("pe", [mm(pb[:, 256:384], nxt[:, 128:256], Qf)], reads=[kn, K("Qf")], writes=[bk])
                Lop("dve", tt(Qf, Qf, pb[:, 256:384], ALU.add), reads=[bk, K("Qf")], writes=[K("Qf")])
                cur, nxt, kc, kn = nxt, cur, kn, kc
            Lop("pool", cp(Qm, Qf), reads=[K("Qf")], writes=[K("Qm")])
            Lgr("pe", [mm(pb[:, 384:512], Xs, Qm)], reads=[K("Xs"), K("Qm")], writes=[bk])
            Lop("act", actf(nWT, pb[:, 384:512], AF.Copy, scale=negone), reads=[bk], writes=[K("nWT")])
            V = pb[:, 384:512]; Oh = pb[:, 0:128]; SSh = pb[:, 128:256]

            def chk():
                assert n == 0 or r_done.get((n - 1, h)), ("emission order violated", n, h)
            ops.append(chk)
            Lgr("pe", [mm(V, Qm, Vb, start=True, stop=False), mm(V, nWT, S_b[:, h, :], start=False, stop=True)],
                reads=[K("Qm"), K("Vb"), K("nWT"), ("S_b", h)], writes=[bk])
            Lop("dve", cp(vn, V), reads=[bk], writes=[K("vn")])
            Lgr("pe", [mm(V, Kd, vn),
                       mm(Oh, S_b[:, h, :], qg, start=True, stop=False), mm(Oh, vn, attnT, start=False, stop=True)],
                reads=[K("Kd"), K("vn"), ("S_b", h), K("qg"), K("attnT")], writes=[bk])
            Lop("dve", stt(S_f[:, h, :], S_f[:, h, :], col(egl_c), V, ALU.mult, ALU.add), reads=[bk, ("S_f", h), "egl_c"], writes=[("S_f", h)])
            Lop("pool", cp(S_b[:, h, :], S_f[:, h, :]), reads=[("S_f", h)], writes=[("S_b", h)])

            def mark():
                r_done[(n, h)] = True
            ops.append(mark)
            rn, on = rn8[L], on8[L]
            Lop("act", actf(osq, Oh, AF.Square), reads=[bk], writes=[K("osq")])
            Lgr("pe", [mm(SSh, ones_b, osq)], reads=[K("osq"), "ones_b"], writes=[bk])
            Lop("dve", ts(rn, SSh, 1.0 / 128, ALU.mult, EPS, ALU.add), reads=[bk], writes=[K("rn")])
            Lop("act", actf(rn, rn, AF.Sqrt), reads=[K("rn")], writes=[K("rn")])
            Lop("dve", lambda e: e.reciprocal(out=rn, in_=rn), reads=[K("rn")], writes=[K("rn")])
            Lop("dve", stt(on, Oh, gdnn_col, rn, ALU.mult, ALU.mult), reads=[bk, K("rn"), "cols"], writes=[K("on")])
            Lop("pool", tt(cat[:, h, tok], on, zs[:, h, tok], ALU.mult), reads=[K("on")], writes=[("cat_o", n, h)])
            return ops

        if NCH:
            u0 = gdn_unit(0, 0)
            LU = len(u0)
            STAG8 = int(os.environ.get('GDN_STAG', '7'))
            lanes = []
            for h in range(4):
                for par in range(2):
                    pad = h * STAG8 + par * (LU // 2)
                    lane = [(lambda: None)] * pad
                    for n in range(par, NCH, 2):
                        lane = lane + gdn_unit(n, h)
                    lanes.append(lane)
            zipper(lanes)
        S.dma("sp", "o_ngp", dmaf(ngp.rearrange("h d e -> d h e"), S_f), reads=[("S_f", h) for h in range(4)])

        S.barrier()

        if 'phasec' in _SK:
            S.finish()
            with nc.Block() as block:
                S.replay(block)
            return nc
        YC = Bump(arena, P_C0, ARENA)
        R = YC.alloc(F32, [128, 8, 528]); xnC = YC.alloc(BF16, [128, 8, 528]); hid = YC.alloc(BF16, [128, 32, 528])
        r8 = [YC.alloc(BF16, [128, 8, 512]) for _ in range(3)]
        r16 = [YC.alloc(BF16, [128, 32, 256]) for _ in range(2)]
        xres = YC.alloc(F32, [128, 4, 1024]); xres_s = YC.alloc(F32, [128, 1024])
        pw = YC.alloc(BF16, [128, 2, 1024]); ptok = [YC.alloc(BF16, [128, 256]) for _ in range(2)]
        pT = YC.alloc(BF16, [128, 2, 528]); sqr = [YC.alloc(BF16, [128, 528]) for _ in range(2)]; rnC = YC.alloc(F32, [128, 528])
        sig = [YC.alloc(F32, [128, 528]) for _ in range(2)]; relu_t = [YC.alloc(F32, [128, 528]) for _ in range(2)]
        ytile = [YC.alloc(F32, [128, 1024]) for _ in range(1)]
        rncol = YC.alloc(F32, [128, 8])
        r8_n = [0]; r16_n = [0]; misc_n = [0]

        r8_seq = []
        for _p in range(4):
            r8_seq += [(w_out_v, 0), (w_out_v, 512)] + [(w_up_v, bb * 512) for bb in range(8)] + [(w_gate_v, 0), (w_gate_v, 512)]
        r8_issued = [0]

        def load_r8(view, c0):
            idx = r8_n[0]
            r8_n[0] += 1
            assert r8_seq[idx][1] == c0
            while r8_issued[0] < min(len(r8_seq), idx + 3):
                j = r8_issued[0]
                vw, cc = r8_seq[j]
                S.dma("pool", "r8_%d" % (j % 3), dmaf(r8[j % 3], vw[:, :, cc:cc + 512]), writes=[("r8", j % 3)])
                r8_issued[0] += 1
            return idx % 3

        def load_r16(c0):
            sl = r16_n[0] % 2
            r16_n[0] += 1
            S.dma("pool", "r16_%d" % sl, dmaf(r16[sl], w_down_v[:, :, c0:c0 + 256]), writes=[("r16", sl)])
            return sl

        S.dma("pool", "pw", dmaf(pw, w_ple_v), writes=["pw"])
        PASSES = [[(0, 512, 0)], [(512, 512, 0)], [(1024, 512, 0)], [(1536, 512, 0), (2048, 16, 512)]]

        def rms_norm_C(which, out_fn, segs, W, tag):
            bns = []
            for (t0, n, l0) in segs:
                bns.append(6 + misc_n[0] % 2)
                misc_n[0] += 1
            for m in range(8):
                sq = sqr[m % 2]
                S.op("act", actf(sq[:, 0:W], R[:, m, 0:W], AF.Square), reads=[("R", m)], writes=[("sqr", m % 2)])
                for si_, (t0, n, l0) in enumerate(segs):
                    bn = bns[si_]
                    S.group("pe", [mm(ps[bn][:, 0:n], ones_b, sq[:, l0:l0 + n], start=(m == 0), stop=(m == 7))],
                            reads=[("sqr", m % 2), "ones_b"], writes=[PK[bn]])
            for si_, (t0, n, l0) in enumerate(segs):
                bn = bns[si_]
                S.op("dve", ts(rnC[:, l0:l0 + n], ps[bn][:, 0:n], 1.0 / 1024, ALU.mult, EPS, ALU.add), reads=[PK[bn]], writes=["rnC"])
            S.op("act", actf(rnC[:, 0:W], rnC[:, 0:W], AF.Sqrt), reads=["rnC"], writes=["rnC"])
            S.op("dve", lambda e: e.reciprocal(out=rnC[:, 0:W], in_=rnC[:, 0:W]), reads=["rnC"], writes=["rnC"])
            for m in range(8):
                out_ap, wkey = out_fn(m)
                S.op("dve", stt(out_ap, R[:, m, 0:W], gcol(which, m), rnC[:, 0:W], ALU.mult, ALU.mult), reads=[("R", m), "rnC", "cols"], writes=[wkey])

        for pi, segs in enumerate(PASSES):
            W = sum(n for (_, n, _) in segs)
            t00 = segs[0][0]
            has_s = len(segs) > 1
            if pi == 0:
                S.dma("sp", "xres", dmaf(xres, x_p[0:512, :].rearrange("(j p) f -> p j f", p=128)), writes=["xres"])
            def stats_act(m):
                S.op("act", actf(sqr[m % 2][:, 0:W], R[:, m, 0:W], AF.Square), reads=[("R", m)], writes=[("sqr", m % 2)])

            def stats_pe(m, bns):
                for si_, (t0, n, l0) in enumerate(segs):
                    S.group("pe", [mm(ps[bns[si_]][:, 0:n], ones_b, sqr[m % 2][:, l0:l0 + n], start=(m == 0), stop=(m == 7))],
                            reads=[("sqr", m % 2), "ones_b"], writes=[PK[bns[si_]]])

            def norm_finish_row(bns, out_t, key, square):
                for si_, (t0, n, l0) in enumerate(segs):
                    S.op("dve", ts(out_t[:, l0:l0 + n], ps[bns[si_]][:, 0:n], 1.0 / 1024, ALU.mult, EPS, ALU.add), reads=[PK[bns[si_]]], writes=[key])
                if not square:
                    S.op("act", actf(out_t[:, 0:W], out_t[:, 0:W], AF.Sqrt), reads=[key], writes=[key])
                S.op("dve", lambda e: e.reciprocal(out=out_t[:, 0:W], in_=out_t[:, 0:W]), reads=[key], writes=[key])

            def pick_bns():
                o = []
                for _ in segs:
                    o.append(6 + misc_n[0] % 2)
                    misc_n[0] += 1
                return o

            bns1 = pick_bns()
            for blk in range(2):
                sl = load_r8(w_out_v, blk * 512)
                for m4 in range(4):
                    m = blk * 4 + m4
                    for (t0, n, l0) in segs:
                        b = next_bank()
                        fns = [mm(ps[b][:, 0:n], r8[sl][:, k, m4 * 128:(m4 + 1) * 128], cat[:, k, t0:t0 + n], start=(k == 0), stop=False) for k in range(8)]
                        if n == 512:
                            fns += [mm(ps[b][:, j * 128:(j + 1) * 128], xres[:, j, m * 128:(m + 1) * 128], ident_f, start=False, stop=(j == 3)) for j in range(4)]
                            rk = ["xres"]
                        else:
                            fns += [mm(ps[b][:, 0:16], xres_s[0:16, m * 128:(m + 1) * 128], ident_f[0:16, 0:16], start=False, stop=True)]
                            rk = ["xres_s"]
                        S.group("pe", fns, reads=[("r8", sl), "cat", "ident_f"] + rk, writes=[PK[b]])
                        S.op("act", actf(R[:, m, l0:l0 + n], ps[b][:, 0:n], AF.Copy), reads=[PK[b]], writes=[("R", m)])
                        S.op("act", actf(xnC[:, m, l0:l0 + n], ps[b][:, 0:n], AF.Copy, scale=gcol(0, m)), reads=[PK[b], "cols"], writes=[("xnC", m)])
                    stats_act(m)
                    if m >= 1:
                        stats_pe(m - 1, bns1)
            stats_pe(7, bns1)
            norm_finish_row(bns1, rnC, "rnC", True)
            if pi + 1 < len(PASSES):
                tn = PASSES[pi + 1][0][0]
                S.dma("sp", "xres", dmaf(xres, x_p[tn:tn + 512, :].rearrange("(j p) f -> p j f", p=128)), writes=["xres"])
                if len(PASSES[pi + 1]) > 1:
                    S.dma("sp", "xres_s", dmaf(xres_s[0:16, :], x_s), writes=["xres_s"])
            for blk in range(8):
                sl = load_r8(w_up_v, blk * 512)
                for m4 in range(4):
                    hc = blk * 4 + m4
                    for (t0, n, l0) in segs:
                        b = next_bank()
                        S.group("pe", [mm(ps[b][:, 0:n], r8[sl][:, k, m4 * 128:(m4 + 1) * 128], xnC[:, k, l0:l0 + n], start=(k == 0), stop=(k == 7)) for k in range(8)],
                                reads=[("r8", sl)] + [("xnC", k) for k in range(8)], writes=[PK[b]])
                        rt = relu_t[misc_n[0] % 2]; rkey = ("relu_t", misc_n[0] % 2)
                        misc_n[0] += 1
                        S.op("act", actf(rt[:, 0:n], ps[b][:, 0:n], AF.Relu), reads=[PK[b]], writes=[rkey])
                        S.op("dve", tt(hid[:, hc, l0:l0 + n], rt[:, 0:n], rt[:, 0:n], ALU.mult), reads=[rkey], writes=[("hid", hc)])
            bns2 = pick_bns()
            for blk in range(4):
                sl = load_r16(blk * 256)
                for m2 in range(2):
                    m = blk * 2 + m2
                    for (t0, n, l0) in segs:
                        b = next_bank()
                        S.group("pe", [mm(ps[b][:, 0:n], r16[sl][:, k, m2 * 128:(m2 + 1) * 128], hid[:, k, l0:l0 + n], start=(k == 0), stop=(k == 31)) for k in range(32)],
                                reads=[("r16", sl)] + [("hid", k) for k in range(32)], writes=[PK[b]])
                        sg = sig[misc_n[0] % 2]; skey = ("sig", misc_n[0] % 2)
                        misc_n[0] += 1
                        S.op("dve", tt(sg[:, 0:n], ps[b][:, 0:n], rnC[:, l0:l0 + n], ALU.mult), reads=[PK[b], "rnC"], writes=[skey])
                        S.op("dve", tt(R[:, m, l0:l0 + n], R[:, m, l0:l0 + n], sg[:, 0:n], ALU.add), reads=[skey, ("R", m)], writes=[("R", m)])
                    S.op("act", actf(xnC[:, m, 0:W], R[:, m, 0:W], AF.Copy, scale=gcol(1, m)), reads=[("R", m), "cols"], writes=[("xnC", m)])
                    stats_act(m)
                    if m >= 1:
                        stats_pe(m - 1, bns2)
            stats_pe(7, bns2)
            norm_finish_row(bns2, rnC, "rnC", False)
            for (t0, n, l0) in segs:
                ntile = (n + 127) // 128
                for j in range(ntile):
                    r = min(128, n - j * 128)
                    sl = misc_n[0] % 2
                    misc_n[0] += 1
                    src = p_p[t0 + j * 128:t0 + j * 128 + r, :] if n == 512 else p_s
                    S.dma("pool", "ptok%d" % sl, dmaf(ptok[sl][0:r, :], src), writes=[("ptok", sl)])
                    S.group("pe", [lambda e, kk=kk, sl=sl, r=r: e.transpose(out=psb[5][:, kk * 128:kk * 128 + r], in_=ptok[sl][0:r, kk * 128:(kk + 1) * 128], identity=ident_b[0:r, 0:r]) for kk in range(2)],
                            reads=[("ptok", sl), "ident_b"], writes=[PK[5]])
                    S.op("act", actf(pT[:, :, l0 + j * 128:l0 + j * 128 + r], psb[5][:, 0:256].rearrange("p (k t) -> p k t", k=2)[:, :, 0:r], AF.Copy), reads=[PK[5]], writes=["pT"])
            ntt = sum((n + 127) // 128 for (_, n, _) in segs)
            sigbufs = [(sig[0], ("sig", 0)), (sig[1], ("sig", 1)), (relu_t[0], ("relu_t", 0)), (relu_t[1], ("relu_t", 1))]

            def gate_chunk(m, sl, m4):
                ops = []
                Lop, Lgr, Ldma = mk_recorders(S, ops)
                for si_, (t0, n, l0) in enumerate(segs):
                    b = (2 * m + si_) % 4
                    pb_ = 6 + m % 2
                    sg, skey = sigbufs[(2 * m + si_) % 4]
                    Lgr("pe", [mm(ps[b][:, 0:n], r8[sl][:, k, m4 * 128:(m4 + 1) * 128], xnC[:, k, l0:l0 + n], start=(k == 0), stop=(k == 7)) for k in range(8)],
                        reads=[("r8", sl)] + [("xnC", k) for k in range(8)], writes=[PK[b]])
                    Lgr("pe", [mm(ps[pb_][:, 0:n], pw[:, kk, m * 128:(m + 1) * 128], pT[:, kk, l0:l0 + n], start=(kk == 0), stop=(kk == 1)) for kk in range(2)],
                        reads=["pw", "pT"], writes=[PK[pb_]])
                    Lop("dve", tt(sg[:, 0:n], ps[b][:, 0:n], rnC[:, l0:l0 + n], ALU.mult), reads=[PK[b], "rnC"], writes=[skey])
                    Lop("act", actf(sg[:, 0:n], sg[:, 0:n], AF.Sigmoid), reads=[skey], writes=[skey])
                    Lop("dve", tt(sg[:, 0:n], sg[:, 0:n], ps[pb_][:, 0:n], ALU.mult), reads=[PK[pb_], skey], writes=[skey])
                    Lop("dve", tt(R[:, m, l0:l0 + n], R[:, m, l0:l0 + n], sg[:, 0:n], ALU.add), reads=[skey, ("R", m)], writes=[("R", m)])
                sq = sqr[m % 2]
                Lop("act", actf(sq[:, 0:W], R[:, m, 0:W], AF.Square), reads=[("R", m)], writes=[("sqr", m % 2)])
                fns = []
                if m == 0:
                    fns.append(mm(ps[5][:, 256:256 + ntt], zeros_f, zeros_f[:, 0:ntt], start=True, stop=False))
                jt = 0
                for (t0, n, l0) in segs:
                    for j in range((n + 127) // 128):
                        r = min(128, n - j * 128)
                        fns.append(mm(ps[5][0:r, 256 + jt:257 + jt], sq[:, l0 + j * 128:l0 + j * 128 + r], ones_b[:, 0:1], start=False, stop=False))
                        jt += 1
                if m == 7:
                    fns.append(mm(ps[5][:, 256:256 + ntt], zeros_f, zeros_f[:, 0:ntt], start=False, stop=True))
                Lgr("pe", fns, reads=[("sqr", m % 2), "ones_b"], writes=[PK[5]])
                Lop("act", actf(R[:, m, 0:W], R[:, m, 0:W], AF.Copy, scale=gcol(2, m)), reads=[("R", m), ("sqr", m % 2), "cols"], writes=[("R", m)])
                return ops

            for blk in range(2):
                sl = load_r8(w_gate_v, blk * 512)
                chunks = [gate_chunk(blk * 4 + m4, sl, m4) for m4 in range(4)]
                zipper(chunks[0:2])
                zipper(chunks[2:4])
            S.op("dve", ts(rncol[:, 0:ntt], ps[5][:, 256:256 + ntt], 1.0 / 1024, ALU.mult, EPS, ALU.add), reads=[PK[5]], writes=["rncol"])
            S.op("act", actf(rncol[:, 0:ntt], rncol[:, 0:ntt], AF.Sqrt), reads=["rncol"], writes=["rncol"])
            S.op("dve", lambda e: e.reciprocal(out=rncol[:, 0:ntt], in_=rncol[:, 0:ntt]), reads=["rncol"], writes=["rncol"])
            jt = 0
            for (t0, n, l0) in segs:
                ntile = (n + 127) // 128
                for j in range(ntile):
                    r = min(128, n - j * 128)
                    ysl = 0
                    for half in range(2):
                        b = next_bank()
                        S.group("pe", [(lambda e, m4=m4, b=b, r=r, half=half, l0=l0, j=j: e.transpose(out=ps[b][0:r, m4 * 128:(m4 + 1) * 128], in_=R[:, half * 4 + m4, l0 + j * 128:l0 + j * 128 + r], identity=ident_f)) for m4 in range(4)],
                                reads=[("R", half * 4 + m4) for m4 in range(4)] + ["ident_f"], writes=[PK[b]])
                        S.op("act", actf(ytile[ysl][0:r, half * 512:(half + 1) * 512], ps[b][0:r, :], AF.Copy, scale=rncol[0:r, jt:jt + 1]), reads=[PK[b], "rncol"], writes=[("ytile", ysl)])
                    jt += 1
                    dst = y_p[t0 + j * 128:t0 + j * 128 + r, :] if n == 512 else y_s
                    S.dma("sp", "o_y%d" % ysl, dmaf(dst, ytile[ysl][0:r, :]), reads=[("ytile", ysl)])
        S.finish()
        with nc.Block() as block:
            S.replay(block)
    return nc


_PROG = {}


def _make_in_maps(inputs):
    f = lambda a: np.ascontiguousarray(np.asarray(a, dtype=np.float32))
    g = {k: f(v) for k, v in inputs.items()}
    shared = {
        "g_mix": g["g_mix"].reshape(1, 1024), "w_in": g["w_in"][0], "w_conv": g["w_conv"][0],
        "a_log": g["a_log"].reshape(1, 4), "dt_bias": g["dt_bias"].reshape(1, 4), "gdn_norm": g["gdn_norm"].reshape(1, 128),
        "ln_g": g["sgu_ln_g"].reshape(1, 512), "ln_b": g["sgu_ln_b"].reshape(1, 512), "w_s": g["w_s"][0],
        "b_s": g["b_s"].reshape(1, 512), "w_out": g["w_out"][0], "g_ff": g["g_ff"].reshape(8, 128), "w_up": g["w_up"][0],
        "w_down": g["w_down"][0], "g_ple": g["g_ple"].reshape(8, 128), "w_ple": g["w_ple"][0], "w_gate": g["w_ple_gate"][0],
        "g_fin": g["g_final"].reshape(8, 128),
    }
    maps = []
    for i in range(8):
        m = dict(shared)
        sl = slice(16 * i, 16 * i + 16)
        m["x_p"] = g["x_prompt"][i]
        m["x_s"] = g["x_sample"][sl, 0]
        m["st_conv"] = g["state_conv"][0, sl]
        m["st_gdn"] = g["state_gdn"][0, sl]
        m["p_p"] = g["p_prompt"][0, i]
        m["p_s"] = g["p_sample"][0, sl, 0]
        maps.append(m)
    return maps


def kernel(**inputs):
    if "nc" not in _PROG:
        _PROG["nc"] = build_program()
    nc = _PROG["nc"]
    maps = _make_in_maps(inputs)
    res = run_bass_kernel_spmd(nc, maps, core_ids=list(range(8)))
    R = res.results
    st = lambda name: np.stack([np.asarray(r[name], dtype=np.float32) for r in R])
    cc = lambda name: np.concatenate([np.asarray(r[name], dtype=np.float32) for r in R], axis=0)
    y_prompt = st("y_p")
    y_sample = cc("y_s")[:, None, :]
    new_conv_prompt = st("ncp")[None]
    new_gdn_prompt = st("ngp")[None]
    new_conv_sample = cc("ncs")[None]
    new_gdn_sample = cc("ngs")[None]
    new_sgu_v_sample = cc("nsv")[None, :, None, :]
    return (y_prompt, y_sample, new_conv_prompt, new_gdn_prompt, new_conv_sample, new_gdn_sample, new_sgu_v_sample)
```

```python
import os
import numpy as np
import concourse.bass as bass
import concourse.mybir as mybir
from concourse.bass_utils import run_bass_kernel_spmd

F32 = mybir.dt.float32
BF16 = mybir.dt.bfloat16
AF = mybir.ActivationFunctionType
ALU = mybir.AluOpType
AX = mybir.AxisListType


class Sched:
    ENGS = ("pe", "act", "dve", "pool", "sp")

    def __init__(self, nc, stack):
        self.nc = nc
        self.stack = stack
        self.streams = {e: [] for e in self.ENGS}
        self.esem = {e: stack.enter_context(nc.semaphore("c_" + e)) for e in self.ENGS[:4]}
        self.ecnt = {e: 0 for e in self.ENGS}
        self.waited = {e: {} for e in self.ENGS}
        self.res = {}
        self.dsem = {}
        self.sem_by_name = {}
        for e in self.ENGS[:4]:
            self.sem_by_name[self.esem[e].name] = self.esem[e]

    def _need(self, eng, ev, waits):
        if ev is None:
            return
        name, val, src = ev
        if src == eng and eng == "pe":
            return
        cur = waits.get(name, 0)
        if val > cur:
            waits[name] = val

    def _deps(self, eng, reads, writes):
        waits = {}
        for k in reads:
            r = self.res.get(k)
            if r is not None:
                self._need(eng, r[0], waits)
        for k in writes:
            r = self.res.get(k)
            if r is not None:
                if r[0] is not None and not (r[0][2] == eng):
                    self._need(eng, r[0], waits)
                for ev in r[1]:
                    self._need(eng, ev, waits)
        out = []
        w = self.waited[eng]
        for name, val in waits.items():
            if w.get(name, 0) < val:
                w[name] = val
                out.append((name, val))
        return out

    def _commit(self, ev, reads, writes):
        for k in reads:
            r = self.res.setdefault(k, [None, []])
            r[1].append(ev)
        for k in writes:
            self.res[k] = [ev, []]

    def op(self, eng, fn, reads=(), writes=()):
        waits = self._deps(eng, reads, writes)
        self.ecnt[eng] += 1
        ev = (self.esem[eng].name, self.ecnt[eng], eng)
        self.streams[eng].append((waits, [fn], ("inc", self.esem[eng], 1)))
        self._commit(ev, reads, writes)
        return ev

    def group(self, eng, fns, reads=(), writes=()):
        waits = self._deps(eng, reads, writes)
        self.ecnt[eng] += 1
        ev = (self.esem[eng].name, self.ecnt[eng], eng)
        self.streams[eng].append((waits, list(fns), ("inc", self.esem[eng], 1)))
        self._commit(ev, reads, writes)
        return ev

    def dma(self, eng, slot, fn, reads=(), writes=(), n=1):
        if slot not in self.dsem:
            s = self.stack.enter_context(self.nc.semaphore("d_" + slot))
            self.dsem[slot] = [s, 0]
            self.sem_by_name[s.name] = s
        waits = self._deps(eng, reads, writes)
        d = self.dsem[slot]
        fns = fn if isinstance(fn, (list, tuple)) else [fn]
        d[1] += 16 * len(fns)
        ev = (d[0].name, d[1], "dma")
        self.streams[eng].append((waits, list(fns), ("dmainc", d[0], 16)))
        self._commit(ev, reads, writes)
        return ev

    def barrier(self, skip=()):
        evs = []
        for e in self.ENGS[:4]:
            if self.ecnt[e] > 0:
                evs.append((self.esem[e].name, self.ecnt[e]))
        for slot, (s, c) in self.dsem.items():
            if c > 0 and not any(slot.startswith(p) for p in skip):
                evs.append((s.name, c))
        for eng in self.ENGS:
            w = self.waited[eng]
            waits = []
            for name, val in evs:
                if w.get(name, 0) < val:
                    w[name] = val
                    waits.append((name, val))
            if waits:
                self.streams[eng].append((waits, [], None))
        self.res.clear()

    def finish(self):
        eng = "sp"
        waits = []
        for slot, (s, c) in self.dsem.items():
            if c > 0:
                waits.append((s.name, c))
        for e in self.ENGS[:4]:
            if self.ecnt[e] > 0:
                waits.append((self.esem[e].name, self.ecnt[e]))
        self.streams[eng].append((waits, [], None))

    def replay(self, block):
        sbn = self.sem_by_name

        def run(e, items):
            for waits, fns, inc in items:
                for name, val in waits:
                    e.wait_ge(sbn[name], val)
                last = None
                for i, f in enumerate(fns):
                    ins = f(e)
                    if inc is not None and inc[0] == "dmainc":
                        ins.then_inc(inc[1], 16)
                    last = ins
                if inc is not None and inc[0] == "inc" and last is not None:
                    last.then_inc(inc[1], 1)

        st = self.streams

        @block.tensor
        def _(e):
            run(e, st["pe"])

        @block.scalar
        def _(e):
            run(e, st["act"])

        @block.vector
        def _(e):
            run(e, st["dve"])

        @block.gpsimd
        def _(e):
            run(e, st["pool"])

        @block.sync
        def _(e):
            run(e, st["sp"])


U8 = mybir.dt.uint8
T_P = 2048
T_S = 16
T_ALL = T_P + T_S
SEGS = [(0, 512), (512, 512), (1024, 512), (1536, 512), (2048, 16)]
NT = 17
EPS = 1e-6
D_IN = 3080
C_Q, C_K, C_V, C_Z, C_BA, C_U, C_VS = 0, 512, 1024, 1536, 2048, 2056, 2568


def mm(out, lhsT, rhs, start=True, stop=True):
    return lambda e: e.matmul(out, lhsT=lhsT, rhs=rhs, start=start, stop=stop)


def actf(out, in_, func, **kw):
    return lambda e: e.activation(out=out, in_=in_, func=func, **kw)


def tt(out, a, b, op):
    return lambda e: e.tensor_tensor(out=out, in0=a, in1=b, op=op)


def ts(out, a, s1, op0, s2=None, op1=None):
    if op1 is None:
        return lambda e: e.tensor_scalar(out=out, in0=a, scalar1=s1, scalar2=None, op0=op0)
    return lambda e: e.tensor_scalar(out=out, in0=a, scalar1=s1, scalar2=s2, op0=op0, op1=op1)


def stt(out, a, s, b, op0, op1):
    return lambda e: e.scalar_tensor_tensor(out=out, in0=a, scalar=s, in1=b, op0=op0, op1=op1)


def stt_pool(out, a, colap):
    return lambda e: e.tensor_tensor(out=out, in0=a, in1=colap.to_broadcast([128, 128]), op=ALU.mult)


def cp(out, in_):
    return lambda e: e.tensor_copy(out=out, in_=in_)


def dmaf(out, in_):
    return lambda e: e.dma_start(out=out, in_=in_)


class _Item:
    __slots__ = ("thunk", "eng", "reads", "writes", "dur")

    def __init__(self, thunk, eng, reads, writes, dur):
        self.thunk, self.eng, self.reads, self.writes, self.dur = thunk, eng, tuple(reads), tuple(writes), dur


_DUR = {"act": 0.5, "dve": 0.45, "pool": 0.5}


def mk_recorders(S, ops):
    def Lop(eng, fn, reads=(), writes=()):
        ops.append(_Item(lambda: S.op(eng, fn, reads=reads, writes=writes), eng, reads, writes, _DUR.get(eng, 0.4)))

    def Lgr(eng, fns, reads=(), writes=()):
        ops.append(_Item(lambda: S.group(eng, fns, reads=reads, writes=writes), eng, reads, writes, 0.1 + 0.13 * len(fns)))

    def Ldma(eng, slot, fn, reads=(), writes=()):
        ops.append(_Item(lambda: S.dma(eng, slot, fn, reads=reads, writes=writes), "q_" + eng, reads, writes, 2.5))
    return Lop, Lgr, Ldma


def zipper(lists):
    lists = [l for l in lists if l]
    idx = [0] * len(lists)
    if os.environ.get("ZIP", "rr") == "rr":
        live = True
        while live:
            live = False
            for i, l in enumerate(lists):
                if idx[i] < len(l):
                    it = l[idx[i]]
                    idx[i] += 1
                    live = True
                    if isinstance(it, _Item):
                        it.thunk()
                    else:
                        it()
        return
    t_eng, t_w, t_r = {}, {}, {}
    remaining = sum(len(l) for l in lists)
    while remaining:
        best = None
        for i, l in enumerate(lists):
            if idx[i] >= len(l):
                continue
            it = l[idx[i]]
            if not isinstance(it, _Item):
                best = (-1.0, i, it)
                break
            rdy = t_eng.get(it.eng, 0.0)
            for k in it.reads:
                rdy = max(rdy, t_w.get(k, 0.0))
            for k in it.writes:
                rdy = max(rdy, t_w.get(k, 0.0), t_r.get(k, 0.0))
            if best is None or rdy < best[0]:
                best = (rdy, i, it)
        rdy, i, it = best
        idx[i] += 1
        remaining -= 1
        if not isinstance(it, _Item):
            it()
            continue
        it.thunk()
        fin = rdy + it.dur
        if it.eng.startswith("q_"):
            t_eng[it.eng] = rdy + 0.1
        else:
            t_eng[it.eng] = fin
        for k in it.reads:
            t_r[k] = max(t_r.get(k, 0.0), fin)
        for k in it.writes:
            t_w[k] = fin
            t_r[k] = 0.0


class Bump:
    def __init__(self, arena, start, limit):
        self.t, self.off, self.limit = arena, start, limit

    def alloc(self, dtype, shape):
        esz = 4 if dtype == F32 else 2
        n = 1
        for s in shape[1:]:
            n *= s
        nb = (n * esz + 63) // 64 * 64
        o = self.off
        self.off += nb
        assert self.off <= self.limit, ("SBUF arena overflow", self.off, self.limit)
        ap = self.t[:, o:o + n * esz].bitcast(dtype)
        if len(shape) == 3:
            ap = ap.rearrange("p (a b) -> p a b", a=shape[1])
        elif len(shape) == 4:
            ap = ap.rearrange("p (a b c) -> p a b c", a=shape[1], b=shape[2])
        return ap


def build_program():
    from contextlib import ExitStack
    nc = bass.Bass("TRN2", target_bir_lowering=False)

    def din(name, shape):
        return nc.dram_tensor(name, shape, F32, kind="ExternalInput").ap()

    def dout(name, shape):
        return nc.dram_tensor(name, shape, F32, kind="ExternalOutput").ap()

    x_p = din("x_p", [T_P, 1024]); x_s = din("x_s", [T_S, 1024])
    st_conv = din("st_conv", [T_S, 3, 1536]); st_gdn = din("st_gdn", [T_S, 4, 128, 128])
    p_p = din("p_p", [T_P, 256]); p_s = din("p_s", [T_S, 256])
    g_mix = din("g_mix", [1, 1024]); w_in = din("w_in", [1024, D_IN]); w_conv = din("w_conv", [4, 1536])
    a_log = din("a_log", [1, 4]); dt_bias = din("dt_bias", [1, 4]); gdn_norm = din("gdn_norm", [1, 128])
    ln_g = din("ln_g", [1, 512]); ln_b = din("ln_b", [1, 512]); w_s = din("w_s", [4, 128, 128]); b_s = din("b_s", [1, 512])
    w_out = din("w_out", [1024, 1024]); g_ff = din("g_ff", [8, 128]); w_up = din("w_up", [1024, 4096]); w_down = din("w_down", [4096, 1024])
    g_ple = din("g_ple", [8, 128]); w_ple = din("w_ple", [256, 1024]); w_gate = din("w_gate", [1024, 1024]); g_fin = din("g_fin", [8, 128])
    y_p = dout("y_p", [T_P, 1024]); y_s = dout("y_s", [T_S, 1024])
    ncp = dout("ncp", [3, 1536]); ngp = dout("ngp", [4, 128, 128])
    ncs = dout("ncs", [T_S, 3, 1536]); ngs = dout("ngs", [T_S, 4, 128, 128]); nsv = dout("nsv", [T_S, 512])

    w_in_v = w_in.rearrange("(k p) c -> p k c", p=128)
    w_out_v = w_out.rearrange("(k p) c -> p k c", p=128)
    w_up_v = w_up.rearrange("(k p) c -> p k c", p=128)
    w_down_v = w_down.rearrange("(k p) c -> p k c", p=128)
    w_gate_v = w_gate.rearrange("(k p) c -> p k c", p=128)
    w_ple_v = w_ple.rearrange("(k p) c -> p k c", p=128)

    with ExitStack() as st:
        S = Sched(nc, st)
        ARENA = 206 * 1024
        arena = st.enter_context(nc.sbuf_tensor("arena", [128, ARENA], U8))
        ps = [st.enter_context(nc.psum_tensor("ps%d" % i, [128, 512], F32)) for i in range(8)]
        psb = [p[:, :].bitcast(BF16) for p in ps]
        PK = [("ps", i) for i in range(8)]

        P = Bump(arena, 0, ARENA)
        ident_f = P.alloc(F32, [128, 128]); ident_b = P.alloc(BF16, [128, 128])
        ones_f = P.alloc(F32, [128, 128]); ones_b = P.alloc(BF16, [128, 128])
        mask_incl = P.alloc(F32, [128, 128])
        mask_su = P.alloc(F32, [128, 128])
        nmask_sl = P.alloc(F32, [128, 128])
        sel127 = P.alloc(F32, [128, 128])
        nmask_su = P.alloc(F32, [128, 128])
        mask_sl = P.alloc(F32, [128, 128])
        rowstage = P.alloc(F32, [128, 128])
        cols = P.alloc(F32, [128, 128])
        wsT = P.alloc(BF16, [128, 4, 128])
        selws = P.alloc(BF16, [128, 4, 16])
        ws00 = P.alloc(F32, [128, 4])
        bs_row = P.alloc(F32, [128, 4, 128])
        lng_row = P.alloc(F32, [128, 512]); lnb_row = P.alloc(F32, [128, 512])
        alog_row = P.alloc(F32, [128, 4]); dtb_row = P.alloc(F32, [128, 4]); nexpA_row = P.alloc(F32, [128, 4])
        zcol = P.alloc(F32, [128, 4])
        zeros_f = P.alloc(F32, [128, 128])
        cat = P.alloc(BF16, [128, 8, T_ALL])
        P_C0 = P.off
        ba = P.alloc(F32, [128, NT, 8])
        beta_c = P.alloc(F32, [128, NT, 4]); g_c = P.alloc(F32, [128, NT, 4]); gc_c = P.alloc(F32, [128, NT, 4])
        bexp_c = P.alloc(F32, [128, NT, 4]); kd_c = P.alloc(F32, [128, NT, 4]); egl_c = P.alloc(F32, [128, NT, 4])
        tmp68 = P.alloc(F32, [128, NT, 4])
        qkv = P.alloc(BF16, [128, 12, T_ALL])
        zs = P.alloc(BF16, [128, 4, T_ALL])
        qks_f = P.alloc(F32, [128, 12, 16])
        histT = P.alloc(F32, [128, 12, 3, 16])
        ncp_st = P.alloc(F32, [128, 12, 3]); ncs_st = P.alloc(F32, [128, 12, 16])
        S_f = P.alloc(F32, [128, 4, 128]); S_b = P.alloc(BF16, [128, 4, 128])
        X0 = P.off
        XB = Bump(arena, X0, ARENA)
        hT = XB.alloc(BF16, [128, 8, T_ALL])
        Y0 = XB.off

        def wcol(j, c):
            return cols[:, 24 + j * 12 + c: 24 + j * 12 + c + 1]

        def gcol(which, m):
            return cols[:, which * 8 + m: which * 8 + m + 1]
        gdnn_col = cols[:, 72:73]
        negone = zcol[:, 1:2]

        S.op("pool", lambda e: e.memset(ones_f, 1.0), writes=["ones_f"])
        S.op("pool", lambda e: e.memset(ones_b, 1.0), writes=["ones_b"])
        S.op("pool", lambda e: e.memset(zcol, 0.0), writes=["zcol"])
        S.op("pool", lambda e: e.memset(zcol[:, 1:2], -1.0), reads=["zcol"], writes=["zcol"])
        S.op("pool", lambda e: e.memset(zeros_f, 0.0), writes=["zeros_f"])
        S.op("pool", lambda e: e.affine_select(out=ident_f, in_=ones_f, pattern=[[-1, 128]], compare_op=ALU.is_equal, fill=0.0, base=0, channel_multiplier=1), reads=["ones_f"], writes=["ident_f"])
        S.op("pool", lambda e: e.affine_select(out=mask_incl, in_=ones_f, pattern=[[1, 128]], compare_op=ALU.is_ge, fill=0.0, base=0, channel_multiplier=-1), reads=["ones_f"], writes=["mask_incl"])
        S.op("pool", lambda e: e.affine_select(out=mask_su, in_=ones_f, pattern=[[1, 128]], compare_op=ALU.is_gt, fill=0.0, base=0, channel_multiplier=-1), reads=["ones_f"], writes=["mask_su"])
        S.op("pool", lambda e: e.affine_select(out=nmask_sl, in_=ones_f, pattern=[[-1, 128]], compare_op=ALU.is_gt, fill=0.0, base=0, channel_multiplier=1), reads=["ones_f"], writes=["nmask_sl"])
        S.op("pool", ts(nmask_sl, nmask_sl, -1.0, ALU.mult), reads=["nmask_sl"], writes=["nmask_sl"])
        S.op("pool", ts(nmask_su, mask_su, -1.0, ALU.mult), reads=["mask_su"], writes=["nmask_su"])
        S.op("pool", ts(mask_sl, nmask_sl, -1.0, ALU.mult), reads=["nmask_sl"], writes=["mask_sl"])
        S.op("pool", lambda e: e.affine_select(out=sel127, in_=ones_f, pattern=[[0, 128]], compare_op=ALU.is_equal, fill=0.0, base=-127, channel_multiplier=1), reads=["ones_f"], writes=["sel127"])
        S.op("dve", cp(ident_b, ident_f), reads=["ident_f"], writes=["ident_b"])
        S.op("pool", lambda e: e.memset(rowstage, 0.0), writes=["rowstage"])
        S.dma("sp", "c0", [dmaf(rowstage[0:8, :], g_ff), dmaf(rowstage[8:16, :], g_ple), dmaf(rowstage[16:24, :], g_fin),
                           dmaf(rowstage[24:72, :], w_conv.rearrange("j (c p) -> (j c) p", p=128)), dmaf(rowstage[72:73, :], gdn_norm)],
              writes=["rowstage"])
        S.group("pe", [mm(ps[0][:, 0:128], rowstage, ident_f)], reads=["rowstage", "ident_f"], writes=[PK[0]])
        S.op("dve", cp(cols, ps[0][:, 0:128]), reads=[PK[0]], writes=["cols"])
        S.dma("sp", "c1", [dmaf(bs_row.rearrange("p h t -> p (h t)"), b_s.partition_broadcast(128)),
                           dmaf(lng_row, ln_g.partition_broadcast(128)), dmaf(lnb_row, ln_b.partition_broadcast(128)),
                           dmaf(alog_row, a_log.partition_broadcast(128)), dmaf(dtb_row, dt_bias.partition_broadcast(128)),
                           ] + [dmaf(ws00[:, h:h + 1], w_s[h, 0, 0:1].partition_broadcast(128)) for h in range(4)],
              writes=["rows"])
        S.op("act", actf(nexpA_row, alog_row, AF.Exp), reads=["rows"], writes=["nexpA"])
        S.op("dve", ts(nexpA_row, nexpA_row, -1.0, ALU.mult), reads=["nexpA"], writes=["nexpA"])
        for h in range(4):
            S.op("dve", ts(selws[0:16, h, :], ident_f[0:16, 0:16], ws00[0:16, h:h + 1], ALU.mult), reads=["rows", "ident_f"], writes=[("selws", h)])

        YA = Bump(arena, Y0, ARENA)
        wstmp = YA.alloc(F32, [128, 4, 128])
        S.dma("sp", "c2", dmaf(wstmp, w_s.rearrange("h t s -> t h s")), writes=["wstmp"])
        for h in range(4):
            S.op("pool", lambda e, h=h: e.affine_select(out=wstmp[:, h, :], in_=wstmp[:, h, :], pattern=[[-1, 128]], compare_op=ALU.is_ge, fill=0.0, base=0, channel_multiplier=1),
                 reads=["wstmp"], writes=["wstmp"])
        S.group("pe", [mm(ps[1][:, h * 128:(h + 1) * 128], wstmp[:, h, :], ident_f) for h in range(4)], reads=["wstmp", "ident_f"], writes=[PK[1]])
        S.op("dve", cp(wsT.rearrange("p h t -> p (h t)"), ps[1][:, 0:512]), reads=[PK[1]], writes=["wsT"])

        gmix_row = YA.alloc(F32, [128, 1024])
        S.dma("sp", "c3", dmaf(gmix_row, g_mix.partition_broadcast(128)), writes=["gmix"])
        xt = [YA.alloc(F32, [128, 1024]) for _ in range(3)]
        xsq = [YA.alloc(F32, [128, 1024]) for _ in range(3)]
        xn = [YA.alloc(BF16, [128, 1024]) for _ in range(3)]
        stat = YA.alloc(F32, [128, NT, 2])

        def phaseA_tile(i):
            ops = []
            Lop, Lgr, Ldma = mk_recorders(S, ops)
            r = 128 if i < 16 else 16
            sl = i % 3
            src = x_p[i * 128:(i + 1) * 128, :] if i < 16 else x_s
            Ldma("sp", "xt%d" % sl, dmaf(xt[sl][0:r, :], src), writes=[("xt", sl)])
            Lop("act", actf(xsq[sl][0:r, :], xt[sl][0:r, :], AF.Square), reads=[("xt", sl)], writes=[("xsq", sl)])
            Lop("dve", lambda e: e.reduce_sum(out=stat[0:r, i, 0:1], in_=xsq[sl][0:r, :], axis=AX.X), reads=[("xsq", sl)], writes=[("stat", i)])
            Lop("dve", ts(stat[0:r, i, 1:2], stat[0:r, i, 0:1], 1.0 / 1024, ALU.mult, EPS, ALU.add), reads=[("stat", i)], writes=[("stat", i)])
            Lop("act", actf(stat[0:r, i, 1:2], stat[0:r, i, 1:2], AF.Sqrt), reads=[("stat", i)], writes=[("stat", i)])
            Lop("dve", lambda e: e.reciprocal(out=stat[0:r, i, 1:2], in_=stat[0:r, i, 1:2]), reads=[("stat", i)], writes=[("stat", i)])
            Lop("dve", stt(xn[sl][0:r, :], xt[sl][0:r, :], stat[0:r, i, 1:2], gmix_row[0:r, :], ALU.mult, ALU.mult),
                reads=[("xt", sl), ("stat", i), "gmix"], writes=[("xn", sl)])
            b = i % 3
            Lgr("pe", [lambda e, k=k: e.transpose(out=psb[b][:, k * 128:k * 128 + r], in_=xn[sl][0:r, k * 128:(k + 1) * 128], identity=ident_b[0:r, 0:r]) for k in range(8)],
                reads=[("xn", sl), "ident_b"], writes=[PK[b]])
            if i % 2 == 0:
                Lop("act", actf(hT[:, :, i * 128:i * 128 + r], psb[b].rearrange("p (k t) -> p k t", k=8)[:, :, 0:r], AF.Copy), reads=[PK[b]], writes=[("hT", i)])
            else:
                Lop("dve", cp(hT[:, :, i * 128:i * 128 + r], psb[b].rearrange("p (k t) -> p k t", k=8)[:, :, 0:r]), reads=[PK[b]], writes=[("hT", i)])
            return ops

        tilesA = [phaseA_tile(i) for i in range(NT)]
        for g0 in range(0, NT, 3):
            zipper(tilesA[g0:g0 + 3])
        S.barrier()

        YB = Bump(arena, Y0, ARENA)
        wb = [YB.alloc(BF16, [128, 8, 512]) for _ in range(2)]
        wb8 = YB.alloc(BF16, [128, 8, 8])
        lnst = YB.alloc(F32, [128, NT, 8])
        sct = [YB.alloc(F32, [128, 3, 128]) for _ in range(2)]
        YB_MID = YB.off
        vg = YB.alloc(BF16, [128, NT, 512])
        F6 = YB.alloc(F32, [128, 6, 512])
        f512 = [F6[:, i, :] for i in range(6)]
        vgs_f = f512[5]
        YB5 = Bump(arena, YB_MID, ARENA)
        NBS = 4
        pre = [YB5.alloc(F32, [128, 515]) for _ in range(NBS)]
        accb = [YB5.alloc(F32, [128, 512]) for _ in range(NBS)]
        rnb = [YB5.alloc(F32, [128, 512]) for _ in range(NBS)]
        sqb = [YB5.alloc(BF16, [128, 512]) for _ in range(NBS)]
        wdiag = [YB5.alloc(F32, [128, 4, 128]) for _ in range(2)]
        YB6 = Bump(arena, YB_MID, ARENA)
        stage_tok = YB6.alloc(F32, [128, 1536])
        stage2 = YB6.alloc(F32, [128, 1536])
        wb_n = [0]
        bank_n = [0]

        def next_bank(lo=0, hi=4):
            b = lo + bank_n[0] % (hi - lo)
            bank_n[0] += 1
            return b

        wb_seq = [C_VS, C_U, C_Z, 0, 512, 1024]
        wb_issued = [0]

        def load_w(view, c0, ncol=512):
            idx = wb_n[0]
            wb_n[0] += 1
            assert wb_seq[idx] == c0
            while wb_issued[0] < min(len(wb_seq), idx + 2):
                j = wb_issued[0]
                S.dma("pool", "wb%d" % (j % 2), dmaf(wb[j % 2][:, :, 0:512], view[:, :, wb_seq[j]:wb_seq[j] + 512]), writes=[("wb", j % 2)])
                wb_issued[0] += 1
            return idx % 2

        hT_keys = [("hT", i) for i in range(NT)]

        def seg_hT_keys(t0, n):
            return [("hT", i) for i in range(t0 // 128, (t0 + n + 127) // 128)]

        sl = load_w(w_in_v, C_VS)

        def vsgu_tile(i):
            ops = []
            Lop, Lgr, Ldma = mk_recorders(S, ops)
            r = 128 if i < 16 else 16
            b = (i % 3)
            Lgr("pe", [mm(ps[b][0:r, :], hT[:, k, i * 128:i * 128 + r], wb[sl][:, k, :], start=(k == 0), stop=(k == 7)) for k in range(8)],
                    reads=[("hT", i), ("wb", sl)], writes=[PK[b]])
            g1 = f512[i % 3]; g2 = f512[3 + i % 3]
            Lop("act", actf(g1[0:r, :], ps[b][0:r, :], AF.Gelu_apprx_tanh), reads=[PK[b]], writes=[("g1", i % 3)])
            Lop("pool", tt(g2[0:r, :], g1[0:r, :], g1[0:r, :], ALU.mult), reads=[("g1", i % 3)], writes=[("g2", i % 3)])
            Lop("dve", lambda e, i=i, r=r, g1=g1: e.reduce_sum(out=lnst[0:r, i, 0:1], in_=g1[0:r, :], axis=AX.X), reads=[("g1", i % 3)], writes=[("lnst", i)])
            Lop("dve", lambda e, i=i, r=r, g2=g2: e.reduce_sum(out=lnst[0:r, i, 1:2], in_=g2[0:r, :], axis=AX.X), reads=[("g2", i % 3)], writes=[("lnst", i)])
            L = lambda a, bb: lnst[0:r, i, a:bb]
            Lop("dve", ts(L(2, 3), L(0, 1), 1.0 / 512, ALU.mult), reads=[("lnst", i)], writes=[("lnst", i)])
            Lop("dve", tt(L(3, 4), L(2, 3), L(2, 3), ALU.mult), reads=[("lnst", i)], writes=[("lnst", i)])
            Lop("dve", stt(L(4, 5), L(1, 2), 1.0 / 512, L(3, 4), ALU.mult, ALU.subtract), reads=[("lnst", i)], writes=[("lnst", i)])
            Lop("dve", ts(L(4, 5), L(4, 5), EPS, ALU.add), reads=[("lnst", i)], writes=[("lnst", i)])
            Lop("act", actf(L(4, 5), L(4, 5), AF.Sqrt), reads=[("lnst", i)], writes=[("lnst", i)])
            Lop("dve", lambda e, i=i, r=r: e.reciprocal(out=lnst[0:r, i, 5:6], in_=lnst[0:r, i, 4:5]), reads=[("lnst", i)], writes=[("lnst", i)])
            Lop("dve", ts(g2[0:r, :], g1[0:r, :], L(2, 3), ALU.subtract, L(5, 6), ALU.mult), reads=[("g1", i % 3), ("lnst", i)], writes=[("g2", i % 3)])
            Lop("pool", tt(g2[0:r, :], g2[0:r, :], lng_row[0:r, :], ALU.mult), reads=[("g2", i % 3), "rows"], writes=[("g2", i % 3)])
            if i < 16:
                Lop("pool", tt(vg[0:r, i, :], g2[0:r, :], lnb_row[0:r, :], ALU.add), reads=[("g2", i % 3), "rows"], writes=[("vg", i)])
            else:
                Lop("pool", tt(vgs_f[0:r, :], g2[0:r, :], lnb_row[0:r, :], ALU.add), reads=[("g2", i % 3), "rows"], writes=[("g2", 2)])
                Lop("pool", cp(vg[0:r, i, :], vgs_f[0:r, :]), reads=[("g2", 2)], writes=[("vg", i)])
                Ldma("sp", "o_nsv", dmaf(nsv, vgs_f[0:r, :]), reads=[("g2", 2)])
            return ops

        tilesV = [vsgu_tile(i) for i in range(NT)]
        for g0 in range(0, NT, 3):
            zipper(tilesV[g0:g0 + 3])

        S.dma("pool", "wb8", dmaf(wb8, w_in_v[:, :, C_BA:C_BA + 8]), writes=["wb8"])
        S.op("pool", lambda e: e.memset(ba, 0.0), writes=["ba"])
        bq = 4
        for i in range(NT):
            r = 128 if i < 16 else 16
            S.group("pe", [mm(ps[bq][0:r, i * 8:(i + 1) * 8], hT[:, k, i * 128:i * 128 + r], wb8[:, k, :], start=(k == 0), stop=(k == 7)) for k in range(8)],
                    reads=[("hT", i), "wb8"], writes=[PK[bq]])
        S.op("dve", cp(ba[:, 0:16, :], ps[bq][:, 0:128].rearrange("p (i c) -> p i c", c=8)), reads=[PK[bq], "ba"], writes=["ba"])
        S.op("dve", cp(ba[0:16, 16, :], ps[bq][0:16, 128:136]), reads=[PK[bq], "ba"], writes=["ba"])
        S.op("act", actf(beta_c, ba[:, :, 0:4], AF.Sigmoid), reads=["ba"], writes=["beta_c"])
        S.op("dve", tt(tmp68, ba[:, :, 4:8], dtb_row.unsqueeze(1).to_broadcast([128, NT, 4]), ALU.add), reads=["ba", "rows"], writes=["tmp68"])
        S.op("act", actf(tmp68, tmp68, AF.Exp), reads=["tmp68"], writes=["tmp68"])
        S.op("act", actf(tmp68, tmp68, AF.Ln, bias=1.0), reads=["tmp68"], writes=["tmp68"])
        S.op("dve", tt(g_c, tmp68, nexpA_row.unsqueeze(1).to_broadcast([128, NT, 4]), ALU.mult), reads=["tmp68", "nexpA"], writes=["g_c"])
        g68 = g_c.rearrange("p i h -> p (i h)"); gc68 = gc_c.rearrange("p i h -> p (i h)")
        S.group("pe", [mm(ps[5][:, 0:68], mask_incl, g68)], reads=["mask_incl", "g_c"], writes=[PK[5]])
        S.op("dve", cp(gc68, ps[5][:, 0:68]), reads=[PK[5]], writes=["gc_c"])
        S.group("pe", [mm(ps[5][:, 128:196], sel127, gc68)], reads=["sel127", "gc_c"], writes=[PK[5]])
        S.op("dve", cp(egl_c.rearrange("p i h -> p (i h)"), ps[5][:, 128:196]), reads=[PK[5]], writes=["egl_c"])
        S.op("dve", tt(tmp68.rearrange("p i h -> p (i h)"), egl_c.rearrange("p i h -> p (i h)"), gc68, ALU.subtract), reads=["egl_c", "gc_c"], writes=["tmp68"])
        S.op("act", actf(egl_c, egl_c, AF.Exp), reads=["egl_c", "tmp68"], writes=["egl_c"])
        S.op("act", actf(kd_c, tmp68, AF.Exp), reads=["tmp68"], writes=["kd_c"])
        S.op("act", actf(bexp_c, gc_c, AF.Exp), reads=["gc_c"], writes=["bexp_c"])
        S.op("dve", tt(bexp_c, bexp_c, beta_c, ALU.mult), reads=["bexp_c", "beta_c"], writes=["bexp_c"])

        sl = load_w(w_in_v, C_U)
        for h in range(4):
            for (t0, n) in SEGS:
                b = next_bank()
                S.group("pe", [mm(ps[b][:, 0:n], wb[sl][:, k, h * 128:(h + 1) * 128], hT[:, k, t0:t0 + n], start=(k == 0), stop=(k == 7)) for k in range(8)],
                        reads=seg_hT_keys(t0, n) + [("wb", sl)], writes=[PK[b]])
                u = f512[bank_n[0] % 2]
                S.op("act", actf(u[:, 0:n], ps[b][:, 0:n], AF.Gelu_apprx_tanh), reads=[PK[b]], writes=[("u", bank_n[0] % 2)])
                b2 = 4 + bank_n[0] % 2
                if n == 512:
                    tiles = [t0 // 128 + j for j in range(4)]
                    S.group("pe", [mm(ps[b2][:, j * 128:(j + 1) * 128], vg[:, tiles[j], h * 128:(h + 1) * 128], wsT[:, h, :]) for j in range(4)],
                            reads=[("vg", ti) for ti in tiles] + ["wsT"], writes=[PK[b2]])
                    S.op("dve", tt(f512[4][:, :].rearrange("p (j t) -> p j t", j=4), ps[b2][:, :].rearrange("p (j t) -> p j t", j=4),
                                   bs_row[:, h:h + 1, :].to_broadcast([128, 4, 128]), ALU.add), reads=[PK[b2], "rows"], writes=["mixt"])
                else:
                    S.group("pe", [mm(ps[b2][:, 0:16], vg[0:16, 16, h * 128:(h + 1) * 128], selws[0:16, h, :])],
                            reads=[("vg", 16), ("selws", h)], writes=[PK[b2]])
                    S.op("dve", tt(f512[4][:, 0:16], ps[b2][:, 0:16], bs_row[:, h, 0:1].to_broadcast([128, 16]), ALU.add), reads=[PK[b2], "rows"], writes=["mixt"])
                S.op("pool", tt(cat[:, 4 + h, t0:t0 + n], f512[4][:, 0:n], u[:, 0:n], ALU.mult), reads=["mixt", ("u", bank_n[0] % 2)], writes=[("cat", 4 + h, t0)])

        sl = load_w(w_in_v, C_Z)
        for h in range(4):
            for (t0, n) in SEGS:
                b = next_bank()
                S.group("pe", [mm(ps[b][:, 0:n], wb[sl][:, k, h * 128:(h + 1) * 128], hT[:, k, t0:t0 + n], start=(k == 0), stop=(k == 7)) for k in range(8)],
                        reads=seg_hT_keys(t0, n) + [("wb", sl)], writes=[PK[b]])
                S.op("act", actf(zs[:, h, t0:t0 + n], ps[b][:, 0:n], AF.Silu), reads=[PK[b]], writes=[("zs", h, t0)])

        S.barrier()

        def qkv_unit(blk, h, si, ui, wsl):
            ops = []
            Lop, Lgr, Ldma = mk_recorders(S, ops)
            c = blk * 4 + h
            t0, n = SEGS[si]
            bs = ui % NBS
            pr, acc, sq, rn = pre[bs], accb[bs], sqb[bs], rnb[bs]
            kp, ka, ks, kr = ("pre", bs), ("acc", bs), ("sq", bs), ("rn", bs)
            b = ui % 4; bn = 4 + ui % 4
            Lgr("pe", [mm(ps[b][:, 0:n], wb[wsl][:, k, h * 128:(h + 1) * 128], hT[:, k, t0:t0 + n], start=(k == 0), stop=(k == 7)) for k in range(8)],
                reads=[("wb", wsl)], writes=[PK[b]])
            if si == 0:
                Lop("dve", lambda e: e.memset(pr[:, 0:3], 0.0), writes=[kp])
            elif si < 4:
                Lgr("pe", [mm(ps[bn][:, 0:3], wb[wsl][:, k, h * 128:(h + 1) * 128], hT[:, k, t0 - 3:t0], start=(k == 0), stop=(k == 7)) for k in range(8)],
                    reads=[("wb", wsl)], writes=[PK[bn]])
                Lop("dve", cp(pr[:, 0:3], ps[bn][:, 0:3]), reads=[PK[bn]], writes=[kp])
            Lop("act", actf(pr[:, 3:3 + n], ps[b][:, 0:n], AF.Copy), reads=[PK[b], kp], writes=[kp])
            if si == 3:
                Lop("dve", cp(ncp_st[:, c, :], pr[:, 512:515]), reads=[kp], writes=[("ncp_st", c)])
            if si < 4 and os.environ.get("CONV", "dve") == "dve":
                Lop("act", actf(acc[:, 0:n], pr[:, 3:3 + n], AF.Copy, scale=wcol(3, c)), reads=[kp, "cols"], writes=[ka])
                Lop("dve", stt(acc[:, 0:n], pr[:, 2:2 + n], wcol(2, c), acc[:, 0:n], ALU.mult, ALU.add), reads=[kp, ka, "cols"], writes=[ka])
                Lop("dve", stt(acc[:, 0:n], pr[:, 1:1 + n], wcol(1, c), acc[:, 0:n], ALU.mult, ALU.add), reads=[kp, ka, "cols"], writes=[ka])
                Lop("dve", stt(acc[:, 0:n], pr[:, 0:n], wcol(0, c), acc[:, 0:n], ALU.mult, ALU.add), reads=[kp, ka, "cols"], writes=[ka])
                Lop("act", actf(acc[:, 0:n], acc[:, 0:n], AF.Silu), reads=[ka], writes=[ka])
            elif si < 4:
                wd = wdiag[c % 2]
                bc_ = 4 + ui % 4
                fns = []
                for t4 in range(4):
                    for j in range(4):
                        fns.append(mm(ps[bc_][:, t4 * 128:(t4 + 1) * 128], wd[:, j, :], pr[:, j + t4 * 128:j + (t4 + 1) * 128], start=(j == 0), stop=(j == 3)))
                Lgr("pe", fns, reads=[kp, ("wdiag", c % 2)], writes=[PK[bc_]])
                Lop("act", actf(acc[:, 0:n], ps[bc_][:, 0:n], AF.Silu), reads=[PK[bc_]], writes=[ka])
            else:
                Lop("dve", cp(ncs_st[:, c, :], pr[:, 3:19]), reads=[kp], writes=[("ncs_st", c)])
                Lop("act", actf(acc[:, 0:n], pr[:, 3:3 + n], AF.Copy, scale=wcol(3, c)), reads=[kp, "cols"], writes=[ka])
                for j in (2, 1, 0):
                    Lop("dve", stt(acc[:, 0:n], histT[:, c, j, :], wcol(j, c), acc[:, 0:n], ALU.mult, ALU.add), reads=[("histT", c), ka, "cols"], writes=[ka])
                Lop("act", actf(acc[:, 0:n], acc[:, 0:n], AF.Silu), reads=[ka], writes=[ka])
            if blk == 2:
                Lop("pool", cp(qkv[:, c, t0:t0 + n], acc[:, 0:n]), reads=[ka], writes=[("qkv", c, t0)])
                if si == 4:
                    Lop("pool", cp(qks_f[:, c, :], acc[:, 0:16]), reads=[ka], writes=[("qks_f", c)])
            else:
                Lop("pool", tt(sq[:, 0:n], acc[:, 0:n], acc[:, 0:n], ALU.mult), reads=[ka], writes=[ks])
                Lgr("pe", [mm(ps[bn][:, 0:n], ones_b, sq[:, 0:n])], reads=[ks, "ones_b"], writes=[PK[bn]])
                Lop("act", actf(rn[:, 0:n], ps[bn][:, 0:n], AF.Sqrt, bias=1e-6), reads=[PK[bn]], writes=[kr])
                Lop("dve", lambda e: e.reciprocal(out=rn[:, 0:n], in_=rn[:, 0:n]), reads=[kr], writes=[kr])
                scl = (128.0 ** -0.5) if blk == 0 else 1.0
                Lop("dve", stt(qkv[:, c, t0:t0 + n], acc[:, 0:n], scl, rn[:, 0:n], ALU.mult, ALU.mult), reads=[ka, kr], writes=[("qkv", c, t0)])
                if si == 4:
                    Lop("dve", stt(qks_f[:, c, :], acc[:, 0:16], scl, rn[:, 0:16], ALU.mult, ALU.mult), reads=[ka, kr], writes=[("qks_f", c)])
            return ops

        ui = 0
        for blk in range(3):
            wsl = load_w(w_in_v, blk * 512)
            units = []
            for h in range(4):
                c = blk * 4 + h
                scs = sct[c % 2]
                S.dma("sp", "sct%d" % (c % 2), dmaf(scs[0:16, :, :], st_conv[:, :, c * 128:(c + 1) * 128]), writes=[("sct", c % 2)])
                S.group("pe", [mm(ps[4 + c % 4][:, j * 16:(j + 1) * 16], scs[0:16, j, :], ident_f[0:16, 0:16]) for j in range(3)],
                        reads=[("sct", c % 2), "ident_f"], writes=[PK[4 + c % 4]])
                S.op("dve", cp(histT[:, c, :, :], ps[4 + c % 4][:, 0:48].rearrange("p (j b) -> p j b", j=3)), reads=[PK[4 + c % 4]], writes=[("histT", c)])
                for si in range(5):
                    uo = qkv_unit(blk, h, si, ui, wsl)
                    if si == 0:
                        pre_ops = [(lambda j=j, c=c: S.op("pool", stt_pool(wdiag[c % 2][:, j, :], ident_f, wcol(j, c)), reads=["ident_f", "cols"], writes=[("wdiag", c % 2)])) for j in range(4)]
                        uo = pre_ops + uo
                    units.append(uo)
                    ui += 1
            for g0 in range(0, len(units), 4):
                zipper(units[g0:g0 + 4])

        S.barrier()
        XS = Bump(arena, X0, ARENA)
        S_all = XS.alloc(F32, [128, 64, 128])
        if 'sample' not in os.environ.get('KSKIP', ''):
            for q4 in range(4):
                S.dma("sp", "sall%d" % q4, dmaf(S_all[:, q4 * 16:(q4 + 1) * 16, :], st_gdn[q4 * 4:(q4 + 1) * 4].rearrange("b h d e -> d (b h) e")), writes=[("S_all", q4)])
        S.group("pe", [mm(ps[c // 4][0:3, (c % 4) * 128:(c % 4 + 1) * 128], ncp_st[:, c, :], ident_f) for c in range(12)],
                reads=[("ncp_st", c) for c in range(12)] + ["ident_f"], writes=[PK[0], PK[1], PK[2]])
        for q3 in range(3):
            S.op("dve", cp(stage_tok[0:3, q3 * 512:(q3 + 1) * 512], ps[q3][0:3, :]), reads=[PK[q3]], writes=["stage_tok"])
        S.dma("sp", "o_ncp", dmaf(ncp, stage_tok[0:3, :]), reads=["stage_tok"])
        S.group("pe", [mm(ps[c // 4][0:16, (c % 4) * 128:(c % 4 + 1) * 128], ncs_st[:, c, :], ident_f) for c in range(12)],
                reads=[("ncs_st", c) for c in range(12)] + ["ident_f"], writes=[PK[0], PK[1], PK[2]])
        for q3 in range(3):
            S.op("act", actf(stage2[0:16, q3 * 512:(q3 + 1) * 512], ps[q3][0:16, :], AF.Copy), reads=[PK[q3]], writes=["stage2"])
        S.dma("sp", "o_ncs", [dmaf(ncs[:, 2, :], stage2[0:16, :]), dmaf(ncs[:, 0:2, :], st_conv[:, 1:3, :])], reads=["stage2"])
        S.barrier()

        YG = Bump(arena, Y0, ARENA)
        osq = YG.alloc(BF16, [128, 512]); rn_o = YG.alloc(F32, [128, 512]); on_o = YG.alloc(F32, [128, 512])
        YG_EPI = YG.off
        NPW, DP = 4, 8
        CHDT = F32
        gN = lambda n_, dt_, shp: [YG.alloc(dt_, shp) for _ in range(n_)]
        Rs = gN(NPW, F32, [128, 256]); rhsR = gN(NPW, F32, [128, 256]); D0 = gN(NPW, F32, [128, 128]); E0 = gN(NPW, F32, [128, 128])
        EGr = gN(NPW, F32, [128, 128]); MB = gN(NPW, F32, [128, 128]); Qf = gN(NPW, F32, [128, 128]); Qs = gN(NPW, BF16, [128, 128])
        NNa = gN(NPW, CHDT, [128, 256]); NNb = gN(NPW, CHDT, [128, 256]); Xs = gN(NPW, BF16, [128, 128])
        NHa = gN(NPW, BF16, [128, 256]); NHb = gN(NPW, BF16, [128, 256])
        J0 = int(os.environ.get('GDN_J0', '6'))
        Qm = gN(DP, BF16, [128, 128]); attnT = gN(DP, BF16, [128, 128]); Kd = gN(DP, BF16, [128, 128]); Vb = gN(DP, BF16, [128, 128])
        qg = gN(DP, BF16, [128, 128]); nWT = gN(DP, BF16, [128, 128]); vn = gN(4, BF16, [128, 128])
        S.op("pool", lambda e: e.memset(S_f.rearrange("p h e -> p (h e)"), 0.0), writes=[("S_f", h) for h in range(4)])
        S.op("pool", lambda e: e.memset(S_b.rearrange("p h e -> p (h e)"), 0.0), writes=[("S_b", h) for h in range(4)])

        def gdn_P(n, h):
            ops = []
            Lop, Lgr, Ldma = mk_recorders(S, ops)
            u = n * 4 + h
            q = u % NPW; s = u % DP
            tok = slice(n * 128, (n + 1) * 128)
            kT = qkv[:, 4 + h, tok]; qT = qkv[:, h, tok]; vT = qkv[:, 8 + h, tok]
            col = lambda t: t[:, n, h:h + 1]
            K = lambda name: (name, q)
            H = lambda name: (name, s)
            bk = PK[q]; pb = ps[q]; pbb = psb[q]
            kk = pb[:, 256:384]; qk = pb[:, 384:512]
            Lop("pool", stt_pool(rhsR[q][:, 0:128], mask_incl, col(g_c)), reads=["mask_incl", "g_c"], writes=[K("rhsR")])
            Lop("pool", stt_pool(rhsR[q][:, 128:256], ident_f, col(beta_c)), reads=["ident_f", "beta_c"], writes=[K("rhsR")])
            Lgr("pe", [lambda e: e.transpose(out=pbb[:, 0:128], in_=kT, identity=ident_b),
                       lambda e: e.transpose(out=pbb[:, 128:256], in_=vT, identity=ident_b)], reads=[("qkv", n), "ident_b"], writes=[bk])
            ktok = pbb[:, 0:128]; vtok = pbb[:, 128:256]
            Lop("act", actf(Xs[q], ktok, AF.Copy, scale=col(bexp_c)), reads=[bk, "bexp_c"], writes=[K("Xs")])
            Lop("act", actf(Kd[s], ktok, AF.Copy, scale=col(kd_c)), reads=[bk, "kd_c"], writes=[H("Kd")])
            Lop("act", actf(Vb[s], vtok, AF.Copy, scale=col(beta_c)), reads=[bk, "beta_c"], writes=[H("Vb")])
            Lgr("pe", [mm(pb[:, 0:128], ones_f, rhsR[q][:, 0:128]), mm(pb[:, 128:256], ones_f, rhsR[q][:, 128:256]),
                       mm(kk, kT, kT), mm(qk, kT, qT)], reads=[K("rhsR"), "ones_f", ("qkv", n)], writes=[bk])
            Lop("dve", cp(Rs[q], pb[:, 0:256]), reads=[bk], writes=[K("Rs")])
            R_gc = Rs[q][:, 0:128]; R_be = Rs[q][:, 128:256]
            Lop("pool", lambda e: e.tensor_tensor(out=D0[q], in0=R_gc, in1=col(gc_c).to_broadcast([128, 128]), op=ALU.subtract), reads=[K("Rs"), "gc_c"], writes=[K("D0")])
            Lop("pool", ts(D0[q], D0[q], 0.0, ALU.min), reads=[K("D0")], writes=[K("D0")])
            Lop("act", actf(D0[q], D0[q], AF.Exp), reads=[K("D0")], writes=[K("D0")])
            Lop("act", actf(EGr[q], R_gc, AF.Exp), reads=[K("Rs")], writes=[K("EGr")])
            Lop("pool", tt(MB[q], R_be, D0[q], ALU.mult), reads=[K("Rs"), K("D0")], writes=[K("MB")])
            Lop("pool", tt(MB[q], MB[q], nmask_su, ALU.mult), reads=[K("MB"), "nmask_su"], writes=[K("MB")])
            Lop("pool", tt(D0[q], D0[q], mask_incl, ALU.mult), reads=[K("D0"), K("MB"), "mask_incl"], writes=[K("D0")])
            Lop("pool", tt(qg[s], qT, EGr[q], ALU.mult), reads=[("qkv", n), K("EGr")], writes=[H("qg")])
            Lop("dve", tt(NNa[q][:, 0:128], kk, MB[q], ALU.mult), reads=[bk, K("MB")], writes=[K("NNa")])
            Lop("dve", tt(attnT[s], qk, D0[q], ALU.mult), reads=[bk, K("D0")], writes=[H("attnT")])
            Lgr("pe", [mm(pb[:, 0:128], NNa[q][:, 0:128], ident_f)], reads=[K("NNa"), "ident_f"], writes=[bk])
            Lop("act", actf(NNa[q][:, 128:256], pb[:, 0:128], AF.Copy), reads=[bk], writes=[K("NNa")])
            Lop("pool", tt(Qf[q], ident_f, NNa[q][:, 0:128], ALU.add), reads=["ident_f", K("NNa")], writes=[K("Qf")])
            cur, nxt, kc, kn = NNa[q], NNb[q], K("NNa"), K("NNb")
            cur16, nxt16, kc16, kn16 = NHa[q], NHb[q], K("NHa"), K("NHb")
            if J0 == 0:
                Lop("pool", cp(cur16, cur), reads=[kc], writes=[kc16])
            for j in range(1, 7):
                f32lvl = j <= J0
                src, ksrc = (cur, kc) if f32lvl else (cur16, kc16)
                fns = []
                if j < 6:
                    fns.append(mm(pb[:, 0:128], src[:, 128:256], src[:, 0:128]))
                fns.append(mm(pb[:, 128:256], src[:, 0:128], src[:, 128:256]))
                Lgr("pe", fns, reads=[ksrc], writes=[bk])
                lo = 0 if j < 6 else 128
                if f32lvl:
                    Lop("act", actf(nxt[:, lo:256], pb[:, lo:256], AF.Copy), reads=[bk], writes=[kn])
                    if j == J0 and j < 6:
                        Lop("pool", cp(nxt16[:, lo:256], nxt[:, lo:256]), reads=[kn], writes=[kn16])
                    Lgr("pe", [mm(pb[:, 256:384], nxt[:, 128:256], Qf[q])], reads=[kn, K("Qf")], writes=[bk])
                else:
                    Lop("act", actf(nxt16[:, lo:256], pb[:, lo:256], AF.Copy), reads=[bk], writes=[kn16])
                    Lop("pool", cp(Qs[q], Qf[q]), reads=[K("Qf")], writes=[K("Qs")])
                    Lgr("pe", [mm(pb[:, 256:384], nxt16[:, 128:256], Qs[q])], reads=[kn16, K("Qs")], writes=[bk])
                Lop("dve", tt(Qf[q], Qf[q], pb[:, 256:384], ALU.add), reads=[bk, K("Qf")], writes=[K("Qf")])
                cur, nxt, kc, kn = nxt, cur, kn, kc
                cur16, nxt16, kc16, kn16 = nxt16, cur16, kn16, kc16
            Lop("pool", cp(Qm[s], Qf[q]), reads=[K("Qf")], writes=[H("Qm")])
            Lgr("pe", [mm(pb[:, 384:512], Xs[q], Qm[s])], reads=[K("Xs"), H("Qm")], writes=[bk])
            Lop("act", actf(nWT[s], pb[:, 384:512], AF.Copy, scale=negone), reads=[bk], writes=[H("nWT")])
            return ops

        def gdn_R(n, h):
            ops = []
            Lop, Lgr, Ldma = mk_recorders(S, ops)
            u = n * 4 + h
            s = u % DP
            H = lambda name: (name, s)
            col = lambda t: t[:, n, h:h + 1]
            bR = PK[4]; ob = 5 + n % 2
            V = ps[4][:, h * 128:(h + 1) * 128]
            Lgr("pe", [mm(V, Qm[s], Vb[s], start=True, stop=False),
                       mm(V, nWT[s], S_b[:, h, :], start=False, stop=True)],
                reads=[H("Qm"), H("Vb"), H("nWT"), ("S_b", h)], writes=[bR])
            Lop("dve", cp(vn[h], V), reads=[bR], writes=[("vn", h)])
            Lgr("pe", [mm(V, Kd[s], vn[h])], reads=[H("Kd"), ("vn", h)], writes=[bR])
            Lgr("pe", [mm(ps[ob][:, h * 128:(h + 1) * 128], S_b[:, h, :], qg[s], start=True, stop=False),
                       mm(ps[ob][:, h * 128:(h + 1) * 128], vn[h], attnT[s], start=False, stop=True)],
                reads=[("S_b", h), H("qg"), ("vn", h), H("attnT")], writes=[PK[ob]])
            Lop("dve", stt(S_f[:, h, :], S_f[:, h, :], col(egl_c), V, ALU.mult, ALU.add), reads=[bR, ("S_f", h), "egl_c"], writes=[("S_f", h)])
            Lop("act", actf(S_b[:, h, :], S_f[:, h, :], AF.Copy), reads=[("S_f", h)], writes=[("S_b", h)])
            return ops

        def gdn_epilogue(o_ps, ss_ps, ncol, t0, okeys, sskey):
            w = 4 * ncol
            S.op("act", actf(osq[:, 0:w], o_ps, AF.Square), reads=okeys, writes=["osq"])
            S.group("pe", [mm(ss_ps, ones_b, osq[:, 0:w])], reads=["osq", "ones_b"], writes=[sskey])
            S.op("dve", ts(rn_o[:, 0:w], ss_ps, 1.0 / 128, ALU.mult, EPS, ALU.add), reads=[sskey], writes=["rn_o"])
            S.op("act", actf(rn_o[:, 0:w], rn_o[:, 0:w], AF.Sqrt), reads=["rn_o"], writes=["rn_o"])
            S.op("dve", lambda e: e.reciprocal(out=rn_o[:, 0:w], in_=rn_o[:, 0:w]), reads=["rn_o"], writes=["rn_o"])
            S.op("dve", stt(on_o[:, 0:w], o_ps, gdnn_col, rn_o[:, 0:w], ALU.mult, ALU.mult), reads=okeys + ["rn_o", "cols"], writes=["on_o"])
            S.op("pool", tt(cat[:, 0:4, t0:t0 + ncol], on_o[:, 0:w].rearrange("p (h t) -> p h t", h=4), zs[:, :, t0:t0 + ncol], ALU.mult),
                 reads=["on_o", "zs"], writes=[("cat_o", t0)])

        _SK = os.environ.get('KSKIP', '')
        NCH = 0 if 'prompt' in _SK else int(os.environ.get('GDN_N', '16'))

        def epi_ops(n):
            ob = 5 + n % 2
            return [lambda: gdn_epilogue(ps[ob][:, :], ps[7][:, :], 128, n * 128, [PK[ob]], PK[7])]

        _DO_SAMPLE = 'sample' not in _SK
        def _sample_section():
            YG = Bump(arena, YG_EPI, ARENA)
            sv = YG.alloc(F32, [128, 8])
            rexp = YG.alloc(F32, [128, 8, 16])
            bcs = YG.alloc(F32, [128, 128])
            dcol = YG.alloc(F32, [128, 64])
            dtok = YG.alloc(F32, [128, 512]); ktoks = YG.alloc(F32, [128, 512])
            kmask = [YG.alloc(F32, [128, 512]) for _ in range(2)]
            S.op("dve", cp(sv[0:16, 0:4], beta_c[0:16, 16, :]), reads=["beta_c"], writes=["sv"])
            S.op("act", actf(sv[0:16, 4:8], g_c[0:16, 16, :], AF.Exp), reads=["g_c"], writes=["sv"])
            for j in range(8):
                S.op("dve", ts(rexp[0:16, j, :], ident_f[0:16, 0:16], sv[0:16, j:j + 1], ALU.mult), reads=["sv", "ident_f"], writes=["rexp"])
            S.group("pe", [mm(ps[0][:, 0:128], ones_f[0:16, :], rexp[0:16, :, :].rearrange("p j b -> p (j b)"))], reads=["rexp", "ones_f"], writes=[PK[0]])
            S.op("dve", cp(bcs, ps[0][:, 0:128]), reads=[PK[0]], writes=["bcs"])
            beta_bc = bcs[:, 0:64]; eg_bc = bcs[:, 64:128]
            S.group("pe", [mm(ps[1][:, h * 16 + b:h * 16 + b + 1], S_all[:, b * 4 + h, :], qks_f[:, 4 + h, b:b + 1]) for b in range(16) for h in range(4)],
                    reads=[("S_all", q4) for q4 in range(4)] + ["qks_f"], writes=[PK[1]])
            S.op("dve", tt(dcol, ps[1][:, 0:64], eg_bc, ALU.mult), reads=[PK[1], "bcs"], writes=["dcol"])
            S.op("dve", tt(dcol, qks_f[:, 8:12, :].rearrange("p h b -> p (h b)"), dcol, ALU.subtract), reads=["dcol", "qks_f"], writes=["dcol"])
            S.op("dve", tt(dcol, dcol, beta_bc, ALU.mult), reads=["dcol", "bcs"], writes=["dcol"])
            S.group("pe", [mm(ps[2][0:16, h * 128:(h + 1) * 128], dcol[:, h * 16:(h + 1) * 16], ident_f) for h in range(4)], reads=["dcol", "ident_f"], writes=[PK[2]])
            S.group("pe", [mm(ps[3][0:16, h * 128:(h + 1) * 128], qks_f[:, 4 + h, :], ident_f) for h in range(4)], reads=["qks_f", "ident_f"], writes=[PK[3]])
            S.op("dve", cp(dtok[0:16, :], ps[2][0:16, :]), reads=[PK[2]], writes=["dtok"])
            S.op("act", actf(ktoks[0:16, :], ps[3][0:16, :], AF.Copy), reads=[PK[3]], writes=["ktoks"])
            for b in range(16):
                km = kmask[b % 2]; pb = 4 + b % 2
                S.op("dve", ts(km[0:16, :], ktoks[0:16, :], ident_f[0:16, b:b + 1], ALU.mult), reads=["ktoks", "ident_f"], writes=[("kmask", b % 2)])
                S.group("pe", [mm(ps[pb][:, h * 128:(h + 1) * 128], km[0:16, h * 128:(h + 1) * 128], dtok[0:16, h * 128:(h + 1) * 128]) for h in range(4)],
                        reads=[("kmask", b % 2), "dtok"], writes=[PK[pb]])
                for h in range(4):
                    S.op("dve", stt(S_all[:, b * 4 + h, :], S_all[:, b * 4 + h, :], eg_bc[:, h * 16 + b:h * 16 + b + 1], ps[pb][:, h * 128:(h + 1) * 128], ALU.mult, ALU.add),
                         reads=[PK[pb], "bcs", ("S_all", b // 4)], writes=[("S_all", b // 4)])
            S.group("pe", [mm(ps[1][:, 64 + h * 16 + b:64 + h * 16 + b + 1], S_all[:, b * 4 + h, :], qks_f[:, h, b:b + 1]) for b in range(16) for h in range(4)],
                    reads=[("S_all", q4) for q4 in range(4)] + ["qks_f"], writes=[PK[1]])
            gdn_epilogue(ps[1][:, 64:128], ps[0][:, 128:192], 16, T_P, [PK[1]], PK[0])
            for q4 in range(4):
                S.dma("sp", "o_ngs%d" % q4, dmaf(ngs[q4 * 4:(q4 + 1) * 4].rearrange("b h d e -> d (b h) e"), S_all[:, q4 * 16:(q4 + 1) * 16, :]), reads=[("S_all", q4)])
        if _DO_SAMPLE:
            _sample_section()
        S.barrier()

        LB = Bump(arena, X0, ARENA)
        NL = 8
        lf32 = lambda shp: [LB.alloc(F32, shp) for _ in range(NL)]
        lbf = lambda shp: [LB.alloc(BF16, shp) for _ in range(NL)]
        Rs8 = lf32([128, 256]); rhsR8 = lf32([128, 256]); D08 = lf32([128, 128]); EGr8 = lf32([128, 128]); MB8 = lf32([128, 128]); Qf8 = lf32([128, 128])
        NNa8 = lf32([128, 256]); NNb8 = lf32([128, 256]); rn8 = lf32([128, 128]); on8 = lf32([128, 128])
        Xs8 = lbf([128, 128]); Kd8 = lbf([128, 128]); Vb8 = lbf([128, 128]); qg8 = lbf([128, 128]); at8 = lbf([128, 128])
        Qm8 = lbf([128, 128]); nWT8 = lbf([128, 128]); vn8 = lbf([128, 128]); osq8 = lbf([128, 128])
        r_done = {}

        def gdn_unit(n, h):
            ops = []
            Lop, Lgr, Ldma = mk_recorders(S, ops)
            L = h * 2 + n % 2
            tok = slice(n * 128, (n + 1) * 128)
            kT = qkv[:, 4 + h, tok]; qT = qkv[:, h, tok]; vT = qkv[:, 8 + h, tok]
            col = lambda t: t[:, n, h:h + 1]
            K = lambda name: (name, L)
            bk = PK[L]; pb = ps[L]; pbb = psb[L]
            kk = pb[:, 256:384]; qk = pb[:, 384:512]
            Rs, rhsR, D0, EGr, MB, Qf = Rs8[L], rhsR8[L], D08[L], EGr8[L], MB8[L], Qf8[L]
            Xs, Kd, Vb, qg, attnT, Qm, nWT, vn, osq = Xs8[L], Kd8[L], Vb8[L], qg8[L], at8[L], Qm8[L], nWT8[L], vn8[L], osq8[L]
            Lop("pool", stt_pool(rhsR[:, 0:128], mask_incl, col(g_c)), reads=["mask_incl", "g_c"], writes=[K("rhsR")])
            Lop("pool", stt_pool(rhsR[:, 128:256], ident_f, col(beta_c)), reads=["ident_f", "beta_c"], writes=[K("rhsR")])
            Lgr("pe", [mm(pb[:, 0:128], mask_sl, rhsR[:, 0:128]), mm(pb[:, 128:256], ones_f, rhsR[:, 0:128]),
                       mm(pb[:, 256:384], nmask_sl, rhsR[:, 128:256]),
                       lambda e: e.transpose(out=pbb[:, 768:896], in_=kT, identity=ident_b),
                       lambda e: e.transpose(out=pbb[:, 896:1024], in_=vT, identity=ident_b)],
                reads=[K("rhsR"), "ones_f", "mask_sl", "nmask_sl", "ident_b"], writes=[bk])
            ktok = pbb[:, 768:896]; vtok = pbb[:, 896:1024]
            Lop("act", actf(Rs, pb[:, 0:256], AF.Exp), reads=[bk], writes=[K("Rs")])
            D0 = Rs[:, 0:128]; EGr = Rs[:, 128:256]
            Lop("act", actf(Xs, ktok, AF.Copy, scale=col(bexp_c)), reads=[bk, "bexp_c"], writes=[K("Xs")])
            Lop("act", actf(Kd, ktok, AF.Copy, scale=col(kd_c)), reads=[bk, "kd_c"], writes=[K("Kd")])
            Lop("act", actf(Vb, vtok, AF.Copy, scale=col(beta_c)), reads=[bk, "beta_c"], writes=[K("Vb")])
            Lop("act", actf(MB, pb[:, 256:384], AF.Copy), reads=[bk], writes=[K("MB")])
            Lop("pool", tt(MB, MB, D0, ALU.mult), reads=[K("MB"), K("Rs")], writes=[K("MB")])
            Lop("pool", tt(qg, qT, EGr, ALU.mult), reads=[K("Rs")], writes=[K("qg")])
            Lgr("pe", [mm(pb[:, 0:128], kT, kT), mm(pb[:, 128:256], kT, qT)], reads=[], writes=[bk])
            kk = pb[:, 0:128]; qk = pb[:, 128:256]
            Lop("pool", tt(D0, D0, mask_incl, ALU.mult), reads=[K("Rs"), K("MB"), K("qg"), "mask_incl"], writes=[K("Rs")])
            NNa, NNb = NNa8[L], NNb8[L]
            Lop("dve", tt(NNa[:, 0:128], kk, MB, ALU.mult), reads=[bk, K("MB")], writes=[K("NNa")])
            Lop("dve", tt(attnT, qk, D0, ALU.mult), reads=[bk, K("Rs")], writes=[K("attnT")])
            Lgr("pe", [mm(pb[:, 256:384], NNa[:, 0:128], ident_f)], reads=[K("NNa"), "ident_f"], writes=[bk])
            Lop("act", actf(NNa[:, 128:256], pb[:, 256:384], AF.Copy), reads=[bk], writes=[K("NNa")])
            Lop("pool", tt(Qf, ident_f, NNa[:, 0:128], ALU.add), reads=["ident_f", K("NNa")], writes=[K("Qf")])
            cur, nxt, kc, kn = NNa, NNb, K("NNa"), K("NNb")
            for j in range(1, 7):
                fns = []
                if j < 6:
                    fns.append(mm(pb[:, 0:128], cur[:, 128:256], cur[:, 0:128]))
                fns.append(mm(pb[:, 128:256], cur[:, 0:128], cur[:, 128:256]))
                Lgr("pe", fns, reads=[kc], writes=[bk])
                lo = 0 if j < 6 else 128
                if j in (3, 5):
                    Lop("dve", cp(nxt[:, lo:256], pb[:, lo:256]), reads=[bk], writes=[kn])
                else:
                    Lop("act", actf(nxt[:, lo:256], pb[:, lo:256], AF.Copy), reads=[bk], writes=[kn])
                Lgr("pe", [mm(pb[:, 256:384], nxt[:, 128:256], Qf)], reads=[kn, K("Qf")], writes=[bk])
                Lop("dve", tt(Qf, Qf, pb[:, 256:384], ALU.add), reads=[bk, K("Qf")], writes=[K("Qf")])
                cur, nxt, kc, kn = nxt, cur, kn, kc
            Lop("pool", cp(Qm, Qf), reads=[K("Qf")], writes=[K("Qm")])
            Lgr("pe", [mm(pb[:, 384:512], Xs, Qm)], reads=[K("Xs"), K("Qm")], writes=[bk])
            Lop("act", actf(nWT, pb[:, 384:512], AF.Copy, scale=negone), reads=[bk], writes=[K("nWT")])
            V = pb[:, 384:512]; Oh = pb[:, 0:128]; SSh = pb[:, 128:256]

            def chk():
                assert n == 0 or r_done.get((n - 1, h)), ("emission order violated", n, h)
            ops.append(chk)
            Lgr("pe", [mm(V, Qm, Vb, start=True, stop=False), mm(V, nWT, S_b[:, h, :], start=False, stop=True)],
                reads=[K("Qm"), K("Vb"), K("nWT"), ("S_b", h)], writes=[bk])
            Lop("dve", cp(vn, V), reads=[bk], writes=[K("vn")])
            Lgr("pe", [mm(V, Kd, vn),
                       mm(Oh, S_b[:, h, :], qg, start=True, stop=False), mm(Oh, vn, attnT, start=False, stop=True)],
                reads=[K("Kd"), K("vn"), ("S_b", h), K("qg"), K("attnT")], writes=[bk])
            Lop("dve", stt(S_f[:, h, :], S_f[:, h, :], col(egl_c), V, ALU.mult, ALU.add), reads=[bk, ("S_f", h), "egl_c"], writes=[("S_f", h)])
            Lop("pool", cp(S_b[:, h, :], S_f[:, h, :]), reads=[("S_f", h)], writes=[("S_b", h)])

            def mark():
                r_done[(n, h)] = True
            ops.append(mark)
            rn, on = rn8[L], on8[L]
            Lop("act", actf(osq, Oh, AF.Square), reads=[bk], writes=[K("osq")])
            Lgr("pe", [mm(SSh, ones_b, osq)], reads=[K("osq"), "ones_b"], writes=[bk])
            Lop("dve", ts(rn, SSh, 1.0 / 128, ALU.mult, EPS, ALU.add), reads=[bk], writes=[K("rn")])
            Lop("act", actf(rn, rn, AF.Sqrt), reads=[K("rn")], writes=[K("rn")])
            Lop("dve", lambda e: e.reciprocal(out=rn, in_=rn), reads=[K("rn")], writes=[K("rn")])
            Lop("dve", stt(on, Oh, gdnn_col, rn, ALU.mult, ALU.mult), reads=[bk, K("rn"), "cols"], writes=[K("on")])
            Lop("pool", tt(cat[:, h, tok], on, zs[:, h, tok], ALU.mult), reads=[K("on")], writes=[("cat_o", n, h)])
            return ops

        if NCH:
            u0 = gdn_unit(0, 0)
            LU = len(u0)
            STAG8 = int(os.environ.get('GDN_STAG', '7'))
            lanes = []
            for h in range(4):
                for par in range(2):
                    pad = h * STAG8 + par * (LU // 2)
                    lane = [(lambda: None)] * pad
                    for n in range(par, NCH, 2):
                        lane = lane + gdn_unit(n, h)
                    lanes.append(lane)
            zipper(lanes)
        S.dma("sp", "o_ngp", dmaf(ngp.rearrange("h d e -> d h e"), S_f), reads=[("S_f", h) for h in range(4)])

        S.barrier()

        if 'phasec' in _SK:
            S.finish()
            with nc.Block() as block:
                S.replay(block)
            return nc
        YC = Bump(arena, P_C0, ARENA)
        R = YC.alloc(F32, [128, 8, 528]); xnC = YC.alloc(BF16, [128, 8, 528]); hid = YC.alloc(BF16, [128, 32, 528])
        r8 = [YC.alloc(BF16, [128, 8, 512]) for _ in range(3)]
        r16 = [YC.alloc(BF16, [128, 32, 256]) for _ in range(2)]
        xres = YC.alloc(F32, [128, 4, 1024]); xres_s = YC.alloc(F32, [128, 1024])
        pw = YC.alloc(BF16, [128, 2, 1024]); ptok = [YC.alloc(BF16, [128, 256]) for _ in range(2)]
        pT = YC.alloc(BF16, [128, 2, 528]); sqr = [YC.alloc(BF16, [128, 528]) for _ in range(2)]; rnC = YC.alloc(F32, [128, 528])
        sig = [YC.alloc(F32, [128, 528]) for _ in range(2)]; relu_t = [YC.alloc(F32, [128, 528]) for _ in range(2)]
        ytile = [YC.alloc(F32, [128, 1024]) for _ in range(1)]
        rncol = YC.alloc(F32, [128, 8])
        r8_n = [0]; r16_n = [0]; misc_n = [0]

        r8_seq = []
        for _p in range(4):
            r8_seq += [(w_out_v, 0), (w_out_v, 512)] + [(w_up_v, bb * 512) for bb in range(8)] + [(w_gate_v, 0), (w_gate_v, 512)]
        r8_issued = [0]

        def load_r8(view, c0):
            idx = r8_n[0]
            r8_n[0] += 1
            assert r8_seq[idx][1] == c0
            while r8_issued[0] < min(len(r8_seq), idx + 3):
                j = r8_issued[0]
                vw, cc = r8_seq[j]
                if not (os.environ.get("NORELOAD") and j >= 12):
                    S.dma("pool", "r8_%d" % (j % 3), dmaf(r8[j % 3], vw[:, :, cc:cc + 512]), writes=[("r8", j % 3)])
                r8_issued[0] += 1
            return idx % 3

        def load_r16(c0):
            sl = r16_n[0] % 2
            r16_n[0] += 1
            if not (os.environ.get("NORELOAD") and r16_n[0] > 4):
                S.dma("pool", "r16_%d" % sl, dmaf(r16[sl], w_down_v[:, :, c0:c0 + 256]), writes=[("r16", sl)])
            return sl

        S.dma("pool", "pw", dmaf(pw, w_ple_v), writes=["pw"])
        PASSES = [[(0, 512, 0)], [(512, 512, 0)], [(1024, 512, 0)], [(1536, 512, 0), (2048, 16, 512)]]

        def rms_norm_C(which, out_fn, segs, W, tag):
            bns = []
            for (t0, n, l0) in segs:
                bns.append(6 + misc_n[0] % 2)
                misc_n[0] += 1
            for m in range(8):
                sq = sqr[m % 2]
                S.op("act", actf(sq[:, 0:W], R[:, m, 0:W], AF.Square), reads=[("R", m)], writes=[("sqr", m % 2)])
                for si_, (t0, n, l0) in enumerate(segs):
                    bn = bns[si_]
                    S.group("pe", [mm(ps[bn][:, 0:n], ones_b, sq[:, l0:l0 + n], start=(m == 0), stop=(m == 7))],
                            reads=[("sqr", m % 2), "ones_b"], writes=[PK[bn]])
            for si_, (t0, n, l0) in enumerate(segs):
                bn = bns[si_]
                S.op("dve", ts(rnC[:, l0:l0 + n], ps[bn][:, 0:n], 1.0 / 1024, ALU.mult, EPS, ALU.add), reads=[PK[bn]], writes=["rnC"])
            S.op("act", actf(rnC[:, 0:W], rnC[:, 0:W], AF.Sqrt), reads=["rnC"], writes=["rnC"])
            S.op("dve", lambda e: e.reciprocal(out=rnC[:, 0:W], in_=rnC[:, 0:W]), reads=["rnC"], writes=["rnC"])
            for m in range(8):
                out_ap, wkey = out_fn(m)
                S.op("dve", stt(out_ap, R[:, m, 0:W], gcol(which, m), rnC[:, 0:W], ALU.mult, ALU.mult), reads=[("R", m), "rnC", "cols"], writes=[wkey])

        for pi, segs in enumerate(PASSES):
            W = sum(n for (_, n, _) in segs)
            t00 = segs[0][0]
            has_s = len(segs) > 1
            if pi == 0:
                S.dma("sp", "xres", dmaf(xres, x_p[0:512, :].rearrange("(j p) f -> p j f", p=128)), writes=["xres"])
            def stats_act(m):
                S.op("act", actf(sqr[m % 2][:, 0:W], R[:, m, 0:W], AF.Square), reads=[("R", m)], writes=[("sqr", m % 2)])

            def stats_pe(m, bns):
                for si_, (t0, n, l0) in enumerate(segs):
                    S.group("pe", [mm(ps[bns[si_]][:, 0:n], ones_b, sqr[m % 2][:, l0:l0 + n], start=(m == 0), stop=(m == 7))],
                            reads=[("sqr", m % 2), "ones_b"], writes=[PK[bns[si_]]])

            def norm_finish_row(bns, out_t, key, square):
                for si_, (t0, n, l0) in enumerate(segs):
                    S.op("dve", ts(out_t[:, l0:l0 + n], ps[bns[si_]][:, 0:n], 1.0 / 1024, ALU.mult, EPS, ALU.add), reads=[PK[bns[si_]]], writes=[key])
                if not square:
                    S.op("act", actf(out_t[:, 0:W], out_t[:, 0:W], AF.Sqrt), reads=[key], writes=[key])
                S.op("dve", lambda e: e.reciprocal(out=out_t[:, 0:W], in_=out_t[:, 0:W]), reads=[key], writes=[key])

            def pick_bns():
                o = []
                for _ in segs:
                    o.append(6 + misc_n[0] % 2)
                    misc_n[0] += 1
                return o

            bns1 = pick_bns()
            for blk in range(2):
                sl = load_r8(w_out_v, blk * 512)
                for m4 in range(4):
                    m = blk * 4 + m4
                    for (t0, n, l0) in segs:
                        b = next_bank()
                        fns = [mm(ps[b][:, 0:n], r8[sl][:, k, m4 * 128:(m4 + 1) * 128], cat[:, k, t0:t0 + n], start=(k == 0), stop=False) for k in range(8)]
                        if n == 512:
                            fns += [mm(ps[b][:, j * 128:(j + 1) * 128], xres[:, j, m * 128:(m + 1) * 128], ident_f, start=False, stop=(j == 3)) for j in range(4)]
                            rk = ["xres"]
                        else:
                            fns += [mm(ps[b][:, 0:16], xres_s[0:16, m * 128:(m + 1) * 128], ident_f[0:16, 0:16], start=False, stop=True)]
                            rk = ["xres_s"]
                        S.group("pe", fns, reads=[("r8", sl), "cat", "ident_f"] + rk, writes=[PK[b]])
                        S.op("act", actf(R[:, m, l0:l0 + n], ps[b][:, 0:n], AF.Copy), reads=[PK[b]], writes=[("R", m)])
                        S.op("act", actf(xnC[:, m, l0:l0 + n], ps[b][:, 0:n], AF.Copy, scale=gcol(0, m)), reads=[PK[b], "cols"], writes=[("xnC", m)])
                    stats_act(m)
                    if m >= 1:
                        stats_pe(m - 1, bns1)
            stats_pe(7, bns1)
            norm_finish_row(bns1, rnC, "rnC", True)
            if pi + 1 < len(PASSES):
                tn = PASSES[pi + 1][0][0]
                S.dma("sp", "xres", dmaf(xres, x_p[tn:tn + 512, :].rearrange("(j p) f -> p j f", p=128)), writes=["xres"])
                if len(PASSES[pi + 1]) > 1:
                    S.dma("sp", "xres_s", dmaf(xres_s[0:16, :], x_s), writes=["xres_s"])
            for blk in range(8):
                sl = load_r8(w_up_v, blk * 512)
                for m4 in range(4):
                    hc = blk * 4 + m4
                    for (t0, n, l0) in segs:
                        b = next_bank()
                        S.group("pe", [mm(ps[b][:, 0:n], r8[sl][:, k, m4 * 128:(m4 + 1) * 128], xnC[:, k, l0:l0 + n], start=(k == 0), stop=(k == 7)) for k in range(8)],
                                reads=[("r8", sl)] + [("xnC", k) for k in range(8)], writes=[PK[b]])
                        rt = relu_t[misc_n[0] % 2]; rkey = ("relu_t", misc_n[0] % 2)
                        misc_n[0] += 1
                        S.op("act", actf(rt[:, 0:n], ps[b][:, 0:n], AF.Relu), reads=[PK[b]], writes=[rkey])
                        S.op("dve", tt(hid[:, hc, l0:l0 + n], rt[:, 0:n], rt[:, 0:n], ALU.mult), reads=[rkey], writes=[("hid", hc)])
            bns2 = pick_bns()
            for blk in range(4):
                sl = load_r16(blk * 256)
                for m2 in range(2):
                    m = blk * 2 + m2
                    for (t0, n, l0) in segs:
                        b = next_bank()
                        S.group("pe", [mm(ps[b][:, 0:n], r16[sl][:, k, m2 * 128:(m2 + 1) * 128], hid[:, k, l0:l0 + n], start=(k == 0), stop=(k == 31)) for k in range(32)],
                                reads=[("r16", sl)] + [("hid", k) for k in range(32)], writes=[PK[b]])
                        sg = sig[misc_n[0] % 2]; skey = ("sig", misc_n[0] % 2)
                        misc_n[0] += 1
                        S.op("dve", tt(sg[:, 0:n], ps[b][:, 0:n], rnC[:, l0:l0 + n], ALU.mult), reads=[PK[b], "rnC"], writes=[skey])
                        S.op("dve", tt(R[:, m, l0:l0 + n], R[:, m, l0:l0 + n], sg[:, 0:n], ALU.add), reads=[skey, ("R", m)], writes=[("R", m)])
                    S.op("act", actf(xnC[:, m, 0:W], R[:, m, 0:W], AF.Copy, scale=gcol(1, m)), reads=[("R", m), "cols"], writes=[("xnC", m)])
                    stats_act(m)
                    if m >= 1:
                        stats_pe(m - 1, bns2)
            stats_pe(7, bns2)
            norm_finish_row(bns2, rnC, "rnC", False)
            for (t0, n, l0) in segs:
                ntile = (n + 127) // 128
                for j in range(ntile):
                    r = min(128, n - j * 128)
                    sl = misc_n[0] % 2
                    misc_n[0] += 1
                    src = p_p[t0 + j * 128:t0 + j * 128 + r, :] if n == 512 else p_s
                    S.dma("pool", "ptok%d" % sl, dmaf(ptok[sl][0:r, :], src), writes=[("ptok", sl)])
                    S.group("pe", [lambda e, kk=kk, sl=sl, r=r: e.transpose(out=psb[5][:, kk * 128:kk * 128 + r], in_=ptok[sl][0:r, kk * 128:(kk + 1) * 128], identity=ident_b[0:r, 0:r]) for kk in range(2)],
                            reads=[("ptok", sl), "ident_b"], writes=[PK[5]])
                    S.op("act", actf(pT[:, :, l0 + j * 128:l0 + j * 128 + r], psb[5][:, 0:256].rearrange("p (k t) -> p k t", k=2)[:, :, 0:r], AF.Copy), reads=[PK[5]], writes=["pT"])
            ntt = sum((n + 127) // 128 for (_, n, _) in segs)
            sigbufs = [(sig[0], ("sig", 0)), (sig[1], ("sig", 1)), (relu_t[0], ("relu_t", 0)), (relu_t[1], ("relu_t", 1))]

            def gate_chunk(m, sl, m4):
                ops = []
                Lop, Lgr, Ldma = mk_recorders(S, ops)
                for si_, (t0, n, l0) in enumerate(segs):
                    b = (2 * m + si_) % 4
                    pb_ = 6 + m % 2
                    sg, skey = sigbufs[(2 * m + si_) % 4]
                    Lgr("pe", [mm(ps[b][:, 0:n], r8[sl][:, k, m4 * 128:(m4 + 1) * 128], xnC[:, k, l0:l0 + n], start=(k == 0), stop=(k == 7)) for k in range(8)],
                        reads=[("r8", sl)] + [("xnC", k) for k in range(8)], writes=[PK[b]])
                    Lgr("pe", [mm(ps[pb_][:, 0:n], pw[:, kk, m * 128:(m + 1) * 128], pT[:, kk, l0:l0 + n], start=(kk == 0), stop=(kk == 1)) for kk in range(2)],
                        reads=["pw", "pT"], writes=[PK[pb_]])
                    Lop("dve", tt(sg[:, 0:n], ps[b][:, 0:n], rnC[:, l0:l0 + n], ALU.mult), reads=[PK[b], "rnC"], writes=[skey])
                    Lop("act", actf(sg[:, 0:n], sg[:, 0:n], AF.Sigmoid), reads=[skey], writes=[skey])
                    Lop("dve", tt(sg[:, 0:n], sg[:, 0:n], ps[pb_][:, 0:n], ALU.mult), reads=[PK[pb_], skey], writes=[skey])
                    Lop("dve", tt(R[:, m, l0:l0 + n], R[:, m, l0:l0 + n], sg[:, 0:n], ALU.add), reads=[skey, ("R", m)], writes=[("R", m)])
                sq = sqr[m % 2]
                Lop("act", actf(sq[:, 0:W], R[:, m, 0:W], AF.Square), reads=[("R", m)], writes=[("sqr", m % 2)])
                fns = []
                if m == 0:
                    fns.append(mm(ps[5][:, 256:256 + ntt], zeros_f, zeros_f[:, 0:ntt], start=True, stop=False))
                jt = 0
                for (t0, n, l0) in segs:
                    for j in range((n + 127) // 128):
                        r = min(128, n - j * 128)
                        fns.append(mm(ps[5][0:r, 256 + jt:257 + jt], sq[:, l0 + j * 128:l0 + j * 128 + r], ones_b[:, 0:1], start=False, stop=False))
                        jt += 1
                if m == 7:
                    fns.append(mm(ps[5][:, 256:256 + ntt], zeros_f, zeros_f[:, 0:ntt], start=False, stop=True))
                Lgr("pe", fns, reads=[("sqr", m % 2), "ones_b"], writes=[PK[5]])
                Lop("act", actf(R[:, m, 0:W], R[:, m, 0:W], AF.Copy, scale=gcol(2, m)), reads=[("R", m), ("sqr", m % 2), "cols"], writes=[("R", m)])
                return ops

            for blk in range(2):
                sl = load_r8(w_gate_v, blk * 512)
                chunks = [gate_chunk(blk * 4 + m4, sl, m4) for m4 in range(4)]
                zipper(chunks[0:2])
                zipper(chunks[2:4])
            S.op("dve", ts(rncol[:, 0:ntt], ps[5][:, 256:256 + ntt], 1.0 / 1024, ALU.mult, EPS, ALU.add), reads=[PK[5]], writes=["rncol"])
            S.op("act", actf(rncol[:, 0:ntt], rncol[:, 0:ntt], AF.Sqrt), reads=["rncol"], writes=["rncol"])
            S.op("dve", lambda e: e.reciprocal(out=rncol[:, 0:ntt], in_=rncol[:, 0:ntt]), reads=["rncol"], writes=["rncol"])
            jt = 0
            for (t0, n, l0) in segs:
                ntile = (n + 127) // 128
                for j in range(ntile):
                    r = min(128, n - j * 128)
                    ysl = 0
                    for half in range(2):
                        b = next_bank()
                        S.group("pe", [(lambda e, m4=m4, b=b, r=r, half=half, l0=l0, j=j: e.transpose(out=ps[b][0:r, m4 * 128:(m4 + 1) * 128], in_=R[:, half * 4 + m4, l0 + j * 128:l0 + j * 128 + r], identity=ident_f)) for m4 in range(4)],
                                reads=[("R", half * 4 + m4) for m4 in range(4)] + ["ident_f"], writes=[PK[b]])
                        S.op("act", actf(ytile[ysl][0:r, half * 512:(half + 1) * 512], ps[b][0:r, :], AF.Copy, scale=rncol[0:r, jt:jt + 1]), reads=[PK[b], "rncol"], writes=[("ytile", ysl)])
                    jt += 1
                    dst = y_p[t0 + j * 128:t0 + j * 128 + r, :] if n == 512 else y_s
                    S.dma("sp", "o_y%d" % ysl, dmaf(dst, ytile[ysl][0:r, :]), reads=[("ytile", ysl)])
        S.finish()
        with nc.Block() as block:
            S.replay(block)
    return nc


_PROG = {}


def _make_in_maps(inputs):
    f = lambda a: np.ascontiguousarray(np.asarray(a, dtype=np.float32))
    g = {k: f(v) for k, v in inputs.items()}
    shared = {
        "g_mix": g["g_mix"].reshape(1, 1024), "w_in": g["w_in"][0], "w_conv": g["w_conv"][0],
        "a_log": g["a_log"].reshape(1, 4), "dt_bias": g["dt_bias"].reshape(1, 4), "gdn_norm": g["gdn_norm"].reshape(1, 128),
        "ln_g": g["sgu_ln_g"].reshape(1, 512), "ln_b": g["sgu_ln_b"].reshape(1, 512), "w_s": g["w_s"][0],
        "b_s": g["b_s"].reshape(1, 512), "w_out": g["w_out"][0], "g_ff": g["g_ff"].reshape(8, 128), "w_up": g["w_up"][0],
        "w_down": g["w_down"][0], "g_ple": g["g_ple"].reshape(8, 128), "w_ple": g["w_ple"][0], "w_gate": g["w_ple_gate"][0],
        "g_fin": g["g_final"].reshape(8, 128),
    }
    maps = []
    for i in range(8):
        m = dict(shared)
        sl = slice(16 * i, 16 * i + 16)
        m["x_p"] = g["x_prompt"][i]
        m["x_s"] = g["x_sample"][sl, 0]
        m["st_conv"] = g["state_conv"][0, sl]
        m["st_gdn"] = g["state_gdn"][0, sl]
        m["p_p"] = g["p_prompt"][0, i]
        m["p_s"] = g["p_sample"][0, sl, 0]
        maps.append(m)
    return maps


def kernel(**inputs):
    if "nc" not in _PROG:
        _PROG["nc"] = build_program()
    nc = _PROG["nc"]
    maps = _make_in_maps(inputs)
    res = run_bass_kernel_spmd(nc, maps, core_ids=list(range(8)))
    R = res.results
    st = lambda name: np.stack([np.asarray(r[name], dtype=np.float32) for r in R])
    cc = lambda name: np.concatenate([np.asarray(r[name], dtype=np.float32) for r in R], axis=0)
    y_prompt = st("y_p")
    y_sample = cc("y_s")[:, None, :]
    new_conv_prompt = st("ncp")[None]
    new_gdn_prompt = st("ngp")[None]
    new_conv_sample = cc("ncs")[None]
    new_gdn_sample = cc("ngs")[None]
    new_sgu_v_sample = cc("nsv")[None, :, None, :]
    return (y_prompt, y_sample, new_conv_prompt, new_gdn_prompt, new_conv_sample, new_gdn_sample, new_sgu_v_sample)
```

```python
import os
import numpy as np
import concourse.bass as bass
import concourse.mybir as mybir
from concourse.bass_utils import run_bass_kernel_spmd

F32 = mybir.dt.float32
BF16 = mybir.dt.bfloat16
AF = mybir.ActivationFunctionType
ALU = mybir.AluOpType
AX = mybir.AxisListType


class Sched:
    ENGS = ("pe", "act", "dve", "pool", "sp")

    def __init__(self, nc, stack):
        self.nc = nc
        self.stack = stack
        self.streams = {e: [] for e in self.ENGS}
        self.esem = {e: stack.enter_context(nc.semaphore("c_" + e)) for e in self.ENGS[:4]}
        self.ecnt = {e: 0 for e in self.ENGS}
        self.waited = {e: {} for e in self.ENGS}
        self.res = {}
        self.dsem = {}
        self.sem_by_name = {}
        for e in self.ENGS[:4]:
            self.sem_by_name[self.esem[e].name] = self.esem[e]

    def _need(self, eng, ev, waits):
        if ev is None:
            return
        name, val, src = ev
        if src == eng and eng == "pe":
            return
        cur = waits.get(name, 0)
        if val > cur:
            waits[name] = val

    def _deps(self, eng, reads, writes):
        waits = {}
        for k in reads:
            r = self.res.get(k)
            if r is not None:
                self._need(eng, r[0], waits)
                if isinstance(k, tuple) and k and k[0] == "ps":
                    for ev in r[1]:
                        if ev[2] != eng:
                            self._need(eng, ev, waits)
        for k in writes:
            r = self.res.get(k)
            if r is not None:
                if r[0] is not None and not (r[0][2] == eng):
                    self._need(eng, r[0], waits)
                for ev in r[1]:
                    self._need(eng, ev, waits)
        out = []
        w = self.waited[eng]
        for name, val in waits.items():
            if w.get(name, 0) < val:
                w[name] = val
                out.append((name, val))
        return out

    def _commit(self, ev, reads, writes):
        for k in reads:
            r = self.res.setdefault(k, [None, []])
            r[1].append(ev)
        for k in writes:
            self.res[k] = [ev, []]

    def op(self, eng, fn, reads=(), writes=()):
        waits = self._deps(eng, reads, writes)
        self.ecnt[eng] += 1
        ev = (self.esem[eng].name, self.ecnt[eng], eng)
        self.streams[eng].append((waits, [fn], ("inc", self.esem[eng], 1)))
        self._commit(ev, reads, writes)
        return ev

    def group(self, eng, fns, reads=(), writes=()):
        waits = self._deps(eng, reads, writes)
        self.ecnt[eng] += 1
        ev = (self.esem[eng].name, self.ecnt[eng], eng)
        self.streams[eng].append((waits, list(fns), ("inc", self.esem[eng], 1)))
        self._commit(ev, reads, writes)
        return ev

    def dma(self, eng, slot, fn, reads=(), writes=(), n=1):
        if slot not in self.dsem:
            s = self.stack.enter_context(self.nc.semaphore("d_" + slot))
            self.dsem[slot] = [s, 0]
            self.sem_by_name[s.name] = s
        waits = self._deps(eng, reads, writes)
        d = self.dsem[slot]
        fns = fn if isinstance(fn, (list, tuple)) else [fn]
        d[1] += 16 * len(fns)
        ev = (d[0].name, d[1], "dma")
        self.streams[eng].append((waits, list(fns), ("dmainc", d[0], 16)))
        self._commit(ev, reads, writes)
        return ev

    def barrier(self, skip=()):
        evs = []
        for e in self.ENGS[:4]:
            if self.ecnt[e] > 0:
                evs.append((self.esem[e].name, self.ecnt[e]))
        for slot, (s, c) in self.dsem.items():
            if c > 0 and not any(slot.startswith(p) for p in skip):
                evs.append((s.name, c))
        for eng in self.ENGS:
            w = self.waited[eng]
            waits = []
            for name, val in evs:
                if w.get(name, 0) < val:
                    w[name] = val
                    waits.append((name, val))
            if waits:
                self.streams[eng].append((waits, [], None))
        self.res.clear()

    def finish(self):
        eng = "sp"
        waits = []
        for slot, (s, c) in self.dsem.items():
            if c > 0:
                waits.append((s.name, c))
        for e in self.ENGS[:4]:
            if self.ecnt[e] > 0:
                waits.append((self.esem[e].name, self.ecnt[e]))
        self.streams[eng].append((waits, [], None))

    def replay(self, block):
        sbn = self.sem_by_name

        def run(e, items):
            for waits, fns, inc in items:
                for name, val in waits:
                    e.wait_ge(sbn[name], val)
                last = None
                for i, f in enumerate(fns):
                    ins = f(e)
                    if inc is not None and inc[0] == "dmainc":
                        ins.then_inc(inc[1], 16)
                    last = ins
                if inc is not None and inc[0] == "inc" and last is not None:
                    last.then_inc(inc[1], 1)

        st = self.streams

        @block.tensor
        def _(e):
            run(e, st["pe"])

        @block.scalar
        def _(e):
            run(e, st["act"])

        @block.vector
        def _(e):
            run(e, st["dve"])

        @block.gpsimd
        def _(e):
            run(e, st["pool"])

        @block.sync
        def _(e):
            run(e, st["sp"])


U8 = mybir.dt.uint8
T_P = 2048
T_S = 16
T_ALL = T_P + T_S
SEGS = [(0, 512), (512, 512), (1024, 512), (1536, 512), (2048, 16)]
NT = 17
EPS = 1e-6
D_IN = 3080
C_Q, C_K, C_V, C_Z, C_BA, C_U, C_VS = 0, 512, 1024, 1536, 2048, 2056, 2568


def mm(out, lhsT, rhs, start=True, stop=True):
    return lambda e: e.matmul(out, lhsT=lhsT, rhs=rhs, start=start, stop=stop)


def actf(out, in_, func, **kw):
    return lambda e: e.activation(out=out, in_=in_, func=func, **kw)


def tt(out, a, b, op):
    return lambda e: e.tensor_tensor(out=out, in0=a, in1=b, op=op)


def ts(out, a, s1, op0, s2=None, op1=None):
    if op1 is None:
        return lambda e: e.tensor_scalar(out=out, in0=a, scalar1=s1, scalar2=None, op0=op0)
    return lambda e: e.tensor_scalar(out=out, in0=a, scalar1=s1, scalar2=s2, op0=op0, op1=op1)


def stt(out, a, s, b, op0, op1):
    return lambda e: e.scalar_tensor_tensor(out=out, in0=a, scalar=s, in1=b, op0=op0, op1=op1)


def stt_pool(out, a, colap):
    return lambda e: e.tensor_tensor(out=out, in0=a, in1=colap.to_broadcast([128, 128]), op=ALU.mult)


def cp(out, in_):
    return lambda e: e.tensor_copy(out=out, in_=in_)


def dmaf(out, in_):
    return lambda e: e.dma_start(out=out, in_=in_)


class _Item:
    __slots__ = ("thunk", "eng", "reads", "writes", "dur")

    def __init__(self, thunk, eng, reads, writes, dur):
        self.thunk, self.eng, self.reads, self.writes, self.dur = thunk, eng, tuple(reads), tuple(writes), dur


_DUR = {"act": 0.5, "dve": 0.45, "pool": 0.5}


def mk_recorders(S, ops):
    def Lop(eng, fn, reads=(), writes=()):
        ops.append(_Item(lambda: S.op(eng, fn, reads=reads, writes=writes), eng, reads, writes, _DUR.get(eng, 0.4)))

    def Lgr(eng, fns, reads=(), writes=()):
        ops.append(_Item(lambda: S.group(eng, fns, reads=reads, writes=writes), eng, reads, writes, 0.1 + 0.13 * len(fns)))

    def Ldma(eng, slot, fn, reads=(), writes=()):
        ops.append(_Item(lambda: S.dma(eng, slot, fn, reads=reads, writes=writes), "q_" + eng, reads, writes, 2.5))
    return Lop, Lgr, Ldma


def zipper(lists):
    lists = [l for l in lists if l]
    idx = [0] * len(lists)
    if os.environ.get("ZIP", "rr") == "rr":
        live = True
        while live:
            live = False
            for i, l in enumerate(lists):
                if idx[i] < len(l):
                    it = l[idx[i]]
                    idx[i] += 1
                    live = True
                    if isinstance(it, _Item):
                        it.thunk()
                    else:
                        it()
        return
    t_eng, t_w, t_r = {}, {}, {}
    remaining = sum(len(l) for l in lists)
    while remaining:
        best = None
        for i, l in enumerate(lists):
            if idx[i] >= len(l):
                continue
            it = l[idx[i]]
            if not isinstance(it, _Item):
                best = (-1.0, i, it)
                break
            rdy = t_eng.get(it.eng, 0.0)
            for k in it.reads:
                rdy = max(rdy, t_w.get(k, 0.0))
            for k in it.writes:
                rdy = max(rdy, t_w.get(k, 0.0), t_r.get(k, 0.0))
            if best is None or rdy < best[0]:
                best = (rdy, i, it)
        rdy, i, it = best
        idx[i] += 1
        remaining -= 1
        if not isinstance(it, _Item):
            it()
            continue
        it.thunk()
        fin = rdy + it.dur
        if it.eng.startswith("q_"):
            t_eng[it.eng] = rdy + 0.1
        else:
            t_eng[it.eng] = fin
        for k in it.reads:
            t_r[k] = max(t_r.get(k, 0.0), fin)
        for k in it.writes:
            t_w[k] = fin
            t_r[k] = 0.0


class Bump:
    def __init__(self, arena, start, limit):
        self.t, self.off, self.limit = arena, start, limit

    def alloc(self, dtype, shape):
        esz = 4 if dtype == F32 else 2
        n = 1
        for s in shape[1:]:
            n *= s
        nb = (n * esz + 63) // 64 * 64
        o = self.off
        self.off += nb
        assert self.off <= self.limit, ("SBUF arena overflow", self.off, self.limit)
        ap = self.t[:, o:o + n * esz].bitcast(dtype)
        if len(shape) == 3:
            ap = ap.rearrange("p (a b) -> p a b", a=shape[1])
        elif len(shape) == 4:
            ap = ap.rearrange("p (a b c) -> p a b c", a=shape[1], b=shape[2])
        return ap


def build_program():
    from contextlib import ExitStack
    nc = bass.Bass("TRN2", target_bir_lowering=False)

    def din(name, shape):
        return nc.dram_tensor(name, shape, F32, kind="ExternalInput").ap()

    def dout(name, shape):
        return nc.dram_tensor(name, shape, F32, kind="ExternalOutput").ap()

    x_p = din("x_p", [T_P, 1024]); x_s = din("x_s", [T_S, 1024])
    st_conv = din("st_conv", [T_S, 3, 1536]); st_gdn = din("st_gdn", [T_S, 4, 128, 128])
    p_p = din("p_p", [T_P, 256]); p_s = din("p_s", [T_S, 256])
    g_mix = din("g_mix", [1, 1024]); w_in = din("w_in", [1024, D_IN]); w_conv = din("w_conv", [4, 1536])
    a_log = din("a_log", [1, 4]); dt_bias = din("dt_bias", [1, 4]); gdn_norm = din("gdn_norm", [1, 128])
    ln_g = din("ln_g", [1, 512]); ln_b = din("ln_b", [1, 512]); w_s = din("w_s", [4, 128, 128]); b_s = din("b_s", [1, 512])
    w_out = din("w_out", [1024, 1024]); g_ff = din("g_ff", [8, 128]); w_up = din("w_up", [1024, 4096]); w_down = din("w_down", [4096, 1024])
    g_ple = din("g_ple", [8, 128]); w_ple = din("w_ple", [256, 1024]); w_gate = din("w_gate", [1024, 1024]); g_fin = din("g_fin", [8, 128])
    y_p = dout("y_p", [T_P, 1024]); y_s = dout("y_s", [T_S, 1024])
    ncp = dout("ncp", [3, 1536]); ngp = dout("ngp", [4, 128, 128])
    ncs = dout("ncs", [T_S, 3, 1536]); ngs = dout("ngs", [T_S, 4, 128, 128]); nsv = dout("nsv", [T_S, 512])

    w_in_v = w_in.rearrange("(k p) c -> p k c", p=128)
    w_out_v = w_out.rearrange("(k p) c -> p k c", p=128)
    w_up_v = w_up.rearrange("(k p) c -> p k c", p=128)
    w_down_v = w_down.rearrange("(k p) c -> p k c", p=128)
    w_gate_v = w_gate.rearrange("(k p) c -> p k c", p=128)
    w_ple_v = w_ple.rearrange("(k p) c -> p k c", p=128)

    with ExitStack() as st:
        S = Sched(nc, st)
        ARENA = 206 * 1024
        arena = st.enter_context(nc.sbuf_tensor("arena", [128, ARENA], U8))
        ps = [st.enter_context(nc.psum_tensor("ps%d" % i, [128, 512], F32)) for i in range(8)]
        psb = [p[:, :].bitcast(BF16) for p in ps]
        PK = [("ps", i) for i in range(8)]

        P = Bump(arena, 0, ARENA)
        ident_f = P.alloc(F32, [128, 128]); ident_b = P.alloc(BF16, [128, 128])
        ones_f = P.alloc(F32, [128, 128]); ones_b = P.alloc(BF16, [128, 128])
        mask_incl = P.alloc(F32, [128, 128])
        mask_su = P.alloc(F32, [128, 128])
        nmask_sl = P.alloc(F32, [128, 128])
        sel127 = P.alloc(F32, [128, 128])
        nmask_su = P.alloc(F32, [128, 128])
        mask_sl = P.alloc(F32, [128, 128])
        rowstage = P.alloc(F32, [128, 128])
        cols = P.alloc(F32, [128, 128])
        wsT = P.alloc(BF16, [128, 4, 128])
        selws = P.alloc(BF16, [128, 4, 16])
        ws00 = P.alloc(F32, [128, 4])
        bs_row = P.alloc(F32, [128, 4, 128])
        lng_row = P.alloc(F32, [128, 512]); lnb_row = P.alloc(F32, [128, 512])
        alog_row = P.alloc(F32, [128, 4]); dtb_row = P.alloc(F32, [128, 4]); nexpA_row = P.alloc(F32, [128, 4])
        zcol = P.alloc(F32, [128, 4])
        zeros_f = P.alloc(F32, [128, 128])
        cat = P.alloc(BF16, [128, 8, T_ALL])
        P_C0 = P.off
        ba = P.alloc(F32, [128, NT, 8])
        beta_c = P.alloc(F32, [128, NT, 4]); g_c = P.alloc(F32, [128, NT, 4]); gc_c = P.alloc(F32, [128, NT, 4])
        bexp_c = P.alloc(F32, [128, NT, 4]); kd_c = P.alloc(F32, [128, NT, 4]); egl_c = P.alloc(F32, [128, NT, 4])
        tmp68 = P.alloc(F32, [128, NT, 4])
        qkv = P.alloc(BF16, [128, 12, T_ALL])
        zs = P.alloc(BF16, [128, 4, T_ALL])
        qks_f = P.alloc(F32, [128, 12, 16])
        histT = P.alloc(F32, [128, 12, 3, 16])
        ncp_st = P.alloc(F32, [128, 12, 3]); ncs_st = P.alloc(F32, [128, 12, 16])
        S_f = P.alloc(F32, [128, 4, 128]); S_b = P.alloc(BF16, [128, 4, 128])
        X0 = P.off
        XB = Bump(arena, X0, ARENA)
        hT = XB.alloc(BF16, [128, 8, T_ALL])
        Y0 = XB.off

        def wcol(j, c):
            return cols[:, 24 + j * 12 + c: 24 + j * 12 + c + 1]

        def gcol(which, m):
            return cols[:, which * 8 + m: which * 8 + m + 1]
        gdnn_col = cols[:, 72:73]
        negone = zcol[:, 1:2]

        S.op("pool", lambda e: e.memset(ones_f, 1.0), writes=["ones_f"])
        S.op("pool", lambda e: e.memset(ones_b, 1.0), writes=["ones_b"])
        S.op("pool", lambda e: e.memset(zcol, 0.0), writes=["zcol"])
        S.op("pool", lambda e: e.memset(zcol[:, 1:2], -1.0), reads=["zcol"], writes=["zcol"])
        S.op("pool", lambda e: e.memset(zeros_f, 0.0), writes=["zeros_f"])
        S.op("pool", lambda e: e.affine_select(out=ident_f, in_=ones_f, pattern=[[-1, 128]], compare_op=ALU.is_equal, fill=0.0, base=0, channel_multiplier=1), reads=["ones_f"], writes=["ident_f"])
        S.op("pool", lambda e: e.affine_select(out=mask_incl, in_=ones_f, pattern=[[1, 128]], compare_op=ALU.is_ge, fill=0.0, base=0, channel_multiplier=-1), reads=["ones_f"], writes=["mask_incl"])
        S.op("pool", lambda e: e.affine_select(out=mask_su, in_=ones_f, pattern=[[1, 128]], compare_op=ALU.is_gt, fill=0.0, base=0, channel_multiplier=-1), reads=["ones_f"], writes=["mask_su"])
        S.op("pool", lambda e: e.affine_select(out=nmask_sl, in_=ones_f, pattern=[[-1, 128]], compare_op=ALU.is_gt, fill=0.0, base=0, channel_multiplier=1), reads=["ones_f"], writes=["nmask_sl"])
        S.op("pool", ts(nmask_sl, nmask_sl, -1.0, ALU.mult), reads=["nmask_sl"], writes=["nmask_sl"])
        S.op("pool", ts(nmask_su, mask_su, -1.0, ALU.mult), reads=["mask_su"], writes=["nmask_su"])
        S.op("pool", ts(mask_sl, nmask_sl, -1.0, ALU.mult), reads=["nmask_sl"], writes=["mask_sl"])
        S.op("pool", lambda e: e.affine_select(out=sel127, in_=ones_f, pattern=[[0, 128]], compare_op=ALU.is_equal, fill=0.0, base=-127, channel_multiplier=1), reads=["ones_f"], writes=["sel127"])
        S.op("dve", cp(ident_b, ident_f), reads=["ident_f"], writes=["ident_b"])
        S.op("pool", lambda e: e.memset(rowstage, 0.0), writes=["rowstage"])
        S.dma("sp", "c0", [dmaf(rowstage[0:8, :], g_ff), dmaf(rowstage[8:16, :], g_ple), dmaf(rowstage[16:24, :], g_fin),
                           dmaf(rowstage[24:72, :], w_conv.rearrange("j (c p) -> (j c) p", p=128)), dmaf(rowstage[72:73, :], gdn_norm)],
              writes=["rowstage"])
        S.group("pe", [mm(ps[0][:, 0:128], rowstage, ident_f)], reads=["rowstage", "ident_f"], writes=[PK[0]])
        S.op("dve", cp(cols, ps[0][:, 0:128]), reads=[PK[0]], writes=["cols"])
        S.dma("sp", "c1", [dmaf(bs_row.rearrange("p h t -> p (h t)"), b_s.partition_broadcast(128)),
                           dmaf(lng_row, ln_g.partition_broadcast(128)), dmaf(lnb_row, ln_b.partition_broadcast(128)),
                           dmaf(alog_row, a_log.partition_broadcast(128)), dmaf(dtb_row, dt_bias.partition_broadcast(128)),
                           ] + [dmaf(ws00[:, h:h + 1], w_s[h, 0, 0:1].partition_broadcast(128)) for h in range(4)],
              writes=["rows"])
        S.op("act", actf(nexpA_row, alog_row, AF.Exp), reads=["rows"], writes=["nexpA"])
        S.op("dve", ts(nexpA_row, nexpA_row, -1.0, ALU.mult), reads=["nexpA"], writes=["nexpA"])
        for h in range(4):
            S.op("dve", ts(selws[0:16, h, :], ident_f[0:16, 0:16], ws00[0:16, h:h + 1], ALU.mult), reads=["rows", "ident_f"], writes=[("selws", h)])

        YA = Bump(arena, Y0, ARENA)
        wstmp = YA.alloc(F32, [128, 4, 128])
        S.dma("sp", "c2", dmaf(wstmp, w_s.rearrange("h t s -> t h s")), writes=["wstmp"])
        for h in range(4):
            S.op("pool", lambda e, h=h: e.affine_select(out=wstmp[:, h, :], in_=wstmp[:, h, :], pattern=[[-1, 128]], compare_op=ALU.is_ge, fill=0.0, base=0, channel_multiplier=1),
                 reads=["wstmp"], writes=["wstmp"])
        S.group("pe", [mm(ps[1][:, h * 128:(h + 1) * 128], wstmp[:, h, :], ident_f) for h in range(4)], reads=["wstmp", "ident_f"], writes=[PK[1]])
        S.op("dve", cp(wsT.rearrange("p h t -> p (h t)"), ps[1][:, 0:512]), reads=[PK[1]], writes=["wsT"])

        gmix_row = YA.alloc(F32, [128, 1024])
        S.dma("sp", "c3", dmaf(gmix_row, g_mix.partition_broadcast(128)), writes=["gmix"])
        xt = [YA.alloc(F32, [128, 1024]) for _ in range(3)]
        xsq = [YA.alloc(F32, [128, 1024]) for _ in range(3)]
        xn = [YA.alloc(BF16, [128, 1024]) for _ in range(3)]
        stat = YA.alloc(F32, [128, NT, 2])

        def phaseA_tile(i):
            ops = []
            Lop, Lgr, Ldma = mk_recorders(S, ops)
            r = 128 if i < 16 else 16
            sl = i % 3
            src = x_p[i * 128:(i + 1) * 128, :] if i < 16 else x_s
            Ldma("sp", "xt%d" % sl, dmaf(xt[sl][0:r, :], src), writes=[("xt", sl)])
            Lop("act", actf(xsq[sl][0:r, :], xt[sl][0:r, :], AF.Square), reads=[("xt", sl)], writes=[("xsq", sl)])
            Lop("dve", lambda e: e.reduce_sum(out=stat[0:r, i, 0:1], in_=xsq[sl][0:r, :], axis=AX.X), reads=[("xsq", sl)], writes=[("stat", i)])
            Lop("dve", ts(stat[0:r, i, 1:2], stat[0:r, i, 0:1], 1.0 / 1024, ALU.mult, EPS, ALU.add), reads=[("stat", i)], writes=[("stat", i)])
            Lop("act", actf(stat[0:r, i, 1:2], stat[0:r, i, 1:2], AF.Sqrt), reads=[("stat", i)], writes=[("stat", i)])
            Lop("dve", lambda e: e.reciprocal(out=stat[0:r, i, 1:2], in_=stat[0:r, i, 1:2]), reads=[("stat", i)], writes=[("stat", i)])
            Lop("dve", stt(xn[sl][0:r, :], xt[sl][0:r, :], stat[0:r, i, 1:2], gmix_row[0:r, :], ALU.mult, ALU.mult),
                reads=[("xt", sl), ("stat", i), "gmix"], writes=[("xn", sl)])
            b = i % 3
            Lgr("pe", [lambda e, k=k: e.transpose(out=psb[b][:, k * 128:k * 128 + r], in_=xn[sl][0:r, k * 128:(k + 1) * 128], identity=ident_b[0:r, 0:r]) for k in range(8)],
                reads=[("xn", sl), "ident_b"], writes=[PK[b]])
            if i % 2 == 0:
                Lop("act", actf(hT[:, :, i * 128:i * 128 + r], psb[b].rearrange("p (k t) -> p k t", k=8)[:, :, 0:r], AF.Copy), reads=[PK[b]], writes=[("hT", i)])
            else:
                Lop("dve", cp(hT[:, :, i * 128:i * 128 + r], psb[b].rearrange("p (k t) -> p k t", k=8)[:, :, 0:r]), reads=[PK[b]], writes=[("hT", i)])
            return ops

        tilesA = [phaseA_tile(i) for i in range(NT)]
        for g0 in range(0, NT, 3):
            zipper(tilesA[g0:g0 + 3])
        S.barrier()

        YB = Bump(arena, Y0, ARENA)
        wb = [YB.alloc(BF16, [128, 8, 512]) for _ in range(2)]
        wb8 = YB.alloc(BF16, [128, 8, 8])
        lnst = YB.alloc(F32, [128, NT, 8])
        sct = [YB.alloc(F32, [128, 3, 128]) for _ in range(2)]
        YB_MID = YB.off
        vg = YB.alloc(BF16, [128, NT, 512])
        F6 = YB.alloc(F32, [128, 6, 512])
        f512 = [F6[:, i, :] for i in range(6)]
        vgs_f = f512[5]
        YB5 = Bump(arena, YB_MID, ARENA)
        NBS = 4
        pre = [YB5.alloc(F32, [128, 515]) for _ in range(NBS)]
        accb = [YB5.alloc(F32, [128, 512]) for _ in range(NBS)]
        rnb = [YB5.alloc(F32, [128, 512]) for _ in range(NBS)]
        sqb = [YB5.alloc(BF16, [128, 512]) for _ in range(NBS)]
        wdiag = [YB5.alloc(F32, [128, 4, 128]) for _ in range(2)]
        YB6 = Bump(arena, YB_MID, ARENA)
        stage_tok = YB6.alloc(F32, [128, 1536])
        stage2 = YB6.alloc(F32, [128, 1536])
        wb_n = [0]
        bank_n = [0]

        def next_bank(lo=0, hi=4):
            b = lo + bank_n[0] % (hi - lo)
            bank_n[0] += 1
            return b

        wb_seq = [C_VS, C_U, C_Z, 0, 512, 1024]
        wb_issued = [0]

        def load_w(view, c0, ncol=512):
            idx = wb_n[0]
            wb_n[0] += 1
            assert wb_seq[idx] == c0
            while wb_issued[0] < min(len(wb_seq), idx + 2):
                j = wb_issued[0]
                S.dma("pool", "wb%d" % (j % 2), dmaf(wb[j % 2][:, :, 0:512], view[:, :, wb_seq[j]:wb_seq[j] + 512]), writes=[("wb", j % 2)])
                wb_issued[0] += 1
            return idx % 2

        hT_keys = [("hT", i) for i in range(NT)]

        def seg_hT_keys(t0, n):
            return [("hT", i) for i in range(t0 // 128, (t0 + n + 127) // 128)]

        sl = load_w(w_in_v, C_VS)

        def vsgu_tile(i):
            ops = []
            Lop, Lgr, Ldma = mk_recorders(S, ops)
            r = 128 if i < 16 else 16
            b = (i % 3)
            Lgr("pe", [mm(ps[b][0:r, :], hT[:, k, i * 128:i * 128 + r], wb[sl][:, k, :], start=(k == 0), stop=(k == 7)) for k in range(8)],
                    reads=[("hT", i), ("wb", sl)], writes=[PK[b]])
            g1 = f512[i % 3]; g2 = f512[3 + i % 3]
            Lop("act", actf(g1[0:r, :], ps[b][0:r, :], AF.Gelu_apprx_tanh), reads=[PK[b]], writes=[("g1", i % 3)])
            Lop("pool", tt(g2[0:r, :], g1[0:r, :], g1[0:r, :], ALU.mult), reads=[("g1", i % 3)], writes=[("g2", i % 3)])
            Lop("dve", lambda e, i=i, r=r, g1=g1: e.reduce_sum(out=lnst[0:r, i, 0:1], in_=g1[0:r, :], axis=AX.X), reads=[("g1", i % 3)], writes=[("lnst", i)])
            Lop("dve", lambda e, i=i, r=r, g2=g2: e.reduce_sum(out=lnst[0:r, i, 1:2], in_=g2[0:r, :], axis=AX.X), reads=[("g2", i % 3)], writes=[("lnst", i)])
            L = lambda a, bb: lnst[0:r, i, a:bb]
            Lop("dve", ts(L(2, 3), L(0, 1), 1.0 / 512, ALU.mult), reads=[("lnst", i)], writes=[("lnst", i)])
            Lop("dve", tt(L(3, 4), L(2, 3), L(2, 3), ALU.mult), reads=[("lnst", i)], writes=[("lnst", i)])
            Lop("dve", stt(L(4, 5), L(1, 2), 1.0 / 512, L(3, 4), ALU.mult, ALU.subtract), reads=[("lnst", i)], writes=[("lnst", i)])
            Lop("dve", ts(L(4, 5), L(4, 5), EPS, ALU.add), reads=[("lnst", i)], writes=[("lnst", i)])
            Lop("act", actf(L(4, 5), L(4, 5), AF.Sqrt), reads=[("lnst", i)], writes=[("lnst", i)])
            Lop("dve", lambda e, i=i, r=r: e.reciprocal(out=lnst[0:r, i, 5:6], in_=lnst[0:r, i, 4:5]), reads=[("lnst", i)], writes=[("lnst", i)])
            Lop("dve", ts(g2[0:r, :], g1[0:r, :], L(2, 3), ALU.subtract, L(5, 6), ALU.mult), reads=[("g1", i % 3), ("lnst", i)], writes=[("g2", i % 3)])
            Lop("pool", tt(g2[0:r, :], g2[0:r, :], lng_row[0:r, :], ALU.mult), reads=[("g2", i % 3), "rows"], writes=[("g2", i % 3)])
            if i < 16:
                Lop("pool", tt(vg[0:r, i, :], g2[0:r, :], lnb_row[0:r, :], ALU.add), reads=[("g2", i % 3), "rows"], writes=[("vg", i)])
            else:
                Lop("pool", tt(vgs_f[0:r, :], g2[0:r, :], lnb_row[0:r, :], ALU.add), reads=[("g2", i % 3), "rows"], writes=[("g2", 2)])
                Lop("pool", cp(vg[0:r, i, :], vgs_f[0:r, :]), reads=[("g2", 2)], writes=[("vg", i)])
                Ldma("sp", "o_nsv", dmaf(nsv, vgs_f[0:r, :]), reads=[("g2", 2)])
            return ops

        tilesV = [vsgu_tile(i) for i in range(NT)]
        for g0 in range(0, NT, 3):
            zipper(tilesV[g0:g0 + 3])

        S.dma("pool", "wb8", dmaf(wb8, w_in_v[:, :, C_BA:C_BA + 8]), writes=["wb8"])
        S.op("pool", lambda e: e.memset(ba, 0.0), writes=["ba"])
        bq = 4
        for i in range(NT):
            r = 128 if i < 16 else 16
            S.group("pe", [mm(ps[bq][0:r, i * 8:(i + 1) * 8], hT[:, k, i * 128:i * 128 + r], wb8[:, k, :], start=(k == 0), stop=(k == 7)) for k in range(8)],
                    reads=[("hT", i), "wb8"], writes=[PK[bq]])
        S.op("dve", cp(ba[:, 0:16, :], ps[bq][:, 0:128].rearrange("p (i c) -> p i c", c=8)), reads=[PK[bq], "ba"], writes=["ba"])
        S.op("dve", cp(ba[0:16, 16, :], ps[bq][0:16, 128:136]), reads=[PK[bq], "ba"], writes=["ba"])
        S.op("act", actf(beta_c, ba[:, :, 0:4], AF.Sigmoid), reads=["ba"], writes=["beta_c"])
        S.op("dve", tt(tmp68, ba[:, :, 4:8], dtb_row.unsqueeze(1).to_broadcast([128, NT, 4]), ALU.add), reads=["ba", "rows"], writes=["tmp68"])
        S.op("act", actf(tmp68, tmp68, AF.Exp), reads=["tmp68"], writes=["tmp68"])
        S.op("act", actf(tmp68, tmp68, AF.Ln, bias=1.0), reads=["tmp68"], writes=["tmp68"])
        S.op("dve", tt(g_c, tmp68, nexpA_row.unsqueeze(1).to_broadcast([128, NT, 4]), ALU.mult), reads=["tmp68", "nexpA"], writes=["g_c"])
        g68 = g_c.rearrange("p i h -> p (i h)"); gc68 = gc_c.rearrange("p i h -> p (i h)")
        S.group("pe", [mm(ps[5][:, 0:68], mask_incl, g68)], reads=["mask_incl", "g_c"], writes=[PK[5]])
        S.op("dve", cp(gc68, ps[5][:, 0:68]), reads=[PK[5]], writes=["gc_c"])
        S.group("pe", [mm(ps[5][:, 128:196], sel127, gc68)], reads=["sel127", "gc_c"], writes=[PK[5]])
        S.op("dve", cp(egl_c.rearrange("p i h -> p (i h)"), ps[5][:, 128:196]), reads=[PK[5]], writes=["egl_c"])
        S.op("dve", tt(tmp68.rearrange("p i h -> p (i h)"), egl_c.rearrange("p i h -> p (i h)"), gc68, ALU.subtract), reads=["egl_c", "gc_c"], writes=["tmp68"])
        S.op("act", actf(egl_c, egl_c, AF.Exp), reads=["egl_c", "tmp68"], writes=["egl_c"])
        S.op("act", actf(kd_c, tmp68, AF.Exp), reads=["tmp68"], writes=["kd_c"])
        S.op("act", actf(bexp_c, gc_c, AF.Exp), reads=["gc_c"], writes=["bexp_c"])
        S.op("dve", tt(bexp_c, bexp_c, beta_c, ALU.mult), reads=["bexp_c", "beta_c"], writes=["bexp_c"])

        sl = load_w(w_in_v, C_U)
        for h in range(4):
            for (t0, n) in SEGS:
                b = next_bank()
                S.group("pe", [mm(ps[b][:, 0:n], wb[sl][:, k, h * 128:(h + 1) * 128], hT[:, k, t0:t0 + n], start=(k == 0), stop=(k == 7)) for k in range(8)],
                        reads=seg_hT_keys(t0, n) + [("wb", sl)], writes=[PK[b]])
                u = f512[bank_n[0] % 2]
                S.op("act", actf(u[:, 0:n], ps[b][:, 0:n], AF.Gelu_apprx_tanh), reads=[PK[b]], writes=[("u", bank_n[0] % 2)])
                b2 = 4 + bank_n[0] % 2
                if n == 512:
                    tiles = [t0 // 128 + j for j in range(4)]
                    S.group("pe", [mm(ps[b2][:, j * 128:(j + 1) * 128], vg[:, tiles[j], h * 128:(h + 1) * 128], wsT[:, h, :]) for j in range(4)],
                            reads=[("vg", ti) for ti in tiles] + ["wsT"], writes=[PK[b2]])
                    S.op("dve", tt(f512[4][:, :].rearrange("p (j t) -> p j t", j=4), ps[b2][:, :].rearrange("p (j t) -> p j t", j=4),
                                   bs_row[:, h:h + 1, :].to_broadcast([128, 4, 128]), ALU.add), reads=[PK[b2], "rows"], writes=["mixt"])
                else:
                    S.group("pe", [mm(ps[b2][:, 0:16], vg[0:16, 16, h * 128:(h + 1) * 128], selws[0:16, h, :])],
                            reads=[("vg", 16), ("selws", h)], writes=[PK[b2]])
                    S.op("dve", tt(f512[4][:, 0:16], ps[b2][:, 0:16], bs_row[:, h, 0:1].to_broadcast([128, 16]), ALU.add), reads=[PK[b2], "rows"], writes=["mixt"])
                S.op("pool", tt(cat[:, 4 + h, t0:t0 + n], f512[4][:, 0:n], u[:, 0:n], ALU.mult), reads=["mixt", ("u", bank_n[0] % 2)], writes=[("cat", 4 + h, t0)])

        sl = load_w(w_in_v, C_Z)
        for h in range(4):
            for (t0, n) in SEGS:
                b = next_bank()
                S.group("pe", [mm(ps[b][:, 0:n], wb[sl][:, k, h * 128:(h + 1) * 128], hT[:, k, t0:t0 + n], start=(k == 0), stop=(k == 7)) for k in range(8)],
                        reads=seg_hT_keys(t0, n) + [("wb", sl)], writes=[PK[b]])
                S.op("act", actf(zs[:, h, t0:t0 + n], ps[b][:, 0:n], AF.Silu), reads=[PK[b]], writes=[("zs", h, t0)])

        S.barrier()

        def qkv_unit(blk, h, si, ui, wsl):
            ops = []
            Lop, Lgr, Ldma = mk_recorders(S, ops)
            c = blk * 4 + h
            t0, n = SEGS[si]
            bs = ui % NBS
            pr, acc, sq, rn = pre[bs], accb[bs], sqb[bs], rnb[bs]
            kp, ka, ks, kr = ("pre", bs), ("acc", bs), ("sq", bs), ("rn", bs)
            b = ui % 4; bn = 4 + ui % 4
            Lgr("pe", [mm(ps[b][:, 0:n], wb[wsl][:, k, h * 128:(h + 1) * 128], hT[:, k, t0:t0 + n], start=(k == 0), stop=(k == 7)) for k in range(8)],
                reads=[("wb", wsl)], writes=[PK[b]])
            if si == 0:
                Lop("dve", lambda e: e.memset(pr[:, 0:3], 0.0), writes=[kp])
            elif si < 4:
                Lgr("pe", [mm(ps[bn][:, 0:3], wb[wsl][:, k, h * 128:(h + 1) * 128], hT[:, k, t0 - 3:t0], start=(k == 0), stop=(k == 7)) for k in range(8)],
                    reads=[("wb", wsl)], writes=[PK[bn]])
                Lop("dve", cp(pr[:, 0:3], ps[bn][:, 0:3]), reads=[PK[bn]], writes=[kp])
            Lop("act", actf(pr[:, 3:3 + n], ps[b][:, 0:n], AF.Copy), reads=[PK[b], kp], writes=[kp])
            if si == 3:
                Lop("dve", cp(ncp_st[:, c, :], pr[:, 512:515]), reads=[kp], writes=[("ncp_st", c)])
            if si < 4 and os.environ.get("CONV", "dve") == "dve":
                Lop("act", actf(acc[:, 0:n], pr[:, 3:3 + n], AF.Copy, scale=wcol(3, c)), reads=[kp, "cols"], writes=[ka])
                Lop("dve", stt(acc[:, 0:n], pr[:, 2:2 + n], wcol(2, c), acc[:, 0:n], ALU.mult, ALU.add), reads=[kp, ka, "cols"], writes=[ka])
                Lop("dve", stt(acc[:, 0:n], pr[:, 1:1 + n], wcol(1, c), acc[:, 0:n], ALU.mult, ALU.add), reads=[kp, ka, "cols"], writes=[ka])
                Lop("dve", stt(acc[:, 0:n], pr[:, 0:n], wcol(0, c), acc[:, 0:n], ALU.mult, ALU.add), reads=[kp, ka, "cols"], writes=[ka])
                Lop("act", actf(acc[:, 0:n], acc[:, 0:n], AF.Silu), reads=[ka], writes=[ka])
            elif si < 4:
                wd = wdiag[c % 2]
                bc_ = 4 + ui % 4
                fns = []
                for t4 in range(4):
                    for j in range(4):
                        fns.append(mm(ps[bc_][:, t4 * 128:(t4 + 1) * 128], wd[:, j, :], pr[:, j + t4 * 128:j + (t4 + 1) * 128], start=(j == 0), stop=(j == 3)))
                Lgr("pe", fns, reads=[kp, ("wdiag", c % 2)], writes=[PK[bc_]])
                Lop("act", actf(acc[:, 0:n], ps[bc_][:, 0:n], AF.Silu), reads=[PK[bc_]], writes=[ka])
            else:
                Lop("dve", cp(ncs_st[:, c, :], pr[:, 3:19]), reads=[kp], writes=[("ncs_st", c)])
                Lop("act", actf(acc[:, 0:n], pr[:, 3:3 + n], AF.Copy, scale=wcol(3, c)), reads=[kp, "cols"], writes=[ka])
                for j in (2, 1, 0):
                    Lop("dve", stt(acc[:, 0:n], histT[:, c, j, :], wcol(j, c), acc[:, 0:n], ALU.mult, ALU.add), reads=[("histT", c), ka, "cols"], writes=[ka])
                Lop("act", actf(acc[:, 0:n], acc[:, 0:n], AF.Silu), reads=[ka], writes=[ka])
            if blk == 2:
                Lop("pool", cp(qkv[:, c, t0:t0 + n], acc[:, 0:n]), reads=[ka], writes=[("qkv", c, t0)])
                if si == 4:
                    Lop("pool", cp(qks_f[:, c, :], acc[:, 0:16]), reads=[ka], writes=[("qks_f", c)])
            else:
                Lop("pool", tt(sq[:, 0:n], acc[:, 0:n], acc[:, 0:n], ALU.mult), reads=[ka], writes=[ks])
                Lgr("pe", [mm(ps[bn][:, 0:n], ones_b, sq[:, 0:n])], reads=[ks, "ones_b"], writes=[PK[bn]])
                Lop("act", actf(rn[:, 0:n], ps[bn][:, 0:n], AF.Sqrt, bias=1e-6), reads=[PK[bn]], writes=[kr])
                Lop("dve", lambda e: e.reciprocal(out=rn[:, 0:n], in_=rn[:, 0:n]), reads=[kr], writes=[kr])
                scl = (128.0 ** -0.5) if blk == 0 else 1.0
                Lop("dve", stt(qkv[:, c, t0:t0 + n], acc[:, 0:n], scl, rn[:, 0:n], ALU.mult, ALU.mult), reads=[ka, kr], writes=[("qkv", c, t0)])
                if si == 4:
                    Lop("dve", stt(qks_f[:, c, :], acc[:, 0:16], scl, rn[:, 0:16], ALU.mult, ALU.mult), reads=[ka, kr], writes=[("qks_f", c)])
            return ops

        ui = 0
        for blk in range(3):
            wsl = load_w(w_in_v, blk * 512)
            units = []
            for h in range(4):
                c = blk * 4 + h
                scs = sct[c % 2]
                S.dma("sp", "sct%d" % (c % 2), dmaf(scs[0:16, :, :], st_conv[:, :, c * 128:(c + 1) * 128]), writes=[("sct", c % 2)])
                S.group("pe", [mm(ps[4 + c % 4][:, j * 16:(j + 1) * 16], scs[0:16, j, :], ident_f[0:16, 0:16]) for j in range(3)],
                        reads=[("sct", c % 2), "ident_f"], writes=[PK[4 + c % 4]])
                S.op("dve", cp(histT[:, c, :, :], ps[4 + c % 4][:, 0:48].rearrange("p (j b) -> p j b", j=3)), reads=[PK[4 + c % 4]], writes=[("histT", c)])
                for si in range(5):
                    uo = qkv_unit(blk, h, si, ui, wsl)
                    if si == 0:
                        pre_ops = [(lambda j=j, c=c: S.op("pool", stt_pool(wdiag[c % 2][:, j, :], ident_f, wcol(j, c)), reads=["ident_f", "cols"], writes=[("wdiag", c % 2)])) for j in range(4)]
                        uo = pre_ops + uo
                    units.append(uo)
                    ui += 1
            for g0 in range(0, len(units), 4):
                zipper(units[g0:g0 + 4])

        S.barrier()
        XS = Bump(arena, X0, ARENA)
        S_all = XS.alloc(F32, [128, 64, 128])
        if 'sample' not in os.environ.get('KSKIP', ''):
            for q4 in range(4):
                S.dma("sp", "sall%d" % q4, dmaf(S_all[:, q4 * 16:(q4 + 1) * 16, :], st_gdn[q4 * 4:(q4 + 1) * 4].rearrange("b h d e -> d (b h) e")), writes=[("S_all", q4)])
        S.group("pe", [mm(ps[c // 4][0:3, (c % 4) * 128:(c % 4 + 1) * 128], ncp_st[:, c, :], ident_f) for c in range(12)],
                reads=[("ncp_st", c) for c in range(12)] + ["ident_f"], writes=[PK[0], PK[1], PK[2]])
        for q3 in range(3):
            S.op("dve", cp(stage_tok[0:3, q3 * 512:(q3 + 1) * 512], ps[q3][0:3, :]), reads=[PK[q3]], writes=["stage_tok"])
        S.dma("sp", "o_ncp", dmaf(ncp, stage_tok[0:3, :]), reads=["stage_tok"])
        S.group("pe", [mm(ps[c // 4][0:16, (c % 4) * 128:(c % 4 + 1) * 128], ncs_st[:, c, :], ident_f) for c in range(12)],
                reads=[("ncs_st", c) for c in range(12)] + ["ident_f"], writes=[PK[0], PK[1], PK[2]])
        for q3 in range(3):
            S.op("act", actf(stage2[0:16, q3 * 512:(q3 + 1) * 512], ps[q3][0:16, :], AF.Copy), reads=[PK[q3]], writes=["stage2"])
        S.dma("sp", "o_ncs", [dmaf(ncs[:, 2, :], stage2[0:16, :]), dmaf(ncs[:, 0:2, :], st_conv[:, 1:3, :])], reads=["stage2"])
        S.barrier()

        YG = Bump(arena, Y0, ARENA)
        osq = YG.alloc(BF16, [128, 512]); rn_o = YG.alloc(F32, [128, 512]); on_o = YG.alloc(F32, [128, 512])
        YG_EPI = YG.off
        NPW, DP = 4, 8
        CHDT = F32
        gN = lambda n_, dt_, shp: [YG.alloc(dt_, shp) for _ in range(n_)]
        Rs = gN(NPW, F32, [128, 256]); rhsR = gN(NPW, F32, [128, 256]); D0 = gN(NPW, F32, [128, 128]); E0 = gN(NPW, F32, [128, 128])
        EGr = gN(NPW, F32, [128, 128]); MB = gN(NPW, F32, [128, 128]); Qf = gN(NPW, F32, [128, 128]); Qs = gN(NPW, BF16, [128, 128])
        NNa = gN(NPW, CHDT, [128, 256]); NNb = gN(NPW, CHDT, [128, 256]); Xs = gN(NPW, BF16, [128, 128])
        NHa = gN(NPW, BF16, [128, 256]); NHb = gN(NPW, BF16, [128, 256])
        J0 = int(os.environ.get('GDN_J0', '6'))
        Qm = gN(DP, BF16, [128, 128]); attnT = gN(DP, BF16, [128, 128]); Kd = gN(DP, BF16, [128, 128]); Vb = gN(DP, BF16, [128, 128])
        qg = gN(DP, BF16, [128, 128]); nWT = gN(DP, BF16, [128, 128]); vn = gN(4, BF16, [128, 128])
        S.op("pool", lambda e: e.memset(S_f.rearrange("p h e -> p (h e)"), 0.0), writes=[("S_f", h) for h in range(4)])
        S.op("pool", lambda e: e.memset(S_b.rearrange("p h e -> p (h e)"), 0.0), writes=[("S_b", h) for h in range(4)])

        def gdn_P(n, h):
            ops = []
            Lop, Lgr, Ldma = mk_recorders(S, ops)
            u = n * 4 + h
            q = u % NPW; s = u % DP
            tok = slice(n * 128, (n + 1) * 128)
            kT = qkv[:, 4 + h, tok]; qT = qkv[:, h, tok]; vT = qkv[:, 8 + h, tok]
            col = lambda t: t[:, n, h:h + 1]
            K = lambda name: (name, q)
            H = lambda name: (name, s)
            bk = PK[q]; pb = ps[q]; pbb = psb[q]
            kk = pb[:, 256:384]; qk = pb[:, 384:512]
            Lop("pool", stt_pool(rhsR[q][:, 0:128], mask_incl, col(g_c)), reads=["mask_incl", "g_c"], writes=[K("rhsR")])
            Lop("pool", stt_pool(rhsR[q][:, 128:256], ident_f, col(beta_c)), reads=["ident_f", "beta_c"], writes=[K("rhsR")])
            Lgr("pe", [lambda e: e.transpose(out=pbb[:, 0:128], in_=kT, identity=ident_b),
                       lambda e: e.transpose(out=pbb[:, 128:256], in_=vT, identity=ident_b)], reads=[("qkv", n), "ident_b"], writes=[bk])
            ktok = pbb[:, 0:128]; vtok = pbb[:, 128:256]
            Lop("act", actf(Xs[q], ktok, AF.Copy, scale=col(bexp_c)), reads=[bk, "bexp_c"], writes=[K("Xs")])
            Lop("act", actf(Kd[s], ktok, AF.Copy, scale=col(kd_c)), reads=[bk, "kd_c"], writes=[H("Kd")])
            Lop("act", actf(Vb[s], vtok, AF.Copy, scale=col(beta_c)), reads=[bk, "beta_c"], writes=[H("Vb")])
            Lgr("pe", [mm(pb[:, 0:128], ones_f, rhsR[q][:, 0:128]), mm(pb[:, 128:256], ones_f, rhsR[q][:, 128:256]),
                       mm(kk, kT, kT), mm(qk, kT, qT)], reads=[K("rhsR"), "ones_f", ("qkv", n)], writes=[bk])
            Lop("dve", cp(Rs[q], pb[:, 0:256]), reads=[bk], writes=[K("Rs")])
            R_gc = Rs[q][:, 0:128]; R_be = Rs[q][:, 128:256]
            Lop("pool", lambda e: e.tensor_tensor(out=D0[q], in0=R_gc, in1=col(gc_c).to_broadcast([128, 128]), op=ALU.subtract), reads=[K("Rs"), "gc_c"], writes=[K("D0")])
            Lop("pool", ts(D0[q], D0[q], 0.0, ALU.min), reads=[K("D0")], writes=[K("D0")])
            Lop("act", actf(D0[q], D0[q], AF.Exp), reads=[K("D0")], writes=[K("D0")])
            Lop("act", actf(EGr[q], R_gc, AF.Exp), reads=[K("Rs")], writes=[K("EGr")])
            Lop("pool", tt(MB[q], R_be, D0[q], ALU.mult), reads=[K("Rs"), K("D0")], writes=[K("MB")])
            Lop("pool", tt(MB[q], MB[q], nmask_su, ALU.mult), reads=[K("MB"), "nmask_su"], writes=[K("MB")])
            Lop("pool", tt(D0[q], D0[q], mask_incl, ALU.mult), reads=[K("D0"), K("MB"), "mask_incl"], writes=[K("D0")])
            Lop("pool", tt(qg[s], qT, EGr[q], ALU.mult), reads=[("qkv", n), K("EGr")], writes=[H("qg")])
            Lop("dve", tt(NNa[q][:, 0:128], kk, MB[q], ALU.mult), reads=[bk, K("MB")], writes=[K("NNa")])
            Lop("dve", tt(attnT[s], qk, D0[q], ALU.mult), reads=[bk, K("D0")], writes=[H("attnT")])
            Lgr("pe", [mm(pb[:, 0:128], NNa[q][:, 0:128], ident_f)], reads=[K("NNa"), "ident_f"], writes=[bk])
            Lop("act", actf(NNa[q][:, 128:256], pb[:, 0:128], AF.Copy), reads=[bk], writes=[K("NNa")])
            Lop("pool", tt(Qf[q], ident_f, NNa[q][:, 0:128], ALU.add), reads=["ident_f", K("NNa")], writes=[K("Qf")])
            cur, nxt, kc, kn = NNa[q], NNb[q], K("NNa"), K("NNb")
            cur16, nxt16, kc16, kn16 = NHa[q], NHb[q], K("NHa"), K("NHb")
            if J0 == 0:
                Lop("pool", cp(cur16, cur), reads=[kc], writes=[kc16])
            for j in range(1, 7):
                f32lvl = j <= J0
                src, ksrc = (cur, kc) if f32lvl else (cur16, kc16)
                fns = []
                if j < 6:
                    fns.append(mm(pb[:, 0:128], src[:, 128:256], src[:, 0:128]))
                fns.append(mm(pb[:, 128:256], src[:, 0:128], src[:, 128:256]))
                Lgr("pe", fns, reads=[ksrc], writes=[bk])
                lo = 0 if j < 6 else 128
                if f32lvl:
                    Lop("act", actf(nxt[:, lo:256], pb[:, lo:256], AF.Copy), reads=[bk], writes=[kn])
                    if j == J0 and j < 6:
                        Lop("pool", cp(nxt16[:, lo:256], nxt[:, lo:256]), reads=[kn], writes=[kn16])
                    Lgr("pe", [mm(pb[:, 256:384], nxt[:, 128:256], Qf[q])], reads=[kn, K("Qf")], writes=[bk])
                else:
                    Lop("act", actf(nxt16[:, lo:256], pb[:, lo:256], AF.Copy), reads=[bk], writes=[kn16])
                    Lop("pool", cp(Qs[q], Qf[q]), reads=[K("Qf")], writes=[K("Qs")])
                    Lgr("pe", [mm(pb[:, 256:384], nxt16[:, 128:256], Qs[q])], reads=[kn16, K("Qs")], writes=[bk])
                Lop("dve", tt(Qf[q], Qf[q], pb[:, 256:384], ALU.add), reads=[bk, K("Qf")], writes=[K("Qf")])
                cur, nxt, kc, kn = nxt, cur, kn, kc
                cur16, nxt16, kc16, kn16 = nxt16, cur16, kn16, kc16
            Lop("pool", cp(Qm[s], Qf[q]), reads=[K("Qf")], writes=[H("Qm")])
            Lgr("pe", [mm(pb[:, 384:512], Xs[q], Qm[s])], reads=[K("Xs"), H("Qm")], writes=[bk])
            Lop("act", actf(nWT[s], pb[:, 384:512], AF.Copy, scale=negone), reads=[bk], writes=[H("nWT")])
            return ops

        def gdn_R(n, h):
            ops = []
            Lop, Lgr, Ldma = mk_recorders(S, ops)
            u = n * 4 + h
            s = u % DP
            H = lambda name: (name, s)
            col = lambda t: t[:, n, h:h + 1]
            bR = PK[4]; ob = 5 + n % 2
            V = ps[4][:, h * 128:(h + 1) * 128]
            Lgr("pe", [mm(V, Qm[s], Vb[s], start=True, stop=False),
                       mm(V, nWT[s], S_b[:, h, :], start=False, stop=True)],
                reads=[H("Qm"), H("Vb"), H("nWT"), ("S_b", h)], writes=[bR])
            Lop("dve", cp(vn[h], V), reads=[bR], writes=[("vn", h)])
            Lgr("pe", [mm(V, Kd[s], vn[h])], reads=[H("Kd"), ("vn", h)], writes=[bR])
            Lgr("pe", [mm(ps[ob][:, h * 128:(h + 1) * 128], S_b[:, h, :], qg[s], start=True, stop=False),
                       mm(ps[ob][:, h * 128:(h + 1) * 128], vn[h], attnT[s], start=False, stop=True)],
                reads=[("S_b", h), H("qg"), ("vn", h), H("attnT")], writes=[PK[ob]])
            Lop("dve", stt(S_f[:, h, :], S_f[:, h, :], col(egl_c), V, ALU.mult, ALU.add), reads=[bR, ("S_f", h), "egl_c"], writes=[("S_f", h)])
            Lop("act", actf(S_b[:, h, :], S_f[:, h, :], AF.Copy), reads=[("S_f", h)], writes=[("S_b", h)])
            return ops

        def gdn_epilogue(o_ps, ss_ps, ncol, t0, okeys, sskey):
            w = 4 * ncol
            S.op("act", actf(osq[:, 0:w], o_ps, AF.Square), reads=okeys, writes=["osq"])
            S.group("pe", [mm(ss_ps, ones_b, osq[:, 0:w])], reads=["osq", "ones_b"], writes=[sskey])
            S.op("dve", ts(rn_o[:, 0:w], ss_ps, 1.0 / 128, ALU.mult, EPS, ALU.add), reads=[sskey], writes=["rn_o"])
            S.op("act", actf(rn_o[:, 0:w], rn_o[:, 0:w], AF.Sqrt), reads=["rn_o"], writes=["rn_o"])
            S.op("dve", lambda e: e.reciprocal(out=rn_o[:, 0:w], in_=rn_o[:, 0:w]), reads=["rn_o"], writes=["rn_o"])
            S.op("dve", stt(on_o[:, 0:w], o_ps, gdnn_col, rn_o[:, 0:w], ALU.mult, ALU.mult), reads=okeys + ["rn_o", "cols"], writes=["on_o"])
            S.op("pool", tt(cat[:, 0:4, t0:t0 + ncol], on_o[:, 0:w].rearrange("p (h t) -> p h t", h=4), zs[:, :, t0:t0 + ncol], ALU.mult),
                 reads=["on_o", "zs"], writes=[("cat_o", t0)])

        _SK = os.environ.get('KSKIP', '')
        NCH = 0 if 'prompt' in _SK else int(os.environ.get('GDN_N', '16'))

        def epi_ops(n):
            ob = 5 + n % 2
            return [lambda: gdn_epilogue(ps[ob][:, :], ps[7][:, :], 128, n * 128, [PK[ob]], PK[7])]

        _DO_SAMPLE = 'sample' not in _SK
        def _sample_section():
            YG = Bump(arena, YG_EPI, ARENA)
            sv = YG.alloc(F32, [128, 8])
            rexp = YG.alloc(F32, [128, 8, 16])
            bcs = YG.alloc(F32, [128, 128])
            dcol = YG.alloc(F32, [128, 64])
            dtok = YG.alloc(F32, [128, 512]); ktoks = YG.alloc(F32, [128, 512])
            kmask = [YG.alloc(F32, [128, 512]) for _ in range(2)]
            S.op("dve", cp(sv[0:16, 0:4], beta_c[0:16, 16, :]), reads=["beta_c"], writes=["sv"])
            S.op("act", actf(sv[0:16, 4:8], g_c[0:16, 16, :], AF.Exp), reads=["g_c"], writes=["sv"])
            for j in range(8):
                S.op("dve", ts(rexp[0:16, j, :], ident_f[0:16, 0:16], sv[0:16, j:j + 1], ALU.mult), reads=["sv", "ident_f"], writes=["rexp"])
            S.group("pe", [mm(ps[0][:, 0:128], ones_f[0:16, :], rexp[0:16, :, :].rearrange("p j b -> p (j b)"))], reads=["rexp", "ones_f"], writes=[PK[0]])
            S.op("dve", cp(bcs, ps[0][:, 0:128]), reads=[PK[0]], writes=["bcs"])
            beta_bc = bcs[:, 0:64]; eg_bc = bcs[:, 64:128]
            S.group("pe", [mm(ps[1][:, h * 16 + b:h * 16 + b + 1], S_all[:, b * 4 + h, :], qks_f[:, 4 + h, b:b + 1]) for b in range(16) for h in range(4)],
                    reads=[("S_all", q4) for q4 in range(4)] + ["qks_f"], writes=[PK[1]])
            S.op("dve", tt(dcol, ps[1][:, 0:64], eg_bc, ALU.mult), reads=[PK[1], "bcs"], writes=["dcol"])
            S.op("dve", tt(dcol, qks_f[:, 8:12, :].rearrange("p h b -> p (h b)"), dcol, ALU.subtract), reads=["dcol", "qks_f"], writes=["dcol"])
            S.op("dve", tt(dcol, dcol, beta_bc, ALU.mult), reads=["dcol", "bcs"], writes=["dcol"])
            S.group("pe", [mm(ps[2][0:16, h * 128:(h + 1) * 128], dcol[:, h * 16:(h + 1) * 16], ident_f) for h in range(4)], reads=["dcol", "ident_f"], writes=[PK[2]])
            S.group("pe", [mm(ps[3][0:16, h * 128:(h + 1) * 128], qks_f[:, 4 + h, :], ident_f) for h in range(4)], reads=["qks_f", "ident_f"], writes=[PK[3]])
            S.op("dve", cp(dtok[0:16, :], ps[2][0:16, :]), reads=[PK[2]], writes=["dtok"])
            S.op("act", actf(ktoks[0:16, :], ps[3][0:16, :], AF.Copy), reads=[PK[3]], writes=["ktoks"])
            for b in range(16):
                km = kmask[b % 2]; pb = 4 + b % 2
                S.op("dve", ts(km[0:16, :], ktoks[0:16, :], ident_f[0:16, b:b + 1], ALU.mult), reads=["ktoks", "ident_f"], writes=[("kmask", b % 2)])
                S.group("pe", [mm(ps[pb][:, h * 128:(h + 1) * 128], km[0:16, h * 128:(h + 1) * 128], dtok[0:16, h * 128:(h + 1) * 128]) for h in range(4)],
                        reads=[("kmask", b % 2), "dtok"], writes=[PK[pb]])
                for h in range(4):
                    S.op("dve", stt(S_all[:, b * 4 + h, :], S_all[:, b * 4 + h, :], eg_bc[:, h * 16 + b:h * 16 + b + 1], ps[pb][:, h * 128:(h + 1) * 128], ALU.mult, ALU.add),
                         reads=[PK[pb], "bcs", ("S_all", b // 4)], writes=[("S_all", b // 4)])
            S.group("pe", [mm(ps[1][:, 64 + h * 16 + b:64 + h * 16 + b + 1], S_all[:, b * 4 + h, :], qks_f[:, h, b:b + 1]) for b in range(16) for h in range(4)],
                    reads=[("S_all", q4) for q4 in range(4)] + ["qks_f"], writes=[PK[1]])
            gdn_epilogue(ps[1][:, 64:128], ps[0][:, 128:192], 16, T_P, [PK[1]], PK[0])
            for q4 in range(4):
                S.dma("sp", "o_ngs%d" % q4, dmaf(ngs[q4 * 4:(q4 + 1) * 4].rearrange("b h d e -> d (b h) e"), S_all[:, q4 * 16:(q4 + 1) * 16, :]), reads=[("S_all", q4)])
        if _DO_SAMPLE:
            _sample_section()
        S.barrier()

        LB = Bump(arena, X0, ARENA)
        NL = 8
        lf32 = lambda shp: [LB.alloc(F32, shp) for _ in range(NL)]
        lbf = lambda shp: [LB.alloc(BF16, shp) for _ in range(NL)]
        Rs8 = lf32([128, 256]); rhsR8 = lf32([128, 256]); D08 = lf32([128, 128]); EGr8 = lf32([128, 128]); MB8 = lf32([128, 128]); Qf8 = lf32([128, 128])
        NNa8 = lf32([128, 256]); NNb8 = lf32([128, 256]); rn8 = lf32([128, 128]); on8 = lf32([128, 128])
        Xs8 = lbf([128, 128]); Kd8 = lbf([128, 128]); Vb8 = lbf([128, 128]); qg8 = lbf([128, 128]); at8 = lbf([128, 128])
        Qm8 = lbf([128, 128]); nWT8 = lbf([128, 128]); vn8 = lbf([128, 128]); osq8 = lbf([128, 128])
        r_done = {}

        def gdn_unit(n, h):
            ops = []
            Lop, Lgr, Ldma = mk_recorders(S, ops)
            L = h * 2 + n % 2
            tok = slice(n * 128, (n + 1) * 128)
            kT = qkv[:, 4 + h, tok]; qT = qkv[:, h, tok]; vT = qkv[:, 8 + h, tok]
            col = lambda t: t[:, n, h:h + 1]
            K = lambda name: (name, L)
            bk = PK[L]; pb = ps[L]; pbb = psb[L]
            kk = pb[:, 256:384]; qk = pb[:, 384:512]
            Rs, rhsR, D0, EGr, MB, Qf = Rs8[L], rhsR8[L], D08[L], EGr8[L], MB8[L], Qf8[L]
            Xs, Kd, Vb, qg, attnT, Qm, nWT, vn, osq = Xs8[L], Kd8[L], Vb8[L], qg8[L], at8[L], Qm8[L], nWT8[L], vn8[L], osq8[L]
            Lop("pool", stt_pool(rhsR[:, 0:128], mask_incl, col(g_c)), reads=["mask_incl", "g_c"], writes=[K("rhsR")])
            Lop("pool", stt_pool(rhsR[:, 128:256], ident_f, col(beta_c)), reads=["ident_f", "beta_c"], writes=[K("rhsR")])
            Lgr("pe", [mm(pb[:, 0:128], mask_sl, rhsR[:, 0:128]), mm(pb[:, 128:256], ones_f, rhsR[:, 0:128]),
                       mm(pb[:, 256:384], nmask_sl, rhsR[:, 128:256]),
                       lambda e: e.transpose(out=pbb[:, 768:896], in_=kT, identity=ident_b),
                       lambda e: e.transpose(out=pbb[:, 896:1024], in_=vT, identity=ident_b)],
                reads=[K("rhsR"), "ones_f", "mask_sl", "nmask_sl", "ident_b"], writes=[bk])
            ktok = pbb[:, 768:896]; vtok = pbb[:, 896:1024]
            Lop("act", actf(Rs, pb[:, 0:256], AF.Exp), reads=[bk], writes=[K("Rs")])
            D0 = Rs[:, 0:128]; EGr = Rs[:, 128:256]
            Lop("act", actf(Xs, ktok, AF.Copy, scale=col(bexp_c)), reads=[bk, "bexp_c"], writes=[K("Xs")])
            Lop("act", actf(Kd, ktok, AF.Copy, scale=col(kd_c)), reads=[bk, "kd_c"], writes=[K("Kd")])
            Lop("act", actf(Vb, vtok, AF.Copy, scale=col(beta_c)), reads=[bk, "beta_c"], writes=[K("Vb")])
            Lop("act", actf(MB, pb[:, 256:384], AF.Copy), reads=[bk], writes=[K("MB")])
            Lop("pool", tt(MB, MB, D0, ALU.mult), reads=[K("MB"), K("Rs")], writes=[K("MB")])
            Lop("pool", tt(qg, qT, EGr, ALU.mult), reads=[K("Rs")], writes=[K("qg")])
            Lgr("pe", [mm(pb[:, 0:128], kT, kT), mm(pb[:, 128:256], kT, qT)], reads=[], writes=[bk])
            kk = pb[:, 0:128]; qk = pb[:, 128:256]
            Lop("pool", tt(D0, D0, mask_incl, ALU.mult), reads=[K("Rs"), K("MB"), K("qg"), "mask_incl"], writes=[K("Rs")])
            NNa, NNb = NNa8[L], NNb8[L]
            Lop("dve", tt(NNa[:, 0:128], kk, MB, ALU.mult), reads=[bk, K("MB")], writes=[K("NNa")])
            Lop("dve", tt(attnT, qk, D0, ALU.mult), reads=[bk, K("Rs")], writes=[K("attnT")])
            Lgr("pe", [mm(pb[:, 256:384], NNa[:, 0:128], ident_f)], reads=[K("NNa"), "ident_f"], writes=[bk])
            Lop("act", actf(NNa[:, 128:256], pb[:, 256:384], AF.Copy), reads=[bk], writes=[K("NNa")])
            Lop("pool", tt(Qf, ident_f, NNa[:, 0:128], ALU.add), reads=["ident_f", K("NNa")], writes=[K("Qf")])
            cur, nxt, kc, kn = NNa, NNb, K("NNa"), K("NNb")
            for j in range(1, 7):
                fns = []
                if j < 6:
                    fns.append(mm(pb[:, 0:128], cur[:, 128:256], cur[:, 0:128]))
                fns.append(mm(pb[:, 128:256], cur[:, 0:128], cur[:, 128:256]))
                Lgr("pe", fns, reads=[kc], writes=[bk])
                lo = 0 if j < 6 else 128
                if j in (3, 5):
                    Lop("dve", cp(nxt[:, lo:256], pb[:, lo:256]), reads=[bk], writes=[kn])
                else:
                    Lop("act", actf(nxt[:, lo:256], pb[:, lo:256], AF.Copy), reads=[bk], writes=[kn])
                Lgr("pe", [mm(pb[:, 256:384], nxt[:, 128:256], Qf)], reads=[kn, K("Qf")], writes=[bk])
                Lop("dve", tt(Qf, Qf, pb[:, 256:384], ALU.add), reads=[bk, K("Qf")], writes=[K("Qf")])
                cur, nxt, kc, kn = nxt, cur, kn, kc
            Lop("pool", cp(Qm, Qf), reads=[K("Qf")], writes=[K("Qm")])
            Lgr("pe", [mm(pb[:, 384:512], Xs, Qm)], reads=[K("Xs"), K("Qm")], writes=[bk])
            Lop("act", actf(nWT, pb[:, 384:512], AF.Copy, scale=negone), reads=[bk], writes=[K("nWT")])
            V = pb[:, 384:512]; Oh = pb[:, 0:128]; SSh = pb[:, 128:256]

            def chk():
                assert n == 0 or r_done.get((n - 1, h)), ("emission order violated", n, h)
            ops.append(chk)
            Lgr("pe", [mm(V, Qm, Vb, start=True, stop=False), mm(V, nWT, S_b[:, h, :], start=False, stop=True)],
                reads=[K("Qm"), K("Vb"), K("nWT"), ("S_b", h)], writes=[bk])
            Lop("dve", cp(vn, V), reads=[bk], writes=[K("vn")])
            Lgr("pe", [mm(V, Kd, vn),
                       mm(Oh, S_b[:, h, :], qg, start=True, stop=False), mm(Oh, vn, attnT, start=False, stop=True)],
                reads=[K("Kd"), K("vn"), ("S_b", h), K("qg"), K("attnT")], writes=[bk])
            Lop("dve", stt(S_f[:, h, :], S_f[:, h, :], col(egl_c), V, ALU.mult, ALU.add), reads=[bk, ("S_f", h), "egl_c"], writes=[("S_f", h)])
            Lop("pool", cp(S_b[:, h, :], S_f[:, h, :]), reads=[("S_f", h)], writes=[("S_b", h)])

            def mark():
                r_done[(n, h)] = True
            ops.append(mark)
            rn, on = rn8[L], on8[L]
            Lop("act", actf(osq, Oh, AF.Square), reads=[bk], writes=[K("osq")])
            Lgr("pe", [mm(SSh, ones_b, osq)], reads=[K("osq"), "ones_b"], writes=[bk])
            Lop("dve", ts(rn, SSh, 1.0 / 128, ALU.mult, EPS, ALU.add), reads=[bk], writes=[K("rn")])
            Lop("act", actf(rn, rn, AF.Sqrt), reads=[K("rn")], writes=[K("rn")])
            Lop("dve", lambda e: e.reciprocal(out=rn, in_=rn), reads=[K("rn")], writes=[K("rn")])
            Lop("dve", stt(on, Oh, gdnn_col, rn, ALU.mult, ALU.mult), reads=[bk, K("rn"), "cols"], writes=[K("on")])
            Lop("pool", tt(cat[:, h, tok], on, zs[:, h, tok], ALU.mult), reads=[K("on")], writes=[("cat_o", n, h)])
            return ops

        if NCH:
            u0 = gdn_unit(0, 0)
            LU = len(u0)
            STAG8 = int(os.environ.get('GDN_STAG', '22'))
            lanes = []
            for h in range(4):
                for par in range(2):
                    pad = h * STAG8 + par * (LU // 2)
                    lane = [(lambda: None)] * pad
                    for n in range(par, NCH, 2):
                        lane = lane + gdn_unit(n, h)
                    lanes.append(lane)
            zipper(lanes)
        S.dma("sp", "o_ngp", dmaf(ngp.rearrange("h d e -> d h e"), S_f), reads=[("S_f", h) for h in range(4)])

        S.barrier()

        if 'phasec' in _SK:
            S.finish()
            with nc.Block() as block:
                S.replay(block)
            return nc
        YC = Bump(arena, P_C0, ARENA)
        R = YC.alloc(F32, [128, 8, 528]); xnC = YC.alloc(BF16, [128, 8, 528]); hid = YC.alloc(BF16, [128, 32, 528])
        r8 = [YC.alloc(BF16, [128, 8, 512]) for _ in range(3)]
        r16 = [YC.alloc(BF16, [128, 32, 256]) for _ in range(2)]
        xres = YC.alloc(F32, [128, 4, 1024]); xres_s = YC.alloc(F32, [128, 1024])
        pw = YC.alloc(BF16, [128, 2, 1024]); ptok = [YC.alloc(BF16, [128, 256]) for _ in range(2)]
        pT = YC.alloc(BF16, [128, 2, 528]); sqr = [YC.alloc(BF16, [128, 528]) for _ in range(2)]; rnC = YC.alloc(F32, [128, 528])
        sig = [YC.alloc(F32, [128, 528]) for _ in range(2)]; relu_t = [YC.alloc(F32, [128, 528]) for _ in range(2)]
        ytile = [YC.alloc(F32, [128, 1024]) for _ in range(1)]
        rncol = YC.alloc(F32, [128, 8])
        r8_n = [0]; r16_n = [0]; misc_n = [0]

        r8_seq = []
        for _p in range(4):
            r8_seq += [(w_out_v, 0), (w_out_v, 512)] + [(w_up_v, bb * 512) for bb in range(8)] + [(w_gate_v, 0), (w_gate_v, 512)]
        r8_issued = [0]

        def load_r8(view, c0):
            idx = r8_n[0]
            r8_n[0] += 1
            assert r8_seq[idx][1] == c0
            while r8_issued[0] < min(len(r8_seq), idx + 3):
                j = r8_issued[0]
                vw, cc = r8_seq[j]
                if not (os.environ.get("NORELOAD") and j >= 12):
                    S.dma("pool", "r8_%d" % (j % 3), dmaf(r8[j % 3], vw[:, :, cc:cc + 512]), writes=[("r8", j % 3)])
                r8_issued[0] += 1
            return idx % 3

        def load_r16(c0):
            sl = r16_n[0] % 2
            r16_n[0] += 1
            if not (os.environ.get("NORELOAD") and r16_n[0] > 4):
                S.dma("pool", "r16_%d" % sl, dmaf(r16[sl], w_down_v[:, :, c0:c0 + 256]), writes=[("r16", sl)])
            return sl

        S.dma("pool", "pw", dmaf(pw, w_ple_v), writes=["pw"])
        PASSES = [[(0, 512, 0)], [(512, 512, 0)], [(1024, 512, 0)], [(1536, 512, 0), (2048, 16, 512)]]

        def rms_norm_C(which, out_fn, segs, W, tag):
            bns = []
            for (t0, n, l0) in segs:
                bns.append(6 + misc_n[0] % 2)
                misc_n[0] += 1
            for m in range(8):
                sq = sqr[m % 2]
                S.op("act", actf(sq[:, 0:W], R[:, m, 0:W], AF.Square), reads=[("R", m)], writes=[("sqr", m % 2)])
                for si_, (t0, n, l0) in enumerate(segs):
                    bn = bns[si_]
                    S.group("pe", [mm(ps[bn][:, 0:n], ones_b, sq[:, l0:l0 + n], start=(m == 0), stop=(m == 7))],
                            reads=[("sqr", m % 2), "ones_b"], writes=[PK[bn]])
            for si_, (t0, n, l0) in enumerate(segs):
                bn = bns[si_]
                S.op("dve", ts(rnC[:, l0:l0 + n], ps[bn][:, 0:n], 1.0 / 1024, ALU.mult, EPS, ALU.add), reads=[PK[bn]], writes=["rnC"])
            S.op("act", actf(rnC[:, 0:W], rnC[:, 0:W], AF.Sqrt), reads=["rnC"], writes=["rnC"])
            S.op("dve", lambda e: e.reciprocal(out=rnC[:, 0:W], in_=rnC[:, 0:W]), reads=["rnC"], writes=["rnC"])
            for m in range(8):
                out_ap, wkey = out_fn(m)
                S.op("dve", stt(out_ap, R[:, m, 0:W], gcol(which, m), rnC[:, 0:W], ALU.mult, ALU.mult), reads=[("R", m), "rnC", "cols"], writes=[wkey])

        for pi, segs in enumerate(PASSES):
            W = sum(n for (_, n, _) in segs)
            t00 = segs[0][0]
            has_s = len(segs) > 1
            if pi == 0:
                S.dma("sp", "xres", dmaf(xres, x_p[0:512, :].rearrange("(j p) f -> p j f", p=128)), writes=["xres"])
            def stats_act(m):
                S.op("act", actf(sqr[m % 2][:, 0:W], R[:, m, 0:W], AF.Square), reads=[("R", m)], writes=[("sqr", m % 2)])

            def stats_pe(m, bns):
                for si_, (t0, n, l0) in enumerate(segs):
                    S.group("pe", [mm(ps[bns[si_]][:, 0:n], ones_b, sqr[m % 2][:, l0:l0 + n], start=(m == 0), stop=(m == 7))],
                            reads=[("sqr", m % 2), "ones_b"], writes=[PK[bns[si_]]])

            def norm_finish_row(bns, out_t, key, square):
                for si_, (t0, n, l0) in enumerate(segs):
                    S.op("dve", ts(out_t[:, l0:l0 + n], ps[bns[si_]][:, 0:n], 1.0 / 1024, ALU.mult, EPS, ALU.add), reads=[PK[bns[si_]]], writes=[key])
                if not square:
                    S.op("act", actf(out_t[:, 0:W], out_t[:, 0:W], AF.Sqrt), reads=[key], writes=[key])
                S.op("dve", lambda e: e.reciprocal(out=out_t[:, 0:W], in_=out_t[:, 0:W]), reads=[key], writes=[key])

            def pick_bns():
                o = []
                for _ in segs:
                    o.append(6 + misc_n[0] % 2)
                    misc_n[0] += 1
                return o

            bns1 = pick_bns()
            for blk in range(2):
                sl = load_r8(w_out_v, blk * 512)
                for m4 in range(4):
                    m = blk * 4 + m4
                    for (t0, n, l0) in segs:
                        b = next_bank()
                        fns = [mm(ps[b][:, 0:n], r8[sl][:, k, m4 * 128:(m4 + 1) * 128], cat[:, k, t0:t0 + n], start=(k == 0), stop=False) for k in range(8)]
                        if n == 512:
                            fns += [mm(ps[b][:, j * 128:(j + 1) * 128], xres[:, j, m * 128:(m + 1) * 128], ident_f, start=False, stop=(j == 3)) for j in range(4)]
                            rk = ["xres"]
                        else:
                            fns += [mm(ps[b][:, 0:16], xres_s[0:16, m * 128:(m + 1) * 128], ident_f[0:16, 0:16], start=False, stop=True)]
                            rk = ["xres_s"]
                        S.group("pe", fns, reads=[("r8", sl), "cat", "ident_f"] + rk, writes=[PK[b]])
                        S.op("act", actf(R[:, m, l0:l0 + n], ps[b][:, 0:n], AF.Copy), reads=[PK[b]], writes=[("R", m)])
                        S.op("act", actf(xnC[:, m, l0:l0 + n], ps[b][:, 0:n], AF.Copy, scale=gcol(0, m)), reads=[PK[b], "cols"], writes=[("xnC", m)])
                    stats_act(m)
                    if m >= 1:
                        stats_pe(m - 1, bns1)
            stats_pe(7, bns1)
            norm_finish_row(bns1, rnC, "rnC", True)
            if pi + 1 < len(PASSES):
                tn = PASSES[pi + 1][0][0]
                S.dma("sp", "xres", dmaf(xres, x_p[tn:tn + 512, :].rearrange("(j p) f -> p j f", p=128)), writes=["xres"])
                if len(PASSES[pi + 1]) > 1:
                    S.dma("sp", "xres_s", dmaf(xres_s[0:16, :], x_s), writes=["xres_s"])
            for blk in range(8):
                sl = load_r8(w_up_v, blk * 512)
                for m4 in range(4):
                    hc = blk * 4 + m4
                    for (t0, n, l0) in segs:
                        b = next_bank()
                        S.group("pe", [mm(ps[b][:, 0:n], r8[sl][:, k, m4 * 128:(m4 + 1) * 128], xnC[:, k, l0:l0 + n], start=(k == 0), stop=(k == 7)) for k in range(8)],
                                reads=[("r8", sl)] + [("xnC", k) for k in range(8)], writes=[PK[b]])
                        rt = relu_t[misc_n[0] % 2]; rkey = ("relu_t", misc_n[0] % 2)
                        misc_n[0] += 1
                        S.op("act", actf(rt[:, 0:n], ps[b][:, 0:n], AF.Relu), reads=[PK[b]], writes=[rkey])
                        S.op("dve", tt(hid[:, hc, l0:l0 + n], rt[:, 0:n], rt[:, 0:n], ALU.mult), reads=[rkey], writes=[("hid", hc)])
            bns2 = pick_bns()
            for blk in range(4):
                sl = load_r16(blk * 256)
                for m2 in range(2):
                    m = blk * 2 + m2
                    for (t0, n, l0) in segs:
                        b = next_bank()
                        S.group("pe", [mm(ps[b][:, 0:n], r16[sl][:, k, m2 * 128:(m2 + 1) * 128], hid[:, k, l0:l0 + n], start=(k == 0), stop=(k == 31)) for k in range(32)],
                                reads=[("r16", sl)] + [("hid", k) for k in range(32)], writes=[PK[b]])
                        sg = sig[misc_n[0] % 2]; skey = ("sig", misc_n[0] % 2)
                        misc_n[0] += 1
                        S.op("dve", tt(sg[:, 0:n], ps[b][:, 0:n], rnC[:, l0:l0 + n], ALU.mult), reads=[PK[b], "rnC"], writes=[skey])
                        S.op("dve", tt(R[:, m, l0:l0 + n], R[:, m, l0:l0 + n], sg[:, 0:n], ALU.add), reads=[skey, ("R", m)], writes=[("R", m)])
                    S.op("act", actf(xnC[:, m, 0:W], R[:, m, 0:W], AF.Copy, scale=gcol(1, m)), reads=[("R", m), "cols"], writes=[("xnC", m)])
                    stats_act(m)
                    if m >= 1:
                        stats_pe(m - 1, bns2)
            stats_pe(7, bns2)
            norm_finish_row(bns2, rnC, "rnC", False)
            for (t0, n, l0) in segs:
                ntile = (n + 127) // 128
                for j in range(ntile):
                    r = min(128, n - j * 128)
                    sl = misc_n[0] % 2
                    misc_n[0] += 1
                    src = p_p[t0 + j * 128:t0 + j * 128 + r, :] if n == 512 else p_s
                    S.dma("pool", "ptok%d" % sl, dmaf(ptok[sl][0:r, :], src), writes=[("ptok", sl)])
                    S.group("pe", [lambda e, kk=kk, sl=sl, r=r: e.transpose(out=psb[5][:, kk * 128:kk * 128 + r], in_=ptok[sl][0:r, kk * 128:(kk + 1) * 128], identity=ident_b[0:r, 0:r]) for kk in range(2)],
                            reads=[("ptok", sl), "ident_b"], writes=[PK[5]])
                    S.op("act", actf(pT[:, :, l0 + j * 128:l0 + j * 128 + r], psb[5][:, 0:256].rearrange("p (k t) -> p k t", k=2)[:, :, 0:r], AF.Copy), reads=[PK[5]], writes=["pT"])
            ntt = sum((n + 127) // 128 for (_, n, _) in segs)
            sigbufs = [(sig[0], ("sig", 0)), (sig[1], ("sig", 1)), (relu_t[0], ("relu_t", 0)), (relu_t[1], ("relu_t", 1))]

            def gate_chunk(m, sl, m4):
                ops = []
                Lop, Lgr, Ldma = mk_recorders(S, ops)
                for si_, (t0, n, l0) in enumerate(segs):
                    b = (2 * m + si_) % 4
                    pb_ = 6 + m % 2
                    sg, skey = sigbufs[(2 * m + si_) % 4]
                    Lgr("pe", [mm(ps[b][:, 0:n], r8[sl][:, k, m4 * 128:(m4 + 1) * 128], xnC[:, k, l0:l0 + n], start=(k == 0), stop=(k == 7)) for k in range(8)],
                        reads=[("r8", sl)] + [("xnC", k) for k in range(8)], writes=[PK[b]])
                    Lgr("pe", [mm(ps[pb_][:, 0:n], pw[:, kk, m * 128:(m + 1) * 128], pT[:, kk, l0:l0 + n], start=(kk == 0), stop=(kk == 1)) for kk in range(2)],
                        reads=["pw", "pT"], writes=[PK[pb_]])
                    Lop("dve", tt(sg[:, 0:n], ps[b][:, 0:n], rnC[:, l0:l0 + n], ALU.mult), reads=[PK[b], "rnC"], writes=[skey])
                    Lop("act", actf(sg[:, 0:n], sg[:, 0:n], AF.Sigmoid), reads=[skey], writes=[skey])
                    Lop("dve", tt(sg[:, 0:n], sg[:, 0:n], ps[pb_][:, 0:n], ALU.mult), reads=[PK[pb_], skey], writes=[skey])
                    Lop("dve", tt(R[:, m, l0:l0 + n], R[:, m, l0:l0 + n], sg[:, 0:n], ALU.add), reads=[skey, ("R", m)], writes=[("R", m)])
                sq = sqr[m % 2]
                Lop("act", actf(sq[:, 0:W], R[:, m, 0:W], AF.Square), reads=[("R", m)], writes=[("sqr", m % 2)])
                fns = []
                if m == 0:
                    fns.append(mm(ps[5][:, 256:256 + ntt], zeros_f, zeros_f[:, 0:ntt], start=True, stop=False))
                jt = 0
                for (t0, n, l0) in segs:
                    for j in range((n + 127) // 128):
                        r = min(128, n - j * 128)
                        fns.append(mm(ps[5][0:r, 256 + jt:257 + jt], sq[:, l0 + j * 128:l0 + j * 128 + r], ones_b[:, 0:1], start=False, stop=False))
                        jt += 1
                if m == 7:
                    fns.append(mm(ps[5][:, 256:256 + ntt], zeros_f, zeros_f[:, 0:ntt], start=False, stop=True))
                Lgr("pe", fns, reads=[("sqr", m % 2), "ones_b"], writes=[PK[5]])
                Lop("act", actf(R[:, m, 0:W], R[:, m, 0:W], AF.Copy, scale=gcol(2, m)), reads=[("R", m), ("sqr", m % 2), "cols"], writes=[("R", m)])
                return ops

            for blk in range(2):
                sl = load_r8(w_gate_v, blk * 512)
                chunks = [gate_chunk(blk * 4 + m4, sl, m4) for m4 in range(4)]
                zipper(chunks[0:2])
                zipper(chunks[2:4])
            S.op("dve", ts(rncol[:, 0:ntt], ps[5][:, 256:256 + ntt], 1.0 / 1024, ALU.mult, EPS, ALU.add), reads=[PK[5]], writes=["rncol"])
            S.op("act", actf(rncol[:, 0:ntt], rncol[:, 0:ntt], AF.Sqrt), reads=["rncol"], writes=["rncol"])
            S.op("dve", lambda e: e.reciprocal(out=rncol[:, 0:ntt], in_=rncol[:, 0:ntt]), reads=["rncol"], writes=["rncol"])
            jt = 0
            for (t0, n, l0) in segs:
                ntile = (n + 127) // 128
                for j in range(ntile):
                    r = min(128, n - j * 128)
                    ysl = 0
                    for half in range(2):
                        b = next_bank()
                        S.group("pe", [(lambda e, m4=m4, b=b, r=r, half=half, l0=l0, j=j: e.transpose(out=ps[b][0:r, m4 * 128:(m4 + 1) * 128], in_=R[:, half * 4 + m4, l0 + j * 128:l0 + j * 128 + r], identity=ident_f)) for m4 in range(4)],
                                reads=[("R", half * 4 + m4) for m4 in range(4)] + ["ident_f"], writes=[PK[b]])
                        S.op("act", actf(ytile[ysl][0:r, half * 512:(half + 1) * 512], ps[b][0:r, :], AF.Copy, scale=rncol[0:r, jt:jt + 1]), reads=[PK[b], "rncol"], writes=[("ytile", ysl)])
                    jt += 1
                    dst = y_p[t0 + j * 128:t0 + j * 128 + r, :] if n == 512 else y_s
                    S.dma("sp", "o_y%d" % ysl, dmaf(dst, ytile[ysl][0:r, :]), reads=[("ytile", ysl)])
        S.finish()
        with nc.Block() as block:
            S.replay(block)
    return nc


_PROG = {}


def _make_in_maps(inputs):
    f = lambda a: np.ascontiguousarray(np.asarray(a, dtype=np.float32))
    g = {k: f(v) for k, v in inputs.items()}
    shared = {
        "g_mix": g["g_mix"].reshape(1, 1024), "w_in": g["w_in"][0], "w_conv": g["w_conv"][0],
        "a_log": g["a_log"].reshape(1, 4), "dt_bias": g["dt_bias"].reshape(1, 4), "gdn_norm": g["gdn_norm"].reshape(1, 128),
        "ln_g": g["sgu_ln_g"].reshape(1, 512), "ln_b": g["sgu_ln_b"].reshape(1, 512), "w_s": g["w_s"][0],
        "b_s": g["b_s"].reshape(1, 512), "w_out": g["w_out"][0], "g_ff": g["g_ff"].reshape(8, 128), "w_up": g["w_up"][0],
        "w_down": g["w_down"][0], "g_ple": g["g_ple"].reshape(8, 128), "w_ple": g["w_ple"][0], "w_gate": g["w_ple_gate"][0],
        "g_fin": g["g_final"].reshape(8, 128),
    }
    maps = []
    for i in range(8):
        m = dict(shared)
        sl = slice(16 * i, 16 * i + 16)
        m["x_p"] = g["x_prompt"][i]
        m["x_s"] = g["x_sample"][sl, 0]
        m["st_conv"] = g["state_conv"][0, sl]
        m["st_gdn"] = g["state_gdn"][0, sl]
        m["p_p"] = g["p_prompt"][0, i]
        m["p_s"] = g["p_sample"][0, sl, 0]
        maps.append(m)
    return maps


def kernel(**inputs):
    if "nc" not in _PROG:
        _PROG["nc"] = build_program()
    nc = _PROG["nc"]
    maps = _make_in_maps(inputs)
    res = run_bass_kernel_spmd(nc, maps, core_ids=list(range(8)))
    R = res.results
    st = lambda name: np.stack([np.asarray(r[name], dtype=np.float32) for r in R])
    cc = lambda name: np.concatenate([np.asarray(r[name], dtype=np.float32) for r in R], axis=0)
    y_prompt = st("y_p")
    y_sample = cc("y_s")[:, None, :]
    new_conv_prompt = st("ncp")[None]
    new_gdn_prompt = st("ngp")[None]
    new_conv_sample = cc("ncs")[None]
    new_gdn_sample = cc("ngs")[None]
    new_sgu_v_sample = cc("nsv")[None, :, None, :]
    return (y_prompt, y_sample, new_conv_prompt, new_gdn_prompt, new_conv_sample, new_gdn_sample, new_sgu_v_sample)
```

```python
import os
import numpy as np
import concourse.bass as bass
import concourse.mybir as mybir
from concourse.bass_utils import run_bass_kernel_spmd

F32 = mybir.dt.float32
BF16 = mybir.dt.bfloat16
AF = mybir.ActivationFunctionType
ALU = mybir.AluOpType
AX = mybir.AxisListType


class Sched:
    ENGS = ("pe", "act", "dve", "pool", "sp")

    def __init__(self, nc, stack):
        self.nc = nc
        self.stack = stack
        self.streams = {e: [] for e in self.ENGS}
        self.esem = {e: stack.enter_context(nc.semaphore("c_" + e)) for e in self.ENGS[:4]}
        self.ecnt = {e: 0 for e in self.ENGS}
        self.waited = {e: {} for e in self.ENGS}
        self.res = {}
        self.dsem = {}
        self.sem_by_name = {}
        for e in self.ENGS[:4]:
            self.sem_by_name[self.esem[e].name] = self.esem[e]

    def _need(self, eng, ev, waits):
        if ev is None:
            return
        name, val, src = ev
        if src == eng and eng == "pe":
            return
        cur = waits.get(name, 0)
        if val > cur:
            waits[name] = val

    def _deps(self, eng, reads, writes):
        waits = {}
        for k in reads:
            r = self.res.get(k)
            if r is not None:
                self._need(eng, r[0], waits)
                if isinstance(k, tuple) and k and k[0] == "ps":
                    for ev in r[1]:
                        if ev[2] != eng:
                            self._need(eng, ev, waits)
        for k in writes:
            r = self.res.get(k)
            if r is not None:
                if r[0] is not None:
                    self._need(eng, r[0], waits)
                for ev in r[1]:
                    self._need(eng, ev, waits)
        out = []
        w = self.waited[eng]
        for name, val in waits.items():
            if w.get(name, 0) < val:
                w[name] = val
                out.append((name, val))
        return out

    def _commit(self, ev, reads, writes):
        for k in reads:
            r = self.res.setdefault(k, [None, []])
            r[1].append(ev)
        for k in writes:
            self.res[k] = [ev, []]

    def op(self, eng, fn, reads=(), writes=()):
        waits = self._deps(eng, reads, writes)
        self.ecnt[eng] += 1
        ev = (self.esem[eng].name, self.ecnt[eng], eng)
        self.streams[eng].append((waits, [fn], ("inc", self.esem[eng], 1)))
        self._commit(ev, reads, writes)
        return ev

    def group(self, eng, fns, reads=(), writes=()):
        waits = self._deps(eng, reads, writes)
        self.ecnt[eng] += 1
        ev = (self.esem[eng].name, self.ecnt[eng], eng)
        self.streams[eng].append((waits, list(fns), ("inc", self.esem[eng], 1)))
        self._commit(ev, reads, writes)
        return ev

    def dma(self, eng, slot, fn, reads=(), writes=(), n=1):
        if slot not in self.dsem:
            s = self.stack.enter_context(self.nc.semaphore("d_" + slot))
            self.dsem[slot] = [s, 0]
            self.sem_by_name[s.name] = s
        waits = self._deps(eng, reads, writes)
        d = self.dsem[slot]
        fns = fn if isinstance(fn, (list, tuple)) else [fn]
        d[1] += 16 * len(fns)
        ev = (d[0].name, d[1], "dma")
        self.streams[eng].append((waits, list(fns), ("dmainc", d[0], 16)))
        self._commit(ev, reads, writes)
        return ev

    def barrier(self, skip=()):
        evs = []
        for e in self.ENGS[:4]:
            if self.ecnt[e] > 0:
                evs.append((self.esem[e].name, self.ecnt[e]))
        for slot, (s, c) in self.dsem.items():
            if c > 0 and not any(slot.startswith(p) for p in skip):
                evs.append((s.name, c))
        for eng in self.ENGS:
            w = self.waited[eng]
            waits = []
            for name, val in evs:
                if w.get(name, 0) < val:
                    w[name] = val
                    waits.append((name, val))
            if waits:
                self.streams[eng].append((waits, [], None))
        self.res.clear()

    def finish(self):
        eng = "sp"
        waits = []
        for slot, (s, c) in self.dsem.items():
            if c > 0:
                waits.append((s.name, c))
        for e in self.ENGS[:4]:
            if self.ecnt[e] > 0:
                waits.append((self.esem[e].name, self.ecnt[e]))
        self.streams[eng].append((waits, [], None))

    def replay(self, block):
        sbn = self.sem_by_name

        def run(e, items):
            for waits, fns, inc in items:
                for name, val in waits:
                    e.wait_ge(sbn[name], val)
                last = None
                for i, f in enumerate(fns):
                    ins = f(e)
                    if inc is not None and inc[0] == "dmainc":
                        ins.then_inc(inc[1], 16)
                    last = ins
                if inc is not None and inc[0] == "inc" and last is not None:
                    last.then_inc(inc[1], 1)

        st = self.streams

        @block.tensor
        def _(e):
            run(e, st["pe"])

        @block.scalar
        def _(e):
            run(e, st["act"])

        @block.vector
        def _(e):
            run(e, st["dve"])

        @block.gpsimd
        def _(e):
            run(e, st["pool"])

        @block.sync
        def _(e):
            run(e, st["sp"])


U8 = mybir.dt.uint8
T_P = 2048
T_S = 16
T_ALL = T_P + T_S
SEGS = [(0, 512), (512, 512), (1024, 512), (1536, 512), (2048, 16)]
NT = 17
EPS = 1e-6
D_IN = 3080
C_Q, C_K, C_V, C_Z, C_BA, C_U, C_VS = 0, 512, 1024, 1536, 2048, 2056, 2568


def mm(out, lhsT, rhs, start=True, stop=True):
    return lambda e: e.matmul(out, lhsT=lhsT, rhs=rhs, start=start, stop=stop)


def actf(out, in_, func, **kw):
    return lambda e: e.activation(out=out, in_=in_, func=func, **kw)


def tt(out, a, b, op):
    return lambda e: e.tensor_tensor(out=out, in0=a, in1=b, op=op)


def ts(out, a, s1, op0, s2=None, op1=None):
    if op1 is None:
        return lambda e: e.tensor_scalar(out=out, in0=a, scalar1=s1, scalar2=None, op0=op0)
    return lambda e: e.tensor_scalar(out=out, in0=a, scalar1=s1, scalar2=s2, op0=op0, op1=op1)


def stt(out, a, s, b, op0, op1):
    return lambda e: e.scalar_tensor_tensor(out=out, in0=a, scalar=s, in1=b, op0=op0, op1=op1)


def stt_pool(out, a, colap):
    return lambda e: e.tensor_tensor(out=out, in0=a, in1=colap.to_broadcast([128, 128]), op=ALU.mult)


def cp(out, in_):
    return lambda e: e.tensor_copy(out=out, in_=in_)


def dmaf(out, in_):
    return lambda e: e.dma_start(out=out, in_=in_)


class _Item:
    __slots__ = ("thunk", "eng", "reads", "writes", "dur")

    def __init__(self, thunk, eng, reads, writes, dur):
        self.thunk, self.eng, self.reads, self.writes, self.dur = thunk, eng, tuple(reads), tuple(writes), dur


_DUR = {"act": 0.5, "dve": 0.45, "pool": 0.5}


def mk_recorders(S, ops):
    def Lop(eng, fn, reads=(), writes=()):
        ops.append(_Item(lambda: S.op(eng, fn, reads=reads, writes=writes), eng, reads, writes, _DUR.get(eng, 0.4)))

    def Lgr(eng, fns, reads=(), writes=()):
        ops.append(_Item(lambda: S.group(eng, fns, reads=reads, writes=writes), eng, reads, writes, 0.1 + 0.13 * len(fns)))

    def Ldma(eng, slot, fn, reads=(), writes=()):
        ops.append(_Item(lambda: S.dma(eng, slot, fn, reads=reads, writes=writes), "q_" + eng, reads, writes, 2.5))
    return Lop, Lgr, Ldma


def zipper(lists):
    lists = [l for l in lists if l]
    idx = [0] * len(lists)
    if os.environ.get("ZIP", "rr") == "rr":
        live = True
        while live:
            live = False
            for i, l in enumerate(lists):
                if idx[i] < len(l):
                    it = l[idx[i]]
                    idx[i] += 1
                    live = True
                    if isinstance(it, _Item):
                        it.thunk()
                    else:
                        it()
        return
    t_eng, t_w, t_r = {}, {}, {}
    remaining = sum(len(l) for l in lists)
    while remaining:
        best = None
        for i, l in enumerate(lists):
            if idx[i] >= len(l):
                continue
            it = l[idx[i]]
            if not isinstance(it, _Item):
                best = (-1.0, i, it)
                break
            rdy = t_eng.get(it.eng, 0.0)
            for k in it.reads:
                rdy = max(rdy, t_w.get(k, 0.0))
            for k in it.writes:
                rdy = max(rdy, t_w.get(k, 0.0), t_r.get(k, 0.0))
            if best is None or rdy < best[0]:
                best = (rdy, i, it)
        rdy, i, it = best
        idx[i] += 1
        remaining -= 1
        if not isinstance(it, _Item):
            it()
            continue
        it.thunk()
        fin = rdy + it.dur
        if it.eng.startswith("q_"):
            t_eng[it.eng] = rdy + 0.1
        else:
            t_eng[it.eng] = fin
        for k in it.reads:
            t_r[k] = max(t_r.get(k, 0.0), fin)
        for k in it.writes:
            t_w[k] = fin
            t_r[k] = 0.0


def run_lanes(units, nl, stag):
    lanes = [[(lambda: None)] * (k * stag) for k in range(nl)]
    for i, u in enumerate(units):
        lanes[i % nl] += u
    zipper(lanes)


class Bump:
    def __init__(self, arena, start, limit):
        self.t, self.off, self.limit = arena, start, limit

    def alloc(self, dtype, shape):
        esz = 4 if dtype == F32 else 2
        n = 1
        for s in shape[1:]:
            n *= s
        nb = (n * esz + 63) // 64 * 64
        o = self.off
        self.off += nb
        assert self.off <= self.limit, ("SBUF arena overflow", self.off, self.limit)
        ap = self.t[:, o:o + n * esz].bitcast(dtype)
        if len(shape) == 3:
            ap = ap.rearrange("p (a b) -> p a b", a=shape[1])
        elif len(shape) == 4:
            ap = ap.rearrange("p (a b c) -> p a b c", a=shape[1], b=shape[2])
        return ap


def build_program():
    from contextlib import ExitStack
    nc = bass.Bass("TRN2", target_bir_lowering=False)

    def din(name, shape):
        return nc.dram_tensor(name, shape, F32, kind="ExternalInput").ap()

    def dout(name, shape):
        return nc.dram_tensor(name, shape, F32, kind="ExternalOutput").ap()

    x_p = din("x_p", [T_P, 1024]); x_s = din("x_s", [T_S, 1024])
    st_conv = din("st_conv", [T_S, 3, 1536]); st_gdn = din("st_gdn", [T_S, 4, 128, 128])
    p_p = din("p_p", [T_P, 256]); p_s = din("p_s", [T_S, 256])
    g_mix = din("g_mix", [1, 1024]); w_in = din("w_in", [1024, D_IN]); w_conv = din("w_conv", [4, 1536])
    a_log = din("a_log", [1, 4]); dt_bias = din("dt_bias", [1, 4]); gdn_norm = din("gdn_norm", [1, 128])
    ln_g = din("ln_g", [1, 512]); ln_b = din("ln_b", [1, 512]); w_s = din("w_s", [4, 128, 128]); b_s = din("b_s", [1, 512])
    w_out = din("w_out", [1024, 1024]); g_ff = din("g_ff", [8, 128]); w_up = din("w_up", [1024, 4096]); w_down = din("w_down", [4096, 1024])
    g_ple = din("g_ple", [8, 128]); w_ple = din("w_ple", [256, 1024]); w_gate = din("w_gate", [1024, 1024]); g_fin = din("g_fin", [8, 128])
    y_p = dout("y_p", [T_P, 1024]); y_s = dout("y_s", [T_S, 1024])
    ncp = dout("ncp", [3, 1536]); ngp = dout("ngp", [4, 128, 128])
    ncs = dout("ncs", [T_S, 3, 1536]); ngs = dout("ngs", [T_S, 4, 128, 128]); nsv = dout("nsv", [T_S, 512])

    w_in_v = w_in.rearrange("(k p) c -> p k c", p=128)
    w_out_v = w_out.rearrange("(k p) c -> p k c", p=128)
    w_up_v = w_up.rearrange("(k p) c -> p k c", p=128)
    w_down_v = w_down.rearrange("(k p) c -> p k c", p=128)
    w_gate_v = w_gate.rearrange("(k p) c -> p k c", p=128)
    w_ple_v = w_ple.rearrange("(k p) c -> p k c", p=128)

    with ExitStack() as st:
        S = Sched(nc, st)
        ARENA = 206 * 1024
        arena = st.enter_context(nc.sbuf_tensor("arena", [128, ARENA], U8))
        ps = [st.enter_context(nc.psum_tensor("ps%d" % i, [128, 512], F32)) for i in range(8)]
        psb = [p[:, :].bitcast(BF16) for p in ps]
        PK = [("ps", i) for i in range(8)]

        P = Bump(arena, 0, ARENA)
        ident_f = P.alloc(F32, [128, 128]); ident_b = P.alloc(BF16, [128, 128])
        ones_f = P.alloc(F32, [128, 128]); ones_b = P.alloc(BF16, [128, 128])
        mask_incl = P.alloc(F32, [128, 128])
        mask_su = P.alloc(F32, [128, 128])
        nmask_sl = P.alloc(F32, [128, 128])
        sel127 = P.alloc(F32, [128, 128])
        nmask_su = P.alloc(F32, [128, 128])
        mask_sl = P.alloc(F32, [128, 128])
        rowstage = P.alloc(F32, [128, 128])
        cols = P.alloc(F32, [128, 128])
        wsT = P.alloc(BF16, [128, 4, 128])
        selws = P.alloc(BF16, [128, 4, 16])
        ws00 = P.alloc(F32, [128, 4])
        bs_row = P.alloc(F32, [128, 4, 128])
        lng_row = P.alloc(F32, [128, 512]); lnb_row = P.alloc(F32, [128, 512])
        alog_row = P.alloc(F32, [128, 4]); dtb_row = P.alloc(F32, [128, 4]); nexpA_row = P.alloc(F32, [128, 4])
        zcol = P.alloc(F32, [128, 4])
        zeros_f = P.alloc(F32, [128, 128])
        cat = P.alloc(BF16, [128, 8, T_ALL])
        P_C0 = P.off
        ba = P.alloc(F32, [128, NT, 8])
        beta_c = P.alloc(F32, [128, NT, 4]); g_c = P.alloc(F32, [128, NT, 4]); gc_c = P.alloc(F32, [128, NT, 4])
        bexp_c = P.alloc(F32, [128, NT, 4]); kd_c = P.alloc(F32, [128, NT, 4]); egl_c = P.alloc(F32, [128, NT, 4])
        tmp68 = P.alloc(F32, [128, NT, 4])
        qkv = P.alloc(BF16, [128, 12, T_ALL])
        zs = P.alloc(BF16, [128, 4, T_ALL])
        qks_f = P.alloc(F32, [128, 12, 16])
        histT = P.alloc(F32, [128, 12, 3, 16])
        ncp_st = P.alloc(F32, [128, 12, 3]); ncs_st = P.alloc(F32, [128, 12, 16])
        S_f = P.alloc(F32, [128, 4, 128]); S_b = P.alloc(BF16, [128, 4, 128])
        X0 = P.off
        XB = Bump(arena, X0, ARENA)
        hT = XB.alloc(BF16, [128, 8, T_ALL])
        Y0 = XB.off

        def wcol(j, c):
            return cols[:, 24 + j * 12 + c: 24 + j * 12 + c + 1]

        def gcol(which, m):
            return cols[:, which * 8 + m: which * 8 + m + 1]
        gdnn_col = cols[:, 72:73]
        negone = zcol[:, 1:2]

        S.op("pool", lambda e: e.memset(ones_f, 1.0), writes=["ones_f"])
        S.op("pool", lambda e: e.memset(ones_b, 1.0), writes=["ones_b"])
        S.op("pool", lambda e: e.memset(zcol, 0.0), writes=["zcol"])
        S.op("pool", lambda e: e.memset(zcol[:, 1:2], -1.0), reads=["zcol"], writes=["zcol"])
        S.op("pool", lambda e: e.memset(zeros_f, 0.0), writes=["zeros_f"])
        S.op("pool", lambda e: e.affine_select(out=ident_f, in_=ones_f, pattern=[[-1, 128]], compare_op=ALU.is_equal, fill=0.0, base=0, channel_multiplier=1), reads=["ones_f"], writes=["ident_f"])
        S.op("pool", lambda e: e.affine_select(out=mask_incl, in_=ones_f, pattern=[[1, 128]], compare_op=ALU.is_ge, fill=0.0, base=0, channel_multiplier=-1), reads=["ones_f"], writes=["mask_incl"])
        S.op("pool", lambda e: e.affine_select(out=mask_su, in_=ones_f, pattern=[[1, 128]], compare_op=ALU.is_gt, fill=0.0, base=0, channel_multiplier=-1), reads=["ones_f"], writes=["mask_su"])
        S.op("pool", lambda e: e.affine_select(out=nmask_sl, in_=ones_f, pattern=[[-1, 128]], compare_op=ALU.is_gt, fill=0.0, base=0, channel_multiplier=1), reads=["ones_f"], writes=["nmask_sl"])
        S.op("pool", ts(nmask_sl, nmask_sl, -1.0, ALU.mult), reads=["nmask_sl"], writes=["nmask_sl"])
        S.op("pool", ts(nmask_su, mask_su, -1.0, ALU.mult), reads=["mask_su"], writes=["nmask_su"])
        S.op("pool", ts(mask_sl, nmask_sl, -1.0, ALU.mult), reads=["nmask_sl"], writes=["mask_sl"])
        S.op("pool", lambda e: e.affine_select(out=sel127, in_=ones_f, pattern=[[0, 128]], compare_op=ALU.is_equal, fill=0.0, base=-127, channel_multiplier=1), reads=["ones_f"], writes=["sel127"])
        S.op("dve", cp(ident_b, ident_f), reads=["ident_f"], writes=["ident_b"])
        S.op("pool", lambda e: e.memset(rowstage, 0.0), writes=["rowstage"])
        S.dma("sp", "c0", [dmaf(rowstage[0:8, :], g_ff), dmaf(rowstage[8:16, :], g_ple), dmaf(rowstage[16:24, :], g_fin),
                           dmaf(rowstage[24:72, :], w_conv.rearrange("j (c p) -> (j c) p", p=128)), dmaf(rowstage[72:73, :], gdn_norm)],
              writes=["rowstage"])
        S.group("pe", [mm(ps[0][:, 0:128], rowstage, ident_f)], reads=["rowstage", "ident_f"], writes=[PK[0]])
        S.op("dve", cp(cols, ps[0][:, 0:128]), reads=[PK[0]], writes=["cols"])
        S.dma("sp", "c1", [dmaf(bs_row.rearrange("p h t -> p (h t)"), b_s.partition_broadcast(128)),
                           dmaf(lng_row, ln_g.partition_broadcast(128)), dmaf(lnb_row, ln_b.partition_broadcast(128)),
                           dmaf(alog_row, a_log.partition_broadcast(128)), dmaf(dtb_row, dt_bias.partition_broadcast(128)),
                           ] + [dmaf(ws00[:, h:h + 1], w_s[h, 0, 0:1].partition_broadcast(128)) for h in range(4)],
              writes=["rows"])
        S.op("act", actf(nexpA_row, alog_row, AF.Exp), reads=["rows"], writes=["nexpA"])
        S.op("dve", ts(nexpA_row, nexpA_row, -1.0, ALU.mult), reads=["nexpA"], writes=["nexpA"])
        for h in range(4):
            S.op("dve", ts(selws[0:16, h, :], ident_f[0:16, 0:16], ws00[0:16, h:h + 1], ALU.mult), reads=["rows", "ident_f"], writes=[("selws", h)])

        YA = Bump(arena, Y0, ARENA)
        wstmp = YA.alloc(F32, [128, 4, 128])
        S.dma("sp", "c2", dmaf(wstmp, w_s.rearrange("h t s -> t h s")), writes=["wstmp"])
        for h in range(4):
            S.op("pool", lambda e, h=h: e.affine_select(out=wstmp[:, h, :], in_=wstmp[:, h, :], pattern=[[-1, 128]], compare_op=ALU.is_ge, fill=0.0, base=0, channel_multiplier=1),
                 reads=["wstmp"], writes=["wstmp"])
        S.group("pe", [mm(ps[1][:, h * 128:(h + 1) * 128], wstmp[:, h, :], ident_f) for h in range(4)], reads=["wstmp", "ident_f"], writes=[PK[1]])
        S.op("dve", cp(wsT.rearrange("p h t -> p (h t)"), ps[1][:, 0:512]), reads=[PK[1]], writes=["wsT"])

        gmix_row = YA.alloc(F32, [128, 1024])
        S.dma("sp", "c3", dmaf(gmix_row, g_mix.partition_broadcast(128)), writes=["gmix"])
        xt = [YA.alloc(F32, [128, 1024]) for _ in range(3)]
        xsq = [YA.alloc(F32, [128, 1024]) for _ in range(3)]
        xn = [YA.alloc(BF16, [128, 1024]) for _ in range(3)]
        stat = YA.alloc(F32, [128, NT, 2])

        def phaseA_tile(i):
            ops = []
            Lop, Lgr, Ldma = mk_recorders(S, ops)
            r = 128 if i < 16 else 16
            sl = i % 3
            src = x_p[i * 128:(i + 1) * 128, :] if i < 16 else x_s
            Ldma("sp", "xt%d" % sl, dmaf(xt[sl][0:r, :], src), writes=[("xt", sl)])
            Lop("act", actf(xsq[sl][0:r, :], xt[sl][0:r, :], AF.Square), reads=[("xt", sl)], writes=[("xsq", sl)])
            Lop("dve", lambda e: e.reduce_sum(out=stat[0:r, i, 0:1], in_=xsq[sl][0:r, :], axis=AX.X), reads=[("xsq", sl)], writes=[("stat", i)])
            Lop("dve", ts(stat[0:r, i, 1:2], stat[0:r, i, 0:1], 1.0 / 1024, ALU.mult, EPS, ALU.add), reads=[("stat", i)], writes=[("stat", i)])
            Lop("act", actf(stat[0:r, i, 1:2], stat[0:r, i, 1:2], AF.Sqrt), reads=[("stat", i)], writes=[("stat", i)])
            Lop("dve", lambda e: e.reciprocal(out=stat[0:r, i, 1:2], in_=stat[0:r, i, 1:2]), reads=[("stat", i)], writes=[("stat", i)])
            Lop("dve", stt(xn[sl][0:r, :], xt[sl][0:r, :], stat[0:r, i, 1:2], gmix_row[0:r, :], ALU.mult, ALU.mult),
                reads=[("xt", sl), ("stat", i), "gmix"], writes=[("xn", sl)])
            b = i % 3
            Lgr("pe", [lambda e, k=k: e.transpose(out=psb[b][:, k * 128:k * 128 + r], in_=xn[sl][0:r, k * 128:(k + 1) * 128], identity=ident_b[0:r, 0:r]) for k in range(8)],
                reads=[("xn", sl), "ident_b"], writes=[PK[b]])
            if i % 2 == 0:
                Lop("act", actf(hT[:, :, i * 128:i * 128 + r], psb[b].rearrange("p (k t) -> p k t", k=8)[:, :, 0:r], AF.Copy), reads=[PK[b]], writes=[("hT", i)])
            else:
                Lop("dve", cp(hT[:, :, i * 128:i * 128 + r], psb[b].rearrange("p (k t) -> p k t", k=8)[:, :, 0:r]), reads=[PK[b]], writes=[("hT", i)])
            return ops

        tilesA = [phaseA_tile(i) for i in range(NT)]
        run_lanes(tilesA, 3, int(os.environ.get("STAG_A", "2")))
        S.barrier()

        YB = Bump(arena, Y0, ARENA)
        wb = [YB.alloc(BF16, [128, 8, 512]) for _ in range(2)]
        wb8 = YB.alloc(BF16, [128, 8, 8])
        lnst = YB.alloc(F32, [128, NT, 8])
        sct = [YB.alloc(F32, [128, 3, 128]) for _ in range(2)]
        YB_MID = YB.off
        vg = YB.alloc(BF16, [128, NT, 512])
        F6 = YB.alloc(F32, [128, 6, 512])
        f512 = [F6[:, i, :] for i in range(6)]
        vgs_f = f512[5]
        YB5 = Bump(arena, YB_MID, ARENA)
        NBS = 4
        pre = [YB5.alloc(F32, [128, 515]) for _ in range(NBS)]
        accb = [YB5.alloc(F32, [128, 512]) for _ in range(NBS)]
        rnb = [YB5.alloc(F32, [128, 512]) for _ in range(NBS)]
        sqb = [YB5.alloc(BF16, [128, 512]) for _ in range(NBS)]
        wdiag = [YB5.alloc(F32, [128, 4, 128]) for _ in range(2)]
        YB6 = Bump(arena, YB_MID, ARENA)
        stage_tok = YB6.alloc(F32, [128, 1536])
        stage2 = YB6.alloc(F32, [128, 1536])
        wb_n = [0]
        bank_n = [0]

        def next_bank(lo=0, hi=4):
            b = lo + bank_n[0] % (hi - lo)
            bank_n[0] += 1
            return b

        wb_seq = [C_VS, C_U, C_Z, 0, 512, 1024]
        wb_issued = [0]

        def load_w(view, c0, ncol=512):
            idx = wb_n[0]
            wb_n[0] += 1
            assert wb_seq[idx] == c0
            while wb_issued[0] < min(len(wb_seq), idx + 2):
                j = wb_issued[0]
                S.dma("pool", "wb%d" % (j % 2), dmaf(wb[j % 2][:, :, 0:512], view[:, :, wb_seq[j]:wb_seq[j] + 512]), writes=[("wb", j % 2)])
                wb_issued[0] += 1
            return idx % 2

        hT_keys = [("hT", i) for i in range(NT)]

        def seg_hT_keys(t0, n):
            return [("hT", i) for i in range(t0 // 128, (t0 + n + 127) // 128)]

        sl = load_w(w_in_v, C_VS)

        def vsgu_tile(i):
            ops = []
            Lop, Lgr, Ldma = mk_recorders(S, ops)
            r = 128 if i < 16 else 16
            b = (i % 3)
            Lgr("pe", [mm(ps[b][0:r, :], hT[:, k, i * 128:i * 128 + r], wb[sl][:, k, :], start=(k == 0), stop=(k == 7)) for k in range(8)],
                    reads=[("hT", i), ("wb", sl)], writes=[PK[b]])
            g1 = f512[i % 3]; g2 = f512[3 + i % 3]
            Lop("act", actf(g1[0:r, :], ps[b][0:r, :], AF.Gelu_apprx_tanh), reads=[PK[b]], writes=[("g1", i % 3)])
            Lop("pool", tt(g2[0:r, :], g1[0:r, :], g1[0:r, :], ALU.mult), reads=[("g1", i % 3)], writes=[("g2", i % 3)])
            Lop("dve", lambda e, i=i, r=r, g1=g1: e.reduce_sum(out=lnst[0:r, i, 0:1], in_=g1[0:r, :], axis=AX.X), reads=[("g1", i % 3)], writes=[("lnst", i)])
            Lop("dve", lambda e, i=i, r=r, g2=g2: e.reduce_sum(out=lnst[0:r, i, 1:2], in_=g2[0:r, :], axis=AX.X), reads=[("g2", i % 3)], writes=[("lnst", i)])
            L = lambda a, bb: lnst[0:r, i, a:bb]
            Lop("dve", ts(L(2, 3), L(0, 1), 1.0 / 512, ALU.mult), reads=[("lnst", i)], writes=[("lnst", i)])
            Lop("dve", tt(L(3, 4), L(2, 3), L(2, 3), ALU.mult), reads=[("lnst", i)], writes=[("lnst", i)])
            Lop("dve", stt(L(4, 5), L(1, 2), 1.0 / 512, L(3, 4), ALU.mult, ALU.subtract), reads=[("lnst", i)], writes=[("lnst", i)])
            Lop("dve", ts(L(4, 5), L(4, 5), EPS, ALU.add), reads=[("lnst", i)], writes=[("lnst", i)])
            Lop("act", actf(L(4, 5), L(4, 5), AF.Sqrt), reads=[("lnst", i)], writes=[("lnst", i)])
            Lop("dve", lambda e, i=i, r=r: e.reciprocal(out=lnst[0:r, i, 5:6], in_=lnst[0:r, i, 4:5]), reads=[("lnst", i)], writes=[("lnst", i)])
            Lop("dve", ts(g2[0:r, :], g1[0:r, :], L(2, 3), ALU.subtract, L(5, 6), ALU.mult), reads=[("g1", i % 3), ("lnst", i)], writes=[("g2", i % 3)])
            Lop("pool", tt(g2[0:r, :], g2[0:r, :], lng_row[0:r, :], ALU.mult), reads=[("g2", i % 3), "rows"], writes=[("g2", i % 3)])
            if i < 16:
                Lop("pool", tt(vg[0:r, i, :], g2[0:r, :], lnb_row[0:r, :], ALU.add), reads=[("g2", i % 3), "rows"], writes=[("vg", i)])
            else:
                Lop("pool", tt(vgs_f[0:r, :], g2[0:r, :], lnb_row[0:r, :], ALU.add), reads=[("g2", i % 3), "rows"], writes=[("g2", 2)])
                Lop("pool", cp(vg[0:r, i, :], vgs_f[0:r, :]), reads=[("g2", 2)], writes=[("vg", i)])
                Ldma("sp", "o_nsv", dmaf(nsv, vgs_f[0:r, :]), reads=[("g2", 2)])
            return ops

        tilesV = [vsgu_tile(i) for i in range(NT)]
        run_lanes(tilesV, 3, int(os.environ.get("STAG_V", "4")))

        S.dma("pool", "wb8", dmaf(wb8, w_in_v[:, :, C_BA:C_BA + 8]), writes=["wb8"])
        S.op("pool", lambda e: e.memset(ba, 0.0), writes=["ba"])
        bq = 4
        for i in range(NT):
            r = 128 if i < 16 else 16
            S.group("pe", [mm(ps[bq][0:r, i * 8:(i + 1) * 8], hT[:, k, i * 128:i * 128 + r], wb8[:, k, :], start=(k == 0), stop=(k == 7)) for k in range(8)],
                    reads=[("hT", i), "wb8"], writes=[PK[bq]])
        S.op("dve", cp(ba[:, 0:16, :], ps[bq][:, 0:128].rearrange("p (i c) -> p i c", c=8)), reads=[PK[bq], "ba"], writes=["ba"])
        S.op("dve", cp(ba[0:16, 16, :], ps[bq][0:16, 128:136]), reads=[PK[bq], "ba"], writes=["ba"])
        S.op("act", actf(beta_c, ba[:, :, 0:4], AF.Sigmoid), reads=["ba"], writes=["beta_c"])
        S.op("dve", tt(tmp68, ba[:, :, 4:8], dtb_row.unsqueeze(1).to_broadcast([128, NT, 4]), ALU.add), reads=["ba", "rows"], writes=["tmp68"])
        S.op("act", actf(tmp68, tmp68, AF.Exp), reads=["tmp68"], writes=["tmp68"])
        S.op("act", actf(tmp68, tmp68, AF.Ln, bias=1.0), reads=["tmp68"], writes=["tmp68"])
        S.op("dve", tt(g_c, tmp68, nexpA_row.unsqueeze(1).to_broadcast([128, NT, 4]), ALU.mult), reads=["tmp68", "nexpA"], writes=["g_c"])
        g68 = g_c.rearrange("p i h -> p (i h)"); gc68 = gc_c.rearrange("p i h -> p (i h)")
        S.group("pe", [mm(ps[5][:, 0:68], mask_incl, g68)], reads=["mask_incl", "g_c"], writes=[PK[5]])
        S.op("dve", cp(gc68, ps[5][:, 0:68]), reads=[PK[5]], writes=["gc_c"])
        S.group("pe", [mm(ps[5][:, 128:196], sel127, gc68)], reads=["sel127", "gc_c"], writes=[PK[5]])
        S.op("dve", cp(egl_c.rearrange("p i h -> p (i h)"), ps[5][:, 128:196]), reads=[PK[5]], writes=["egl_c"])
        S.op("dve", tt(tmp68.rearrange("p i h -> p (i h)"), egl_c.rearrange("p i h -> p (i h)"), gc68, ALU.subtract), reads=["egl_c", "gc_c"], writes=["tmp68"])
        S.op("act", actf(egl_c, egl_c, AF.Exp), reads=["egl_c", "tmp68"], writes=["egl_c"])
        S.op("act", actf(kd_c, tmp68, AF.Exp), reads=["tmp68"], writes=["kd_c"])
        S.op("act", actf(bexp_c, gc_c, AF.Exp), reads=["gc_c"], writes=["bexp_c"])
        S.op("dve", tt(bexp_c, bexp_c, beta_c, ALU.mult), reads=["bexp_c", "beta_c"], writes=["bexp_c"])

        sl = load_w(w_in_v, C_U)
        for h in range(4):
            for (t0, n) in SEGS:
                b = next_bank()
                S.group("pe", [mm(ps[b][:, 0:n], wb[sl][:, k, h * 128:(h + 1) * 128], hT[:, k, t0:t0 + n], start=(k == 0), stop=(k == 7)) for k in range(8)],
                        reads=seg_hT_keys(t0, n) + [("wb", sl)], writes=[PK[b]])
                u = f512[bank_n[0] % 2]
                S.op("act", actf(u[:, 0:n], ps[b][:, 0:n], AF.Gelu_apprx_tanh), reads=[PK[b]], writes=[("u", bank_n[0] % 2)])
                b2 = 4 + bank_n[0] % 2
                if n == 512:
                    tiles = [t0 // 128 + j for j in range(4)]
                    S.group("pe", [mm(ps[b2][:, j * 128:(j + 1) * 128], vg[:, tiles[j], h * 128:(h + 1) * 128], wsT[:, h, :]) for j in range(4)],
                            reads=[("vg", ti) for ti in tiles] + ["wsT"], writes=[PK[b2]])
                    S.op("dve", tt(f512[4][:, :].rearrange("p (j t) -> p j t", j=4), ps[b2][:, :].rearrange("p (j t) -> p j t", j=4),
                                   bs_row[:, h:h + 1, :].to_broadcast([128, 4, 128]), ALU.add), reads=[PK[b2], "rows"], writes=["mixt"])
                else:
                    S.group("pe", [mm(ps[b2][:, 0:16], vg[0:16, 16, h * 128:(h + 1) * 128], selws[0:16, h, :])],
                            reads=[("vg", 16), ("selws", h)], writes=[PK[b2]])
                    S.op("dve", tt(f512[4][:, 0:16], ps[b2][:, 0:16], bs_row[:, h, 0:1].to_broadcast([128, 16]), ALU.add), reads=[PK[b2], "rows"], writes=["mixt"])
                S.op("pool", tt(cat[:, 4 + h, t0:t0 + n], f512[4][:, 0:n], u[:, 0:n], ALU.mult), reads=["mixt", ("u", bank_n[0] % 2)], writes=[("cat", 4 + h, t0)])

        sl = load_w(w_in_v, C_Z)
        for h in range(4):
            for (t0, n) in SEGS:
                b = next_bank()
                S.group("pe", [mm(ps[b][:, 0:n], wb[sl][:, k, h * 128:(h + 1) * 128], hT[:, k, t0:t0 + n], start=(k == 0), stop=(k == 7)) for k in range(8)],
                        reads=seg_hT_keys(t0, n) + [("wb", sl)], writes=[PK[b]])
                S.op("act", actf(zs[:, h, t0:t0 + n], ps[b][:, 0:n], AF.Silu), reads=[PK[b]], writes=[("zs", h, t0)])

        S.barrier()

        def qkv_unit(blk, h, si, ui, wsl):
            ops = []
            Lop, Lgr, Ldma = mk_recorders(S, ops)
            c = blk * 4 + h
            t0, n = SEGS[si]
            bs = ui % NBS
            pr, acc, sq, rn = pre[bs], accb[bs], sqb[bs], rnb[bs]
            kp, ka, ks, kr = ("pre", bs), ("acc", bs), ("sq", bs), ("rn", bs)
            b = ui % 4; bn = 4 + ui % 4
            Lgr("pe", [mm(ps[b][:, 0:n], wb[wsl][:, k, h * 128:(h + 1) * 128], hT[:, k, t0:t0 + n], start=(k == 0), stop=(k == 7)) for k in range(8)],
                reads=[("wb", wsl)], writes=[PK[b]])
            if si == 0:
                Lop("dve", lambda e: e.memset(pr[:, 0:3], 0.0), writes=[kp])
            elif si < 4:
                Lgr("pe", [mm(ps[bn][:, 0:3], wb[wsl][:, k, h * 128:(h + 1) * 128], hT[:, k, t0 - 3:t0], start=(k == 0), stop=(k == 7)) for k in range(8)],
                    reads=[("wb", wsl)], writes=[PK[bn]])
                Lop("dve", cp(pr[:, 0:3], ps[bn][:, 0:3]), reads=[PK[bn]], writes=[kp])
            Lop("act", actf(pr[:, 3:3 + n], ps[b][:, 0:n], AF.Copy), reads=[PK[b], kp], writes=[kp])
            if si == 3:
                Lop("dve", cp(ncp_st[:, c, :], pr[:, 512:515]), reads=[kp], writes=[("ncp_st", c)])
            if si < 4 and os.environ.get("CONV", "dve") == "dve":
                Lop("act", actf(acc[:, 0:n], pr[:, 3:3 + n], AF.Copy, scale=wcol(3, c)), reads=[kp, "cols"], writes=[ka])
                Lop("dve", stt(acc[:, 0:n], pr[:, 2:2 + n], wcol(2, c), acc[:, 0:n], ALU.mult, ALU.add), reads=[kp, ka, "cols"], writes=[ka])
                Lop("dve", stt(acc[:, 0:n], pr[:, 1:1 + n], wcol(1, c), acc[:, 0:n], ALU.mult, ALU.add), reads=[kp, ka, "cols"], writes=[ka])
                Lop("dve", stt(acc[:, 0:n], pr[:, 0:n], wcol(0, c), acc[:, 0:n], ALU.mult, ALU.add), reads=[kp, ka, "cols"], writes=[ka])
                Lop("act", actf(acc[:, 0:n], acc[:, 0:n], AF.Silu), reads=[ka], writes=[ka])
            elif si < 4:
                wd = wdiag[c % 2]
                bc_ = 4 + ui % 4
                fns = []
                for t4 in range(4):
                    for j in range(4):
                        fns.append(mm(ps[bc_][:, t4 * 128:(t4 + 1) * 128], wd[:, j, :], pr[:, j + t4 * 128:j + (t4 + 1) * 128], start=(j == 0), stop=(j == 3)))
                Lgr("pe", fns, reads=[kp, ("wdiag", c % 2)], writes=[PK[bc_]])
                Lop("act", actf(acc[:, 0:n], ps[bc_][:, 0:n], AF.Silu), reads=[PK[bc_]], writes=[ka])
            else:
                Lop("dve", cp(ncs_st[:, c, :], pr[:, 3:19]), reads=[kp], writes=[("ncs_st", c)])
                Lop("act", actf(acc[:, 0:n], pr[:, 3:3 + n], AF.Copy, scale=wcol(3, c)), reads=[kp, "cols"], writes=[ka])
                for j in (2, 1, 0):
                    Lop("dve", stt(acc[:, 0:n], histT[:, c, j, :], wcol(j, c), acc[:, 0:n], ALU.mult, ALU.add), reads=[("histT", c), ka, "cols"], writes=[ka])
                Lop("act", actf(acc[:, 0:n], acc[:, 0:n], AF.Silu), reads=[ka], writes=[ka])
            if blk == 2:
                Lop("pool", cp(qkv[:, c, t0:t0 + n], acc[:, 0:n]), reads=[ka], writes=[("qkv", c, t0)])
                if si == 4:
                    Lop("pool", cp(qks_f[:, c, :], acc[:, 0:16]), reads=[ka], writes=[("qks_f", c)])
            else:
                Lop("pool", tt(sq[:, 0:n], acc[:, 0:n], acc[:, 0:n], ALU.mult), reads=[ka], writes=[ks])
                Lgr("pe", [mm(ps[bn][:, 0:n], ones_b, sq[:, 0:n])], reads=[ks, "ones_b"], writes=[PK[bn]])
                Lop("act", actf(rn[:, 0:n], ps[bn][:, 0:n], AF.Sqrt, bias=1e-6), reads=[PK[bn]], writes=[kr])
                Lop("dve", lambda e: e.reciprocal(out=rn[:, 0:n], in_=rn[:, 0:n]), reads=[kr], writes=[kr])
                scl = (128.0 ** -0.5) if blk == 0 else 1.0
                Lop("dve", stt(qkv[:, c, t0:t0 + n], acc[:, 0:n], scl, rn[:, 0:n], ALU.mult, ALU.mult), reads=[ka, kr], writes=[("qkv", c, t0)])
                if si == 4:
                    Lop("dve", stt(qks_f[:, c, :], acc[:, 0:16], scl, rn[:, 0:16], ALU.mult, ALU.mult), reads=[ka, kr], writes=[("qks_f", c)])
            return ops

        ui = 0
        units = []
        for blk in range(3):
            wsl = (3 + blk) % 2
            for h in range(4):
                c = blk * 4 + h
                for si in range(5):
                    uo = qkv_unit(blk, h, si, ui, wsl)
                    pre_ops = []
                    if h == 0 and si == 0:
                        pre_ops.append(lambda blk=blk: load_w(w_in_v, blk * 512))
                    if si == 0:
                        scs = sct[c % 2]
                        pre_ops.append(lambda c=c, scs=scs: S.dma("sp", "sct%d" % (c % 2), dmaf(scs[0:16, :, :], st_conv[:, :, c * 128:(c + 1) * 128]), writes=[("sct", c % 2)]))
                        pre_ops.append(lambda c=c, scs=scs: S.group("pe", [mm(ps[4 + c % 4][:, j * 16:(j + 1) * 16], scs[0:16, j, :], ident_f[0:16, 0:16]) for j in range(3)],
                                                                   reads=[("sct", c % 2), "ident_f"], writes=[PK[4 + c % 4]]))
                        pre_ops.append(lambda c=c: S.op("dve", cp(histT[:, c, :, :], ps[4 + c % 4][:, 0:48].rearrange("p (j b) -> p j b", j=3)), reads=[PK[4 + c % 4]], writes=[("histT", c)]))
                    units.append(pre_ops + uo)
                    ui += 1
        run_lanes(units, 4, int(os.environ.get("STAG_Q", "2")))

        S.barrier()
        XS = Bump(arena, X0, ARENA)
        S_all = XS.alloc(F32, [128, 64, 128])
        if 'sample' not in os.environ.get('KSKIP', ''):
            for q4 in range(4):
                S.dma("sp", "sall%d" % q4, dmaf(S_all[:, q4 * 16:(q4 + 1) * 16, :], st_gdn[q4 * 4:(q4 + 1) * 4].rearrange("b h d e -> d (b h) e")), writes=[("S_all", q4)])
        S.group("pe", [mm(ps[c // 4][0:3, (c % 4) * 128:(c % 4 + 1) * 128], ncp_st[:, c, :], ident_f) for c in range(12)],
                reads=[("ncp_st", c) for c in range(12)] + ["ident_f"], writes=[PK[0], PK[1], PK[2]])
        for q3 in range(3):
            S.op("dve", cp(stage_tok[0:3, q3 * 512:(q3 + 1) * 512], ps[q3][0:3, :]), reads=[PK[q3]], writes=["stage_tok"])
        S.dma("sp", "o_ncp", dmaf(ncp, stage_tok[0:3, :]), reads=["stage_tok"])
        S.group("pe", [mm(ps[c // 4][0:16, (c % 4) * 128:(c % 4 + 1) * 128], ncs_st[:, c, :], ident_f) for c in range(12)],
                reads=[("ncs_st", c) for c in range(12)] + ["ident_f"], writes=[PK[0], PK[1], PK[2]])
        for q3 in range(3):
            S.op("act", actf(stage2[0:16, q3 * 512:(q3 + 1) * 512], ps[q3][0:16, :], AF.Copy), reads=[PK[q3]], writes=["stage2"])
        S.dma("sp", "o_ncs", [dmaf(ncs[:, 2, :], stage2[0:16, :]), dmaf(ncs[:, 0:2, :], st_conv[:, 1:3, :])], reads=["stage2"])
        S.barrier()

        YG = Bump(arena, Y0, ARENA)
        osq = YG.alloc(BF16, [128, 512]); rn_o = YG.alloc(F32, [128, 512]); on_o = YG.alloc(F32, [128, 512])
        YG_EPI = YG.off
        NPW, DP = 4, 8
        CHDT = F32
        gN = lambda n_, dt_, shp: [YG.alloc(dt_, shp) for _ in range(n_)]
        Rs = gN(NPW, F32, [128, 256]); rhsR = gN(NPW, F32, [128, 256]); D0 = gN(NPW, F32, [128, 128]); E0 = gN(NPW, F32, [128, 128])
        EGr = gN(NPW, F32, [128, 128]); MB = gN(NPW, F32, [128, 128]); Qf = gN(NPW, F32, [128, 128]); Qs = gN(NPW, BF16, [128, 128])
        NNa = gN(NPW, CHDT, [128, 256]); NNb = gN(NPW, CHDT, [128, 256]); Xs = gN(NPW, BF16, [128, 128])
        NHa = gN(NPW, BF16, [128, 256]); NHb = gN(NPW, BF16, [128, 256])
        J0 = int(os.environ.get('GDN_J0', '6'))
        Qm = gN(DP, BF16, [128, 128]); attnT = gN(DP, BF16, [128, 128]); Kd = gN(DP, BF16, [128, 128]); Vb = gN(DP, BF16, [128, 128])
        qg = gN(DP, BF16, [128, 128]); nWT = gN(DP, BF16, [128, 128]); vn = gN(4, BF16, [128, 128])
        S.op("pool", lambda e: e.memset(S_f.rearrange("p h e -> p (h e)"), 0.0), writes=[("S_f", h) for h in range(4)])
        S.op("pool", lambda e: e.memset(S_b.rearrange("p h e -> p (h e)"), 0.0), writes=[("S_b", h) for h in range(4)])

        def gdn_P(n, h):
            ops = []
            Lop, Lgr, Ldma = mk_recorders(S, ops)
            u = n * 4 + h
            q = u % NPW; s = u % DP
            tok = slice(n * 128, (n + 1) * 128)
            kT = qkv[:, 4 + h, tok]; qT = qkv[:, h, tok]; vT = qkv[:, 8 + h, tok]
            col = lambda t: t[:, n, h:h + 1]
            K = lambda name: (name, q)
            H = lambda name: (name, s)
            bk = PK[q]; pb = ps[q]; pbb = psb[q]
            kk = pb[:, 256:384]; qk = pb[:, 384:512]
            Lop("pool", stt_pool(rhsR[q][:, 0:128], mask_incl, col(g_c)), reads=["mask_incl", "g_c"], writes=[K("rhsR")])
            Lop("pool", stt_pool(rhsR[q][:, 128:256], ident_f, col(beta_c)), reads=["ident_f", "beta_c"], writes=[K("rhsR")])
            Lgr("pe", [lambda e: e.transpose(out=pbb[:, 0:128], in_=kT, identity=ident_b),
                       lambda e: e.transpose(out=pbb[:, 128:256], in_=vT, identity=ident_b)], reads=[("qkv", n), "ident_b"], writes=[bk])
            ktok = pbb[:, 0:128]; vtok = pbb[:, 128:256]
            Lop("act", actf(Xs[q], ktok, AF.Copy, scale=col(bexp_c)), reads=[bk, "bexp_c"], writes=[K("Xs")])
            Lop("act", actf(Kd[s], ktok, AF.Copy, scale=col(kd_c)), reads=[bk, "kd_c"], writes=[H("Kd")])
            Lop("act", actf(Vb[s], vtok, AF.Copy, scale=col(beta_c)), reads=[bk, "beta_c"], writes=[H("Vb")])
            Lgr("pe", [mm(pb[:, 0:128], ones_f, rhsR[q][:, 0:128]), mm(pb[:, 128:256], ones_f, rhsR[q][:, 128:256]),
                       mm(kk, kT, kT), mm(qk, kT, qT)], reads=[K("rhsR"), "ones_f", ("qkv", n)], writes=[bk])
            Lop("dve", cp(Rs[q], pb[:, 0:256]), reads=[bk], writes=[K("Rs")])
            R_gc = Rs[q][:, 0:128]; R_be = Rs[q][:, 128:256]
            Lop("pool", lambda e: e.tensor_tensor(out=D0[q], in0=R_gc, in1=col(gc_c).to_broadcast([128, 128]), op=ALU.subtract), reads=[K("Rs"), "gc_c"], writes=[K("D0")])
            Lop("pool", ts(D0[q], D0[q], 0.0, ALU.min), reads=[K("D0")], writes=[K("D0")])
            Lop("act", actf(D0[q], D0[q], AF.Exp), reads=[K("D0")], writes=[K("D0")])
            Lop("act", actf(EGr[q], R_gc, AF.Exp), reads=[K("Rs")], writes=[K("EGr")])
            Lop("pool", tt(MB[q], R_be, D0[q], ALU.mult), reads=[K("Rs"), K("D0")], writes=[K("MB")])
            Lop("pool", tt(MB[q], MB[q], nmask_su, ALU.mult), reads=[K("MB"), "nmask_su"], writes=[K("MB")])
            Lop("pool", tt(D0[q], D0[q], mask_incl, ALU.mult), reads=[K("D0"), K("MB"), "mask_incl"], writes=[K("D0")])
            Lop("pool", tt(qg[s], qT, EGr[q], ALU.mult), reads=[("qkv", n), K("EGr")], writes=[H("qg")])
            Lop("dve", tt(NNa[q][:, 0:128], kk, MB[q], ALU.mult), reads=[bk, K("MB")], writes=[K("NNa")])
            Lop("dve", tt(attnT[s], qk, D0[q], ALU.mult), reads=[bk, K("D0")], writes=[H("attnT")])
            Lgr("pe", [mm(pb[:, 0:128], NNa[q][:, 0:128], ident_f)], reads=[K("NNa"), "ident_f"], writes=[bk])
            Lop("act", actf(NNa[q][:, 128:256], pb[:, 0:128], AF.Copy), reads=[bk], writes=[K("NNa")])
            Lop("pool", tt(Qf[q], ident_f, NNa[q][:, 0:128], ALU.add), reads=["ident_f", K("NNa")], writes=[K("Qf")])
            cur, nxt, kc, kn = NNa[q], NNb[q], K("NNa"), K("NNb")
            cur16, nxt16, kc16, kn16 = NHa[q], NHb[q], K("NHa"), K("NHb")
            if J0 == 0:
                Lop("pool", cp(cur16, cur), reads=[kc], writes=[kc16])
            for j in range(1, 7):
                f32lvl = j <= J0
                src, ksrc = (cur, kc) if f32lvl else (cur16, kc16)
                fns = []
                if j < 6:
                    fns.append(mm(pb[:, 0:128], src[:, 128:256], src[:, 0:128]))
                fns.append(mm(pb[:, 128:256], src[:, 0:128], src[:, 128:256]))
                Lgr("pe", fns, reads=[ksrc], writes=[bk])
                lo = 0 if j < 6 else 128
                if f32lvl:
                    Lop("act", actf(nxt[:, lo:256], pb[:, lo:256], AF.Copy), reads=[bk], writes=[kn])
                    if j == J0 and j < 6:
                        Lop("pool", cp(nxt16[:, lo:256], nxt[:, lo:256]), reads=[kn], writes=[kn16])
                    Lgr("pe", [mm(pb[:, 256:384], nxt[:, 128:256], Qf[q])], reads=[kn, K("Qf")], writes=[bk])
                else:
                    Lop("act", actf(nxt16[:, lo:256], pb[:, lo:256], AF.Copy), reads=[bk], writes=[kn16])
                    Lop("pool", cp(Qs[q], Qf[q]), reads=[K("Qf")], writes=[K("Qs")])
                    Lgr("pe", [mm(pb[:, 256:384], nxt16[:, 128:256], Qs[q])], reads=[kn16, K("Qs")], writes=[bk])
                Lop("dve", tt(Qf[q], Qf[q], pb[:, 256:384], ALU.add), reads=[bk, K("Qf")], writes=[K("Qf")])
                cur, nxt, kc, kn = nxt, cur, kn, kc
                cur16, nxt16, kc16, kn16 = nxt16, cur16, kn16, kc16
            Lop("pool", cp(Qm[s], Qf[q]), reads=[K("Qf")], writes=[H("Qm")])
            Lgr("pe", [mm(pb[:, 384:512], Xs[q], Qm[s])], reads=[K("Xs"), H("Qm")], writes=[bk])
            Lop("act", actf(nWT[s], pb[:, 384:512], AF.Copy, scale=negone), reads=[bk], writes=[H("nWT")])
            return ops

        def gdn_R(n, h):
            ops = []
            Lop, Lgr, Ldma = mk_recorders(S, ops)
            u = n * 4 + h
            s = u % DP
            H = lambda name: (name, s)
            col = lambda t: t[:, n, h:h + 1]
            bR = PK[4]; ob = 5 + n % 2
            V = ps[4][:, h * 128:(h + 1) * 128]
            Lgr("pe", [mm(V, Qm[s], Vb[s], start=True, stop=False),
                       mm(V, nWT[s], S_b[:, h, :], start=False, stop=True)],
                reads=[H("Qm"), H("Vb"), H("nWT"), ("S_b", h)], writes=[bR])
            Lop("dve", cp(vn[h], V), reads=[bR], writes=[("vn", h)])
            Lgr("pe", [mm(V, Kd[s], vn[h])], reads=[H("Kd"), ("vn", h)], writes=[bR])
            Lgr("pe", [mm(ps[ob][:, h * 128:(h + 1) * 128], S_b[:, h, :], qg[s], start=True, stop=False),
                       mm(ps[ob][:, h * 128:(h + 1) * 128], vn[h], attnT[s], start=False, stop=True)],
                reads=[("S_b", h), H("qg"), ("vn", h), H("attnT")], writes=[PK[ob]])
            Lop("dve", stt(S_f[:, h, :], S_f[:, h, :], col(egl_c), V, ALU.mult, ALU.add), reads=[bR, ("S_f", h), "egl_c"], writes=[("S_f", h)])
            Lop("act", actf(S_b[:, h, :], S_f[:, h, :], AF.Copy), reads=[("S_f", h)], writes=[("S_b", h)])
            return ops

        def gdn_epilogue(o_ps, ss_ps, ncol, t0, okeys, sskey):
            w = 4 * ncol
            S.op("act", actf(osq[:, 0:w], o_ps, AF.Square), reads=okeys, writes=["osq"])
            S.group("pe", [mm(ss_ps, ones_b, osq[:, 0:w])], reads=["osq", "ones_b"], writes=[sskey])
            S.op("dve", ts(rn_o[:, 0:w], ss_ps, 1.0 / 128, ALU.mult, EPS, ALU.add), reads=[sskey], writes=["rn_o"])
            S.op("act", actf(rn_o[:, 0:w], rn_o[:, 0:w], AF.Sqrt), reads=["rn_o"], writes=["rn_o"])
            S.op("dve", lambda e: e.reciprocal(out=rn_o[:, 0:w], in_=rn_o[:, 0:w]), reads=["rn_o"], writes=["rn_o"])
            S.op("dve", stt(on_o[:, 0:w], o_ps, gdnn_col, rn_o[:, 0:w], ALU.mult, ALU.mult), reads=okeys + ["rn_o", "cols"], writes=["on_o"])
            S.op("pool", tt(cat[:, 0:4, t0:t0 + ncol], on_o[:, 0:w].rearrange("p (h t) -> p h t", h=4), zs[:, :, t0:t0 + ncol], ALU.mult),
                 reads=["on_o", "zs"], writes=[("cat_o", t0)])

        _SK = os.environ.get('KSKIP', '')
        NCH = 0 if 'prompt' in _SK else int(os.environ.get('GDN_N', '16'))

        def epi_ops(n):
            ob = 5 + n % 2
            return [lambda: gdn_epilogue(ps[ob][:, :], ps[7][:, :], 128, n * 128, [PK[ob]], PK[7])]

        _DO_SAMPLE = 'sample' not in _SK
        def _sample_section():
            YG = Bump(arena, YG_EPI, ARENA)
            sv = YG.alloc(F32, [128, 8])
            rexp = YG.alloc(F32, [128, 8, 16])
            bcs = YG.alloc(F32, [128, 128])
            dcol = YG.alloc(F32, [128, 64])
            dtok = YG.alloc(F32, [128, 512]); ktoks = YG.alloc(F32, [128, 512])
            kmask = [YG.alloc(F32, [128, 512]) for _ in range(2)]
            S.op("dve", cp(sv[0:16, 0:4], beta_c[0:16, 16, :]), reads=["beta_c"], writes=["sv"])
            S.op("act", actf(sv[0:16, 4:8], g_c[0:16, 16, :], AF.Exp), reads=["g_c"], writes=["sv"])
            for j in range(8):
                S.op("dve", ts(rexp[0:16, j, :], ident_f[0:16, 0:16], sv[0:16, j:j + 1], ALU.mult), reads=["sv", "ident_f"], writes=["rexp"])
            S.group("pe", [mm(ps[0][:, 0:128], ones_f[0:16, :], rexp[0:16, :, :].rearrange("p j b -> p (j b)"))], reads=["rexp", "ones_f"], writes=[PK[0]])
            S.op("dve", cp(bcs, ps[0][:, 0:128]), reads=[PK[0]], writes=["bcs"])
            beta_bc = bcs[:, 0:64]; eg_bc = bcs[:, 64:128]
            S.group("pe", [mm(ps[1][:, h * 16 + b:h * 16 + b + 1], S_all[:, b * 4 + h, :], qks_f[:, 4 + h, b:b + 1]) for b in range(16) for h in range(4)],
                    reads=[("S_all", q4) for q4 in range(4)] + ["qks_f"], writes=[PK[1]])
            S.op("dve", tt(dcol, ps[1][:, 0:64], eg_bc, ALU.mult), reads=[PK[1], "bcs"], writes=["dcol"])
            S.op("dve", tt(dcol, qks_f[:, 8:12, :].rearrange("p h b -> p (h b)"), dcol, ALU.subtract), reads=["dcol", "qks_f"], writes=["dcol"])
            S.op("dve", tt(dcol, dcol, beta_bc, ALU.mult), reads=["dcol", "bcs"], writes=["dcol"])
            S.group("pe", [mm(ps[2][0:16, h * 128:(h + 1) * 128], dcol[:, h * 16:(h + 1) * 16], ident_f) for h in range(4)], reads=["dcol", "ident_f"], writes=[PK[2]])
            S.group("pe", [mm(ps[3][0:16, h * 128:(h + 1) * 128], qks_f[:, 4 + h, :], ident_f) for h in range(4)], reads=["qks_f", "ident_f"], writes=[PK[3]])
            S.op("dve", cp(dtok[0:16, :], ps[2][0:16, :]), reads=[PK[2]], writes=["dtok"])
            S.op("act", actf(ktoks[0:16, :], ps[3][0:16, :], AF.Copy), reads=[PK[3]], writes=["ktoks"])
            for b in range(16):
                km = kmask[b % 2]; pb = 4 + b % 2
                S.op("dve", ts(km[0:16, :], ktoks[0:16, :], ident_f[0:16, b:b + 1], ALU.mult), reads=["ktoks", "ident_f"], writes=[("kmask", b % 2)])
                S.group("pe", [mm(ps[pb][:, h * 128:(h + 1) * 128], km[0:16, h * 128:(h + 1) * 128], dtok[0:16, h * 128:(h + 1) * 128]) for h in range(4)],
                        reads=[("kmask", b % 2), "dtok"], writes=[PK[pb]])
                for h in range(4):
                    S.op("dve", stt(S_all[:, b * 4 + h, :], S_all[:, b * 4 + h, :], eg_bc[:, h * 16 + b:h * 16 + b + 1], ps[pb][:, h * 128:(h + 1) * 128], ALU.mult, ALU.add),
                         reads=[PK[pb], "bcs", ("S_all", b // 4)], writes=[("S_all", b // 4)])
            S.group("pe", [mm(ps[1][:, 64 + h * 16 + b:64 + h * 16 + b + 1], S_all[:, b * 4 + h, :], qks_f[:, h, b:b + 1]) for b in range(16) for h in range(4)],
                    reads=[("S_all", q4) for q4 in range(4)] + ["qks_f"], writes=[PK[1]])
            gdn_epilogue(ps[1][:, 64:128], ps[0][:, 128:192], 16, T_P, [PK[1]], PK[0])
            for q4 in range(4):
                S.dma("sp", "o_ngs%d" % q4, dmaf(ngs[q4 * 4:(q4 + 1) * 4].rearrange("b h d e -> d (b h) e"), S_all[:, q4 * 16:(q4 + 1) * 16, :]), reads=[("S_all", q4)])
        if _DO_SAMPLE:
            _sample_section()
        S.barrier()

        LB = Bump(arena, X0, ARENA)
        NL = 8
        lf32 = lambda shp: [LB.alloc(F32, shp) for _ in range(NL)]
        lbf = lambda shp: [LB.alloc(BF16, shp) for _ in range(NL)]
        Rs8 = lf32([128, 256]); rhsR8 = lf32([128, 256]); D08 = lf32([128, 128]); EGr8 = lf32([128, 128]); MB8 = lf32([128, 128]); Qf8 = lf32([128, 128])
        NNa8 = lf32([128, 256]); NNb8 = lf32([128, 256]); rn8 = lf32([128, 128]); on8 = lf32([128, 128])
        Xs8 = lbf([128, 128]); Kd8 = lbf([128, 128]); Vb8 = lbf([128, 128]); qg8 = lbf([128, 128]); at8 = lbf([128, 128])
        Qm8 = lbf([128, 128]); nWT8 = lbf([128, 128]); vn8 = lbf([128, 128]); osq8 = lbf([128, 128])
        r_done = {}

        def gdn_unit(n, h):
            ops = []
            Lop, Lgr, Ldma = mk_recorders(S, ops)
            L = h * 2 + n % 2
            tok = slice(n * 128, (n + 1) * 128)
            kT = qkv[:, 4 + h, tok]; qT = qkv[:, h, tok]; vT = qkv[:, 8 + h, tok]
            col = lambda t: t[:, n, h:h + 1]
            K = lambda name: (name, L)
            bk = PK[L]; pb = ps[L]; pbb = psb[L]
            kk = pb[:, 256:384]; qk = pb[:, 384:512]
            Rs, rhsR, D0, EGr, MB, Qf = Rs8[L], rhsR8[L], D08[L], EGr8[L], MB8[L], Qf8[L]
            Xs, Kd, Vb, qg, attnT, Qm, nWT, vn, osq = Xs8[L], Kd8[L], Vb8[L], qg8[L], at8[L], Qm8[L], nWT8[L], vn8[L], osq8[L]
            Lop("pool", stt_pool(rhsR[:, 0:128], mask_incl, col(g_c)), reads=["mask_incl", "g_c"], writes=[K("rhsR")])
            Lop("pool", stt_pool(rhsR[:, 128:256], ident_f, col(beta_c)), reads=["ident_f", "beta_c"], writes=[K("rhsR")])
            Lgr("pe", [mm(pb[:, 0:128], mask_sl, rhsR[:, 0:128]), mm(pb[:, 128:256], ones_f, rhsR[:, 0:128]),
                       mm(pb[:, 256:384], nmask_sl, rhsR[:, 128:256]),
                       lambda e: e.transpose(out=pbb[:, 768:896], in_=kT, identity=ident_b),
                       lambda e: e.transpose(out=pbb[:, 896:1024], in_=vT, identity=ident_b)],
                reads=[K("rhsR"), "ones_f", "mask_sl", "nmask_sl", "ident_b"], writes=[bk])
            ktok = pbb[:, 768:896]; vtok = pbb[:, 896:1024]
            Lop("act", actf(Rs, pb[:, 0:256], AF.Exp), reads=[bk], writes=[K("Rs")])
            D0 = Rs[:, 0:128]; EGr = Rs[:, 128:256]
            Lop("act", actf(Xs, ktok, AF.Copy, scale=col(bexp_c)), reads=[bk, "bexp_c"], writes=[K("Xs")])
            Lop("act", actf(Kd, ktok, AF.Copy, scale=col(kd_c)), reads=[bk, "kd_c"], writes=[K("Kd")])
            Lop("act", actf(Vb, vtok, AF.Copy, scale=col(beta_c)), reads=[bk, "beta_c"], writes=[K("Vb")])
            Lop("act", actf(MB, pb[:, 256:384], AF.Copy), reads=[bk], writes=[K("MB")])
            Lop("pool", tt(MB, MB, D0, ALU.mult), reads=[K("MB"), K("Rs")], writes=[K("MB")])
            Lop("pool", tt(qg, qT, EGr, ALU.mult), reads=[K("Rs")], writes=[K("qg")])
            Lgr("pe", [mm(pb[:, 0:128], kT, kT), mm(pb[:, 128:256], kT, qT)], reads=[], writes=[bk])
            kk = pb[:, 0:128]; qk = pb[:, 128:256]
            Lop("pool", tt(D0, D0, mask_incl, ALU.mult), reads=[K("Rs"), K("MB"), K("qg"), "mask_incl"], writes=[K("Rs")])
            NNa, NNb = NNa8[L], NNb8[L]
            Lop("dve", tt(NNa[:, 0:128], kk, MB, ALU.mult), reads=[bk, K("MB")], writes=[K("NNa")])
            Lop("dve", tt(attnT, qk, D0, ALU.mult), reads=[bk, K("Rs")], writes=[K("attnT")])
            Lgr("pe", [mm(pb[:, 256:384], NNa[:, 0:128], ident_f)], reads=[K("NNa"), "ident_f"], writes=[bk])
            Lop("act", actf(NNa[:, 128:256], pb[:, 256:384], AF.Copy), reads=[bk], writes=[K("NNa")])
            Lop("pool", tt(Qf, ident_f, NNa[:, 0:128], ALU.add), reads=["ident_f", K("NNa")], writes=[K("Qf")])
            cur, nxt, kc, kn = NNa, NNb, K("NNa"), K("NNb")
            for j in range(1, 7):
                fns = []
                if j < 6:
                    fns.append(mm(pb[:, 0:128], cur[:, 128:256], cur[:, 0:128]))
                fns.append(mm(pb[:, 128:256], cur[:, 0:128], cur[:, 128:256]))
                Lgr("pe", fns, reads=[kc], writes=[bk])
                lo = 0 if j < 6 else 128
                if j in (3, 5):
                    Lop("dve", cp(nxt[:, lo:256], pb[:, lo:256]), reads=[bk], writes=[kn])
                else:
                    Lop("act", actf(nxt[:, lo:256], pb[:, lo:256], AF.Copy), reads=[bk], writes=[kn])
                Lgr("pe", [mm(pb[:, 256:384], nxt[:, 128:256], Qf)], reads=[kn, K("Qf")], writes=[bk])
                Lop("dve", tt(Qf, Qf, pb[:, 256:384], ALU.add), reads=[bk, K("Qf")], writes=[K("Qf")])
                cur, nxt, kc, kn = nxt, cur, kn, kc
            Lop("pool", cp(Qm, Qf), reads=[K("Qf")], writes=[K("Qm")])
            Lgr("pe", [mm(pb[:, 384:512], Xs, Qm)], reads=[K("Xs"), K("Qm")], writes=[bk])
            Lop("act", actf(nWT, pb[:, 384:512], AF.Copy, scale=negone), reads=[bk], writes=[K("nWT")])
            V = pb[:, 384:512]; Oh = pb[:, 0:128]; SSh = pb[:, 128:256]

            def chk():
                assert n == 0 or r_done.get((n - 1, h)), ("emission order violated", n, h)
            ops.append(chk)
            Lgr("pe", [mm(V, Qm, Vb, start=True, stop=False), mm(V, nWT, S_b[:, h, :], start=False, stop=True)],
                reads=[K("Qm"), K("Vb"), K("nWT"), ("S_b", h)], writes=[bk])
            Lop("dve", cp(vn, V), reads=[bk], writes=[K("vn")])
            Lgr("pe", [mm(V, Kd, vn),
                       mm(Oh, S_b[:, h, :], qg, start=True, stop=False), mm(Oh, vn, attnT, start=False, stop=True)],
                reads=[K("Kd"), K("vn"), ("S_b", h), K("qg"), K("attnT")], writes=[bk])
            Lop("dve", stt(S_f[:, h, :], S_f[:, h, :], col(egl_c), V, ALU.mult, ALU.add), reads=[bk, ("S_f", h), "egl_c"], writes=[("S_f", h)])
            Lop("pool", cp(S_b[:, h, :], S_f[:, h, :]), reads=[("S_f", h)], writes=[("S_b", h)])

            def mark():
                r_done[(n, h)] = True
            ops.append(mark)
            rn, on = rn8[L], on8[L]
            Lop("act", actf(osq, Oh, AF.Square), reads=[bk], writes=[K("osq")])
            Lgr("pe", [mm(SSh, ones_b, osq)], reads=[K("osq"), "ones_b"], writes=[bk])
            Lop("dve", ts(rn, SSh, 1.0 / 128, ALU.mult, EPS, ALU.add), reads=[bk], writes=[K("rn")])
            Lop("act", actf(rn, rn, AF.Sqrt), reads=[K("rn")], writes=[K("rn")])
            Lop("dve", lambda e: e.reciprocal(out=rn, in_=rn), reads=[K("rn")], writes=[K("rn")])
            Lop("dve", stt(on, Oh, gdnn_col, rn, ALU.mult, ALU.mult), reads=[bk, K("rn"), "cols"], writes=[K("on")])
            Lop("pool", tt(cat[:, h, tok], on, zs[:, h, tok], ALU.mult), reads=[K("on")], writes=[("cat_o", n, h)])
            return ops

        if NCH:
            u0 = gdn_unit(0, 0)
            LU = len(u0)
            STAG8 = int(os.environ.get('GDN_STAG', '22'))
            lanes = []
            for h in range(4):
                for par in range(2):
                    pad = h * STAG8 + par * (LU // 2)
                    lane = [(lambda: None)] * pad
                    for n in range(par, NCH, 2):
                        lane = lane + gdn_unit(n, h)
                    lanes.append(lane)
            zipper(lanes)
        S.dma("sp", "o_ngp", dmaf(ngp.rearrange("h d e -> d h e"), S_f), reads=[("S_f", h) for h in range(4)])

        S.barrier()

        if 'phasec' in _SK:
            S.finish()
            with nc.Block() as block:
                S.replay(block)
            return nc
        YC = Bump(arena, P_C0, ARENA)
        R = YC.alloc(F32, [128, 8, 528]); xnC = YC.alloc(BF16, [128, 8, 528]); hid = YC.alloc(BF16, [128, 32, 528])
        r8 = [YC.alloc(BF16, [128, 8, 512]) for _ in range(3)]
        r16 = [YC.alloc(BF16, [128, 32, 256]) for _ in range(2)]
        xres = YC.alloc(F32, [128, 4, 1024]); xres_s = YC.alloc(F32, [128, 1024])
        pw = YC.alloc(BF16, [128, 2, 1024]); ptok = [YC.alloc(BF16, [128, 256]) for _ in range(2)]
        pT = YC.alloc(BF16, [128, 2, 528]); sqr = [YC.alloc(BF16, [128, 528]) for _ in range(2)]; rnC = YC.alloc(F32, [128, 528])
        sig = [YC.alloc(F32, [128, 528]) for _ in range(2)]; relu_t = [YC.alloc(F32, [128, 528]) for _ in range(2)]
        ytile = [YC.alloc(F32, [128, 1024]) for _ in range(1)]
        rncol = YC.alloc(F32, [128, 8])
        RNCOL_INIT = [False]
        r8_n = [0]; r16_n = [0]; misc_n = [0]

        r8_seq = []
        for _p in range(4):
            r8_seq += [(w_out_v, 0), (w_out_v, 512)] + [(w_up_v, bb * 512) for bb in range(8)] + [(w_gate_v, 0), (w_gate_v, 512)]
        r8_issued = [0]

        def load_r8(view, c0):
            idx = r8_n[0]
            r8_n[0] += 1
            assert r8_seq[idx][1] == c0
            while r8_issued[0] < min(len(r8_seq), idx + 3):
                j = r8_issued[0]
                vw, cc = r8_seq[j]
                if not (os.environ.get("NORELOAD") and j >= 12):
                    S.dma("pool", "r8_%d" % (j % 3), dmaf(r8[j % 3], vw[:, :, cc:cc + 512]), writes=[("r8", j % 3)])
                r8_issued[0] += 1
            return idx % 3

        def load_r16(c0):
            sl = r16_n[0] % 2
            r16_n[0] += 1
            if not (os.environ.get("NORELOAD") and r16_n[0] > 4):
                S.dma("pool", "r16_%d" % sl, dmaf(r16[sl], w_down_v[:, :, c0:c0 + 256]), writes=[("r16", sl)])
            return sl

        S.dma("pool", "pw", dmaf(pw, w_ple_v), writes=["pw"])
        PASSES = [[(0, 512, 0)], [(512, 512, 0)], [(1024, 512, 0)], [(1536, 512, 0), (2048, 16, 512)]]

        def rms_norm_C(which, out_fn, segs, W, tag):
            bns = []
            for (t0, n, l0) in segs:
                bns.append(6 + misc_n[0] % 2)
                misc_n[0] += 1
            for m in range(8):
                sq = sqr[m % 2]
                S.op("act", actf(sq[:, 0:W], R[:, m, 0:W], AF.Square), reads=[("R", m)], writes=[("sqr", m % 2)])
                for si_, (t0, n, l0) in enumerate(segs):
                    bn = bns[si_]
                    S.group("pe", [mm(ps[bn][:, 0:n], ones_b, sq[:, l0:l0 + n], start=(m == 0), stop=(m == 7))],
                            reads=[("sqr", m % 2), "ones_b"], writes=[PK[bn]])
            for si_, (t0, n, l0) in enumerate(segs):
                bn = bns[si_]
                S.op("dve", ts(rnC[:, l0:l0 + n], ps[bn][:, 0:n], 1.0 / 1024, ALU.mult, EPS, ALU.add), reads=[PK[bn]], writes=["rnC"])
            S.op("act", actf(rnC[:, 0:W], rnC[:, 0:W], AF.Sqrt), reads=["rnC"], writes=["rnC"])
            S.op("dve", lambda e: e.reciprocal(out=rnC[:, 0:W], in_=rnC[:, 0:W]), reads=["rnC"], writes=["rnC"])
            for m in range(8):
                out_ap, wkey = out_fn(m)
                S.op("dve", stt(out_ap, R[:, m, 0:W], gcol(which, m), rnC[:, 0:W], ALU.mult, ALU.mult), reads=[("R", m), "rnC", "cols"], writes=[wkey])

        for pi, segs in enumerate(PASSES):
            W = sum(n for (_, n, _) in segs)
            t00 = segs[0][0]
            has_s = len(segs) > 1
            if pi == 0:
                S.dma("sp", "xres", dmaf(xres, x_p[0:512, :].rearrange("(j p) f -> p j f", p=128)), writes=["xres"])
            def stats_act(m):
                S.op("act", actf(sqr[m % 2][:, 0:W], R[:, m, 0:W], AF.Square), reads=[("R", m)], writes=[("sqr", m % 2)])

            def stats_pe(m, bns):
                for si_, (t0, n, l0) in enumerate(segs):
                    S.group("pe", [mm(ps[bns[si_]][:, 0:n], ones_b, sqr[m % 2][:, l0:l0 + n], start=(m == 0), stop=(m == 7))],
                            reads=[("sqr", m % 2), "ones_b"], writes=[PK[bns[si_]]])

            def norm_finish_row(bns, out_t, key, square):
                for si_, (t0, n, l0) in enumerate(segs):
                    S.op("dve", ts(out_t[:, l0:l0 + n], ps[bns[si_]][:, 0:n], 1.0 / 1024, ALU.mult, EPS, ALU.add), reads=[PK[bns[si_]]], writes=[key])
                if not square:
                    S.op("act", actf(out_t[:, 0:W], out_t[:, 0:W], AF.Sqrt), reads=[key], writes=[key])
                S.op("dve", lambda e: e.reciprocal(out=out_t[:, 0:W], in_=out_t[:, 0:W]), reads=[key], writes=[key])

            def pick_bns():
                o = []
                for _ in segs:
                    o.append(6 + misc_n[0] % 2)
                    misc_n[0] += 1
                return o

            bns1 = pick_bns()
            for blk in range(2):
                sl = load_r8(w_out_v, blk * 512)
                for m4 in range(4):
                    m = blk * 4 + m4
                    for (t0, n, l0) in segs:
                        b = next_bank()
                        fns = [mm(ps[b][:, 0:n], r8[sl][:, k, m4 * 128:(m4 + 1) * 128], cat[:, k, t0:t0 + n], start=(k == 0), stop=False) for k in range(8)]
                        if n == 512:
                            fns += [mm(ps[b][:, j * 128:(j + 1) * 128], xres[:, j, m * 128:(m + 1) * 128], ident_f, start=False, stop=(j == 3)) for j in range(4)]
                            rk = ["xres"]
                        else:
                            fns += [mm(ps[b][:, 0:16], xres_s[0:16, m * 128:(m + 1) * 128], ident_f[0:16, 0:16], start=False, stop=True)]
                            rk = ["xres_s"]
                        S.group("pe", fns, reads=[("r8", sl), "cat", "ident_f"] + rk, writes=[PK[b]])
                        S.op("act", actf(R[:, m, l0:l0 + n], ps[b][:, 0:n], AF.Copy), reads=[PK[b]], writes=[("R", m)])
                        S.op("act", actf(xnC[:, m, l0:l0 + n], ps[b][:, 0:n], AF.Copy, scale=gcol(0, m)), reads=[PK[b], "cols"], writes=[("xnC", m)])
                    stats_act(m)
                    if m >= 1:
                        stats_pe(m - 1, bns1)
            stats_pe(7, bns1)
            norm_finish_row(bns1, rnC, "rnC", True)
            if pi + 1 < len(PASSES):
                tn = PASSES[pi + 1][0][0]
                S.dma("sp", "xres", dmaf(xres, x_p[tn:tn + 512, :].rearrange("(j p) f -> p j f", p=128)), writes=["xres"])
                if len(PASSES[pi + 1]) > 1:
                    S.dma("sp", "xres_s", dmaf(xres_s[0:16, :], x_s), writes=["xres_s"])
            for blk in range(8):
                sl = load_r8(w_up_v, blk * 512)
                for m4 in range(4):
                    hc = blk * 4 + m4
                    for (t0, n, l0) in segs:
                        b = next_bank()
                        S.group("pe", [mm(ps[b][:, 0:n], r8[sl][:, k, m4 * 128:(m4 + 1) * 128], xnC[:, k, l0:l0 + n], start=(k == 0), stop=(k == 7)) for k in range(8)],
                                reads=[("r8", sl)] + [("xnC", k) for k in range(8)], writes=[PK[b]])
                        rt = relu_t[misc_n[0] % 2]; rkey = ("relu_t", misc_n[0] % 2)
                        misc_n[0] += 1
                        S.op("act", actf(rt[:, 0:n], ps[b][:, 0:n], AF.Relu), reads=[PK[b]], writes=[rkey])
                        S.op("dve", tt(hid[:, hc, l0:l0 + n], rt[:, 0:n], rt[:, 0:n], ALU.mult), reads=[rkey], writes=[("hid", hc)])
            bns2 = pick_bns()
            for blk in range(4):
                sl = load_r16(blk * 256)
                for m2 in range(2):
                    m = blk * 2 + m2
                    for (t0, n, l0) in segs:
                        b = next_bank()
                        S.group("pe", [mm(ps[b][:, 0:n], r16[sl][:, k, m2 * 128:(m2 + 1) * 128], hid[:, k, l0:l0 + n], start=(k == 0), stop=(k == 31)) for k in range(32)],
                                reads=[("r16", sl)] + [("hid", k) for k in range(32)], writes=[PK[b]])
                        sg = sig[misc_n[0] % 2]; skey = ("sig", misc_n[0] % 2)
                        misc_n[0] += 1
                        S.op("dve", tt(sg[:, 0:n], ps[b][:, 0:n], rnC[:, l0:l0 + n], ALU.mult), reads=[PK[b], "rnC"], writes=[skey])
                        S.op("dve", tt(R[:, m, l0:l0 + n], R[:, m, l0:l0 + n], sg[:, 0:n], ALU.add), reads=[skey, ("R", m)], writes=[("R", m)])
                    S.op("act", actf(xnC[:, m, 0:W], R[:, m, 0:W], AF.Copy, scale=gcol(1, m)), reads=[("R", m), "cols"], writes=[("xnC", m)])
                    stats_act(m)
                    if m >= 1:
                        stats_pe(m - 1, bns2)
            stats_pe(7, bns2)
            norm_finish_row(bns2, rnC, "rnC", False)
            for (t0, n, l0) in segs:
                ntile = (n + 127) // 128
                for j in range(ntile):
                    r = min(128, n - j * 128)
                    sl = misc_n[0] % 2
                    misc_n[0] += 1
                    src = p_p[t0 + j * 128:t0 + j * 128 + r, :] if n == 512 else p_s
                    S.dma("pool", "ptok%d" % sl, dmaf(ptok[sl][0:r, :], src), writes=[("ptok", sl)])
                    S.group("pe", [lambda e, kk=kk, sl=sl, r=r: e.transpose(out=psb[5][:, kk * 128:kk * 128 + r], in_=ptok[sl][0:r, kk * 128:(kk + 1) * 128], identity=ident_b[0:r, 0:r]) for kk in range(2)],
                            reads=[("ptok", sl), "ident_b"], writes=[PK[5]])
                    S.op("act", actf(pT[:, :, l0 + j * 128:l0 + j * 128 + r], psb[5][:, 0:256].rearrange("p (k t) -> p k t", k=2)[:, :, 0:r], AF.Copy), reads=[PK[5]], writes=["pT"])
            ntt = sum((n + 127) // 128 for (_, n, _) in segs)
            sigbufs = [(sig[0], ("sig", 0)), (sig[1], ("sig", 1)), (relu_t[0], ("relu_t", 0)), (relu_t[1], ("relu_t", 1))]

            def gate_chunk(m, sl, m4):
                ops = []
                Lop, Lgr, Ldma = mk_recorders(S, ops)
                for si_, (t0, n, l0) in enumerate(segs):
                    b = (2 * m + si_) % 4
                    pb_ = 6 + m % 2
                    sg, skey = sigbufs[(2 * m + si_) % 4]
                    Lgr("pe", [mm(ps[b][:, 0:n], r8[sl][:, k, m4 * 128:(m4 + 1) * 128], xnC[:, k, l0:l0 + n], start=(k == 0), stop=(k == 7)) for k in range(8)],
                        reads=[("r8", sl)] + [("xnC", k) for k in range(8)], writes=[PK[b]])
                    Lgr("pe", [mm(ps[pb_][:, 0:n], pw[:, kk, m * 128:(m + 1) * 128], pT[:, kk, l0:l0 + n], start=(kk == 0), stop=(kk == 1)) for kk in range(2)],
                        reads=["pw", "pT"], writes=[PK[pb_]])
                    Lop("dve", tt(sg[:, 0:n], ps[b][:, 0:n], rnC[:, l0:l0 + n], ALU.mult), reads=[PK[b], "rnC"], writes=[skey])
                    Lop("act", actf(sg[:, 0:n], sg[:, 0:n], AF.Sigmoid), reads=[skey], writes=[skey])
                    Lop("dve", tt(sg[:, 0:n], sg[:, 0:n], ps[pb_][:, 0:n], ALU.mult), reads=[PK[pb_], skey], writes=[skey])
                    Lop("dve", tt(R[:, m, l0:l0 + n], R[:, m, l0:l0 + n], sg[:, 0:n], ALU.add), reads=[skey, ("R", m)], writes=[("R", m)])
                sq = sqr[m % 2]
                Lop("act", actf(sq[:, 0:W], R[:, m, 0:W], AF.Square), reads=[("R", m)], writes=[("sqr", m % 2)])
                fns = []
                if m == 0:
                    fns.append(mm(ps[5][:, 256:256 + ntt], zeros_f, zeros_f[:, 0:ntt], start=True, stop=False))
                jt = 0
                for (t0, n, l0) in segs:
                    for j in range((n + 127) // 128):
                        r = min(128, n - j * 128)
                        fns.append(mm(ps[5][0:r, 256 + jt:257 + jt], sq[:, l0 + j * 128:l0 + j * 128 + r], ones_b[:, 0:1], start=False, stop=False))
                        jt += 1
                if m == 7:
                    fns.append(mm(ps[5][:, 256:256 + ntt], zeros_f, zeros_f[:, 0:ntt], start=False, stop=True))
                Lgr("pe", fns, reads=[("sqr", m % 2), "ones_b"], writes=[PK[5]])
                Lop("act", actf(R[:, m, 0:W], R[:, m, 0:W], AF.Copy, scale=gcol(2, m)), reads=[("R", m), ("sqr", m % 2), "cols"], writes=[("R", m)])
                return ops

            for blk in range(2):
                sl = load_r8(w_gate_v, blk * 512)
                chunks = [gate_chunk(blk * 4 + m4, sl, m4) for m4 in range(4)]
                zipper(chunks[0:2])
                zipper(chunks[2:4])
            if not RNCOL_INIT[0]:
                RNCOL_INIT[0] = True
                S.op("dve", lambda e: e.memset(rncol, 1.0), writes=["rncol"])
            S.op("dve", ts(rncol[:, 0:ntt], ps[5][:, 256:256 + ntt], 1.0 / 1024, ALU.mult, EPS, ALU.add), reads=[PK[5]], writes=["rncol"])
            S.op("act", actf(rncol[:, 0:ntt], rncol[:, 0:ntt], AF.Sqrt), reads=["rncol"], writes=["rncol"])
            S.op("dve", lambda e: e.reciprocal(out=rncol[:, 0:ntt], in_=rncol[:, 0:ntt]), reads=["rncol"], writes=["rncol"])
            jt = 0
            for (t0, n, l0) in segs:
                ntile = (n + 127) // 128
                for j in range(ntile):
                    r = min(128, n - j * 128)
                    ysl = 0
                    for half in range(2):
                        b = next_bank()
                        S.group("pe", [(lambda e, m4=m4, b=b, r=r, half=half, l0=l0, j=j: e.transpose(out=ps[b][0:r, m4 * 128:(m4 + 1) * 128], in_=R[:, half * 4 + m4, l0 + j * 128:l0 + j * 128 + r], identity=ident_f)) for m4 in range(4)],
                                reads=[("R", half * 4 + m4) for m4 in range(4)] + ["ident_f"], writes=[PK[b]])
                        S.op("act", actf(ytile[ysl][0:r, half * 512:(half + 1) * 512], ps[b][0:r, :], AF.Copy, scale=rncol[0:r, jt:jt + 1]), reads=[PK[b], "rncol"], writes=[("ytile", ysl)])
                    jt += 1
                    dst = y_p[t0 + j * 128:t0 + j * 128 + r, :] if n == 512 else y_s
                    S.dma("sp", "o_y%d" % ysl, dmaf(dst, ytile[ysl][0:r, :]), reads=[("ytile", ysl)])
        S.finish()
        with nc.Block() as block:
            S.replay(block)
    return nc


_PROG = {}


def _make_in_maps(inputs):
    f = lambda a: np.ascontiguousarray(np.asarray(a, dtype=np.float32))
    g = {k: f(v) for k, v in inputs.items()}
    shared = {
        "g_mix": g["g_mix"].reshape(1, 1024), "w_in": g["w_in"][0], "w_conv": g["w_conv"][0],
        "a_log": g["a_log"].reshape(1, 4), "dt_bias": g["dt_bias"].reshape(1, 4), "gdn_norm": g["gdn_norm"].reshape(1, 128),
        "ln_g": g["sgu_ln_g"].reshape(1, 512), "ln_b": g["sgu_ln_b"].reshape(1, 512), "w_s": g["w_s"][0],
        "b_s": g["b_s"].reshape(1, 512), "w_out": g["w_out"][0], "g_ff": g["g_ff"].reshape(8, 128), "w_up": g["w_up"][0],
        "w_down": g["w_down"][0], "g_ple": g["g_ple"].reshape(8, 128), "w_ple": g["w_ple"][0], "w_gate": g["w_ple_gate"][0],
        "g_fin": g["g_final"].reshape(8, 128),
    }
    maps = []
    for i in range(8):
        m = dict(shared)
        sl = slice(16 * i, 16 * i + 16)
        m["x_p"] = g["x_prompt"][i]
        m["x_s"] = g["x_sample"][sl, 0]
        m["st_conv"] = g["state_conv"][0, sl]
        m["st_gdn"] = g["state_gdn"][0, sl]
        m["p_p"] = g["p_prompt"][0, i]
        m["p_s"] = g["p_sample"][0, sl, 0]
        maps.append(m)
    return maps


def kernel(**inputs):
    if "nc" not in _PROG:
        _PROG["nc"] = build_program()
    nc = _PROG["nc"]
    maps = _make_in_maps(inputs)
    res = run_bass_kernel_spmd(nc, maps, core_ids=list(range(8)))
    R = res.results
    st = lambda name: np.stack([np.asarray(r[name], dtype=np.float32) for r in R])
    cc = lambda name: np.concatenate([np.asarray(r[name], dtype=np.float32) for r in R], axis=0)
    y_prompt = st("y_p")
    y_sample = cc("y_s")[:, None, :]
    new_conv_prompt = st("ncp")[None]
    new_gdn_prompt = st("ngp")[None]
    new_conv_sample = cc("ncs")[None]
    new_gdn_sample = cc("ngs")[None]
    new_sgu_v_sample = cc("nsv")[None, :, None, :]
    return (y_prompt, y_sample, new_conv_prompt, new_gdn_prompt, new_conv_sample, new_gdn_sample, new_sgu_v_sample)
```

```python
import os
import numpy as np
import concourse.bass as bass
import concourse.mybir as mybir
from concourse.bass_utils import run_bass_kernel_spmd

F32 = mybir.dt.float32
BF16 = mybir.dt.bfloat16
AF = mybir.ActivationFunctionType
ALU = mybir.AluOpType
AX = mybir.AxisListType


class Sched:
    ENGS = ("pe", "act", "dve", "pool", "sp")

    def __init__(self, nc, stack):
        self.nc = nc
        self.stack = stack
        self.streams = {e: [] for e in self.ENGS}
        self.esem = {e: stack.enter_context(nc.semaphore("c_" + e)) for e in self.ENGS[:4]}
        self.ecnt = {e: 0 for e in self.ENGS}
        self.waited = {e: {} for e in self.ENGS}
        self.res = {}
        self.dsem = {}
        self.sem_by_name = {}
        for e in self.ENGS[:4]:
            self.sem_by_name[self.esem[e].name] = self.esem[e]

    def _need(self, eng, ev, waits):
        if ev is None:
            return
        name, val, src = ev
        if src == eng and eng == "pe":
            return
        cur = waits.get(name, 0)
        if val > cur:
            waits[name] = val

    def _deps(self, eng, reads, writes):
        waits = {}
        for k in reads:
            r = self.res.get(k)
            if r is not None:
                self._need(eng, r[0], waits)
                if isinstance(k, tuple) and k and k[0] == "ps":
                    for ev in r[1]:
                        if ev[2] != eng:
                            self._need(eng, ev, waits)
        for k in writes:
            r = self.res.get(k)
            if r is not None:
                if r[0] is not None:
                    self._need(eng, r[0], waits)
                for ev in r[1]:
                    self._need(eng, ev, waits)
        out = []
        w = self.waited[eng]
        for name, val in waits.items():
            if w.get(name, 0) < val:
                w[name] = val
                out.append((name, val))
        return out

    def _commit(self, ev, reads, writes):
        for k in reads:
            r = self.res.setdefault(k, [None, []])
            r[1].append(ev)
        for k in writes:
            self.res[k] = [ev, []]

    def op(self, eng, fn, reads=(), writes=()):
        waits = self._deps(eng, reads, writes)
        self.ecnt[eng] += 1
        ev = (self.esem[eng].name, self.ecnt[eng], eng)
        self.streams[eng].append((waits, [fn], ("inc", self.esem[eng], 1)))
        self._commit(ev, reads, writes)
        return ev

    def group(self, eng, fns, reads=(), writes=()):
        waits = self._deps(eng, reads, writes)
        self.ecnt[eng] += 1
        ev = (self.esem[eng].name, self.ecnt[eng], eng)
        self.streams[eng].append((waits, list(fns), ("inc", self.esem[eng], 1)))
        self._commit(ev, reads, writes)
        return ev

    def dma(self, eng, slot, fn, reads=(), writes=(), n=1):
        if slot not in self.dsem:
            s = self.stack.enter_context(self.nc.semaphore("d_" + slot))
            self.dsem[slot] = [s, 0]
            self.sem_by_name[s.name] = s
        waits = self._deps(eng, reads, writes)
        d = self.dsem[slot]
        fns = fn if isinstance(fn, (list, tuple)) else [fn]
        d[1] += 16 * len(fns)
        ev = (d[0].name, d[1], "dma")
        self.streams[eng].append((waits, list(fns), ("dmainc", d[0], 16)))
        self._commit(ev, reads, writes)
        return ev

    def barrier(self, skip=()):
        evs = []
        for e in self.ENGS[:4]:
            if self.ecnt[e] > 0:
                evs.append((self.esem[e].name, self.ecnt[e]))
        for slot, (s, c) in self.dsem.items():
            if c > 0 and not any(slot.startswith(p) for p in skip):
                evs.append((s.name, c))
        for eng in self.ENGS:
            w = self.waited[eng]
            waits = []
            for name, val in evs:
                if w.get(name, 0) < val:
                    w[name] = val
                    waits.append((name, val))
            if waits:
                self.streams[eng].append((waits, [], None))
        self.res.clear()

    def finish(self):
        eng = "sp"
        waits = []
        for slot, (s, c) in self.dsem.items():
            if c > 0:
                waits.append((s.name, c))
        for e in self.ENGS[:4]:
            if self.ecnt[e] > 0:
                waits.append((self.esem[e].name, self.ecnt[e]))
        self.streams[eng].append((waits, [], None))

    def replay(self, block):
        sbn = self.sem_by_name

        def run(e, items):
            for waits, fns, inc in items:
                for name, val in waits:
                    e.wait_ge(sbn[name], val)
                last = None
                for i, f in enumerate(fns):
                    ins = f(e)
                    if inc is not None and inc[0] == "dmainc":
                        ins.then_inc(inc[1], 16)
                    last = ins
                if inc is not None and inc[0] == "inc" and last is not None:
                    last.then_inc(inc[1], 1)

        st = self.streams

        @block.tensor
        def _(e):
            run(e, st["pe"])

        @block.scalar
        def _(e):
            run(e, st["act"])

        @block.vector
        def _(e):
            run(e, st["dve"])

        @block.gpsimd
        def _(e):
            run(e, st["pool"])

        @block.sync
        def _(e):
            run(e, st["sp"])


U8 = mybir.dt.uint8
T_P = 2048
T_S = 16
T_ALL = T_P + T_S
SEGS = [(0, 512), (512, 512), (1024, 512), (1536, 512), (2048, 16)]
NT = 17
EPS = 1e-6
D_IN = 3080
C_Q, C_K, C_V, C_Z, C_BA, C_U, C_VS = 0, 512, 1024, 1536, 2048, 2056, 2568


def mm(out, lhsT, rhs, start=True, stop=True):
    return lambda e: e.matmul(out, lhsT=lhsT, rhs=rhs, start=start, stop=stop)


def actf(out, in_, func, **kw):
    return lambda e: e.activation(out=out, in_=in_, func=func, **kw)


def tt(out, a, b, op):
    return lambda e: e.tensor_tensor(out=out, in0=a, in1=b, op=op)


def ts(out, a, s1, op0, s2=None, op1=None):
    if op1 is None:
        return lambda e: e.tensor_scalar(out=out, in0=a, scalar1=s1, scalar2=None, op0=op0)
    return lambda e: e.tensor_scalar(out=out, in0=a, scalar1=s1, scalar2=s2, op0=op0, op1=op1)


def stt(out, a, s, b, op0, op1):
    return lambda e: e.scalar_tensor_tensor(out=out, in0=a, scalar=s, in1=b, op0=op0, op1=op1)


def stt_pool(out, a, colap):
    return lambda e: e.tensor_tensor(out=out, in0=a, in1=colap.to_broadcast([128, 128]), op=ALU.mult)


def cp(out, in_):
    return lambda e: e.tensor_copy(out=out, in_=in_)


def dmaf(out, in_):
    return lambda e: e.dma_start(out=out, in_=in_)


class _Item:
    __slots__ = ("thunk", "eng", "reads", "writes", "dur")

    def __init__(self, thunk, eng, reads, writes, dur):
        self.thunk, self.eng, self.reads, self.writes, self.dur = thunk, eng, tuple(reads), tuple(writes), dur


_DUR = {"act": 0.5, "dve": 0.45, "pool": 0.5}


def mk_recorders(S, ops):
    def Lop(eng, fn, reads=(), writes=()):
        ops.append(_Item(lambda: S.op(eng, fn, reads=reads, writes=writes), eng, reads, writes, _DUR.get(eng, 0.4)))

    def Lgr(eng, fns, reads=(), writes=()):
        ops.append(_Item(lambda: S.group(eng, fns, reads=reads, writes=writes), eng, reads, writes, 0.1 + 0.13 * len(fns)))

    def Ldma(eng, slot, fn, reads=(), writes=()):
        ops.append(_Item(lambda: S.dma(eng, slot, fn, reads=reads, writes=writes), "q_" + eng, reads, writes, 2.5))
    return Lop, Lgr, Ldma


def zipper(lists):
    lists = [l for l in lists if l]
    idx = [0] * len(lists)
    if os.environ.get("ZIP", "rr") == "rr":
        live = True
        while live:
            live = False
            for i, l in enumerate(lists):
                if idx[i] < len(l):
                    it = l[idx[i]]
                    idx[i] += 1
                    live = True
                    if isinstance(it, _Item):
                        it.thunk()
                    else:
                        it()
        return
    t_eng, t_w, t_r = {}, {}, {}
    remaining = sum(len(l) for l in lists)
    while remaining:
        best = None
        for i, l in enumerate(lists):
            if idx[i] >= len(l):
                continue
            it = l[idx[i]]
            if not isinstance(it, _Item):
                best = (-1.0, i, it)
                break
            rdy = t_eng.get(it.eng, 0.0)
            for k in it.reads:
                rdy = max(rdy, t_w.get(k, 0.0))
            for k in it.writes:
                rdy = max(rdy, t_w.get(k, 0.0), t_r.get(k, 0.0))
            if best is None or rdy < best[0]:
                best = (rdy, i, it)
        rdy, i, it = best
        idx[i] += 1
        remaining -= 1
        if not isinstance(it, _Item):
            it()
            continue
        it.thunk()
        fin = rdy + it.dur
        if it.eng.startswith("q_"):
            t_eng[it.eng] = rdy + 0.1
        else:
            t_eng[it.eng] = fin
        for k in it.reads:
            t_r[k] = max(t_r.get(k, 0.0), fin)
        for k in it.writes:
            t_w[k] = fin
            t_r[k] = 0.0


def run_lanes(units, nl, stag):
    lanes = [[(lambda: None)] * (k * stag) for k in range(nl)]
    for i, u in enumerate(units):
        lanes[i % nl] += u
    zipper(lanes)


class Bump:
    def __init__(self, arena, start, limit):
        self.t, self.off, self.limit = arena, start, limit

    def alloc(self, dtype, shape):
        esz = 4 if dtype == F32 else 2
        n = 1
        for s in shape[1:]:
            n *= s
        nb = (n * esz + 63) // 64 * 64
        o = self.off
        self.off += nb
        assert self.off <= self.limit, ("SBUF arena overflow", self.off, self.limit)
        ap = self.t[:, o:o + n * esz].bitcast(dtype)
        if len(shape) == 3:
            ap = ap.rearrange("p (a b) -> p a b", a=shape[1])
        elif len(shape) == 4:
            ap = ap.rearrange("p (a b c) -> p a b c", a=shape[1], b=shape[2])
        return ap


def build_program():
    from contextlib import ExitStack
    nc = bass.Bass("TRN2", target_bir_lowering=False)

    def din(name, shape):
        return nc.dram_tensor(name, shape, F32, kind="ExternalInput").ap()

    def dout(name, shape):
        return nc.dram_tensor(name, shape, F32, kind="ExternalOutput").ap()

    x_p = din("x_p", [T_P, 1024]); x_s = din("x_s", [T_S, 1024])
    st_conv = din("st_conv", [T_S, 3, 1536]); st_gdn = din("st_gdn", [T_S, 4, 128, 128])
    p_p = din("p_p", [T_P, 256]); p_s = din("p_s", [T_S, 256])
    g_mix = din("g_mix", [1, 1024]); w_in = din("w_in", [1024, D_IN]); w_conv = din("w_conv", [4, 1536])
    a_log = din("a_log", [1, 4]); dt_bias = din("dt_bias", [1, 4]); gdn_norm = din("gdn_norm", [1, 128])
    ln_g = din("ln_g", [1, 512]); ln_b = din("ln_b", [1, 512]); w_s = din("w_s", [4, 128, 128]); b_s = din("b_s", [1, 512])
    w_out = din("w_out", [1024, 1024]); g_ff = din("g_ff", [8, 128]); w_up = din("w_up", [1024, 4096]); w_down = din("w_down", [4096, 1024])
    g_ple = din("g_ple", [8, 128]); w_ple = din("w_ple", [256, 1024]); w_gate = din("w_gate", [1024, 1024]); g_fin = din("g_fin", [8, 128])
    y_p = dout("y_p", [T_P, 1024]); y_s = dout("y_s", [T_S, 1024])
    ncp = dout("ncp", [3, 1536]); ngp = dout("ngp", [4, 128, 128])
    ncs = dout("ncs", [T_S, 3, 1536]); ngs = dout("ngs", [T_S, 4, 128, 128]); nsv = dout("nsv", [T_S, 512])

    w_in_v = w_in.rearrange("(k p) c -> p k c", p=128)
    w_out_v = w_out.rearrange("(k p) c -> p k c", p=128)
    w_up_v = w_up.rearrange("(k p) c -> p k c", p=128)
    w_down_v = w_down.rearrange("(k p) c -> p k c", p=128)
    w_gate_v = w_gate.rearrange("(k p) c -> p k c", p=128)
    w_ple_v = w_ple.rearrange("(k p) c -> p k c", p=128)

    with ExitStack() as st:
        S = Sched(nc, st)
        ARENA = 206 * 1024
        arena = st.enter_context(nc.sbuf_tensor("arena", [128, ARENA], U8))
        ps = [st.enter_context(nc.psum_tensor("ps%d" % i, [128, 512], F32)) for i in range(8)]
        psb = [p[:, :].bitcast(BF16) for p in ps]
        PK = [("ps", i) for i in range(8)]

        P = Bump(arena, 0, ARENA)
        ident_f = P.alloc(F32, [128, 128]); ident_b = P.alloc(BF16, [128, 128])
        ones_f = P.alloc(F32, [128, 128]); ones_b = P.alloc(BF16, [128, 128])
        mask_incl = P.alloc(F32, [128, 128])
        mask_su = P.alloc(F32, [128, 128])
        nmask_sl = P.alloc(F32, [128, 128])
        sel127 = P.alloc(F32, [128, 128])
        nmask_su = P.alloc(F32, [128, 128])
        mask_sl = P.alloc(F32, [128, 128])
        rowstage = P.alloc(F32, [128, 128])
        cols = P.alloc(F32, [128, 128])
        wsT = P.alloc(BF16, [128, 4, 128])
        selws = P.alloc(BF16, [128, 4, 16])
        ws00 = P.alloc(F32, [128, 4])
        bs_row = P.alloc(F32, [128, 4, 128])
        lng_row = P.alloc(F32, [128, 512]); lnb_row = P.alloc(F32, [128, 512])
        alog_row = P.alloc(F32, [128, 4]); dtb_row = P.alloc(F32, [128, 4]); nexpA_row = P.alloc(F32, [128, 4])
        zcol = P.alloc(F32, [128, 4])
        zeros_f = P.alloc(F32, [128, 128])
        cat = P.alloc(BF16, [128, 8, T_ALL])
        P_C0 = P.off
        ba = P.alloc(F32, [128, NT, 8])
        beta_c = P.alloc(F32, [128, NT, 4]); g_c = P.alloc(F32, [128, NT, 4]); gc_c = P.alloc(F32, [128, NT, 4])
        bexp_c = P.alloc(F32, [128, NT, 4]); kd_c = P.alloc(F32, [128, NT, 4]); egl_c = P.alloc(F32, [128, NT, 4])
        tmp68 = P.alloc(F32, [128, NT, 4])
        qkv = P.alloc(BF16, [128, 12, T_ALL])
        zs = P.alloc(BF16, [128, 4, T_ALL])
        qks_f = P.alloc(F32, [128, 12, 16])
        histT = P.alloc(F32, [128, 12, 3, 16])
        ncp_st = P.alloc(F32, [128, 12, 3]); ncs_st = P.alloc(F32, [128, 12, 16])
        S_f = P.alloc(F32, [128, 4, 128]); S_b = P.alloc(BF16, [128, 4, 128])
        X0 = P.off
        XB = Bump(arena, X0, ARENA)
        hT = XB.alloc(BF16, [128, 8, T_ALL])
        Y0 = XB.off

        def wcol(j, c):
            return cols[:, 24 + j * 12 + c: 24 + j * 12 + c + 1]

        def gcol(which, m):
            return cols[:, which * 8 + m: which * 8 + m + 1]
        gdnn_col = cols[:, 72:73]
        negone = zcol[:, 1:2]
        mhalf = zcol[:, 2:3]

        S.op("pool", lambda e: e.memset(ones_f, 1.0), writes=["ones_f"])
        S.op("pool", lambda e: e.memset(ones_b, 1.0), writes=["ones_b"])
        S.op("pool", lambda e: e.memset(zcol, 0.0), writes=["zcol"])
        S.op("pool", lambda e: e.memset(zcol[:, 1:2], -1.0), reads=["zcol"], writes=["zcol"])
        S.op("pool", lambda e: e.memset(zcol[:, 2:3], -0.5), reads=["zcol"], writes=["zcol"])
        S.op("pool", lambda e: e.memset(zeros_f, 0.0), writes=["zeros_f"])
        S.op("pool", lambda e: e.affine_select(out=ident_f, in_=ones_f, pattern=[[-1, 128]], compare_op=ALU.is_equal, fill=0.0, base=0, channel_multiplier=1), reads=["ones_f"], writes=["ident_f"])
        S.op("pool", lambda e: e.affine_select(out=mask_incl, in_=ones_f, pattern=[[1, 128]], compare_op=ALU.is_ge, fill=0.0, base=0, channel_multiplier=-1), reads=["ones_f"], writes=["mask_incl"])
        S.op("pool", lambda e: e.affine_select(out=mask_su, in_=ones_f, pattern=[[1, 128]], compare_op=ALU.is_gt, fill=0.0, base=0, channel_multiplier=-1), reads=["ones_f"], writes=["mask_su"])
        S.op("pool", lambda e: e.affine_select(out=nmask_sl, in_=ones_f, pattern=[[-1, 128]], compare_op=ALU.is_gt, fill=0.0, base=0, channel_multiplier=1), reads=["ones_f"], writes=["nmask_sl"])
        S.op("pool", ts(nmask_sl, nmask_sl, -1.0, ALU.mult), reads=["nmask_sl"], writes=["nmask_sl"])
        S.op("pool", ts(nmask_su, mask_su, -1.0, ALU.mult), reads=["mask_su"], writes=["nmask_su"])
        S.op("pool", ts(mask_sl, nmask_sl, -1.0, ALU.mult), reads=["nmask_sl"], writes=["mask_sl"])
        S.op("pool", lambda e: e.affine_select(out=sel127, in_=ones_f, pattern=[[0, 128]], compare_op=ALU.is_equal, fill=0.0, base=-127, channel_multiplier=1), reads=["ones_f"], writes=["sel127"])
        S.op("dve", cp(ident_b, ident_f), reads=["ident_f"], writes=["ident_b"])
        S.op("pool", lambda e: e.memset(rowstage, 0.0), writes=["rowstage"])
        S.dma("sp", "c0", [dmaf(rowstage[0:8, :], g_ff), dmaf(rowstage[8:16, :], g_ple), dmaf(rowstage[16:24, :], g_fin),
                           dmaf(rowstage[24:72, :], w_conv.rearrange("j (c p) -> (j c) p", p=128)), dmaf(rowstage[72:73, :], gdn_norm)],
              writes=["rowstage"])
        S.group("pe", [mm(ps[0][:, 0:128], rowstage, ident_f)], reads=["rowstage", "ident_f"], writes=[PK[0]])
        S.op("dve", cp(cols, ps[0][:, 0:128]), reads=[PK[0]], writes=["cols"])
        S.dma("sp", "c1", [dmaf(bs_row.rearrange("p h t -> p (h t)"), b_s.partition_broadcast(128)),
                           dmaf(lng_row, ln_g.partition_broadcast(128)), dmaf(lnb_row, ln_b.partition_broadcast(128)),
                           dmaf(alog_row, a_log.partition_broadcast(128)), dmaf(dtb_row, dt_bias.partition_broadcast(128)),
                           ] + [dmaf(ws00[:, h:h + 1], w_s[h, 0, 0:1].partition_broadcast(128)) for h in range(4)],
              writes=["rows"])
        S.op("act", actf(nexpA_row, alog_row, AF.Exp), reads=["rows"], writes=["nexpA"])
        S.op("dve", ts(nexpA_row, nexpA_row, -1.0, ALU.mult), reads=["nexpA"], writes=["nexpA"])
        for h in range(4):
            S.op("dve", ts(selws[0:16, h, :], ident_f[0:16, 0:16], ws00[0:16, h:h + 1], ALU.mult), reads=["rows", "ident_f"], writes=[("selws", h)])

        YA = Bump(arena, Y0, ARENA)
        wstmp = YA.alloc(F32, [128, 4, 128])
        S.dma("sp", "c2", dmaf(wstmp, w_s.rearrange("h t s -> t h s")), writes=["wstmp"])
        for h in range(4):
            S.op("pool", lambda e, h=h: e.affine_select(out=wstmp[:, h, :], in_=wstmp[:, h, :], pattern=[[-1, 128]], compare_op=ALU.is_ge, fill=0.0, base=0, channel_multiplier=1),
                 reads=["wstmp"], writes=["wstmp"])
        S.group("pe", [mm(ps[1][:, h * 128:(h + 1) * 128], wstmp[:, h, :], ident_f) for h in range(4)], reads=["wstmp", "ident_f"], writes=[PK[1]])
        S.op("dve", cp(wsT.rearrange("p h t -> p (h t)"), ps[1][:, 0:512]), reads=[PK[1]], writes=["wsT"])

        gmix_row = YA.alloc(F32, [128, 1024])
        S.dma("sp", "c3", dmaf(gmix_row, g_mix.partition_broadcast(128)), writes=["gmix"])
        xt = [YA.alloc(F32, [128, 1024]) for _ in range(3)]
        xsq = [YA.alloc(F32, [128, 1024]) for _ in range(3)]
        xn = [YA.alloc(BF16, [128, 1024]) for _ in range(3)]
        stat = YA.alloc(F32, [128, NT, 2])

        def phaseA_tile(i):
            ops = []
            Lop, Lgr, Ldma = mk_recorders(S, ops)
            r = 128 if i < 16 else 16
            sl = i % 3
            src = x_p[i * 128:(i + 1) * 128, :] if i < 16 else x_s
            Ldma("sp", "xt%d" % sl, dmaf(xt[sl][0:r, :], src), writes=[("xt", sl)])
            Lop("act", actf(xsq[sl][0:r, :], xt[sl][0:r, :], AF.Square), reads=[("xt", sl)], writes=[("xsq", sl)])
            Lop("dve", lambda e: e.reduce_sum(out=stat[0:r, i, 0:1], in_=xsq[sl][0:r, :], axis=AX.X), reads=[("xsq", sl)], writes=[("stat", i)])
            Lop("dve", ts(stat[0:r, i, 1:2], stat[0:r, i, 0:1], 1.0 / 1024, ALU.mult, EPS, ALU.add), reads=[("stat", i)], writes=[("stat", i)])
            Lop("act", actf(stat[0:r, i, 1:2], stat[0:r, i, 1:2], AF.Sqrt), reads=[("stat", i)], writes=[("stat", i)])
            Lop("dve", lambda e: e.reciprocal(out=stat[0:r, i, 1:2], in_=stat[0:r, i, 1:2]), reads=[("stat", i)], writes=[("stat", i)])
            Lop("dve", stt(xn[sl][0:r, :], xt[sl][0:r, :], stat[0:r, i, 1:2], gmix_row[0:r, :], ALU.mult, ALU.mult),
                reads=[("xt", sl), ("stat", i), "gmix"], writes=[("xn", sl)])
            b = i % 3
            Lgr("pe", [lambda e, k=k: e.transpose(out=psb[b][:, k * 128:k * 128 + r], in_=xn[sl][0:r, k * 128:(k + 1) * 128], identity=ident_b[0:r, 0:r]) for k in range(8)],
                reads=[("xn", sl), "ident_b"], writes=[PK[b]])
            if i % 2 == 0:
                Lop("act", actf(hT[:, :, i * 128:i * 128 + r], psb[b].rearrange("p (k t) -> p k t", k=8)[:, :, 0:r], AF.Copy), reads=[PK[b]], writes=[("hT", i)])
            else:
                Lop("dve", cp(hT[:, :, i * 128:i * 128 + r], psb[b].rearrange("p (k t) -> p k t", k=8)[:, :, 0:r]), reads=[PK[b]], writes=[("hT", i)])
            return ops

        tilesA = [phaseA_tile(i) for i in range(NT)]
        run_lanes(tilesA, 3, int(os.environ.get("STAG_A", "2")))
        S.barrier()

        YB = Bump(arena, Y0, ARENA)
        wb = [YB.alloc(BF16, [128, 8, 512]) for _ in range(2)]
        wb8 = YB.alloc(BF16, [128, 8, 8])
        lnst = YB.alloc(F32, [128, NT, 8])
        sct = [YB.alloc(F32, [128, 3, 128]) for _ in range(2)]
        YB_MID = YB.off
        vg = YB.alloc(BF16, [128, NT, 512])
        F6 = YB.alloc(F32, [128, 6, 512])
        f512 = [F6[:, i, :] for i in range(6)]
        vgs_f = f512[5]
        YB5 = Bump(arena, YB_MID, ARENA)
        NBS = 4
        pre = [YB5.alloc(F32, [128, 515]) for _ in range(NBS)]
        accb = [YB5.alloc(F32, [128, 512]) for _ in range(NBS)]
        rnb = [YB5.alloc(F32, [128, 512]) for _ in range(NBS)]
        sqb = [YB5.alloc(BF16, [128, 512]) for _ in range(NBS)]
        wdiag = [YB5.alloc(F32, [128, 4, 128]) for _ in range(2)]
        YB6 = Bump(arena, YB_MID, ARENA)
        stage_tok = YB6.alloc(F32, [128, 1536])
        stage2 = YB6.alloc(F32, [128, 1536])
        wb_n = [0]
        bank_n = [0]

        def next_bank(lo=0, hi=4):
            b = lo + bank_n[0] % (hi - lo)
            bank_n[0] += 1
            return b

        wb_seq = [C_VS, C_U, C_Z, 0, 512, 1024]
        wb_issued = [0]

        def load_w(view, c0, ncol=512):
            idx = wb_n[0]
            wb_n[0] += 1
            assert wb_seq[idx] == c0
            while wb_issued[0] < min(len(wb_seq), idx + 2):
                j = wb_issued[0]
                S.dma("pool", "wb%d" % (j % 2), dmaf(wb[j % 2][:, :, 0:512], view[:, :, wb_seq[j]:wb_seq[j] + 512]), writes=[("wb", j % 2)])
                wb_issued[0] += 1
            return idx % 2

        hT_keys = [("hT", i) for i in range(NT)]

        def seg_hT_keys(t0, n):
            return [("hT", i) for i in range(t0 // 128, (t0 + n + 127) // 128)]

        sl = load_w(w_in_v, C_VS)

        def vsgu_tile(i):
            ops = []
            Lop, Lgr, Ldma = mk_recorders(S, ops)
            r = 128 if i < 16 else 16
            b = (i % 3)
            Lgr("pe", [mm(ps[b][0:r, :], hT[:, k, i * 128:i * 128 + r], wb[sl][:, k, :], start=(k == 0), stop=(k == 7)) for k in range(8)],
                    reads=[("hT", i), ("wb", sl)], writes=[PK[b]])
            g1 = f512[i % 3]; g2 = f512[3 + i % 3]
            Lop("act", actf(g1[0:r, :], ps[b][0:r, :], AF.Gelu_apprx_tanh), reads=[PK[b]], writes=[("g1", i % 3)])
            Lop("pool", tt(g2[0:r, :], g1[0:r, :], g1[0:r, :], ALU.mult), reads=[("g1", i % 3)], writes=[("g2", i % 3)])
            Lop("dve", lambda e, i=i, r=r, g1=g1: e.reduce_sum(out=lnst[0:r, i, 0:1], in_=g1[0:r, :], axis=AX.X), reads=[("g1", i % 3)], writes=[("lnst", i)])
            Lop("dve", lambda e, i=i, r=r, g2=g2: e.reduce_sum(out=lnst[0:r, i, 1:2], in_=g2[0:r, :], axis=AX.X), reads=[("g2", i % 3)], writes=[("lnst", i)])
            L = lambda a, bb: lnst[0:r, i, a:bb]
            Lop("dve", ts(L(2, 3), L(0, 1), 1.0 / 512, ALU.mult), reads=[("lnst", i)], writes=[("lnst", i)])
            Lop("dve", tt(L(3, 4), L(2, 3), L(2, 3), ALU.mult), reads=[("lnst", i)], writes=[("lnst", i)])
            Lop("dve", stt(L(4, 5), L(1, 2), 1.0 / 512, L(3, 4), ALU.mult, ALU.subtract), reads=[("lnst", i)], writes=[("lnst", i)])
            Lop("act", actf(L(4, 5), L(4, 5), AF.Ln, bias=EPS), reads=[("lnst", i)], writes=[("lnst", i)])
            Lop("act", actf(L(5, 6), L(4, 5), AF.Exp, scale=mhalf[0:r, :]), reads=[("lnst", i)], writes=[("lnst", i)])
            Lop("dve", ts(g2[0:r, :], g1[0:r, :], L(2, 3), ALU.subtract, L(5, 6), ALU.mult), reads=[("g1", i % 3), ("lnst", i)], writes=[("g2", i % 3)])
            Lop("pool", tt(g2[0:r, :], g2[0:r, :], lng_row[0:r, :], ALU.mult), reads=[("g2", i % 3), "rows"], writes=[("g2", i % 3)])
            if i < 16:
                Lop("pool", tt(vg[0:r, i, :], g2[0:r, :], lnb_row[0:r, :], ALU.add), reads=[("g2", i % 3), "rows"], writes=[("vg", i)])
            else:
                Lop("pool", tt(vgs_f[0:r, :], g2[0:r, :], lnb_row[0:r, :], ALU.add), reads=[("g2", i % 3), "rows"], writes=[("g2", 2)])
                Lop("pool", cp(vg[0:r, i, :], vgs_f[0:r, :]), reads=[("g2", 2)], writes=[("vg", i)])
                Ldma("sp", "o_nsv", dmaf(nsv, vgs_f[0:r, :]), reads=[("g2", 2)])
            return ops

        tilesV = [vsgu_tile(i) for i in range(NT)]
        run_lanes(tilesV, 3, int(os.environ.get("STAG_V", "4")))

        S.dma("pool", "wb8", dmaf(wb8, w_in_v[:, :, C_BA:C_BA + 8]), writes=["wb8"])
        S.op("pool", lambda e: e.memset(ba, 0.0), writes=["ba"])
        bq = 4
        for i in range(NT):
            r = 128 if i < 16 else 16
            S.group("pe", [mm(ps[bq][0:r, i * 8:(i + 1) * 8], hT[:, k, i * 128:i * 128 + r], wb8[:, k, :], start=(k == 0), stop=(k == 7)) for k in range(8)],
                    reads=[("hT", i), "wb8"], writes=[PK[bq]])
        S.op("dve", cp(ba[:, 0:16, :], ps[bq][:, 0:128].rearrange("p (i c) -> p i c", c=8)), reads=[PK[bq], "ba"], writes=["ba"])
        S.op("dve", cp(ba[0:16, 16, :], ps[bq][0:16, 128:136]), reads=[PK[bq], "ba"], writes=["ba"])
        S.op("act", actf(beta_c, ba[:, :, 0:4], AF.Sigmoid), reads=["ba"], writes=["beta_c"])
        S.op("dve", tt(tmp68, ba[:, :, 4:8], dtb_row.unsqueeze(1).to_broadcast([128, NT, 4]), ALU.add), reads=["ba", "rows"], writes=["tmp68"])
        S.op("act", actf(tmp68, tmp68, AF.Exp), reads=["tmp68"], writes=["tmp68"])
        S.op("act", actf(tmp68, tmp68, AF.Ln, bias=1.0), reads=["tmp68"], writes=["tmp68"])
        S.op("dve", tt(g_c, tmp68, nexpA_row.unsqueeze(1).to_broadcast([128, NT, 4]), ALU.mult), reads=["tmp68", "nexpA"], writes=["g_c"])
        g68 = g_c.rearrange("p i h -> p (i h)"); gc68 = gc_c.rearrange("p i h -> p (i h)")
        S.group("pe", [mm(ps[5][:, 0:68], mask_incl, g68)], reads=["mask_incl", "g_c"], writes=[PK[5]])
        S.op("dve", cp(gc68, ps[5][:, 0:68]), reads=[PK[5]], writes=["gc_c"])
        S.group("pe", [mm(ps[5][:, 128:196], sel127, gc68)], reads=["sel127", "gc_c"], writes=[PK[5]])
        S.op("dve", cp(egl_c.rearrange("p i h -> p (i h)"), ps[5][:, 128:196]), reads=[PK[5]], writes=["egl_c"])
        S.op("dve", tt(tmp68.rearrange("p i h -> p (i h)"), egl_c.rearrange("p i h -> p (i h)"), gc68, ALU.subtract), reads=["egl_c", "gc_c"], writes=["tmp68"])
        S.op("act", actf(egl_c, egl_c, AF.Exp), reads=["egl_c", "tmp68"], writes=["egl_c"])
        S.op("act", actf(kd_c, tmp68, AF.Exp), reads=["tmp68"], writes=["kd_c"])
        S.op("act", actf(bexp_c, gc_c, AF.Exp), reads=["gc_c"], writes=["bexp_c"])
        S.op("dve", tt(bexp_c, bexp_c, beta_c, ALU.mult), reads=["bexp_c", "beta_c"], writes=["bexp_c"])

        sl = load_w(w_in_v, C_U)
        for h in range(4):
            for (t0, n) in SEGS:
                b = next_bank()
                S.group("pe", [mm(ps[b][:, 0:n], wb[sl][:, k, h * 128:(h + 1) * 128], hT[:, k, t0:t0 + n], start=(k == 0), stop=(k == 7)) for k in range(8)],
                        reads=seg_hT_keys(t0, n) + [("wb", sl)], writes=[PK[b]])
                u = f512[bank_n[0] % 2]
                S.op("act", actf(u[:, 0:n], ps[b][:, 0:n], AF.Gelu_apprx_tanh), reads=[PK[b]], writes=[("u", bank_n[0] % 2)])
                b2 = 4 + bank_n[0] % 2
                if n == 512:
                    tiles = [t0 // 128 + j for j in range(4)]
                    S.group("pe", [mm(ps[b2][:, j * 128:(j + 1) * 128], vg[:, tiles[j], h * 128:(h + 1) * 128], wsT[:, h, :]) for j in range(4)],
                            reads=[("vg", ti) for ti in tiles] + ["wsT"], writes=[PK[b2]])
                    S.op("dve", tt(f512[4][:, :].rearrange("p (j t) -> p j t", j=4), ps[b2][:, :].rearrange("p (j t) -> p j t", j=4),
                                   bs_row[:, h:h + 1, :].to_broadcast([128, 4, 128]), ALU.add), reads=[PK[b2], "rows"], writes=["mixt"])
                else:
                    S.group("pe", [mm(ps[b2][:, 0:16], vg[0:16, 16, h * 128:(h + 1) * 128], selws[0:16, h, :])],
                            reads=[("vg", 16), ("selws", h)], writes=[PK[b2]])
                    S.op("dve", tt(f512[4][:, 0:16], ps[b2][:, 0:16], bs_row[:, h, 0:1].to_broadcast([128, 16]), ALU.add), reads=[PK[b2], "rows"], writes=["mixt"])
                S.op("pool", tt(cat[:, 4 + h, t0:t0 + n], f512[4][:, 0:n], u[:, 0:n], ALU.mult), reads=["mixt", ("u", bank_n[0] % 2)], writes=[("cat", 4 + h, t0)])

        sl = load_w(w_in_v, C_Z)
        for h in range(4):
            for (t0, n) in SEGS:
                b = next_bank()
                S.group("pe", [mm(ps[b][:, 0:n], wb[sl][:, k, h * 128:(h + 1) * 128], hT[:, k, t0:t0 + n], start=(k == 0), stop=(k == 7)) for k in range(8)],
                        reads=seg_hT_keys(t0, n) + [("wb", sl)], writes=[PK[b]])
                S.op("act", actf(zs[:, h, t0:t0 + n], ps[b][:, 0:n], AF.Silu), reads=[PK[b]], writes=[("zs", h, t0)])

        S.barrier()

        def qkv_unit(blk, h, si, ui, wsl):
            ops = []
            Lop, Lgr, Ldma = mk_recorders(S, ops)
            c = blk * 4 + h
            t0, n = SEGS[si]
            bs = ui % NBS
            pr, acc, sq, rn = pre[bs], accb[bs], sqb[bs], rnb[bs]
            kp, ka, ks, kr = ("pre", bs), ("acc", bs), ("sq", bs), ("rn", bs)
            b = ui % 4; bn = 4 + ui % 4
            Lgr("pe", [mm(ps[b][:, 0:n], wb[wsl][:, k, h * 128:(h + 1) * 128], hT[:, k, t0:t0 + n], start=(k == 0), stop=(k == 7)) for k in range(8)],
                reads=[("wb", wsl)], writes=[PK[b]])
            if si == 0:
                Lop("dve", lambda e: e.memset(pr[:, 0:3], 0.0), writes=[kp])
            elif si < 4:
                Lgr("pe", [mm(ps[bn][:, 0:3], wb[wsl][:, k, h * 128:(h + 1) * 128], hT[:, k, t0 - 3:t0], start=(k == 0), stop=(k == 7)) for k in range(8)],
                    reads=[("wb", wsl)], writes=[PK[bn]])
                Lop("dve", cp(pr[:, 0:3], ps[bn][:, 0:3]), reads=[PK[bn]], writes=[kp])
            Lop("act", actf(pr[:, 3:3 + n], ps[b][:, 0:n], AF.Copy), reads=[PK[b], kp], writes=[kp])
            if si == 3:
                Lop("dve", cp(ncp_st[:, c, :], pr[:, 512:515]), reads=[kp], writes=[("ncp_st", c)])
            if si < 4 and os.environ.get("CONV", "dve") == "dve":
                Lop("act", actf(acc[:, 0:n], pr[:, 3:3 + n], AF.Copy, scale=wcol(3, c)), reads=[kp, "cols"], writes=[ka])
                Lop("dve", stt(acc[:, 0:n], pr[:, 2:2 + n], wcol(2, c), acc[:, 0:n], ALU.mult, ALU.add), reads=[kp, ka, "cols"], writes=[ka])
                Lop("dve", stt(acc[:, 0:n], pr[:, 1:1 + n], wcol(1, c), acc[:, 0:n], ALU.mult, ALU.add), reads=[kp, ka, "cols"], writes=[ka])
                Lop("dve", stt(acc[:, 0:n], pr[:, 0:n], wcol(0, c), acc[:, 0:n], ALU.mult, ALU.add), reads=[kp, ka, "cols"], writes=[ka])
                Lop("act", actf(acc[:, 0:n], acc[:, 0:n], AF.Silu), reads=[ka], writes=[ka])
            elif si < 4:
                wd = wdiag[c % 2]
                bc_ = 4 + ui % 4
                fns = []
                for t4 in range(4):
                    for j in range(4):
                        fns.append(mm(ps[bc_][:, t4 * 128:(t4 + 1) * 128], wd[:, j, :], pr[:, j + t4 * 128:j + (t4 + 1) * 128], start=(j == 0), stop=(j == 3)))
                Lgr("pe", fns, reads=[kp, ("wdiag", c % 2)], writes=[PK[bc_]])
                Lop("act", actf(acc[:, 0:n], ps[bc_][:, 0:n], AF.Silu), reads=[PK[bc_]], writes=[ka])
            else:
                Lop("dve", cp(ncs_st[:, c, :], pr[:, 3:19]), reads=[kp], writes=[("ncs_st", c)])
                Lop("act", actf(acc[:, 0:n], pr[:, 3:3 + n], AF.Copy, scale=wcol(3, c)), reads=[kp, "cols"], writes=[ka])
                for j in (2, 1, 0):
                    Lop("dve", stt(acc[:, 0:n], histT[:, c, j, :], wcol(j, c), acc[:, 0:n], ALU.mult, ALU.add), reads=[("histT", c), ka, "cols"], writes=[ka])
                Lop("act", actf(acc[:, 0:n], acc[:, 0:n], AF.Silu), reads=[ka], writes=[ka])
            if blk == 2:
                Lop("pool", cp(qkv[:, c, t0:t0 + n], acc[:, 0:n]), reads=[ka], writes=[("qkv", c, t0)])
                if si == 4:
                    Lop("pool", cp(qks_f[:, c, :], acc[:, 0:16]), reads=[ka], writes=[("qks_f", c)])
            else:
                Lop("pool", tt(sq[:, 0:n], acc[:, 0:n], acc[:, 0:n], ALU.mult), reads=[ka], writes=[ks])
                Lgr("pe", [mm(ps[bn][:, 0:n], ones_b, sq[:, 0:n])], reads=[ks, "ones_b"], writes=[PK[bn]])
                Lop("act", actf(rn[:, 0:n], ps[bn][:, 0:n], AF.Ln, bias=1e-6), reads=[PK[bn]], writes=[kr])
                Lop("act", actf(rn[:, 0:n], rn[:, 0:n], AF.Exp, scale=mhalf), reads=[kr], writes=[kr])
                scl = (128.0 ** -0.5) if blk == 0 else 1.0
                Lop("dve", stt(qkv[:, c, t0:t0 + n], acc[:, 0:n], scl, rn[:, 0:n], ALU.mult, ALU.mult), reads=[ka, kr], writes=[("qkv", c, t0)])
                if si == 4:
                    Lop("dve", stt(qks_f[:, c, :], acc[:, 0:16], scl, rn[:, 0:16], ALU.mult, ALU.mult), reads=[ka, kr], writes=[("qks_f", c)])
            return ops

        ui = 0
        units = []
        for blk in range(3):
            wsl = (3 + blk) % 2
            for h in range(4):
                c = blk * 4 + h
                for si in range(5):
                    uo = qkv_unit(blk, h, si, ui, wsl)
                    pre_ops = []
                    if h == 0 and si == 0:
                        pre_ops.append(lambda blk=blk: load_w(w_in_v, blk * 512))
                    if si == 0:
                        scs = sct[c % 2]
                        pre_ops.append(lambda c=c, scs=scs: S.dma("sp", "sct%d" % (c % 2), dmaf(scs[0:16, :, :], st_conv[:, :, c * 128:(c + 1) * 128]), writes=[("sct", c % 2)]))
                        pre_ops.append(lambda c=c, scs=scs: S.group("pe", [mm(ps[4 + c % 4][:, j * 16:(j + 1) * 16], scs[0:16, j, :], ident_f[0:16, 0:16]) for j in range(3)],
                                                                   reads=[("sct", c % 2), "ident_f"], writes=[PK[4 + c % 4]]))
                        pre_ops.append(lambda c=c: S.op("dve", cp(histT[:, c, :, :], ps[4 + c % 4][:, 0:48].rearrange("p (j b) -> p j b", j=3)), reads=[PK[4 + c % 4]], writes=[("histT", c)]))
                    units.append(pre_ops + uo)
                    ui += 1
        run_lanes(units, 4, int(os.environ.get("STAG_Q", "2")))

        S.barrier()
        XS = Bump(arena, X0, ARENA)
        S_all = XS.alloc(F32, [128, 64, 128])
        if 'sample' not in os.environ.get('KSKIP', ''):
            for q4 in range(4):
                S.dma("sp", "sall%d" % q4, dmaf(S_all[:, q4 * 16:(q4 + 1) * 16, :], st_gdn[q4 * 4:(q4 + 1) * 4].rearrange("b h d e -> d (b h) e")), writes=[("S_all", q4)])
        S.group("pe", [mm(ps[c // 4][0:3, (c % 4) * 128:(c % 4 + 1) * 128], ncp_st[:, c, :], ident_f) for c in range(12)],
                reads=[("ncp_st", c) for c in range(12)] + ["ident_f"], writes=[PK[0], PK[1], PK[2]])
        for q3 in range(3):
            S.op("dve", cp(stage_tok[0:3, q3 * 512:(q3 + 1) * 512], ps[q3][0:3, :]), reads=[PK[q3]], writes=["stage_tok"])
        S.dma("sp", "o_ncp", dmaf(ncp, stage_tok[0:3, :]), reads=["stage_tok"])
        S.group("pe", [mm(ps[c // 4][0:16, (c % 4) * 128:(c % 4 + 1) * 128], ncs_st[:, c, :], ident_f) for c in range(12)],
                reads=[("ncs_st", c) for c in range(12)] + ["ident_f"], writes=[PK[0], PK[1], PK[2]])
        for q3 in range(3):
            S.op("act", actf(stage2[0:16, q3 * 512:(q3 + 1) * 512], ps[q3][0:16, :], AF.Copy), reads=[PK[q3]], writes=["stage2"])
        S.dma("sp", "o_ncs", [dmaf(ncs[:, 2, :], stage2[0:16, :]), dmaf(ncs[:, 0:2, :], st_conv[:, 1:3, :])], reads=["stage2"])
        S.barrier()

        YG = Bump(arena, Y0, ARENA)
        osq = YG.alloc(BF16, [128, 512]); rn_o = YG.alloc(F32, [128, 512]); on_o = YG.alloc(F32, [128, 512])
        YG_EPI = YG.off
        NPW, DP = 4, 8
        CHDT = F32
        gN = lambda n_, dt_, shp: [YG.alloc(dt_, shp) for _ in range(n_)]
        Rs = gN(NPW, F32, [128, 256]); rhsR = gN(NPW, F32, [128, 256]); D0 = gN(NPW, F32, [128, 128]); E0 = gN(NPW, F32, [128, 128])
        EGr = gN(NPW, F32, [128, 128]); MB = gN(NPW, F32, [128, 128]); Qf = gN(NPW, F32, [128, 128]); Qs = gN(NPW, BF16, [128, 128])
        NNa = gN(NPW, CHDT, [128, 256]); NNb = gN(NPW, CHDT, [128, 256]); Xs = gN(NPW, BF16, [128, 128])
        NHa = gN(NPW, BF16, [128, 256]); NHb = gN(NPW, BF16, [128, 256])
        J0 = int(os.environ.get('GDN_J0', '6'))
        Qm = gN(DP, BF16, [128, 128]); attnT = gN(DP, BF16, [128, 128]); Kd = gN(DP, BF16, [128, 128]); Vb = gN(DP, BF16, [128, 128])
        qg = gN(DP, BF16, [128, 128]); nWT = gN(DP, BF16, [128, 128]); vn = gN(4, BF16, [128, 128])
        S.op("pool", lambda e: e.memset(S_f.rearrange("p h e -> p (h e)"), 0.0), writes=[("S_f", h) for h in range(4)])
        S.op("pool", lambda e: e.memset(S_b.rearrange("p h e -> p (h e)"), 0.0), writes=[("S_b", h) for h in range(4)])

        def gdn_P(n, h):
            ops = []
            Lop, Lgr, Ldma = mk_recorders(S, ops)
            u = n * 4 + h
            q = u % NPW; s = u % DP
            tok = slice(n * 128, (n + 1) * 128)
            kT = qkv[:, 4 + h, tok]; qT = qkv[:, h, tok]; vT = qkv[:, 8 + h, tok]
            col = lambda t: t[:, n, h:h + 1]
            K = lambda name: (name, q)
            H = lambda name: (name, s)
            bk = PK[q]; pb = ps[q]; pbb = psb[q]
            kk = pb[:, 256:384]; qk = pb[:, 384:512]
            Lop("pool", stt_pool(rhsR[q][:, 0:128], mask_incl, col(g_c)), reads=["mask_incl", "g_c"], writes=[K("rhsR")])
            Lop("pool", stt_pool(rhsR[q][:, 128:256], ident_f, col(beta_c)), reads=["ident_f", "beta_c"], writes=[K("rhsR")])
            Lgr("pe", [lambda e: e.transpose(out=pbb[:, 0:128], in_=kT, identity=ident_b),
                       lambda e: e.transpose(out=pbb[:, 128:256], in_=vT, identity=ident_b)], reads=[("qkv", n), "ident_b"], writes=[bk])
            ktok = pbb[:, 0:128]; vtok = pbb[:, 128:256]
            Lop("act", actf(Xs[q], ktok, AF.Copy, scale=col(bexp_c)), reads=[bk, "bexp_c"], writes=[K("Xs")])
            Lop("act", actf(Kd[s], ktok, AF.Copy, scale=col(kd_c)), reads=[bk, "kd_c"], writes=[H("Kd")])
            Lop("act", actf(Vb[s], vtok, AF.Copy, scale=col(beta_c)), reads=[bk, "beta_c"], writes=[H("Vb")])
            Lgr("pe", [mm(pb[:, 0:128], ones_f, rhsR[q][:, 0:128]), mm(pb[:, 128:256], ones_f, rhsR[q][:, 128:256]),
                       mm(kk, kT, kT), mm(qk, kT, qT)], reads=[K("rhsR"), "ones_f", ("qkv", n)], writes=[bk])
            Lop("dve", cp(Rs[q], pb[:, 0:256]), reads=[bk], writes=[K("Rs")])
            R_gc = Rs[q][:, 0:128]; R_be = Rs[q][:, 128:256]
            Lop("pool", lambda e: e.tensor_tensor(out=D0[q], in0=R_gc, in1=col(gc_c).to_broadcast([128, 128]), op=ALU.subtract), reads=[K("Rs"), "gc_c"], writes=[K("D0")])
            Lop("pool", ts(D0[q], D0[q], 0.0, ALU.min), reads=[K("D0")], writes=[K("D0")])
            Lop("act", actf(D0[q], D0[q], AF.Exp), reads=[K("D0")], writes=[K("D0")])
            Lop("act", actf(EGr[q], R_gc, AF.Exp), reads=[K("Rs")], writes=[K("EGr")])
            Lop("pool", tt(MB[q], R_be, D0[q], ALU.mult), reads=[K("Rs"), K("D0")], writes=[K("MB")])
            Lop("pool", tt(MB[q], MB[q], nmask_su, ALU.mult), reads=[K("MB"), "nmask_su"], writes=[K("MB")])
            Lop("pool", tt(D0[q], D0[q], mask_incl, ALU.mult), reads=[K("D0"), K("MB"), "mask_incl"], writes=[K("D0")])
            Lop("pool", tt(qg[s], qT, EGr[q], ALU.mult), reads=[("qkv", n), K("EGr")], writes=[H("qg")])
            Lop("dve", tt(NNa[q][:, 0:128], kk, MB[q], ALU.mult), reads=[bk, K("MB")], writes=[K("NNa")])
            Lop("dve", tt(attnT[s], qk, D0[q], ALU.mult), reads=[bk, K("D0")], writes=[H("attnT")])
            Lgr("pe", [mm(pb[:, 0:128], NNa[q][:, 0:128], ident_f)], reads=[K("NNa"), "ident_f"], writes=[bk])
            Lop("act", actf(NNa[q][:, 128:256], pb[:, 0:128], AF.Copy), reads=[bk], writes=[K("NNa")])
            Lop("pool", tt(Qf[q], ident_f, NNa[q][:, 0:128], ALU.add), reads=["ident_f", K("NNa")], writes=[K("Qf")])
            cur, nxt, kc, kn = NNa[q], NNb[q], K("NNa"), K("NNb")
            cur16, nxt16, kc16, kn16 = NHa[q], NHb[q], K("NHa"), K("NHb")
            if J0 == 0:
                Lop("pool", cp(cur16, cur), reads=[kc], writes=[kc16])
            for j in range(1, 7):
                f32lvl = j <= J0
                src, ksrc = (cur, kc) if f32lvl else (cur16, kc16)
                fns = []
                if j < 6:
                    fns.append(mm(pb[:, 0:128], src[:, 128:256], src[:, 0:128]))
                fns.append(mm(pb[:, 128:256], src[:, 0:128], src[:, 128:256]))
                Lgr("pe", fns, reads=[ksrc], writes=[bk])
                lo = 0 if j < 6 else 128
                if f32lvl:
                    Lop("act", actf(nxt[:, lo:256], pb[:, lo:256], AF.Copy), reads=[bk], writes=[kn])
                    if j == J0 and j < 6:
                        Lop("pool", cp(nxt16[:, lo:256], nxt[:, lo:256]), reads=[kn], writes=[kn16])
                    Lgr("pe", [mm(pb[:, 256:384], nxt[:, 128:256], Qf[q])], reads=[kn, K("Qf")], writes=[bk])
                else:
                    Lop("act", actf(nxt16[:, lo:256], pb[:, lo:256], AF.Copy), reads=[bk], writes=[kn16])
                    Lop("pool", cp(Qs[q], Qf[q]), reads=[K("Qf")], writes=[K("Qs")])
                    Lgr("pe", [mm(pb[:, 256:384], nxt16[:, 128:256], Qs[q])], reads=[kn16, K("Qs")], writes=[bk])
                Lop("dve", tt(Qf[q], Qf[q], pb[:, 256:384], ALU.add), reads=[bk, K("Qf")], writes=[K("Qf")])
                cur, nxt, kc, kn = nxt, cur, kn, kc
                cur16, nxt16, kc16, kn16 = nxt16, cur16, kn16, kc16
            Lop("pool", cp(Qm[s], Qf[q]), reads=[K("Qf")], writes=[H("Qm")])
            Lgr("pe", [mm(pb[:, 384:512], Xs[q], Qm[s])], reads=[K("Xs"), H("Qm")], writes=[bk])
            Lop("act", actf(nWT[s], pb[:, 384:512], AF.Copy, scale=negone), reads=[bk], writes=[H("nWT")])
            return ops

        def gdn_R(n, h):
            ops = []
            Lop, Lgr, Ldma = mk_recorders(S, ops)
            u = n * 4 + h
            s = u % DP
            H = lambda name: (name, s)
            col = lambda t: t[:, n, h:h + 1]
            bR = PK[4]; ob = 5 + n % 2
            V = ps[4][:, h * 128:(h + 1) * 128]
            Lgr("pe", [mm(V, Qm[s], Vb[s], start=True, stop=False),
                       mm(V, nWT[s], S_b[:, h, :], start=False, stop=True)],
                reads=[H("Qm"), H("Vb"), H("nWT"), ("S_b", h)], writes=[bR])
            Lop("dve", cp(vn[h], V), reads=[bR], writes=[("vn", h)])
            Lgr("pe", [mm(V, Kd[s], vn[h])], reads=[H("Kd"), ("vn", h)], writes=[bR])
            Lgr("pe", [mm(ps[ob][:, h * 128:(h + 1) * 128], S_b[:, h, :], qg[s], start=True, stop=False),
                       mm(ps[ob][:, h * 128:(h + 1) * 128], vn[h], attnT[s], start=False, stop=True)],
                reads=[("S_b", h), H("qg"), ("vn", h), H("attnT")], writes=[PK[ob]])
            Lop("dve", stt(S_f[:, h, :], S_f[:, h, :], col(egl_c), V, ALU.mult, ALU.add), reads=[bR, ("S_f", h), "egl_c"], writes=[("S_f", h)])
            Lop("act", actf(S_b[:, h, :], S_f[:, h, :], AF.Copy), reads=[("S_f", h)], writes=[("S_b", h)])
            return ops

        def gdn_epilogue(o_ps, ss_ps, ncol, t0, okeys, sskey):
            w = 4 * ncol
            S.op("act", actf(osq[:, 0:w], o_ps, AF.Square), reads=okeys, writes=["osq"])
            S.group("pe", [mm(ss_ps, ones_b, osq[:, 0:w])], reads=["osq", "ones_b"], writes=[sskey])
            S.op("dve", ts(rn_o[:, 0:w], ss_ps, 1.0 / 128, ALU.mult, EPS, ALU.add), reads=[sskey], writes=["rn_o"])
            S.op("act", actf(rn_o[:, 0:w], rn_o[:, 0:w], AF.Sqrt), reads=["rn_o"], writes=["rn_o"])
            S.op("dve", lambda e: e.reciprocal(out=rn_o[:, 0:w], in_=rn_o[:, 0:w]), reads=["rn_o"], writes=["rn_o"])
            S.op("dve", stt(on_o[:, 0:w], o_ps, gdnn_col, rn_o[:, 0:w], ALU.mult, ALU.mult), reads=okeys + ["rn_o", "cols"], writes=["on_o"])
            S.op("pool", tt(cat[:, 0:4, t0:t0 + ncol], on_o[:, 0:w].rearrange("p (h t) -> p h t", h=4), zs[:, :, t0:t0 + ncol], ALU.mult),
                 reads=["on_o", "zs"], writes=[("cat_o", t0)])

        _SK = os.environ.get('KSKIP', '')
        NCH = 0 if 'prompt' in _SK else int(os.environ.get('GDN_N', '16'))

        def epi_ops(n):
            ob = 5 + n % 2
            return [lambda: gdn_epilogue(ps[ob][:, :], ps[7][:, :], 128, n * 128, [PK[ob]], PK[7])]

        _DO_SAMPLE = 'sample' not in _SK
        def _sample_section():
            YG = Bump(arena, YG_EPI, ARENA)
            sv = YG.alloc(F32, [128, 8])
            rexp = YG.alloc(F32, [128, 8, 16])
            bcs = YG.alloc(F32, [128, 128])
            dcol = YG.alloc(F32, [128, 64])
            dtok = YG.alloc(F32, [128, 512]); ktoks = YG.alloc(F32, [128, 512])
            kmask = [YG.alloc(F32, [128, 512]) for _ in range(2)]
            S.op("dve", cp(sv[0:16, 0:4], beta_c[0:16, 16, :]), reads=["beta_c"], writes=["sv"])
            S.op("act", actf(sv[0:16, 4:8], g_c[0:16, 16, :], AF.Exp), reads=["g_c"], writes=["sv"])
            for j in range(8):
                S.op("dve", ts(rexp[0:16, j, :], ident_f[0:16, 0:16], sv[0:16, j:j + 1], ALU.mult), reads=["sv", "ident_f"], writes=["rexp"])
            S.group("pe", [mm(ps[0][:, 0:128], ones_f[0:16, :], rexp[0:16, :, :].rearrange("p j b -> p (j b)"))], reads=["rexp", "ones_f"], writes=[PK[0]])
            S.op("dve", cp(bcs, ps[0][:, 0:128]), reads=[PK[0]], writes=["bcs"])
            beta_bc = bcs[:, 0:64]; eg_bc = bcs[:, 64:128]
            S.group("pe", [mm(ps[1][:, h * 16 + b:h * 16 + b + 1], S_all[:, b * 4 + h, :], qks_f[:, 4 + h, b:b + 1]) for b in range(16) for h in range(4)],
                    reads=[("S_all", q4) for q4 in range(4)] + ["qks_f"], writes=[PK[1]])
            S.op("dve", tt(dcol, ps[1][:, 0:64], eg_bc, ALU.mult), reads=[PK[1], "bcs"], writes=["dcol"])
            S.op("dve", tt(dcol, qks_f[:, 8:12, :].rearrange("p h b -> p (h b)"), dcol, ALU.subtract), reads=["dcol", "qks_f"], writes=["dcol"])
            S.op("dve", tt(dcol, dcol, beta_bc, ALU.mult), reads=["dcol", "bcs"], writes=["dcol"])
            S.group("pe", [mm(ps[2][0:16, h * 128:(h + 1) * 128], dcol[:, h * 16:(h + 1) * 16], ident_f) for h in range(4)], reads=["dcol", "ident_f"], writes=[PK[2]])
            S.group("pe", [mm(ps[3][0:16, h * 128:(h + 1) * 128], qks_f[:, 4 + h, :], ident_f) for h in range(4)], reads=["qks_f", "ident_f"], writes=[PK[3]])
            S.op("dve", cp(dtok[0:16, :], ps[2][0:16, :]), reads=[PK[2]], writes=["dtok"])
            S.op("act", actf(ktoks[0:16, :], ps[3][0:16, :], AF.Copy), reads=[PK[3]], writes=["ktoks"])
            for b in range(16):
                km = kmask[b % 2]; pb = 4 + b % 2
                S.op("dve", ts(km[0:16, :], ktoks[0:16, :], ident_f[0:16, b:b + 1], ALU.mult), reads=["ktoks", "ident_f"], writes=[("kmask", b % 2)])
                S.group("pe", [mm(ps[pb][:, h * 128:(h + 1) * 128], km[0:16, h * 128:(h + 1) * 128], dtok[0:16, h * 128:(h + 1) * 128]) for h in range(4)],
                        reads=[("kmask", b % 2), "dtok"], writes=[PK[pb]])
                for h in range(4):
                    S.op("dve", stt(S_all[:, b * 4 + h, :], S_all[:, b * 4 + h, :], eg_bc[:, h * 16 + b:h * 16 + b + 1], ps[pb][:, h * 128:(h + 1) * 128], ALU.mult, ALU.add),
                         reads=[PK[pb], "bcs", ("S_all", b // 4)], writes=[("S_all", b // 4)])
            S.group("pe", [mm(ps[1][:, 64 + h * 16 + b:64 + h * 16 + b + 1], S_all[:, b * 4 + h, :], qks_f[:, h, b:b + 1]) for b in range(16) for h in range(4)],
                    reads=[("S_all", q4) for q4 in range(4)] + ["qks_f"], writes=[PK[1]])
            gdn_epilogue(ps[1][:, 64:128], ps[0][:, 128:192], 16, T_P, [PK[1]], PK[0])
            for q4 in range(4):
                S.dma("sp", "o_ngs%d" % q4, dmaf(ngs[q4 * 4:(q4 + 1) * 4].rearrange("b h d e -> d (b h) e"), S_all[:, q4 * 16:(q4 + 1) * 16, :]), reads=[("S_all", q4)])
        if _DO_SAMPLE:
            _sample_section()
        S.barrier()

        LB = Bump(arena, X0, ARENA)
        NL = 8
        lf32 = lambda shp: [LB.alloc(F32, shp) for _ in range(NL)]
        lbf = lambda shp: [LB.alloc(BF16, shp) for _ in range(NL)]
        Rs8 = lf32([128, 256]); rhsR8 = lf32([128, 256]); D08 = lf32([128, 128]); EGr8 = lf32([128, 128]); MB8 = lf32([128, 128]); Qf8 = lf32([128, 128])
        NNa8 = lf32([128, 256]); NNb8 = lf32([128, 256]); rn8 = lf32([128, 128]); on8 = lf32([128, 128])
        Xs8 = lbf([128, 128]); Kd8 = lbf([128, 128]); Vb8 = lbf([128, 128]); qg8 = lbf([128, 128]); at8 = lbf([128, 128])
        Qm8 = lbf([128, 128]); nWT8 = lbf([128, 128]); vn8 = lbf([128, 128]); osq8 = lbf([128, 128])
        r_done = {}

        def gdn_unit(n, h):
            ops = []
            Lop, Lgr, Ldma = mk_recorders(S, ops)
            L = h * 2 + n % 2
            tok = slice(n * 128, (n + 1) * 128)
            kT = qkv[:, 4 + h, tok]; qT = qkv[:, h, tok]; vT = qkv[:, 8 + h, tok]
            col = lambda t: t[:, n, h:h + 1]
            K = lambda name: (name, L)
            bk = PK[L]; pb = ps[L]; pbb = psb[L]
            kk = pb[:, 256:384]; qk = pb[:, 384:512]
            Rs, rhsR, D0, EGr, MB, Qf = Rs8[L], rhsR8[L], D08[L], EGr8[L], MB8[L], Qf8[L]
            Xs, Kd, Vb, qg, attnT, Qm, nWT, vn, osq = Xs8[L], Kd8[L], Vb8[L], qg8[L], at8[L], Qm8[L], nWT8[L], vn8[L], osq8[L]
            Lop("pool", stt_pool(rhsR[:, 0:128], mask_incl, col(g_c)), reads=["mask_incl", "g_c"], writes=[K("rhsR")])
            Lop("pool", stt_pool(rhsR[:, 128:256], ident_f, col(beta_c)), reads=["ident_f", "beta_c"], writes=[K("rhsR")])
            Lgr("pe", [mm(pb[:, 0:128], mask_sl, rhsR[:, 0:128]), mm(pb[:, 128:256], ones_f, rhsR[:, 0:128]),
                       mm(pb[:, 256:384], nmask_sl, rhsR[:, 128:256]),
                       lambda e: e.transpose(out=pbb[:, 768:896], in_=kT, identity=ident_b),
                       lambda e: e.transpose(out=pbb[:, 896:1024], in_=vT, identity=ident_b)],
                reads=[K("rhsR"), "ones_f", "mask_sl", "nmask_sl", "ident_b"], writes=[bk])
            ktok = pbb[:, 768:896]; vtok = pbb[:, 896:1024]
            Lop("act", actf(Rs, pb[:, 0:256], AF.Exp), reads=[bk], writes=[K("Rs")])
            D0 = Rs[:, 0:128]; EGr = Rs[:, 128:256]
            Lop("act", actf(Xs, ktok, AF.Copy, scale=col(bexp_c)), reads=[bk, "bexp_c"], writes=[K("Xs")])
            Lop("act", actf(Kd, ktok, AF.Copy, scale=col(kd_c)), reads=[bk, "kd_c"], writes=[K("Kd")])
            Lop("act", actf(Vb, vtok, AF.Copy, scale=col(beta_c)), reads=[bk, "beta_c"], writes=[K("Vb")])
            Lop("act", actf(MB, pb[:, 256:384], AF.Copy), reads=[bk], writes=[K("MB")])
            Lop("pool", tt(MB, MB, D0, ALU.mult), reads=[K("MB"), K("Rs")], writes=[K("MB")])
            Lop("pool", tt(qg, qT, EGr, ALU.mult), reads=[K("Rs")], writes=[K("qg")])
            Lgr("pe", [mm(pb[:, 0:128], kT, kT), mm(pb[:, 128:256], kT, qT)], reads=[], writes=[bk])
            kk = pb[:, 0:128]; qk = pb[:, 128:256]
            Lop("pool", tt(D0, D0, mask_incl, ALU.mult), reads=[K("Rs"), K("MB"), K("qg"), "mask_incl"], writes=[K("Rs")])
            NNa, NNb = NNa8[L], NNb8[L]
            Lop("dve", tt(NNa[:, 0:128], kk, MB, ALU.mult), reads=[bk, K("MB")], writes=[K("NNa")])
            Lop("dve", tt(attnT, qk, D0, ALU.mult), reads=[bk, K("Rs")], writes=[K("attnT")])
            Lgr("pe", [mm(pb[:, 256:384], NNa[:, 0:128], ident_f)], reads=[K("NNa"), "ident_f"], writes=[bk])
            Lop("act", actf(NNa[:, 128:256], pb[:, 256:384], AF.Copy), reads=[bk], writes=[K("NNa")])
            Lop("pool", tt(Qf, ident_f, NNa[:, 0:128], ALU.add), reads=["ident_f", K("NNa")], writes=[K("Qf")])
            cur, nxt, kc, kn = NNa, NNb, K("NNa"), K("NNb")
            for j in range(1, 7):
                fns = []
                if j < 6:
                    fns.append(mm(pb[:, 0:128], cur[:, 128:256], cur[:, 0:128]))
                fns.append(mm(pb[:, 128:256], cur[:, 0:128], cur[:, 128:256]))
                Lgr("pe", fns, reads=[kc], writes=[bk])
                lo = 0 if j < 6 else 128
                if j in (3, 5):
                    Lop("dve", cp(nxt[:, lo:256], pb[:, lo:256]), reads=[bk], writes=[kn])
                else:
                    Lop("act", actf(nxt[:, lo:256], pb[:, lo:256], AF.Copy), reads=[bk], writes=[kn])
                Lgr("pe", [mm(pb[:, 256:384], nxt[:, 128:256], Qf)], reads=[kn, K("Qf")], writes=[bk])
                Lop("dve", tt(Qf, Qf, pb[:, 256:384], ALU.add), reads=[bk, K("Qf")], writes=[K("Qf")])
                cur, nxt, kc, kn = nxt, cur, kn, kc
            Lop("pool", cp(Qm, Qf), reads=[K("Qf")], writes=[K("Qm")])
            Lgr("pe", [mm(pb[:, 384:512], Xs, Qm)], reads=[K("Xs"), K("Qm")], writes=[bk])
            Lop("act", actf(nWT, pb[:, 384:512], AF.Copy, scale=negone), reads=[bk], writes=[K("nWT")])
            V = pb[:, 384:512]; Oh = pb[:, 0:128]; SSh = pb[:, 128:256]

            def chk():
                assert n == 0 or r_done.get((n - 1, h)), ("emission order violated", n, h)
            ops.append(chk)
            Lgr("pe", [mm(V, Qm, Vb, start=True, stop=False), mm(V, nWT, S_b[:, h, :], start=False, stop=True)],
                reads=[K("Qm"), K("Vb"), K("nWT"), ("S_b", h)], writes=[bk])
            Lop("dve", cp(vn, V), reads=[bk], writes=[K("vn")])
            Lgr("pe", [mm(V, Kd, vn),
                       mm(Oh, S_b[:, h, :], qg, start=True, stop=False), mm(Oh, vn, attnT, start=False, stop=True)],
                reads=[K("Kd"), K("vn"), ("S_b", h), K("qg"), K("attnT")], writes=[bk])
            Lop("dve", stt(S_f[:, h, :], S_f[:, h, :], col(egl_c), V, ALU.mult, ALU.add), reads=[bk, ("S_f", h), "egl_c"], writes=[("S_f", h)])
            Lop("pool", cp(S_b[:, h, :], S_f[:, h, :]), reads=[("S_f", h)], writes=[("S_b", h)])

            def mark():
                r_done[(n, h)] = True
            ops.append(mark)
            rn, on = rn8[L], on8[L]
            Lop("act", actf(osq, Oh, AF.Square), reads=[bk], writes=[K("osq")])
            Lgr("pe", [mm(SSh, ones_b, osq)], reads=[K("osq"), "ones_b"], writes=[bk])
            Lop("dve", ts(rn, SSh, 1.0 / 128, ALU.mult, EPS, ALU.add), reads=[bk], writes=[K("rn")])
            Lop("act", actf(rn, rn, AF.Ln), reads=[K("rn")], writes=[K("rn")])
            Lop("act", actf(rn, rn, AF.Exp, scale=mhalf), reads=[K("rn")], writes=[K("rn")])
            Lop("dve", stt(on, Oh, gdnn_col, rn, ALU.mult, ALU.mult), reads=[bk, K("rn"), "cols"], writes=[K("on")])
            Lop("pool", tt(cat[:, h, tok], on, zs[:, h, tok], ALU.mult), reads=[K("on")], writes=[("cat_o", n, h)])
            return ops

        if NCH:
            u0 = gdn_unit(0, 0)
            LU = len(u0)
            STAG8 = int(os.environ.get('GDN_STAG', '22'))
            lanes = []
            for h in range(4):
                for par in range(2):
                    pad = h * STAG8 + par * (LU // 2)
                    lane = [(lambda: None)] * pad
                    for n in range(par, NCH, 2):
                        lane = lane + gdn_unit(n, h)
                    lanes.append(lane)
            zipper(lanes)
        S.dma("sp", "o_ngp", dmaf(ngp.rearrange("h d e -> d h e"), S_f), reads=[("S_f", h) for h in range(4)])

        S.barrier()

        if 'phasec' in _SK:
            S.finish()
            with nc.Block() as block:
                S.replay(block)
            return nc
        YC = Bump(arena, P_C0, ARENA)
        R = YC.alloc(F32, [128, 8, 528]); xnC = YC.alloc(BF16, [128, 8, 528]); hid = YC.alloc(BF16, [128, 32, 528])
        r8 = [YC.alloc(BF16, [128, 8, 512]) for _ in range(3)]
        r16 = [YC.alloc(BF16, [128, 32, 256]) for _ in range(2)]
        xres = YC.alloc(F32, [128, 4, 1024]); xres_s = YC.alloc(F32, [128, 1024])
        pw = YC.alloc(BF16, [128, 2, 1024]); ptok = [YC.alloc(BF16, [128, 256]) for _ in range(2)]
        pT = YC.alloc(BF16, [128, 2, 528]); sqr = [YC.alloc(BF16, [128, 528]) for _ in range(2)]; rnC = YC.alloc(F32, [128, 528])
        sig = [YC.alloc(F32, [128, 528]) for _ in range(2)]; relu_t = [YC.alloc(F32, [128, 528]) for _ in range(2)]
        ytile = [YC.alloc(F32, [128, 1024]) for _ in range(1)]
        rncol = YC.alloc(F32, [128, 8])
        RNCOL_INIT = [False]
        r8_n = [0]; r16_n = [0]; misc_n = [0]

        r8_seq = []
        for _p in range(4):
            r8_seq += [(w_out_v, 0), (w_out_v, 512)] + [(w_up_v, bb * 512) for bb in range(8)] + [(w_gate_v, 0), (w_gate_v, 512)]
        r8_issued = [0]

        def load_r8(view, c0):
            idx = r8_n[0]
            r8_n[0] += 1
            assert r8_seq[idx][1] == c0
            while r8_issued[0] < min(len(r8_seq), idx + 3):
                j = r8_issued[0]
                vw, cc = r8_seq[j]
                if not (os.environ.get("NORELOAD") and j >= 12):
                    S.dma("pool", "r8_%d" % (j % 3), dmaf(r8[j % 3], vw[:, :, cc:cc + 512]), writes=[("r8", j % 3)])
                r8_issued[0] += 1
            return idx % 3

        def load_r16(c0):
            sl = r16_n[0] % 2
            r16_n[0] += 1
            if not (os.environ.get("NORELOAD") and r16_n[0] > 4):
                S.dma("pool", "r16_%d" % sl, dmaf(r16[sl], w_down_v[:, :, c0:c0 + 256]), writes=[("r16", sl)])
            return sl

        S.dma("pool", "pw", dmaf(pw, w_ple_v), writes=["pw"])
        PASSES = [[(0, 512, 0)], [(512, 512, 0)], [(1024, 512, 0)], [(1536, 512, 0), (2048, 16, 512)]]

        def rms_norm_C(which, out_fn, segs, W, tag):
            bns = []
            for (t0, n, l0) in segs:
                bns.append(6 + misc_n[0] % 2)
                misc_n[0] += 1
            for m in range(8):
                sq = sqr[m % 2]
                S.op("act", actf(sq[:, 0:W], R[:, m, 0:W], AF.Square), reads=[("R", m)], writes=[("sqr", m % 2)])
                for si_, (t0, n, l0) in enumerate(segs):
                    bn = bns[si_]
                    S.group("pe", [mm(ps[bn][:, 0:n], ones_b, sq[:, l0:l0 + n], start=(m == 0), stop=(m == 7))],
                            reads=[("sqr", m % 2), "ones_b"], writes=[PK[bn]])
            for si_, (t0, n, l0) in enumerate(segs):
                bn = bns[si_]
                S.op("dve", ts(rnC[:, l0:l0 + n], ps[bn][:, 0:n], 1.0 / 1024, ALU.mult, EPS, ALU.add), reads=[PK[bn]], writes=["rnC"])
            S.op("act", actf(rnC[:, 0:W], rnC[:, 0:W], AF.Sqrt), reads=["rnC"], writes=["rnC"])
            S.op("dve", lambda e: e.reciprocal(out=rnC[:, 0:W], in_=rnC[:, 0:W]), reads=["rnC"], writes=["rnC"])
            for m in range(8):
                out_ap, wkey = out_fn(m)
                S.op("dve", stt(out_ap, R[:, m, 0:W], gcol(which, m), rnC[:, 0:W], ALU.mult, ALU.mult), reads=[("R", m), "rnC", "cols"], writes=[wkey])

        for pi, segs in enumerate(PASSES):
            W = sum(n for (_, n, _) in segs)
            t00 = segs[0][0]
            has_s = len(segs) > 1
            if pi == 0:
                S.dma("sp", "xres", dmaf(xres, x_p[0:512, :].rearrange("(j p) f -> p j f", p=128)), writes=["xres"])
            def stats_act(m):
                S.op("act", actf(sqr[m % 2][:, 0:W], R[:, m, 0:W], AF.Square), reads=[("R", m)], writes=[("sqr", m % 2)])

            def stats_pe(m, bns):
                for si_, (t0, n, l0) in enumerate(segs):
                    S.group("pe", [mm(ps[bns[si_]][:, 0:n], ones_b, sqr[m % 2][:, l0:l0 + n], start=(m == 0), stop=(m == 7))],
                            reads=[("sqr", m % 2), "ones_b"], writes=[PK[bns[si_]]])

            def norm_finish_row(bns, out_t, key, square):
                for si_, (t0, n, l0) in enumerate(segs):
                    S.op("dve", ts(out_t[:, l0:l0 + n], ps[bns[si_]][:, 0:n], 1.0 / 1024, ALU.mult, EPS, ALU.add), reads=[PK[bns[si_]]], writes=[key])
                if not square:
                    S.op("act", actf(out_t[:, 0:W], out_t[:, 0:W], AF.Sqrt), reads=[key], writes=[key])
                S.op("dve", lambda e: e.reciprocal(out=out_t[:, 0:W], in_=out_t[:, 0:W]), reads=[key], writes=[key])

            def pick_bns():
                o = []
                for _ in segs:
                    o.append(6 + misc_n[0] % 2)
                    misc_n[0] += 1
                return o

            bns1 = pick_bns()
            for blk in range(2):
                sl = load_r8(w_out_v, blk * 512)
                for m4 in range(4):
                    m = blk * 4 + m4
                    for (t0, n, l0) in segs:
                        b = next_bank()
                        fns = [mm(ps[b][:, 0:n], r8[sl][:, k, m4 * 128:(m4 + 1) * 128], cat[:, k, t0:t0 + n], start=(k == 0), stop=False) for k in range(8)]
                        if n == 512:
                            fns += [mm(ps[b][:, j * 128:(j + 1) * 128], xres[:, j, m * 128:(m + 1) * 128], ident_f, start=False, stop=(j == 3)) for j in range(4)]
                            rk = ["xres"]
                        else:
                            fns += [mm(ps[b][:, 0:16], xres_s[0:16, m * 128:(m + 1) * 128], ident_f[0:16, 0:16], start=False, stop=True)]
                            rk = ["xres_s"]
                        S.group("pe", fns, reads=[("r8", sl), "cat", "ident_f"] + rk, writes=[PK[b]])
                        S.op("act", actf(R[:, m, l0:l0 + n], ps[b][:, 0:n], AF.Copy), reads=[PK[b]], writes=[("R", m)])
                        S.op("act", actf(xnC[:, m, l0:l0 + n], ps[b][:, 0:n], AF.Copy, scale=gcol(0, m)), reads=[PK[b], "cols"], writes=[("xnC", m)])
                    stats_act(m)
                    if m >= 1:
                        stats_pe(m - 1, bns1)
            stats_pe(7, bns1)
            norm_finish_row(bns1, rnC, "rnC", True)
            if pi + 1 < len(PASSES):
                tn = PASSES[pi + 1][0][0]
                S.dma("sp", "xres", dmaf(xres, x_p[tn:tn + 512, :].rearrange("(j p) f -> p j f", p=128)), writes=["xres"])
                if len(PASSES[pi + 1]) > 1:
                    S.dma("sp", "xres_s", dmaf(xres_s[0:16, :], x_s), writes=["xres_s"])
            for blk in range(8):
                sl = load_r8(w_up_v, blk * 512)
                for m4 in range(4):
                    hc = blk * 4 + m4
                    for (t0, n, l0) in segs:
                        b = next_bank()
                        S.group("pe", [mm(ps[b][:, 0:n], r8[sl][:, k, m4 * 128:(m4 + 1) * 128], xnC[:, k, l0:l0 + n], start=(k == 0), stop=(k == 7)) for k in range(8)],
                                reads=[("r8", sl)] + [("xnC", k) for k in range(8)], writes=[PK[b]])
                        rt = relu_t[misc_n[0] % 2]; rkey = ("relu_t", misc_n[0] % 2)
                        misc_n[0] += 1
                        S.op("act", actf(rt[:, 0:n], ps[b][:, 0:n], AF.Relu), reads=[PK[b]], writes=[rkey])
                        S.op("dve", tt(hid[:, hc, l0:l0 + n], rt[:, 0:n], rt[:, 0:n], ALU.mult), reads=[rkey], writes=[("hid", hc)])
            bns2 = pick_bns()
            for blk in range(4):
                sl = load_r16(blk * 256)
                for m2 in range(2):
                    m = blk * 2 + m2
                    for (t0, n, l0) in segs:
                        b = next_bank()
                        S.group("pe", [mm(ps[b][:, 0:n], r16[sl][:, k, m2 * 128:(m2 + 1) * 128], hid[:, k, l0:l0 + n], start=(k == 0), stop=(k == 31)) for k in range(32)],
                                reads=[("r16", sl)] + [("hid", k) for k in range(32)], writes=[PK[b]])
                        sg = sig[misc_n[0] % 2]; skey = ("sig", misc_n[0] % 2)
                        misc_n[0] += 1
                        S.op("dve", tt(sg[:, 0:n], ps[b][:, 0:n], rnC[:, l0:l0 + n], ALU.mult), reads=[PK[b], "rnC"], writes=[skey])
                        S.op("dve", tt(R[:, m, l0:l0 + n], R[:, m, l0:l0 + n], sg[:, 0:n], ALU.add), reads=[skey, ("R", m)], writes=[("R", m)])
                    S.op("act", actf(xnC[:, m, 0:W], R[:, m, 0:W], AF.Copy, scale=gcol(1, m)), reads=[("R", m), "cols"], writes=[("xnC", m)])
                    stats_act(m)
                    if m >= 1:
                        stats_pe(m - 1, bns2)
            stats_pe(7, bns2)
            norm_finish_row(bns2, rnC, "rnC", False)
            for (t0, n, l0) in segs:
                ntile = (n + 127) // 128
                for j in range(ntile):
                    r = min(128, n - j * 128)
                    sl = misc_n[0] % 2
                    misc_n[0] += 1
                    src = p_p[t0 + j * 128:t0 + j * 128 + r, :] if n == 512 else p_s
                    S.dma("pool", "ptok%d" % sl, dmaf(ptok[sl][0:r, :], src), writes=[("ptok", sl)])
                    S.group("pe", [lambda e, kk=kk, sl=sl, r=r: e.transpose(out=psb[5][:, kk * 128:kk * 128 + r], in_=ptok[sl][0:r, kk * 128:(kk + 1) * 128], identity=ident_b[0:r, 0:r]) for kk in range(2)],
                            reads=[("ptok", sl), "ident_b"], writes=[PK[5]])
                    S.op("act", actf(pT[:, :, l0 + j * 128:l0 + j * 128 + r], psb[5][:, 0:256].rearrange("p (k t) -> p k t", k=2)[:, :, 0:r], AF.Copy), reads=[PK[5]], writes=["pT"])
            ntt = sum((n + 127) // 128 for (_, n, _) in segs)
            sigbufs = [(sig[0], ("sig", 0)), (sig[1], ("sig", 1)), (relu_t[0], ("relu_t", 0)), (relu_t[1], ("relu_t", 1))]

            def gate_chunk(m, sl, m4):
                ops = []
                Lop, Lgr, Ldma = mk_recorders(S, ops)
                for si_, (t0, n, l0) in enumerate(segs):
                    b = (2 * m + si_) % 4
                    pb_ = 6 + m % 2
                    sg, skey = sigbufs[(2 * m + si_) % 4]
                    Lgr("pe", [mm(ps[b][:, 0:n], r8[sl][:, k, m4 * 128:(m4 + 1) * 128], xnC[:, k, l0:l0 + n], start=(k == 0), stop=(k == 7)) for k in range(8)],
                        reads=[("r8", sl)] + [("xnC", k) for k in range(8)], writes=[PK[b]])
                    Lgr("pe", [mm(ps[pb_][:, 0:n], pw[:, kk, m * 128:(m + 1) * 128], pT[:, kk, l0:l0 + n], start=(kk == 0), stop=(kk == 1)) for kk in range(2)],
                        reads=["pw", "pT"], writes=[PK[pb_]])
                    Lop("dve", tt(sg[:, 0:n], ps[b][:, 0:n], rnC[:, l0:l0 + n], ALU.mult), reads=[PK[b], "rnC"], writes=[skey])
                    Lop("act", actf(sg[:, 0:n], sg[:, 0:n], AF.Sigmoid), reads=[skey], writes=[skey])
                    Lop("dve", tt(sg[:, 0:n], sg[:, 0:n], ps[pb_][:, 0:n], ALU.mult), reads=[PK[pb_], skey], writes=[skey])
                    Lop("dve", tt(R[:, m, l0:l0 + n], R[:, m, l0:l0 + n], sg[:, 0:n], ALU.add), reads=[skey, ("R", m)], writes=[("R", m)])
                sq = sqr[m % 2]
                Lop("act", actf(sq[:, 0:W], R[:, m, 0:W], AF.Square), reads=[("R", m)], writes=[("sqr", m % 2)])
                fns = []
                if m == 0:
                    fns.append(mm(ps[5][:, 256:256 + ntt], zeros_f, zeros_f[:, 0:ntt], start=True, stop=False))
                jt = 0
                for (t0, n, l0) in segs:
                    for j in range((n + 127) // 128):
                        r = min(128, n - j * 128)
                        fns.append(mm(ps[5][0:r, 256 + jt:257 + jt], sq[:, l0 + j * 128:l0 + j * 128 + r], ones_b[:, 0:1], start=False, stop=False))
                        jt += 1
                if m == 7:
                    fns.append(mm(ps[5][:, 256:256 + ntt], zeros_f, zeros_f[:, 0:ntt], start=False, stop=True))
                Lgr("pe", fns, reads=[("sqr", m % 2), "ones_b"], writes=[PK[5]])
                Lop("act", actf(R[:, m, 0:W], R[:, m, 0:W], AF.Copy, scale=gcol(2, m)), reads=[("R", m), ("sqr", m % 2), "cols"], writes=[("R", m)])
                return ops

            for blk in range(2):
                sl = load_r8(w_gate_v, blk * 512)
                chunks = [gate_chunk(blk * 4 + m4, sl, m4) for m4 in range(4)]
                zipper(chunks[0:2])
                zipper(chunks[2:4])
            if not RNCOL_INIT[0]:
                RNCOL_INIT[0] = True
                S.op("dve", lambda e: e.memset(rncol, 1.0), writes=["rncol"])
            S.op("dve", ts(rncol[:, 0:ntt], ps[5][:, 256:256 + ntt], 1.0 / 1024, ALU.mult, EPS, ALU.add), reads=[PK[5]], writes=["rncol"])
            S.op("act", actf(rncol[:, 0:ntt], rncol[:, 0:ntt], AF.Sqrt), reads=["rncol"], writes=["rncol"])
            S.op("dve", lambda e: e.reciprocal(out=rncol[:, 0:ntt], in_=rncol[:, 0:ntt]), reads=["rncol"], writes=["rncol"])
            jt = 0
            for (t0, n, l0) in segs:
                ntile = (n + 127) // 128
                for j in range(ntile):
                    r = min(128, n - j * 128)
                    ysl = 0
                    for half in range(2):
                        b = next_bank()
                        S.group("pe", [(lambda e, m4=m4, b=b, r=r, half=half, l0=l0, j=j: e.transpose(out=ps[b][0:r, m4 * 128:(m4 + 1) * 128], in_=R[:, half * 4 + m4, l0 + j * 128:l0 + j * 128 + r], identity=ident_f)) for m4 in range(4)],
                                reads=[("R", half * 4 + m4) for m4 in range(4)] + ["ident_f"], writes=[PK[b]])
                        S.op("act", actf(ytile[ysl][0:r, half * 512:(half + 1) * 512], ps[b][0:r, :], AF.Copy, scale=rncol[0:r, jt:jt + 1]), reads=[PK[b], "rncol"], writes=[("ytile", ysl)])
                    jt += 1
                    dst = y_p[t0 + j * 128:t0 + j * 128 + r, :] if n == 512 else y_s
                    S.dma("sp", "o_y%d" % ysl, dmaf(dst, ytile[ysl][0:r, :]), reads=[("ytile", ysl)])
        S.finish()
        with nc.Block() as block:
            S.replay(block)
    return nc


_PROG = {}


def _make_in_maps(inputs):
    f = lambda a: np.ascontiguousarray(np.asarray(a, dtype=np.float32))
    g = {k: f(v) for k, v in inputs.items()}
    shared = {
        "g_mix": g["g_mix"].reshape(1, 1024), "w_in": g["w_in"][0], "w_conv": g["w_conv"][0],
        "a_log": g["a_log"].reshape(1, 4), "dt_bias": g["dt_bias"].reshape(1, 4), "gdn_norm": g["gdn_norm"].reshape(1, 128),
        "ln_g": g["sgu_ln_g"].reshape(1, 512), "ln_b": g["sgu_ln_b"].reshape(1, 512), "w_s": g["w_s"][0],
        "b_s": g["b_s"].reshape(1, 512), "w_out": g["w_out"][0], "g_ff": g["g_ff"].reshape(8, 128), "w_up": g["w_up"][0],
        "w_down": g["w_down"][0], "g_ple": g["g_ple"].reshape(8, 128), "w_ple": g["w_ple"][0], "w_gate": g["w_ple_gate"][0],
        "g_fin": g["g_final"].reshape(8, 128),
    }
    maps = []
    for i in range(8):
        m = dict(shared)
        sl = slice(16 * i, 16 * i + 16)
        m["x_p"] = g["x_prompt"][i]
        m["x_s"] = g["x_sample"][sl, 0]
        m["st_conv"] = g["state_conv"][0, sl]
        m["st_gdn"] = g["state_gdn"][0, sl]
        m["p_p"] = g["p_prompt"][0, i]
        m["p_s"] = g["p_sample"][0, sl, 0]
        maps.append(m)
    return maps


def kernel(**inputs):
    if "nc" not in _PROG:
        _PROG["nc"] = build_program()
    nc = _PROG["nc"]
    maps = _make_in_maps(inputs)
    res = run_bass_kernel_spmd(nc, maps, core_ids=list(range(8)))
    R = res.results
    st = lambda name: np.stack([np.asarray(r[name], dtype=np.float32) for r in R])
    cc = lambda name: np.concatenate([np.asarray(r[name], dtype=np.float32) for r in R], axis=0)
    y_prompt = st("y_p")
    y_sample = cc("y_s")[:, None, :]
    new_conv_prompt = st("ncp")[None]
    new_gdn_prompt = st("ngp")[None]
    new_conv_sample = cc("ncs")[None]
    new_gdn_sample = cc("ngs")[None]
    new_sgu_v_sample = cc("nsv")[None, :, None, :]
    return (y_prompt, y_sample, new_conv_prompt, new_gdn_prompt, new_conv_sample, new_gdn_sample, new_sgu_v_sample)
```

```python
import os
import numpy as np
import concourse.bass as bass
import concourse.mybir as mybir
from concourse.bass_utils import run_bass_kernel_spmd

F32 = mybir.dt.float32
BF16 = mybir.dt.bfloat16
AF = mybir.ActivationFunctionType
ALU = mybir.AluOpType
AX = mybir.AxisListType


class Sched:
    ENGS = ("pe", "act", "dve", "pool", "sp")

    def __init__(self, nc, stack):
        self.nc = nc
        self.stack = stack
        self.streams = {e: [] for e in self.ENGS}
        self.esem = {e: stack.enter_context(nc.semaphore("c_" + e)) for e in self.ENGS[:4]}
        self.ecnt = {e: 0 for e in self.ENGS}
        self.waited = {e: {} for e in self.ENGS}
        self.res = {}
        self.dsem = {}
        self.sem_by_name = {}
        for e in self.ENGS[:4]:
            self.sem_by_name[self.esem[e].name] = self.esem[e]

    def _need(self, eng, ev, waits):
        if ev is None:
            return
        name, val, src = ev
        if src == eng and eng == "pe":
            return
        cur = waits.get(name, 0)
        if val > cur:
            waits[name] = val

    def _deps(self, eng, reads, writes):
        waits = {}
        for k in reads:
            r = self.res.get(k)
            if r is not None:
                self._need(eng, r[0], waits)
                if isinstance(k, tuple) and k and k[0] == "ps":
                    for ev in r[1]:
                        if ev[2] != eng:
                            self._need(eng, ev, waits)
        for k in writes:
            r = self.res.get(k)
            if r is not None:
                if r[0] is not None:
                    self._need(eng, r[0], waits)
                for ev in r[1]:
                    self._need(eng, ev, waits)
        out = []
        w = self.waited[eng]
        for name, val in waits.items():
            if w.get(name, 0) < val:
                w[name] = val
                out.append((name, val))
        return out

    def _commit(self, ev, reads, writes):
        for k in reads:
            r = self.res.setdefault(k, [None, []])
            r[1].append(ev)
        for k in writes:
            self.res[k] = [ev, []]

    def op(self, eng, fn, reads=(), writes=()):
        waits = self._deps(eng, reads, writes)
        self.ecnt[eng] += 1
        ev = (self.esem[eng].name, self.ecnt[eng], eng)
        self.streams[eng].append((waits, [fn], ("inc", self.esem[eng], 1)))
        self._commit(ev, reads, writes)
        return ev

    def group(self, eng, fns, reads=(), writes=()):
        waits = self._deps(eng, reads, writes)
        self.ecnt[eng] += 1
        ev = (self.esem[eng].name, self.ecnt[eng], eng)
        self.streams[eng].append((waits, list(fns), ("inc", self.esem[eng], 1)))
        self._commit(ev, reads, writes)
        return ev

    def dma(self, eng, slot, fn, reads=(), writes=(), n=1):
        if slot not in self.dsem:
            s = self.stack.enter_context(self.nc.semaphore("d_" + slot))
            self.dsem[slot] = [s, 0]
            self.sem_by_name[s.name] = s
        waits = self._deps(eng, reads, writes)
        d = self.dsem[slot]
        fns = fn if isinstance(fn, (list, tuple)) else [fn]
        d[1] += 16 * len(fns)
        ev = (d[0].name, d[1], "dma")
        self.streams[eng].append((waits, list(fns), ("dmainc", d[0], 16)))
        self._commit(ev, reads, writes)
        return ev

    def barrier(self, skip=()):
        evs = []
        for e in self.ENGS[:4]:
            if self.ecnt[e] > 0:
                evs.append((self.esem[e].name, self.ecnt[e]))
        for slot, (s, c) in self.dsem.items():
            if c > 0 and not any(slot.startswith(p) for p in skip):
                evs.append((s.name, c))
        for eng in self.ENGS:
            w = self.waited[eng]
            waits = []
            for name, val in evs:
                if w.get(name, 0) < val:
                    w[name] = val
                    waits.append((name, val))
            if waits:
                self.streams[eng].append((waits, [], None))
        self.res.clear()

    def finish(self):
        eng = "sp"
        waits = []
        for slot, (s, c) in self.dsem.items():
            if c > 0:
                waits.append((s.name, c))
        for e in self.ENGS[:4]:
            if self.ecnt[e] > 0:
                waits.append((self.esem[e].name, self.ecnt[e]))
        self.streams[eng].append((waits, [], None))

    def replay(self, block):
        sbn = self.sem_by_name

        def run(e, items):
            for waits, fns, inc in items:
                for name, val in waits:
                    e.wait_ge(sbn[name], val)
                last = None
                for i, f in enumerate(fns):
                    ins = f(e)
                    if inc is not None and inc[0] == "dmainc":
                        ins.then_inc(inc[1], 16)
                    last = ins
                if inc is not None and inc[0] == "inc" and last is not None:
                    last.then_inc(inc[1], 1)

        st = self.streams

        @block.tensor
        def _(e):
            run(e, st["pe"])

        @block.scalar
        def _(e):
            run(e, st["act"])

        @block.vector
        def _(e):
            run(e, st["dve"])

        @block.gpsimd
        def _(e):
            run(e, st["pool"])

        @block.sync
        def _(e):
            run(e, st["sp"])


U8 = mybir.dt.uint8
T_P = 2048
T_S = 16
T_ALL = T_P + T_S
SEGS = [(0, 512), (512, 512), (1024, 512), (1536, 512), (2048, 16)]
NT = 17
EPS = 1e-6
D_IN = 3080
C_Q, C_K, C_V, C_Z, C_BA, C_U, C_VS = 0, 512, 1024, 1536, 2048, 2056, 2568


def mm(out, lhsT, rhs, start=True, stop=True):
    return lambda e: e.matmul(out, lhsT=lhsT, rhs=rhs, start=start, stop=stop)


def actf(out, in_, func, **kw):
    return lambda e: e.activation(out=out, in_=in_, func=func, **kw)


def tt(out, a, b, op):
    return lambda e: e.tensor_tensor(out=out, in0=a, in1=b, op=op)


def ts(out, a, s1, op0, s2=None, op1=None):
    if op1 is None:
        return lambda e: e.tensor_scalar(out=out, in0=a, scalar1=s1, scalar2=None, op0=op0)
    return lambda e: e.tensor_scalar(out=out, in0=a, scalar1=s1, scalar2=s2, op0=op0, op1=op1)


def stt(out, a, s, b, op0, op1):
    return lambda e: e.scalar_tensor_tensor(out=out, in0=a, scalar=s, in1=b, op0=op0, op1=op1)


def stt_pool(out, a, colap):
    return lambda e: e.tensor_tensor(out=out, in0=a, in1=colap.to_broadcast([128, 128]), op=ALU.mult)


def cp(out, in_):
    return lambda e: e.tensor_copy(out=out, in_=in_)


def dmaf(out, in_):
    return lambda e: e.dma_start(out=out, in_=in_)


class _Item:
    __slots__ = ("thunk", "eng", "reads", "writes", "dur")

    def __init__(self, thunk, eng, reads, writes, dur):
        self.thunk, self.eng, self.reads, self.writes, self.dur = thunk, eng, tuple(reads), tuple(writes), dur


_DUR = {"act": 0.5, "dve": 0.45, "pool": 0.5}


def mk_recorders(S, ops):
    def Lop(eng, fn, reads=(), writes=()):
        ops.append(_Item(lambda: S.op(eng, fn, reads=reads, writes=writes), eng, reads, writes, _DUR.get(eng, 0.4)))

    def Lgr(eng, fns, reads=(), writes=()):
        ops.append(_Item(lambda: S.group(eng, fns, reads=reads, writes=writes), eng, reads, writes, 0.1 + 0.13 * len(fns)))

    def Ldma(eng, slot, fn, reads=(), writes=()):
        ops.append(_Item(lambda: S.dma(eng, slot, fn, reads=reads, writes=writes), "q_" + eng, reads, writes, 2.5))
    return Lop, Lgr, Ldma


def zipper(lists):
    lists = [l for l in lists if l]
    idx = [0] * len(lists)
    if os.environ.get("ZIP", "rr") == "rr":
        live = True
        while live:
            live = False
            for i, l in enumerate(lists):
                if idx[i] < len(l):
                    it = l[idx[i]]
                    idx[i] += 1
                    live = True
                    if isinstance(it, _Item):
                        it.thunk()
                    else:
                        it()
        return
    t_eng, t_w, t_r = {}, {}, {}
    remaining = sum(len(l) for l in lists)
    while remaining:
        best = None
        for i, l in enumerate(lists):
            if idx[i] >= len(l):
                continue
            it = l[idx[i]]
            if not isinstance(it, _Item):
                best = (-1.0, i, it)
                break
            rdy = t_eng.get(it.eng, 0.0)
            for k in it.reads:
                rdy = max(rdy, t_w.get(k, 0.0))
            for k in it.writes:
                rdy = max(rdy, t_w.get(k, 0.0), t_r.get(k, 0.0))
            if best is None or rdy < best[0]:
                best = (rdy, i, it)
        rdy, i, it = best
        idx[i] += 1
        remaining -= 1
        if not isinstance(it, _Item):
            it()
            continue
        it.thunk()
        fin = rdy + it.dur
        if it.eng.startswith("q_"):
            t_eng[it.eng] = rdy + 0.1
        else:
            t_eng[it.eng] = fin
        for k in it.reads:
            t_r[k] = max(t_r.get(k, 0.0), fin)
        for k in it.writes:
            t_w[k] = fin
            t_r[k] = 0.0


def run_lanes(units, nl, stag):
    lanes = [[(lambda: None)] * (k * stag) for k in range(nl)]
    for i, u in enumerate(units):
        lanes[i % nl] += u
    zipper(lanes)


class Bump:
    def __init__(self, arena, start, limit):
        self.t, self.off, self.limit = arena, start, limit

    def alloc(self, dtype, shape):
        esz = 4 if dtype == F32 else 2
        n = 1
        for s in shape[1:]:
            n *= s
        nb = (n * esz + 63) // 64 * 64
        o = self.off
        self.off += nb
        assert self.off <= self.limit, ("SBUF arena overflow", self.off, self.limit)
        ap = self.t[:, o:o + n * esz].bitcast(dtype)
        if len(shape) == 3:
            ap = ap.rearrange("p (a b) -> p a b", a=shape[1])
        elif len(shape) == 4:
            ap = ap.rearrange("p (a b c) -> p a b c", a=shape[1], b=shape[2])
        return ap


def build_program():
    from contextlib import ExitStack
    nc = bass.Bass("TRN2", target_bir_lowering=False)

    def din(name, shape):
        return nc.dram_tensor(name, shape, F32, kind="ExternalInput").ap()

    def dout(name, shape):
        return nc.dram_tensor(name, shape, F32, kind="ExternalOutput").ap()

    x_p = din("x_p", [T_P, 1024]); x_s = din("x_s", [T_S, 1024])
    st_conv = din("st_conv", [T_S, 3, 1536]); st_gdn = din("st_gdn", [T_S, 4, 128, 128])
    p_p = din("p_p", [T_P, 256]); p_s = din("p_s", [T_S, 256])
    g_mix = din("g_mix", [1, 1024]); w_in = din("w_in", [1024, D_IN]); w_conv = din("w_conv", [4, 1536])
    a_log = din("a_log", [1, 4]); dt_bias = din("dt_bias", [1, 4]); gdn_norm = din("gdn_norm", [1, 128])
    ln_g = din("ln_g", [1, 512]); ln_b = din("ln_b", [1, 512]); w_s = din("w_s", [4, 128, 128]); b_s = din("b_s", [1, 512])
    w_out = din("w_out", [1024, 1024]); g_ff = din("g_ff", [8, 128]); w_up = din("w_up", [1024, 4096]); w_down = din("w_down", [4096, 1024])
    g_ple = din("g_ple", [8, 128]); w_ple = din("w_ple", [256, 1024]); w_gate = din("w_gate", [1024, 1024]); g_fin = din("g_fin", [8, 128])
    y_p = dout("y_p", [T_P, 1024]); y_s = dout("y_s", [T_S, 1024])
    ncp = dout("ncp", [3, 1536]); ngp = dout("ngp", [4, 128, 128])
    ncs = dout("ncs", [T_S, 3, 1536]); ngs = dout("ngs", [T_S, 4, 128, 128]); nsv = dout("nsv", [T_S, 512])

    w_in_v = w_in.rearrange("(k p) c -> p k c", p=128)
    w_out_v = w_out.rearrange("(k p) c -> p k c", p=128)
    w_up_v = w_up.rearrange("(k p) c -> p k c", p=128)
    w_down_v = w_down.rearrange("(k p) c -> p k c", p=128)
    w_gate_v = w_gate.rearrange("(k p) c -> p k c", p=128)
    w_ple_v = w_ple.rearrange("(k p) c -> p k c", p=128)

    with ExitStack() as st:
        S = Sched(nc, st)
        ARENA = 206 * 1024
        arena = st.enter_context(nc.sbuf_tensor("arena", [128, ARENA], U8))
        ps = [st.enter_context(nc.psum_tensor("ps%d" % i, [128, 512], F32)) for i in range(8)]
        psb = [p[:, :].bitcast(BF16) for p in ps]
        PK = [("ps", i) for i in range(8)]

        P = Bump(arena, 0, ARENA)
        ident_f = P.alloc(F32, [128, 128]); ident_b = P.alloc(BF16, [128, 128])
        ones_f = P.alloc(F32, [128, 128]); ones_b = P.alloc(BF16, [128, 128])
        mask_incl = P.alloc(F32, [128, 128])
        mask_su = P.alloc(F32, [128, 128])
        nmask_sl = P.alloc(F32, [128, 128])
        sel127 = P.alloc(F32, [128, 128])
        nmask_su = P.alloc(F32, [128, 128])
        mask_sl = P.alloc(F32, [128, 128])
        rowstage = P.alloc(F32, [128, 128])
        cols = P.alloc(F32, [128, 128])
        wsT = P.alloc(BF16, [128, 4, 128])
        selws = P.alloc(BF16, [128, 4, 16])
        ws00 = P.alloc(F32, [128, 4])
        bs_row = P.alloc(F32, [128, 4, 128])
        lng_row = P.alloc(F32, [128, 512]); lnb_row = P.alloc(F32, [128, 512])
        alog_row = P.alloc(F32, [128, 4]); dtb_row = P.alloc(F32, [128, 4]); nexpA_row = P.alloc(F32, [128, 4])
        zcol = P.alloc(F32, [128, 4])
        zeros_f = P.alloc(F32, [128, 128])
        cat = P.alloc(BF16, [128, 8, T_ALL])
        P_C0 = P.off
        ba = P.alloc(F32, [128, NT, 8])
        beta_c = P.alloc(F32, [128, NT, 4]); g_c = P.alloc(F32, [128, NT, 4]); gc_c = P.alloc(F32, [128, NT, 4])
        bexp_c = P.alloc(F32, [128, NT, 4]); kd_c = P.alloc(F32, [128, NT, 4]); egl_c = P.alloc(F32, [128, NT, 4])
        tmp68 = P.alloc(F32, [128, NT, 4])
        qkv = P.alloc(BF16, [128, 12, T_ALL])
        zs = P.alloc(BF16, [128, 4, T_ALL])
        qks_f = P.alloc(F32, [128, 12, 16])
        histT = P.alloc(F32, [128, 12, 3, 16])
        ncp_st = P.alloc(F32, [128, 12, 3]); ncs_st = P.alloc(F32, [128, 12, 16])
        S_f = P.alloc(F32, [128, 4, 128]); S_b = P.alloc(BF16, [128, 4, 128])
        X0 = P.off
        XB = Bump(arena, X0, ARENA)
        hT = XB.alloc(BF16, [128, 8, T_ALL])
        Y0 = XB.off

        def wcol(j, c):
            return cols[:, 24 + j * 12 + c: 24 + j * 12 + c + 1]

        def gcol(which, m):
            return cols[:, which * 8 + m: which * 8 + m + 1]
        gdnn_col = cols[:, 72:73]
        negone = zcol[:, 1:2]
        mhalf = zcol[:, 2:3]
        inv128 = zcol[:, 3:4]

        S.op("pool", lambda e: e.memset(ones_f, 1.0), writes=["ones_f"])
        S.op("pool", lambda e: e.memset(ones_b, 1.0), writes=["ones_b"])
        S.op("pool", lambda e: e.memset(zcol, 0.0), writes=["zcol"])
        S.op("pool", lambda e: e.memset(zcol[:, 1:2], -1.0), reads=["zcol"], writes=["zcol"])
        S.op("pool", lambda e: e.memset(zcol[:, 2:3], -0.5), reads=["zcol"], writes=["zcol"])
        S.op("pool", lambda e: e.memset(zcol[:, 3:4], 1.0 / 128), reads=["zcol"], writes=["zcol"])
        S.op("pool", lambda e: e.memset(zeros_f, 0.0), writes=["zeros_f"])
        S.op("pool", lambda e: e.affine_select(out=ident_f, in_=ones_f, pattern=[[-1, 128]], compare_op=ALU.is_equal, fill=0.0, base=0, channel_multiplier=1), reads=["ones_f"], writes=["ident_f"])
        S.op("pool", lambda e: e.affine_select(out=mask_incl, in_=ones_f, pattern=[[1, 128]], compare_op=ALU.is_ge, fill=0.0, base=0, channel_multiplier=-1), reads=["ones_f"], writes=["mask_incl"])
        S.op("pool", lambda e: e.affine_select(out=mask_su, in_=ones_f, pattern=[[1, 128]], compare_op=ALU.is_gt, fill=0.0, base=0, channel_multiplier=-1), reads=["ones_f"], writes=["mask_su"])
        S.op("pool", lambda e: e.affine_select(out=nmask_sl, in_=ones_f, pattern=[[-1, 128]], compare_op=ALU.is_gt, fill=0.0, base=0, channel_multiplier=1), reads=["ones_f"], writes=["nmask_sl"])
        S.op("pool", ts(nmask_sl, nmask_sl, -1.0, ALU.mult), reads=["nmask_sl"], writes=["nmask_sl"])
        S.op("pool", ts(nmask_su, mask_su, -1.0, ALU.mult), reads=["mask_su"], writes=["nmask_su"])
        S.op("pool", ts(mask_sl, nmask_sl, -1.0, ALU.mult), reads=["nmask_sl"], writes=["mask_sl"])
        S.op("pool", lambda e: e.affine_select(out=sel127, in_=ones_f, pattern=[[0, 128]], compare_op=ALU.is_equal, fill=0.0, base=-127, channel_multiplier=1), reads=["ones_f"], writes=["sel127"])
        S.op("dve", cp(ident_b, ident_f), reads=["ident_f"], writes=["ident_b"])
        S.op("pool", lambda e: e.memset(rowstage, 0.0), writes=["rowstage"])
        S.dma("sp", "c0", [dmaf(rowstage[0:8, :], g_ff), dmaf(rowstage[8:16, :], g_ple), dmaf(rowstage[16:24, :], g_fin),
                           dmaf(rowstage[24:72, :], w_conv.rearrange("j (c p) -> (j c) p", p=128)), dmaf(rowstage[72:73, :], gdn_norm)],
              writes=["rowstage"])
        S.group("pe", [mm(ps[0][:, 0:128], rowstage, ident_f)], reads=["rowstage", "ident_f"], writes=[PK[0]])
        S.op("dve", cp(cols, ps[0][:, 0:128]), reads=[PK[0]], writes=["cols"])
        S.dma("sp", "c1", [dmaf(bs_row.rearrange("p h t -> p (h t)"), b_s.partition_broadcast(128)),
                           dmaf(lng_row, ln_g.partition_broadcast(128)), dmaf(lnb_row, ln_b.partition_broadcast(128)),
                           dmaf(alog_row, a_log.partition_broadcast(128)), dmaf(dtb_row, dt_bias.partition_broadcast(128)),
                           ] + [dmaf(ws00[:, h:h + 1], w_s[h, 0, 0:1].partition_broadcast(128)) for h in range(4)],
              writes=["rows"])
        S.op("act", actf(nexpA_row, alog_row, AF.Exp), reads=["rows"], writes=["nexpA"])
        S.op("dve", ts(nexpA_row, nexpA_row, -1.0, ALU.mult), reads=["nexpA"], writes=["nexpA"])
        for h in range(4):
            S.op("dve", ts(selws[0:16, h, :], ident_f[0:16, 0:16], ws00[0:16, h:h + 1], ALU.mult), reads=["rows", "ident_f"], writes=[("selws", h)])

        YA = Bump(arena, Y0, ARENA)
        wstmp = YA.alloc(F32, [128, 4, 128])
        S.dma("sp", "c2", dmaf(wstmp, w_s.rearrange("h t s -> t h s")), writes=["wstmp"])
        for h in range(4):
            S.op("pool", lambda e, h=h: e.affine_select(out=wstmp[:, h, :], in_=wstmp[:, h, :], pattern=[[-1, 128]], compare_op=ALU.is_ge, fill=0.0, base=0, channel_multiplier=1),
                 reads=["wstmp"], writes=["wstmp"])
        S.group("pe", [mm(ps[1][:, h * 128:(h + 1) * 128], wstmp[:, h, :], ident_f) for h in range(4)], reads=["wstmp", "ident_f"], writes=[PK[1]])
        S.op("dve", cp(wsT.rearrange("p h t -> p (h t)"), ps[1][:, 0:512]), reads=[PK[1]], writes=["wsT"])

        gmix_row = YA.alloc(F32, [128, 1024])
        S.dma("sp", "c3", dmaf(gmix_row, g_mix.partition_broadcast(128)), writes=["gmix"])
        xt = [YA.alloc(F32, [128, 1024]) for _ in range(3)]
        xsq = [YA.alloc(F32, [128, 1024]) for _ in range(3)]
        xn = [YA.alloc(BF16, [128, 1024]) for _ in range(3)]
        stat = YA.alloc(F32, [128, NT, 2])

        def phaseA_tile(i):
            ops = []
            Lop, Lgr, Ldma = mk_recorders(S, ops)
            r = 128 if i < 16 else 16
            sl = i % 3
            src = x_p[i * 128:(i + 1) * 128, :] if i < 16 else x_s
            Ldma("sp", "xt%d" % sl, dmaf(xt[sl][0:r, :], src), writes=[("xt", sl)])
            Lop("act", actf(xsq[sl][0:r, :], xt[sl][0:r, :], AF.Square), reads=[("xt", sl)], writes=[("xsq", sl)])
            Lop("dve", lambda e: e.reduce_sum(out=stat[0:r, i, 0:1], in_=xsq[sl][0:r, :], axis=AX.X), reads=[("xsq", sl)], writes=[("stat", i)])
            Lop("dve", ts(stat[0:r, i, 1:2], stat[0:r, i, 0:1], 1.0 / 1024, ALU.mult, EPS, ALU.add), reads=[("stat", i)], writes=[("stat", i)])
            Lop("act", actf(stat[0:r, i, 1:2], stat[0:r, i, 1:2], AF.Sqrt), reads=[("stat", i)], writes=[("stat", i)])
            Lop("dve", lambda e: e.reciprocal(out=stat[0:r, i, 1:2], in_=stat[0:r, i, 1:2]), reads=[("stat", i)], writes=[("stat", i)])
            Lop("dve", stt(xn[sl][0:r, :], xt[sl][0:r, :], stat[0:r, i, 1:2], gmix_row[0:r, :], ALU.mult, ALU.mult),
                reads=[("xt", sl), ("stat", i), "gmix"], writes=[("xn", sl)])
            b = i % 3
            Lgr("pe", [lambda e, k=k: e.transpose(out=psb[b][:, k * 128:k * 128 + r], in_=xn[sl][0:r, k * 128:(k + 1) * 128], identity=ident_b[0:r, 0:r]) for k in range(8)],
                reads=[("xn", sl), "ident_b"], writes=[PK[b]])
            if i % 2 == 0:
                Lop("act", actf(hT[:, :, i * 128:i * 128 + r], psb[b].rearrange("p (k t) -> p k t", k=8)[:, :, 0:r], AF.Copy), reads=[PK[b]], writes=[("hT", i)])
            else:
                Lop("dve", cp(hT[:, :, i * 128:i * 128 + r], psb[b].rearrange("p (k t) -> p k t", k=8)[:, :, 0:r]), reads=[PK[b]], writes=[("hT", i)])
            return ops

        tilesA = [phaseA_tile(i) for i in range(NT)]
        run_lanes(tilesA, 3, int(os.environ.get("STAG_A", "2")))
        S.barrier()

        YB = Bump(arena, Y0, ARENA)
        wb = [YB.alloc(BF16, [128, 8, 512]) for _ in range(2)]
        wb8 = YB.alloc(BF16, [128, 8, 8])
        lnst = YB.alloc(F32, [128, NT, 8])
        sct = [YB.alloc(F32, [128, 3, 128]) for _ in range(2)]
        YB_MID = YB.off
        vg = YB.alloc(BF16, [128, NT, 512])
        F6 = YB.alloc(F32, [128, 6, 512])
        f512 = [F6[:, i, :] for i in range(6)]
        vgs_f = f512[5]
        YB5 = Bump(arena, YB_MID, ARENA)
        NBS = 4
        pre = [YB5.alloc(F32, [128, 515]) for _ in range(NBS)]
        accb = [YB5.alloc(F32, [128, 512]) for _ in range(NBS)]
        rnb = [YB5.alloc(F32, [128, 512]) for _ in range(NBS)]
        sqb = [YB5.alloc(BF16, [128, 512]) for _ in range(NBS)]
        wdiag = [YB5.alloc(F32, [128, 4, 128]) for _ in range(2)]
        YB6 = Bump(arena, YB_MID, ARENA)
        stage_tok = YB6.alloc(F32, [128, 1536])
        stage2 = YB6.alloc(F32, [128, 1536])
        wb_n = [0]
        bank_n = [0]

        def next_bank(lo=0, hi=4):
            b = lo + bank_n[0] % (hi - lo)
            bank_n[0] += 1
            return b

        wb_seq = [C_VS, C_U, C_Z, 0, 512, 1024]
        wb_issued = [0]

        def load_w(view, c0, ncol=512):
            idx = wb_n[0]
            wb_n[0] += 1
            assert wb_seq[idx] == c0
            while wb_issued[0] < min(len(wb_seq), idx + 2):
                j = wb_issued[0]
                S.dma("pool", "wb%d" % (j % 2), dmaf(wb[j % 2][:, :, 0:512], view[:, :, wb_seq[j]:wb_seq[j] + 512]), writes=[("wb", j % 2)])
                wb_issued[0] += 1
            return idx % 2

        hT_keys = [("hT", i) for i in range(NT)]

        def seg_hT_keys(t0, n):
            return [("hT", i) for i in range(t0 // 128, (t0 + n + 127) // 128)]

        sl = load_w(w_in_v, C_VS)

        def vsgu_tile(i):
            ops = []
            Lop, Lgr, Ldma = mk_recorders(S, ops)
            r = 128 if i < 16 else 16
            b = (i % 3)
            Lgr("pe", [mm(ps[b][0:r, :], hT[:, k, i * 128:i * 128 + r], wb[sl][:, k, :], start=(k == 0), stop=(k == 7)) for k in range(8)],
                    reads=[("hT", i), ("wb", sl)], writes=[PK[b]])
            g1 = f512[i % 3]; g2 = f512[3 + i % 3]
            Lop("act", actf(g1[0:r, :], ps[b][0:r, :], AF.Gelu_apprx_tanh), reads=[PK[b]], writes=[("g1", i % 3)])
            Lop("pool", tt(g2[0:r, :], g1[0:r, :], g1[0:r, :], ALU.mult), reads=[("g1", i % 3)], writes=[("g2", i % 3)])
            Lop("dve", lambda e, i=i, r=r, g1=g1: e.reduce_sum(out=lnst[0:r, i, 0:1], in_=g1[0:r, :], axis=AX.X), reads=[("g1", i % 3)], writes=[("lnst", i)])
            Lop("dve", lambda e, i=i, r=r, g2=g2: e.reduce_sum(out=lnst[0:r, i, 1:2], in_=g2[0:r, :], axis=AX.X), reads=[("g2", i % 3)], writes=[("lnst", i)])
            L = lambda a, bb: lnst[0:r, i, a:bb]
            Lop("dve", ts(L(2, 3), L(0, 1), 1.0 / 512, ALU.mult), reads=[("lnst", i)], writes=[("lnst", i)])
            Lop("dve", tt(L(3, 4), L(2, 3), L(2, 3), ALU.mult), reads=[("lnst", i)], writes=[("lnst", i)])
            Lop("dve", stt(L(4, 5), L(1, 2), 1.0 / 512, L(3, 4), ALU.mult, ALU.subtract), reads=[("lnst", i)], writes=[("lnst", i)])
            Lop("act", actf(L(4, 5), L(4, 5), AF.Ln, bias=EPS), reads=[("lnst", i)], writes=[("lnst", i)])
            Lop("act", actf(L(5, 6), L(4, 5), AF.Exp, scale=mhalf[0:r, :]), reads=[("lnst", i)], writes=[("lnst", i)])
            Lop("dve", ts(g2[0:r, :], g1[0:r, :], L(2, 3), ALU.subtract, L(5, 6), ALU.mult), reads=[("g1", i % 3), ("lnst", i)], writes=[("g2", i % 3)])
            Lop("pool", tt(g2[0:r, :], g2[0:r, :], lng_row[0:r, :], ALU.mult), reads=[("g2", i % 3), "rows"], writes=[("g2", i % 3)])
            if i < 16:
                Lop("pool", tt(vg[0:r, i, :], g2[0:r, :], lnb_row[0:r, :], ALU.add), reads=[("g2", i % 3), "rows"], writes=[("vg", i)])
            else:
                Lop("pool", tt(vgs_f[0:r, :], g2[0:r, :], lnb_row[0:r, :], ALU.add), reads=[("g2", i % 3), "rows"], writes=[("g2", 2)])
                Lop("pool", cp(vg[0:r, i, :], vgs_f[0:r, :]), reads=[("g2", 2)], writes=[("vg", i)])
                Ldma("sp", "o_nsv", dmaf(nsv, vgs_f[0:r, :]), reads=[("g2", 2)])
            return ops

        tilesV = [vsgu_tile(i) for i in range(NT)]
        run_lanes(tilesV, 3, int(os.environ.get("STAG_V", "4")))

        S.dma("pool", "wb8", dmaf(wb8, w_in_v[:, :, C_BA:C_BA + 8]), writes=["wb8"])
        S.op("pool", lambda e: e.memset(ba, 0.0), writes=["ba"])
        bq = 4
        for i in range(NT):
            r = 128 if i < 16 else 16
            S.group("pe", [mm(ps[bq][0:r, i * 8:(i + 1) * 8], hT[:, k, i * 128:i * 128 + r], wb8[:, k, :], start=(k == 0), stop=(k == 7)) for k in range(8)],
                    reads=[("hT", i), "wb8"], writes=[PK[bq]])
        S.op("dve", cp(ba[:, 0:16, :], ps[bq][:, 0:128].rearrange("p (i c) -> p i c", c=8)), reads=[PK[bq], "ba"], writes=["ba"])
        S.op("dve", cp(ba[0:16, 16, :], ps[bq][0:16, 128:136]), reads=[PK[bq], "ba"], writes=["ba"])
        S.op("act", actf(beta_c, ba[:, :, 0:4], AF.Sigmoid), reads=["ba"], writes=["beta_c"])
        S.op("dve", tt(tmp68, ba[:, :, 4:8], dtb_row.unsqueeze(1).to_broadcast([128, NT, 4]), ALU.add), reads=["ba", "rows"], writes=["tmp68"])
        S.op("act", actf(tmp68, tmp68, AF.Exp), reads=["tmp68"], writes=["tmp68"])
        S.op("act", actf(tmp68, tmp68, AF.Ln, bias=1.0), reads=["tmp68"], writes=["tmp68"])
        S.op("dve", tt(g_c, tmp68, nexpA_row.unsqueeze(1).to_broadcast([128, NT, 4]), ALU.mult), reads=["tmp68", "nexpA"], writes=["g_c"])
        g68 = g_c.rearrange("p i h -> p (i h)"); gc68 = gc_c.rearrange("p i h -> p (i h)")
        S.group("pe", [mm(ps[5][:, 0:68], mask_incl, g68)], reads=["mask_incl", "g_c"], writes=[PK[5]])
        S.op("dve", cp(gc68, ps[5][:, 0:68]), reads=[PK[5]], writes=["gc_c"])
        S.group("pe", [mm(ps[5][:, 128:196], sel127, gc68)], reads=["sel127", "gc_c"], writes=[PK[5]])
        S.op("dve", cp(egl_c.rearrange("p i h -> p (i h)"), ps[5][:, 128:196]), reads=[PK[5]], writes=["egl_c"])
        S.op("dve", tt(tmp68.rearrange("p i h -> p (i h)"), egl_c.rearrange("p i h -> p (i h)"), gc68, ALU.subtract), reads=["egl_c", "gc_c"], writes=["tmp68"])
        S.op("act", actf(egl_c, egl_c, AF.Exp), reads=["egl_c", "tmp68"], writes=["egl_c"])
        S.op("act", actf(kd_c, tmp68, AF.Exp), reads=["tmp68"], writes=["kd_c"])
        S.op("act", actf(bexp_c, gc_c, AF.Exp), reads=["gc_c"], writes=["bexp_c"])
        S.op("dve", tt(bexp_c, bexp_c, beta_c, ALU.mult), reads=["bexp_c", "beta_c"], writes=["bexp_c"])

        sl = load_w(w_in_v, C_U)
        for h in range(4):
            for (t0, n) in SEGS:
                b = next_bank()
                S.group("pe", [mm(ps[b][:, 0:n], wb[sl][:, k, h * 128:(h + 1) * 128], hT[:, k, t0:t0 + n], start=(k == 0), stop=(k == 7)) for k in range(8)],
                        reads=seg_hT_keys(t0, n) + [("wb", sl)], writes=[PK[b]])
                u = f512[bank_n[0] % 2]
                S.op("act", actf(u[:, 0:n], ps[b][:, 0:n], AF.Gelu_apprx_tanh), reads=[PK[b]], writes=[("u", bank_n[0] % 2)])
                b2 = 4 + bank_n[0] % 2
                if n == 512:
                    tiles = [t0 // 128 + j for j in range(4)]
                    S.group("pe", [mm(ps[b2][:, j * 128:(j + 1) * 128], vg[:, tiles[j], h * 128:(h + 1) * 128], wsT[:, h, :]) for j in range(4)],
                            reads=[("vg", ti) for ti in tiles] + ["wsT"], writes=[PK[b2]])
                    S.op("dve", tt(f512[4][:, :].rearrange("p (j t) -> p j t", j=4), ps[b2][:, :].rearrange("p (j t) -> p j t", j=4),
                                   bs_row[:, h:h + 1, :].to_broadcast([128, 4, 128]), ALU.add), reads=[PK[b2], "rows"], writes=["mixt"])
                else:
                    S.group("pe", [mm(ps[b2][:, 0:16], vg[0:16, 16, h * 128:(h + 1) * 128], selws[0:16, h, :])],
                            reads=[("vg", 16), ("selws", h)], writes=[PK[b2]])
                    S.op("dve", tt(f512[4][:, 0:16], ps[b2][:, 0:16], bs_row[:, h, 0:1].to_broadcast([128, 16]), ALU.add), reads=[PK[b2], "rows"], writes=["mixt"])
                S.op("pool", tt(cat[:, 4 + h, t0:t0 + n], f512[4][:, 0:n], u[:, 0:n], ALU.mult), reads=["mixt", ("u", bank_n[0] % 2)], writes=[("cat", 4 + h, t0)])

        sl = load_w(w_in_v, C_Z)
        for h in range(4):
            for (t0, n) in SEGS:
                b = next_bank()
                S.group("pe", [mm(ps[b][:, 0:n], wb[sl][:, k, h * 128:(h + 1) * 128], hT[:, k, t0:t0 + n], start=(k == 0), stop=(k == 7)) for k in range(8)],
                        reads=seg_hT_keys(t0, n) + [("wb", sl)], writes=[PK[b]])
                S.op("act", actf(zs[:, h, t0:t0 + n], ps[b][:, 0:n], AF.Silu), reads=[PK[b]], writes=[("zs", h, t0)])

        S.barrier()

        def qkv_unit(blk, h, si, ui, wsl):
            ops = []
            Lop, Lgr, Ldma = mk_recorders(S, ops)
            c = blk * 4 + h
            t0, n = SEGS[si]
            bs = ui % NBS
            pr, acc, sq, rn = pre[bs], accb[bs], sqb[bs], rnb[bs]
            kp, ka, ks, kr = ("pre", bs), ("acc", bs), ("sq", bs), ("rn", bs)
            b = ui % 4; bn = 4 + ui % 4
            Lgr("pe", [mm(ps[b][:, 0:n], wb[wsl][:, k, h * 128:(h + 1) * 128], hT[:, k, t0:t0 + n], start=(k == 0), stop=(k == 7)) for k in range(8)],
                reads=[("wb", wsl)], writes=[PK[b]])
            if si == 0:
                Lop("dve", lambda e: e.memset(pr[:, 0:3], 0.0), writes=[kp])
            elif si < 4:
                Lgr("pe", [mm(ps[bn][:, 0:3], wb[wsl][:, k, h * 128:(h + 1) * 128], hT[:, k, t0 - 3:t0], start=(k == 0), stop=(k == 7)) for k in range(8)],
                    reads=[("wb", wsl)], writes=[PK[bn]])
                Lop("dve", cp(pr[:, 0:3], ps[bn][:, 0:3]), reads=[PK[bn]], writes=[kp])
            Lop("act", actf(pr[:, 3:3 + n], ps[b][:, 0:n], AF.Copy), reads=[PK[b], kp], writes=[kp])
            if si == 3:
                Lop("dve", cp(ncp_st[:, c, :], pr[:, 512:515]), reads=[kp], writes=[("ncp_st", c)])
            if si < 4 and os.environ.get("CONV", "dve") == "dve":
                Lop("act", actf(acc[:, 0:n], pr[:, 3:3 + n], AF.Copy, scale=wcol(3, c)), reads=[kp, "cols"], writes=[ka])
                Lop("dve", stt(acc[:, 0:n], pr[:, 2:2 + n], wcol(2, c), acc[:, 0:n], ALU.mult, ALU.add), reads=[kp, ka, "cols"], writes=[ka])
                Lop("dve", stt(acc[:, 0:n], pr[:, 1:1 + n], wcol(1, c), acc[:, 0:n], ALU.mult, ALU.add), reads=[kp, ka, "cols"], writes=[ka])
                Lop("dve", stt(acc[:, 0:n], pr[:, 0:n], wcol(0, c), acc[:, 0:n], ALU.mult, ALU.add), reads=[kp, ka, "cols"], writes=[ka])
                Lop("act", actf(acc[:, 0:n], acc[:, 0:n], AF.Silu), reads=[ka], writes=[ka])
            elif si < 4:
                wd = wdiag[c % 2]
                bc_ = 4 + ui % 4
                fns = []
                for t4 in range(4):
                    for j in range(4):
                        fns.append(mm(ps[bc_][:, t4 * 128:(t4 + 1) * 128], wd[:, j, :], pr[:, j + t4 * 128:j + (t4 + 1) * 128], start=(j == 0), stop=(j == 3)))
                Lgr("pe", fns, reads=[kp, ("wdiag", c % 2)], writes=[PK[bc_]])
                Lop("act", actf(acc[:, 0:n], ps[bc_][:, 0:n], AF.Silu), reads=[PK[bc_]], writes=[ka])
            else:
                Lop("dve", cp(ncs_st[:, c, :], pr[:, 3:19]), reads=[kp], writes=[("ncs_st", c)])
                Lop("act", actf(acc[:, 0:n], pr[:, 3:3 + n], AF.Copy, scale=wcol(3, c)), reads=[kp, "cols"], writes=[ka])
                for j in (2, 1, 0):
                    Lop("dve", stt(acc[:, 0:n], histT[:, c, j, :], wcol(j, c), acc[:, 0:n], ALU.mult, ALU.add), reads=[("histT", c), ka, "cols"], writes=[ka])
                Lop("act", actf(acc[:, 0:n], acc[:, 0:n], AF.Silu), reads=[ka], writes=[ka])
            if blk == 2:
                Lop("pool", cp(qkv[:, c, t0:t0 + n], acc[:, 0:n]), reads=[ka], writes=[("qkv", c, t0)])
                if si == 4:
                    Lop("pool", cp(qks_f[:, c, :], acc[:, 0:16]), reads=[ka], writes=[("qks_f", c)])
            else:
                Lop("pool", tt(sq[:, 0:n], acc[:, 0:n], acc[:, 0:n], ALU.mult), reads=[ka], writes=[ks])
                Lgr("pe", [mm(ps[bn][:, 0:n], ones_b, sq[:, 0:n])], reads=[ks, "ones_b"], writes=[PK[bn]])
                Lop("act", actf(rn[:, 0:n], ps[bn][:, 0:n], AF.Ln, bias=1e-6), reads=[PK[bn]], writes=[kr])
                Lop("act", actf(rn[:, 0:n], rn[:, 0:n], AF.Exp, scale=mhalf), reads=[kr], writes=[kr])
                scl = (128.0 ** -0.5) if blk == 0 else 1.0
                Lop("dve", stt(qkv[:, c, t0:t0 + n], acc[:, 0:n], scl, rn[:, 0:n], ALU.mult, ALU.mult), reads=[ka, kr], writes=[("qkv", c, t0)])
                if si == 4:
                    Lop("dve", stt(qks_f[:, c, :], acc[:, 0:16], scl, rn[:, 0:16], ALU.mult, ALU.mult), reads=[ka, kr], writes=[("qks_f", c)])
            return ops

        ui = 0
        units = []
        for blk in range(3):
            wsl = (3 + blk) % 2
            for h in range(4):
                c = blk * 4 + h
                for si in range(5):
                    uo = qkv_unit(blk, h, si, ui, wsl)
                    pre_ops = []
                    if h == 0 and si == 0:
                        pre_ops.append(lambda blk=blk: load_w(w_in_v, blk * 512))
                    if si == 0:
                        scs = sct[c % 2]
                        pre_ops.append(lambda c=c, scs=scs: S.dma("sp", "sct%d" % (c % 2), dmaf(scs[0:16, :, :], st_conv[:, :, c * 128:(c + 1) * 128]), writes=[("sct", c % 2)]))
                        pre_ops.append(lambda c=c, scs=scs: S.group("pe", [mm(ps[4 + c % 4][:, j * 16:(j + 1) * 16], scs[0:16, j, :], ident_f[0:16, 0:16]) for j in range(3)],
                                                                   reads=[("sct", c % 2), "ident_f"], writes=[PK[4 + c % 4]]))
                        pre_ops.append(lambda c=c: S.op("dve", cp(histT[:, c, :, :], ps[4 + c % 4][:, 0:48].rearrange("p (j b) -> p j b", j=3)), reads=[PK[4 + c % 4]], writes=[("histT", c)]))
                    units.append(pre_ops + uo)
                    ui += 1
        run_lanes(units, 4, int(os.environ.get("STAG_Q", "2")))

        S.barrier()
        XS = Bump(arena, X0, ARENA)
        S_all = XS.alloc(F32, [128, 64, 128])
        if 'sample' not in os.environ.get('KSKIP', ''):
            for q4 in range(4):
                S.dma("sp", "sall%d" % q4, dmaf(S_all[:, q4 * 16:(q4 + 1) * 16, :], st_gdn[q4 * 4:(q4 + 1) * 4].rearrange("b h d e -> d (b h) e")), writes=[("S_all", q4)])
        S.group("pe", [mm(ps[c // 4][0:3, (c % 4) * 128:(c % 4 + 1) * 128], ncp_st[:, c, :], ident_f) for c in range(12)],
                reads=[("ncp_st", c) for c in range(12)] + ["ident_f"], writes=[PK[0], PK[1], PK[2]])
        for q3 in range(3):
            S.op("dve", cp(stage_tok[0:3, q3 * 512:(q3 + 1) * 512], ps[q3][0:3, :]), reads=[PK[q3]], writes=["stage_tok"])
        S.dma("sp", "o_ncp", dmaf(ncp, stage_tok[0:3, :]), reads=["stage_tok"])
        S.group("pe", [mm(ps[c // 4][0:16, (c % 4) * 128:(c % 4 + 1) * 128], ncs_st[:, c, :], ident_f) for c in range(12)],
                reads=[("ncs_st", c) for c in range(12)] + ["ident_f"], writes=[PK[0], PK[1], PK[2]])
        for q3 in range(3):
            S.op("act", actf(stage2[0:16, q3 * 512:(q3 + 1) * 512], ps[q3][0:16, :], AF.Copy), reads=[PK[q3]], writes=["stage2"])
        S.dma("sp", "o_ncs", [dmaf(ncs[:, 2, :], stage2[0:16, :]), dmaf(ncs[:, 0:2, :], st_conv[:, 1:3, :])], reads=["stage2"])
        S.barrier()

        YG = Bump(arena, Y0, ARENA)
        osq = YG.alloc(BF16, [128, 512]); rn_o = YG.alloc(F32, [128, 512]); on_o = YG.alloc(F32, [128, 512])
        YG_EPI = YG.off
        NPW, DP = 4, 8
        CHDT = F32
        gN = lambda n_, dt_, shp: [YG.alloc(dt_, shp) for _ in range(n_)]
        Rs = gN(NPW, F32, [128, 256]); rhsR = gN(NPW, F32, [128, 256]); D0 = gN(NPW, F32, [128, 128]); E0 = gN(NPW, F32, [128, 128])
        EGr = gN(NPW, F32, [128, 128]); MB = gN(NPW, F32, [128, 128]); Qf = gN(NPW, F32, [128, 128]); Qs = gN(NPW, BF16, [128, 128])
        NNa = gN(NPW, CHDT, [128, 256]); NNb = gN(NPW, CHDT, [128, 256]); Xs = gN(NPW, BF16, [128, 128])
        NHa = gN(NPW, BF16, [128, 256]); NHb = gN(NPW, BF16, [128, 256])
        J0 = int(os.environ.get('GDN_J0', '6'))
        Qm = gN(DP, BF16, [128, 128]); attnT = gN(DP, BF16, [128, 128]); Kd = gN(DP, BF16, [128, 128]); Vb = gN(DP, BF16, [128, 128])
        qg = gN(DP, BF16, [128, 128]); nWT = gN(DP, BF16, [128, 128]); vn = gN(4, BF16, [128, 128])
        S.op("pool", lambda e: e.memset(S_f.rearrange("p h e -> p (h e)"), 0.0), writes=[("S_f", h) for h in range(4)])
        S.op("pool", lambda e: e.memset(S_b.rearrange("p h e -> p (h e)"), 0.0), writes=[("S_b", h) for h in range(4)])

        def gdn_P(n, h):
            ops = []
            Lop, Lgr, Ldma = mk_recorders(S, ops)
            u = n * 4 + h
            q = u % NPW; s = u % DP
            tok = slice(n * 128, (n + 1) * 128)
            kT = qkv[:, 4 + h, tok]; qT = qkv[:, h, tok]; vT = qkv[:, 8 + h, tok]
            col = lambda t: t[:, n, h:h + 1]
            K = lambda name: (name, q)
            H = lambda name: (name, s)
            bk = PK[q]; pb = ps[q]; pbb = psb[q]
            kk = pb[:, 256:384]; qk = pb[:, 384:512]
            Lop("pool", stt_pool(rhsR[q][:, 0:128], mask_incl, col(g_c)), reads=["mask_incl", "g_c"], writes=[K("rhsR")])
            Lop("pool", stt_pool(rhsR[q][:, 128:256], ident_f, col(beta_c)), reads=["ident_f", "beta_c"], writes=[K("rhsR")])
            Lgr("pe", [lambda e: e.transpose(out=pbb[:, 0:128], in_=kT, identity=ident_b),
                       lambda e: e.transpose(out=pbb[:, 128:256], in_=vT, identity=ident_b)], reads=[("qkv", n), "ident_b"], writes=[bk])
            ktok = pbb[:, 0:128]; vtok = pbb[:, 128:256]
            Lop("act", actf(Xs[q], ktok, AF.Copy, scale=col(bexp_c)), reads=[bk, "bexp_c"], writes=[K("Xs")])
            Lop("act", actf(Kd[s], ktok, AF.Copy, scale=col(kd_c)), reads=[bk, "kd_c"], writes=[H("Kd")])
            Lop("act", actf(Vb[s], vtok, AF.Copy, scale=col(beta_c)), reads=[bk, "beta_c"], writes=[H("Vb")])
            Lgr("pe", [mm(pb[:, 0:128], ones_f, rhsR[q][:, 0:128]), mm(pb[:, 128:256], ones_f, rhsR[q][:, 128:256]),
                       mm(kk, kT, kT), mm(qk, kT, qT)], reads=[K("rhsR"), "ones_f", ("qkv", n)], writes=[bk])
            Lop("dve", cp(Rs[q], pb[:, 0:256]), reads=[bk], writes=[K("Rs")])
            R_gc = Rs[q][:, 0:128]; R_be = Rs[q][:, 128:256]
            Lop("pool", lambda e: e.tensor_tensor(out=D0[q], in0=R_gc, in1=col(gc_c).to_broadcast([128, 128]), op=ALU.subtract), reads=[K("Rs"), "gc_c"], writes=[K("D0")])
            Lop("pool", ts(D0[q], D0[q], 0.0, ALU.min), reads=[K("D0")], writes=[K("D0")])
            Lop("act", actf(D0[q], D0[q], AF.Exp), reads=[K("D0")], writes=[K("D0")])
            Lop("act", actf(EGr[q], R_gc, AF.Exp), reads=[K("Rs")], writes=[K("EGr")])
            Lop("pool", tt(MB[q], R_be, D0[q], ALU.mult), reads=[K("Rs"), K("D0")], writes=[K("MB")])
            Lop("pool", tt(MB[q], MB[q], nmask_su, ALU.mult), reads=[K("MB"), "nmask_su"], writes=[K("MB")])
            Lop("pool", tt(D0[q], D0[q], mask_incl, ALU.mult), reads=[K("D0"), K("MB"), "mask_incl"], writes=[K("D0")])
            Lop("pool", tt(qg[s], qT, EGr[q], ALU.mult), reads=[("qkv", n), K("EGr")], writes=[H("qg")])
            Lop("dve", tt(NNa[q][:, 0:128], kk, MB[q], ALU.mult), reads=[bk, K("MB")], writes=[K("NNa")])
            Lop("dve", tt(attnT[s], qk, D0[q], ALU.mult), reads=[bk, K("D0")], writes=[H("attnT")])
            Lgr("pe", [mm(pb[:, 0:128], NNa[q][:, 0:128], ident_f)], reads=[K("NNa"), "ident_f"], writes=[bk])
            Lop("act", actf(NNa[q][:, 128:256], pb[:, 0:128], AF.Copy), reads=[bk], writes=[K("NNa")])
            Lop("pool", tt(Qf[q], ident_f, NNa[q][:, 0:128], ALU.add), reads=["ident_f", K("NNa")], writes=[K("Qf")])
            cur, nxt, kc, kn = NNa[q], NNb[q], K("NNa"), K("NNb")
            cur16, nxt16, kc16, kn16 = NHa[q], NHb[q], K("NHa"), K("NHb")
            if J0 == 0:
                Lop("pool", cp(cur16, cur), reads=[kc], writes=[kc16])
            for j in range(1, 7):
                f32lvl = j <= J0
                src, ksrc = (cur, kc) if f32lvl else (cur16, kc16)
                fns = []
                if j < 6:
                    fns.append(mm(pb[:, 0:128], src[:, 128:256], src[:, 0:128]))
                fns.append(mm(pb[:, 128:256], src[:, 0:128], src[:, 128:256]))
                Lgr("pe", fns, reads=[ksrc], writes=[bk])
                lo = 0 if j < 6 else 128
                if f32lvl:
                    Lop("act", actf(nxt[:, lo:256], pb[:, lo:256], AF.Copy), reads=[bk], writes=[kn])
                    if j == J0 and j < 6:
                        Lop("pool", cp(nxt16[:, lo:256], nxt[:, lo:256]), reads=[kn], writes=[kn16])
                    Lgr("pe", [mm(pb[:, 256:384], nxt[:, 128:256], Qf[q])], reads=[kn, K("Qf")], writes=[bk])
                else:
                    Lop("act", actf(nxt16[:, lo:256], pb[:, lo:256], AF.Copy), reads=[bk], writes=[kn16])
                    Lop("pool", cp(Qs[q], Qf[q]), reads=[K("Qf")], writes=[K("Qs")])
                    Lgr("pe", [mm(pb[:, 256:384], nxt16[:, 128:256], Qs[q])], reads=[kn16, K("Qs")], writes=[bk])
                Lop("dve", tt(Qf[q], Qf[q], pb[:, 256:384], ALU.add), reads=[bk, K("Qf")], writes=[K("Qf")])
                cur, nxt, kc, kn = nxt, cur, kn, kc
                cur16, nxt16, kc16, kn16 = nxt16, cur16, kn16, kc16
            Lop("pool", cp(Qm[s], Qf[q]), reads=[K("Qf")], writes=[H("Qm")])
            Lgr("pe", [mm(pb[:, 384:512], Xs[q], Qm[s])], reads=[K("Xs"), H("Qm")], writes=[bk])
            Lop("act", actf(nWT[s], pb[:, 384:512], AF.Copy, scale=negone), reads=[bk], writes=[H("nWT")])
            return ops

        def gdn_R(n, h):
            ops = []
            Lop, Lgr, Ldma = mk_recorders(S, ops)
            u = n * 4 + h
            s = u % DP
            H = lambda name: (name, s)
            col = lambda t: t[:, n, h:h + 1]
            bR = PK[4]; ob = 5 + n % 2
            V = ps[4][:, h * 128:(h + 1) * 128]
            Lgr("pe", [mm(V, Qm[s], Vb[s], start=True, stop=False),
                       mm(V, nWT[s], S_b[:, h, :], start=False, stop=True)],
                reads=[H("Qm"), H("Vb"), H("nWT"), ("S_b", h)], writes=[bR])
            Lop("dve", cp(vn[h], V), reads=[bR], writes=[("vn", h)])
            Lgr("pe", [mm(V, Kd[s], vn[h])], reads=[H("Kd"), ("vn", h)], writes=[bR])
            Lgr("pe", [mm(ps[ob][:, h * 128:(h + 1) * 128], S_b[:, h, :], qg[s], start=True, stop=False),
                       mm(ps[ob][:, h * 128:(h + 1) * 128], vn[h], attnT[s], start=False, stop=True)],
                reads=[("S_b", h), H("qg"), ("vn", h), H("attnT")], writes=[PK[ob]])
            Lop("dve", stt(S_f[:, h, :], S_f[:, h, :], col(egl_c), V, ALU.mult, ALU.add), reads=[bR, ("S_f", h), "egl_c"], writes=[("S_f", h)])
            Lop("act", actf(S_b[:, h, :], S_f[:, h, :], AF.Copy), reads=[("S_f", h)], writes=[("S_b", h)])
            return ops

        def gdn_epilogue(o_ps, ss_ps, ncol, t0, okeys, sskey):
            w = 4 * ncol
            S.op("act", actf(osq[:, 0:w], o_ps, AF.Square), reads=okeys, writes=["osq"])
            S.group("pe", [mm(ss_ps, ones_b, osq[:, 0:w])], reads=["osq", "ones_b"], writes=[sskey])
            S.op("dve", ts(rn_o[:, 0:w], ss_ps, 1.0 / 128, ALU.mult, EPS, ALU.add), reads=[sskey], writes=["rn_o"])
            S.op("act", actf(rn_o[:, 0:w], rn_o[:, 0:w], AF.Sqrt), reads=["rn_o"], writes=["rn_o"])
            S.op("dve", lambda e: e.reciprocal(out=rn_o[:, 0:w], in_=rn_o[:, 0:w]), reads=["rn_o"], writes=["rn_o"])
            S.op("dve", stt(on_o[:, 0:w], o_ps, gdnn_col, rn_o[:, 0:w], ALU.mult, ALU.mult), reads=okeys + ["rn_o", "cols"], writes=["on_o"])
            S.op("pool", tt(cat[:, 0:4, t0:t0 + ncol], on_o[:, 0:w].rearrange("p (h t) -> p h t", h=4), zs[:, :, t0:t0 + ncol], ALU.mult),
                 reads=["on_o", "zs"], writes=[("cat_o", t0)])

        _SK = os.environ.get('KSKIP', '')
        NCH = 0 if 'prompt' in _SK else int(os.environ.get('GDN_N', '16'))

        def epi_ops(n):
            ob = 5 + n % 2
            return [lambda: gdn_epilogue(ps[ob][:, :], ps[7][:, :], 128, n * 128, [PK[ob]], PK[7])]

        _DO_SAMPLE = 'sample' not in _SK
        def _sample_section():
            YG = Bump(arena, YG_EPI, ARENA)
            sv = YG.alloc(F32, [128, 8])
            rexp = YG.alloc(F32, [128, 8, 16])
            bcs = YG.alloc(F32, [128, 128])
            dcol = YG.alloc(F32, [128, 64])
            dtok = YG.alloc(F32, [128, 512]); ktoks = YG.alloc(F32, [128, 512])
            kmask = [YG.alloc(F32, [128, 512]) for _ in range(2)]
            S.op("dve", cp(sv[0:16, 0:4], beta_c[0:16, 16, :]), reads=["beta_c"], writes=["sv"])
            S.op("act", actf(sv[0:16, 4:8], g_c[0:16, 16, :], AF.Exp), reads=["g_c"], writes=["sv"])
            for j in range(8):
                S.op("dve", ts(rexp[0:16, j, :], ident_f[0:16, 0:16], sv[0:16, j:j + 1], ALU.mult), reads=["sv", "ident_f"], writes=["rexp"])
            S.group("pe", [mm(ps[0][:, 0:128], ones_f[0:16, :], rexp[0:16, :, :].rearrange("p j b -> p (j b)"))], reads=["rexp", "ones_f"], writes=[PK[0]])
            S.op("dve", cp(bcs, ps[0][:, 0:128]), reads=[PK[0]], writes=["bcs"])
            beta_bc = bcs[:, 0:64]; eg_bc = bcs[:, 64:128]
            S.group("pe", [mm(ps[1][:, h * 16 + b:h * 16 + b + 1], S_all[:, b * 4 + h, :], qks_f[:, 4 + h, b:b + 1]) for b in range(16) for h in range(4)],
                    reads=[("S_all", q4) for q4 in range(4)] + ["qks_f"], writes=[PK[1]])
            S.op("dve", tt(dcol, ps[1][:, 0:64], eg_bc, ALU.mult), reads=[PK[1], "bcs"], writes=["dcol"])
            S.op("dve", tt(dcol, qks_f[:, 8:12, :].rearrange("p h b -> p (h b)"), dcol, ALU.subtract), reads=["dcol", "qks_f"], writes=["dcol"])
            S.op("dve", tt(dcol, dcol, beta_bc, ALU.mult), reads=["dcol", "bcs"], writes=["dcol"])
            S.group("pe", [mm(ps[2][0:16, h * 128:(h + 1) * 128], dcol[:, h * 16:(h + 1) * 16], ident_f) for h in range(4)], reads=["dcol", "ident_f"], writes=[PK[2]])
            S.group("pe", [mm(ps[3][0:16, h * 128:(h + 1) * 128], qks_f[:, 4 + h, :], ident_f) for h in range(4)], reads=["qks_f", "ident_f"], writes=[PK[3]])
            S.op("dve", cp(dtok[0:16, :], ps[2][0:16, :]), reads=[PK[2]], writes=["dtok"])
            S.op("act", actf(ktoks[0:16, :], ps[3][0:16, :], AF.Copy), reads=[PK[3]], writes=["ktoks"])
            for b in range(16):
                km = kmask[b % 2]; pb = 4 + b % 2
                S.op("dve", ts(km[0:16, :], ktoks[0:16, :], ident_f[0:16, b:b + 1], ALU.mult), reads=["ktoks", "ident_f"], writes=[("kmask", b % 2)])
                S.group("pe", [mm(ps[pb][:, h * 128:(h + 1) * 128], km[0:16, h * 128:(h + 1) * 128], dtok[0:16, h * 128:(h + 1) * 128]) for h in range(4)],
                        reads=[("kmask", b % 2), "dtok"], writes=[PK[pb]])
                for h in range(4):
                    S.op("dve", stt(S_all[:, b * 4 + h, :], S_all[:, b * 4 + h, :], eg_bc[:, h * 16 + b:h * 16 + b + 1], ps[pb][:, h * 128:(h + 1) * 128], ALU.mult, ALU.add),
                         reads=[PK[pb], "bcs", ("S_all", b // 4)], writes=[("S_all", b // 4)])
            S.group("pe", [mm(ps[1][:, 64 + h * 16 + b:64 + h * 16 + b + 1], S_all[:, b * 4 + h, :], qks_f[:, h, b:b + 1]) for b in range(16) for h in range(4)],
                    reads=[("S_all", q4) for q4 in range(4)] + ["qks_f"], writes=[PK[1]])
            gdn_epilogue(ps[1][:, 64:128], ps[0][:, 128:192], 16, T_P, [PK[1]], PK[0])
            for q4 in range(4):
                S.dma("sp", "o_ngs%d" % q4, dmaf(ngs[q4 * 4:(q4 + 1) * 4].rearrange("b h d e -> d (b h) e"), S_all[:, q4 * 16:(q4 + 1) * 16, :]), reads=[("S_all", q4)])
        if _DO_SAMPLE:
            _sample_section()
        S.barrier()

        LB = Bump(arena, X0, ARENA)
        NL = 8
        lf32 = lambda shp: [LB.alloc(F32, shp) for _ in range(NL)]
        lbf = lambda shp: [LB.alloc(BF16, shp) for _ in range(NL)]
        Rs8 = lf32([128, 256]); rhsR8 = lf32([128, 256]); D08 = lf32([128, 128]); EGr8 = lf32([128, 128]); MB8 = lf32([128, 128]); Qf8 = lf32([128, 128])
        NNa8 = lf32([128, 256]); NNb8 = lf32([128, 256]); rn8 = lf32([128, 128]); on8 = lf32([128, 128])
        Xs8 = lbf([128, 128]); Kd8 = lbf([128, 128]); Vb8 = lbf([128, 128]); qg8 = lbf([128, 128]); at8 = lbf([128, 128])
        Qm8 = lbf([128, 128]); nWT8 = lbf([128, 128]); vn8 = lbf([128, 128]); osq8 = lbf([128, 128])
        r_done = {}

        def gdn_unit(n, h):
            ops = []
            Lop, Lgr, Ldma = mk_recorders(S, ops)
            L = h * 2 + n % 2
            tok = slice(n * 128, (n + 1) * 128)
            kT = qkv[:, 4 + h, tok]; qT = qkv[:, h, tok]; vT = qkv[:, 8 + h, tok]
            col = lambda t: t[:, n, h:h + 1]
            K = lambda name: (name, L)
            bk = PK[L]; pb = ps[L]; pbb = psb[L]
            kk = pb[:, 256:384]; qk = pb[:, 384:512]
            Rs, rhsR, D0, EGr, MB, Qf = Rs8[L], rhsR8[L], D08[L], EGr8[L], MB8[L], Qf8[L]
            Xs, Kd, Vb, qg, attnT, Qm, nWT, vn, osq = Xs8[L], Kd8[L], Vb8[L], qg8[L], at8[L], Qm8[L], nWT8[L], vn8[L], osq8[L]
            Lop("pool", stt_pool(rhsR[:, 0:128], mask_incl, col(g_c)), reads=["mask_incl", "g_c"], writes=[K("rhsR")])
            Lop("pool", stt_pool(rhsR[:, 128:256], ident_f, col(beta_c)), reads=["ident_f", "beta_c"], writes=[K("rhsR")])
            Lgr("pe", [mm(pb[:, 0:128], mask_sl, rhsR[:, 0:128]), mm(pb[:, 128:256], ones_f, rhsR[:, 0:128]),
                       mm(pb[:, 256:384], nmask_sl, rhsR[:, 128:256]),
                       lambda e: e.transpose(out=pbb[:, 768:896], in_=kT, identity=ident_b),
                       lambda e: e.transpose(out=pbb[:, 896:1024], in_=vT, identity=ident_b)],
                reads=[K("rhsR"), "ones_f", "mask_sl", "nmask_sl", "ident_b"], writes=[bk])
            ktok = pbb[:, 768:896]; vtok = pbb[:, 896:1024]
            Lop("act", actf(Rs, pb[:, 0:256], AF.Exp), reads=[bk], writes=[K("Rs")])
            D0 = Rs[:, 0:128]; EGr = Rs[:, 128:256]
            Lop("act", actf(Xs, ktok, AF.Copy, scale=col(bexp_c)), reads=[bk, "bexp_c"], writes=[K("Xs")])
            Lop("act", actf(Kd, ktok, AF.Copy, scale=col(kd_c)), reads=[bk, "kd_c"], writes=[K("Kd")])
            Lop("act", actf(Vb, vtok, AF.Copy, scale=col(beta_c)), reads=[bk, "beta_c"], writes=[K("Vb")])
            Lop("act", actf(MB, pb[:, 256:384], AF.Copy), reads=[bk], writes=[K("MB")])
            Lop("pool", tt(MB, MB, D0, ALU.mult), reads=[K("MB"), K("Rs")], writes=[K("MB")])
            Lop("pool", tt(qg, qT, EGr, ALU.mult), reads=[K("Rs")], writes=[K("qg")])
            Lgr("pe", [mm(pb[:, 0:128], kT, kT), mm(pb[:, 128:256], kT, qT)], reads=[], writes=[bk])
            kk = pb[:, 0:128]; qk = pb[:, 128:256]
            Lop("pool", tt(D0, D0, mask_incl, ALU.mult), reads=[K("Rs"), K("MB"), K("qg"), "mask_incl"], writes=[K("Rs")])
            NNa, NNb = NNa8[L], NNb8[L]
            Lop("dve", tt(NNa[:, 0:128], kk, MB, ALU.mult), reads=[bk, K("MB")], writes=[K("NNa")])
            Lop("dve", tt(attnT, qk, D0, ALU.mult), reads=[bk, K("Rs")], writes=[K("attnT")])
            Lgr("pe", [mm(pb[:, 256:384], NNa[:, 0:128], ident_f)], reads=[K("NNa"), "ident_f"], writes=[bk])
            Lop("act", actf(NNa[:, 128:256], pb[:, 256:384], AF.Copy), reads=[bk], writes=[K("NNa")])
            Lop("pool", tt(Qf, ident_f, NNa[:, 0:128], ALU.add), reads=["ident_f", K("NNa")], writes=[K("Qf")])
            cur, nxt, kc, kn = NNa, NNb, K("NNa"), K("NNb")
            for j in range(1, 7):
                fns = []
                if j < 6:
                    fns.append(mm(pb[:, 0:128], cur[:, 128:256], cur[:, 0:128]))
                fns.append(mm(pb[:, 128:256], cur[:, 0:128], cur[:, 128:256]))
                Lgr("pe", fns, reads=[kc], writes=[bk])
                lo = 0 if j < 6 else 128
                if j in (3, 5):
                    Lop("dve", cp(nxt[:, lo:256], pb[:, lo:256]), reads=[bk], writes=[kn])
                else:
                    Lop("act", actf(nxt[:, lo:256], pb[:, lo:256], AF.Copy), reads=[bk], writes=[kn])
                Lgr("pe", [mm(pb[:, 256:384], nxt[:, 128:256], Qf)], reads=[kn, K("Qf")], writes=[bk])
                Lop("dve", tt(Qf, Qf, pb[:, 256:384], ALU.add), reads=[bk, K("Qf")], writes=[K("Qf")])
                cur, nxt, kc, kn = nxt, cur, kn, kc
            Lop("pool", cp(Qm, Qf), reads=[K("Qf")], writes=[K("Qm")])
            Lgr("pe", [mm(pb[:, 384:512], Xs, Qm)], reads=[K("Xs"), K("Qm")], writes=[bk])
            Lop("act", actf(nWT, pb[:, 384:512], AF.Copy, scale=negone), reads=[bk], writes=[K("nWT")])
            V = pb[:, 384:512]; Oh = pb[:, 0:128]; SSh = pb[:, 128:256]

            def chk():
                assert n == 0 or r_done.get((n - 1, h)), ("emission order violated", n, h)
            ops.append(chk)
            Lgr("pe", [mm(V, Qm, Vb, start=True, stop=False), mm(V, nWT, S_b[:, h, :], start=False, stop=True)],
                reads=[K("Qm"), K("Vb"), K("nWT"), ("S_b", h)], writes=[bk])
            Lop("dve", cp(vn, V), reads=[bk], writes=[K("vn")])
            Lgr("pe", [mm(V, Kd, vn),
                       mm(Oh, S_b[:, h, :], qg, start=True, stop=False), mm(Oh, vn, attnT, start=False, stop=True)],
                reads=[K("Kd"), K("vn"), ("S_b", h), K("qg"), K("attnT")], writes=[bk])
            Lop("dve", stt(S_f[:, h, :], S_f[:, h, :], col(egl_c), V, ALU.mult, ALU.add), reads=[bk, ("S_f", h), "egl_c"], writes=[("S_f", h)])
            Lop("pool", cp(S_b[:, h, :], S_f[:, h, :]), reads=[("S_f", h)], writes=[("S_b", h)])

            def mark():
                r_done[(n, h)] = True
            ops.append(mark)
            rn, on = rn8[L], on8[L]
            Lop("act", actf(osq, Oh, AF.Square), reads=[bk], writes=[K("osq")])
            Lgr("pe", [mm(SSh, ones_b, osq)], reads=[K("osq"), "ones_b"], writes=[bk])
            Lop("act", actf(rn, SSh, AF.Ln, scale=inv128, bias=EPS), reads=[bk], writes=[K("rn")])
            Lop("act", actf(rn, rn, AF.Exp, scale=mhalf), reads=[K("rn")], writes=[K("rn")])
            Lop("dve", stt(on, Oh, gdnn_col, rn, ALU.mult, ALU.mult), reads=[bk, K("rn"), "cols"], writes=[K("on")])
            Lop("pool", tt(cat[:, h, tok], on, zs[:, h, tok], ALU.mult), reads=[K("on")], writes=[("cat_o", n, h)])
            return ops

        if NCH:
            u0 = gdn_unit(0, 0)
            LU = len(u0)
            STAG8 = int(os.environ.get('GDN_STAG', '22'))
            lanes = []
            for h in range(4):
                for par in range(2):
                    pad = h * STAG8 + par * (LU // 2)
                    lane = [(lambda: None)] * pad
                    for n in range(par, NCH, 2):
                        lane = lane + gdn_unit(n, h)
                    lanes.append(lane)
            zipper(lanes)
        S.dma("sp", "o_ngp", dmaf(ngp.rearrange("h d e -> d h e"), S_f), reads=[("S_f", h) for h in range(4)])

        S.barrier()

        if 'phasec' in _SK:
            S.finish()
            with nc.Block() as block:
                S.replay(block)
            return nc
        YC = Bump(arena, P_C0, ARENA)
        R = YC.alloc(F32, [128, 8, 528]); xnC = YC.alloc(BF16, [128, 8, 528]); hid = YC.alloc(BF16, [128, 32, 528])
        r8 = [YC.alloc(BF16, [128, 8, 512]) for _ in range(3)]
        r16 = [YC.alloc(BF16, [128, 32, 256]) for _ in range(2)]
        xres = YC.alloc(F32, [128, 4, 1024]); xres_s = YC.alloc(F32, [128, 1024])
        pw = YC.alloc(BF16, [128, 2, 1024]); ptok = [YC.alloc(BF16, [128, 256]) for _ in range(2)]
        pT = YC.alloc(BF16, [128, 2, 528]); sqr = [YC.alloc(BF16, [128, 528]) for _ in range(2)]; rnC = YC.alloc(F32, [128, 528])
        sig = [YC.alloc(F32, [128, 528]) for _ in range(2)]; relu_t = [YC.alloc(F32, [128, 528]) for _ in range(2)]
        ytile = [YC.alloc(F32, [128, 1024]) for _ in range(1)]
        rncol = YC.alloc(F32, [128, 8])
        RNCOL_INIT = [False]
        r8_n = [0]; r16_n = [0]; misc_n = [0]

        r8_seq = []
        for _p in range(4):
            r8_seq += [(w_out_v, 0), (w_out_v, 512)] + [(w_up_v, bb * 512) for bb in range(8)] + [(w_gate_v, 0), (w_gate_v, 512)]
        r8_issued = [0]

        def load_r8(view, c0):
            idx = r8_n[0]
            r8_n[0] += 1
            assert r8_seq[idx][1] == c0
            while r8_issued[0] < min(len(r8_seq), idx + 3):
                j = r8_issued[0]
                vw, cc = r8_seq[j]
                if not (os.environ.get("NORELOAD") and j >= 12):
                    S.dma("pool", "r8_%d" % (j % 3), dmaf(r8[j % 3], vw[:, :, cc:cc + 512]), writes=[("r8", j % 3)])
                r8_issued[0] += 1
            return idx % 3

        def load_r16(c0):
            sl = r16_n[0] % 2
            r16_n[0] += 1
            if not (os.environ.get("NORELOAD") and r16_n[0] > 4):
                S.dma("pool", "r16_%d" % sl, dmaf(r16[sl], w_down_v[:, :, c0:c0 + 256]), writes=[("r16", sl)])
            return sl

        S.dma("pool", "pw", dmaf(pw, w_ple_v), writes=["pw"])
        PASSES = [[(0, 512, 0)], [(512, 512, 0)], [(1024, 512, 0)], [(1536, 512, 0), (2048, 16, 512)]]

        def rms_norm_C(which, out_fn, segs, W, tag):
            bns = []
            for (t0, n, l0) in segs:
                bns.append(6 + misc_n[0] % 2)
                misc_n[0] += 1
            for m in range(8):
                sq = sqr[m % 2]
                S.op("act", actf(sq[:, 0:W], R[:, m, 0:W], AF.Square), reads=[("R", m)], writes=[("sqr", m % 2)])
                for si_, (t0, n, l0) in enumerate(segs):
                    bn = bns[si_]
                    S.group("pe", [mm(ps[bn][:, 0:n], ones_b, sq[:, l0:l0 + n], start=(m == 0), stop=(m == 7))],
                            reads=[("sqr", m % 2), "ones_b"], writes=[PK[bn]])
            for si_, (t0, n, l0) in enumerate(segs):
                bn = bns[si_]
                S.op("dve", ts(rnC[:, l0:l0 + n], ps[bn][:, 0:n], 1.0 / 1024, ALU.mult, EPS, ALU.add), reads=[PK[bn]], writes=["rnC"])
            S.op("act", actf(rnC[:, 0:W], rnC[:, 0:W], AF.Sqrt), reads=["rnC"], writes=["rnC"])
            S.op("dve", lambda e: e.reciprocal(out=rnC[:, 0:W], in_=rnC[:, 0:W]), reads=["rnC"], writes=["rnC"])
            for m in range(8):
                out_ap, wkey = out_fn(m)
                S.op("dve", stt(out_ap, R[:, m, 0:W], gcol(which, m), rnC[:, 0:W], ALU.mult, ALU.mult), reads=[("R", m), "rnC", "cols"], writes=[wkey])

        for pi, segs in enumerate(PASSES):
            W = sum(n for (_, n, _) in segs)
            t00 = segs[0][0]
            has_s = len(segs) > 1
            if pi == 0:
                S.dma("sp", "xres", dmaf(xres, x_p[0:512, :].rearrange("(j p) f -> p j f", p=128)), writes=["xres"])
            def stats_act(m):
                S.op("act", actf(sqr[m % 2][:, 0:W], R[:, m, 0:W], AF.Square), reads=[("R", m)], writes=[("sqr", m % 2)])

            def stats_pe(m, bns):
                for si_, (t0, n, l0) in enumerate(segs):
                    S.group("pe", [mm(ps[bns[si_]][:, 0:n], ones_b, sqr[m % 2][:, l0:l0 + n], start=(m == 0), stop=(m == 7))],
                            reads=[("sqr", m % 2), "ones_b"], writes=[PK[bns[si_]]])

            def norm_finish_row(bns, out_t, key, square):
                for si_, (t0, n, l0) in enumerate(segs):
                    S.op("dve", ts(out_t[:, l0:l0 + n], ps[bns[si_]][:, 0:n], 1.0 / 1024, ALU.mult, EPS, ALU.add), reads=[PK[bns[si_]]], writes=[key])
                if not square:
                    S.op("act", actf(out_t[:, 0:W], out_t[:, 0:W], AF.Sqrt), reads=[key], writes=[key])
                S.op("dve", lambda e: e.reciprocal(out=out_t[:, 0:W], in_=out_t[:, 0:W]), reads=[key], writes=[key])

            def pick_bns():
                o = []
                for _ in segs:
                    o.append(6 + misc_n[0] % 2)
                    misc_n[0] += 1
                return o

            bns1 = pick_bns()
            for blk in range(2):
                sl = load_r8(w_out_v, blk * 512)
                for m4 in range(4):
                    m = blk * 4 + m4
                    for (t0, n, l0) in segs:
                        b = next_bank()
                        fns = [mm(ps[b][:, 0:n], r8[sl][:, k, m4 * 128:(m4 + 1) * 128], cat[:, k, t0:t0 + n], start=(k == 0), stop=False) for k in range(8)]
                        if n == 512:
                            fns += [mm(ps[b][:, j * 128:(j + 1) * 128], xres[:, j, m * 128:(m + 1) * 128], ident_f, start=False, stop=(j == 3)) for j in range(4)]
                            rk = ["xres"]
                        else:
                            fns += [mm(ps[b][:, 0:16], xres_s[0:16, m * 128:(m + 1) * 128], ident_f[0:16, 0:16], start=False, stop=True)]
                            rk = ["xres_s"]
                        S.group("pe", fns, reads=[("r8", sl), "cat", "ident_f"] + rk, writes=[PK[b]])
                        S.op("act", actf(R[:, m, l0:l0 + n], ps[b][:, 0:n], AF.Copy), reads=[PK[b]], writes=[("R", m)])
                        S.op("act", actf(xnC[:, m, l0:l0 + n], ps[b][:, 0:n], AF.Copy, scale=gcol(0, m)), reads=[PK[b], "cols"], writes=[("xnC", m)])
                    stats_act(m)
                    if m >= 1:
                        stats_pe(m - 1, bns1)
            stats_pe(7, bns1)
            norm_finish_row(bns1, rnC, "rnC", True)
            if pi + 1 < len(PASSES):
                tn = PASSES[pi + 1][0][0]
                S.dma("sp", "xres", dmaf(xres, x_p[tn:tn + 512, :].rearrange("(j p) f -> p j f", p=128)), writes=["xres"])
                if len(PASSES[pi + 1]) > 1:
                    S.dma("sp", "xres_s", dmaf(xres_s[0:16, :], x_s), writes=["xres_s"])
            for blk in range(8):
                sl = load_r8(w_up_v, blk * 512)
                for m4 in range(4):
                    hc = blk * 4 + m4
                    for (t0, n, l0) in segs:
                        b = next_bank()
                        S.group("pe", [mm(ps[b][:, 0:n], r8[sl][:, k, m4 * 128:(m4 + 1) * 128], xnC[:, k, l0:l0 + n], start=(k == 0), stop=(k == 7)) for k in range(8)],
                                reads=[("r8", sl)] + [("xnC", k) for k in range(8)], writes=[PK[b]])
                        rt = relu_t[misc_n[0] % 2]; rkey = ("relu_t", misc_n[0] % 2)
                        misc_n[0] += 1
                        S.op("act", actf(rt[:, 0:n], ps[b][:, 0:n], AF.Relu), reads=[PK[b]], writes=[rkey])
                        S.op("dve", tt(hid[:, hc, l0:l0 + n], rt[:, 0:n], rt[:, 0:n], ALU.mult), reads=[rkey], writes=[("hid", hc)])
            bns2 = pick_bns()
            for blk in range(4):
                sl = load_r16(blk * 256)
                for m2 in range(2):
                    m = blk * 2 + m2
                    for (t0, n, l0) in segs:
                        b = next_bank()
                        S.group("pe", [mm(ps[b][:, 0:n], r16[sl][:, k, m2 * 128:(m2 + 1) * 128], hid[:, k, l0:l0 + n], start=(k == 0), stop=(k == 31)) for k in range(32)],
                                reads=[("r16", sl)] + [("hid", k) for k in range(32)], writes=[PK[b]])
                        sg = sig[misc_n[0] % 2]; skey = ("sig", misc_n[0] % 2)
                        misc_n[0] += 1
                        S.op("dve", tt(sg[:, 0:n], ps[b][:, 0:n], rnC[:, l0:l0 + n], ALU.mult), reads=[PK[b], "rnC"], writes=[skey])
                        S.op("dve", tt(R[:, m, l0:l0 + n], R[:, m, l0:l0 + n], sg[:, 0:n], ALU.add), reads=[skey, ("R", m)], writes=[("R", m)])
                    S.op("act", actf(xnC[:, m, 0:W], R[:, m, 0:W], AF.Copy, scale=gcol(1, m)), reads=[("R", m), "cols"], writes=[("xnC", m)])
                    stats_act(m)
                    if m >= 1:
                        stats_pe(m - 1, bns2)
            stats_pe(7, bns2)
            norm_finish_row(bns2, rnC, "rnC", False)
            for (t0, n, l0) in segs:
                ntile = (n + 127) // 128
                for j in range(ntile):
                    r = min(128, n - j * 128)
                    sl = misc_n[0] % 2
                    misc_n[0] += 1
                    src = p_p[t0 + j * 128:t0 + j * 128 + r, :] if n == 512 else p_s
                    S.dma("pool", "ptok%d" % sl, dmaf(ptok[sl][0:r, :], src), writes=[("ptok", sl)])
                    S.group("pe", [lambda e, kk=kk, sl=sl, r=r: e.transpose(out=psb[5][:, kk * 128:kk * 128 + r], in_=ptok[sl][0:r, kk * 128:(kk + 1) * 128], identity=ident_b[0:r, 0:r]) for kk in range(2)],
                            reads=[("ptok", sl), "ident_b"], writes=[PK[5]])
                    S.op("act", actf(pT[:, :, l0 + j * 128:l0 + j * 128 + r], psb[5][:, 0:256].rearrange("p (k t) -> p k t", k=2)[:, :, 0:r], AF.Copy), reads=[PK[5]], writes=["pT"])
            ntt = sum((n + 127) // 128 for (_, n, _) in segs)
            sigbufs = [(sig[0], ("sig", 0)), (sig[1], ("sig", 1)), (relu_t[0], ("relu_t", 0)), (relu_t[1], ("relu_t", 1))]

            def gate_chunk(m, sl, m4):
                ops = []
                Lop, Lgr, Ldma = mk_recorders(S, ops)
                for si_, (t0, n, l0) in enumerate(segs):
                    b = (2 * m + si_) % 4
                    pb_ = 6 + m % 2
                    sg, skey = sigbufs[(2 * m + si_) % 4]
                    Lgr("pe", [mm(ps[b][:, 0:n], r8[sl][:, k, m4 * 128:(m4 + 1) * 128], xnC[:, k, l0:l0 + n], start=(k == 0), stop=(k == 7)) for k in range(8)],
                        reads=[("r8", sl)] + [("xnC", k) for k in range(8)], writes=[PK[b]])
                    Lgr("pe", [mm(ps[pb_][:, 0:n], pw[:, kk, m * 128:(m + 1) * 128], pT[:, kk, l0:l0 + n], start=(kk == 0), stop=(kk == 1)) for kk in range(2)],
                        reads=["pw", "pT"], writes=[PK[pb_]])
                    Lop("dve", tt(sg[:, 0:n], ps[b][:, 0:n], rnC[:, l0:l0 + n], ALU.mult), reads=[PK[b], "rnC"], writes=[skey])
                    Lop("act", actf(sg[:, 0:n], sg[:, 0:n], AF.Sigmoid), reads=[skey], writes=[skey])
                    Lop("dve", tt(sg[:, 0:n], sg[:, 0:n], ps[pb_][:, 0:n], ALU.mult), reads=[PK[pb_], skey], writes=[skey])
                    Lop("dve", tt(R[:, m, l0:l0 + n], R[:, m, l0:l0 + n], sg[:, 0:n], ALU.add), reads=[skey, ("R", m)], writes=[("R", m)])
                sq = sqr[m % 2]
                Lop("act", actf(sq[:, 0:W], R[:, m, 0:W], AF.Square), reads=[("R", m)], writes=[("sqr", m % 2)])
                fns = []
                if m == 0:
                    fns.append(mm(ps[5][:, 256:256 + ntt], zeros_f, zeros_f[:, 0:ntt], start=True, stop=False))
                jt = 0
                for (t0, n, l0) in segs:
                    for j in range((n + 127) // 128):
                        r = min(128, n - j * 128)
                        fns.append(mm(ps[5][0:r, 256 + jt:257 + jt], sq[:, l0 + j * 128:l0 + j * 128 + r], ones_b[:, 0:1], start=False, stop=False))
                        jt += 1
                if m == 7:
                    fns.append(mm(ps[5][:, 256:256 + ntt], zeros_f, zeros_f[:, 0:ntt], start=False, stop=True))
                Lgr("pe", fns, reads=[("sqr", m % 2), "ones_b"], writes=[PK[5]])
                Lop("act", actf(R[:, m, 0:W], R[:, m, 0:W], AF.Copy, scale=gcol(2, m)), reads=[("R", m), ("sqr", m % 2), "cols"], writes=[("R", m)])
                return ops

            for blk in range(2):
                sl = load_r8(w_gate_v, blk * 512)
                chunks = [gate_chunk(blk * 4 + m4, sl, m4) for m4 in range(4)]
                zipper(chunks[0:2])
                zipper(chunks[2:4])
            if not RNCOL_INIT[0]:
                RNCOL_INIT[0] = True
                S.op("dve", lambda e: e.memset(rncol, 1.0), writes=["rncol"])
            S.op("dve", ts(rncol[:, 0:ntt], ps[5][:, 256:256 + ntt], 1.0 / 1024, ALU.mult, EPS, ALU.add), reads=[PK[5]], writes=["rncol"])
            S.op("act", actf(rncol[:, 0:ntt], rncol[:, 0:ntt], AF.Sqrt), reads=["rncol"], writes=["rncol"])
            S.op("dve", lambda e: e.reciprocal(out=rncol[:, 0:ntt], in_=rncol[:, 0:ntt]), reads=["rncol"], writes=["rncol"])
            jt = 0
            for (t0, n, l0) in segs:
                ntile = (n + 127) // 128
                for j in range(ntile):
                    r = min(128, n - j * 128)
                    ysl = 0
                    for half in range(2):
                        b = next_bank()
                        S.group("pe", [(lambda e, m4=m4, b=b, r=r, half=half, l0=l0, j=j: e.transpose(out=ps[b][0:r, m4 * 128:(m4 + 1) * 128], in_=R[:, half * 4 + m4, l0 + j * 128:l0 + j * 128 + r], identity=ident_f)) for m4 in range(4)],
                                reads=[("R", half * 4 + m4) for m4 in range(4)] + ["ident_f"], writes=[PK[b]])
                        S.op("act", actf(ytile[ysl][0:r, half * 512:(half + 1) * 512], ps[b][0:r, :], AF.Copy, scale=rncol[0:r, jt:jt + 1]), reads=[PK[b], "rncol"], writes=[("ytile", ysl)])
                    jt += 1
                    dst = y_p[t0 + j * 128:t0 + j * 128 + r, :] if n == 512 else y_s
                    S.dma("sp", "o_y%d" % ysl, dmaf(dst, ytile[ysl][0:r, :]), reads=[("ytile", ysl)])
        S.finish()
        with nc.Block() as block:
            S.replay(block)
    return nc


_PROG = {}


def _make_in_maps(inputs):
    f = lambda a: np.ascontiguousarray(np.asarray(a, dtype=np.float32))
    g = {k: f(v) for k, v in inputs.items()}
    shared = {
        "g_mix": g["g_mix"].reshape(1, 1024), "w_in": g["w_in"][0], "w_conv": g["w_conv"][0],
        "a_log": g["a_log"].reshape(1, 4), "dt_bias": g["dt_bias"].reshape(1, 4), "gdn_norm": g["gdn_norm"].reshape(1, 128),
        "ln_g": g["sgu_ln_g"].reshape(1, 512), "ln_b": g["sgu_ln_b"].reshape(1, 512), "w_s": g["w_s"][0],
        "b_s": g["b_s"].reshape(1, 512), "w_out": g["w_out"][0], "g_ff": g["g_ff"].reshape(8, 128), "w_up": g["w_up"][0],
        "w_down": g["w_down"][0], "g_ple": g["g_ple"].reshape(8, 128), "w_ple": g["w_ple"][0], "w_gate": g["w_ple_gate"][0],
        "g_fin": g["g_final"].reshape(8, 128),
    }
    maps = []
    for i in range(8):
        m = dict(shared)
        sl = slice(16 * i, 16 * i + 16)
        m["x_p"] = g["x_prompt"][i]
        m["x_s"] = g["x_sample"][sl, 0]
        m["st_conv"] = g["state_conv"][0, sl]
        m["st_gdn"] = g["state_gdn"][0, sl]
        m["p_p"] = g["p_prompt"][0, i]
        m["p_s"] = g["p_sample"][0, sl, 0]
        maps.append(m)
    return maps


def kernel(**inputs):
    if "nc" not in _PROG:
        _PROG["nc"] = build_program()
    nc = _PROG["nc"]
    maps = _make_in_maps(inputs)
    res = run_bass_kernel_spmd(nc, maps, core_ids=list(range(8)))
    R = res.results
    st = lambda name: np.stack([np.asarray(r[name], dtype=np.float32) for r in R])
    cc = lambda name: np.concatenate([np.asarray(r[name], dtype=np.float32) for r in R], axis=0)
    y_prompt = st("y_p")
    y_sample = cc("y_s")[:, None, :]
    new_conv_prompt = st("ncp")[None]
    new_gdn_prompt = st("ngp")[None]
    new_conv_sample = cc("ncs")[None]
    new_gdn_sample = cc("ngs")[None]
    new_sgu_v_sample = cc("nsv")[None, :, None, :]
    return (y_prompt, y_sample, new_conv_prompt, new_gdn_prompt, new_conv_sample, new_gdn_sample, new_sgu_v_sample)
```
